# Optimizing a Trainium2 kernel written in Bass

```python
import math
import jax
import jax.numpy as jnp
from jax import lax
import numpy as np

D_MODEL = 1024
BATCH = 2
SEQ = 8192
DEPTH = 2

GRID_W = 64
CTX_LEN = 256
N_DIR = 2
SHORT_CONV = 3

D_MIX = D_MODEL
D_MLSTM = D_MIX // 4
MLSTM_HEAD_DIM = 64
MLSTM_HEADS = D_MLSTM // MLSTM_HEAD_DIM
MLSTM_GATES = N_DIR * 2 * MLSTM_HEADS
MLSTM_CHUNK = 128

D_ATTN = D_MIX // 2
ATTN_HEAD_DIM = 64
ATTN_HEADS = D_ATTN // ATTN_HEAD_DIM
ATTN_KV_HEADS = 2
ATTN_GROUP = ATTN_HEADS // ATTN_KV_HEADS
D_KV = ATTN_KV_HEADS * ATTN_HEAD_DIM
ATTN_BLOCK = 128
ROPE_THETA = 10000.0

D_HYENA = D_MIX - D_MLSTM - D_ATTN
HYENA_ORDER = 2
HYENA_BANDS = 16
HYENA_EMB = 1 + 2 * HYENA_BANDS
HYENA_FILTER_HIDDEN = 64
HYENA_FAST_DECAY = 0.3
HYENA_SLOW_DECAY = 1.5
HYENA_DECAY_TARGET = 1e-2
HYENA_WINDOW_SHIFT = 0.05

N_IN = 4 * D_MLSTM + MLSTM_GATES + D_ATTN + 2 * D_KV + 3 * D_HYENA

N_EXPERTS = 16
EC_CAPACITY_FACTOR = 2
D_FF_EXPERT = 2816

N_MOD = 6
LN_EPS = 1e-5
RMS_EPS = 1e-6

kernel_name = "hybrid_mlstm_gqa_hyena_ecmoe_dit"


def _layer_norm(x, w=None, b=None):
    xf = x.astype(jnp.float32)
    mu = jnp.mean(xf, axis=-1, keepdims=True)
    var = jnp.mean(jnp.square(xf - mu), axis=-1, keepdims=True)
    y = (xf - mu) * lax.rsqrt(var + LN_EPS)
    if w is not None:
        y = y * w.astype(jnp.float32) + b.astype(jnp.float32)
    return y.astype(x.dtype)


def _rms_norm(x, w):
    xf = x.astype(jnp.float32)
    y = xf * lax.rsqrt(jnp.mean(xf * xf, axis=-1, keepdims=True) + RMS_EPS)
    return (y * w.astype(jnp.float32)).astype(x.dtype)


def _modulate(x, shift, scale):
    return _layer_norm(x) * (1 + scale) + shift


def _centred_dwconv(x, w, b):
    ch = x.shape[-1]
    pad = SHORT_CONV // 2
    y = lax.conv_general_dilated(x, w.astype(x.dtype)[:, None, :], window_strides=(1,),
                                 padding=[(pad, pad)], dimension_numbers=("NWC", "WIO", "NWC"),
                                 feature_group_count=ch)
    return y + b.astype(x.dtype)


def _split_projection(p):
    sizes = (2 * D_MLSTM, D_MLSTM, D_MLSTM, MLSTM_GATES, D_ATTN, D_KV, D_KV, 3 * D_HYENA)
    idx, acc = [], 0
    for s in sizes[:-1]:
        acc += s
        idx.append(acc)
    return jnp.split(p, idx, axis=-1)


def _mlstm_chunk_scan(q, k, v, li, lf, state):
    n_dir, bsz, nh, L, dh = q.shape
    nc = L // MLSTM_CHUNK

    def chunks(a):
        a = a.reshape(a.shape[:3] + (nc, MLSTM_CHUNK) + a.shape[4:])
        return jnp.moveaxis(a, 3, 0)

    lower = jnp.tril(jnp.ones((MLSTM_CHUNK, MLSTM_CHUNK), dtype=bool))

    def step(carry, inp):
        c0, n0, m0 = carry
        qc, kc, vc, ic, fc = inp
        bcum = jnp.cumsum(fc, axis=-1)
        log_inter = bcum + m0[..., None]
        log_intra = jnp.where(lower, bcum[..., :, None] - bcum[..., None, :] + ic[..., None, :], -jnp.inf)
        m = jnp.maximum(log_inter, jnp.max(log_intra, axis=-1))
        w_inter = jnp.exp(log_inter - m)
        scores = jnp.einsum("nbhjd,nbhsd->nbhjs", qc, kc) * jnp.exp(log_intra - m[..., None])
        num = (w_inter[..., None] * jnp.einsum("nbhed,nbhjd->nbhje", c0, qc)
               + jnp.einsum("nbhjs,nbhse->nbhje", scores, vc))
        den = w_inter * jnp.einsum("nbhd,nbhjd->nbhj", n0, qc) + jnp.sum(scores, axis=-1)
        hc = num / jnp.maximum(jnp.abs(den), jnp.exp(-m))[..., None]
        b_last = bcum[..., -1:]
        log_src = b_last - bcum + ic
        m_new = jnp.maximum(b_last[..., 0] + m0, jnp.max(log_src, axis=-1))
        w_src = jnp.exp(log_src - m_new[..., None])
        decay = jnp.exp(b_last[..., 0] + m0 - m_new)
        c_new = decay[..., None, None] * c0 + jnp.einsum("nbhs,nbhse,nbhsd->nbhed", w_src, vc, kc)
        n_new = decay[..., None] * n0 + jnp.einsum("nbhs,nbhsd->nbhd", w_src, kc)
        return (c_new, n_new, m_new), hc

    state, hs = lax.scan(step, state, tuple(chunks(a) for a in (q, k, v, li, lf)))
    hs = jnp.moveaxis(hs, 0, 3).reshape(n_dir, bsz, nh, L, dh)
    return hs, state


def _mlstm_zero_state(bsz):
    f32 = jnp.float32
    return (jnp.zeros((N_DIR, bsz, MLSTM_HEADS, MLSTM_HEAD_DIM, MLSTM_HEAD_DIM), f32),
            jnp.zeros((N_DIR, bsz, MLSTM_HEADS, MLSTM_HEAD_DIM), f32),
            jnp.zeros((N_DIR, bsz, MLSTM_HEADS), f32))


def _mlstm_stream(q, k, v, gates, gate_b, state):
    bsz, n, _ = q.shape

    def heads(a):
        return a.reshape(bsz, n, MLSTM_HEADS, MLSTM_HEAD_DIM).transpose(0, 2, 1, 3).astype(jnp.float32)

    def both_dirs(a):
        return jnp.stack([a, jnp.flip(a, axis=2)])

    g = gates.astype(jnp.float32).reshape(bsz, n, N_DIR, 2, MLSTM_HEADS) + gate_b.astype(jnp.float32)
    g = jnp.transpose(g, (2, 3, 0, 4, 1))
    li = jnp.stack([g[0, 0], jnp.flip(g[1, 0], axis=-1)])
    lf = jax.nn.log_sigmoid(jnp.stack([g[0, 1], jnp.flip(g[1, 1], axis=-1)]))
    h, state = _mlstm_chunk_scan(both_dirs(heads(q) * MLSTM_HEAD_DIM ** -0.5), both_dirs(heads(k)),
                                 both_dirs(heads(v)), li, lf, state)
    return h[0] + jnp.flip(h[1], axis=2), state


def _mlstm_output(h, o, norm_w):
    bsz, _, n, _ = h.shape
    h = _rms_norm(jnp.transpose(h, (0, 2, 1, 3)), norm_w.reshape(MLSTM_HEADS, MLSTM_HEAD_DIM))
    return (jax.nn.sigmoid(o.astype(jnp.float32)) * h.reshape(bsz, n, D_MLSTM)).astype(o.dtype)


def _axial_rope_tables(n_lat):
    rows = n_lat // GRID_W
    row = jnp.repeat(jnp.arange(rows, dtype=jnp.float32), GRID_W)
    col = (jnp.arange(n_lat) % GRID_W).astype(jnp.float32)
    nf = ATTN_HEAD_DIM // 4
    inv = ROPE_THETA ** (-jnp.arange(nf, dtype=jnp.float32) / nf)
    ar = row[:, None] * inv
    ac = col[:, None] * inv
    ang = jnp.concatenate([ar, ar, ac, ac], axis=-1)
    return jnp.cos(ang), jnp.sin(ang)


def _apply_rope(x, cos, sin):
    nf = ATTN_HEAD_DIM // 4
    xf = x.astype(jnp.float32)
    xr = xf.reshape(xf.shape[:-1] + (2, 2, nf))
    rot = jnp.stack([-xr[..., 1, :], xr[..., 0, :]], axis=-2).reshape(xf.shape)
    return (xf * cos + rot * sin).astype(x.dtype)


def _attend(q, k, v):
    s = jnp.einsum("bkgqd,bksd->bkgqs", q, k, preferred_element_type=jnp.float32)
    p = jax.nn.softmax(s, axis=-1).astype(v.dtype)
    return jnp.einsum("bkgqs,bksd->bkgqd", p, v)


def _gqa_mixer(q_lat, k_lat, v_lat, q_ctx, k_ctx, v_ctx, q_norm_w, k_norm_w, with_ctx_out):
    bsz, n_lat, _ = q_lat.shape
    n_ctx = q_ctx.shape[1]
    scale = ATTN_HEAD_DIM ** -0.5

    def q_heads(q):
        n = q.shape[1]
        q = q.reshape(bsz, n, ATTN_KV_HEADS, ATTN_GROUP, ATTN_HEAD_DIM).transpose(0, 2, 3, 1, 4)
        return _rms_norm(q, q_norm_w) * scale

    def kv_heads(t):
        n = t.shape[1]
        return t.reshape(bsz, n, ATTN_KV_HEADS, ATTN_HEAD_DIM).transpose(0, 2, 1, 3)

    cos, sin = _axial_rope_tables(n_lat)
    ql = _apply_rope(q_heads(q_lat), cos, sin)
    kl = _apply_rope(_rms_norm(kv_heads(k_lat), k_norm_w), cos, sin)
    kc = _rms_norm(kv_heads(k_ctx), k_norm_w)
    vc = kv_heads(v_ctx)
    k_all = jnp.concatenate([kl, kc], axis=2)
    v_all = jnp.concatenate([kv_heads(v_lat), vc], axis=2)
    nb = n_lat // ATTN_BLOCK
    qb = jnp.moveaxis(ql.reshape(bsz, ATTN_KV_HEADS, ATTN_GROUP, nb, ATTN_BLOCK, ATTN_HEAD_DIM), 3, 0)
    o = lax.map(lambda qi: _attend(qi, k_all, v_all), qb)
    y_lat = jnp.transpose(o, (1, 0, 4, 2, 3, 5)).reshape(bsz, n_lat, D_ATTN)
    if not with_ctx_out:
        return y_lat, None
    oc = _attend(q_heads(q_ctx), kc, vc)
    return y_lat, oc.transpose(0, 3, 1, 2, 4).reshape(bsz, n_ctx, D_ATTN)


def _hyena_filters(n, f_w1, f_b1, f_freq, f_w2, f_b2, f_w3):
    f32 = jnp.float32
    t = jnp.arange(n, dtype=f32) / n
    bands = jnp.arange(1, HYENA_BANDS + 1, dtype=f32)
    ang = (2.0 * math.pi) * t[:, None] * bands
    feats = jnp.concatenate([t[:, None], jnp.cos(ang), jnp.sin(ang)], axis=-1)
    hid = jnp.sin(f_freq[0].astype(f32) * (feats @ f_w1.astype(f32) + f_b1.astype(f32)))
    hid = jnp.sin(f_freq[1].astype(f32) * (hid @ f_w2.astype(f32) + f_b2.astype(f32)))
    filt = (hid @ f_w3.astype(f32)).reshape(n, N_DIR, HYENA_ORDER, D_HYENA)
    log_target = abs(math.log(HYENA_DECAY_TARGET))
    deltas = jnp.linspace(log_target / HYENA_SLOW_DECAY, log_target / HYENA_FAST_DECAY, D_HYENA, dtype=f32)
    window = jnp.exp(-t[:, None] * deltas) + HYENA_WINDOW_SHIFT
    filt = filt * window[:, None, None, :]
    fwd = filt[:, 0]
    bwd = filt[1:, 1]
    l1 = jnp.sum(jnp.abs(fwd), axis=0) + jnp.sum(jnp.abs(bwd), axis=0)
    taps = jnp.concatenate([fwd, jnp.zeros((1, HYENA_ORDER, D_HYENA), f32), bwd[::-1]], axis=0)
    return taps / l1


def _fft_long_conv(z, taps):
    n = z.shape[1]
    zf = jnp.fft.rfft(z, n=2 * n, axis=1)
    tf = jnp.fft.rfft(taps, axis=0)
    return jnp.fft.irfft(zf * tf[None], n=2 * n, axis=1)[:, :n]


def _hyena(u, filter_params, skip):
    n = u.shape[1]
    taps = _hyena_filters(n, *filter_params)
    v, x1, x2 = jnp.split(u.astype(jnp.float32), 3, axis=-1)
    z = v
    for o, gate in enumerate((x1, x2)):
        z = gate * (_fft_long_conv(z, taps[:, o]) + skip[o].astype(jnp.float32) * z)
    return z.astype(u.dtype)


def _mixing_sublayer(h_lat, h_ctx, w_in, mlstm_conv_w, mlstm_conv_b, mlstm_gate_b, mlstm_norm_w,
                     attn_q_norm_w, attn_k_norm_w, hyena_conv_w, hyena_conv_b, hyena_filter,
                     hyena_skip, w_out, with_ctx_out):
    bsz = h_lat.shape[0]
    mqk_l, mv_l, mo_l, mg_l, aq_l, ak_l, av_l, hy_l = _split_projection(h_lat @ w_in)
    mqk_c, mv_c, mo_c, mg_c, aq_c, ak_c, av_c, hy_c = _split_projection(h_ctx @ w_in)
    q_c, k_c = jnp.split(jax.nn.silu(_centred_dwconv(mqk_c, mlstm_conv_w, mlstm_conv_b)), 2, axis=-1)
    q_l, k_l = jnp.split(jax.nn.silu(_centred_dwconv(mqk_l, mlstm_conv_w, mlstm_conv_b)), 2, axis=-1)
    hm_c, ctx_state = _mlstm_stream(q_c, k_c, mv_c, mg_c, mlstm_gate_b, _mlstm_zero_state(bsz))
    hm_l, _ = _mlstm_stream(q_l, k_l, mv_l, mg_l, mlstm_gate_b, ctx_state)
    ym_l = _mlstm_output(hm_l, mo_l, mlstm_norm_w)
    ya_l, ya_c = _gqa_mixer(aq_l, ak_l, av_l, aq_c, ak_c, av_c, attn_q_norm_w, attn_k_norm_w, with_ctx_out)
    yh_l = _hyena(_centred_dwconv(hy_l, hyena_conv_w, hyena_conv_b), hyena_filter, hyena_skip)
    y_lat = jnp.concatenate([ym_l, ya_l, yh_l], axis=-1) @ w_out
    if not with_ctx_out:
        return y_lat, None
    ym_c = _mlstm_output(hm_c, mo_c, mlstm_norm_w)
    yh_c = _hyena(_centred_dwconv(hy_c, hyena_conv_w, hyena_conv_b), hyena_filter, hyena_skip)
    y_ctx = jnp.concatenate([ym_c, ya_c, yh_c], axis=-1) @ w_out
    return y_lat, y_ctx


def _expert_choice_ffn(h, router_w, router_b, w_gate, w_up, w_down):
    bsz, n, d = h.shape
    cap = EC_CAPACITY_FACTOR * n // N_EXPERTS
    logits = (h @ router_w).astype(jnp.float32) + router_b.astype(jnp.float32)
    aff = jax.nn.softmax(logits, axis=-1)
    top_val, top_idx = lax.top_k(jnp.swapaxes(aff, 1, 2), cap)
    xs = jax.vmap(lambda hb, ib: hb[ib])(h, top_idx)
    g = jnp.einsum("becd,edf->becf", xs, w_gate)
    u = jnp.einsum("becd,edf->becf", xs, w_up)
    y = jnp.einsum("becf,efd->becd", jax.nn.silu(g) * u, w_down) * top_val[..., None].astype(h.dtype)
    return jax.vmap(lambda yb, ib: jnp.zeros((n, d), yb.dtype).at[ib.reshape(-1)].add(yb.reshape(-1, d)))(y, top_idx)


def setup_inputs(seed: int = 0) -> dict:
    key = jax.random.key(seed)
    ks = iter(jax.random.split(key, 40))
    f32 = jnp.float32

    def nrm(shape, scale):
        return scale * jax.random.normal(next(ks), shape, f32)

    beta = (8.0 * DEPTH) ** -0.25
    fg_base = jnp.linspace(3.0, 6.0, MLSTM_HEADS, dtype=f32)
    mlstm_gate_b = jnp.concatenate([nrm((DEPTH, N_DIR, 1, MLSTM_HEADS), 0.1),
                                    fg_base + nrm((DEPTH, N_DIR, 1, MLSTM_HEADS), 0.1)], axis=2)
    return {
        "x": nrm((BATCH, SEQ, D_MODEL), 1.0),
        "c": nrm((BATCH, D_MODEL), 1.0),
        "ctx": nrm((BATCH, CTX_LEN, D_MODEL), 1.0),
        "c_ctx": nrm((D_MODEL,), 1.0),
        "w_mod": nrm((DEPTH, D_MODEL, N_MOD * D_MODEL), 0.5 * D_MODEL ** -0.5),
        "b_mod": nrm((DEPTH, N_MOD * D_MODEL), 0.02),
        "w_in": nrm((DEPTH, D_MODEL, N_IN), D_MODEL ** -0.5),
        "mlstm_conv_w": nrm((DEPTH, SHORT_CONV, 2 * D_MLSTM), SHORT_CONV ** -0.5),
        "mlstm_conv_b": nrm((DEPTH, 2 * D_MLSTM), 0.02),
        "mlstm_gate_b": mlstm_gate_b,
        "mlstm_norm_w": 1.0 + nrm((DEPTH, D_MLSTM), 0.02),
        "attn_q_norm_w": 1.0 + nrm((DEPTH, ATTN_HEAD_DIM), 0.02),
        "attn_k_norm_w": 1.0 + nrm((DEPTH, ATTN_HEAD_DIM), 0.02),
        "hyena_conv_w": nrm((DEPTH, SHORT_CONV, 3 * D_HYENA), SHORT_CONV ** -0.5),
        "hyena_conv_b": nrm((DEPTH, 3 * D_HYENA), 0.02),
        "hyena_f_w1": nrm((DEPTH, HYENA_EMB, HYENA_FILTER_HIDDEN), HYENA_EMB ** -0.5),
        "hyena_f_b1": nrm((DEPTH, HYENA_FILTER_HIDDEN), 0.5),
        "hyena_f_freq": 1.0 + nrm((DEPTH, 2, HYENA_FILTER_HIDDEN), 0.1),
        "hyena_f_w2": nrm((DEPTH, HYENA_FILTER_HIDDEN, HYENA_FILTER_HIDDEN), HYENA_FILTER_HIDDEN ** -0.5),
        "hyena_f_b2": nrm((DEPTH, HYENA_FILTER_HIDDEN), 0.5),
        "hyena_f_w3": nrm((DEPTH, HYENA_FILTER_HIDDEN, N_DIR * HYENA_ORDER * D_HYENA), HYENA_FILTER_HIDDEN ** -0.5),
        "hyena_skip": nrm((DEPTH, HYENA_ORDER, D_HYENA), 0.5),
        "w_out": nrm((DEPTH, D_MIX, D_MODEL), beta * D_MIX ** -0.5),
        "ln_mix_w": 1.0 + nrm((DEPTH, D_MODEL), 0.02),
        "ln_mix_b": nrm((DEPTH, D_MODEL), 0.02),
        "router_w": nrm((DEPTH, D_MODEL, N_EXPERTS), D_MODEL ** -0.5),
        "router_b": nrm((DEPTH, N_EXPERTS), 0.01),
        "exp_w_gate": nrm((DEPTH, N_EXPERTS, D_MODEL, D_FF_EXPERT), D_MODEL ** -0.5),
        "exp_w_up": nrm((DEPTH, N_EXPERTS, D_MODEL, D_FF_EXPERT), D_MODEL ** -0.5),
        "exp_w_down": nrm((DEPTH, N_EXPERTS, D_FF_EXPERT, D_MODEL), beta * D_FF_EXPERT ** -0.5),
        "ln_ffn_w": 1.0 + nrm((DEPTH, D_MODEL), 0.02),
        "ln_ffn_b": nrm((DEPTH, D_MODEL), 0.02),
    }


def reference(x, c, ctx, c_ctx, w_mod, b_mod, w_in, mlstm_conv_w, mlstm_conv_b, mlstm_gate_b,
              mlstm_norm_w, attn_q_norm_w, attn_k_norm_w, hyena_conv_w, hyena_conv_b,
              hyena_f_w1, hyena_f_b1, hyena_f_freq, hyena_f_w2, hyena_f_b2, hyena_f_w3,
              hyena_skip, w_out, ln_mix_w, ln_mix_b, router_w, router_b,
              exp_w_gate, exp_w_up, exp_w_down, ln_ffn_w, ln_ffn_b):
    alpha = (2.0 * DEPTH) ** 0.25
    for l in range(DEPTH):
        last = l == DEPTH - 1
        mod_x = jnp.split((jax.nn.silu(c) @ w_mod[l] + b_mod[l])[:, None, :], N_MOD, axis=-1)
        mod_c = jnp.split((jax.nn.silu(c_ctx) @ w_mod[l] + b_mod[l])[None, None, :], N_MOD, axis=-1)
        mix_x, mix_c = _mixing_sublayer(
            _modulate(x, mod_x[0], mod_x[1]), _modulate(ctx, mod_c[0], mod_c[1]),
            w_in[l], mlstm_conv_w[l], mlstm_conv_b[l], mlstm_gate_b[l], mlstm_norm_w[l],
            attn_q_norm_w[l], attn_k_norm_w[l], hyena_conv_w[l], hyena_conv_b[l],
            (hyena_f_w1[l], hyena_f_b1[l], hyena_f_freq[l], hyena_f_w2[l], hyena_f_b2[l], hyena_f_w3[l]),
            hyena_skip[l], w_out[l], not last)
        x = _layer_norm(alpha * x + mod_x[2] * mix_x, ln_mix_w[l], ln_mix_b[l])
        ffn_x = _expert_choice_ffn(_modulate(x, mod_x[3], mod_x[4]), router_w[l], router_b[l],
                                   exp_w_gate[l], exp_w_up[l], exp_w_down[l])
        x = _layer_norm(alpha * x + mod_x[5] * ffn_x, ln_ffn_w[l], ln_ffn_b[l])
        if not last:
            ctx = _layer_norm(alpha * ctx + mod_c[2] * mix_c, ln_mix_w[l], ln_mix_b[l])
            ffn_c = _expert_choice_ffn(_modulate(ctx, mod_c[3], mod_c[4]), router_w[l], router_b[l],
                                       exp_w_gate[l], exp_w_up[l], exp_w_down[l])
            ctx = _layer_norm(alpha * ctx + mod_c[5] * ffn_c, ln_ffn_w[l], ln_ffn_b[l])
    return x
```

```python
import numpy as np
from contextlib import ExitStack
import concourse.bass as bass
import concourse.mybir as mybir
from concourse.bass_utils import run_bass_kernel_spmd

F32 = mybir.dt.float32
BF16 = mybir.dt.bfloat16
I32 = mybir.dt.int32
AF = mybir.ActivationFunctionType
ALU = mybir.AluOpType
AX = mybir.AxisListType

NCORES = 8
SELF_SYNC = True


class Prog:
    ENGS = ("sync", "scalar", "vector", "gpsimd", "tensor")

    def __init__(self, nc, es, n_dma_sems=12):
        self.nc, self.es = nc, es
        self.ops = {e: [] for e in self.ENGS}
        self.esem = {}
        self.ecount = {}
        for e in ("scalar", "vector", "gpsimd", "tensor"):
            self.esem[e] = es.enter_context(nc.semaphore(f"sem_{e}"))
            self.ecount[e] = 0
        self.dpool = {}
        for q in ("sync", "scalar", "gpsimd"):
            self.dpool[q] = dict(
                sems=[es.enter_context(nc.semaphore(f"dsem_{q}_{i}")) for i in range(n_dma_sems)],
                cnt=[0] * n_dma_sems, nxt=0, know=[None] * n_dma_sems)
        self.semobj = {}
        self.lastw = {}
        self.readers = {}
        self.know = {e: {} for e in self.ENGS}
        self.final_tokens = []

    def _need(self, eng, tok, waits):
        sk, v, kn = tok
        if self.know[eng].get(sk, 0) >= v:
            return
        waits.append((sk, v))
        k = self.know[eng]
        for a, b in kn.items():
            if k.get(a, 0) < b:
                k[a] = b
        if k.get(sk, 0) < v:
            k[sk] = v

    def op(self, eng, fn, reads=(), writes=(), dma=False, final=False):
        waits = []
        toks = []
        for key in reads:
            t = self.lastw.get(key)
            if t is not None:
                toks.append(t)
        for key in writes:
            t = self.lastw.get(key)
            if t is not None:
                toks.append(t)
            toks.extend(self.readers.get(key, ()))
        own = None if dma else ("e", eng)
        for t in toks:
            if t[0] == own and (eng == "tensor" or not SELF_SYNC):
                continue
            self._need(eng, t, waits)
        if dma:
            pool = self.dpool[eng]
            j = pool["nxt"]
            pool["nxt"] = (j + 1) % len(pool["sems"])
            if pool["cnt"][j] > 0:
                self._need(eng, (("d", eng, j), pool["cnt"][j], pool["know"][j]), waits)
            pool["cnt"][j] += 16
            sk = ("d", eng, j)
            self.semobj[sk] = pool["sems"][j]
            kn = dict(self.know[eng])
            pool["know"][j] = kn
            tok = (sk, pool["cnt"][j], kn)
            inc = (pool["sems"][j], 16)
        else:
            self.ecount[eng] += 1
            sk = ("e", eng)
            self.semobj[sk] = self.esem[eng]
            tok = (sk, self.ecount[eng], dict(self.know[eng]))
            inc = (self.esem[eng], 1)
        self.ops[eng].append((waits, fn, inc))
        for key in reads:
            self.readers.setdefault(key, []).append(tok)
        for key in writes:
            self.lastw[key] = tok
            self.readers[key] = []
        if final:
            self.final_tokens.append(tok)
        return tok

    def emit(self):
        waits = []
        for t in self.final_tokens:
            self._need("sync", t, waits)
        if waits:
            self.ops["sync"].append((waits, None, None))
        nc = self.nc
        with nc.Block() as block:
            def run(eng_name):
                def body(eng):
                    for waits, fn, inc in self.ops[eng_name]:
                        for sk, v in waits:
                            eng.wait_ge(self.semobj[sk], v)
                        if fn is not None:
                            fn(eng).then_inc(inc[0], inc[1])
                return body
            block.sync(run("sync"))
            block.scalar(run("scalar"))
            block.vector(run("vector"))
            block.gpsimd(run("gpsimd"))
            block.tensor(run("tensor"))

    def dma(self, q, out, in_, reads=(), writes=(), final=False, **kw):
        return self.op(q, lambda e: e.dma_start(out=out, in_=in_, **kw), reads, writes, dma=True, final=final)

    def mm(self, out, lhsT, rhs, start, stop, reads=(), writes=()):
        return self.op("tensor", lambda e: e.matmul(out, lhsT, rhs, start=start, stop=stop), reads, writes)

    def act(self, out, in_, func, reads=(), writes=(), eng="scalar", **kw):
        return self.op(eng, lambda e: e.activation(out=out, in_=in_, func=func, **kw), reads, writes)

    def tt(self, eng, out, in0, in1, op, reads=(), writes=()):
        return self.op(eng, lambda e: e.tensor_tensor(out=out, in0=in0, in1=in1, op=op), reads, writes)

    def ts(self, eng, out, in0, s1, s2, op0, op1=None, reads=(), writes=(), accum_out=None):
        kw = {}
        if op1 is not None:
            kw["op1"] = op1
        if accum_out is not None:
            kw["accum_out"] = accum_out
        return self.op(eng, lambda e: e.tensor_scalar(out=out, in0=in0, scalar1=s1, scalar2=s2, op0=op0, **kw),
                       reads, writes)

    def stt(self, eng, out, in0, scalar, in1, op0, op1, reads=(), writes=()):
        return self.op(eng, lambda e: e.scalar_tensor_tensor(out=out, in0=in0, scalar=scalar, in1=in1,
                                                             op0=op0, op1=op1), reads, writes)

    def copy(self, eng, out, in_, reads=(), writes=()):
        if eng == "scalar":
            return self.op(eng, lambda e: e.activation(out=out, in_=in_, func=AF.Copy), reads, writes)
        return self.op(eng, lambda e: e.tensor_copy(out=out, in_=in_), reads, writes)

    def tr(self, out, in_, ident, reads=(), writes=()):
        return self.op("tensor", lambda e: e.transpose(out, in_, ident), reads, writes)

    def memset(self, eng, ap, val, writes=()):
        return self.op(eng, lambda e: e.memset(ap, val), (), writes)

    def gen(self, eng, f, reads=(), writes=()):
        return self.op(eng, f, reads, writes)


def _run(nc, in_maps):
    return run_bass_kernel_spmd(nc, in_maps, core_ids=list(range(NCORES)))


D_MODEL = 1024
DEPTH = 2
N_MOD = 6
MODC = N_MOD * D_MODEL // NCORES


def build_k0():
    nc = bass.Bass("TRN2", target_bir_lowering=False)
    cvT = nc.dram_tensor("cvT", [128, 8, 3], F32, kind="ExternalInput").ap()
    wm = nc.dram_tensor("wm", [DEPTH, 128, 8, MODC], F32, kind="ExternalInput").ap()
    bm = nc.dram_tensor("bm", [DEPTH, 3, MODC], F32, kind="ExternalInput").ap()
    out = nc.dram_tensor("mod", [DEPTH, 3, MODC], F32, kind="ExternalOutput").ap()
    with ExitStack() as es:
        P = Prog(nc, es)
        sb = lambda name, shape, dt=F32: es.enter_context(nc.sbuf_tensor(name, shape, dt))
        cv = sb("cv", [128, 8, 3])
        cs = sb("cs", [128, 8, 3])
        w = [sb(f"w{l}", [128, 8, MODC]) for l in range(DEPTH)]
        b = sb("b", [3, DEPTH, MODC])
        o = sb("o", [3, DEPTH, MODC])
        ps = [es.enter_context(nc.psum_tensor(f"ps{i}", [128, 512], F32)) for i in range(2)]
        P.dma("sync", cv[:], cvT, writes=["cv"])
        for l in range(DEPTH):
            P.dma("sync" if l == 0 else "gpsimd", w[l][:], wm[l], writes=[f"w{l}"])
            P.dma("sync", b[:, l, :], bm[l], writes=[f"b{l}"])
        P.act(cs[:], cv[:], AF.Silu, reads=["cv"], writes=["cs"])
        H = MODC // 2
        for l in range(DEPTH):
            for h in range(2):
                pt = ps[h]
                for kc in range(8):
                    P.mm(pt[0:3, 0:H], cs[:, kc, :], w[l][:, kc, h * H:(h + 1) * H], kc == 0, kc == 7,
                         reads=["cs", f"w{l}"], writes=[f"ps{h}"])
                P.op("vector", lambda e, l=l, h=h, pt=pt: e.tensor_tensor(
                    out=o[:, l, h * H:(h + 1) * H], in0=pt[0:3, 0:H], in1=b[:, l, h * H:(h + 1) * H], op=ALU.add),
                    reads=[f"ps{h}", f"b{l}"], writes=[f"o{l}{h}"])
            P.dma("sync", out[l], o[:, l, :], reads=[f"o{l}0", f"o{l}1"], final=True)
        P.emit()
    return nc


def run_k0(c, c_ctx, w_mod, b_mod):
    cv = np.concatenate([c, c_ctx[None]], 0)
    cvT = np.ascontiguousarray(cv.T.reshape(8, 128, 3).transpose(1, 0, 2))
    nc = build_k0()
    in_maps = []
    for i in range(NCORES):
        sl = slice(i * MODC, (i + 1) * MODC)
        wm = np.ascontiguousarray(w_mod[:, :, sl].reshape(DEPTH, 8, 128, MODC).transpose(0, 2, 1, 3))
        bm = np.ascontiguousarray(np.broadcast_to(b_mod[:, None, sl], (DEPTH, 3, MODC)))
        in_maps.append({"cvT": cvT, "wm": wm, "bm": bm})
    res = _run(nc, in_maps)
    return np.concatenate([r["mod"] for r in res.results], axis=-1)


class Ring:
    def __init__(self, items):
        self.items, self.i = items, 0

    def next(self):
        it = self.items[self.i % len(self.items)]
        self.i += 1
        return it


def mk_alloc(nc, es):
    def sb(name, shape, dt=F32):
        return es.enter_context(nc.sbuf_tensor(name, shape, dt))

    def ps(name, shape, dt=F32):
        return es.enter_context(nc.psum_tensor(name, shape, dt))

    def ring(name, n, shape, dt=F32, psum=False):
        return Ring([((ps if psum else sb)(f"{name}{i}", shape, dt), f"{name}{i}") for i in range(n)])
    return sb, ps, ring


def bcast_rows(ap1d, nparts):
    return bass.AP(tensor=ap1d.tensor, offset=ap1d.offset, ap=[[0, nparts]] + [list(x) for x in ap1d.ap])


LN_EPS = 1e-5


def emit_layernorm(P, x, xkey, xn, xnkey, st, mv, rs, skey, n=1024):
    for j in range(n // 512):
        P.gen("vector", lambda e, j=j: e.bn_stats(out=st[:, j, :], in_=x[:, j * 512:(j + 1) * 512]),
              reads=[xkey], writes=[skey + f"st{j}"])
    P.gen("vector", lambda e: e.bn_aggr(out=mv[:], in_=st[:]),
          reads=[skey + f"st{j}" for j in range(n // 512)], writes=[skey + "mv"])
    P.ts("vector", rs[:], mv[:, 1:2], LN_EPS, None, ALU.add, reads=[skey + "mv"], writes=[skey + "rs"])
    P.act(rs[:], rs[:], AF.Sqrt, reads=[skey + "rs"], writes=[skey + "rs"])
    P.gen("vector", lambda e: e.reciprocal(out=rs[:], in_=rs[:]), reads=[skey + "rs"], writes=[skey + "rs"])
    P.ts("vector", xn, x, mv[:, 0:1], rs[:, 0:1], ALU.subtract, ALU.mult,
         reads=[xkey, skey + "mv", skey + "rs"], writes=[xnkey])


N_IN = 2576
K1_TILES = 17


def build_k1():
    nc = bass.Bass("TRN2", target_bir_lowering=False)
    xt = nc.dram_tensor("xt", [K1_TILES, 128, 1024], F32, kind="ExternalInput").ap()
    modr = nc.dram_tensor("modr", [2, 2, 1024], F32, kind="ExternalInput").ap()
    win = nc.dram_tensor("win", [128, 8, N_IN], F32, kind="ExternalInput").ap()
    identd = nc.dram_tensor("ident", [128, 128], F32, kind="ExternalInput").ap()
    out = nc.dram_tensor("p", [K1_TILES, 128, N_IN], F32, kind="ExternalOutput").ap()
    with ExitStack() as es:
        P = Prog(nc, es)
        sb, ps, ring = mk_alloc(nc, es)
        idf = sb("idf", [128, 128])
        idb = sb("idb", [128, 128], BF16)
        P.dma("sync", idf[:], identd, writes=["idf"])
        P.copy("vector", idb[:], idf[:], reads=["idf"], writes=["idb"])
        modt = sb("modt", [128, 2, 2, 1024])
        for g in range(2):
            for j in range(2):
                P.dma("sync", modt[:, g, j, :], bcast_rows(modr[g, j], 128), writes=[f"mod{g}{j}"])
            P.ts("vector", modt[:, g, 0, :], modt[:, g, 0, :], 1.0, None, ALU.add,
                 reads=[f"mod{g}0"], writes=[f"mod{g}0"])
        wbf = sb("wbf", [128, 8, N_IN], BF16)
        wst = ring("wst", 2, [128, N_IN])
        for kc in range(8):
            t, k = wst.next()
            P.dma("gpsimd" if kc % 2 else "sync", t[:], win[:, kc, :], writes=[k])
            P.copy("gpsimd" if kc % 2 else "vector", wbf[:, kc, :], t[:], reads=[k], writes=[f"wbf{kc}"])
        wkeys = [f"wbf{kc}" for kc in range(8)]
        xr = ring("x", 2, [128, 1024])
        xnr = ring("xn", 2, [128, 1024])
        h1r = ring("h1", 2, [128, 1024])
        hr = ring("h", 2, [128, 1024], BF16)
        hTr = ring("hT", 2, [128, 1024], BF16)
        orr = ring("o", 2, [128, N_IN])
        str_ = ring("st", 2, [128, 2, 6])
        mvr = ring("mv", 2, [128, 2])
        rsr = ring("rs", 2, [128, 1])
        pTr = ring("pT", 2, [128, 1024], BF16, psum=True)
        pmr = ring("pm", 4, [128, 512], F32, psum=True)
        ev = 0
        for t in range(K1_TILES):
            g = 0 if t < 16 else 1
            x, xk = xr.next()
            P.dma("sync", x[:], xt[t], writes=[xk])
            xn, xnk = xnr.next()
            st, _ = str_.next(); mv, _ = mvr.next(); rs, _ = rsr.next()
            emit_layernorm(P, x[:], xk, xn[:], xnk, st, mv, rs, f"ln{t % 2}")
            h1, h1k = h1r.next()
            P.tt("gpsimd", h1[:], xn[:], modt[:, g, 0, :], ALU.mult, reads=[xnk, f"mod{g}0"], writes=[h1k])
            h, hk = hr.next()
            P.tt("gpsimd", h[:], h1[:], modt[:, g, 1, :], ALU.add, reads=[h1k, f"mod{g}1"], writes=[hk])
            pT, pTk = pTr.next()
            for kc in range(8):
                P.tr(pT[:, kc * 128:(kc + 1) * 128], h[:, kc * 128:(kc + 1) * 128], idb[:],
                     reads=[hk, "idb"], writes=[pTk])
            hT, hTk = hTr.next()
            P.copy("scalar", hT[:], pT[:], reads=[pTk], writes=[hTk])
            o, ok = orr.next()
            for cg in range(6):
                c0 = cg * 512
                n = min(512, N_IN - c0)
                pm, pmk = pmr.next()
                for kc in range(8):
                    P.mm(pm[:, 0:n], hT[:, kc * 128:(kc + 1) * 128], wbf[:, kc, c0:c0 + n], kc == 0, kc == 7,
                         reads=[hTk, wkeys[kc]], writes=[pmk])
                P.copy("scalar" if ev % 2 else "vector", o[:, c0:c0 + n], pm[:, 0:n], reads=[pmk], writes=[ok + f"c{cg}"])
                ev += 1
            P.dma("gpsimd", out[t], o[:], reads=[ok + f"c{cg}" for cg in range(6)], writes=[ok + "dma"], final=True)
        P.emit()
    return nc


def lay_w(w, kchunks):
    return np.ascontiguousarray(w.reshape(kchunks, 128, w.shape[1]).transpose(1, 0, 2))


def run_k1(x, ctx, mod_l, w_in_l):
    nc = build_k1()
    B, n, D = x.shape
    seg = n // 4
    ctxf = ctx.reshape(-1, D)
    ident = np.eye(128, dtype=np.float32)
    win = lay_w(w_in_l, 8)
    in_maps = []
    for i in range(NCORES):
        b, s = i // 4, i % 4
        xt = np.zeros((K1_TILES * 128, D), np.float32)
        xt[:seg] = x[b, s * seg:(s + 1) * seg]
        xt[seg:seg + 64] = ctxf[i * 64:(i + 1) * 64]
        modr = np.stack([np.stack([mod_l[b, 1024:2048], mod_l[b, 0:1024]]),
                         np.stack([mod_l[2, 1024:2048], mod_l[2, 0:1024]])])
        in_maps.append({"xt": xt.reshape(K1_TILES, 128, D), "modr": np.ascontiguousarray(modr), "win": win,
                        "ident": ident})
    res = _run(nc, in_maps)
    P_lat = np.zeros((B, n, N_IN), np.float32)
    P_ctx = np.zeros((B * ctx.shape[1], N_IN), np.float32)
    for i in range(NCORES):
        b, s = i // 4, i % 4
        p = res.results[i]["p"].reshape(K1_TILES * 128, N_IN)
        P_lat[b, s * seg:(s + 1) * seg] = p[:seg]
        P_ctx[i * 64:(i + 1) * 64] = p[seg:seg + 64]
    return P_lat, P_ctx.reshape(B, ctx.shape[1], N_IN)


ALPHA = (2.0 * DEPTH) ** 0.25
N_EXP = 16


def build_k5():
    nc = bass.Bass("TRN2", target_bir_lowering=False)
    T = K1_TILES
    yt = nc.dram_tensor("yt", [T, 128, 1024], F32, kind="ExternalInput").ap()
    xt = nc.dram_tensor("xt", [T, 128, 1024], F32, kind="ExternalInput").ap()
    wout = nc.dram_tensor("wout", [128, 8, 1024], F32, kind="ExternalInput").ap()
    rows = nc.dram_tensor("rows", [2, 3, 1024], F32, kind="ExternalInput").ap()
    lnr = nc.dram_tensor("lnr", [2, 1024], F32, kind="ExternalInput").ap()
    rwd = nc.dram_tensor("rw", [128, 8, N_EXP], F32, kind="ExternalInput").ap()
    rbd = nc.dram_tensor("rb", [N_EXP], F32, kind="ExternalInput").ap()
    identd = nc.dram_tensor("ident", [128, 128], F32, kind="ExternalInput").ap()
    x1o = nc.dram_tensor("o_x1", [T, 128, 1024], F32, kind="ExternalOutput").ap()
    h2o = nc.dram_tensor("o_h2", [T, 128, 1024], BF16, kind="ExternalOutput").ap()
    affo = nc.dram_tensor("o_aff", [T, 128, N_EXP], F32, kind="ExternalOutput").ap()
    with ExitStack() as es:
        P = Prog(nc, es)
        sb, ps, ring = mk_alloc(nc, es)
        idf = sb("idf", [128, 128])
        idb = sb("idb", [128, 128], BF16)
        P.dma("sync", idf[:], identd, writes=["idf"])
        P.copy("vector", idb[:], idf[:], reads=["idf"], writes=["idb"])
        rowt = sb("rowt", [128, 2, 3, 1024])
        for g in range(2):
            for j in range(3):
                P.dma("sync", rowt[:, g, j, :], bcast_rows(rows[g, j], 128), writes=[f"row{g}{j}"])
            P.ts("vector", rowt[:, g, 1, :], rowt[:, g, 1, :], 1.0, None, ALU.add,
                 reads=[f"row{g}1"], writes=[f"row{g}1"])
        lnt = sb("lnt", [128, 2, 1024])
        for j in range(2):
            P.dma("sync", lnt[:, j, :], bcast_rows(lnr[j], 128), writes=[f"ln{j}"])
        rw = sb("rwt", [128, 8, N_EXP])
        P.dma("sync", rw[:], rwd, writes=["rw"])
        rb = sb("rbt", [128, N_EXP])
        P.dma("sync", rb[:], bcast_rows(rbd, 128), writes=["rb"])
        wbf = sb("wbf", [128, 8, 1024], BF16)
        wst = ring("wst", 2, [128, 1024])
        for kc in range(8):
            t, k = wst.next()
            P.dma("gpsimd" if kc % 2 else "sync", t[:], wout[:, kc, :], writes=[k])
            P.copy("gpsimd" if kc % 2 else "vector", wbf[:, kc, :], t[:], reads=[k], writes=[f"wbf{kc}"])
        wkeys = [f"wbf{kc}" for kc in range(8)]
        yr = ring("y", 2, [128, 1024]); ybr = ring("yb", 2, [128, 1024], BF16)
        yTr = ring("yT", 2, [128, 1024], BF16)
        xr = ring("x", 2, [128, 1024]); tmpr = ring("tmp", 2, [128, 1024]); rr = ring("r", 2, [128, 1024])
        xnr = ring("xn", 2, [128, 1024]); x1r = ring("x1_", 2, [128, 1024]); x1ar = ring("x1a", 2, [128, 1024])
        xn2r = ring("xn2", 2, [128, 1024]); h2fr = ring("h2f", 2, [128, 1024]); h2ar = ring("h2a", 2, [128, 1024])
        h2br = ring("h2b", 2, [128, 1024], BF16)
        h2Tr = ring("h2T", 2, [128, 1024])
        str_ = ring("st", 4, [128, 2, 6]); mvr = ring("mv", 4, [128, 2]); rsr = ring("rs", 4, [128, 1])
        lgr = ring("lg", 2, [128, N_EXP]); exr = ring("ex", 2, [128, N_EXP]); afr = ring("af", 2, [128, N_EXP])
        smr = ring("sm", 2, [128, 4])
        pTr = ring("pT", 1, [128, 1024], BF16, psum=True)
        pmr = ring("pm", 2, [128, 512], F32, psum=True)
        pTfr = ring("pTf", 1, [128, 1024], F32, psum=True)
        plr = ring("pl", 1, [128, N_EXP], F32, psum=True)
        lnc = 0
        for t in range(T):
            g = 0 if t < 16 else 1
            y, yk = yr.next(); P.dma("sync", y[:], yt[t], writes=[yk])
            x, xk = xr.next(); P.dma("sync", x[:], xt[t], writes=[xk])
            yb, ybk = ybr.next(); P.copy("gpsimd", yb[:], y[:], reads=[yk], writes=[ybk])
            pT, pTk = pTr.next()
            for kc in range(8):
                P.tr(pT[:, kc * 128:(kc + 1) * 128], yb[:, kc * 128:(kc + 1) * 128], idb[:], reads=[ybk, "idb"], writes=[pTk])
            yT, yTk = yTr.next(); P.copy("scalar", yT[:], pT[:], reads=[pTk], writes=[yTk])
            tmp, tmpk = tmpr.next()
            for hf in range(2):
                pm, pmk = pmr.next()
                for kc in range(8):
                    P.mm(pm[:], yT[:, kc * 128:(kc + 1) * 128], wbf[:, kc, hf * 512:(hf + 1) * 512], kc == 0, kc == 7,
                         reads=[yTk, wkeys[kc]], writes=[pmk])
                P.tt("vector", tmp[:, hf * 512:(hf + 1) * 512], pm[:], rowt[:, g, 0, hf * 512:(hf + 1) * 512], ALU.mult,
                     reads=[pmk, f"row{g}0"], writes=[tmpk + str(hf)])
            r, rk = rr.next()
            P.stt("vector", r[:], x[:], ALPHA, tmp[:], ALU.mult, ALU.add, reads=[xk, tmpk + "0", tmpk + "1"], writes=[rk])
            xn, xnk = xnr.next(); st, _ = str_.next(); mv, _ = mvr.next(); rs, _ = rsr.next()
            emit_layernorm(P, r[:], rk, xn[:], xnk, st, mv, rs, f"lnA{lnc % 4}"); lnc += 1
            x1a, x1ak = x1ar.next(); x1, x1k = x1r.next()
            P.tt("gpsimd", x1a[:], xn[:], lnt[:, 0, :], ALU.mult, reads=[xnk, "ln0"], writes=[x1ak])
            P.tt("gpsimd", x1[:], x1a[:], lnt[:, 1, :], ALU.add, reads=[x1ak, "ln1"], writes=[x1k])
            P.dma("gpsimd", x1o[t], x1[:], reads=[x1k], writes=[x1k + "d"], final=True)
            xn2, xn2k = xn2r.next(); st, _ = str_.next(); mv, _ = mvr.next(); rs, _ = rsr.next()
            emit_layernorm(P, x1[:], x1k, xn2[:], xn2k, st, mv, rs, f"lnA{lnc % 4}"); lnc += 1
            h2a, h2ak = h2ar.next(); h2f, h2fk = h2fr.next(); h2b, h2bk = h2br.next()
            P.tt("gpsimd", h2a[:], xn2[:], rowt[:, g, 1, :], ALU.mult, reads=[xn2k, f"row{g}1"], writes=[h2ak])
            P.tt("vector", h2f[:], h2a[:], rowt[:, g, 2, :], ALU.add, reads=[h2ak, f"row{g}2"], writes=[h2fk])
            P.copy("scalar", h2b[:], h2f[:], reads=[h2fk], writes=[h2bk])
            P.dma("sync", h2o[t], h2b[:], reads=[h2bk], writes=[h2bk + "d"], final=True)
            pTf, pTfk = pTfr.next()
            for kc in range(8):
                P.tr(pTf[:, kc * 128:(kc + 1) * 128], h2f[:, kc * 128:(kc + 1) * 128], idf[:], reads=[h2fk, "idf"], writes=[pTfk])
            h2T, h2Tk = h2Tr.next()
            P.copy("scalar", h2T[:, 0:512], pTf[:, 0:512], reads=[pTfk], writes=[h2Tk + "a"])
            P.copy("vector", h2T[:, 512:1024], pTf[:, 512:1024], reads=[pTfk], writes=[h2Tk + "b"])
            pl, plk = plr.next()
            for kc in range(8):
                P.mm(pl[:], h2T[:, kc * 128:(kc + 1) * 128], rw[:, kc, :], kc == 0, kc == 7,
                     reads=[h2Tk + "a", h2Tk + "b", "rw"], writes=[plk])
            lg, lgk = lgr.next(); ex, exk = exr.next(); af, afk = afr.next(); sm, smk = smr.next()
            P.tt("vector", lg[:], pl[:], rb[:], ALU.add, reads=[plk, "rb"], writes=[lgk])
            P.gen("vector", lambda e, sm=sm, lg=lg: e.reduce_max(out=sm[:, 0:1], in_=lg[:], axis=AX.X), reads=[lgk], writes=[smk + "m"])
            P.ts("vector", sm[:, 1:2], sm[:, 0:1], -1.0, None, ALU.mult, reads=[smk + "m"], writes=[smk + "n"])
            P.act(ex[:], lg[:], AF.Exp, reads=[lgk, smk + "n"], writes=[exk, smk + "s"], bias=sm[:, 1:2], scale=1.0,
                  accum_out=sm[:, 2:3])
            P.gen("vector", lambda e, sm=sm: e.reciprocal(out=sm[:, 3:4], in_=sm[:, 2:3]), reads=[smk + "s"], writes=[smk + "r"])
            P.ts("vector", af[:], ex[:], sm[:, 3:4], None, ALU.mult, reads=[exk, smk + "r"], writes=[afk])
            P.dma("gpsimd", affo[t], af[:], reads=[afk], writes=[afk + "d"], final=True)
        P.emit()
    return nc


def tok_shard(lat, ctx, i):
    B, n, D = lat.shape
    seg = n // 4
    b, s = i // 4, i % 4
    out = np.zeros((K1_TILES * 128, D), lat.dtype)
    out[:seg] = lat[b, s * seg:(s + 1) * seg]
    out[seg:seg + 64] = ctx.reshape(-1, D)[i * 64:(i + 1) * 64]
    return out.reshape(K1_TILES, 128, D)


def tok_unshard(parts, B, n, nctx):
    D = parts[0].shape[-1]
    seg = n // 4
    lat = np.zeros((B, n, D), parts[0].dtype)
    ctx = np.zeros((B * nctx, D), parts[0].dtype)
    for i in range(NCORES):
        b, s = i // 4, i % 4
        p = parts[i].reshape(K1_TILES * 128, D)
        lat[b, s * seg:(s + 1) * seg] = p[:seg]
        ctx[i * 64:(i + 1) * 64] = p[seg:seg + 64]
    return lat, ctx.reshape(B, nctx, D)


def run_k5(ycat_l, ycat_c, x, ctx, mod_l, w_out_l, ln_w, ln_b, router_w_l, router_b_l):
    nc = build_k5()
    B, n, D = x.shape
    ident = np.eye(128, dtype=np.float32)
    wout = lay_w(w_out_l, 8)
    rw = lay_w(router_w_l, 8)
    lnr = np.ascontiguousarray(np.stack([ln_w, ln_b]))
    in_maps = []
    for i in range(NCORES):
        b = i // 4
        rows = np.stack([np.stack([mod_l[m, 2048:3072], mod_l[m, 4096:5120], mod_l[m, 3072:4096]]) for m in (b, 2)])
        in_maps.append({"yt": tok_shard(ycat_l, ycat_c, i), "xt": tok_shard(x, ctx, i), "wout": wout,
                        "rows": np.ascontiguousarray(rows), "lnr": lnr, "rw": rw,
                        "rb": np.ascontiguousarray(router_b_l), "ident": ident})
    res = _run(nc, in_maps)
    nctx = ctx.shape[1]
    x1, c1 = tok_unshard([r["o_x1"] for r in res.results], B, n, nctx)
    h2, h2c = tok_unshard([r["o_h2"] for r in res.results], B, n, nctx)
    aff, affc = tok_unshard([r["o_aff"] for r in res.results], B, n, nctx)
    return x1, c1, h2, h2c, aff, affc


K6_ITERS = 26


def build_k6(F_lat, k_lat, F_ctx, k_ctx):
    nc = bass.Bass("TRN2", target_bir_lowering=False)
    R = 32
    ald = nc.dram_tensor("al", [R, F_lat], F32, kind="ExternalInput").ap()
    acd = nc.dram_tensor("ac", [R, F_ctx], F32, kind="ExternalInput").ap()
    thro = nc.dram_tensor("thr", [R, 2], F32, kind="ExternalOutput").ap()
    with ExitStack() as es:
        P = Prog(nc, es)
        sb, ps, ring = mk_alloc(nc, es)
        res = sb("res", [R, 2])
        for pi, (src, F, k) in enumerate(((ald, F_lat, k_lat), (acd, F_ctx, k_ctx))):
            A = sb(f"A{pi}", [R, F])
            junk = sb(f"junk{pi}", [R, F], BF16)
            sc = sb(f"sc{pi}", [R, 8])
            lo, hi, mid, cnt, cond, t1, t2 = (sc[:, j:j + 1] for j in range(7))
            kk = f"p{pi}"
            P.dma("sync", A[:], src, writes=[kk + "A"])
            P.memset("vector", lo, 0.0, writes=[kk + "lo"])
            P.memset("vector", hi, 1.0, writes=[kk + "hi"])
            for it in range(K6_ITERS):
                P.tt("vector", mid, lo, hi, ALU.add, reads=[kk + "lo", kk + "hi"], writes=[kk + "mid"])
                P.ts("vector", mid, mid, 0.5, None, ALU.mult, reads=[kk + "mid"], writes=[kk + "mid"])
                P.ts("vector", junk[:], A[:], mid, None, ALU.is_ge, ALU.add, reads=[kk + "A", kk + "mid"],
                     writes=[kk + "junk", kk + "cnt"], accum_out=cnt)
                P.ts("vector", cond, cnt, float(k) - 0.5, None, ALU.is_ge, reads=[kk + "cnt"], writes=[kk + "cond"])
                P.tt("vector", t1, cond, mid, ALU.mult, reads=[kk + "cond", kk + "mid"], writes=[kk + "t1"])
                P.stt("vector", t2, cond, 2.0, mid, ALU.mult, ALU.add, reads=[kk + "cond", kk + "mid"], writes=[kk + "t2"])
                P.tt("vector", lo, lo, t1, ALU.max, reads=[kk + "lo", kk + "t1"], writes=[kk + "lo"])
                P.tt("vector", hi, hi, t2, ALU.min, reads=[kk + "hi", kk + "t2"], writes=[kk + "hi"])
            P.copy("vector", res[:, pi:pi + 1], lo, reads=[kk + "lo"], writes=[f"res{pi}"])
        P.dma("sync", thro, res[:], reads=["res0", "res1"], final=True)
        P.emit()
    return nc


def run_k6(aff, affc):
    B, n, E = aff.shape
    ncx = affc.shape[1]
    nc = build_k6(n, 2 * n // E, ncx, 2 * ncx // E)
    al = np.ascontiguousarray(aff.transpose(0, 2, 1).reshape(B * E, n))
    ac = np.ascontiguousarray(affc.transpose(0, 2, 1).reshape(B * E, ncx))
    res = _run(nc, [{"al": al, "ac": ac} for _ in range(NCORES)])
    thr = res.results[0]["thr"]
    return thr[:, 0].reshape(B, E), thr[:, 1].reshape(B, E)


D_FF = 2816
NFC = D_FF // 128
K7_TOK = K1_TILES * 128


def build_k7(n_exp=N_EXP):
    nc = bass.Bass("TRN2", target_bir_lowering=False)
    T = K1_TILES
    h2Td = nc.dram_tensor("h2T", [128, 8, K7_TOK], BF16, kind="ExternalInput").ap()
    affd = nc.dram_tensor("aff", [T, 128, N_EXP], F32, kind="ExternalInput").ap()
    thrd = nc.dram_tensor("thr", [2, N_EXP], F32, kind="ExternalInput").ap()
    x1d = nc.dram_tensor("i_x1", [T, 128, 1024], F32, kind="ExternalInput").ap()
    rows = nc.dram_tensor("rows", [2, 1024], F32, kind="ExternalInput").ap()
    lnr = nc.dram_tensor("lnr", [2, 1024], F32, kind="ExternalInput").ap()
    wgd = nc.dram_tensor("wg", [n_exp, 128, 8, D_FF], F32, kind="ExternalInput").ap()
    wud = nc.dram_tensor("wu", [n_exp, 128, 8, D_FF], F32, kind="ExternalInput").ap()
    wdd = nc.dram_tensor("wd", [n_exp, 128, NFC, 1024], F32, kind="ExternalInput").ap()
    x2o = nc.dram_tensor("o_x2", [T, 128, 1024], F32, kind="ExternalOutput").ap()
    with ExitStack() as es:
        P = Prog(nc, es)
        sb, ps, ring = mk_alloc(nc, es)
        h2T = sb("h2Ts", [128, 8, K7_TOK], BF16)
        for kc in range(8):
            P.dma("sync", h2T[:, kc, :], h2Td[:, kc, :], writes=["h2T"] if kc == 7 else [f"h2T_{kc}"])
        h2keys = ["h2T"] + [f"h2T_{kc}" for kc in range(7)]
        thrb = sb("thrb", [128, 2, N_EXP])
        for g in range(2):
            P.dma("sync", thrb[:, g, :], bcast_rows(thrd[g], 128), writes=[f"thr{g}"])
        wgt = sb("wgt", [128, T, N_EXP])
        msk = sb("msk", [128, T, N_EXP])
        for t in range(T):
            g = 0 if t < 16 else 1
            P.dma("sync", wgt[:, t, :], affd[t], writes=[f"aff{t}"])
            P.tt("vector", msk[:, t, :], wgt[:, t, :], thrb[:, g, :], ALU.is_ge, reads=[f"aff{t}", f"thr{g}"], writes=[f"msk{t}"])
            P.tt("vector", wgt[:, t, :], wgt[:, t, :], msk[:, t, :], ALU.mult, reads=[f"aff{t}", f"msk{t}"], writes=[f"aff{t}"])
        acc = sb("acc", [128, T, 1024])
        for t in range(T):
            P.memset("gpsimd", acc[:, t, :], 0.0, writes=[f"acc{t}a", f"acc{t}b"])
        FG = 4
        wgr = ring("wgb", 2, [128, 8, FG * 128], BF16)
        wur = ring("wub", 2, [128, 8, FG * 128], BF16)
        wdr = ring("wdb", 2, [128, FG, 1024], BF16)
        stg = ring("stg", 4, [128, 1024])
        actr = ring("actT", 2, [128, FG, 512], BF16)
        sgr = ring("sg", 2, [128, 512])
        pgr = ring("pg", 2, [128, 512], F32, psum=True)
        pur = ring("pu", 2, [128, 512], F32, psum=True)
        pyr = ring("py", 2, [128, 512], F32, psum=True)
        tgs = [(s, min(512, K7_TOK - s)) for s in range(0, K7_TOK, 512)]
        ci = 0
        for e in range(n_exp):
            for f0 in range(0, NFC, FG):
                nf = min(FG, NFC - f0)
                wgb, wgk = wgr.next(); wub, wuk = wur.next(); wdb, wdk = wdr.next()
                for (src, dst, dk) in ((wgd, wgb, wgk), (wud, wub, wuk)):
                    for kp in range(0, 8, 2):
                        st, sk = stg.next()
                        q = "sync"
                        sv = st[:, 0:2 * nf * 128].rearrange("p (a f) -> p a f", a=2)
                        P.dma(q, sv, src[e, :, kp:kp + 2, f0 * 128:(f0 + nf) * 128], writes=[sk])
                        P.copy("gpsimd", dst[:, kp:kp + 2, 0:nf * 128], sv, reads=[sk], writes=[dk + f"k{kp}"])
                        ci += 1
                for fc in range(nf):
                    st, sk = stg.next()
                    P.dma("sync", st[:], wdd[e, :, f0 + fc, :], writes=[sk])
                    P.copy("gpsimd", wdb[:, fc, :], st[:], reads=[sk], writes=[wdk + f"f{fc}"])
                    ci += 1
                for (s0, ns) in tgs:
                    actT, ak = actr.next()
                    for fc in range(nf):
                        pg, pgk = pgr.next(); pu, puk = pur.next()
                        for kc in range(8):
                            P.mm(pg[:, 0:ns], wgb[:, kc, fc * 128:(fc + 1) * 128], h2T[:, kc, s0:s0 + ns], kc == 0, kc == 7,
                                 reads=h2keys + [wgk + f"k{kc - kc % 2}"], writes=[pgk])
                        for kc in range(8):
                            P.mm(pu[:, 0:ns], wub[:, kc, fc * 128:(fc + 1) * 128], h2T[:, kc, s0:s0 + ns], kc == 0, kc == 7,
                                 reads=h2keys + [wuk + f"k{kc - kc % 2}"], writes=[puk])
                        sg, sgk = sgr.next()
                        P.act(sg[:, 0:ns], pg[:, 0:ns], AF.Silu, reads=[pgk], writes=[sgk])
                        P.tt("vector", actT[:, fc, 0:ns], pu[:, 0:ns], sg[:, 0:ns], ALU.mult, reads=[puk, sgk], writes=[ak + f"f{fc}"])
                    for tt in range(s0 // 128, (s0 + ns) // 128):
                        for hf in range(2):
                            py, pyk = pyr.next()
                            for fc in range(nf):
                                P.mm(py[:], actT[:, fc, tt * 128 - s0:(tt + 1) * 128 - s0], wdb[:, fc, hf * 512:(hf + 1) * 512],
                                     fc == 0, fc == nf - 1, reads=[ak + f"f{fc}", wdk + f"f{fc}"], writes=[pyk])
                            ah = f"acc{tt}" + "ab"[hf]
                            P.stt("vector", acc[:, tt, hf * 512:(hf + 1) * 512], py[:], wgt[:, tt, e:e + 1],
                                  acc[:, tt, hf * 512:(hf + 1) * 512], ALU.mult, ALU.add, reads=[pyk, f"aff{tt}", ah], writes=[ah])
        rowt = sb("rowt", [128, 2, 1024])
        lnt = sb("lnt", [128, 2, 1024])
        for j in range(2):
            P.dma("sync", rowt[:, j, :], bcast_rows(rows[j], 128), writes=[f"row{j}"])
            P.dma("sync", lnt[:, j, :], bcast_rows(lnr[j], 128), writes=[f"ln{j}"])
        xr = ring("x", 2, [128, 1024])
        str_ = ring("st", 2, [128, 2, 6]); mvr = ring("mv", 2, [128, 2]); rsr = ring("rs", 2, [128, 1])
        for t in range(T):
            g = 0 if t < 16 else 1
            ak2 = [f"acc{t}a", f"acc{t}b"]
            x, xk = xr.next(); P.dma("sync", x[:], x1d[t], writes=[xk])
            P.tt("gpsimd", acc[:, t, :], acc[:, t, :], rowt[:, g, :], ALU.mult, reads=ak2 + [f"row{g}"], writes=ak2)
            P.stt("vector", acc[:, t, :], x[:], ALPHA, acc[:, t, :], ALU.mult, ALU.add, reads=[xk] + ak2, writes=ak2)
            st, _ = str_.next(); mv, _ = mvr.next(); rs, _ = rsr.next()
            emit_layernorm(P, acc[:, t, :], ak2[0], x[:], xk, st, mv, rs, f"lnB{t % 2}")
            P.tt("gpsimd", acc[:, t, :], x[:], lnt[:, 0, :], ALU.mult, reads=[xk, "ln0"], writes=ak2)
            P.tt("gpsimd", x[:], acc[:, t, :], lnt[:, 1, :], ALU.add, reads=ak2 + ["ln1"], writes=[xk])
            P.dma("sync", x2o[t], x[:], reads=[xk], writes=[xk], final=True)
        P.emit()
    return nc


def run_k7(h2, h2c, aff, affc, thr, thrc, x1, c1, mod_l, ln_w, ln_b, wg, wu, wd, n_exp=N_EXP):
    nc = build_k7(n_exp)
    B, n, D = x1.shape
    nctx = c1.shape[1]
    wgl = np.ascontiguousarray(wg[:n_exp].reshape(n_exp, 8, 128, D_FF).transpose(0, 2, 1, 3))
    wul = np.ascontiguousarray(wu[:n_exp].reshape(n_exp, 8, 128, D_FF).transpose(0, 2, 1, 3))
    wdl = np.ascontiguousarray(wd[:n_exp].reshape(n_exp, NFC, 128, D).transpose(0, 2, 1, 3))
    lnr = np.ascontiguousarray(np.stack([ln_w, ln_b]))
    in_maps = []
    for i in range(NCORES):
        b = i // 4
        h2s = tok_shard(h2, h2c, i).reshape(K7_TOK, D)
        h2T = np.ascontiguousarray(h2s.T.reshape(8, 128, K7_TOK).transpose(1, 0, 2))
        rows = np.ascontiguousarray(np.stack([mod_l[b, 5120:6144], mod_l[2, 5120:6144]]))
        in_maps.append({"h2T": h2T, "aff": tok_shard(aff, affc, i), "thr": np.ascontiguousarray(np.stack([thr[b], thrc[b]])),
                        "i_x1": tok_shard(x1, c1, i), "rows": rows, "lnr": lnr, "wg": wgl, "wu": wul, "wd": wdl})
    res = _run(nc, in_maps)
    return tok_unshard([r["o_x2"] for r in res.results], B, n, nctx)


RMS_EPS = 1e-6
NKEY = 8192 + 256
NKT = NKEY // 128
QCOLS = K1_TILES * 512


def build_k3():
    nc = bass.Bass("TRN2", target_bir_lowering=False)
    T = K1_TILES
    qd = nc.dram_tensor("qT", [128, QCOLS], F32, kind="ExternalInput").ap()
    kd = nc.dram_tensor("kT", [128, NKEY], F32, kind="ExternalInput").ap()
    vd = nc.dram_tensor("v", [128, NKT, 2, 64], F32, kind="ExternalInput").ap()
    cqd = nc.dram_tensor("cosq", [128, 16 * 512], F32, kind="ExternalInput").ap()
    sqd = nc.dram_tensor("sinq", [128, 16 * 512], F32, kind="ExternalInput").ap()
    ckd = nc.dram_tensor("cosk", [128, 8192], F32, kind="ExternalInput").ap()
    skd = nc.dram_tensor("sink", [128, 8192], F32, kind="ExternalInput").ap()
    cst = nc.dram_tensor("cst", [128, 258], F32, kind="ExternalInput").ap()
    yo = nc.dram_tensor("o_ya", [T, 128, 512], F32, kind="ExternalOutput").ap()
    with ExitStack() as es:
        P = Prog(nc, es)
        sb, ps, ring = mk_alloc(nc, es)
        cs = sb("cs", [128, 258])
        P.dma("sync", cs[:], cst, writes=["cs"])
        Rm, onesb, qw2, kw2 = cs[:, 0:128], cs[:, 128:256], cs[:, 256:257], cs[:, 257:258]
        bq = sb("bq", [128, 2])
        P.memset("vector", bq[:, 0:1], 64.0 * RMS_EPS, writes=["bq0"])
        P.memset("vector", bq[:, 1:2], RMS_EPS, writes=["bq1"])
        qr = sb("qr", [128, QCOLS], BF16)
        kr = sb("kr", [128, NKEY], BF16)
        vst = ring("vst", 2, [128, 2, 64])
        vaug = sb("vaug", [128, NKT, 2, 65], BF16)
        P.memset("gpsimd", vaug[:], 1.0, writes=["vaug_init"])
        for kt in range(NKT):
            v, vk = vst.next()
            P.dma("sync", v[:], vd[:, kt], writes=[vk])
            P.copy("gpsimd", vaug[:, kt, :, 0:64], v[:], reads=[vk, "vaug_init"], writes=[f"vaug{kt}"])
        xr = ring("px", 2, [128, 512]); sqr = ring("psq", 2, [128, 512]); sdr = ring("psd", 2, [128, 512])
        xnr = ring("pxn", 2, [128, 512]); cr = ring("pc", 2, [128, 512]); sr = ring("psn", 2, [128, 512])
        t1r = ring("pt1", 2, [128, 512]); t2r = ring("pt2", 2, [128, 512])
        pssr = ring("pss", 1, [128, 512], F32, psum=True)
        prot = ring("prot", 1, [128, 512], F32, psum=True)

        def prep(src, dst, dkey, ncols, nrope, w2, bcol, scale, cosd, sind):
            for c0 in range(0, ncols, 512):
                n = min(512, ncols - c0)
                x, xk = xr.next(); P.dma("sync", x[:, 0:n], src[:, c0:c0 + n], writes=[xk])
                sq, sqk = sqr.next(); P.tt("gpsimd", sq[:, 0:n], x[:, 0:n], x[:, 0:n], ALU.mult, reads=[xk], writes=[sqk])
                pss, pssk = pssr.next()
                P.mm(pss[:, 0:n], onesb, sq[:, 0:n], True, True, reads=[sqk, "cs"], writes=[pssk])
                sd, sdk = sdr.next()
                P.act(sd[:, 0:n], pss[:, 0:n], AF.Sqrt, reads=[pssk, f"bq{bcol}"], writes=[sdk], bias=bq[:, bcol:bcol + 1], scale=scale)
                P.gen("vector", lambda e, sd=sd, n=n: e.reciprocal(out=sd[:, 0:n], in_=sd[:, 0:n]), reads=[sdk], writes=[sdk])
                xn, xnk = xnr.next()
                P.stt("vector", xn[:, 0:n], x[:, 0:n], w2, sd[:, 0:n], ALU.mult, ALU.mult, reads=[xk, sdk, "cs"], writes=[xnk])
                dk = f"{dkey}{c0 // 512}"
                if c0 < nrope:
                    pr, prk = prot.next()
                    P.mm(pr[:, 0:n], Rm, xn[:, 0:n], True, True, reads=[xnk, "cs"], writes=[prk])
                    c, ck = cr.next(); P.dma("sync", c[:, 0:n], cosd[:, c0:c0 + n], writes=[ck])
                    s, sk = sr.next(); P.dma("sync", s[:, 0:n], sind[:, c0:c0 + n], writes=[sk])
                    t1, t1k = t1r.next(); P.tt("gpsimd", t1[:, 0:n], xn[:, 0:n], c[:, 0:n], ALU.mult, reads=[xnk, ck], writes=[t1k])
                    t2, t2k = t2r.next(); P.tt("vector", t2[:, 0:n], pr[:, 0:n], s[:, 0:n], ALU.mult, reads=[prk, sk], writes=[t2k])
                    P.tt("gpsimd", dst[:, c0:c0 + n], t1[:, 0:n], t2[:, 0:n], ALU.add, reads=[t1k, t2k], writes=[dk])
                else:
                    P.copy("gpsimd", dst[:, c0:c0 + n], xn[:, 0:n], reads=[xnk], writes=[dk])

        prep(kd, kr, "kr", NKEY, 8192, kw2, 1, 1.0 / 64.0, ckd, skd)
        prep(qd, qr, "qr", QCOLS, 16 * 512, qw2, 0, 1.0, cqd, sqd)
        pstr = ring("pst", 2, [128, 512], F32, psum=True)
        por = [ring(f"po{j}_", 1, [128, 65], F32, psum=True) for j in range(4)]
        ptr = ring("PT", 3, [128, 512], BF16)
        yr = ring("yo", 2, [128, 512])
        rcr = ring("rc", 2, [128, 4])
        for qt in range(T):
            kts = list(range(NKT)) if qt < 16 else [64, 65]
            y, yk = yr.next()
            for g in range(2):
                pos = [por[j].next() for j in range(4)]
                for ii, kt in enumerate(kts):
                    pst, pstk = pstr.next()
                    P.mm(pst[:], kr[g * 64:(g + 1) * 64, kt * 128:(kt + 1) * 128], qr[g * 64:(g + 1) * 64, qt * 512:(qt + 1) * 512],
                         True, True, reads=[f"kr{kt // 4}", f"qr{qt}"], writes=[pstk])
                    PT, PTk = ptr.next()
                    P.act(PT[:], pst[:], AF.Exp, reads=[pstk], writes=[PTk])
                    for j in range(4):
                        P.mm(pos[j][0][:], PT[:, j * 128:(j + 1) * 128], vaug[:, kt, g, :], ii == 0, ii == len(kts) - 1,
                             reads=[PTk, f"vaug{kt}"], writes=[pos[j][1]])
                rc, rck = rcr.next()
                for j in range(4):
                    po, pok = pos[j]
                    P.gen("vector", lambda e, rc=rc, po=po, j=j: e.reciprocal(out=rc[:, j:j + 1], in_=po[:, 64:65]), reads=[pok], writes=[rck + str(j)])
                    c0 = (g * 4 + j) * 64
                    P.ts("vector", y[:, c0:c0 + 64], po[:, 0:64], rc[:, j:j + 1], None, ALU.mult, reads=[pok, rck + str(j)], writes=[yk + f"{g}{j}"])
            P.dma("sync", yo[qt], y[:], reads=[yk + f"{g}{j}" for g in range(2) for j in range(4)], writes=[yk + "d"], final=True)
        P.emit()
    return nc


def rope_tables(n_lat, grid_w=64, theta=10000.0, hd=64):
    nf = hd // 4
    t = np.arange(n_lat)
    row = (t // grid_w).astype(np.float32)
    col = (t % grid_w).astype(np.float32)
    inv = (theta ** (-np.arange(nf, dtype=np.float32) / nf)).astype(np.float32)
    ar = row[:, None] * inv
    ac = col[:, None] * inv
    ang = np.concatenate([ar, ar, ac, ac], axis=-1)
    return np.cos(ang).astype(np.float32), np.sin(ang).astype(np.float32)


def rope_rot_matrix():
    R = np.zeros((64, 64), np.float32)
    for a in range(2):
        for f in range(16):
            R[a * 32 + 16 + f, a * 32 + f] = -1.0
            R[a * 32 + f, a * 32 + 16 + f] = 1.0
    return R


def run_k3(P_lat, P_ctx, qw, kw):
    nc = build_k3()
    B, n, _ = P_lat.shape
    nctx = P_ctx.shape[1]
    aq_l, ak_l, av_l = P_lat[..., 1040:1552], P_lat[..., 1552:1680], P_lat[..., 1680:1808]
    aq_c, ak_c, av_c = P_ctx[..., 1040:1552], P_ctx[..., 1552:1680], P_ctx[..., 1680:1808]
    cos, sin = rope_tables(n)
    R = rope_rot_matrix()
    Rm = np.zeros((128, 128), np.float32); Rm[:64, :64] = R; Rm[64:, 64:] = R
    ob = np.zeros((128, 128), np.float32); ob[:64, :64] = 1; ob[64:, 64:] = 1
    cst = np.concatenate([Rm, ob, np.tile(qw, 2)[:, None], np.tile(kw, 2)[:, None]], 1).astype(np.float32)
    cosk = np.ascontiguousarray(np.tile(cos.T, (2, 1))); sink = np.ascontiguousarray(np.tile(sin.T, (2, 1)))
    seg = n // 4
    in_maps = []
    for i in range(NCORES):
        b, s = i // 4, i % 4
        q = tok_shard(aq_l, aq_c, i).reshape(K1_TILES, 128, 2, 4, 64)
        qT = np.ascontiguousarray(q.transpose(2, 4, 0, 3, 1)).reshape(128, QCOLS)
        k = np.concatenate([ak_l[b], ak_c[b]], 0).reshape(NKEY, 2, 64)
        kT = np.ascontiguousarray(k.transpose(1, 2, 0)).reshape(128, NKEY)
        v = np.concatenate([av_l[b], av_c[b]], 0).reshape(NKT, 128, 2, 64)
        vv = np.ascontiguousarray(v.transpose(1, 0, 2, 3))
        cq = cos[s * seg:(s + 1) * seg].reshape(16, 128, 64)
        sq = sin[s * seg:(s + 1) * seg].reshape(16, 128, 64)
        cq = np.broadcast_to(cq.transpose(2, 0, 1)[None, :, :, None, :], (2, 64, 16, 4, 128)).reshape(128, 16 * 512)
        sq = np.broadcast_to(sq.transpose(2, 0, 1)[None, :, :, None, :], (2, 64, 16, 4, 128)).reshape(128, 16 * 512)
        in_maps.append({"qT": qT, "kT": kT, "v": vv, "cosq": np.ascontiguousarray(cq), "sinq": np.ascontiguousarray(sq),
                        "cosk": cosk, "sink": sink, "cst": cst})
    res = _run(nc, in_maps)
    return tok_unshard([r["o_ya"] for r in res.results], B, n, nctx)


NCH = NKT
NSEQ = NKEY
MASK_NEG = -30000.0


def build_k2():
    nc = bass.Bass("TRN2", target_bir_lowering=False)
    qpd = nc.dram_tensor("qpT", [64, NSEQ], F32, kind="ExternalInput").ap()
    kpd = nc.dram_tensor("kpT", [64, NSEQ], F32, kind="ExternalInput").ap()
    vd = nc.dram_tensor("v", [128, NCH, 64], F32, kind="ExternalInput").ap()
    od = nc.dram_tensor("og", [128, NCH, 64], F32, kind="ExternalInput").ap()
    gd = nc.dram_tensor("g4", [128, NCH, 4], F32, kind="ExternalInput").ap()
    gbd = nc.dram_tensor("gb", [NCH * 4], F32, kind="ExternalInput").ap()
    cwd = nc.dram_tensor("cw", [64, 8], F32, kind="ExternalInput").ap()
    nwd = nc.dram_tensor("nw", [64], F32, kind="ExternalInput").ap()
    cstd = nc.dram_tensor("cst", [128, 6, 128], F32, kind="ExternalInput").ap()
    yo = nc.dram_tensor("o_ym", [128, NCH, 64], F32, kind="ExternalOutput").ap()
    with ExitStack() as es:
        P = Prog(nc, es)
        sb, ps, ring = mk_alloc(nc, es)
        cst = sb("cst_s", [128, 6, 128])
        P.dma("sync", cst[:], cstd, writes=["cst"])
        Lm = [cst[:, 0, :], cst[:, 1, :]]
        ones, ident = cst[:, 2, :], cst[:, 3, :]
        mneg = [cst[:, 4, :], cst[:, 5, :]]
        idb = sb("idb", [128, 128], BF16)
        P.copy("vector", idb[:], ident, reads=["cst"], writes=["idb"])
        one1 = sb("one1", [128, 1])
        P.memset("vector", one1[:], 1.0, writes=["one1"])
        cw = sb("cw_s", [64, 8])
        P.dma("sync", cw[:], cwd, writes=["cw"])
        banks = [ps(f"bank{i}", [128, 512], F32) for i in range(8)]
        xin = sb("xin", [64, NSEQ]); cacc = sb("cacc", [64, NSEQ])
        qT = sb("qT_s", [64, NSEQ], BF16); kT = sb("kT_s", [64, NSEQ], BF16)
        segs = [(0, 256), (256, NSEQ)]
        for wi, (src, dst, post) in enumerate(((qpd, qT, 0.125), (kpd, kT, 1.0))):
            o = wi * 4
            half = NSEQ // 2
            P.dma("sync", xin[:, 0:half], src[:, 0:half], writes=["xin_a"])
            P.dma("gpsimd", xin[:, half:], src[:, half:], writes=["xin_b"])
            P.ts("vector", cacc[:], xin[:], cw[:, o + 1:o + 2], cw[:, o + 3:o + 4], ALU.mult, ALU.add,
                 reads=["xin_a", "xin_b", "cw"], writes=["cacc"])
            for (a, b) in segs:
                P.stt("vector", cacc[:, a + 1:b], xin[:, a:b - 1], cw[:, o:o + 1], cacc[:, a + 1:b], ALU.mult, ALU.add,
                      reads=["xin_a", "xin_b", "cw", "cacc"], writes=["cacc"])
                P.stt("vector", cacc[:, a:b - 1], xin[:, a + 1:b], cw[:, o + 2:o + 3], cacc[:, a:b - 1], ALU.mult, ALU.add,
                      reads=["xin_a", "xin_b", "cw", "cacc"], writes=["cacc"])
            P.act(cacc[:], cacc[:], AF.Silu, reads=["cacc"], writes=["cacc"])
            P.ts("gpsimd", dst[:], cacc[:], post, None, ALU.mult, reads=["cacc"], writes=[f"T{wi}"])
            P.memset("vector", xin[:, 0:1], 0.0, writes=["xin_a", "xin_b"]) if wi == 0 else None
        ktok = sb("ktok", [128, NCH, 64], BF16)
        for c0 in range(0, NCH, 8):
            n = min(8, NCH - c0)
            pb = banks[7]
            for c in range(c0, c0 + n):
                pt = pb[:, :].bitcast(BF16)[:, (c - c0) * 64:(c - c0 + 1) * 64]
                P.tr(pt, kT[:, c * 128:(c + 1) * 128], idb[0:64, 0:64], reads=["T1", "idb"], writes=["bank7"])
            P.copy("vector", ktok[:, c0:c0 + n, :], pb[:, :].bitcast(BF16)[:, 0:n * 64].rearrange("p (c d) -> p c d", d=64),
                   reads=["bank7"], writes=["ktok"])
        vf = sb("vf", [128, NCH, 65]); vb = sb("vb", [128, NCH, 65], BF16)
        P.memset("gpsimd", vf[:], 1.0, writes=["vf"])
        vtmp = sb("vtmp", [128, NCH, 64])
        P.dma("sync", vtmp[:], vd, writes=["vtmp"])
        P.copy("gpsimd", vf[:, :, 0:64], vtmp[:], reads=["vtmp", "vf"], writes=["vf"])
        P.copy("gpsimd", vb[:], vf[:], reads=["vf"], writes=["vb"])
        G = sb("G", [128, NCH, 4]); GB = sb("GB", [128, NCH, 4])
        P.dma("sync", G[:], gd, writes=["G"])
        P.dma("sync", GB[:].rearrange("p c g -> p (c g)"), bcast_rows(gbd, 128), writes=["GB"])
        P.tt("vector", G[:], G[:], GB[:], ALU.add, reads=["G", "GB"], writes=["G"])
        LF = sb("LF", [128, 2, NCH]); LI = sb("LI", [128, 2, NCH]); TA = sb("TA", [128, 2, NCH]); TB = sb("TB", [128, 2, NCH])
        for dd in range(2):
            P.copy("vector", LI[:, dd, :], G[:, :, 2 * dd], reads=["G"], writes=[f"LI{dd}"])
            P.copy("vector", TA[:, dd, :], G[:, :, 2 * dd + 1], reads=["G"], writes=["TA"])
        P.act(TB[:], TA[:], AF.Abs, reads=["TA"], writes=["TB"])
        P.act(TB[:], TB[:], AF.Exp, reads=["TB"], writes=["TB"], scale=-1.0)
        P.act(TB[:], TB[:], AF.Ln, reads=["TB", "one1"], writes=["TB"], bias=one1[:, 0:1], scale=1.0)
        P.ts("vector", TA[:], TA[:], 0.0, None, ALU.min, reads=["TA"], writes=["TA"])
        P.tt("vector", LF[:], TA[:], TB[:], ALU.subtract, reads=["TA", "TB"], writes=["LF"])
        BC = sb("BC", [128, 2, NCH]); TOT = sb("TOT", [128, 2, NCH]); AA = sb("AA", [128, 2, NCH])
        BD = sb("BD", [128, 2, NCH]); WW = sb("WW", [128, 2, NCH]); DEC = sb("DEC", [128, 2, NCH])
        b6 = banks[6]
        for dd in range(2):
            P.mm(b6[:, dd * NCH:(dd + 1) * NCH], Lm[dd], LF[:, dd, :], True, True, reads=["LF", "cst"], writes=["bank6"])
        P.mm(b6[:, 2 * NCH:4 * NCH], ones, LF[:].rearrange("p a c -> p (a c)"), True, True, reads=["LF", "cst"], writes=["bank6"])
        P.copy("vector", BC[:].rearrange("p a c -> p (a c)"), b6[:, 0:2 * NCH], reads=["bank6"], writes=["BC"])
        P.copy("vector", TOT[:].rearrange("p a c -> p (a c)"), b6[:, 2 * NCH:4 * NCH], reads=["bank6"], writes=["TOT"])
        P.act(AA[:], BC[:], AF.Exp, reads=["BC"], writes=["AA"])
        P.tt("vector", BD[:], LI[:], BC[:], ALU.subtract, reads=["LI0", "LI1", "BC"], writes=["BD"])
        P.tt("vector", WW[:], TOT[:], BD[:], ALU.add, reads=["TOT", "BD"], writes=["WW"])
        P.act(WW[:], WW[:], AF.Exp, reads=["WW"], writes=["WW"])
        P.act(DEC[:], TOT[:], AF.Exp, reads=["TOT"], writes=["DEC"])
        S = [sb(f"S{dd}", [64, 65]) for dd in range(2)]
        Sb = [sb(f"Sb{dd}", [64, 65], BF16) for dd in range(2)]
        for dd in range(2):
            P.memset("vector", S[dd][:], 0.0, writes=[f"S{dd}"])
            P.memset("vector", Sb[dd][:], 0.0, writes=[f"Sb{dd}"])
        hb = [sb(f"hb{dd}", [128, NCH, 64]) for dd in range(2)]
        lfr = ring("lfrep", 2, [128, 128]); dtr = ring("Dt", 2, [128, 128]); ptr = ring("PTm", 2, [128, 128], BF16)
        tmr = ring("tmpi", 2, [128, 65]); ttr = ring("tot", 2, [128, 65]); dnr = ring("den", 2, [128, 4])
        wvr = ring("wv", 2, [128, 65], BF16)
        pD = Ring([(banks[0], "bank0"), (banks[1], "bank1")])
        pST = Ring([(banks[2], "bank2"), (banks[3], "bank3")])
        pOI = Ring([(banks[4], "bank4"), (banks[5], "bank5")])
        order = [list(range(NCH)), [1, 0] + list(range(NCH - 1, 1, -1))]
        for step in range(NCH):
            for dd in range(2):
                c = order[dd][step]
                cs_ = slice(c * 128, (c + 1) * 128)
                lf, lfk = lfr.next()
                P.ts("vector", lf[:], ones, LF[:, dd, c:c + 1], None, ALU.mult, reads=["cst", "LF"], writes=[lfk])
                pd, pdk = pD.next()
                P.mm(pd[:, 0:128], lf[:], Lm[dd], True, False, reads=[lfk, "cst"], writes=[pdk])
                P.mm(pd[:, 0:128], ident, mneg[dd], False, True, reads=["cst"], writes=[pdk])
                dt_, dtk = dtr.next()
                P.act(dt_[:], pd[:, 0:128], AF.Exp, reads=[pdk, "BD"], writes=[dtk], bias=BD[:, dd, c:c + 1], scale=1.0)
                pst, pstk = pST.next()
                P.mm(pst[:, 0:128], kT[:, cs_], qT[:, cs_], True, True, reads=["T0", "T1"], writes=[pstk])
                PT, PTk = ptr.next()
                P.tt("vector", PT[:], pst[:, 0:128], dt_[:], ALU.mult, reads=[pstk, dtk], writes=[PTk])
                poi, poik = pOI.next()
                P.mm(poi[:, 0:65], PT[:], vb[:, c, :], True, True, reads=[PTk, "vb"], writes=[poik + "o"])
                P.mm(poi[:, 128:193], qT[:, cs_], Sb[dd][:], True, True, reads=["T0", f"Sb{dd}"], writes=[poik + "i"])
                tm, tmk = tmr.next()
                P.act(tm[:], poi[:, 128:193], AF.Copy, reads=[poik + "i", "AA"], writes=[tmk], scale=AA[:, dd, c:c + 1])
                tt_, ttk = ttr.next()
                P.tt("vector", tt_[:], poi[:, 0:65], tm[:], ALU.add, reads=[poik + "o", tmk], writes=[ttk])
                dn, dnk = dnr.next()
                P.ts("vector", dn[:, 0:1], tt_[:, 64:65], -1.0, None, ALU.mult, reads=[ttk], writes=[dnk])
                P.stt("vector", dn[:, 1:2], dn[:, 0:1], 1.0, tt_[:, 64:65], ALU.max, ALU.max, reads=[dnk, ttk], writes=[dnk])
                P.gen("vector", lambda e, dn=dn: e.reciprocal(out=dn[:, 2:3], in_=dn[:, 1:2]), reads=[dnk], writes=[dnk])
                P.ts("vector", hb[dd][:, c, :], tt_[:, 0:64], dn[:, 2:3], None, ALU.mult, reads=[ttk, dnk], writes=[f"hb{dd}_{c}"])
                wv, wvk = wvr.next()
                P.ts("vector", wv[:], vf[:, c, :], WW[:, dd, c:c + 1], None, ALU.mult, reads=["vf", "WW"], writes=[wvk])
                p7 = banks[7]
                P.mm(p7[0:64, 256 + dd * 128:256 + dd * 128 + 65], ktok[:, c, :], wv[:], True, True, reads=["ktok", wvk], writes=[f"b7s{dd}"])
                P.stt("vector", S[dd][:], S[dd][:], DEC[0:64, dd, c:c + 1], p7[0:64, 256 + dd * 128:256 + dd * 128 + 65], ALU.mult, ALU.add,
                      reads=[f"S{dd}", "DEC", f"b7s{dd}"], writes=[f"S{dd}"])
                P.copy("gpsimd", Sb[dd][:], S[dd][:], reads=[f"S{dd}"], writes=[f"Sb{dd}"])
        hk = [f"hb{dd}_{c}" for dd in range(2) for c in range(NCH)]
        P.tt("vector", hb[0][:], hb[0][:], hb[1][:], ALU.add, reads=hk, writes=["hsum"])
        sq = hb[1]
        P.tt("gpsimd", sq[:], hb[0][:], hb[0][:], ALU.mult, reads=["hsum"], writes=["hsq"])
        ssum = sb("ssum", [128, NCH])
        P.gen("vector", lambda e: e.reduce_sum(out=ssum[:], in_=sq[:], axis=AX.X), reads=["hsq"], writes=["ssum"])
        P.ts("vector", ssum[:], ssum[:], 1.0 / 64.0, RMS_EPS, ALU.mult, ALU.add, reads=["ssum"], writes=["ssum"])
        P.act(ssum[:], ssum[:], AF.Sqrt, reads=["ssum"], writes=["ssum"])
        P.gen("vector", lambda e: e.reciprocal(out=ssum[:], in_=ssum[:]), reads=["ssum"], writes=["ssum"])
        nw = sb("nw_s", [128, 64])
        P.dma("sync", nw[:], bcast_rows(nwd, 128), writes=["nw"])
        og = vtmp
        P.dma("sync", og[:], od, reads=["vf"], writes=["og"])
        P.act(og[:], og[:], AF.Sigmoid, reads=["og"], writes=["og"])
        for c in range(NCH):
            P.stt("vector", hb[0][:, c, :], hb[0][:, c, :], ssum[:, c:c + 1], nw[:], ALU.mult, ALU.mult,
                  reads=["hsum", "ssum", "nw"], writes=[f"hn{c}"])
        P.tt("gpsimd", hb[0][:], hb[0][:], og[:], ALU.mult, reads=[f"hn{c}" for c in range(NCH)] + ["og"], writes=["ym"])
        P.dma("sync", yo, hb[0][:], reads=["ym"], final=True)
        P.emit()
    return nc


def run_k2(P_lat, P_ctx, conv_w, conv_b, gate_b, norm_w):
    nc = build_k2()
    B, n, _ = P_lat.shape
    nctx = P_ctx.shape[1]
    s_idx, j_idx = np.meshgrid(np.arange(128), np.arange(128), indexing="ij")
    Lf = (s_idx <= j_idx).astype(np.float32); Lb = (s_idx >= j_idx).astype(np.float32)
    cst = np.stack([Lf, Lb, np.ones((128, 128), np.float32), np.eye(128, dtype=np.float32),
                    np.where(s_idx <= j_idx, 0.0, MASK_NEG).astype(np.float32),
                    np.where(s_idx >= j_idx, 0.0, MASK_NEG).astype(np.float32)], 1)
    in_maps = []
    for i in range(NCORES):
        b, h = i // 4, i % 4
        seq = np.concatenate([P_ctx[b], P_lat[b]], 0)
        qs, ks = slice(h * 64, (h + 1) * 64), slice(256 + h * 64, 256 + (h + 1) * 64)
        tm = lambda a: np.ascontiguousarray(a.reshape(NCH, 128, -1).transpose(1, 0, 2))
        gcols = [1024 + 0 * 8 + 0 * 4 + h, 1024 + 0 * 8 + 1 * 4 + h, 1024 + 1 * 8 + 0 * 4 + h, 1024 + 1 * 8 + 1 * 4 + h]
        gb = np.array([gate_b[0, 0, h], gate_b[0, 1, h], gate_b[1, 0, h], gate_b[1, 1, h]], np.float32)
        cw = np.concatenate([conv_w[:, qs].T, conv_b[qs][:, None], conv_w[:, ks].T, conv_b[ks][:, None]], 1).astype(np.float32)
        in_maps.append({"qpT": np.ascontiguousarray(seq[:, qs].T), "kpT": np.ascontiguousarray(seq[:, ks].T),
                        "v": tm(seq[:, 512 + h * 64:512 + (h + 1) * 64]), "og": tm(seq[:, 768 + h * 64:768 + (h + 1) * 64]),
                        "g4": tm(seq[:, gcols]), "gb": np.ascontiguousarray(np.tile(gb, NCH)), "cw": np.ascontiguousarray(cw),
                        "nw": np.ascontiguousarray(norm_w[h * 64:(h + 1) * 64]), "cst": np.ascontiguousarray(cst)})
    res = _run(nc, in_maps)
    ym_l = np.zeros((B, n, 256), np.float32); ym_c = np.zeros((B, nctx, 256), np.float32)
    for i in range(NCORES):
        b, h = i // 4, i % 4
        y = res.results[i]["o_ym"].transpose(1, 0, 2).reshape(NSEQ, 64)
        ym_c[b, :, h * 64:(h + 1) * 64] = y[:nctx]
        ym_l[b, :, h * 64:(h + 1) * 64] = y[nctx:]
    return ym_l, ym_c


HCH = 32
TWO_PI = 2.0 * np.pi
RND_MAGIC = 12582912.0


def fft_tables(N1):
    N2 = 128
    N = N1 * N2
    ar = np.arange
    c, s = np.cos, np.sin
    th = TWO_PI * ar(N1)[:, None] * ar(N1)[None] / N1
    F1c = np.concatenate([c(th), -s(th)], 1)
    th = TWO_PI * ar(N2)[:, None] * ar(N1)[None] / N
    twRR = np.concatenate([c(th), c(th)], 1); twII = np.concatenate([-s(th), -s(th)], 1)
    th = TWO_PI * ar(N2)[:, None] * ar(N2)[None] / N2
    F2re, F2im, nF2im = c(th), -s(th), s(th)
    G2c = np.concatenate([c(th), s(th)], 1); G2s = np.concatenate([-s(th), c(th)], 1)
    th = TWO_PI * ar(N1)[:, None] * ar(N2)[None] / N
    twcRR = np.concatenate([c(th), c(th)], 1); twcII = np.concatenate([s(th), s(th)], 1)
    th = TWO_PI * ar(N1)[:, None] * ar(N1 // 2)[None] / N1
    G1re, nG1im = c(th) / N, -s(th) / N
    f = lambda a: np.ascontiguousarray(a.astype(np.float32))
    return dict(F1c=f(F1c), twRR=f(twRR), twII=f(twII), F2re=f(F2re), F2im=f(F2im), nF2im=f(nF2im), G2c=f(G2c), G2s=f(G2s),
                twcRR=f(twcRR), twcII=f(twcII), G1re=f(G1re), nG1im=f(nG1im))


TAB_ORDER = ["F1c", "twRR", "twII", "F2re", "F2im", "nF2im", "G2c", "G2s", "twcRR", "twcII", "G1re", "nG1im"]


def hyena_consts(n):
    N = 2 * n
    tau = np.arange(N)
    pos = np.where(tau < n, tau, N - tau).astype(np.float32)
    t = (pos / np.float32(n)).astype(np.float32)
    bands = np.arange(1, 17, dtype=np.float32)
    ang = (np.float32(TWO_PI) * t[:, None] * bands).astype(np.float32)
    feats = np.concatenate([t[:, None], np.cos(ang), np.sin(ang)], -1).astype(np.float32)
    lt = abs(np.log(1e-2))
    deltas = np.linspace(lt / 1.5, lt / 0.3, 256, dtype=np.float32)
    win = (np.exp(-t[:, None] * deltas) + np.float32(0.05)).astype(np.float32)
    win[n] = 0.0
    return np.ascontiguousarray(feats.T), np.ascontiguousarray(win.T)


def build_k4(sizes):
    nc = bass.Bass("TRN2", target_bir_lowering=False)
    B = 2
    dr = {}
    for si, n in enumerate(sizes):
        N1 = 2 * n // 128
        dr[si] = dict(
            u=nc.dram_tensor(f"u{si}", [3, HCH, B, n + 2], F32, kind="ExternalInput").ap(),
            feats=nc.dram_tensor(f"feats{si}", [33, 2 * n], F32, kind="ExternalInput").ap(),
            win=nc.dram_tensor(f"win{si}", [64, 2 * n], F32, kind="ExternalInput").ap(),
            taps=nc.dram_tensor(f"taps{si}", [64, 2 * n], F32, kind="ExternalOutput").ap(),
            out=nc.dram_tensor(f"o_yh{si}", [HCH, B, n], F32, kind="ExternalOutput").ap(),
            tabs={k: nc.dram_tensor(f"t{si}_{k}", list(v.shape), F32, kind="ExternalInput").ap()
                  for k, v in fft_tables(N1).items()})
    mlpd = nc.dram_tensor("mlp", [64, 64 + 64 + 128 + 4], F32, kind="ExternalInput").ap()
    cwd = nc.dram_tensor("cwv", [3 * HCH * 4], F32, kind="ExternalInput").ap()
    skd = nc.dram_tensor("skv", [2 * HCH], F32, kind="ExternalInput").ap()
    cstd = nc.dram_tensor("cst", [128, 256], F32, kind="ExternalInput").ap()
    with ExitStack() as es:
        P = Prog(nc, es)
        sb, ps, ring = mk_alloc(nc, es)
        banks = [(ps(f"bank{i}", [128, 512], F32), f"bank{i}") for i in range(8)]
        cst = sb("cst_s", [128, 256]); P.dma("sync", cst[:], cstd, writes=["cst"])
        ident, ones = cst[:, 0:128], cst[:, 128:256]
        mlp = sb("mlp_s", [64, 260]); P.dma("sync", mlp[:], mlpd, writes=["mlp"])
        w1, w2 = mlp[0:33, 0:64], mlp[:, 64:128]
        w3 = [mlp[:, 128:192], mlp[:, 192:256]]
        b1, f0, b2, f1 = (mlp[:, 256 + j:257 + j] for j in range(4))
        cwb = sb("cwb", [128, 3 * HCH * 4]); P.dma("sync", cwb[:], bcast_rows(cwd, 128), writes=["cwb"])
        skb = sb("skb", [128, 2 * HCH]); P.dma("sync", skb[:], bcast_rows(skd, 128), writes=["skb"])
        a1r = ring("a1", 2, [64, 512]); rrr = ring("rr", 2, [64, 512]); hhr = ring("hh", 2, [64, 512])
        ftr = ring("ft", 2, [33, 512]); wnr = ring("wn", 2, [64, 512]); tpr = ring("tp", 2, [64, 512])
        As_r = ring("As", 2, [128, 256]); t1r = ring("t1", 2, [128, 256]); t2r = ring("t2", 2, [128, 256])
        Br = ring("Bc", 2, [128, 256]); Yr = ring("Yc", 2, [128, 256]); Dr = ring("Dc", 2, [128, 256])
        pr4 = [ring(f"pp{j}", 2, [128, 128]) for j in range(4)]
        xir = ring("xi", 2, [64, 3, 130]); cvr = ring("cv", 2, [64, 3, 128]); tgr = ring("tg", 2, [64, 128])
        z1r = ring("z1", 2, [64, 128]); z2r = ring("z2", 2, [64, 128]); tlr = ring("tl", 2, [128, 128])
        bkA = Ring(banks[0:2]); bkX = Ring(banks[2:4]); bkC = Ring(banks[4:6]); bkY = Ring(banks[6:8])

        def sin_layer(psrc, pk, n_, bias, freq, dst, dstk):
            a1, a1k = a1r.next(); rr, rrk = rrr.next()
            P.ts("vector", a1[:, 0:n_], psrc, bias, freq, ALU.add, ALU.mult, reads=[pk, "mlp"], writes=[a1k])
            P.ts("gpsimd", rr[:, 0:n_], a1[:, 0:n_], 1.0 / TWO_PI, RND_MAGIC, ALU.mult, ALU.add, reads=[a1k], writes=[rrk])
            P.ts("gpsimd", rr[:, 0:n_], rr[:, 0:n_], RND_MAGIC, -TWO_PI, ALU.subtract, ALU.mult, reads=[rrk], writes=[rrk])
            P.tt("gpsimd", rr[:, 0:n_], rr[:, 0:n_], a1[:, 0:n_], ALU.add, reads=[rrk, a1k], writes=[rrk])
            P.ts("gpsimd", rr[:, 0:n_], rr[:, 0:n_], np.pi, -np.pi, ALU.min, ALU.max, reads=[rrk], writes=[rrk])
            P.act(dst, rr[:, 0:n_], AF.Sin, reads=[rrk], writes=[dstk])

        def cmul(src, srck, n1p, W, tRR, tII, tk, dst, dstk):
            t1, t1k = t1r.next(); t2, t2k = t2r.next()
            P.tt("vector", t1[0:n1p, 0:2 * W], src, tRR, ALU.mult, reads=[srck, tk], writes=[t1k])
            P.tt("gpsimd", t2[0:n1p, 0:2 * W], src, tII, ALU.mult, reads=[srck, tk], writes=[t2k])
            P.tt("vector", dst[0:n1p, 0:W], t1[0:n1p, 0:W], t2[0:n1p, W:2 * W], ALU.subtract, reads=[t1k, t2k], writes=[dstk + "r"])
            P.tt("gpsimd", dst[0:n1p, W:2 * W], t2[0:n1p, 0:W], t1[0:n1p, W:2 * W], ALU.add, reads=[t1k, t2k], writes=[dstk + "i"])

        for si, n in enumerate(sizes):
            N = 2 * n
            N1 = N // 128
            Kd = N1 // 2
            d = dr[si]
            T = {}
            for k in TAB_ORDER:
                shp = list(d["tabs"][k].shape)
                T[k] = sb(f"T{si}_{k}", shp)
                P.dma("sync", T[k][:], d["tabs"][k], writes=[f"tab{si}"] if k == TAB_ORDER[-1] else [f"tab{si}_{k}"])
            tabk = [f"tab{si}"] + [f"tab{si}_{k}" for k in TAB_ORDER[:-1]]
            CH = min(512, n)
            nchunk = N // CH
            l1p = sb(f"l1p{si}", [64, nchunk])
            for ci in range(nchunk):
                c0 = ci * CH
                dirn = 0 if c0 < n else 1
                ft, ftk = ftr.next(); P.dma("sync", ft[:, 0:CH], d["feats"][:, c0:c0 + CH], writes=[ftk])
                wn, wnk = wnr.next(); P.dma("sync", wn[:, 0:CH], d["win"][:, c0:c0 + CH], writes=[wnk])
                bk, bkk = bkA.next()
                P.mm(bk[0:64, 0:CH], w1, ft[:, 0:CH], True, True, reads=[ftk, "mlp"], writes=[bkk])
                h1, h1k = hhr.next()
                sin_layer(bk[0:64, 0:CH], bkk, CH, b1, f0, h1[:, 0:CH], h1k)
                bk, bkk = bkX.next()
                P.mm(bk[0:64, 0:CH], w2, h1[:, 0:CH], True, True, reads=[h1k, "mlp"], writes=[bkk])
                h2, h2k = hhr.next()
                sin_layer(bk[0:64, 0:CH], bkk, CH, b2, f1, h2[:, 0:CH], h2k)
                bk, bkk = bkC.next()
                P.mm(bk[0:64, 0:CH], w3[dirn], h2[:, 0:CH], True, True, reads=[h2k, "mlp"], writes=[bkk])
                tp, tpk = tpr.next()
                P.tt("vector", tp[:, 0:CH], bk[0:64, 0:CH], wn[:, 0:CH], ALU.mult, reads=[bkk, wnk], writes=[tpk])
                P.gen("vector", lambda e, tp=tp, ci=ci, CH=CH, l1p=l1p: e.reduce_sum(out=l1p[:, ci:ci + 1], in_=tp[:, 0:CH], axis=AX.X,
                                                                         apply_absolute_value=True), reads=[tpk], writes=[f"l1p{si}_{ci}"])
                P.dma("sync", d["taps"][:, c0:c0 + CH], tp[:, 0:CH], reads=[tpk], writes=[f"taps{si}"], final=True)
            l1 = sb(f"l1_{si}", [64, 2])
            P.gen("vector", lambda e, l1=l1, l1p=l1p: e.reduce_sum(out=l1[:, 0:1], in_=l1p[:], axis=AX.X),
                  reads=[f"l1p{si}_{ci}" for ci in range(nchunk)], writes=[f"l1{si}"])
            P.gen("vector", lambda e, l1=l1: e.reciprocal(out=l1[:, 1:2], in_=l1[:, 0:1]), reads=[f"l1{si}"], writes=[f"l1{si}"])
            dg = sb(f"dg{si}", [64, 64])
            P.ts("vector", dg[:], ident[0:64, 0:64], l1[:, 1:2], None, ALU.mult, reads=["cst", f"l1{si}"], writes=[f"dg{si}"])
            bk, bkk = bkY.next()
            P.mm(bk[:, 0:64], ones[0:64, :], dg[:], True, True, reads=["cst", f"dg{si}"], writes=[bkk])
            rl1b = sb(f"rl1b{si}", [128, 64])
            P.copy("vector", rl1b[:], bk[:, 0:64], reads=[bkk], writes=[f"rl1b{si}"])

            def fwd_fft(xt, xk, Krows):
                bA, bAk = bkA.next()
                P.mm(bA[:, 0:2 * N1], xt, T["F1c"][0:Krows, :], True, True, reads=[xk] + tabk, writes=[bAk])
                As, Ask = As_r.next()
                P.copy("scalar", As[:, 0:2 * N1], bA[:, 0:2 * N1], reads=[bAk], writes=[Ask])
                Bc, Bck = Br.next()
                cmul(As[:, 0:2 * N1], Ask, 128, N1, T["twRR"][:], T["twII"][:], tabk[0], Bc, Bck)
                bX, bXk = bkX.next()
                Bre, Bim = Bc[:, 0:N1], Bc[:, N1:2 * N1]
                P.mm(bX[:, 0:N1], T["F2re"][:], Bre, True, False, reads=[Bck + "r"] + tabk, writes=[bXk])
                P.mm(bX[:, 0:N1], T["nF2im"][:], Bim, False, True, reads=[Bck + "i"] + tabk, writes=[bXk])
                P.mm(bX[:, N1:2 * N1], T["F2re"][:], Bim, True, False, reads=[Bck + "i"] + tabk, writes=[bXk])
                P.mm(bX[:, N1:2 * N1], T["F2im"][:], Bre, False, True, reads=[Bck + "r"] + tabk, writes=[bXk])
                return bX, bXk

            H = sb(f"H{si}", [128, 64, 2 * N1])
            for oc in range(64):
                tl, tlk = tlr.next()
                P.dma("sync", tl[0:N1, :], d["taps"][oc].rearrange("(a b) -> a b", b=128), reads=[f"taps{si}"], writes=[tlk])
                bX, bXk = fwd_fft(tl[0:N1, :], tlk, N1)
                P.ts("vector", H[:, oc, :], bX[:, 0:2 * N1], rl1b[:, oc:oc + 1], None, ALU.mult, reads=[bXk, f"rl1b{si}"], writes=[f"H{si}_{oc}"])

            def long_conv(zt, zk, o, c):
                bX, bXk = fwd_fft(zt, zk, Kd)
                oc = o * HCH + c
                Hre, Him = H[:, oc, 0:N1], H[:, oc, N1:2 * N1]
                hk = f"H{si}_{oc}"
                pp = [r.next() for r in pr4]
                P.tt("vector", pp[0][0][:, 0:N1], bX[:, 0:N1], Hre, ALU.mult, reads=[bXk, hk], writes=[pp[0][1]])
                P.tt("vector", pp[1][0][:, 0:N1], bX[:, N1:2 * N1], Him, ALU.mult, reads=[bXk, hk], writes=[pp[1][1]])
                P.tt("vector", pp[2][0][:, 0:N1], bX[:, 0:N1], Him, ALU.mult, reads=[bXk, hk], writes=[pp[2][1]])
                P.tt("vector", pp[3][0][:, 0:N1], bX[:, N1:2 * N1], Hre, ALU.mult, reads=[bXk, hk], writes=[pp[3][1]])
                Yc, Yck = Yr.next()
                P.tt("gpsimd", Yc[:, 0:N1], pp[0][0][:, 0:N1], pp[1][0][:, 0:N1], ALU.subtract, reads=[pp[0][1], pp[1][1]], writes=[Yck + "r"])
                P.tt("gpsimd", Yc[:, N1:2 * N1], pp[2][0][:, 0:N1], pp[3][0][:, 0:N1], ALU.add, reads=[pp[2][1], pp[3][1]], writes=[Yck + "i"])
                bC, bCk = bkC.next()
                P.mm(bC[0:N1, 0:256], Yc[:, 0:N1], T["G2c"][:], True, False, reads=[Yck + "r"] + tabk, writes=[bCk])
                P.mm(bC[0:N1, 0:256], Yc[:, N1:2 * N1], T["G2s"][:], False, True, reads=[Yck + "i"] + tabk, writes=[bCk])
                Cs, Csk = As_r.next()
                P.copy("scalar", Cs[0:N1, :], bC[0:N1, 0:256], reads=[bCk], writes=[Csk])
                Dc, Dck = Dr.next()
                cmul(Cs[0:N1, :], Csk, N1, 128, T["twcRR"][:], T["twcII"][:], tabk[0], Dc, Dck)
                bY, bYk = bkY.next()
                P.mm(bY[0:Kd, 0:128], T["G1re"][:], Dc[0:N1, 0:128], True, False, reads=[Dck + "r"] + tabk, writes=[bYk])
                P.mm(bY[0:Kd, 0:128], T["nG1im"][:], Dc[0:N1, 128:256], False, True, reads=[Dck + "i"] + tabk, writes=[bYk])
                return bY, bYk

            for c in range(HCH):
                for b in range(B):
                    xi, xik = xir.next()
                    src = bass.AP(tensor=d["u"].tensor, offset=d["u"][0, c, b, 0].offset,
                                  ap=[[128, Kd], [HCH * B * (n + 2), 3], [1, 130]])
                    P.dma("gpsimd", xi[0:Kd], src, writes=[xik])
                    cv, cvk = cvr.next()
                    for p in range(3):
                        wo = (p * HCH + c) * 4
                        P.ts("vector", cv[0:Kd, p, :], xi[0:Kd, p, 1:129], cwb[0:Kd, wo + 1:wo + 2], cwb[0:Kd, wo + 3:wo + 4], ALU.mult, ALU.add,
                             reads=[xik, "cwb"], writes=[cvk + str(p)])
                        P.stt("vector", cv[0:Kd, p, :], xi[0:Kd, p, 0:128], cwb[0:Kd, wo:wo + 1], cv[0:Kd, p, :], ALU.mult, ALU.add,
                              reads=[xik, "cwb", cvk + str(p)], writes=[cvk + str(p)])
                        P.stt("vector", cv[0:Kd, p, :], xi[0:Kd, p, 2:130], cwb[0:Kd, wo + 2:wo + 3], cv[0:Kd, p, :], ALU.mult, ALU.add,
                              reads=[xik, "cwb", cvk + str(p)], writes=[cvk + str(p)])
                    bY, bYk = long_conv(cv[0:Kd, 0, :], cvk + "0", 0, c)
                    tg, tgk = tgr.next()
                    P.stt("vector", tg[0:Kd, :], cv[0:Kd, 0, :], skb[0:Kd, c:c + 1], bY[0:Kd, 0:128], ALU.mult, ALU.add,
                          reads=[cvk + "0", "skb", bYk], writes=[tgk])
                    z1, z1k = z1r.next()
                    P.tt("gpsimd", z1[0:Kd, :], tg[0:Kd, :], cv[0:Kd, 1, :], ALU.mult, reads=[tgk, cvk + "1"], writes=[z1k])
                    bY, bYk = long_conv(z1[0:Kd, :], z1k, 1, c)
                    tg, tgk = tgr.next()
                    P.stt("vector", tg[0:Kd, :], z1[0:Kd, :], skb[0:Kd, HCH + c:HCH + c + 1], bY[0:Kd, 0:128], ALU.mult, ALU.add,
                          reads=[z1k, "skb", bYk], writes=[tgk])
                    z2, z2k = z2r.next()
                    P.tt("gpsimd", z2[0:Kd, :], tg[0:Kd, :], cv[0:Kd, 2, :], ALU.mult, reads=[tgk, cvk + "2"], writes=[z2k])
                    P.dma("sync", d["out"][c, b].rearrange("(a b) -> a b", b=128), z2[0:Kd, :], reads=[z2k], writes=[z2k + "d"], final=True)
        P.emit()
    return nc


def run_k4(hy_list, conv_w, conv_b, fparams, skip):
    f_w1, f_b1, f_freq, f_w2, f_b2, f_w3 = fparams
    sizes = [h.shape[1] for h in hy_list]
    nc = build_k4(sizes)
    B = 2
    cst = np.concatenate([np.eye(128, dtype=np.float32), np.ones((128, 128), np.float32)], 1)
    consts = [hyena_consts(n) for n in sizes]
    tabs = [fft_tables(2 * n // 128) for n in sizes]
    w3r = f_w3.reshape(64, 2, 2, 256)
    in_maps = []
    for i in range(NCORES):
        cs = slice(i * HCH, (i + 1) * HCH)
        m = {"cst": cst}
        mlp = np.zeros((64, 260), np.float32)
        mlp[0:33, 0:64] = f_w1
        mlp[:, 64:128] = f_w2
        mlp[:, 128:192] = w3r[:, 0, :, cs].reshape(64, 64)
        mlp[:, 192:256] = w3r[:, 1, :, cs].reshape(64, 64)
        mlp[:, 256] = f_b1; mlp[:, 257] = f_freq[0]; mlp[:, 258] = f_b2; mlp[:, 259] = f_freq[1]
        m["mlp"] = mlp
        cw = np.zeros((3, HCH, 4), np.float32)
        for p in range(3):
            ch = slice(p * 256 + i * HCH, p * 256 + (i + 1) * HCH)
            cw[p, :, 0:3] = conv_w[:, ch].T
            cw[p, :, 3] = conv_b[ch]
        m["cwv"] = cw.reshape(-1)
        m["skv"] = np.ascontiguousarray(skip[:, cs]).reshape(-1)
        for si, (hy, n) in enumerate(zip(hy_list, sizes)):
            u = np.zeros((3, HCH, B, n + 2), np.float32)
            for p in range(3):
                u[p, :, :, 1:n + 1] = hy[:, :, p * 256 + i * HCH:p * 256 + (i + 1) * HCH].transpose(2, 0, 1)
            m[f"u{si}"] = u
            feats, win = consts[si]
            m[f"feats{si}"] = feats
            m[f"win{si}"] = np.ascontiguousarray(np.tile(win[cs], (2, 1)))
            for k, v in tabs[si].items():
                m[f"t{si}_{k}"] = v
        in_maps.append(m)
    res = _run(nc, in_maps)
    outs = []
    for si, n in enumerate(sizes):
        y = np.zeros((B, n, 256), np.float32)
        for i in range(NCORES):
            y[:, :, i * HCH:(i + 1) * HCH] = res.results[i][f"o_yh{si}"].transpose(1, 2, 0)
        outs.append(y)
    return outs, [np.concatenate([res.results[i][f"taps{si}"] for i in range(NCORES)], 0) for si in range(len(sizes))]


def kernel(x, c, ctx, c_ctx, w_mod, b_mod, w_in, mlstm_conv_w, mlstm_conv_b, mlstm_gate_b,
           mlstm_norm_w, attn_q_norm_w, attn_k_norm_w, hyena_conv_w, hyena_conv_b,
           hyena_f_w1, hyena_f_b1, hyena_f_freq, hyena_f_w2, hyena_f_b2, hyena_f_w3,
           hyena_skip, w_out, ln_mix_w, ln_mix_b, router_w, router_b,
           exp_w_gate, exp_w_up, exp_w_down, ln_ffn_w, ln_ffn_b):
    f = lambda a: np.asarray(a, dtype=np.float32)
    x, c, ctx, c_ctx = f(x), f(c), f(ctx), f(c_ctx)
    mod = run_k0(c, c_ctx, f(w_mod), f(b_mod))
    depth = w_in.shape[0]
    for l in range(depth):
        last = l == depth - 1
        P_lat, P_ctx = run_k1(x, ctx, mod[l], f(w_in[l]))
        ym_l, ym_c = run_k2(P_lat, P_ctx, f(mlstm_conv_w[l]), f(mlstm_conv_b[l]), f(mlstm_gate_b[l]), f(mlstm_norm_w[l]))
        ya_l, ya_c = run_k3(P_lat, P_ctx, f(attn_q_norm_w[l]), f(attn_k_norm_w[l]))
        hy = [P_lat[..., 1808:]] + ([] if last else [P_ctx[..., 1808:]])
        fpar = (f(hyena_f_w1[l]), f(hyena_f_b1[l]), f(hyena_f_freq[l]), f(hyena_f_w2[l]), f(hyena_f_b2[l]), f(hyena_f_w3[l]))
        yhs, _ = run_k4(hy, f(hyena_conv_w[l]), f(hyena_conv_b[l]), fpar, f(hyena_skip[l]))
        yh_c = np.zeros_like(ym_c) if last else yhs[1]
        ycat_l = np.concatenate([ym_l, ya_l, yhs[0]], -1)
        ycat_c = np.concatenate([ym_c, ya_c, yh_c], -1)
        x1, c1, h2, h2c, aff, affc = run_k5(ycat_l, ycat_c, x, ctx, mod[l], f(w_out[l]), f(ln_mix_w[l]), f(ln_mix_b[l]),
                                            f(router_w[l]), f(router_b[l]))
        thr, thrc = run_k6(aff, affc)
        x, ctx = run_k7(h2, h2c, aff, affc, thr, thrc, x1, c1, mod[l], f(ln_ffn_w[l]), f(ln_ffn_b[l]),
                        f(exp_w_gate[l]), f(exp_w_up[l]), f(exp_w_down[l]))
    return x.astype(np.float32)
```

```python
import numpy as np
from contextlib import ExitStack
import concourse.bass as bass
import concourse.mybir as mybir
from concourse.bass_utils import run_bass_kernel_spmd

F32 = mybir.dt.float32
BF16 = mybir.dt.bfloat16
I32 = mybir.dt.int32
AF = mybir.ActivationFunctionType
ALU = mybir.AluOpType
AX = mybir.AxisListType

NCORES = 8
SELF_SYNC = True


class Prog:
    ENGS = ("sync", "scalar", "vector", "gpsimd", "tensor")

    def __init__(self, nc, es, n_dma_sems=12):
        self.nc, self.es = nc, es
        self.ops = {e: [] for e in self.ENGS}
        self.esem = {}
        self.ecount = {}
        for e in ("scalar", "vector", "gpsimd", "tensor"):
            self.esem[e] = es.enter_context(nc.semaphore(f"sem_{e}"))
            self.ecount[e] = 0
        self.dpool = {}
        for q in ("sync", "scalar", "gpsimd"):
            self.dpool[q] = dict(
                sems=[es.enter_context(nc.semaphore(f"dsem_{q}_{i}")) for i in range(n_dma_sems)],
                cnt=[0] * n_dma_sems, nxt=0, know=[None] * n_dma_sems)
        self.semobj = {}
        self.lastw = {}
        self.readers = {}
        self.know = {e: {} for e in self.ENGS}
        self.final_tokens = []

    def _need(self, eng, tok, waits):
        sk, v, kn = tok
        if self.know[eng].get(sk, 0) >= v:
            return
        waits.append((sk, v))
        k = self.know[eng]
        for a, b in kn.items():
            if k.get(a, 0) < b:
                k[a] = b
        if k.get(sk, 0) < v:
            k[sk] = v

    def op(self, eng, fn, reads=(), writes=(), dma=False, final=False):
        waits = []
        toks = []
        for key in reads:
            t = self.lastw.get(key)
            if t is not None:
                toks.append(t)
        for key in writes:
            t = self.lastw.get(key)
            if t is not None:
                toks.append(t)
            toks.extend(self.readers.get(key, ()))
        own = None if dma else ("e", eng)
        for t in toks:
            if t[0] == own and (eng == "tensor" or not SELF_SYNC):
                continue
            self._need(eng, t, waits)
        if dma:
            pool = self.dpool[eng]
            j = pool["nxt"]
            pool["nxt"] = (j + 1) % len(pool["sems"])
            if pool["cnt"][j] > 0:
                self._need(eng, (("d", eng, j), pool["cnt"][j], pool["know"][j]), waits)
            pool["cnt"][j] += 16
            sk = ("d", eng, j)
            self.semobj[sk] = pool["sems"][j]
            kn = dict(self.know[eng])
            pool["know"][j] = kn
            tok = (sk, pool["cnt"][j], kn)
            inc = (pool["sems"][j], 16)
        else:
            self.ecount[eng] += 1
            sk = ("e", eng)
            self.semobj[sk] = self.esem[eng]
            tok = (sk, self.ecount[eng], dict(self.know[eng]))
            inc = (self.esem[eng], 1)
        self.ops[eng].append((waits, fn, inc))
        for key in reads:
            self.readers.setdefault(key, []).append(tok)
        for key in writes:
            self.lastw[key] = tok
            self.readers[key] = []
        if final:
            self.final_tokens.append(tok)
        return tok

    def emit(self):
        waits = []
        for t in self.final_tokens:
            self._need("sync", t, waits)
        if waits:
            self.ops["sync"].append((waits, None, None))
        nc = self.nc
        with nc.Block() as block:
            def run(eng_name):
                def body(eng):
                    for waits, fn, inc in self.ops[eng_name]:
                        for sk, v in waits:
                            eng.wait_ge(self.semobj[sk], v)
                        if fn is not None:
                            fn(eng).then_inc(inc[0], inc[1])
                return body
            block.sync(run("sync"))
            block.scalar(run("scalar"))
            block.vector(run("vector"))
            block.gpsimd(run("gpsimd"))
            block.tensor(run("tensor"))

    def dma(self, q, out, in_, reads=(), writes=(), final=False, **kw):
        return self.op(q, lambda e: e.dma_start(out=out, in_=in_, **kw), reads, writes, dma=True, final=final)

    def mm(self, out, lhsT, rhs, start, stop, reads=(), writes=()):
        return self.op("tensor", lambda e: e.matmul(out, lhsT, rhs, start=start, stop=stop), reads, writes)

    def act(self, out, in_, func, reads=(), writes=(), eng="scalar", **kw):
        return self.op(eng, lambda e: e.activation(out=out, in_=in_, func=func, **kw), reads, writes)

    def tt(self, eng, out, in0, in1, op, reads=(), writes=()):
        return self.op(eng, lambda e: e.tensor_tensor(out=out, in0=in0, in1=in1, op=op), reads, writes)

    def ts(self, eng, out, in0, s1, s2, op0, op1=None, reads=(), writes=(), accum_out=None):
        kw = {}
        if op1 is not None:
            kw["op1"] = op1
        if accum_out is not None:
            kw["accum_out"] = accum_out
        return self.op(eng, lambda e: e.tensor_scalar(out=out, in0=in0, scalar1=s1, scalar2=s2, op0=op0, **kw),
                       reads, writes)

    def stt(self, eng, out, in0, scalar, in1, op0, op1, reads=(), writes=()):
        return self.op(eng, lambda e: e.scalar_tensor_tensor(out=out, in0=in0, scalar=scalar, in1=in1,
                                                             op0=op0, op1=op1), reads, writes)

    def copy(self, eng, out, in_, reads=(), writes=()):
        if eng == "scalar":
            return self.op(eng, lambda e: e.activation(out=out, in_=in_, func=AF.Copy), reads, writes)
        return self.op(eng, lambda e: e.tensor_copy(out=out, in_=in_), reads, writes)

    def tr(self, out, in_, ident, reads=(), writes=()):
        return self.op("tensor", lambda e: e.transpose(out, in_, ident), reads, writes)

    def memset(self, eng, ap, val, writes=()):
        return self.op(eng, lambda e: e.memset(ap, val), (), writes)

    def gen(self, eng, f, reads=(), writes=()):
        return self.op(eng, f, reads, writes)


def _run(nc, in_maps):
    return run_bass_kernel_spmd(nc, in_maps, core_ids=list(range(NCORES)))


D_MODEL = 1024
DEPTH = 2
N_MOD = 6
MODC = N_MOD * D_MODEL // NCORES


def build_k0():
    nc = bass.Bass("TRN2", target_bir_lowering=False)
    cvT = nc.dram_tensor("cvT", [128, 8, 3], F32, kind="ExternalInput").ap()
    wm = nc.dram_tensor("wm", [DEPTH, 128, 8, MODC], F32, kind="ExternalInput").ap()
    bm = nc.dram_tensor("bm", [DEPTH, 3, MODC], F32, kind="ExternalInput").ap()
    out = nc.dram_tensor("mod", [DEPTH, 3, MODC], F32, kind="ExternalOutput").ap()
    with ExitStack() as es:
        P = Prog(nc, es)
        sb = lambda name, shape, dt=F32: es.enter_context(nc.sbuf_tensor(name, shape, dt))
        cv = sb("cv", [128, 8, 3])
        cs = sb("cs", [128, 8, 3])
        w = [sb(f"w{l}", [128, 8, MODC]) for l in range(DEPTH)]
        b = sb("b", [3, DEPTH, MODC])
        o = sb("o", [3, DEPTH, MODC])
        ps = [es.enter_context(nc.psum_tensor(f"ps{i}", [128, 512], F32)) for i in range(2)]
        P.dma("sync", cv[:], cvT, writes=["cv"])
        for l in range(DEPTH):
            P.dma("sync" if l == 0 else "gpsimd", w[l][:], wm[l], writes=[f"w{l}"])
            P.dma("sync", b[:, l, :], bm[l], writes=[f"b{l}"])
        P.act(cs[:], cv[:], AF.Silu, reads=["cv"], writes=["cs"])
        H = MODC // 2
        for l in range(DEPTH):
            for h in range(2):
                pt = ps[h]
                for kc in range(8):
                    P.mm(pt[0:3, 0:H], cs[:, kc, :], w[l][:, kc, h * H:(h + 1) * H], kc == 0, kc == 7,
                         reads=["cs", f"w{l}"], writes=[f"ps{h}"])
                P.op("vector", lambda e, l=l, h=h, pt=pt: e.tensor_tensor(
                    out=o[:, l, h * H:(h + 1) * H], in0=pt[0:3, 0:H], in1=b[:, l, h * H:(h + 1) * H], op=ALU.add),
                    reads=[f"ps{h}", f"b{l}"], writes=[f"o{l}{h}"])
            P.dma("sync", out[l], o[:, l, :], reads=[f"o{l}0", f"o{l}1"], final=True)
        P.emit()
    return nc


def run_k0(c, c_ctx, w_mod, b_mod):
    cv = np.concatenate([c, c_ctx[None]], 0)
    cvT = np.ascontiguousarray(cv.T.reshape(8, 128, 3).transpose(1, 0, 2))
    nc = build_k0()
    in_maps = []
    for i in range(NCORES):
        sl = slice(i * MODC, (i + 1) * MODC)
        wm = np.ascontiguousarray(w_mod[:, :, sl].reshape(DEPTH, 8, 128, MODC).transpose(0, 2, 1, 3))
        bm = np.ascontiguousarray(np.broadcast_to(b_mod[:, None, sl], (DEPTH, 3, MODC)))
        in_maps.append({"cvT": cvT, "wm": wm, "bm": bm})
    res = _run(nc, in_maps)
    return np.concatenate([r["mod"] for r in res.results], axis=-1)


class Ring:
    def __init__(self, items):
        self.items, self.i = items, 0

    def next(self):
        it = self.items[self.i % len(self.items)]
        self.i += 1
        return it


def mk_alloc(nc, es):
    def sb(name, shape, dt=F32):
        return es.enter_context(nc.sbuf_tensor(name, shape, dt))

    def ps(name, shape, dt=F32):
        return es.enter_context(nc.psum_tensor(name, shape, dt))

    def ring(name, n, shape, dt=F32, psum=False):
        return Ring([((ps if psum else sb)(f"{name}{i}", shape, dt), f"{name}{i}") for i in range(n)])
    return sb, ps, ring


def bcast_rows(ap1d, nparts):
    return bass.AP(tensor=ap1d.tensor, offset=ap1d.offset, ap=[[0, nparts]] + [list(x) for x in ap1d.ap])


LN_EPS = 1e-5


def emit_layernorm(P, x, xkey, xn, xnkey, st, mv, rs, skey, n=1024):
    for j in range(n // 512):
        P.gen("vector", lambda e, j=j: e.bn_stats(out=st[:, j, :], in_=x[:, j * 512:(j + 1) * 512]),
              reads=[xkey], writes=[skey + f"st{j}"])
    P.gen("vector", lambda e: e.bn_aggr(out=mv[:], in_=st[:]),
          reads=[skey + f"st{j}" for j in range(n // 512)], writes=[skey + "mv"])
    P.ts("vector", rs[:], mv[:, 1:2], LN_EPS, None, ALU.add, reads=[skey + "mv"], writes=[skey + "rs"])
    P.act(rs[:], rs[:], AF.Sqrt, reads=[skey + "rs"], writes=[skey + "rs"])
    P.gen("vector", lambda e: e.reciprocal(out=rs[:], in_=rs[:]), reads=[skey + "rs"], writes=[skey + "rs"])
    P.ts("vector", xn, x, mv[:, 0:1], rs[:, 0:1], ALU.subtract, ALU.mult,
         reads=[xkey, skey + "mv", skey + "rs"], writes=[xnkey])


N_IN = 2576
K1_TILES = 17


def build_k1():
    nc = bass.Bass("TRN2", target_bir_lowering=False)
    xt = nc.dram_tensor("xt", [K1_TILES, 128, 1024], F32, kind="ExternalInput").ap()
    modr = nc.dram_tensor("modr", [2, 2, 1024], F32, kind="ExternalInput").ap()
    win = nc.dram_tensor("win", [128, 8, N_IN], F32, kind="ExternalInput").ap()
    identd = nc.dram_tensor("ident", [128, 128], F32, kind="ExternalInput").ap()
    out = nc.dram_tensor("p", [K1_TILES, 128, N_IN], F32, kind="ExternalOutput").ap()
    with ExitStack() as es:
        P = Prog(nc, es)
        sb, ps, ring = mk_alloc(nc, es)
        idf = sb("idf", [128, 128])
        idb = sb("idb", [128, 128], BF16)
        P.dma("sync", idf[:], identd, writes=["idf"])
        P.copy("vector", idb[:], idf[:], reads=["idf"], writes=["idb"])
        modt = sb("modt", [128, 2, 2, 1024])
        for g in range(2):
            for j in range(2):
                P.dma("sync", modt[:, g, j, :], bcast_rows(modr[g, j], 128), writes=[f"mod{g}{j}"])
            P.ts("vector", modt[:, g, 0, :], modt[:, g, 0, :], 1.0, None, ALU.add,
                 reads=[f"mod{g}0"], writes=[f"mod{g}0"])
        wbf = sb("wbf", [128, 8, N_IN], BF16)
        wst = ring("wst", 2, [128, N_IN])
        for kc in range(8):
            t, k = wst.next()
            P.dma("gpsimd" if kc % 2 else "sync", t[:], win[:, kc, :], writes=[k])
            P.copy("gpsimd" if kc % 2 else "vector", wbf[:, kc, :], t[:], reads=[k], writes=[f"wbf{kc}"])
        wkeys = [f"wbf{kc}" for kc in range(8)]
        xr = ring("x", 2, [128, 1024])
        xnr = ring("xn", 2, [128, 1024])
        h1r = ring("h1", 2, [128, 1024])
        hr = ring("h", 2, [128, 1024], BF16)
        hTr = ring("hT", 2, [128, 1024], BF16)
        orr = ring("o", 2, [128, N_IN])
        str_ = ring("st", 2, [128, 2, 6])
        mvr = ring("mv", 2, [128, 2])
        rsr = ring("rs", 2, [128, 1])
        pTr = ring("pT", 2, [128, 1024], BF16, psum=True)
        pmr = ring("pm", 4, [128, 512], F32, psum=True)
        ev = 0
        for t in range(K1_TILES):
            g = 0 if t < 16 else 1
            x, xk = xr.next()
            P.dma("sync", x[:], xt[t], writes=[xk])
            xn, xnk = xnr.next()
            st, _ = str_.next(); mv, _ = mvr.next(); rs, _ = rsr.next()
            emit_layernorm(P, x[:], xk, xn[:], xnk, st, mv, rs, f"ln{t % 2}")
            h1, h1k = h1r.next()
            P.tt("gpsimd", h1[:], xn[:], modt[:, g, 0, :], ALU.mult, reads=[xnk, f"mod{g}0"], writes=[h1k])
            h, hk = hr.next()
            P.tt("gpsimd", h[:], h1[:], modt[:, g, 1, :], ALU.add, reads=[h1k, f"mod{g}1"], writes=[hk])
            pT, pTk = pTr.next()
            for kc in range(8):
                P.tr(pT[:, kc * 128:(kc + 1) * 128], h[:, kc * 128:(kc + 1) * 128], idb[:],
                     reads=[hk, "idb"], writes=[pTk])
            hT, hTk = hTr.next()
            P.copy("scalar", hT[:], pT[:], reads=[pTk], writes=[hTk])
            o, ok = orr.next()
            for cg in range(6):
                c0 = cg * 512
                n = min(512, N_IN - c0)
                pm, pmk = pmr.next()
                for kc in range(8):
                    P.mm(pm[:, 0:n], hT[:, kc * 128:(kc + 1) * 128], wbf[:, kc, c0:c0 + n], kc == 0, kc == 7,
                         reads=[hTk, wkeys[kc]], writes=[pmk])
                P.copy("scalar" if ev % 2 else "vector", o[:, c0:c0 + n], pm[:, 0:n], reads=[pmk], writes=[ok + f"c{cg}"])
                ev += 1
            P.dma("gpsimd", out[t], o[:], reads=[ok + f"c{cg}" for cg in range(6)], writes=[ok + "dma"], final=True)
        P.emit()
    return nc


def lay_w(w, kchunks):
    return np.ascontiguousarray(w.reshape(kchunks, 128, w.shape[1]).transpose(1, 0, 2))


def run_k1(x, ctx, mod_l, w_in_l):
    nc = build_k1()
    B, n, D = x.shape
    seg = n // 4
    ctxf = ctx.reshape(-1, D)
    ident = np.eye(128, dtype=np.float32)
    win = lay_w(w_in_l, 8)
    in_maps = []
    for i in range(NCORES):
        b, s = i // 4, i % 4
        xt = np.zeros((K1_TILES * 128, D), np.float32)
        xt[:seg] = x[b, s * seg:(s + 1) * seg]
        xt[seg:seg + 64] = ctxf[i * 64:(i + 1) * 64]
        modr = np.stack([np.stack([mod_l[b, 1024:2048], mod_l[b, 0:1024]]),
                         np.stack([mod_l[2, 1024:2048], mod_l[2, 0:1024]])])
        in_maps.append({"xt": xt.reshape(K1_TILES, 128, D), "modr": np.ascontiguousarray(modr), "win": win,
                        "ident": ident})
    res = _run(nc, in_maps)
    P_lat = np.zeros((B, n, N_IN), np.float32)
    P_ctx = np.zeros((B * ctx.shape[1], N_IN), np.float32)
    for i in range(NCORES):
        b, s = i // 4, i % 4
        p = res.results[i]["p"].reshape(K1_TILES * 128, N_IN)
        P_lat[b, s * seg:(s + 1) * seg] = p[:seg]
        P_ctx[i * 64:(i + 1) * 64] = p[seg:seg + 64]
    return P_lat, P_ctx.reshape(B, ctx.shape[1], N_IN)


ALPHA = (2.0 * DEPTH) ** 0.25
N_EXP = 16


def build_k5():
    nc = bass.Bass("TRN2", target_bir_lowering=False)
    T = K1_TILES
    yt = nc.dram_tensor("yt", [T, 128, 1024], F32, kind="ExternalInput").ap()
    xt = nc.dram_tensor("xt", [T, 128, 1024], F32, kind="ExternalInput").ap()
    wout = nc.dram_tensor("wout", [128, 8, 1024], F32, kind="ExternalInput").ap()
    rows = nc.dram_tensor("rows", [2, 3, 1024], F32, kind="ExternalInput").ap()
    lnr = nc.dram_tensor("lnr", [2, 1024], F32, kind="ExternalInput").ap()
    rwd = nc.dram_tensor("rw", [128, 8, N_EXP], F32, kind="ExternalInput").ap()
    rbd = nc.dram_tensor("rb", [N_EXP], F32, kind="ExternalInput").ap()
    identd = nc.dram_tensor("ident", [128, 128], F32, kind="ExternalInput").ap()
    x1o = nc.dram_tensor("o_x1", [T, 128, 1024], F32, kind="ExternalOutput").ap()
    h2o = nc.dram_tensor("o_h2", [T, 128, 1024], BF16, kind="ExternalOutput").ap()
    affo = nc.dram_tensor("o_aff", [T, 128, N_EXP], F32, kind="ExternalOutput").ap()
    with ExitStack() as es:
        P = Prog(nc, es)
        sb, ps, ring = mk_alloc(nc, es)
        idf = sb("idf", [128, 128])
        idb = sb("idb", [128, 128], BF16)
        P.dma("sync", idf[:], identd, writes=["idf"])
        P.copy("vector", idb[:], idf[:], reads=["idf"], writes=["idb"])
        rowt = sb("rowt", [128, 2, 3, 1024])
        for g in range(2):
            for j in range(3):
                P.dma("sync", rowt[:, g, j, :], bcast_rows(rows[g, j], 128), writes=[f"row{g}{j}"])
            P.ts("vector", rowt[:, g, 1, :], rowt[:, g, 1, :], 1.0, None, ALU.add,
                 reads=[f"row{g}1"], writes=[f"row{g}1"])
        lnt = sb("lnt", [128, 2, 1024])
        for j in range(2):
            P.dma("sync", lnt[:, j, :], bcast_rows(lnr[j], 128), writes=[f"ln{j}"])
        rw = sb("rwt", [128, 8, N_EXP])
        P.dma("sync", rw[:], rwd, writes=["rw"])
        rb = sb("rbt", [128, N_EXP])
        P.dma("sync", rb[:], bcast_rows(rbd, 128), writes=["rb"])
        wbf = sb("wbf", [128, 8, 1024], BF16)
        wst = ring("wst", 2, [128, 1024])
        for kc in range(8):
            t, k = wst.next()
            P.dma("gpsimd" if kc % 2 else "sync", t[:], wout[:, kc, :], writes=[k])
            P.copy("gpsimd" if kc % 2 else "vector", wbf[:, kc, :], t[:], reads=[k], writes=[f"wbf{kc}"])
        wkeys = [f"wbf{kc}" for kc in range(8)]
        yr = ring("y", 2, [128, 1024]); ybr = ring("yb", 2, [128, 1024], BF16)
        yTr = ring("yT", 2, [128, 1024], BF16)
        xr = ring("x", 2, [128, 1024]); tmpr = ring("tmp", 2, [128, 1024]); rr = ring("r", 2, [128, 1024])
        xnr = ring("xn", 2, [128, 1024]); x1r = ring("x1_", 2, [128, 1024]); x1ar = ring("x1a", 2, [128, 1024])
        xn2r = ring("xn2", 2, [128, 1024]); h2fr = ring("h2f", 2, [128, 1024]); h2ar = ring("h2a", 2, [128, 1024])
        h2br = ring("h2b", 2, [128, 1024], BF16)
        h2Tr = ring("h2T", 2, [128, 1024])
        str_ = ring("st", 4, [128, 2, 6]); mvr = ring("mv", 4, [128, 2]); rsr = ring("rs", 4, [128, 1])
        lgr = ring("lg", 2, [128, N_EXP]); exr = ring("ex", 2, [128, N_EXP]); afr = ring("af", 2, [128, N_EXP])
        smr = ring("sm", 2, [128, 4])
        pTr = ring("pT", 1, [128, 1024], BF16, psum=True)
        pmr = ring("pm", 2, [128, 512], F32, psum=True)
        pTfr = ring("pTf", 1, [128, 1024], F32, psum=True)
        plr = ring("pl", 1, [128, N_EXP], F32, psum=True)
        lnc = 0
        for t in range(T):
            g = 0 if t < 16 else 1
            y, yk = yr.next(); P.dma("sync", y[:], yt[t], writes=[yk])
            x, xk = xr.next(); P.dma("sync", x[:], xt[t], writes=[xk])
            yb, ybk = ybr.next(); P.copy("gpsimd", yb[:], y[:], reads=[yk], writes=[ybk])
            pT, pTk = pTr.next()
            for kc in range(8):
                P.tr(pT[:, kc * 128:(kc + 1) * 128], yb[:, kc * 128:(kc + 1) * 128], idb[:], reads=[ybk, "idb"], writes=[pTk])
            yT, yTk = yTr.next(); P.copy("scalar", yT[:], pT[:], reads=[pTk], writes=[yTk])
            tmp, tmpk = tmpr.next()
            for hf in range(2):
                pm, pmk = pmr.next()
                for kc in range(8):
                    P.mm(pm[:], yT[:, kc * 128:(kc + 1) * 128], wbf[:, kc, hf * 512:(hf + 1) * 512], kc == 0, kc == 7,
                         reads=[yTk, wkeys[kc]], writes=[pmk])
                P.tt("vector", tmp[:, hf * 512:(hf + 1) * 512], pm[:], rowt[:, g, 0, hf * 512:(hf + 1) * 512], ALU.mult,
                     reads=[pmk, f"row{g}0"], writes=[tmpk + str(hf)])
            r, rk = rr.next()
            P.stt("vector", r[:], x[:], ALPHA, tmp[:], ALU.mult, ALU.add, reads=[xk, tmpk + "0", tmpk + "1"], writes=[rk])
            xn, xnk = xnr.next(); st, _ = str_.next(); mv, _ = mvr.next(); rs, _ = rsr.next()
            emit_layernorm(P, r[:], rk, xn[:], xnk, st, mv, rs, f"lnA{lnc % 4}"); lnc += 1
            x1a, x1ak = x1ar.next(); x1, x1k = x1r.next()
            P.tt("gpsimd", x1a[:], xn[:], lnt[:, 0, :], ALU.mult, reads=[xnk, "ln0"], writes=[x1ak])
            P.tt("gpsimd", x1[:], x1a[:], lnt[:, 1, :], ALU.add, reads=[x1ak, "ln1"], writes=[x1k])
            P.dma("gpsimd", x1o[t], x1[:], reads=[x1k], writes=[x1k + "d"], final=True)
            xn2, xn2k = xn2r.next(); st, _ = str_.next(); mv, _ = mvr.next(); rs, _ = rsr.next()
            emit_layernorm(P, x1[:], x1k, xn2[:], xn2k, st, mv, rs, f"lnA{lnc % 4}"); lnc += 1
            h2a, h2ak = h2ar.next(); h2f, h2fk = h2fr.next(); h2b, h2bk = h2br.next()
            P.tt("gpsimd", h2a[:], xn2[:], rowt[:, g, 1, :], ALU.mult, reads=[xn2k, f"row{g}1"], writes=[h2ak])
            P.tt("vector", h2f[:], h2a[:], rowt[:, g, 2, :], ALU.add, reads=[h2ak, f"row{g}2"], writes=[h2fk])
            P.copy("scalar", h2b[:], h2f[:], reads=[h2fk], writes=[h2bk])
            P.dma("sync", h2o[t], h2b[:], reads=[h2bk], writes=[h2bk + "d"], final=True)
            pTf, pTfk = pTfr.next()
            for kc in range(8):
                P.tr(pTf[:, kc * 128:(kc + 1) * 128], h2f[:, kc * 128:(kc + 1) * 128], idf[:], reads=[h2fk, "idf"], writes=[pTfk])
            h2T, h2Tk = h2Tr.next()
            P.copy("scalar", h2T[:, 0:512], pTf[:, 0:512], reads=[pTfk], writes=[h2Tk + "a"])
            P.copy("vector", h2T[:, 512:1024], pTf[:, 512:1024], reads=[pTfk], writes=[h2Tk + "b"])
            pl, plk = plr.next()
            for kc in range(8):
                P.mm(pl[:], h2T[:, kc * 128:(kc + 1) * 128], rw[:, kc, :], kc == 0, kc == 7,
                     reads=[h2Tk + "a", h2Tk + "b", "rw"], writes=[plk])
            lg, lgk = lgr.next(); ex, exk = exr.next(); af, afk = afr.next(); sm, smk = smr.next()
            P.tt("vector", lg[:], pl[:], rb[:], ALU.add, reads=[plk, "rb"], writes=[lgk])
            P.gen("vector", lambda e, sm=sm, lg=lg: e.reduce_max(out=sm[:, 0:1], in_=lg[:], axis=AX.X), reads=[lgk], writes=[smk + "m"])
            P.ts("vector", sm[:, 1:2], sm[:, 0:1], -1.0, None, ALU.mult, reads=[smk + "m"], writes=[smk + "n"])
            P.act(ex[:], lg[:], AF.Exp, reads=[lgk, smk + "n"], writes=[exk, smk + "s"], bias=sm[:, 1:2], scale=1.0,
                  accum_out=sm[:, 2:3])
            P.gen("vector", lambda e, sm=sm: e.reciprocal(out=sm[:, 3:4], in_=sm[:, 2:3]), reads=[smk + "s"], writes=[smk + "r"])
            P.ts("vector", af[:], ex[:], sm[:, 3:4], None, ALU.mult, reads=[exk, smk + "r"], writes=[afk])
            P.dma("gpsimd", affo[t], af[:], reads=[afk], writes=[afk + "d"], final=True)
        P.emit()
    return nc


def tok_shard(lat, ctx, i):
    B, n, D = lat.shape
    seg = n // 4
    b, s = i // 4, i % 4
    out = np.zeros((K1_TILES * 128, D), lat.dtype)
    out[:seg] = lat[b, s * seg:(s + 1) * seg]
    out[seg:seg + 64] = ctx.reshape(-1, D)[i * 64:(i + 1) * 64]
    return out.reshape(K1_TILES, 128, D)


def tok_unshard(parts, B, n, nctx):
    D = parts[0].shape[-1]
    seg = n // 4
    lat = np.zeros((B, n, D), parts[0].dtype)
    ctx = np.zeros((B * nctx, D), parts[0].dtype)
    for i in range(NCORES):
        b, s = i // 4, i % 4
        p = parts[i].reshape(K1_TILES * 128, D)
        lat[b, s * seg:(s + 1) * seg] = p[:seg]
        ctx[i * 64:(i + 1) * 64] = p[seg:seg + 64]
    return lat, ctx.reshape(B, nctx, D)


def run_k5(ycat_l, ycat_c, x, ctx, mod_l, w_out_l, ln_w, ln_b, router_w_l, router_b_l):
    nc = build_k5()
    B, n, D = x.shape
    ident = np.eye(128, dtype=np.float32)
    wout = lay_w(w_out_l, 8)
    rw = lay_w(router_w_l, 8)
    lnr = np.ascontiguousarray(np.stack([ln_w, ln_b]))
    in_maps = []
    for i in range(NCORES):
        b = i // 4
        rows = np.stack([np.stack([mod_l[m, 2048:3072], mod_l[m, 4096:5120], mod_l[m, 3072:4096]]) for m in (b, 2)])
        in_maps.append({"yt": tok_shard(ycat_l, ycat_c, i), "xt": tok_shard(x, ctx, i), "wout": wout,
                        "rows": np.ascontiguousarray(rows), "lnr": lnr, "rw": rw,
                        "rb": np.ascontiguousarray(router_b_l), "ident": ident})
    res = _run(nc, in_maps)
    nctx = ctx.shape[1]
    x1, c1 = tok_unshard([r["o_x1"] for r in res.results], B, n, nctx)
    h2, h2c = tok_unshard([r["o_h2"] for r in res.results], B, n, nctx)
    aff, affc = tok_unshard([r["o_aff"] for r in res.results], B, n, nctx)
    return x1, c1, h2, h2c, aff, affc


K6_ITERS = 26


def build_k6(F_lat, k_lat, F_ctx, k_ctx):
    nc = bass.Bass("TRN2", target_bir_lowering=False)
    R = 32
    ald = nc.dram_tensor("al", [R, F_lat], F32, kind="ExternalInput").ap()
    acd = nc.dram_tensor("ac", [R, F_ctx], F32, kind="ExternalInput").ap()
    thro = nc.dram_tensor("thr", [R, 2], F32, kind="ExternalOutput").ap()
    with ExitStack() as es:
        P = Prog(nc, es)
        sb, ps, ring = mk_alloc(nc, es)
        res = sb("res", [R, 2])
        for pi, (src, F, k) in enumerate(((ald, F_lat, k_lat), (acd, F_ctx, k_ctx))):
            A = sb(f"A{pi}", [R, F])
            junk = sb(f"junk{pi}", [R, F], BF16)
            sc = sb(f"sc{pi}", [R, 8])
            lo, hi, mid, cnt, cond, t1, t2 = (sc[:, j:j + 1] for j in range(7))
            kk = f"p{pi}"
            P.dma("sync", A[:], src, writes=[kk + "A"])
            P.memset("vector", lo, 0.0, writes=[kk + "lo"])
            P.memset("vector", hi, 1.0, writes=[kk + "hi"])
            for it in range(K6_ITERS):
                P.tt("vector", mid, lo, hi, ALU.add, reads=[kk + "lo", kk + "hi"], writes=[kk + "mid"])
                P.ts("vector", mid, mid, 0.5, None, ALU.mult, reads=[kk + "mid"], writes=[kk + "mid"])
                P.ts("vector", junk[:], A[:], mid, None, ALU.is_ge, ALU.add, reads=[kk + "A", kk + "mid"],
                     writes=[kk + "junk", kk + "cnt"], accum_out=cnt)
                P.ts("vector", cond, cnt, float(k) - 0.5, None, ALU.is_ge, reads=[kk + "cnt"], writes=[kk + "cond"])
                P.tt("vector", t1, cond, mid, ALU.mult, reads=[kk + "cond", kk + "mid"], writes=[kk + "t1"])
                P.stt("vector", t2, cond, 2.0, mid, ALU.mult, ALU.add, reads=[kk + "cond", kk + "mid"], writes=[kk + "t2"])
                P.tt("vector", lo, lo, t1, ALU.max, reads=[kk + "lo", kk + "t1"], writes=[kk + "lo"])
                P.tt("vector", hi, hi, t2, ALU.min, reads=[kk + "hi", kk + "t2"], writes=[kk + "hi"])
            P.copy("vector", res[:, pi:pi + 1], lo, reads=[kk + "lo"], writes=[f"res{pi}"])
        P.dma("sync", thro, res[:], reads=["res0", "res1"], final=True)
        P.emit()
    return nc


def run_k6(aff, affc):
    B, n, E = aff.shape
    ncx = affc.shape[1]
    nc = build_k6(n, 2 * n // E, ncx, 2 * ncx // E)
    al = np.ascontiguousarray(aff.transpose(0, 2, 1).reshape(B * E, n))
    ac = np.ascontiguousarray(affc.transpose(0, 2, 1).reshape(B * E, ncx))
    res = _run(nc, [{"al": al, "ac": ac} for _ in range(NCORES)])
    thr = res.results[0]["thr"]
    return thr[:, 0].reshape(B, E), thr[:, 1].reshape(B, E)


D_FF = 2816
NFC = D_FF // 128
K7_TOK = K1_TILES * 128


def build_k7(n_exp=N_EXP):
    nc = bass.Bass("TRN2", target_bir_lowering=False)
    T = K1_TILES
    h2Td = nc.dram_tensor("h2T", [128, 8, K7_TOK], BF16, kind="ExternalInput").ap()
    affd = nc.dram_tensor("aff", [T, 128, N_EXP], F32, kind="ExternalInput").ap()
    thrd = nc.dram_tensor("thr", [2, N_EXP], F32, kind="ExternalInput").ap()
    x1d = nc.dram_tensor("i_x1", [T, 128, 1024], F32, kind="ExternalInput").ap()
    rows = nc.dram_tensor("rows", [2, 1024], F32, kind="ExternalInput").ap()
    lnr = nc.dram_tensor("lnr", [2, 1024], F32, kind="ExternalInput").ap()
    wgd = nc.dram_tensor("wg", [n_exp, 128, 8, D_FF], F32, kind="ExternalInput").ap()
    wud = nc.dram_tensor("wu", [n_exp, 128, 8, D_FF], F32, kind="ExternalInput").ap()
    wdd = nc.dram_tensor("wd", [n_exp, 128, NFC, 1024], F32, kind="ExternalInput").ap()
    x2o = nc.dram_tensor("o_x2", [T, 128, 1024], F32, kind="ExternalOutput").ap()
    with ExitStack() as es:
        P = Prog(nc, es)
        sb, ps, ring = mk_alloc(nc, es)
        h2T = sb("h2Ts", [128, 8, K7_TOK], BF16)
        for kc in range(8):
            P.dma("sync", h2T[:, kc, :], h2Td[:, kc, :], writes=["h2T"] if kc == 7 else [f"h2T_{kc}"])
        h2keys = ["h2T"] + [f"h2T_{kc}" for kc in range(7)]
        thrb = sb("thrb", [128, 2, N_EXP])
        for g in range(2):
            P.dma("sync", thrb[:, g, :], bcast_rows(thrd[g], 128), writes=[f"thr{g}"])
        wgt = sb("wgt", [128, T, N_EXP])
        msk = sb("msk", [128, T, N_EXP])
        for t in range(T):
            g = 0 if t < 16 else 1
            P.dma("sync", wgt[:, t, :], affd[t], writes=[f"aff{t}"])
            P.tt("vector", msk[:, t, :], wgt[:, t, :], thrb[:, g, :], ALU.is_ge, reads=[f"aff{t}", f"thr{g}"], writes=[f"msk{t}"])
            P.tt("vector", wgt[:, t, :], wgt[:, t, :], msk[:, t, :], ALU.mult, reads=[f"aff{t}", f"msk{t}"], writes=[f"aff{t}"])
        acc = sb("acc", [128, T, 1024])
        for t in range(T):
            P.memset("gpsimd", acc[:, t, :], 0.0, writes=[f"acc{t}a", f"acc{t}b"])
        FG = 4
        wgr = ring("wgb", 2, [128, 8, FG * 128], BF16)
        wur = ring("wub", 2, [128, 8, FG * 128], BF16)
        wdr = ring("wdb", 2, [128, FG, 1024], BF16)
        stg = ring("stg", 4, [128, 1024])
        actr = ring("actT", 2, [128, FG, 512], BF16)
        sgr = ring("sg", 2, [128, 512])
        pgr = ring("pg", 2, [128, 512], F32, psum=True)
        pur = ring("pu", 2, [128, 512], F32, psum=True)
        pyr = ring("py", 2, [128, 512], F32, psum=True)
        tgs = [(s, min(512, K7_TOK - s)) for s in range(0, K7_TOK, 512)]
        ci = 0
        for e in range(n_exp):
            for f0 in range(0, NFC, FG):
                nf = min(FG, NFC - f0)
                wgb, wgk = wgr.next(); wub, wuk = wur.next(); wdb, wdk = wdr.next()
                for (src, dst, dk) in ((wgd, wgb, wgk), (wud, wub, wuk)):
                    for kp in range(0, 8, 2):
                        st, sk = stg.next()
                        q = "sync"
                        sv = st[:, 0:2 * nf * 128].rearrange("p (a f) -> p a f", a=2)
                        P.dma(q, sv, src[e, :, kp:kp + 2, f0 * 128:(f0 + nf) * 128], writes=[sk])
                        P.copy("gpsimd", dst[:, kp:kp + 2, 0:nf * 128], sv, reads=[sk], writes=[dk + f"k{kp}"])
                        ci += 1
                for fc in range(nf):
                    st, sk = stg.next()
                    P.dma("sync", st[:], wdd[e, :, f0 + fc, :], writes=[sk])
                    P.copy("gpsimd", wdb[:, fc, :], st[:], reads=[sk], writes=[wdk + f"f{fc}"])
                    ci += 1
                for (s0, ns) in tgs:
                    actT, ak = actr.next()
                    for fc in range(nf):
                        pg, pgk = pgr.next(); pu, puk = pur.next()
                        for kc in range(8):
                            P.mm(pg[:, 0:ns], wgb[:, kc, fc * 128:(fc + 1) * 128], h2T[:, kc, s0:s0 + ns], kc == 0, kc == 7,
                                 reads=h2keys + [wgk + f"k{kc - kc % 2}"], writes=[pgk])
                        for kc in range(8):
                            P.mm(pu[:, 0:ns], wub[:, kc, fc * 128:(fc + 1) * 128], h2T[:, kc, s0:s0 + ns], kc == 0, kc == 7,
                                 reads=h2keys + [wuk + f"k{kc - kc % 2}"], writes=[puk])
                        sg, sgk = sgr.next()
                        P.act(sg[:, 0:ns], pg[:, 0:ns], AF.Silu, reads=[pgk], writes=[sgk])
                        P.tt("vector", actT[:, fc, 0:ns], pu[:, 0:ns], sg[:, 0:ns], ALU.mult, reads=[puk, sgk], writes=[ak + f"f{fc}"])
                    for tt in range(s0 // 128, (s0 + ns) // 128):
                        for hf in range(2):
                            py, pyk = pyr.next()
                            for fc in range(nf):
                                P.mm(py[:], actT[:, fc, tt * 128 - s0:(tt + 1) * 128 - s0], wdb[:, fc, hf * 512:(hf + 1) * 512],
                                     fc == 0, fc == nf - 1, reads=[ak + f"f{fc}", wdk + f"f{fc}"], writes=[pyk])
                            ah = f"acc{tt}" + "ab"[hf]
                            P.stt("vector", acc[:, tt, hf * 512:(hf + 1) * 512], py[:], wgt[:, tt, e:e + 1],
                                  acc[:, tt, hf * 512:(hf + 1) * 512], ALU.mult, ALU.add, reads=[pyk, f"aff{tt}", ah], writes=[ah])
        rowt = sb("rowt", [128, 2, 1024])
        lnt = sb("lnt", [128, 2, 1024])
        for j in range(2):
            P.dma("sync", rowt[:, j, :], bcast_rows(rows[j], 128), writes=[f"row{j}"])
            P.dma("sync", lnt[:, j, :], bcast_rows(lnr[j], 128), writes=[f"ln{j}"])
        xr = ring("x", 2, [128, 1024])
        str_ = ring("st", 2, [128, 2, 6]); mvr = ring("mv", 2, [128, 2]); rsr = ring("rs", 2, [128, 1])
        for t in range(T):
            g = 0 if t < 16 else 1
            ak2 = [f"acc{t}a", f"acc{t}b"]
            x, xk = xr.next(); P.dma("sync", x[:], x1d[t], writes=[xk])
            P.tt("gpsimd", acc[:, t, :], acc[:, t, :], rowt[:, g, :], ALU.mult, reads=ak2 + [f"row{g}"], writes=ak2)
            P.stt("vector", acc[:, t, :], x[:], ALPHA, acc[:, t, :], ALU.mult, ALU.add, reads=[xk] + ak2, writes=ak2)
            st, _ = str_.next(); mv, _ = mvr.next(); rs, _ = rsr.next()
            emit_layernorm(P, acc[:, t, :], ak2[0], x[:], xk, st, mv, rs, f"lnB{t % 2}")
            P.tt("gpsimd", acc[:, t, :], x[:], lnt[:, 0, :], ALU.mult, reads=[xk, "ln0"], writes=ak2)
            P.tt("gpsimd", x[:], acc[:, t, :], lnt[:, 1, :], ALU.add, reads=ak2 + ["ln1"], writes=[xk])
            P.dma("sync", x2o[t], x[:], reads=[xk], writes=[xk], final=True)
        P.emit()
    return nc


def run_k7(h2, h2c, aff, affc, thr, thrc, x1, c1, mod_l, ln_w, ln_b, wg, wu, wd, n_exp=N_EXP):
    nc = build_k7(n_exp)
    B, n, D = x1.shape
    nctx = c1.shape[1]
    wgl = np.ascontiguousarray(wg[:n_exp].reshape(n_exp, 8, 128, D_FF).transpose(0, 2, 1, 3))
    wul = np.ascontiguousarray(wu[:n_exp].reshape(n_exp, 8, 128, D_FF).transpose(0, 2, 1, 3))
    wdl = np.ascontiguousarray(wd[:n_exp].reshape(n_exp, NFC, 128, D).transpose(0, 2, 1, 3))
    lnr = np.ascontiguousarray(np.stack([ln_w, ln_b]))
    in_maps = []
    for i in range(NCORES):
        b = i // 4
        h2s = tok_shard(h2, h2c, i).reshape(K7_TOK, D)
        h2T = np.ascontiguousarray(h2s.T.reshape(8, 128, K7_TOK).transpose(1, 0, 2))
        rows = np.ascontiguousarray(np.stack([mod_l[b, 5120:6144], mod_l[2, 5120:6144]]))
        in_maps.append({"h2T": h2T, "aff": tok_shard(aff, affc, i), "thr": np.ascontiguousarray(np.stack([thr[b], thrc[b]])),
                        "i_x1": tok_shard(x1, c1, i), "rows": rows, "lnr": lnr, "wg": wgl, "wu": wul, "wd": wdl})
    res = _run(nc, in_maps)
    return tok_unshard([r["o_x2"] for r in res.results], B, n, nctx)


RMS_EPS = 1e-6
NKEY = 8192 + 256
NKT = NKEY // 128
QCOLS = K1_TILES * 512


def build_k3():
    nc = bass.Bass("TRN2", target_bir_lowering=False)
    T = K1_TILES
    qd = nc.dram_tensor("qT", [128, QCOLS], F32, kind="ExternalInput").ap()
    kd = nc.dram_tensor("kT", [128, NKEY], F32, kind="ExternalInput").ap()
    vd = nc.dram_tensor("v", [128, NKT, 2, 64], F32, kind="ExternalInput").ap()
    cqd = nc.dram_tensor("cosq", [128, 16 * 512], F32, kind="ExternalInput").ap()
    sqd = nc.dram_tensor("sinq", [128, 16 * 512], F32, kind="ExternalInput").ap()
    ckd = nc.dram_tensor("cosk", [128, 8192], F32, kind="ExternalInput").ap()
    skd = nc.dram_tensor("sink", [128, 8192], F32, kind="ExternalInput").ap()
    cst = nc.dram_tensor("cst", [128, 258], F32, kind="ExternalInput").ap()
    yo = nc.dram_tensor("o_ya", [T, 128, 512], F32, kind="ExternalOutput").ap()
    with ExitStack() as es:
        P = Prog(nc, es)
        sb, ps, ring = mk_alloc(nc, es)
        cs = sb("cs", [128, 258])
        P.dma("sync", cs[:], cst, writes=["cs"])
        Rm, onesb, qw2, kw2 = cs[:, 0:128], cs[:, 128:256], cs[:, 256:257], cs[:, 257:258]
        bq = sb("bq", [128, 2])
        P.memset("vector", bq[:, 0:1], 64.0 * RMS_EPS, writes=["bq0"])
        P.memset("vector", bq[:, 1:2], RMS_EPS, writes=["bq1"])
        qr = sb("qr", [128, QCOLS], BF16)
        kr = sb("kr", [128, NKEY], BF16)
        vst = ring("vst", 2, [128, 2, 64])
        vaug = sb("vaug", [128, NKT, 2, 65], BF16)
        P.memset("gpsimd", vaug[:], 1.0, writes=["vaug_init"])
        for kt in range(NKT):
            v, vk = vst.next()
            P.dma("sync", v[:], vd[:, kt], writes=[vk])
            P.copy("gpsimd", vaug[:, kt, :, 0:64], v[:], reads=[vk, "vaug_init"], writes=[f"vaug{kt}"])
        xr = ring("px", 2, [128, 512]); sqr = ring("psq", 2, [128, 512]); sdr = ring("psd", 2, [128, 512])
        xnr = ring("pxn", 2, [128, 512]); cr = ring("pc", 2, [128, 512]); sr = ring("psn", 2, [128, 512])
        t1r = ring("pt1", 2, [128, 512]); t2r = ring("pt2", 2, [128, 512])
        bank = [(ps(f"mb{i}", [128, 512], F32), f"mb{i}") for i in range(7)]
        pssr = Ring(bank[0:1])
        prot = Ring(bank[1:2])

        def prep(src, dst, dkey, ncols, nrope, w2, bcol, scale, cosd, sind):
            for c0 in range(0, ncols, 512):
                n = min(512, ncols - c0)
                x, xk = xr.next(); P.dma("sync", x[:, 0:n], src[:, c0:c0 + n], writes=[xk])
                sq, sqk = sqr.next(); P.tt("gpsimd", sq[:, 0:n], x[:, 0:n], x[:, 0:n], ALU.mult, reads=[xk], writes=[sqk])
                pss, pssk = pssr.next()
                P.mm(pss[:, 0:n], onesb, sq[:, 0:n], True, True, reads=[sqk, "cs"], writes=[pssk])
                sd, sdk = sdr.next()
                P.act(sd[:, 0:n], pss[:, 0:n], AF.Sqrt, reads=[pssk, f"bq{bcol}"], writes=[sdk], bias=bq[:, bcol:bcol + 1], scale=scale)
                P.gen("vector", lambda e, sd=sd, n=n: e.reciprocal(out=sd[:, 0:n], in_=sd[:, 0:n]), reads=[sdk], writes=[sdk])
                xn, xnk = xnr.next()
                P.stt("vector", xn[:, 0:n], x[:, 0:n], w2, sd[:, 0:n], ALU.mult, ALU.mult, reads=[xk, sdk, "cs"], writes=[xnk])
                dk = f"{dkey}{c0 // 512}"
                if c0 < nrope:
                    pr, prk = prot.next()
                    P.mm(pr[:, 0:n], Rm, xn[:, 0:n], True, True, reads=[xnk, "cs"], writes=[prk])
                    c, ck = cr.next(); P.dma("sync", c[:, 0:n], cosd[:, c0:c0 + n], writes=[ck])
                    s, sk = sr.next(); P.dma("sync", s[:, 0:n], sind[:, c0:c0 + n], writes=[sk])
                    t1, t1k = t1r.next(); P.tt("gpsimd", t1[:, 0:n], xn[:, 0:n], c[:, 0:n], ALU.mult, reads=[xnk, ck], writes=[t1k])
                    t2, t2k = t2r.next(); P.tt("vector", t2[:, 0:n], pr[:, 0:n], s[:, 0:n], ALU.mult, reads=[prk, sk], writes=[t2k])
                    P.tt("gpsimd", dst[:, c0:c0 + n], t1[:, 0:n], t2[:, 0:n], ALU.add, reads=[t1k, t2k], writes=[dk])
                else:
                    P.copy("gpsimd", dst[:, c0:c0 + n], xn[:, 0:n], reads=[xnk], writes=[dk])

        prep(kd, kr, "kr", NKEY, 8192, kw2, 1, 1.0 / 64.0, ckd, skd)
        prep(qd, qr, "qr", QCOLS, 16 * 512, qw2, 0, 1.0, cqd, sqd)
        LA = 2
        pstR = Ring(bank[0:3])
        ptr = ring("PT", 4, [128, 512], BF16)
        yr = ring("yo", 2, [128, 512])
        rcr = ring("rc", 2, [128, 4])
        its = []
        for qt in range(T):
            kts = list(range(NKT)) if qt < 16 else [64, 65]
            for g in range(2):
                for ii, kt in enumerate(kts):
                    its.append((qt, g, ii, kt, len(kts)))
        pend = {}
        ycur = {}
        for idx in range(len(its) + LA):
            if idx < len(its):
                qt, g, ii, kt, nk = its[idx]
                pst, pstk = pstR.next()
                P.mm(pst[:], kr[g * 64:(g + 1) * 64, kt * 128:(kt + 1) * 128], qr[g * 64:(g + 1) * 64, qt * 512:(qt + 1) * 512],
                     True, True, reads=[f"kr{kt // 4}", f"qr{qt}"], writes=[pstk])
                pend[idx] = (pst, pstk)
            j0 = idx - LA
            if j0 < 0:
                continue
            qt, g, ii, kt, nk = its[j0]
            pst, pstk = pend.pop(j0)
            PT, PTk = ptr.next()
            P.act(PT[:], pst[:], AF.Exp, reads=[pstk], writes=[PTk])
            for j in range(4):
                pb, pbk = bank[3 + j]
                P.mm(pb[:, 0:65], PT[:, j * 128:(j + 1) * 128], vaug[:, kt, g, :], ii == 0, ii == nk - 1,
                     reads=[PTk, f"vaug{kt}"], writes=[pbk])
            if ii == nk - 1:
                if g == 0:
                    ycur[qt] = yr.next()
                y, yk = ycur[qt]
                rc, rck = rcr.next()
                for j in range(4):
                    pb, pok = bank[3 + j]
                    po = pb[:, 0:65]
                    P.gen("vector", lambda e, rc=rc, po=po, j=j: e.reciprocal(out=rc[:, j:j + 1], in_=po[:, 64:65]), reads=[pok], writes=[rck + str(j)])
                    c0 = (g * 4 + j) * 64
                    P.ts("vector", y[:, c0:c0 + 64], po[:, 0:64], rc[:, j:j + 1], None, ALU.mult, reads=[pok, rck + str(j)], writes=[yk + f"{g}{j}"])
                if g == 1:
                    P.dma("sync", yo[qt], y[:], reads=[yk + f"{g_}{j}" for g_ in range(2) for j in range(4)], writes=[yk + "d"], final=True)
        P.emit()
    return nc


def rope_tables(n_lat, grid_w=64, theta=10000.0, hd=64):
    nf = hd // 4
    t = np.arange(n_lat)
    row = (t // grid_w).astype(np.float32)
    col = (t % grid_w).astype(np.float32)
    inv = (theta ** (-np.arange(nf, dtype=np.float32) / nf)).astype(np.float32)
    ar = row[:, None] * inv
    ac = col[:, None] * inv
    ang = np.concatenate([ar, ar, ac, ac], axis=-1)
    return np.cos(ang).astype(np.float32), np.sin(ang).astype(np.float32)


def rope_rot_matrix():
    R = np.zeros((64, 64), np.float32)
    for a in range(2):
        for f in range(16):
            R[a * 32 + 16 + f, a * 32 + f] = -1.0
            R[a * 32 + f, a * 32 + 16 + f] = 1.0
    return R


def run_k3(P_lat, P_ctx, qw, kw):
    nc = build_k3()
    B, n, _ = P_lat.shape
    nctx = P_ctx.shape[1]
    aq_l, ak_l, av_l = P_lat[..., 1040:1552], P_lat[..., 1552:1680], P_lat[..., 1680:1808]
    aq_c, ak_c, av_c = P_ctx[..., 1040:1552], P_ctx[..., 1552:1680], P_ctx[..., 1680:1808]
    cos, sin = rope_tables(n)
    R = rope_rot_matrix()
    Rm = np.zeros((128, 128), np.float32); Rm[:64, :64] = R; Rm[64:, 64:] = R
    ob = np.zeros((128, 128), np.float32); ob[:64, :64] = 1; ob[64:, 64:] = 1
    cst = np.concatenate([Rm, ob, np.tile(qw, 2)[:, None], np.tile(kw, 2)[:, None]], 1).astype(np.float32)
    cosk = np.ascontiguousarray(np.tile(cos.T, (2, 1))); sink = np.ascontiguousarray(np.tile(sin.T, (2, 1)))
    seg = n // 4
    in_maps = []
    for i in range(NCORES):
        b, s = i // 4, i % 4
        q = tok_shard(aq_l, aq_c, i).reshape(K1_TILES, 128, 2, 4, 64)
        qT = np.ascontiguousarray(q.transpose(2, 4, 0, 3, 1)).reshape(128, QCOLS)
        k = np.concatenate([ak_l[b], ak_c[b]], 0).reshape(NKEY, 2, 64)
        kT = np.ascontiguousarray(k.transpose(1, 2, 0)).reshape(128, NKEY)
        v = np.concatenate([av_l[b], av_c[b]], 0).reshape(NKT, 128, 2, 64)
        vv = np.ascontiguousarray(v.transpose(1, 0, 2, 3))
        cq = cos[s * seg:(s + 1) * seg].reshape(16, 128, 64)
        sq = sin[s * seg:(s + 1) * seg].reshape(16, 128, 64)
        cq = np.broadcast_to(cq.transpose(2, 0, 1)[None, :, :, None, :], (2, 64, 16, 4, 128)).reshape(128, 16 * 512)
        sq = np.broadcast_to(sq.transpose(2, 0, 1)[None, :, :, None, :], (2, 64, 16, 4, 128)).reshape(128, 16 * 512)
        in_maps.append({"qT": qT, "kT": kT, "v": vv, "cosq": np.ascontiguousarray(cq), "sinq": np.ascontiguousarray(sq),
                        "cosk": cosk, "sink": sink, "cst": cst})
    res = _run(nc, in_maps)
    return tok_unshard([r["o_ya"] for r in res.results], B, n, nctx)


NCH = NKT
NSEQ = NKEY
MASK_NEG = -30000.0


def build_k2():
    nc = bass.Bass("TRN2", target_bir_lowering=False)
    qpd = nc.dram_tensor("qpT", [64, NSEQ], F32, kind="ExternalInput").ap()
    kpd = nc.dram_tensor("kpT", [64, NSEQ], F32, kind="ExternalInput").ap()
    vd = nc.dram_tensor("v", [128, NCH, 64], F32, kind="ExternalInput").ap()
    od = nc.dram_tensor("og", [128, NCH, 64], F32, kind="ExternalInput").ap()
    gd = nc.dram_tensor("g4", [128, NCH, 4], F32, kind="ExternalInput").ap()
    gbd = nc.dram_tensor("gb", [NCH * 4], F32, kind="ExternalInput").ap()
    cwd = nc.dram_tensor("cw", [64, 8], F32, kind="ExternalInput").ap()
    nwd = nc.dram_tensor("nw", [64], F32, kind="ExternalInput").ap()
    cstd = nc.dram_tensor("cst", [128, 6, 128], F32, kind="ExternalInput").ap()
    yo = nc.dram_tensor("o_ym", [128, NCH, 64], F32, kind="ExternalOutput").ap()
    with ExitStack() as es:
        P = Prog(nc, es)
        sb, ps, ring = mk_alloc(nc, es)
        cst = sb("cst_s", [128, 6, 128])
        P.dma("sync", cst[:], cstd, writes=["cst"])
        Lm = [cst[:, 0, :], cst[:, 1, :]]
        ones, ident = cst[:, 2, :], cst[:, 3, :]
        mneg = [cst[:, 4, :], cst[:, 5, :]]
        idb = sb("idb", [128, 128], BF16)
        P.copy("vector", idb[:], ident, reads=["cst"], writes=["idb"])
        one1 = sb("one1", [128, 1])
        P.memset("vector", one1[:], 1.0, writes=["one1"])
        cw = sb("cw_s", [64, 8])
        P.dma("sync", cw[:], cwd, writes=["cw"])
        banks = [ps(f"bank{i}", [128, 512], F32) for i in range(8)]
        xin = sb("xin", [64, NSEQ]); cacc = sb("cacc", [64, NSEQ])
        qT = sb("qT_s", [64, NSEQ], BF16); kT = sb("kT_s", [64, NSEQ], BF16)
        segs = [(0, 256), (256, NSEQ)]
        for wi, (src, dst, post) in enumerate(((qpd, qT, 0.125), (kpd, kT, 1.0))):
            o = wi * 4
            half = NSEQ // 2
            P.dma("sync", xin[:, 0:half], src[:, 0:half], writes=["xin_a"])
            P.dma("gpsimd", xin[:, half:], src[:, half:], writes=["xin_b"])
            P.ts("vector", cacc[:], xin[:], cw[:, o + 1:o + 2], cw[:, o + 3:o + 4], ALU.mult, ALU.add,
                 reads=["xin_a", "xin_b", "cw"], writes=["cacc"])
            for (a, b) in segs:
                P.stt("vector", cacc[:, a + 1:b], xin[:, a:b - 1], cw[:, o:o + 1], cacc[:, a + 1:b], ALU.mult, ALU.add,
                      reads=["xin_a", "xin_b", "cw", "cacc"], writes=["cacc"])
                P.stt("vector", cacc[:, a:b - 1], xin[:, a + 1:b], cw[:, o + 2:o + 3], cacc[:, a:b - 1], ALU.mult, ALU.add,
                      reads=["xin_a", "xin_b", "cw", "cacc"], writes=["cacc"])
            P.act(cacc[:], cacc[:], AF.Silu, reads=["cacc"], writes=["cacc"])
            P.ts("gpsimd", dst[:], cacc[:], post, None, ALU.mult, reads=["cacc"], writes=[f"T{wi}"])
            P.memset("vector", xin[:, 0:1], 0.0, writes=["xin_a", "xin_b"]) if wi == 0 else None
        ktok = sb("ktok", [128, NCH, 64], BF16)
        for c0 in range(0, NCH, 8):
            n = min(8, NCH - c0)
            pb = banks[7]
            for c in range(c0, c0 + n):
                pt = pb[:, :].bitcast(BF16)[:, (c - c0) * 64:(c - c0 + 1) * 64]
                P.tr(pt, kT[:, c * 128:(c + 1) * 128], idb[0:64, 0:64], reads=["T1", "idb"], writes=["bank7"])
            P.copy("vector", ktok[:, c0:c0 + n, :], pb[:, :].bitcast(BF16)[:, 0:n * 64].rearrange("p (c d) -> p c d", d=64),
                   reads=["bank7"], writes=["ktok"])
        vf = sb("vf", [128, NCH, 65]); vb = sb("vb", [128, NCH, 65], BF16)
        P.memset("gpsimd", vf[:], 1.0, writes=["vf"])
        vtmp = sb("vtmp", [128, NCH, 64])
        P.dma("sync", vtmp[:], vd, writes=["vtmp"])
        P.copy("gpsimd", vf[:, :, 0:64], vtmp[:], reads=["vtmp", "vf"], writes=["vf"])
        P.copy("gpsimd", vb[:], vf[:], reads=["vf"], writes=["vb"])
        G = sb("G", [128, NCH, 4]); GB = sb("GB", [128, NCH, 4])
        P.dma("sync", G[:], gd, writes=["G"])
        P.dma("sync", GB[:].rearrange("p c g -> p (c g)"), bcast_rows(gbd, 128), writes=["GB"])
        P.tt("vector", G[:], G[:], GB[:], ALU.add, reads=["G", "GB"], writes=["G"])
        LF = sb("LF", [128, 2, NCH]); LI = sb("LI", [128, 2, NCH]); TA = sb("TA", [128, 2, NCH]); TB = sb("TB", [128, 2, NCH])
        for dd in range(2):
            P.copy("vector", LI[:, dd, :], G[:, :, 2 * dd], reads=["G"], writes=[f"LI{dd}"])
            P.copy("vector", TA[:, dd, :], G[:, :, 2 * dd + 1], reads=["G"], writes=["TA"])
        P.act(TB[:], TA[:], AF.Abs, reads=["TA"], writes=["TB"])
        P.act(TB[:], TB[:], AF.Exp, reads=["TB"], writes=["TB"], scale=-1.0)
        P.act(TB[:], TB[:], AF.Ln, reads=["TB", "one1"], writes=["TB"], bias=one1[:, 0:1], scale=1.0)
        P.ts("vector", TA[:], TA[:], 0.0, None, ALU.min, reads=["TA"], writes=["TA"])
        P.tt("vector", LF[:], TA[:], TB[:], ALU.subtract, reads=["TA", "TB"], writes=["LF"])
        BC = sb("BC", [128, 2, NCH]); TOT = sb("TOT", [128, 2, NCH]); AA = sb("AA", [128, 2, NCH])
        BD = sb("BD", [128, 2, NCH]); WW = sb("WW", [128, 2, NCH]); DEC = sb("DEC", [128, 2, NCH])
        b6 = banks[6]
        for dd in range(2):
            P.mm(b6[:, dd * NCH:(dd + 1) * NCH], Lm[dd], LF[:, dd, :], True, True, reads=["LF", "cst"], writes=["bank6"])
        P.mm(b6[:, 2 * NCH:4 * NCH], ones, LF[:].rearrange("p a c -> p (a c)"), True, True, reads=["LF", "cst"], writes=["bank6"])
        P.copy("vector", BC[:].rearrange("p a c -> p (a c)"), b6[:, 0:2 * NCH], reads=["bank6"], writes=["BC"])
        P.copy("vector", TOT[:].rearrange("p a c -> p (a c)"), b6[:, 2 * NCH:4 * NCH], reads=["bank6"], writes=["TOT"])
        P.act(AA[:], BC[:], AF.Exp, reads=["BC"], writes=["AA"])
        P.tt("vector", BD[:], LI[:], BC[:], ALU.subtract, reads=["LI0", "LI1", "BC"], writes=["BD"])
        P.tt("vector", WW[:], TOT[:], BD[:], ALU.add, reads=["TOT", "BD"], writes=["WW"])
        P.act(WW[:], WW[:], AF.Exp, reads=["WW"], writes=["WW"])
        P.act(DEC[:], TOT[:], AF.Exp, reads=["TOT"], writes=["DEC"])
        S = [sb(f"S{dd}", [64, 65]) for dd in range(2)]
        Sb = [sb(f"Sb{dd}", [64, 65], BF16) for dd in range(2)]
        for dd in range(2):
            P.memset("vector", S[dd][:], 0.0, writes=[f"S{dd}"])
            P.memset("vector", Sb[dd][:], 0.0, writes=[f"Sb{dd}"])
        hb = [sb(f"hb{dd}", [128, NCH, 64]) for dd in range(2)]
        lfr = ring("lfrep", 2, [128, 128]); dtr = ring("Dt", 2, [128, 128]); ptr = ring("PTm", 2, [128, 128], BF16)
        tmr = ring("tmpi", 2, [128, 65]); ttr = ring("tot", 2, [128, 65]); dnr = ring("den", 2, [128, 4])
        wvr = ring("wv", 2, [128, 65], BF16)
        pD = Ring([(banks[0], "bank0"), (banks[1], "bank1")])
        pST = Ring([(banks[2], "bank2"), (banks[3], "bank3")])
        pOI = Ring([(banks[4], "bank4"), (banks[5], "bank5")])
        order = [list(range(NCH)), [1, 0] + list(range(NCH - 1, 1, -1))]
        for step in range(NCH):
            for dd in range(2):
                c = order[dd][step]
                cs_ = slice(c * 128, (c + 1) * 128)
                lf, lfk = lfr.next()
                P.ts("vector", lf[:], ones, LF[:, dd, c:c + 1], None, ALU.mult, reads=["cst", "LF"], writes=[lfk])
                pd, pdk = pD.next()
                P.mm(pd[:, 0:128], lf[:], Lm[dd], True, False, reads=[lfk, "cst"], writes=[pdk])
                P.mm(pd[:, 0:128], ident, mneg[dd], False, True, reads=["cst"], writes=[pdk])
                dt_, dtk = dtr.next()
                P.act(dt_[:], pd[:, 0:128], AF.Exp, reads=[pdk, "BD"], writes=[dtk], bias=BD[:, dd, c:c + 1], scale=1.0)
                pst, pstk = pST.next()
                P.mm(pst[:, 0:128], kT[:, cs_], qT[:, cs_], True, True, reads=["T0", "T1"], writes=[pstk])
                PT, PTk = ptr.next()
                P.tt("vector", PT[:], pst[:, 0:128], dt_[:], ALU.mult, reads=[pstk, dtk], writes=[PTk])
                poi, poik = pOI.next()
                P.mm(poi[:, 0:65], PT[:], vb[:, c, :], True, True, reads=[PTk, "vb"], writes=[poik + "o"])
                P.mm(poi[:, 128:193], qT[:, cs_], Sb[dd][:], True, True, reads=["T0", f"Sb{dd}"], writes=[poik + "i"])
                tm, tmk = tmr.next()
                P.act(tm[:], poi[:, 128:193], AF.Copy, reads=[poik + "i", "AA"], writes=[tmk], scale=AA[:, dd, c:c + 1])
                tt_, ttk = ttr.next()
                P.tt("vector", tt_[:], poi[:, 0:65], tm[:], ALU.add, reads=[poik + "o", tmk], writes=[ttk])
                dn, dnk = dnr.next()
                P.ts("vector", dn[:, 0:1], tt_[:, 64:65], -1.0, None, ALU.mult, reads=[ttk], writes=[dnk])
                P.stt("vector", dn[:, 1:2], dn[:, 0:1], 1.0, tt_[:, 64:65], ALU.max, ALU.max, reads=[dnk, ttk], writes=[dnk])
                P.gen("vector", lambda e, dn=dn: e.reciprocal(out=dn[:, 2:3], in_=dn[:, 1:2]), reads=[dnk], writes=[dnk])
                P.ts("vector", hb[dd][:, c, :], tt_[:, 0:64], dn[:, 2:3], None, ALU.mult, reads=[ttk, dnk], writes=[f"hb{dd}_{c}"])
                wv, wvk = wvr.next()
                P.ts("vector", wv[:], vf[:, c, :], WW[:, dd, c:c + 1], None, ALU.mult, reads=["vf", "WW"], writes=[wvk])
                p7 = banks[7]
                P.mm(p7[0:64, 256 + dd * 128:256 + dd * 128 + 65], ktok[:, c, :], wv[:], True, True, reads=["ktok", wvk], writes=[f"b7s{dd}"])
                P.stt("vector", S[dd][:], S[dd][:], DEC[0:64, dd, c:c + 1], p7[0:64, 256 + dd * 128:256 + dd * 128 + 65], ALU.mult, ALU.add,
                      reads=[f"S{dd}", "DEC", f"b7s{dd}"], writes=[f"S{dd}"])
                P.copy("gpsimd", Sb[dd][:], S[dd][:], reads=[f"S{dd}"], writes=[f"Sb{dd}"])
        hk = [f"hb{dd}_{c}" for dd in range(2) for c in range(NCH)]
        P.tt("vector", hb[0][:], hb[0][:], hb[1][:], ALU.add, reads=hk, writes=["hsum"])
        sq = hb[1]
        P.tt("gpsimd", sq[:], hb[0][:], hb[0][:], ALU.mult, reads=["hsum"], writes=["hsq"])
        ssum = sb("ssum", [128, NCH])
        P.gen("vector", lambda e: e.reduce_sum(out=ssum[:], in_=sq[:], axis=AX.X), reads=["hsq"], writes=["ssum"])
        P.ts("vector", ssum[:], ssum[:], 1.0 / 64.0, RMS_EPS, ALU.mult, ALU.add, reads=["ssum"], writes=["ssum"])
        P.act(ssum[:], ssum[:], AF.Sqrt, reads=["ssum"], writes=["ssum"])
        P.gen("vector", lambda e: e.reciprocal(out=ssum[:], in_=ssum[:]), reads=["ssum"], writes=["ssum"])
        nw = sb("nw_s", [128, 64])
        P.dma("sync", nw[:], bcast_rows(nwd, 128), writes=["nw"])
        og = vtmp
        P.dma("sync", og[:], od, reads=["vf"], writes=["og"])
        P.act(og[:], og[:], AF.Sigmoid, reads=["og"], writes=["og"])
        for c in range(NCH):
            P.stt("vector", hb[0][:, c, :], hb[0][:, c, :], ssum[:, c:c + 1], nw[:], ALU.mult, ALU.mult,
                  reads=["hsum", "ssum", "nw"], writes=[f"hn{c}"])
        P.tt("gpsimd", hb[0][:], hb[0][:], og[:], ALU.mult, reads=[f"hn{c}" for c in range(NCH)] + ["og"], writes=["ym"])
        P.dma("sync", yo, hb[0][:], reads=["ym"], final=True)
        P.emit()
    return nc


def run_k2(P_lat, P_ctx, conv_w, conv_b, gate_b, norm_w):
    nc = build_k2()
    B, n, _ = P_lat.shape
    nctx = P_ctx.shape[1]
    s_idx, j_idx = np.meshgrid(np.arange(128), np.arange(128), indexing="ij")
    Lf = (s_idx <= j_idx).astype(np.float32); Lb = (s_idx >= j_idx).astype(np.float32)
    cst = np.stack([Lf, Lb, np.ones((128, 128), np.float32), np.eye(128, dtype=np.float32),
                    np.where(s_idx <= j_idx, 0.0, MASK_NEG).astype(np.float32),
                    np.where(s_idx >= j_idx, 0.0, MASK_NEG).astype(np.float32)], 1)
    in_maps = []
    for i in range(NCORES):
        b, h = i // 4, i % 4
        seq = np.concatenate([P_ctx[b], P_lat[b]], 0)
        qs, ks = slice(h * 64, (h + 1) * 64), slice(256 + h * 64, 256 + (h + 1) * 64)
        tm = lambda a: np.ascontiguousarray(a.reshape(NCH, 128, -1).transpose(1, 0, 2))
        gcols = [1024 + 0 * 8 + 0 * 4 + h, 1024 + 0 * 8 + 1 * 4 + h, 1024 + 1 * 8 + 0 * 4 + h, 1024 + 1 * 8 + 1 * 4 + h]
        gb = np.array([gate_b[0, 0, h], gate_b[0, 1, h], gate_b[1, 0, h], gate_b[1, 1, h]], np.float32)
        cw = np.concatenate([conv_w[:, qs].T, conv_b[qs][:, None], conv_w[:, ks].T, conv_b[ks][:, None]], 1).astype(np.float32)
        in_maps.append({"qpT": np.ascontiguousarray(seq[:, qs].T), "kpT": np.ascontiguousarray(seq[:, ks].T),
                        "v": tm(seq[:, 512 + h * 64:512 + (h + 1) * 64]), "og": tm(seq[:, 768 + h * 64:768 + (h + 1) * 64]),
                        "g4": tm(seq[:, gcols]), "gb": np.ascontiguousarray(np.tile(gb, NCH)), "cw": np.ascontiguousarray(cw),
                        "nw": np.ascontiguousarray(norm_w[h * 64:(h + 1) * 64]), "cst": np.ascontiguousarray(cst)})
    res = _run(nc, in_maps)
    ym_l = np.zeros((B, n, 256), np.float32); ym_c = np.zeros((B, nctx, 256), np.float32)
    for i in range(NCORES):
        b, h = i // 4, i % 4
        y = res.results[i]["o_ym"].transpose(1, 0, 2).reshape(NSEQ, 64)
        ym_c[b, :, h * 64:(h + 1) * 64] = y[:nctx]
        ym_l[b, :, h * 64:(h + 1) * 64] = y[nctx:]
    return ym_l, ym_c


HCH = 32
TWO_PI = 2.0 * np.pi
RND_MAGIC = 12582912.0


def fft_tables(N1):
    N2 = 128
    N = N1 * N2
    ar = np.arange
    c, s = np.cos, np.sin
    th = TWO_PI * ar(N1)[:, None] * ar(N1)[None] / N1
    F1c = np.concatenate([c(th), -s(th)], 1)
    th = TWO_PI * ar(N2)[:, None] * ar(N1)[None] / N
    twRR = np.concatenate([c(th), c(th)], 1); twII = np.concatenate([-s(th), -s(th)], 1)
    th = TWO_PI * ar(N2)[:, None] * ar(N2)[None] / N2
    F2re, F2im, nF2im = c(th), -s(th), s(th)
    G2c = np.concatenate([c(th), s(th)], 1); G2s = np.concatenate([-s(th), c(th)], 1)
    th = TWO_PI * ar(N1)[:, None] * ar(N2)[None] / N
    twcRR = np.concatenate([c(th), c(th)], 1); twcII = np.concatenate([s(th), s(th)], 1)
    th = TWO_PI * ar(N1)[:, None] * ar(N1 // 2)[None] / N1
    G1re, nG1im = c(th) / N, -s(th) / N
    f = lambda a: np.ascontiguousarray(a.astype(np.float32))
    return dict(F1c=f(F1c), twRR=f(twRR), twII=f(twII), F2re=f(F2re), F2im=f(F2im), nF2im=f(nF2im), G2c=f(G2c), G2s=f(G2s),
                twcRR=f(twcRR), twcII=f(twcII), G1re=f(G1re), nG1im=f(nG1im))


TAB_ORDER = ["F1c", "twRR", "twII", "F2re", "F2im", "nF2im", "G2c", "G2s", "twcRR", "twcII", "G1re", "nG1im"]


def hyena_consts(n):
    N = 2 * n
    tau = np.arange(N)
    pos = np.where(tau < n, tau, N - tau).astype(np.float32)
    t = (pos / np.float32(n)).astype(np.float32)
    bands = np.arange(1, 17, dtype=np.float32)
    ang = (np.float32(TWO_PI) * t[:, None] * bands).astype(np.float32)
    feats = np.concatenate([t[:, None], np.cos(ang), np.sin(ang)], -1).astype(np.float32)
    lt = abs(np.log(1e-2))
    deltas = np.linspace(lt / 1.5, lt / 0.3, 256, dtype=np.float32)
    win = (np.exp(-t[:, None] * deltas) + np.float32(0.05)).astype(np.float32)
    win[n] = 0.0
    return np.ascontiguousarray(feats.T), np.ascontiguousarray(win.T)


def interleave(gens, width):
    it = iter(gens)
    active = []
    while True:
        while len(active) < width:
            g = next(it, None)
            if g is None:
                break
            active.append(g)
        if not active:
            return
        for g in list(active):
            try:
                next(g)
            except StopIteration:
                active.remove(g)
        yield


def build_k4(sizes):
    nc = bass.Bass("TRN2", target_bir_lowering=False)
    B = 2
    dr = {}
    for si, n in enumerate(sizes):
        N1 = 2 * n // 128
        dr[si] = dict(
            u=nc.dram_tensor(f"u{si}", [3, HCH, B, n + 2], F32, kind="ExternalInput").ap(),
            feats=nc.dram_tensor(f"feats{si}", [33, 2 * n], F32, kind="ExternalInput").ap(),
            win=nc.dram_tensor(f"win{si}", [64, 2 * n], F32, kind="ExternalInput").ap(),
            taps=nc.dram_tensor(f"taps{si}", [64, 2 * n], F32, kind="ExternalOutput").ap(),
            out=nc.dram_tensor(f"o_yh{si}", [HCH, B, n], F32, kind="ExternalOutput").ap(),
            tabs={k: nc.dram_tensor(f"t{si}_{k}", list(v.shape), F32, kind="ExternalInput").ap()
                  for k, v in fft_tables(N1).items()})
    mlpd = nc.dram_tensor("mlp", [64, 64 + 64 + 128 + 4], F32, kind="ExternalInput").ap()
    cwd = nc.dram_tensor("cwv", [3 * HCH * 4], F32, kind="ExternalInput").ap()
    skd = nc.dram_tensor("skv", [2 * HCH], F32, kind="ExternalInput").ap()
    cstd = nc.dram_tensor("cst", [128, 256], F32, kind="ExternalInput").ap()
    with ExitStack() as es:
        P = Prog(nc, es)
        sb, ps, ring = mk_alloc(nc, es)
        banks = [(ps(f"bank{i}", [128, 512], F32), f"bank{i}") for i in range(8)]
        cst = sb("cst_s", [128, 256]); P.dma("sync", cst[:], cstd, writes=["cst"])
        ident, ones = cst[:, 0:128], cst[:, 128:256]
        mlp = sb("mlp_s", [64, 260]); P.dma("sync", mlp[:], mlpd, writes=["mlp"])
        w1, w2 = mlp[0:33, 0:64], mlp[:, 64:128]
        w3 = [mlp[:, 128:192], mlp[:, 192:256]]
        b1, f0, b2, f1 = (mlp[:, 256 + j:257 + j] for j in range(4))
        cwb = sb("cwb", [128, 3 * HCH * 4]); P.dma("sync", cwb[:], bcast_rows(cwd, 128), writes=["cwb"])
        skb = sb("skb", [128, 2 * HCH]); P.dma("sync", skb[:], bcast_rows(skd, 128), writes=["skb"])
        a1r = ring("a1", 4, [64, 512]); rrr = ring("rr", 4, [64, 512]); hhr = ring("hh", 4, [64, 512])
        ftr = ring("ft", 4, [33, 512]); wnr = ring("wn", 4, [64, 512]); tpr = ring("tp", 4, [64, 512])
        As_r = ring("As", 4, [128, 256]); t1r = ring("t1", 4, [128, 256]); t2r = ring("t2", 4, [128, 256])
        Br = ring("Bc", 4, [128, 256]); Yr = ring("Yc", 4, [128, 256]); Dr = ring("Dc", 4, [128, 256])
        pr4 = [ring(f"pp{j}", 4, [128, 128]) for j in range(4)]
        xir = ring("xi", 4, [64, 3, 130]); cvr = ring("cv", 4, [64, 3, 128]); tgr = ring("tg", 4, [64, 128])
        z1r = ring("z1", 4, [64, 128]); z2r = ring("z2", 4, [64, 128]); tlr = ring("tl", 4, [128, 128])
        bkA = Ring(banks[0:2]); bkX = Ring(banks[2:4]); bkC = Ring(banks[4:6]); bkY = Ring(banks[6:8])

        def sin_layer(psrc, pk, n_, bias, freq, dst, dstk):
            a1, a1k = a1r.next(); rr, rrk = rrr.next()
            P.ts("vector", a1[:, 0:n_], psrc, bias, freq, ALU.add, ALU.mult, reads=[pk, "mlp"], writes=[a1k])
            yield
            P.ts("gpsimd", rr[:, 0:n_], a1[:, 0:n_], 1.0 / TWO_PI, RND_MAGIC, ALU.mult, ALU.add, reads=[a1k], writes=[rrk])
            yield
            P.ts("gpsimd", rr[:, 0:n_], rr[:, 0:n_], RND_MAGIC, -TWO_PI, ALU.subtract, ALU.mult, reads=[rrk], writes=[rrk])
            yield
            P.tt("gpsimd", rr[:, 0:n_], rr[:, 0:n_], a1[:, 0:n_], ALU.add, reads=[rrk, a1k], writes=[rrk])
            yield
            P.ts("gpsimd", rr[:, 0:n_], rr[:, 0:n_], np.pi, -np.pi, ALU.min, ALU.max, reads=[rrk], writes=[rrk])
            yield
            P.act(dst, rr[:, 0:n_], AF.Sin, reads=[rrk], writes=[dstk])
            yield

        def cmul(src, srck, n1p, W, tRR, tII, tk, dst, dstk):
            t1, t1k = t1r.next(); t2, t2k = t2r.next()
            P.tt("vector", t1[0:n1p, 0:2 * W], src, tRR, ALU.mult, reads=[srck, tk], writes=[t1k])
            yield
            P.tt("gpsimd", t2[0:n1p, 0:2 * W], src, tII, ALU.mult, reads=[srck, tk], writes=[t2k])
            yield
            P.tt("vector", dst[0:n1p, 0:W], t1[0:n1p, 0:W], t2[0:n1p, W:2 * W], ALU.subtract, reads=[t1k, t2k], writes=[dstk + "r"])
            yield
            P.tt("gpsimd", dst[0:n1p, W:2 * W], t2[0:n1p, 0:W], t1[0:n1p, W:2 * W], ALU.add, reads=[t1k, t2k], writes=[dstk + "i"])
            yield

        def size_body(si, n):
            N = 2 * n
            N1 = N // 128
            Kd = N1 // 2
            d = dr[si]
            T = {}
            for k in TAB_ORDER:
                shp = list(d["tabs"][k].shape)
                T[k] = sb(f"T{si}_{k}", shp)
                P.dma("sync", T[k][:], d["tabs"][k], writes=[f"tab{si}"] if k == TAB_ORDER[-1] else [f"tab{si}_{k}"])
                yield
            tabk = [f"tab{si}"] + [f"tab{si}_{k}" for k in TAB_ORDER[:-1]]
            CH = min(512, n)
            nchunk = N // CH
            l1p = sb(f"l1p{si}", [64, nchunk])
            def mlp_chain(ci):
                c0 = ci * CH
                dirn = 0 if c0 < n else 1
                ft, ftk = ftr.next(); P.dma("sync", ft[:, 0:CH], d["feats"][:, c0:c0 + CH], writes=[ftk])
                wn, wnk = wnr.next(); P.dma("sync", wn[:, 0:CH], d["win"][:, c0:c0 + CH], writes=[wnk])
                bk, bkk = bkA.next()
                P.mm(bk[0:64, 0:CH], w1, ft[:, 0:CH], True, True, reads=[ftk, "mlp"], writes=[bkk])
                yield
                h1, h1k = hhr.next()
                yield from sin_layer(bk[0:64, 0:CH], bkk, CH, b1, f0, h1[:, 0:CH], h1k)
                bk, bkk = bkX.next()
                P.mm(bk[0:64, 0:CH], w2, h1[:, 0:CH], True, True, reads=[h1k, "mlp"], writes=[bkk])
                yield
                h2, h2k = hhr.next()
                yield from sin_layer(bk[0:64, 0:CH], bkk, CH, b2, f1, h2[:, 0:CH], h2k)
                bk, bkk = bkC.next()
                P.mm(bk[0:64, 0:CH], w3[dirn], h2[:, 0:CH], True, True, reads=[h2k, "mlp"], writes=[bkk])
                yield
                tp, tpk = tpr.next()
                P.tt("vector", tp[:, 0:CH], bk[0:64, 0:CH], wn[:, 0:CH], ALU.mult, reads=[bkk, wnk], writes=[tpk])
                yield
                P.gen("vector", lambda e, tp=tp, ci=ci, CH=CH, l1p=l1p: e.reduce_sum(out=l1p[:, ci:ci + 1], in_=tp[:, 0:CH], axis=AX.X,
                                                                         apply_absolute_value=True), reads=[tpk], writes=[f"l1p{si}_{ci}"])
                yield
                P.dma("sync", d["taps"][:, c0:c0 + CH], tp[:, 0:CH], reads=[tpk], writes=[f"taps{si}"], final=True)
                yield
            yield from interleave([mlp_chain(ci) for ci in range(nchunk)], 2)
            l1 = sb(f"l1_{si}", [64, 2])
            P.gen("vector", lambda e, l1=l1, l1p=l1p: e.reduce_sum(out=l1[:, 0:1], in_=l1p[:], axis=AX.X),
                  reads=[f"l1p{si}_{ci}" for ci in range(nchunk)], writes=[f"l1{si}"])
            yield
            P.gen("vector", lambda e, l1=l1: e.reciprocal(out=l1[:, 1:2], in_=l1[:, 0:1]), reads=[f"l1{si}"], writes=[f"l1{si}"])
            yield
            dg = sb(f"dg{si}", [64, 64])
            P.ts("vector", dg[:], ident[0:64, 0:64], l1[:, 1:2], None, ALU.mult, reads=["cst", f"l1{si}"], writes=[f"dg{si}"])
            yield
            bk, bkk = bkY.next()
            P.mm(bk[:, 0:64], ones[0:64, :], dg[:], True, True, reads=["cst", f"dg{si}"], writes=[bkk])
            yield
            rl1b = sb(f"rl1b{si}", [128, 64])
            P.copy("vector", rl1b[:], bk[:, 0:64], reads=[bkk], writes=[f"rl1b{si}"])
            yield

            def fwd_fft(xt, xk, Krows):
                bA, bAk = bkA.next()
                P.mm(bA[:, 0:2 * N1], xt, T["F1c"][0:Krows, :], True, True, reads=[xk] + tabk, writes=[bAk])
                yield
                As, Ask = As_r.next()
                P.copy("scalar", As[:, 0:2 * N1], bA[:, 0:2 * N1], reads=[bAk], writes=[Ask])
                yield
                Bc, Bck = Br.next()
                yield from cmul(As[:, 0:2 * N1], Ask, 128, N1, T["twRR"][:], T["twII"][:], tabk[0], Bc, Bck)
                bX, bXk = bkX.next()
                Bre, Bim = Bc[:, 0:N1], Bc[:, N1:2 * N1]
                P.mm(bX[:, 0:N1], T["F2re"][:], Bre, True, False, reads=[Bck + "r"] + tabk, writes=[bXk])
                yield
                P.mm(bX[:, 0:N1], T["nF2im"][:], Bim, False, True, reads=[Bck + "i"] + tabk, writes=[bXk])
                yield
                P.mm(bX[:, N1:2 * N1], T["F2re"][:], Bim, True, False, reads=[Bck + "i"] + tabk, writes=[bXk])
                yield
                P.mm(bX[:, N1:2 * N1], T["F2im"][:], Bre, False, True, reads=[Bck + "r"] + tabk, writes=[bXk])
                yield
                return bX, bXk

            H = sb(f"H{si}", [128, 64, 2 * N1])
            def filt_chain(oc):
                tl, tlk = tlr.next()
                P.dma("sync", tl[0:N1, :], d["taps"][oc].rearrange("(a b) -> a b", b=128), reads=[f"taps{si}"], writes=[tlk])
                yield
                bX, bXk = yield from fwd_fft(tl[0:N1, :], tlk, N1)
                P.ts("vector", H[:, oc, :], bX[:, 0:2 * N1], rl1b[:, oc:oc + 1], None, ALU.mult, reads=[bXk, f"rl1b{si}"], writes=[f"H{si}_{oc}"])
                yield

            yield from interleave([filt_chain(oc) for oc in range(64)], 2)
            def long_conv(zt, zk, o, c):
                bX, bXk = yield from fwd_fft(zt, zk, Kd)
                oc = o * HCH + c
                Hre, Him = H[:, oc, 0:N1], H[:, oc, N1:2 * N1]
                hk = f"H{si}_{oc}"
                pp = [r.next() for r in pr4]
                P.tt("vector", pp[0][0][:, 0:N1], bX[:, 0:N1], Hre, ALU.mult, reads=[bXk, hk], writes=[pp[0][1]])
                yield
                P.tt("vector", pp[1][0][:, 0:N1], bX[:, N1:2 * N1], Him, ALU.mult, reads=[bXk, hk], writes=[pp[1][1]])
                yield
                P.tt("vector", pp[2][0][:, 0:N1], bX[:, 0:N1], Him, ALU.mult, reads=[bXk, hk], writes=[pp[2][1]])
                yield
                P.tt("vector", pp[3][0][:, 0:N1], bX[:, N1:2 * N1], Hre, ALU.mult, reads=[bXk, hk], writes=[pp[3][1]])
                yield
                Yc, Yck = Yr.next()
                P.tt("gpsimd", Yc[:, 0:N1], pp[0][0][:, 0:N1], pp[1][0][:, 0:N1], ALU.subtract, reads=[pp[0][1], pp[1][1]], writes=[Yck + "r"])
                yield
                P.tt("gpsimd", Yc[:, N1:2 * N1], pp[2][0][:, 0:N1], pp[3][0][:, 0:N1], ALU.add, reads=[pp[2][1], pp[3][1]], writes=[Yck + "i"])
                yield
                bC, bCk = bkC.next()
                P.mm(bC[0:N1, 0:256], Yc[:, 0:N1], T["G2c"][:], True, False, reads=[Yck + "r"] + tabk, writes=[bCk])
                yield
                P.mm(bC[0:N1, 0:256], Yc[:, N1:2 * N1], T["G2s"][:], False, True, reads=[Yck + "i"] + tabk, writes=[bCk])
                yield
                Cs, Csk = As_r.next()
                P.copy("scalar", Cs[0:N1, :], bC[0:N1, 0:256], reads=[bCk], writes=[Csk])
                yield
                Dc, Dck = Dr.next()
                yield from cmul(Cs[0:N1, :], Csk, N1, 128, T["twcRR"][:], T["twcII"][:], tabk[0], Dc, Dck)
                bY, bYk = bkY.next()
                P.mm(bY[0:Kd, 0:128], T["G1re"][:], Dc[0:N1, 0:128], True, False, reads=[Dck + "r"] + tabk, writes=[bYk])
                yield
                P.mm(bY[0:Kd, 0:128], T["nG1im"][:], Dc[0:N1, 128:256], False, True, reads=[Dck + "i"] + tabk, writes=[bYk])
                yield
                return bY, bYk

            def data_chain(c, b):
                xi, xik = xir.next()
                src = bass.AP(tensor=d["u"].tensor, offset=d["u"][0, c, b, 0].offset,
                              ap=[[128, Kd], [HCH * B * (n + 2), 3], [1, 130]])
                P.dma("gpsimd", xi[0:Kd], src, writes=[xik])
                yield
                cv, cvk = cvr.next()
                for p in range(3):
                    wo = (p * HCH + c) * 4
                    P.ts("vector", cv[0:Kd, p, :], xi[0:Kd, p, 1:129], cwb[0:Kd, wo + 1:wo + 2], cwb[0:Kd, wo + 3:wo + 4], ALU.mult, ALU.add,
                         reads=[xik, "cwb"], writes=[cvk + str(p)])
                    yield
                    P.stt("vector", cv[0:Kd, p, :], xi[0:Kd, p, 0:128], cwb[0:Kd, wo:wo + 1], cv[0:Kd, p, :], ALU.mult, ALU.add,
                          reads=[xik, "cwb", cvk + str(p)], writes=[cvk + str(p)])
                    yield
                    P.stt("vector", cv[0:Kd, p, :], xi[0:Kd, p, 2:130], cwb[0:Kd, wo + 2:wo + 3], cv[0:Kd, p, :], ALU.mult, ALU.add,
                          reads=[xik, "cwb", cvk + str(p)], writes=[cvk + str(p)])
                    yield
                bY, bYk = yield from long_conv(cv[0:Kd, 0, :], cvk + "0", 0, c)
                tg, tgk = tgr.next()
                P.stt("vector", tg[0:Kd, :], cv[0:Kd, 0, :], skb[0:Kd, c:c + 1], bY[0:Kd, 0:128], ALU.mult, ALU.add,
                      reads=[cvk + "0", "skb", bYk], writes=[tgk])
                yield
                z1, z1k = z1r.next()
                P.tt("gpsimd", z1[0:Kd, :], tg[0:Kd, :], cv[0:Kd, 1, :], ALU.mult, reads=[tgk, cvk + "1"], writes=[z1k])
                yield
                bY, bYk = yield from long_conv(z1[0:Kd, :], z1k, 1, c)
                tg, tgk = tgr.next()
                P.stt("vector", tg[0:Kd, :], z1[0:Kd, :], skb[0:Kd, HCH + c:HCH + c + 1], bY[0:Kd, 0:128], ALU.mult, ALU.add,
                      reads=[z1k, "skb", bYk], writes=[tgk])
                yield
                z2, z2k = z2r.next()
                P.tt("gpsimd", z2[0:Kd, :], tg[0:Kd, :], cv[0:Kd, 2, :], ALU.mult, reads=[tgk, cvk + "2"], writes=[z2k])
                yield
                P.dma("sync", d["out"][c, b].rearrange("(a b) -> a b", b=128), z2[0:Kd, :], reads=[z2k], writes=[z2k + "d"], final=True)
                yield
            yield from interleave([data_chain(c, b) for c in range(HCH) for b in range(B)], 2)

        for si, n in enumerate(sizes):
            for _ in size_body(si, n):
                pass
        P.emit()
    return nc


def run_k4(hy_list, conv_w, conv_b, fparams, skip):
    f_w1, f_b1, f_freq, f_w2, f_b2, f_w3 = fparams
    sizes = [h.shape[1] for h in hy_list]
    nc = build_k4(sizes)
    B = 2
    cst = np.concatenate([np.eye(128, dtype=np.float32), np.ones((128, 128), np.float32)], 1)
    consts = [hyena_consts(n) for n in sizes]
    tabs = [fft_tables(2 * n // 128) for n in sizes]
    w3r = f_w3.reshape(64, 2, 2, 256)
    in_maps = []
    for i in range(NCORES):
        cs = slice(i * HCH, (i + 1) * HCH)
        m = {"cst": cst}
        mlp = np.zeros((64, 260), np.float32)
        mlp[0:33, 0:64] = f_w1
        mlp[:, 64:128] = f_w2
        mlp[:, 128:192] = w3r[:, 0, :, cs].reshape(64, 64)
        mlp[:, 192:256] = w3r[:, 1, :, cs].reshape(64, 64)
        mlp[:, 256] = f_b1; mlp[:, 257] = f_freq[0]; mlp[:, 258] = f_b2; mlp[:, 259] = f_freq[1]
        m["mlp"] = mlp
        cw = np.zeros((3, HCH, 4), np.float32)
        for p in range(3):
            ch = slice(p * 256 + i * HCH, p * 256 + (i + 1) * HCH)
            cw[p, :, 0:3] = conv_w[:, ch].T
            cw[p, :, 3] = conv_b[ch]
        m["cwv"] = cw.reshape(-1)
        m["skv"] = np.ascontiguousarray(skip[:, cs]).reshape(-1)
        for si, (hy, n) in enumerate(zip(hy_list, sizes)):
            u = np.zeros((3, HCH, B, n + 2), np.float32)
            for p in range(3):
                u[p, :, :, 1:n + 1] = hy[:, :, p * 256 + i * HCH:p * 256 + (i + 1) * HCH].transpose(2, 0, 1)
            m[f"u{si}"] = u
            feats, win = consts[si]
            m[f"feats{si}"] = feats
            m[f"win{si}"] = np.ascontiguousarray(np.tile(win[cs], (2, 1)))
            for k, v in tabs[si].items():
                m[f"t{si}_{k}"] = v
        in_maps.append(m)
    res = _run(nc, in_maps)
    outs = []
    for si, n in enumerate(sizes):
        y = np.zeros((B, n, 256), np.float32)
        for i in range(NCORES):
            y[:, :, i * HCH:(i + 1) * HCH] = res.results[i][f"o_yh{si}"].transpose(1, 2, 0)
        outs.append(y)
    return outs, [np.concatenate([res.results[i][f"taps{si}"] for i in range(NCORES)], 0) for si in range(len(sizes))]


def kernel(x, c, ctx, c_ctx, w_mod, b_mod, w_in, mlstm_conv_w, mlstm_conv_b, mlstm_gate_b,
           mlstm_norm_w, attn_q_norm_w, attn_k_norm_w, hyena_conv_w, hyena_conv_b,
           hyena_f_w1, hyena_f_b1, hyena_f_freq, hyena_f_w2, hyena_f_b2, hyena_f_w3,
           hyena_skip, w_out, ln_mix_w, ln_mix_b, router_w, router_b,
           exp_w_gate, exp_w_up, exp_w_down, ln_ffn_w, ln_ffn_b):
    f = lambda a: np.asarray(a, dtype=np.float32)
    x, c, ctx, c_ctx = f(x), f(c), f(ctx), f(c_ctx)
    mod = run_k0(c, c_ctx, f(w_mod), f(b_mod))
    depth = w_in.shape[0]
    for l in range(depth):
        last = l == depth - 1
        P_lat, P_ctx = run_k1(x, ctx, mod[l], f(w_in[l]))
        ym_l, ym_c = run_k2(P_lat, P_ctx, f(mlstm_conv_w[l]), f(mlstm_conv_b[l]), f(mlstm_gate_b[l]), f(mlstm_norm_w[l]))
        ya_l, ya_c = run_k3(P_lat, P_ctx, f(attn_q_norm_w[l]), f(attn_k_norm_w[l]))
        hy = [P_lat[..., 1808:]] + ([] if last else [P_ctx[..., 1808:]])
        fpar = (f(hyena_f_w1[l]), f(hyena_f_b1[l]), f(hyena_f_freq[l]), f(hyena_f_w2[l]), f(hyena_f_b2[l]), f(hyena_f_w3[l]))
        yhs, _ = run_k4(hy, f(hyena_conv_w[l]), f(hyena_conv_b[l]), fpar, f(hyena_skip[l]))
        yh_c = np.zeros_like(ym_c) if last else yhs[1]
        ycat_l = np.concatenate([ym_l, ya_l, yhs[0]], -1)
        ycat_c = np.concatenate([ym_c, ya_c, yh_c], -1)
        x1, c1, h2, h2c, aff, affc = run_k5(ycat_l, ycat_c, x, ctx, mod[l], f(w_out[l]), f(ln_mix_w[l]), f(ln_mix_b[l]),
                                            f(router_w[l]), f(router_b[l]))
        thr, thrc = run_k6(aff, affc)
        x, ctx = run_moe(h2, h2c, aff, affc, thr, thrc, x1, c1, mod[l], f(ln_ffn_w[l]), f(ln_ffn_b[l]),
                         f(exp_w_gate[l]), f(exp_w_up[l]), f(exp_w_down[l]))
    return x.astype(np.float32)


CAP_L, CAP_C = 1024, 32
SLOTS_B = CAP_L + CAP_C
NTOK_ALL = 2 * 8192 + 2 * 256


def build_k7g():
    nc = bass.Bass("TRN2", target_bir_lowering=False)
    T = 18
    TOK = T * 128
    h2d = nc.dram_tensor("h2all", [NTOK_ALL, 1024], BF16, kind="ExternalInput").ap()
    affLd = nc.dram_tensor("affL", [2, 2, 128, 66], F32, kind="ExternalInput").ap()
    thrd = nc.dram_tensor("thr8", [8], F32, kind="ExternalInput").ap()
    tidd = nc.dram_tensor("tid", [2, 128, 66], F32, kind="ExternalInput").ap()
    iotad = nc.dram_tensor("iota", [128, CAP_L + 128], F32, kind="ExternalInput").ap()
    cstd = nc.dram_tensor("cst", [128, 448], F32, kind="ExternalInput").ap()
    wgd = nc.dram_tensor("wg", [2, 128, 8, D_FF], F32, kind="ExternalInput").ap()
    wud = nc.dram_tensor("wu", [2, 128, 8, D_FF], F32, kind="ExternalInput").ap()
    wdd = nc.dram_tensor("wd", [2, 128, NFC, 1024], F32, kind="ExternalInput").ap()
    Yo = nc.dram_tensor("o_Y", [2, TOK, 1024], BF16, kind="ExternalOutput").ap()
    posKo = nc.dram_tensor("o_pos", [2, 2, 128, 66], I32, kind="ExternalOutput").ap()
    with ExitStack() as es:
        P = Prog(nc, es)
        sb, ps, ring = mk_alloc(nc, es)
        banks = [(ps(f"bk{i}", [128, 512], F32), f"bk{i}") for i in range(8)]
        cst = sb("cst_s", [128, 448]); P.dma("sync", cst[:], cstd, writes=["cst"])
        Ust, ones, identf, Ust64 = cst[:, 0:128], cst[:, 128:256], cst[:, 256:384], cst[0:64, 384:448]
        idb = sb("idb", [128, 128], BF16); P.copy("vector", idb[:], identf, reads=["cst"], writes=["idb"])
        thrt = sb("thrt", [128, 8]); P.dma("sync", thrt[:], bcast_rows(thrd, 128), writes=["thrt"])
        iota = sb("iota_s", [128, CAP_L + 128]); P.dma("sync", iota[:], iotad, writes=["iota"])
        h2T = sb("h2T_s", [128, 8, TOK], BF16)
        acc = sb("acc", [128, T, 1024])
        tv = sb("tv", [128, T])
        Ar = ring("A", 2, [128, 66]); Mr = ring("M", 2, [128, 66]); wir = ring("wi", 2, [128, 66]); pfr = ring("pf", 2, [128, 66])
        m2r = ring("m2", 2, [128, 66]); pir = ring("pi", 2, [128, 66], I32)
        tdfr = ring("tdf", 2, [128, 66]); TAr = ring("TA", 2, [128, 66, 2]); selr = ring("sel", 4, [128, 128]); lfr = ring("lf", 2, [128, 18])
        tcr = ring("tc", 2, [64, 1]); tbr = ring("tb", 2, [64, 128])
        lsr = ring("ls", 2, [128, 9], I32)
        xsr = ring("xs", 2, [128, 1024], BF16)
        FG = 4
        wgr = ring("wgb", 2, [128, 8, FG * 128], BF16); wur = ring("wub", 2, [128, 8, FG * 128], BF16)
        wdr = ring("wdb", 2, [128, FG, 1024], BF16); stg = ring("stg", 3, [128, 1024])
        actr = ring("actT", 2, [128, FG, 512], BF16); sgr = ring("sg", 2, [128, 512]); yor = ring("yrow", 2, [128, 1024], BF16)
        pgr = Ring(banks[0:2]); pur = Ring(banks[2:4]); pyr = Ring(banks[4:6])
        tgs = [(s, min(512, TOK - s)) for s in range(0, TOK, 512)]
        for e in range(2):
            P.memset("gpsimd", h2T[:], 0.0, writes=["h2T"])
            P.memset("vector", tv[:], 0.0, writes=["tv"])
            for t in range(T):
                P.memset("gpsimd", acc[:, t, :], 0.0, writes=[f"acc{t}a", f"acc{t}b"])
            for b in range(2):
                A, Ak = Ar.next(); P.dma("sync", A[:], affLd[e, b], writes=[Ak])
                M, Mk = Mr.next()
                to = (e * 2 + b) * 2
                P.ts("vector", M[:, 0:64], A[:, 0:64], thrt[:, to:to + 1], None, ALU.is_ge, reads=[Ak, "thrt"], writes=[Mk + "l"])
                P.ts("vector", M[:, 64:66], A[:, 64:66], thrt[:, to + 1:to + 2], None, ALU.is_ge, reads=[Ak, "thrt"], writes=[Mk + "c"])
                wi, wik = wir.next(); pf, pfk = pfr.next(); m2, m2k = m2r.next(); pi, pik = pir.next()
                for (c0, ncol, cap, base, sfx) in ((0, 64, CAP_L, 0, "l"), (64, 2, CAP_C, CAP_L, "c")):
                    bw, bwk = banks[6]
                    P.mm(bw[:, c0:c0 + ncol], Ust, M[:, c0:c0 + ncol], True, True, reads=["cst", Mk + sfx], writes=[bwk + sfx])
                    P.copy("scalar", wi[:, c0:c0 + ncol], bw[:, c0:c0 + ncol], reads=[bwk + sfx], writes=[wik + sfx])
                    bt, btk = banks[7]
                    P.mm(bt[0:ncol, c0:c0 + 1], M[:, c0:c0 + ncol], ones[:, 0:1], True, True, reads=["cst", Mk + sfx], writes=[btk + "t" + sfx])
                    tc, tck = tcr.next()
                    P.copy("vector", tc[0:ncol, :], bt[0:ncol, c0:c0 + 1], reads=[btk + "t" + sfx], writes=[tck])
                    tb, tbk = tbr.next()
                    P.ts("vector", tb[0:ncol, :], ones[0:ncol, :], tc[0:ncol, 0:1], None, ALU.mult, reads=["cst", tck], writes=[tbk])
                    P.mm(bt[:, 128 + c0:128 + c0 + ncol], tb[0:ncol, :], Ust64[0:ncol, 0:ncol], True, True, reads=[tbk, "cst"], writes=[btk + "o" + sfx])
                    sl = slice(c0, c0 + ncol)
                    P.tt("vector", pf[:, sl], bt[:, 128 + c0:128 + c0 + ncol], wi[:, sl], ALU.add, reads=[btk + "o" + sfx, wik + sfx], writes=[pfk + sfx])
                    P.ts("vector", m2[:, sl], pf[:, sl], float(cap) - 0.5, None, ALU.is_lt, reads=[pfk + sfx], writes=[m2k + sfx])
                    P.tt("vector", m2[:, sl], m2[:, sl], M[:, sl], ALU.mult, reads=[m2k + sfx, Mk + sfx], writes=[m2k + sfx])
                    P.ts("vector", pf[:, sl], pf[:, sl], float(base - SLOTS_B), None, ALU.add, reads=[pfk + sfx], writes=[pfk + sfx])
                    P.tt("vector", pf[:, sl], pf[:, sl], m2[:, sl], ALU.mult, reads=[pfk + sfx, m2k + sfx], writes=[pfk + sfx])
                    P.ts("vector", pf[:, sl], pf[:, sl], float(SLOTS_B), None, ALU.add, reads=[pfk + sfx], writes=[pfk + sfx])
                P.copy("vector", pi[:], pf[:], reads=[pfk + "l", pfk + "c"], writes=[pik])
                P.dma("sync", posKo[e, b], pi[:], reads=[pik], writes=[pik + "d"], final=True)
                TA, TAk = TAr.next()
                tdf, tdfk = tdfr.next()
                P.dma("sync", tdf[:], tidd[b], writes=[tdfk])
                P.copy("gpsimd", TA[:, :, 0], tdf[:], reads=[tdfk], writes=[TAk + "t"])
                P.copy("gpsimd", TA[:, :, 1], A[:], reads=[Ak], writes=[TAk + "a"])
                pc, pck = banks[7]
                for c in range(9):
                    js = range(64) if c < 8 else (64, 65)
                    ns_ = 128 if c < 8 else 32
                    i0 = c * 128
                    sfx = "l" if c < 8 else "c"
                    for jj, j in enumerate(js):
                        se, sek = selr.next()
                        P.ts("vector", se[:, 0:ns_], iota[:, i0:i0 + ns_], pf[:, j:j + 1], None, ALU.is_equal, reads=["iota", pfk + sfx], writes=[sek])
                        P.mm(pc[0:ns_, 256 + 2 * c:256 + 2 * c + 2], se[:, 0:ns_], TA[:, j, :], jj == 0, jj == len(js) - 1,
                             reads=[sek, TAk + "t", TAk + "a"], writes=[pck + f"q{c}"])
                lf, lfk = lfr.next()
                P.copy("vector", lf[:], pc[:, 256:274], reads=[pck + f"q{c}" for c in range(9)], writes=[lfk])
                ls, lsk = lsr.next()
                P.copy("vector", ls[:], lf[:].rearrange("p (c t) -> p c t", t=2)[:, :, 0], reads=[lfk], writes=[lsk])
                for c in range(9):
                    npart = 128 if c < 8 else 32
                    p0 = 0
                    col0 = b * CAP_L + c * 128 if c < 8 else (16 + b) * 128
                    tcol = col0 // 128
                    P.copy("vector", tv[p0:p0 + npart, tcol:tcol + 1], lf[p0:p0 + npart, 2 * c + 1:2 * c + 2], reads=[lfk, "tv"], writes=["tv"])
                    xs, xsk = xsr.next()
                    P.op("gpsimd", lambda en, xs=xs, ls=ls, c=c, npart=npart, p0=p0: en.indirect_dma_start(
                        out=xs[p0:p0 + npart, :], out_offset=None, in_=h2d[:, :],
                        in_offset=bass.IndirectOffsetOnAxis(ap=ls[p0:p0 + npart, c:c + 1], axis=0)), reads=[lsk], writes=[xsk], dma=True)
                    pb, pbk = banks[6]
                    pT = pb[:, :].bitcast(BF16)
                    for kc in range(8):
                        P.tr(pT[:, kc * 128:kc * 128 + npart], xs[p0:p0 + npart, kc * 128:(kc + 1) * 128], idb[p0:p0 + npart, p0:p0 + npart],
                             reads=[xsk, "idb"], writes=[pbk + "l", pbk + "c"])
                    P.copy("scalar", h2T[:, :, col0:col0 + npart], pT[:, 0:1024].rearrange("p (k s) -> p k s", k=8)[:, :, 0:npart],
                           reads=[pbk + "l", pbk + "c"], writes=["h2T"])
            ci = 0
            for f0 in range(0, NFC, FG):
                nf = min(FG, NFC - f0)
                wgb, wgk = wgr.next(); wub, wuk = wur.next(); wdb, wdk = wdr.next()
                for (src, dst, dk) in ((wgd, wgb, wgk), (wud, wub, wuk)):
                    for kp in range(0, 8, 2):
                        st, sk = stg.next()
                        sv = st[:, 0:2 * nf * 128].rearrange("p (a f) -> p a f", a=2)
                        P.dma("sync", sv, src[e, :, kp:kp + 2, f0 * 128:(f0 + nf) * 128], writes=[sk])
                        P.copy("gpsimd" if ci % 2 else "vector", dst[:, kp:kp + 2, 0:nf * 128], sv, reads=[sk], writes=[dk + f"k{kp}"])
                        ci += 1
                for fc in range(nf):
                    st, sk = stg.next()
                    P.dma("sync", st[:], wdd[e, :, f0 + fc, :], writes=[sk])
                    P.copy("gpsimd" if ci % 2 else "vector", wdb[:, fc, :], st[:], reads=[sk], writes=[wdk + f"f{fc}"])
                    ci += 1
                for (s0, ns) in tgs:
                    actT, ak = actr.next()
                    for fc in range(nf):
                        pg, pgk = pgr.next(); pu, puk = pur.next()
                        for kc in range(8):
                            P.mm(pg[:, 0:ns], wgb[:, kc, fc * 128:(fc + 1) * 128], h2T[:, kc, s0:s0 + ns], kc == 0, kc == 7,
                                 reads=["h2T", wgk + f"k{kc - kc % 2}"], writes=[pgk])
                        for kc in range(8):
                            P.mm(pu[:, 0:ns], wub[:, kc, fc * 128:(fc + 1) * 128], h2T[:, kc, s0:s0 + ns], kc == 0, kc == 7,
                                 reads=["h2T", wuk + f"k{kc - kc % 2}"], writes=[puk])
                        sg, sgk = sgr.next()
                        P.act(sg[:, 0:ns], pg[:, 0:ns], AF.Silu, reads=[pgk], writes=[sgk])
                        P.tt("vector", actT[:, fc, 0:ns], pu[:, 0:ns], sg[:, 0:ns], ALU.mult, reads=[puk, sgk], writes=[ak + f"f{fc}"])
                    for tt in range(s0 // 128, (s0 + ns) // 128):
                        for hf in range(2):
                            py, pyk = pyr.next()
                            for fc in range(nf):
                                P.mm(py[:], actT[:, fc, tt * 128 - s0:(tt + 1) * 128 - s0], wdb[:, fc, hf * 512:(hf + 1) * 512],
                                     fc == 0, fc == nf - 1, reads=[ak + f"f{fc}", wdk + f"f{fc}"], writes=[pyk])
                            ah = f"acc{tt}" + "ab"[hf]
                            P.tt("vector", acc[:, tt, hf * 512:(hf + 1) * 512], py[:], acc[:, tt, hf * 512:(hf + 1) * 512], ALU.add,
                                 reads=[pyk, ah], writes=[ah])
            for t in range(T):
                yr_, yrk = yor.next()
                P.act(yr_[:], acc[:, t, :], AF.Copy, reads=[f"acc{t}a", f"acc{t}b", "tv"], writes=[yrk], scale=tv[:, t:t + 1])
                P.dma("sync", Yo[e, t * 128:(t + 1) * 128, :], yr_[:], reads=[yrk], writes=[yrk], final=True)
        P.emit()
    return nc


def build_k8():
    nc = bass.Bass("TRN2", target_bir_lowering=False)
    T = K1_TILES
    Yb = [nc.dram_tensor(f"Yb{e}", [SLOTS_B + 1, 1024], BF16, kind="ExternalInput").ap() for e in range(N_EXP)]
    idxd = nc.dram_tensor("idx", [T, 128, N_EXP], I32, kind="ExternalInput").ap()
    x1d = nc.dram_tensor("i_x1", [T, 128, 1024], F32, kind="ExternalInput").ap()
    rows = nc.dram_tensor("rows", [2, 1024], F32, kind="ExternalInput").ap()
    lnr = nc.dram_tensor("lnr", [2, 1024], F32, kind="ExternalInput").ap()
    x2o = nc.dram_tensor("o_x2", [T, 128, 1024], F32, kind="ExternalOutput").ap()
    with ExitStack() as es:
        P = Prog(nc, es)
        sb, ps, ring = mk_alloc(nc, es)
        rowt = sb("rowt", [128, 2, 1024]); lnt = sb("lnt", [128, 2, 1024])
        for j in range(2):
            P.dma("sync", rowt[:, j, :], bcast_rows(rows[j], 128), writes=[f"row{j}"])
            P.dma("sync", lnt[:, j, :], bcast_rows(lnr[j], 128), writes=[f"ln{j}"])
        idr = ring("idx", 2, [128, N_EXP], I32)
        gr = ring("g", 6, [128, 1024], BF16)
        accr = ring("acc", 2, [128, 1024]); acc2r = ring("accb", 2, [128, 1024])
        xr = ring("x", 2, [128, 1024])
        str_ = ring("st", 2, [128, 2, 6]); mvr = ring("mv", 2, [128, 2]); rsr = ring("rs", 2, [128, 1])
        for t in range(T):
            g_ = 0 if t < 16 else 1
            ix, ixk = idr.next(); P.dma("sync", ix[:], idxd[t], writes=[ixk])
            acc, ack = accr.next(); acc2, ac2k = acc2r.next()
            for e in range(N_EXP):
                gt, gk = gr.next()
                P.op("gpsimd", lambda en, gt=gt, ix=ix, e=e: en.indirect_dma_start(
                    out=gt[:, :], out_offset=None, in_=Yb[e][:, :], in_offset=bass.IndirectOffsetOnAxis(ap=ix[:, e:e + 1], axis=0)),
                    reads=[ixk], writes=[gk], dma=True)
                tgt, tk, eng = (acc, ack, "vector") if e % 2 == 0 else (acc2, ac2k, "gpsimd")
                if e < 2:
                    P.copy(eng, tgt[:], gt[:], reads=[gk], writes=[tk])
                else:
                    P.tt(eng, tgt[:], tgt[:], gt[:], ALU.add, reads=[tk, gk], writes=[tk])
            P.tt("vector", acc[:], acc[:], acc2[:], ALU.add, reads=[ack, ac2k], writes=[ack])
            x, xk = xr.next(); P.dma("sync", x[:], x1d[t], writes=[xk])
            P.tt("gpsimd", acc[:], acc[:], rowt[:, g_, :], ALU.mult, reads=[ack, f"row{g_}"], writes=[ack])
            P.stt("vector", acc[:], x[:], ALPHA, acc[:], ALU.mult, ALU.add, reads=[xk, ack], writes=[ack])
            st, _ = str_.next(); mv, _ = mvr.next(); rs, _ = rsr.next()
            emit_layernorm(P, acc[:], ack, x[:], xk, st, mv, rs, f"lnC{t % 2}")
            P.tt("gpsimd", acc[:], x[:], lnt[:, 0, :], ALU.mult, reads=[xk, "ln0"], writes=[ack])
            P.tt("gpsimd", x[:], acc[:], lnt[:, 1, :], ALU.add, reads=[ack, "ln1"], writes=[xk])
            P.dma("sync", x2o[t], x[:], reads=[xk], writes=[xk], final=True)
        P.emit()
    return nc


def run_moe(h2, h2c, aff, affc, thr, thrc, x1, c1, mod_l, ln_w, ln_b, wg, wu, wd):
    B, n, D = x1.shape
    nctx = c1.shape[1]
    E = aff.shape[-1]
    h2all = np.ascontiguousarray(np.concatenate([h2.reshape(B * n, D), h2c.reshape(B * nctx, D)], 0))
    affall = np.concatenate([aff.reshape(B * n, E), affc.reshape(B * nctx, E)], 0)
    s_idx, j_idx = np.meshgrid(np.arange(128), np.arange(128), indexing="ij")
    cst = np.zeros((128, 448), np.float32)
    cst[:, 0:128] = (s_idx < j_idx); cst[:, 128:256] = 1.0; cst[:, 256:384] = np.eye(128); cst[0:64, 384:448] = (s_idx < j_idx)[:64, :64]
    tid = np.zeros((B, 128, 66), np.float32)
    iota = np.full((128, CAP_L + 128), -1.0, np.float32)
    iota[:, 0:CAP_L] = np.arange(CAP_L)
    iota[:, CAP_L:CAP_L + 32] = CAP_L + np.arange(32)
    for b in range(B):
        tid[b, :, 0:64] = (b * n + np.arange(n)).reshape(64, 128).T
        tid[b, :, 64:66] = (B * n + b * nctx + np.arange(nctx)).reshape(2, 128).T
    nc = build_k7g()
    in_maps = []
    for i in range(NCORES):
        es = [2 * i, 2 * i + 1]
        affL = np.zeros((2, B, 128, 66), np.float32)
        thr8 = np.zeros((2, B, 2), np.float32)
        for el, e in enumerate(es):
            for b in range(B):
                affL[el, b, :, 0:64] = aff[b, :, e].reshape(64, 128).T
                affL[el, b, :, 64:66] = affc[b, :, e].reshape(2, 128).T
                thr8[el, b] = (thr[b, e], thrc[b, e])
        lw = lambda w, kch: np.ascontiguousarray(w.reshape(2, kch, 128, w.shape[-1]).transpose(0, 2, 1, 3))
        in_maps.append({"h2all": h2all, "affL": affL,
                        "thr8": thr8.reshape(-1), "tid": tid, "iota": iota, "cst": cst,
                        "wg": lw(wg[es], 8), "wu": lw(wu[es], 8), "wd": lw(wd[es], NFC)})
    res = _run(nc, in_maps)
    Yb = np.zeros((B, E, SLOTS_B + 1, D), h2all.dtype)
    posL = np.zeros((B, n, E), np.int32); posC = np.zeros((B, nctx, E), np.int32)
    for i in range(NCORES):
        Y = res.results[i]["o_Y"]; pos = res.results[i]["o_pos"]
        for el in range(2):
            e = 2 * i + el
            for b in range(B):
                Yb[b, e, 0:CAP_L] = Y[el, b * CAP_L:(b + 1) * CAP_L]
                Yb[b, e, CAP_L:SLOTS_B] = Y[el, (16 + b) * 128:(16 + b) * 128 + CAP_C]
                posL[b, :, e] = pos[el, b, :, 0:64].T.reshape(n)
                posC[b, :, e] = pos[el, b, :, 64:66].T.reshape(nctx)
    nc8 = build_k8()
    lnr = np.ascontiguousarray(np.stack([ln_w, ln_b]))
    in_maps = []
    for i in range(NCORES):
        b = i // 4
        rows = np.ascontiguousarray(np.stack([mod_l[b, 5120:6144], mod_l[2, 5120:6144]]))
        idx = tok_shard(posL, posC, i)
        idx[-1, 64:, :] = SLOTS_B
        m = {"idx": idx, "i_x1": tok_shard(x1, c1, i), "rows": rows, "lnr": lnr}
        for e in range(E):
            m[f"Yb{e}"] = np.ascontiguousarray(Yb[b, e])
        in_maps.append(m)
    res = _run(nc8, in_maps)
    return tok_unshard([r["o_x2"] for r in res.results], B, n, nctx)
```

```python
import numpy as np
from contextlib import ExitStack
import concourse.bass as bass
import concourse.mybir as mybir
from concourse.bass_utils import run_bass_kernel_spmd

F32 = mybir.dt.float32
BF16 = mybir.dt.bfloat16
I32 = mybir.dt.int32
AF = mybir.ActivationFunctionType
ALU = mybir.AluOpType
AX = mybir.AxisListType

NCORES = 8
STAGE_EXP = False
FUSE_WAIT = True
NO_RAW_SELF = False
SELF_SYNC = True


class Prog:
    ENGS = ("sync", "scalar", "vector", "gpsimd", "tensor")

    def __init__(self, nc, es, n_dma_sems=12):
        self.nc, self.es = nc, es
        self.ops = {e: [] for e in self.ENGS}
        self.esem = {}
        self.ecount = {}
        for e in ("scalar", "vector", "gpsimd", "tensor"):
            self.esem[e] = es.enter_context(nc.semaphore(f"sem_{e}"))
            self.ecount[e] = 0
        self.dpool = {}
        for q in ("sync", "scalar", "gpsimd"):
            self.dpool[q] = dict(
                sems=[es.enter_context(nc.semaphore(f"dsem_{q}_{i}")) for i in range(n_dma_sems)],
                cnt=[0] * n_dma_sems, nxt=0, know=[None] * n_dma_sems)
        self.semobj = {}
        self.lastw = {}
        self.readers = {}
        self.know = {e: {} for e in self.ENGS}
        self.final_tokens = []

    def _need(self, eng, tok, waits):
        sk, v, kn = tok
        if self.know[eng].get(sk, 0) >= v:
            return
        waits.append((sk, v))
        k = self.know[eng]
        for a, b in kn.items():
            if k.get(a, 0) < b:
                k[a] = b
        if k.get(sk, 0) < v:
            k[sk] = v

    def op(self, eng, fn, reads=(), writes=(), dma=False, final=False):
        waits = []
        toks = []
        own = None if dma else ("e", eng)
        for key in reads:
            t = self.lastw.get(key)
            if t is not None:
                toks.append((t, True))
        for key in writes:
            t = self.lastw.get(key)
            if t is not None:
                toks.append((t, False))
            toks.extend((r, False) for r in self.readers.get(key, ()))
        for t, raw in toks:
            if t[0] == own and (eng == "tensor" or not SELF_SYNC or (not raw and eng != "gpsimd") or (NO_RAW_SELF and eng in ("vector", "scalar"))):
                continue
            self._need(eng, t, waits)
        if dma:
            pool = self.dpool[eng]
            j = pool["nxt"]
            pool["nxt"] = (j + 1) % len(pool["sems"])
            if pool["cnt"][j] > 0:
                self._need(eng, (("d", eng, j), pool["cnt"][j], pool["know"][j]), waits)
            pool["cnt"][j] += 16
            sk = ("d", eng, j)
            self.semobj[sk] = pool["sems"][j]
            kn = dict(self.know[eng])
            pool["know"][j] = kn
            tok = (sk, pool["cnt"][j], kn)
            inc = (pool["sems"][j], 16)
        else:
            self.ecount[eng] += 1
            sk = ("e", eng)
            self.semobj[sk] = self.esem[eng]
            tok = (sk, self.ecount[eng], dict(self.know[eng]))
            inc = (self.esem[eng], 1)
        self.ops[eng].append((waits, fn, inc))
        for key in reads:
            self.readers.setdefault(key, []).append(tok)
        for key in writes:
            self.lastw[key] = tok
            self.readers[key] = []
        if final:
            self.final_tokens.append(tok)
        return tok

    def emit(self):
        waits = []
        for t in self.final_tokens:
            self._need("sync", t, waits)
        if waits:
            self.ops["sync"].append((waits, None, None))
        nc = self.nc
        with nc.Block() as block:
            def run(eng_name):
                def body(eng):
                    for waits, fn, inc in self.ops[eng_name]:
                        fused = FUSE_WAIT and fn is not None and len(waits) > 0
                        for sk, v in (waits[:-1] if fused else waits):
                            eng.wait_ge(self.semobj[sk], v)
                        if fn is not None:
                            ins = fn(eng)
                            if fused:
                                ins._wait_ge(self.semobj[waits[-1][0]], waits[-1][1])
                            ins.then_inc(inc[0], inc[1])
                return body
            block.sync(run("sync"))
            block.scalar(run("scalar"))
            block.vector(run("vector"))
            block.gpsimd(run("gpsimd"))
            block.tensor(run("tensor"))

    def dma(self, q, out, in_, reads=(), writes=(), final=False, **kw):
        return self.op(q, lambda e: e.dma_start(out=out, in_=in_, **kw), reads, writes, dma=True, final=final)

    def mm(self, out, lhsT, rhs, start, stop, reads=(), writes=()):
        return self.op("tensor", lambda e: e.matmul(out, lhsT, rhs, start=start, stop=stop), reads, writes)

    def act(self, out, in_, func, reads=(), writes=(), eng="scalar", **kw):
        return self.op(eng, lambda e: e.activation(out=out, in_=in_, func=func, **kw), reads, writes)

    def tt(self, eng, out, in0, in1, op, reads=(), writes=()):
        return self.op(eng, lambda e: e.tensor_tensor(out=out, in0=in0, in1=in1, op=op), reads, writes)

    def ts(self, eng, out, in0, s1, s2, op0, op1=None, reads=(), writes=(), accum_out=None):
        kw = {}
        if op1 is not None:
            kw["op1"] = op1
        if accum_out is not None:
            kw["accum_out"] = accum_out
        return self.op(eng, lambda e: e.tensor_scalar(out=out, in0=in0, scalar1=s1, scalar2=s2, op0=op0, **kw),
                       reads, writes)

    def stt(self, eng, out, in0, scalar, in1, op0, op1, reads=(), writes=()):
        return self.op(eng, lambda e: e.scalar_tensor_tensor(out=out, in0=in0, scalar=scalar, in1=in1,
                                                             op0=op0, op1=op1), reads, writes)

    def copy(self, eng, out, in_, reads=(), writes=()):
        if eng == "scalar":
            return self.op(eng, lambda e: e.activation(out=out, in_=in_, func=AF.Copy), reads, writes)
        return self.op(eng, lambda e: e.tensor_copy(out=out, in_=in_), reads, writes)

    def tr(self, out, in_, ident, reads=(), writes=()):
        return self.op("tensor", lambda e: e.transpose(out, in_, ident), reads, writes)

    def memset(self, eng, ap, val, writes=()):
        return self.op(eng, lambda e: e.memset(ap, val), (), writes)

    def gen(self, eng, f, reads=(), writes=()):
        return self.op(eng, f, reads, writes)


def _run(nc, in_maps):
    return run_bass_kernel_spmd(nc, in_maps, core_ids=list(range(NCORES)))


D_MODEL = 1024
DEPTH = 2
N_MOD = 6
MODC = N_MOD * D_MODEL // NCORES


def build_k0():
    nc = bass.Bass("TRN2", target_bir_lowering=False)
    cvT = nc.dram_tensor("cvT", [128, 8, 3], F32, kind="ExternalInput").ap()
    wm = nc.dram_tensor("wm", [DEPTH, 128, 8, MODC], F32, kind="ExternalInput").ap()
    bm = nc.dram_tensor("bm", [DEPTH, 3, MODC], F32, kind="ExternalInput").ap()
    out = nc.dram_tensor("mod", [DEPTH, 3, MODC], F32, kind="ExternalOutput").ap()
    with ExitStack() as es:
        P = Prog(nc, es)
        sb = lambda name, shape, dt=F32: es.enter_context(nc.sbuf_tensor(name, shape, dt))
        cv = sb("cv", [128, 8, 3])
        cs = sb("cs", [128, 8, 3])
        w = [sb(f"w{l}", [128, 8, MODC]) for l in range(DEPTH)]
        b = sb("b", [3, DEPTH, MODC])
        o = sb("o", [3, DEPTH, MODC])
        ps = [es.enter_context(nc.psum_tensor(f"ps{i}", [128, 512], F32)) for i in range(2)]
        P.dma("sync", cv[:], cvT, writes=["cv"])
        for l in range(DEPTH):
            P.dma("sync" if l == 0 else "gpsimd", w[l][:], wm[l], writes=[f"w{l}"])
            P.dma("sync", b[:, l, :], bm[l], writes=[f"b{l}"])
        P.act(cs[:], cv[:], AF.Silu, reads=["cv"], writes=["cs"])
        H = MODC // 2
        for l in range(DEPTH):
            for h in range(2):
                pt = ps[h]
                for kc in range(8):
                    P.mm(pt[0:3, 0:H], cs[:, kc, :], w[l][:, kc, h * H:(h + 1) * H], kc == 0, kc == 7,
                         reads=["cs", f"w{l}"], writes=[f"ps{h}"])
                P.op("vector", lambda e, l=l, h=h, pt=pt: e.tensor_tensor(
                    out=o[:, l, h * H:(h + 1) * H], in0=pt[0:3, 0:H], in1=b[:, l, h * H:(h + 1) * H], op=ALU.add),
                    reads=[f"ps{h}", f"b{l}"], writes=[f"o{l}{h}"])
            P.dma("sync", out[l], o[:, l, :], reads=[f"o{l}0", f"o{l}1"], final=True)
        P.emit()
    return nc


def run_k0(c, c_ctx, w_mod, b_mod):
    cv = np.concatenate([c, c_ctx[None]], 0)
    cvT = np.ascontiguousarray(cv.T.reshape(8, 128, 3).transpose(1, 0, 2))
    nc = build_k0()
    in_maps = []
    for i in range(NCORES):
        sl = slice(i * MODC, (i + 1) * MODC)
        wm = np.ascontiguousarray(w_mod[:, :, sl].reshape(DEPTH, 8, 128, MODC).transpose(0, 2, 1, 3))
        bm = np.ascontiguousarray(np.broadcast_to(b_mod[:, None, sl], (DEPTH, 3, MODC)))
        in_maps.append({"cvT": cvT, "wm": wm, "bm": bm})
    res = _run(nc, in_maps)
    return np.concatenate([r["mod"] for r in res.results], axis=-1)


class Ring:
    def __init__(self, items):
        self.items, self.i = items, 0

    def next(self):
        it = self.items[self.i % len(self.items)]
        self.i += 1
        return it


def mk_alloc(nc, es):
    def sb(name, shape, dt=F32):
        return es.enter_context(nc.sbuf_tensor(name, shape, dt))

    def ps(name, shape, dt=F32):
        return es.enter_context(nc.psum_tensor(name, shape, dt))

    def ring(name, n, shape, dt=F32, psum=False):
        return Ring([((ps if psum else sb)(f"{name}{i}", shape, dt), f"{name}{i}") for i in range(n)])
    return sb, ps, ring


def bcast_rows(ap1d, nparts):
    return bass.AP(tensor=ap1d.tensor, offset=ap1d.offset, ap=[[0, nparts]] + [list(x) for x in ap1d.ap])


LN_EPS = 1e-5


def emit_layernorm(P, x, xkey, xn, xnkey, st, mv, rs, skey, n=1024):
    for j in range(n // 512):
        P.gen("vector", lambda e, j=j: e.bn_stats(out=st[:, j, :], in_=x[:, j * 512:(j + 1) * 512]),
              reads=[xkey], writes=[skey + f"st{j}"])
    P.gen("vector", lambda e: e.bn_aggr(out=mv[:], in_=st[:]),
          reads=[skey + f"st{j}" for j in range(n // 512)], writes=[skey + "mv"])
    P.ts("vector", rs[:], mv[:, 1:2], LN_EPS, None, ALU.add, reads=[skey + "mv"], writes=[skey + "rs"])
    P.act(rs[:], rs[:], AF.Sqrt, reads=[skey + "rs"], writes=[skey + "rs"])
    P.gen("vector", lambda e: e.reciprocal(out=rs[:], in_=rs[:]), reads=[skey + "rs"], writes=[skey + "rs"])
    P.ts("vector", xn, x, mv[:, 0:1], rs[:, 0:1], ALU.subtract, ALU.mult,
         reads=[xkey, skey + "mv", skey + "rs"], writes=[xnkey])


N_IN = 2576
K1_TILES = 17


def build_k1():
    nc = bass.Bass("TRN2", target_bir_lowering=False)
    xt = nc.dram_tensor("xt", [K1_TILES, 128, 1024], F32, kind="ExternalInput").ap()
    modr = nc.dram_tensor("modr", [2, 2, 1024], F32, kind="ExternalInput").ap()
    win = nc.dram_tensor("win", [128, 8, N_IN], F32, kind="ExternalInput").ap()
    identd = nc.dram_tensor("ident", [128, 128], F32, kind="ExternalInput").ap()
    out = nc.dram_tensor("p", [K1_TILES, 128, N_IN], F32, kind="ExternalOutput").ap()
    with ExitStack() as es:
        P = Prog(nc, es)
        sb, ps, ring = mk_alloc(nc, es)
        idf = sb("idf", [128, 128])
        idb = sb("idb", [128, 128], BF16)
        P.dma("sync", idf[:], identd, writes=["idf"])
        P.copy("vector", idb[:], idf[:], reads=["idf"], writes=["idb"])
        modt = sb("modt", [128, 2, 2, 1024])
        for g in range(2):
            for j in range(2):
                P.dma("sync", modt[:, g, j, :], bcast_rows(modr[g, j], 128), writes=[f"mod{g}{j}"])
            P.ts("vector", modt[:, g, 0, :], modt[:, g, 0, :], 1.0, None, ALU.add,
                 reads=[f"mod{g}0"], writes=[f"mod{g}0"])
        wbf = sb("wbf", [128, 8, N_IN], BF16)
        wst = ring("wst", 2, [128, N_IN])
        for kc in range(8):
            t, k = wst.next()
            P.dma("gpsimd" if kc % 2 else "sync", t[:], win[:, kc, :], writes=[k])
            P.copy("gpsimd" if kc % 2 else "vector", wbf[:, kc, :], t[:], reads=[k], writes=[f"wbf{kc}"])
        wkeys = [f"wbf{kc}" for kc in range(8)]
        xr = ring("x", 2, [128, 1024])
        xnr = ring("xn", 2, [128, 1024])
        h1r = ring("h1", 2, [128, 1024])
        hr = ring("h", 2, [128, 1024], BF16)
        hTr = ring("hT", 2, [128, 1024], BF16)
        orr = ring("o", 2, [128, N_IN])
        str_ = ring("st", 2, [128, 2, 6])
        mvr = ring("mv", 2, [128, 2])
        rsr = ring("rs", 2, [128, 1])
        pTr = ring("pT", 2, [128, 1024], BF16, psum=True)
        pmr = ring("pm", 4, [128, 512], F32, psum=True)
        ev = 0
        for t in range(K1_TILES):
            g = 0 if t < 16 else 1
            x, xk = xr.next()
            P.dma("sync", x[:], xt[t], writes=[xk])
            xn, xnk = xnr.next()
            st, _ = str_.next(); mv, _ = mvr.next(); rs, _ = rsr.next()
            emit_layernorm(P, x[:], xk, xn[:], xnk, st, mv, rs, f"ln{t % 2}")
            h1, h1k = h1r.next()
            P.tt("gpsimd", h1[:], xn[:], modt[:, g, 0, :], ALU.mult, reads=[xnk, f"mod{g}0"], writes=[h1k])
            h, hk = hr.next()
            P.tt("gpsimd", h[:], h1[:], modt[:, g, 1, :], ALU.add, reads=[h1k, f"mod{g}1"], writes=[hk])
            pT, pTk = pTr.next()
            for kc in range(8):
                P.tr(pT[:, kc * 128:(kc + 1) * 128], h[:, kc * 128:(kc + 1) * 128], idb[:],
                     reads=[hk, "idb"], writes=[pTk])
            hT, hTk = hTr.next()
            P.copy("scalar", hT[:], pT[:], reads=[pTk], writes=[hTk])
            o, ok = orr.next()
            for cg in range(6):
                c0 = cg * 512
                n = min(512, N_IN - c0)
                pm, pmk = pmr.next()
                for kc in range(8):
                    P.mm(pm[:, 0:n], hT[:, kc * 128:(kc + 1) * 128], wbf[:, kc, c0:c0 + n], kc == 0, kc == 7,
                         reads=[hTk, wkeys[kc]], writes=[pmk])
                P.copy("scalar" if ev % 2 else "vector", o[:, c0:c0 + n], pm[:, 0:n], reads=[pmk], writes=[ok + f"c{cg}"])
                ev += 1
            P.dma("gpsimd", out[t], o[:], reads=[ok + f"c{cg}" for cg in range(6)], writes=[ok + "dma"], final=True)
        P.emit()
    return nc


def lay_w(w, kchunks):
    return np.ascontiguousarray(w.reshape(kchunks, 128, w.shape[1]).transpose(1, 0, 2))


def run_k1(x, ctx, mod_l, w_in_l):
    nc = build_k1()
    B, n, D = x.shape
    seg = n // 4
    ctxf = ctx.reshape(-1, D)
    ident = np.eye(128, dtype=np.float32)
    win = lay_w(w_in_l, 8)
    in_maps = []
    for i in range(NCORES):
        b, s = i // 4, i % 4
        xt = np.zeros((K1_TILES * 128, D), np.float32)
        xt[:seg] = x[b, s * seg:(s + 1) * seg]
        xt[seg:seg + 64] = ctxf[i * 64:(i + 1) * 64]
        modr = np.stack([np.stack([mod_l[b, 1024:2048], mod_l[b, 0:1024]]),
                         np.stack([mod_l[2, 1024:2048], mod_l[2, 0:1024]])])
        in_maps.append({"xt": xt.reshape(K1_TILES, 128, D), "modr": np.ascontiguousarray(modr), "win": win,
                        "ident": ident})
    res = _run(nc, in_maps)
    P_lat = np.zeros((B, n, N_IN), np.float32)
    P_ctx = np.zeros((B * ctx.shape[1], N_IN), np.float32)
    for i in range(NCORES):
        b, s = i // 4, i % 4
        p = res.results[i]["p"].reshape(K1_TILES * 128, N_IN)
        P_lat[b, s * seg:(s + 1) * seg] = p[:seg]
        P_ctx[i * 64:(i + 1) * 64] = p[seg:seg + 64]
    return P_lat, P_ctx.reshape(B, ctx.shape[1], N_IN)


ALPHA = (2.0 * DEPTH) ** 0.25
N_EXP = 16


def build_k5():
    nc = bass.Bass("TRN2", target_bir_lowering=False)
    T = K1_TILES
    yt = nc.dram_tensor("yt", [T, 128, 1024], F32, kind="ExternalInput").ap()
    xt = nc.dram_tensor("xt", [T, 128, 1024], F32, kind="ExternalInput").ap()
    wout = nc.dram_tensor("wout", [128, 8, 1024], F32, kind="ExternalInput").ap()
    rows = nc.dram_tensor("rows", [2, 3, 1024], F32, kind="ExternalInput").ap()
    lnr = nc.dram_tensor("lnr", [2, 1024], F32, kind="ExternalInput").ap()
    rwd = nc.dram_tensor("rw", [128, 8, N_EXP], F32, kind="ExternalInput").ap()
    rbd = nc.dram_tensor("rb", [N_EXP], F32, kind="ExternalInput").ap()
    identd = nc.dram_tensor("ident", [128, 128], F32, kind="ExternalInput").ap()
    x1o = nc.dram_tensor("o_x1", [T, 128, 1024], F32, kind="ExternalOutput").ap()
    h2o = nc.dram_tensor("o_h2", [T, 128, 1024], BF16, kind="ExternalOutput").ap()
    affo = nc.dram_tensor("o_aff", [T, 128, N_EXP], F32, kind="ExternalOutput").ap()
    with ExitStack() as es:
        P = Prog(nc, es)
        sb, ps, ring = mk_alloc(nc, es)
        idf = sb("idf", [128, 128])
        idb = sb("idb", [128, 128], BF16)
        P.dma("sync", idf[:], identd, writes=["idf"])
        P.copy("vector", idb[:], idf[:], reads=["idf"], writes=["idb"])
        rowt = sb("rowt", [128, 2, 3, 1024])
        for g in range(2):
            for j in range(3):
                P.dma("sync", rowt[:, g, j, :], bcast_rows(rows[g, j], 128), writes=[f"row{g}{j}"])
            P.ts("vector", rowt[:, g, 1, :], rowt[:, g, 1, :], 1.0, None, ALU.add,
                 reads=[f"row{g}1"], writes=[f"row{g}1"])
        lnt = sb("lnt", [128, 2, 1024])
        for j in range(2):
            P.dma("sync", lnt[:, j, :], bcast_rows(lnr[j], 128), writes=[f"ln{j}"])
        rw = sb("rwt", [128, 8, N_EXP])
        P.dma("sync", rw[:], rwd, writes=["rw"])
        rb = sb("rbt", [128, N_EXP])
        P.dma("sync", rb[:], bcast_rows(rbd, 128), writes=["rb"])
        wbf = sb("wbf", [128, 8, 1024], BF16)
        wst = ring("wst", 2, [128, 1024])
        for kc in range(8):
            t, k = wst.next()
            P.dma("gpsimd" if kc % 2 else "sync", t[:], wout[:, kc, :], writes=[k])
            P.copy("gpsimd" if kc % 2 else "vector", wbf[:, kc, :], t[:], reads=[k], writes=[f"wbf{kc}"])
        wkeys = [f"wbf{kc}" for kc in range(8)]
        yr = ring("y", 2, [128, 1024]); ybr = ring("yb", 2, [128, 1024], BF16)
        yTr = ring("yT", 2, [128, 1024], BF16)
        xr = ring("x", 2, [128, 1024]); tmpr = ring("tmp", 2, [128, 1024]); rr = ring("r", 2, [128, 1024])
        xnr = ring("xn", 2, [128, 1024]); x1r = ring("x1_", 2, [128, 1024]); x1ar = ring("x1a", 2, [128, 1024])
        xn2r = ring("xn2", 2, [128, 1024]); h2fr = ring("h2f", 2, [128, 1024]); h2ar = ring("h2a", 2, [128, 1024])
        h2br = ring("h2b", 2, [128, 1024], BF16)
        h2Tr = ring("h2T", 2, [128, 1024])
        str_ = ring("st", 4, [128, 2, 6]); mvr = ring("mv", 4, [128, 2]); rsr = ring("rs", 4, [128, 1])
        lgr = ring("lg", 2, [128, N_EXP]); exr = ring("ex", 2, [128, N_EXP]); afr = ring("af", 2, [128, N_EXP])
        smr = ring("sm", 2, [128, 4])
        pTr = ring("pT", 1, [128, 1024], BF16, psum=True)
        pmr = ring("pm", 2, [128, 512], F32, psum=True)
        pTfr = ring("pTf", 1, [128, 1024], F32, psum=True)
        plr = ring("pl", 1, [128, N_EXP], F32, psum=True)
        lnc = 0
        for t in range(T):
            g = 0 if t < 16 else 1
            y, yk = yr.next(); P.dma("sync", y[:], yt[t], writes=[yk])
            x, xk = xr.next(); P.dma("sync", x[:], xt[t], writes=[xk])
            yb, ybk = ybr.next(); P.copy("gpsimd", yb[:], y[:], reads=[yk], writes=[ybk])
            pT, pTk = pTr.next()
            for kc in range(8):
                P.tr(pT[:, kc * 128:(kc + 1) * 128], yb[:, kc * 128:(kc + 1) * 128], idb[:], reads=[ybk, "idb"], writes=[pTk])
            yT, yTk = yTr.next(); P.copy("scalar", yT[:], pT[:], reads=[pTk], writes=[yTk])
            tmp, tmpk = tmpr.next()
            for hf in range(2):
                pm, pmk = pmr.next()
                for kc in range(8):
                    P.mm(pm[:], yT[:, kc * 128:(kc + 1) * 128], wbf[:, kc, hf * 512:(hf + 1) * 512], kc == 0, kc == 7,
                         reads=[yTk, wkeys[kc]], writes=[pmk])
                P.tt("vector", tmp[:, hf * 512:(hf + 1) * 512], pm[:], rowt[:, g, 0, hf * 512:(hf + 1) * 512], ALU.mult,
                     reads=[pmk, f"row{g}0"], writes=[tmpk + str(hf)])
            r, rk = rr.next()
            P.stt("vector", r[:], x[:], ALPHA, tmp[:], ALU.mult, ALU.add, reads=[xk, tmpk + "0", tmpk + "1"], writes=[rk])
            xn, xnk = xnr.next(); st, _ = str_.next(); mv, _ = mvr.next(); rs, _ = rsr.next()
            emit_layernorm(P, r[:], rk, xn[:], xnk, st, mv, rs, f"lnA{lnc % 4}"); lnc += 1
            x1a, x1ak = x1ar.next(); x1, x1k = x1r.next()
            P.tt("gpsimd", x1a[:], xn[:], lnt[:, 0, :], ALU.mult, reads=[xnk, "ln0"], writes=[x1ak])
            P.tt("gpsimd", x1[:], x1a[:], lnt[:, 1, :], ALU.add, reads=[x1ak, "ln1"], writes=[x1k])
            P.dma("gpsimd", x1o[t], x1[:], reads=[x1k], writes=[x1k + "d"], final=True)
            xn2, xn2k = xn2r.next(); st, _ = str_.next(); mv, _ = mvr.next(); rs, _ = rsr.next()
            emit_layernorm(P, x1[:], x1k, xn2[:], xn2k, st, mv, rs, f"lnA{lnc % 4}"); lnc += 1
            h2a, h2ak = h2ar.next(); h2f, h2fk = h2fr.next(); h2b, h2bk = h2br.next()
            P.tt("gpsimd", h2a[:], xn2[:], rowt[:, g, 1, :], ALU.mult, reads=[xn2k, f"row{g}1"], writes=[h2ak])
            P.tt("vector", h2f[:], h2a[:], rowt[:, g, 2, :], ALU.add, reads=[h2ak, f"row{g}2"], writes=[h2fk])
            P.copy("scalar", h2b[:], h2f[:], reads=[h2fk], writes=[h2bk])
            P.dma("sync", h2o[t], h2b[:], reads=[h2bk], writes=[h2bk + "d"], final=True)
            pTf, pTfk = pTfr.next()
            for kc in range(8):
                P.tr(pTf[:, kc * 128:(kc + 1) * 128], h2f[:, kc * 128:(kc + 1) * 128], idf[:], reads=[h2fk, "idf"], writes=[pTfk])
            h2T, h2Tk = h2Tr.next()
            P.copy("scalar", h2T[:, 0:512], pTf[:, 0:512], reads=[pTfk], writes=[h2Tk + "a"])
            P.copy("vector", h2T[:, 512:1024], pTf[:, 512:1024], reads=[pTfk], writes=[h2Tk + "b"])
            pl, plk = plr.next()
            for kc in range(8):
                P.mm(pl[:], h2T[:, kc * 128:(kc + 1) * 128], rw[:, kc, :], kc == 0, kc == 7,
                     reads=[h2Tk + "a", h2Tk + "b", "rw"], writes=[plk])
            lg, lgk = lgr.next(); ex, exk = exr.next(); af, afk = afr.next(); sm, smk = smr.next()
            P.tt("vector", lg[:], pl[:], rb[:], ALU.add, reads=[plk, "rb"], writes=[lgk])
            P.gen("vector", lambda e, sm=sm, lg=lg: e.reduce_max(out=sm[:, 0:1], in_=lg[:], axis=AX.X), reads=[lgk], writes=[smk + "m"])
            P.ts("vector", sm[:, 1:2], sm[:, 0:1], -1.0, None, ALU.mult, reads=[smk + "m"], writes=[smk + "n"])
            P.act(ex[:], lg[:], AF.Exp, reads=[lgk, smk + "n"], writes=[exk, smk + "s"], bias=sm[:, 1:2], scale=1.0,
                  accum_out=sm[:, 2:3])
            P.gen("vector", lambda e, sm=sm: e.reciprocal(out=sm[:, 3:4], in_=sm[:, 2:3]), reads=[smk + "s"], writes=[smk + "r"])
            P.ts("vector", af[:], ex[:], sm[:, 3:4], None, ALU.mult, reads=[exk, smk + "r"], writes=[afk])
            P.dma("gpsimd", affo[t], af[:], reads=[afk], writes=[afk + "d"], final=True)
        P.emit()
    return nc


def tok_shard(lat, ctx, i):
    B, n, D = lat.shape
    seg = n // 4
    b, s = i // 4, i % 4
    out = np.zeros((K1_TILES * 128, D), lat.dtype)
    out[:seg] = lat[b, s * seg:(s + 1) * seg]
    out[seg:seg + 64] = ctx.reshape(-1, D)[i * 64:(i + 1) * 64]
    return out.reshape(K1_TILES, 128, D)


def tok_unshard(parts, B, n, nctx):
    D = parts[0].shape[-1]
    seg = n // 4
    lat = np.zeros((B, n, D), parts[0].dtype)
    ctx = np.zeros((B * nctx, D), parts[0].dtype)
    for i in range(NCORES):
        b, s = i // 4, i % 4
        p = parts[i].reshape(K1_TILES * 128, D)
        lat[b, s * seg:(s + 1) * seg] = p[:seg]
        ctx[i * 64:(i + 1) * 64] = p[seg:seg + 64]
    return lat, ctx.reshape(B, nctx, D)


def run_k5(ycat_l, ycat_c, x, ctx, mod_l, w_out_l, ln_w, ln_b, router_w_l, router_b_l):
    nc = build_k5()
    B, n, D = x.shape
    ident = np.eye(128, dtype=np.float32)
    wout = lay_w(w_out_l, 8)
    rw = lay_w(router_w_l, 8)
    lnr = np.ascontiguousarray(np.stack([ln_w, ln_b]))
    in_maps = []
    for i in range(NCORES):
        b = i // 4
        rows = np.stack([np.stack([mod_l[m, 2048:3072], mod_l[m, 4096:5120], mod_l[m, 3072:4096]]) for m in (b, 2)])
        in_maps.append({"yt": tok_shard(ycat_l, ycat_c, i), "xt": tok_shard(x, ctx, i), "wout": wout,
                        "rows": np.ascontiguousarray(rows), "lnr": lnr, "rw": rw,
                        "rb": np.ascontiguousarray(router_b_l), "ident": ident})
    res = _run(nc, in_maps)
    nctx = ctx.shape[1]
    x1, c1 = tok_unshard([r["o_x1"] for r in res.results], B, n, nctx)
    h2, h2c = tok_unshard([r["o_h2"] for r in res.results], B, n, nctx)
    aff, affc = tok_unshard([r["o_aff"] for r in res.results], B, n, nctx)
    return x1, c1, h2, h2c, aff, affc


K6_ITERS = 26


def build_k6(F_lat, k_lat, F_ctx, k_ctx):
    nc = bass.Bass("TRN2", target_bir_lowering=False)
    R = 32
    ald = nc.dram_tensor("al", [R, F_lat], F32, kind="ExternalInput").ap()
    acd = nc.dram_tensor("ac", [R, F_ctx], F32, kind="ExternalInput").ap()
    thro = nc.dram_tensor("thr", [R, 2], F32, kind="ExternalOutput").ap()
    with ExitStack() as es:
        P = Prog(nc, es)
        sb, ps, ring = mk_alloc(nc, es)
        res = sb("res", [R, 2])
        for pi, (src, F, k) in enumerate(((ald, F_lat, k_lat), (acd, F_ctx, k_ctx))):
            A = sb(f"A{pi}", [R, F])
            junk = sb(f"junk{pi}", [R, F], BF16)
            sc = sb(f"sc{pi}", [R, 8])
            lo, hi, mid, cnt, cond, t1, t2 = (sc[:, j:j + 1] for j in range(7))
            kk = f"p{pi}"
            P.dma("sync", A[:], src, writes=[kk + "A"])
            P.memset("vector", lo, 0.0, writes=[kk + "lo"])
            P.memset("vector", hi, 1.0, writes=[kk + "hi"])
            for it in range(K6_ITERS):
                P.tt("vector", mid, lo, hi, ALU.add, reads=[kk + "lo", kk + "hi"], writes=[kk + "mid"])
                P.ts("vector", mid, mid, 0.5, None, ALU.mult, reads=[kk + "mid"], writes=[kk + "mid"])
                P.ts("vector", junk[:], A[:], mid, None, ALU.is_ge, ALU.add, reads=[kk + "A", kk + "mid"],
                     writes=[kk + "junk", kk + "cnt"], accum_out=cnt)
                P.ts("vector", cond, cnt, float(k) - 0.5, None, ALU.is_ge, reads=[kk + "cnt"], writes=[kk + "cond"])
                P.tt("vector", t1, cond, mid, ALU.mult, reads=[kk + "cond", kk + "mid"], writes=[kk + "t1"])
                P.stt("vector", t2, cond, 2.0, mid, ALU.mult, ALU.add, reads=[kk + "cond", kk + "mid"], writes=[kk + "t2"])
                P.tt("vector", lo, lo, t1, ALU.max, reads=[kk + "lo", kk + "t1"], writes=[kk + "lo"])
                P.tt("vector", hi, hi, t2, ALU.min, reads=[kk + "hi", kk + "t2"], writes=[kk + "hi"])
            P.copy("vector", res[:, pi:pi + 1], lo, reads=[kk + "lo"], writes=[f"res{pi}"])
        P.dma("sync", thro, res[:], reads=["res0", "res1"], final=True)
        P.emit()
    return nc


def run_k6(aff, affc):
    B, n, E = aff.shape
    ncx = affc.shape[1]
    nc = build_k6(n, 2 * n // E, ncx, 2 * ncx // E)
    al = np.ascontiguousarray(aff.transpose(0, 2, 1).reshape(B * E, n))
    ac = np.ascontiguousarray(affc.transpose(0, 2, 1).reshape(B * E, ncx))
    res = _run(nc, [{"al": al, "ac": ac} for _ in range(NCORES)])
    thr = res.results[0]["thr"]
    return thr[:, 0].reshape(B, E), thr[:, 1].reshape(B, E)


D_FF = 2816
NFC = D_FF // 128
K7_TOK = K1_TILES * 128


def build_k7(n_exp=N_EXP):
    nc = bass.Bass("TRN2", target_bir_lowering=False)
    T = K1_TILES
    h2Td = nc.dram_tensor("h2T", [128, 8, K7_TOK], BF16, kind="ExternalInput").ap()
    affd = nc.dram_tensor("aff", [T, 128, N_EXP], F32, kind="ExternalInput").ap()
    thrd = nc.dram_tensor("thr", [2, N_EXP], F32, kind="ExternalInput").ap()
    x1d = nc.dram_tensor("i_x1", [T, 128, 1024], F32, kind="ExternalInput").ap()
    rows = nc.dram_tensor("rows", [2, 1024], F32, kind="ExternalInput").ap()
    lnr = nc.dram_tensor("lnr", [2, 1024], F32, kind="ExternalInput").ap()
    wgd = nc.dram_tensor("wg", [n_exp, 128, 8, D_FF], F32, kind="ExternalInput").ap()
    wud = nc.dram_tensor("wu", [n_exp, 128, 8, D_FF], F32, kind="ExternalInput").ap()
    wdd = nc.dram_tensor("wd", [n_exp, 128, NFC, 1024], F32, kind="ExternalInput").ap()
    x2o = nc.dram_tensor("o_x2", [T, 128, 1024], F32, kind="ExternalOutput").ap()
    with ExitStack() as es:
        P = Prog(nc, es)
        sb, ps, ring = mk_alloc(nc, es)
        h2T = sb("h2Ts", [128, 8, K7_TOK], BF16)
        for kc in range(8):
            P.dma("sync", h2T[:, kc, :], h2Td[:, kc, :], writes=["h2T"] if kc == 7 else [f"h2T_{kc}"])
        h2keys = ["h2T"] + [f"h2T_{kc}" for kc in range(7)]
        thrb = sb("thrb", [128, 2, N_EXP])
        for g in range(2):
            P.dma("sync", thrb[:, g, :], bcast_rows(thrd[g], 128), writes=[f"thr{g}"])
        wgt = sb("wgt", [128, T, N_EXP])
        msk = sb("msk", [128, T, N_EXP])
        for t in range(T):
            g = 0 if t < 16 else 1
            P.dma("sync", wgt[:, t, :], affd[t], writes=[f"aff{t}"])
            P.tt("vector", msk[:, t, :], wgt[:, t, :], thrb[:, g, :], ALU.is_ge, reads=[f"aff{t}", f"thr{g}"], writes=[f"msk{t}"])
            P.tt("vector", wgt[:, t, :], wgt[:, t, :], msk[:, t, :], ALU.mult, reads=[f"aff{t}", f"msk{t}"], writes=[f"aff{t}"])
        acc = sb("acc", [128, T, 1024])
        for t in range(T):
            P.memset("gpsimd", acc[:, t, :], 0.0, writes=[f"acc{t}a", f"acc{t}b"])
        FG = 4
        wgr = ring("wgb", 2, [128, 8, FG * 128], BF16)
        wur = ring("wub", 2, [128, 8, FG * 128], BF16)
        wdr = ring("wdb", 2, [128, FG, 1024], BF16)
        stg = ring("stg", 4, [128, 1024])
        actr = ring("actT", 2, [128, FG, 512], BF16)
        sgr = ring("sg", 2, [128, 512])
        pgr = ring("pg", 2, [128, 512], F32, psum=True)
        pur = ring("pu", 2, [128, 512], F32, psum=True)
        pyr = ring("py", 2, [128, 512], F32, psum=True)
        tgs = [(s, min(512, K7_TOK - s)) for s in range(0, K7_TOK, 512)]
        ci = 0
        for e in range(n_exp):
            for f0 in range(0, NFC, FG):
                nf = min(FG, NFC - f0)
                wgb, wgk = wgr.next(); wub, wuk = wur.next(); wdb, wdk = wdr.next()
                for (src, dst, dk) in ((wgd, wgb, wgk), (wud, wub, wuk)):
                    for kp in range(0, 8, 2):
                        st, sk = stg.next()
                        q = "sync"
                        sv = st[:, 0:2 * nf * 128].rearrange("p (a f) -> p a f", a=2)
                        P.dma(q, sv, src[e, :, kp:kp + 2, f0 * 128:(f0 + nf) * 128], writes=[sk])
                        P.copy("gpsimd", dst[:, kp:kp + 2, 0:nf * 128], sv, reads=[sk], writes=[dk + f"k{kp}"])
                        ci += 1
                for fc in range(nf):
                    st, sk = stg.next()
                    P.dma("sync", st[:], wdd[e, :, f0 + fc, :], writes=[sk])
                    P.copy("gpsimd", wdb[:, fc, :], st[:], reads=[sk], writes=[wdk + f"f{fc}"])
                    ci += 1
                for (s0, ns) in tgs:
                    actT, ak = actr.next()
                    for fc in range(nf):
                        pg, pgk = pgr.next(); pu, puk = pur.next()
                        for kc in range(8):
                            P.mm(pg[:, 0:ns], wgb[:, kc, fc * 128:(fc + 1) * 128], h2T[:, kc, s0:s0 + ns], kc == 0, kc == 7,
                                 reads=h2keys + [wgk + f"k{kc - kc % 2}"], writes=[pgk])
                        for kc in range(8):
                            P.mm(pu[:, 0:ns], wub[:, kc, fc * 128:(fc + 1) * 128], h2T[:, kc, s0:s0 + ns], kc == 0, kc == 7,
                                 reads=h2keys + [wuk + f"k{kc - kc % 2}"], writes=[puk])
                        sg, sgk = sgr.next()
                        P.act(sg[:, 0:ns], pg[:, 0:ns], AF.Silu, reads=[pgk], writes=[sgk])
                        P.tt("vector", actT[:, fc, 0:ns], pu[:, 0:ns], sg[:, 0:ns], ALU.mult, reads=[puk, sgk], writes=[ak + f"f{fc}"])
                    for tt in range(s0 // 128, (s0 + ns) // 128):
                        for hf in range(2):
                            py, pyk = pyr.next()
                            for fc in range(nf):
                                P.mm(py[:], actT[:, fc, tt * 128 - s0:(tt + 1) * 128 - s0], wdb[:, fc, hf * 512:(hf + 1) * 512],
                                     fc == 0, fc == nf - 1, reads=[ak + f"f{fc}", wdk + f"f{fc}"], writes=[pyk])
                            ah = f"acc{tt}" + "ab"[hf]
                            P.stt("vector", acc[:, tt, hf * 512:(hf + 1) * 512], py[:], wgt[:, tt, e:e + 1],
                                  acc[:, tt, hf * 512:(hf + 1) * 512], ALU.mult, ALU.add, reads=[pyk, f"aff{tt}", ah], writes=[ah])
        rowt = sb("rowt", [128, 2, 1024])
        lnt = sb("lnt", [128, 2, 1024])
        for j in range(2):
            P.dma("sync", rowt[:, j, :], bcast_rows(rows[j], 128), writes=[f"row{j}"])
            P.dma("sync", lnt[:, j, :], bcast_rows(lnr[j], 128), writes=[f"ln{j}"])
        xr = ring("x", 2, [128, 1024])
        str_ = ring("st", 2, [128, 2, 6]); mvr = ring("mv", 2, [128, 2]); rsr = ring("rs", 2, [128, 1])
        for t in range(T):
            g = 0 if t < 16 else 1
            ak2 = [f"acc{t}a", f"acc{t}b"]
            x, xk = xr.next(); P.dma("sync", x[:], x1d[t], writes=[xk])
            P.tt("gpsimd", acc[:, t, :], acc[:, t, :], rowt[:, g, :], ALU.mult, reads=ak2 + [f"row{g}"], writes=ak2)
            P.stt("vector", acc[:, t, :], x[:], ALPHA, acc[:, t, :], ALU.mult, ALU.add, reads=[xk] + ak2, writes=ak2)
            st, _ = str_.next(); mv, _ = mvr.next(); rs, _ = rsr.next()
            emit_layernorm(P, acc[:, t, :], ak2[0], x[:], xk, st, mv, rs, f"lnB{t % 2}")
            P.tt("gpsimd", acc[:, t, :], x[:], lnt[:, 0, :], ALU.mult, reads=[xk, "ln0"], writes=ak2)
            P.tt("gpsimd", x[:], acc[:, t, :], lnt[:, 1, :], ALU.add, reads=ak2 + ["ln1"], writes=[xk])
            P.dma("sync", x2o[t], x[:], reads=[xk], writes=[xk], final=True)
        P.emit()
    return nc


def run_k7(h2, h2c, aff, affc, thr, thrc, x1, c1, mod_l, ln_w, ln_b, wg, wu, wd, n_exp=N_EXP):
    nc = build_k7(n_exp)
    B, n, D = x1.shape
    nctx = c1.shape[1]
    wgl = np.ascontiguousarray(wg[:n_exp].reshape(n_exp, 8, 128, D_FF).transpose(0, 2, 1, 3))
    wul = np.ascontiguousarray(wu[:n_exp].reshape(n_exp, 8, 128, D_FF).transpose(0, 2, 1, 3))
    wdl = np.ascontiguousarray(wd[:n_exp].reshape(n_exp, NFC, 128, D).transpose(0, 2, 1, 3))
    lnr = np.ascontiguousarray(np.stack([ln_w, ln_b]))
    in_maps = []
    for i in range(NCORES):
        b = i // 4
        h2s = tok_shard(h2, h2c, i).reshape(K7_TOK, D)
        h2T = np.ascontiguousarray(h2s.T.reshape(8, 128, K7_TOK).transpose(1, 0, 2))
        rows = np.ascontiguousarray(np.stack([mod_l[b, 5120:6144], mod_l[2, 5120:6144]]))
        in_maps.append({"h2T": h2T, "aff": tok_shard(aff, affc, i), "thr": np.ascontiguousarray(np.stack([thr[b], thrc[b]])),
                        "i_x1": tok_shard(x1, c1, i), "rows": rows, "lnr": lnr, "wg": wgl, "wu": wul, "wd": wdl})
    res = _run(nc, in_maps)
    return tok_unshard([r["o_x2"] for r in res.results], B, n, nctx)


RMS_EPS = 1e-6
NKEY = 8192 + 256
NKT = NKEY // 128
QCOLS = K1_TILES * 512


def build_k3():
    nc = bass.Bass("TRN2", target_bir_lowering=False)
    T = K1_TILES
    qd = nc.dram_tensor("qT", [128, QCOLS], F32, kind="ExternalInput").ap()
    kd = nc.dram_tensor("kT", [128, NKEY], F32, kind="ExternalInput").ap()
    vd = nc.dram_tensor("v", [128, NKT, 2, 64], F32, kind="ExternalInput").ap()
    cqd = nc.dram_tensor("cosq", [128, 16 * 512], F32, kind="ExternalInput").ap()
    sqd = nc.dram_tensor("sinq", [128, 16 * 512], F32, kind="ExternalInput").ap()
    ckd = nc.dram_tensor("cosk", [128, 8192], F32, kind="ExternalInput").ap()
    skd = nc.dram_tensor("sink", [128, 8192], F32, kind="ExternalInput").ap()
    cst = nc.dram_tensor("cst", [128, 258], F32, kind="ExternalInput").ap()
    yo = nc.dram_tensor("o_ya", [T, 128, 512], F32, kind="ExternalOutput").ap()
    with ExitStack() as es:
        P = Prog(nc, es)
        sb, ps, ring = mk_alloc(nc, es)
        cs = sb("cs", [128, 258])
        P.dma("sync", cs[:], cst, writes=["cs"])
        Rm, onesb, qw2, kw2 = cs[:, 0:128], cs[:, 128:256], cs[:, 256:257], cs[:, 257:258]
        bq = sb("bq", [128, 2])
        P.memset("vector", bq[:, 0:1], 64.0 * RMS_EPS, writes=["bq0"])
        P.memset("vector", bq[:, 1:2], RMS_EPS, writes=["bq1"])
        qr = sb("qr", [128, QCOLS], BF16)
        kr = sb("kr", [128, NKEY], BF16)
        vst = ring("vst", 2, [128, 2, 64])
        vaug = sb("vaug", [128, NKT, 2, 65], BF16)
        P.memset("gpsimd", vaug[:], 1.0, writes=["vaug_init"])
        for kt in range(NKT):
            v, vk = vst.next()
            P.dma("sync", v[:], vd[:, kt], writes=[vk])
            P.copy("gpsimd", vaug[:, kt, :, 0:64], v[:], reads=[vk, "vaug_init"], writes=[f"vaug{kt}"])
        xr = ring("px", 2, [128, 512]); sqr = ring("psq", 2, [128, 512]); sdr = ring("psd", 2, [128, 512])
        xnr = ring("pxn", 2, [128, 512]); cr = ring("pc", 2, [128, 512]); sr = ring("psn", 2, [128, 512])
        t1r = ring("pt1", 2, [128, 512]); t2r = ring("pt2", 2, [128, 512])
        bank = [(ps(f"mb{i}", [128, 512], F32), f"mb{i}") for i in range(7)]
        pssr = Ring(bank[0:1])
        prot = Ring(bank[1:2])

        def prep(src, dst, dkey, ncols, nrope, w2, bcol, scale, cosd, sind):
            for c0 in range(0, ncols, 512):
                n = min(512, ncols - c0)
                x, xk = xr.next(); P.dma("sync", x[:, 0:n], src[:, c0:c0 + n], writes=[xk])
                sq, sqk = sqr.next(); P.tt("gpsimd", sq[:, 0:n], x[:, 0:n], x[:, 0:n], ALU.mult, reads=[xk], writes=[sqk])
                pss, pssk = pssr.next()
                P.mm(pss[:, 0:n], onesb, sq[:, 0:n], True, True, reads=[sqk, "cs"], writes=[pssk])
                sd, sdk = sdr.next()
                P.act(sd[:, 0:n], pss[:, 0:n], AF.Sqrt, reads=[pssk, f"bq{bcol}"], writes=[sdk], bias=bq[:, bcol:bcol + 1], scale=scale)
                P.gen("vector", lambda e, sd=sd, n=n: e.reciprocal(out=sd[:, 0:n], in_=sd[:, 0:n]), reads=[sdk], writes=[sdk])
                xn, xnk = xnr.next()
                P.stt("vector", xn[:, 0:n], x[:, 0:n], w2, sd[:, 0:n], ALU.mult, ALU.mult, reads=[xk, sdk, "cs"], writes=[xnk])
                dk = f"{dkey}{c0 // 512}"
                if c0 < nrope:
                    pr, prk = prot.next()
                    P.mm(pr[:, 0:n], Rm, xn[:, 0:n], True, True, reads=[xnk, "cs"], writes=[prk])
                    c, ck = cr.next(); P.dma("sync", c[:, 0:n], cosd[:, c0:c0 + n], writes=[ck])
                    s, sk = sr.next(); P.dma("sync", s[:, 0:n], sind[:, c0:c0 + n], writes=[sk])
                    t1, t1k = t1r.next(); P.tt("gpsimd", t1[:, 0:n], xn[:, 0:n], c[:, 0:n], ALU.mult, reads=[xnk, ck], writes=[t1k])
                    t2, t2k = t2r.next(); P.tt("vector", t2[:, 0:n], pr[:, 0:n], s[:, 0:n], ALU.mult, reads=[prk, sk], writes=[t2k])
                    P.tt("gpsimd", dst[:, c0:c0 + n], t1[:, 0:n], t2[:, 0:n], ALU.add, reads=[t1k, t2k], writes=[dk])
                else:
                    P.copy("gpsimd", dst[:, c0:c0 + n], xn[:, 0:n], reads=[xnk], writes=[dk])

        prep(kd, kr, "kr", NKEY, 8192, kw2, 1, 1.0 / 64.0, ckd, skd)
        prep(qd, qr, "qr", QCOLS, 16 * 512, qw2, 0, 1.0, cqd, sqd)
        LA = 2
        pstR = Ring(bank[0:3])
        ptr = ring("PT", 4, [128, 512], BF16)
        sfr = ring("sf", 3, [128, 512])
        yr = ring("yo", 2, [128, 512])
        rcr = ring("rc", 2, [128, 4])
        its = []
        for qt in range(T):
            kts = list(range(NKT)) if qt < 16 else [64, 65]
            for g in range(2):
                for ii, kt in enumerate(kts):
                    its.append((qt, g, ii, kt, len(kts)))
        pend = {}
        ycur = {}
        for idx in range(len(its) + LA):
            if idx < len(its):
                qt, g, ii, kt, nk = its[idx]
                pst, pstk = pstR.next()
                P.mm(pst[:], kr[g * 64:(g + 1) * 64, kt * 128:(kt + 1) * 128], qr[g * 64:(g + 1) * 64, qt * 512:(qt + 1) * 512],
                     True, True, reads=[f"kr{kt // 4}", f"qr{qt}"], writes=[pstk])
                pend[idx] = (pst, pstk)
            j0 = idx - LA
            if j0 < 0:
                continue
            qt, g, ii, kt, nk = its[j0]
            pst, pstk = pend.pop(j0)
            PT, PTk = ptr.next()
            if STAGE_EXP:
                sf, sfk = sfr.next()
                P.copy("vector", sf[:], pst[:], reads=[pstk], writes=[sfk])
                P.act(PT[:], sf[:], AF.Exp, reads=[sfk], writes=[PTk])
            else:
                P.act(PT[:], pst[:], AF.Exp, reads=[pstk], writes=[PTk])
            for j in range(4):
                pb, pbk = bank[3 + j]
                P.mm(pb[:, 0:65], PT[:, j * 128:(j + 1) * 128], vaug[:, kt, g, :], ii == 0, ii == nk - 1,
                     reads=[PTk, f"vaug{kt}"], writes=[pbk])
            if ii == nk - 1:
                if g == 0:
                    ycur[qt] = yr.next()
                y, yk = ycur[qt]
                rc, rck = rcr.next()
                for j in range(4):
                    pb, pok = bank[3 + j]
                    po = pb[:, 0:65]
                    P.gen("vector", lambda e, rc=rc, po=po, j=j: e.reciprocal(out=rc[:, j:j + 1], in_=po[:, 64:65]), reads=[pok], writes=[rck + str(j)])
                    c0 = (g * 4 + j) * 64
                    P.ts("vector", y[:, c0:c0 + 64], po[:, 0:64], rc[:, j:j + 1], None, ALU.mult, reads=[pok, rck + str(j)], writes=[yk + f"{g}{j}"])
                if g == 1:
                    P.dma("sync", yo[qt], y[:], reads=[yk + f"{g_}{j}" for g_ in range(2) for j in range(4)], writes=[yk + "d"], final=True)
        P.emit()
    return nc


def rope_tables(n_lat, grid_w=64, theta=10000.0, hd=64):
    nf = hd // 4
    t = np.arange(n_lat)
    row = (t // grid_w).astype(np.float32)
    col = (t % grid_w).astype(np.float32)
    inv = (theta ** (-np.arange(nf, dtype=np.float32) / nf)).astype(np.float32)
    ar = row[:, None] * inv
    ac = col[:, None] * inv
    ang = np.concatenate([ar, ar, ac, ac], axis=-1)
    return np.cos(ang).astype(np.float32), np.sin(ang).astype(np.float32)


def rope_rot_matrix():
    R = np.zeros((64, 64), np.float32)
    for a in range(2):
        for f in range(16):
            R[a * 32 + 16 + f, a * 32 + f] = -1.0
            R[a * 32 + f, a * 32 + 16 + f] = 1.0
    return R


def run_k3(P_lat, P_ctx, qw, kw):
    nc = build_k3()
    B, n, _ = P_lat.shape
    nctx = P_ctx.shape[1]
    aq_l, ak_l, av_l = P_lat[..., 1040:1552], P_lat[..., 1552:1680], P_lat[..., 1680:1808]
    aq_c, ak_c, av_c = P_ctx[..., 1040:1552], P_ctx[..., 1552:1680], P_ctx[..., 1680:1808]
    cos, sin = rope_tables(n)
    R = rope_rot_matrix()
    Rm = np.zeros((128, 128), np.float32); Rm[:64, :64] = R; Rm[64:, 64:] = R
    ob = np.zeros((128, 128), np.float32); ob[:64, :64] = 1; ob[64:, 64:] = 1
    cst = np.concatenate([Rm, ob, np.tile(qw, 2)[:, None], np.tile(kw, 2)[:, None]], 1).astype(np.float32)
    cosk = np.ascontiguousarray(np.tile(cos.T, (2, 1))); sink = np.ascontiguousarray(np.tile(sin.T, (2, 1)))
    seg = n // 4
    in_maps = []
    for i in range(NCORES):
        b, s = i // 4, i % 4
        q = tok_shard(aq_l, aq_c, i).reshape(K1_TILES, 128, 2, 4, 64)
        qT = np.ascontiguousarray(q.transpose(2, 4, 0, 3, 1)).reshape(128, QCOLS)
        k = np.concatenate([ak_l[b], ak_c[b]], 0).reshape(NKEY, 2, 64)
        kT = np.ascontiguousarray(k.transpose(1, 2, 0)).reshape(128, NKEY)
        v = np.concatenate([av_l[b], av_c[b]], 0).reshape(NKT, 128, 2, 64)
        vv = np.ascontiguousarray(v.transpose(1, 0, 2, 3))
        cq = cos[s * seg:(s + 1) * seg].reshape(16, 128, 64)
        sq = sin[s * seg:(s + 1) * seg].reshape(16, 128, 64)
        cq = np.broadcast_to(cq.transpose(2, 0, 1)[None, :, :, None, :], (2, 64, 16, 4, 128)).reshape(128, 16 * 512)
        sq = np.broadcast_to(sq.transpose(2, 0, 1)[None, :, :, None, :], (2, 64, 16, 4, 128)).reshape(128, 16 * 512)
        in_maps.append({"qT": qT, "kT": kT, "v": vv, "cosq": np.ascontiguousarray(cq), "sinq": np.ascontiguousarray(sq),
                        "cosk": cosk, "sink": sink, "cst": cst})
    res = _run(nc, in_maps)
    return tok_unshard([r["o_ya"] for r in res.results], B, n, nctx)


NCH = NKT
NSEQ = NKEY
MASK_NEG = -30000.0


def build_k2():
    nc = bass.Bass("TRN2", target_bir_lowering=False)
    qpd = nc.dram_tensor("qpT", [64, NSEQ], F32, kind="ExternalInput").ap()
    kpd = nc.dram_tensor("kpT", [64, NSEQ], F32, kind="ExternalInput").ap()
    vd = nc.dram_tensor("v", [128, NCH, 64], F32, kind="ExternalInput").ap()
    od = nc.dram_tensor("og", [128, NCH, 64], F32, kind="ExternalInput").ap()
    gd = nc.dram_tensor("g4", [128, NCH, 4], F32, kind="ExternalInput").ap()
    gbd = nc.dram_tensor("gb", [NCH * 4], F32, kind="ExternalInput").ap()
    cwd = nc.dram_tensor("cw", [64, 8], F32, kind="ExternalInput").ap()
    nwd = nc.dram_tensor("nw", [64], F32, kind="ExternalInput").ap()
    cstd = nc.dram_tensor("cst", [128, 6, 128], F32, kind="ExternalInput").ap()
    yo = nc.dram_tensor("o_ym", [128, NCH, 64], F32, kind="ExternalOutput").ap()
    with ExitStack() as es:
        P = Prog(nc, es)
        sb, ps, ring = mk_alloc(nc, es)
        cst = sb("cst_s", [128, 6, 128])
        P.dma("sync", cst[:], cstd, writes=["cst"])
        Lm = [cst[:, 0, :], cst[:, 1, :]]
        ones, ident = cst[:, 2, :], cst[:, 3, :]
        mneg = [cst[:, 4, :], cst[:, 5, :]]
        idb = sb("idb", [128, 128], BF16)
        P.copy("vector", idb[:], ident, reads=["cst"], writes=["idb"])
        one1 = sb("one1", [128, 1])
        P.memset("vector", one1[:], 1.0, writes=["one1"])
        cw = sb("cw_s", [64, 8])
        P.dma("sync", cw[:], cwd, writes=["cw"])
        banks = [ps(f"bank{i}", [128, 512], F32) for i in range(8)]
        xin = sb("xin", [64, NSEQ]); cacc = sb("cacc", [64, NSEQ])
        qT = sb("qT_s", [64, NSEQ], BF16); kT = sb("kT_s", [64, NSEQ], BF16)
        segs = [(0, 256), (256, NSEQ)]
        for wi, (src, dst, post) in enumerate(((qpd, qT, 0.125), (kpd, kT, 1.0))):
            o = wi * 4
            half = NSEQ // 2
            P.dma("sync", xin[:, 0:half], src[:, 0:half], writes=["xin_a"])
            P.dma("gpsimd", xin[:, half:], src[:, half:], writes=["xin_b"])
            P.ts("vector", cacc[:], xin[:], cw[:, o + 1:o + 2], cw[:, o + 3:o + 4], ALU.mult, ALU.add,
                 reads=["xin_a", "xin_b", "cw"], writes=["cacc"])
            for (a, b) in segs:
                P.stt("vector", cacc[:, a + 1:b], xin[:, a:b - 1], cw[:, o:o + 1], cacc[:, a + 1:b], ALU.mult, ALU.add,
                      reads=["xin_a", "xin_b", "cw", "cacc"], writes=["cacc"])
                P.stt("vector", cacc[:, a:b - 1], xin[:, a + 1:b], cw[:, o + 2:o + 3], cacc[:, a:b - 1], ALU.mult, ALU.add,
                      reads=["xin_a", "xin_b", "cw", "cacc"], writes=["cacc"])
            P.act(cacc[:], cacc[:], AF.Silu, reads=["cacc"], writes=["cacc"])
            P.ts("gpsimd", dst[:], cacc[:], post, None, ALU.mult, reads=["cacc"], writes=[f"T{wi}"])
            P.memset("vector", xin[:, 0:1], 0.0, writes=["xin_a", "xin_b"]) if wi == 0 else None
        ktok = sb("ktok", [128, NCH, 64], BF16)
        for c0 in range(0, NCH, 8):
            n = min(8, NCH - c0)
            pb = banks[7]
            for c in range(c0, c0 + n):
                pt = pb[:, :].bitcast(BF16)[:, (c - c0) * 64:(c - c0 + 1) * 64]
                P.tr(pt, kT[:, c * 128:(c + 1) * 128], idb[0:64, 0:64], reads=["T1", "idb"], writes=["bank7"])
            P.copy("vector", ktok[:, c0:c0 + n, :], pb[:, :].bitcast(BF16)[:, 0:n * 64].rearrange("p (c d) -> p c d", d=64),
                   reads=["bank7"], writes=["ktok"])
        vf = sb("vf", [128, NCH, 65]); vb = sb("vb", [128, NCH, 65], BF16)
        P.memset("gpsimd", vf[:], 1.0, writes=["vf"])
        vtmp = sb("vtmp", [128, NCH, 64])
        P.dma("sync", vtmp[:], vd, writes=["vtmp"])
        P.copy("gpsimd", vf[:, :, 0:64], vtmp[:], reads=["vtmp", "vf"], writes=["vf"])
        P.copy("gpsimd", vb[:], vf[:], reads=["vf"], writes=["vb"])
        G = sb("G", [128, NCH, 4]); GB = sb("GB", [128, NCH, 4])
        P.dma("sync", G[:], gd, writes=["G"])
        P.dma("sync", GB[:].rearrange("p c g -> p (c g)"), bcast_rows(gbd, 128), writes=["GB"])
        P.tt("vector", G[:], G[:], GB[:], ALU.add, reads=["G", "GB"], writes=["G"])
        LF = sb("LF", [128, 2, NCH]); LI = sb("LI", [128, 2, NCH]); TA = sb("TA", [128, 2, NCH]); TB = sb("TB", [128, 2, NCH])
        for dd in range(2):
            P.copy("vector", LI[:, dd, :], G[:, :, 2 * dd], reads=["G"], writes=[f"LI{dd}"])
            P.copy("vector", TA[:, dd, :], G[:, :, 2 * dd + 1], reads=["G"], writes=["TA"])
        P.act(TB[:], TA[:], AF.Abs, reads=["TA"], writes=["TB"])
        P.act(TB[:], TB[:], AF.Exp, reads=["TB"], writes=["TB"], scale=-1.0)
        P.act(TB[:], TB[:], AF.Ln, reads=["TB", "one1"], writes=["TB"], bias=one1[:, 0:1], scale=1.0)
        P.ts("vector", TA[:], TA[:], 0.0, None, ALU.min, reads=["TA"], writes=["TA"])
        P.tt("vector", LF[:], TA[:], TB[:], ALU.subtract, reads=["TA", "TB"], writes=["LF"])
        BC = sb("BC", [128, 2, NCH]); TOT = sb("TOT", [128, 2, NCH]); AA = sb("AA", [128, 2, NCH])
        BD = sb("BD", [128, 2, NCH]); WW = sb("WW", [128, 2, NCH]); DEC = sb("DEC", [128, 2, NCH])
        b6 = banks[6]
        for dd in range(2):
            P.mm(b6[:, dd * NCH:(dd + 1) * NCH], Lm[dd], LF[:, dd, :], True, True, reads=["LF", "cst"], writes=["bank6"])
        P.mm(b6[:, 2 * NCH:4 * NCH], ones, LF[:].rearrange("p a c -> p (a c)"), True, True, reads=["LF", "cst"], writes=["bank6"])
        P.copy("vector", BC[:].rearrange("p a c -> p (a c)"), b6[:, 0:2 * NCH], reads=["bank6"], writes=["BC"])
        P.copy("vector", TOT[:].rearrange("p a c -> p (a c)"), b6[:, 2 * NCH:4 * NCH], reads=["bank6"], writes=["TOT"])
        P.act(AA[:], BC[:], AF.Exp, reads=["BC"], writes=["AA"])
        P.tt("vector", BD[:], LI[:], BC[:], ALU.subtract, reads=["LI0", "LI1", "BC"], writes=["BD"])
        P.tt("vector", WW[:], TOT[:], BD[:], ALU.add, reads=["TOT", "BD"], writes=["WW"])
        P.act(WW[:], WW[:], AF.Exp, reads=["WW"], writes=["WW"])
        P.act(DEC[:], TOT[:], AF.Exp, reads=["TOT"], writes=["DEC"])
        S = [sb(f"S{dd}", [64, 65]) for dd in range(2)]
        Sb = [sb(f"Sb{dd}", [64, 65], BF16) for dd in range(2)]
        for dd in range(2):
            P.memset("vector", S[dd][:], 0.0, writes=[f"S{dd}"])
            P.memset("vector", Sb[dd][:], 0.0, writes=[f"Sb{dd}"])
        hb = [sb(f"hb{dd}", [128, NCH, 64]) for dd in range(2)]
        lfr = ring("lfrep", 2, [128, 128]); dtr = ring("Dt", 2, [128, 128]); ptr = ring("PTm", 2, [128, 128], BF16)
        tmr = ring("tmpi", 2, [128, 65]); ttr = ring("tot", 2, [128, 65]); dnr = ring("den", 2, [128, 4])
        wvr = ring("wv", 2, [128, 65], BF16)
        pD = Ring([(banks[0], "bank0"), (banks[1], "bank1")])
        pST = Ring([(banks[2], "bank2"), (banks[3], "bank3")])
        pOI = Ring([(banks[4], "bank4"), (banks[5], "bank5")])
        order = [list(range(NCH)), [1, 0] + list(range(NCH - 1, 1, -1))]
        for step in range(NCH):
            for dd in range(2):
                c = order[dd][step]
                cs_ = slice(c * 128, (c + 1) * 128)
                lf, lfk = lfr.next()
                P.ts("vector", lf[:], ones, LF[:, dd, c:c + 1], None, ALU.mult, reads=["cst", "LF"], writes=[lfk])
                pd, pdk = pD.next()
                P.mm(pd[:, 0:128], lf[:], Lm[dd], True, False, reads=[lfk, "cst"], writes=[pdk])
                P.mm(pd[:, 0:128], ident, mneg[dd], False, True, reads=["cst"], writes=[pdk])
                dt_, dtk = dtr.next()
                P.act(dt_[:], pd[:, 0:128], AF.Exp, reads=[pdk, "BD"], writes=[dtk], bias=BD[:, dd, c:c + 1], scale=1.0)
                pst, pstk = pST.next()
                P.mm(pst[:, 0:128], kT[:, cs_], qT[:, cs_], True, True, reads=["T0", "T1"], writes=[pstk])
                PT, PTk = ptr.next()
                P.tt("vector", PT[:], pst[:, 0:128], dt_[:], ALU.mult, reads=[pstk, dtk], writes=[PTk])
                poi, poik = pOI.next()
                P.mm(poi[:, 0:65], PT[:], vb[:, c, :], True, True, reads=[PTk, "vb"], writes=[poik + "o"])
                P.mm(poi[:, 128:193], qT[:, cs_], Sb[dd][:], True, True, reads=["T0", f"Sb{dd}"], writes=[poik + "i"])
                tm, tmk = tmr.next()
                P.act(tm[:], poi[:, 128:193], AF.Copy, reads=[poik + "i", "AA"], writes=[tmk], scale=AA[:, dd, c:c + 1])
                tt_, ttk = ttr.next()
                P.tt("vector", tt_[:], poi[:, 0:65], tm[:], ALU.add, reads=[poik + "o", tmk], writes=[ttk])
                dn, dnk = dnr.next()
                P.ts("vector", dn[:, 0:1], tt_[:, 64:65], -1.0, None, ALU.mult, reads=[ttk], writes=[dnk])
                P.stt("vector", dn[:, 1:2], dn[:, 0:1], 1.0, tt_[:, 64:65], ALU.max, ALU.max, reads=[dnk, ttk], writes=[dnk])
                P.gen("vector", lambda e, dn=dn: e.reciprocal(out=dn[:, 2:3], in_=dn[:, 1:2]), reads=[dnk], writes=[dnk])
                P.ts("vector", hb[dd][:, c, :], tt_[:, 0:64], dn[:, 2:3], None, ALU.mult, reads=[ttk, dnk], writes=[f"hb{dd}_{c}"])
                wv, wvk = wvr.next()
                P.ts("vector", wv[:], vf[:, c, :], WW[:, dd, c:c + 1], None, ALU.mult, reads=["vf", "WW"], writes=[wvk])
                p7 = banks[7]
                P.mm(p7[0:64, 256 + dd * 128:256 + dd * 128 + 65], ktok[:, c, :], wv[:], True, True, reads=["ktok", wvk], writes=[f"b7s{dd}"])
                P.stt("vector", S[dd][:], S[dd][:], DEC[0:64, dd, c:c + 1], p7[0:64, 256 + dd * 128:256 + dd * 128 + 65], ALU.mult, ALU.add,
                      reads=[f"S{dd}", "DEC", f"b7s{dd}"], writes=[f"S{dd}"])
                P.copy("gpsimd", Sb[dd][:], S[dd][:], reads=[f"S{dd}"], writes=[f"Sb{dd}"])
        hk = [f"hb{dd}_{c}" for dd in range(2) for c in range(NCH)]
        P.tt("vector", hb[0][:], hb[0][:], hb[1][:], ALU.add, reads=hk, writes=["hsum"])
        sq = hb[1]
        P.tt("gpsimd", sq[:], hb[0][:], hb[0][:], ALU.mult, reads=["hsum"], writes=["hsq"])
        ssum = sb("ssum", [128, NCH])
        P.gen("vector", lambda e: e.reduce_sum(out=ssum[:], in_=sq[:], axis=AX.X), reads=["hsq"], writes=["ssum"])
        P.ts("vector", ssum[:], ssum[:], 1.0 / 64.0, RMS_EPS, ALU.mult, ALU.add, reads=["ssum"], writes=["ssum"])
        P.act(ssum[:], ssum[:], AF.Sqrt, reads=["ssum"], writes=["ssum"])
        P.gen("vector", lambda e: e.reciprocal(out=ssum[:], in_=ssum[:]), reads=["ssum"], writes=["ssum"])
        nw = sb("nw_s", [128, 64])
        P.dma("sync", nw[:], bcast_rows(nwd, 128), writes=["nw"])
        og = vtmp
        P.dma("sync", og[:], od, reads=["vf"], writes=["og"])
        P.act(og[:], og[:], AF.Sigmoid, reads=["og"], writes=["og"])
        for c in range(NCH):
            P.stt("vector", hb[0][:, c, :], hb[0][:, c, :], ssum[:, c:c + 1], nw[:], ALU.mult, ALU.mult,
                  reads=["hsum", "ssum", "nw"], writes=[f"hn{c}"])
        P.tt("gpsimd", hb[0][:], hb[0][:], og[:], ALU.mult, reads=[f"hn{c}" for c in range(NCH)] + ["og"], writes=["ym"])
        P.dma("sync", yo, hb[0][:], reads=["ym"], final=True)
        P.emit()
    return nc


def run_k2(P_lat, P_ctx, conv_w, conv_b, gate_b, norm_w):
    nc = build_k2()
    B, n, _ = P_lat.shape
    nctx = P_ctx.shape[1]
    s_idx, j_idx = np.meshgrid(np.arange(128), np.arange(128), indexing="ij")
    Lf = (s_idx <= j_idx).astype(np.float32); Lb = (s_idx >= j_idx).astype(np.float32)
    cst = np.stack([Lf, Lb, np.ones((128, 128), np.float32), np.eye(128, dtype=np.float32),
                    np.where(s_idx <= j_idx, 0.0, MASK_NEG).astype(np.float32),
                    np.where(s_idx >= j_idx, 0.0, MASK_NEG).astype(np.float32)], 1)
    in_maps = []
    for i in range(NCORES):
        b, h = i // 4, i % 4
        seq = np.concatenate([P_ctx[b], P_lat[b]], 0)
        qs, ks = slice(h * 64, (h + 1) * 64), slice(256 + h * 64, 256 + (h + 1) * 64)
        tm = lambda a: np.ascontiguousarray(a.reshape(NCH, 128, -1).transpose(1, 0, 2))
        gcols = [1024 + 0 * 8 + 0 * 4 + h, 1024 + 0 * 8 + 1 * 4 + h, 1024 + 1 * 8 + 0 * 4 + h, 1024 + 1 * 8 + 1 * 4 + h]
        gb = np.array([gate_b[0, 0, h], gate_b[0, 1, h], gate_b[1, 0, h], gate_b[1, 1, h]], np.float32)
        cw = np.concatenate([conv_w[:, qs].T, conv_b[qs][:, None], conv_w[:, ks].T, conv_b[ks][:, None]], 1).astype(np.float32)
        in_maps.append({"qpT": np.ascontiguousarray(seq[:, qs].T), "kpT": np.ascontiguousarray(seq[:, ks].T),
                        "v": tm(seq[:, 512 + h * 64:512 + (h + 1) * 64]), "og": tm(seq[:, 768 + h * 64:768 + (h + 1) * 64]),
                        "g4": tm(seq[:, gcols]), "gb": np.ascontiguousarray(np.tile(gb, NCH)), "cw": np.ascontiguousarray(cw),
                        "nw": np.ascontiguousarray(norm_w[h * 64:(h + 1) * 64]), "cst": np.ascontiguousarray(cst)})
    res = _run(nc, in_maps)
    ym_l = np.zeros((B, n, 256), np.float32); ym_c = np.zeros((B, nctx, 256), np.float32)
    for i in range(NCORES):
        b, h = i // 4, i % 4
        y = res.results[i]["o_ym"].transpose(1, 0, 2).reshape(NSEQ, 64)
        ym_c[b, :, h * 64:(h + 1) * 64] = y[:nctx]
        ym_l[b, :, h * 64:(h + 1) * 64] = y[nctx:]
    return ym_l, ym_c


HCH = 32
TWO_PI = 2.0 * np.pi
RND_MAGIC = 12582912.0


def fft_tables(N1):
    N2 = 128
    N = N1 * N2
    ar = np.arange
    c, s = np.cos, np.sin
    th = TWO_PI * ar(N1)[:, None] * ar(N1)[None] / N1
    F1c = np.concatenate([c(th), -s(th)], 1)
    th = TWO_PI * ar(N2)[:, None] * ar(N1)[None] / N
    twRR = np.concatenate([c(th), c(th)], 1); twII = np.concatenate([-s(th), -s(th)], 1)
    th = TWO_PI * ar(N2)[:, None] * ar(N2)[None] / N2
    F2re, F2im, nF2im = c(th), -s(th), s(th)
    G2c = np.concatenate([c(th), s(th)], 1); G2s = np.concatenate([-s(th), c(th)], 1)
    th = TWO_PI * ar(N1)[:, None] * ar(N2)[None] / N
    twcRR = np.concatenate([c(th), c(th)], 1); twcII = np.concatenate([s(th), s(th)], 1)
    th = TWO_PI * ar(N1)[:, None] * ar(N1 // 2)[None] / N1
    G1re, nG1im = c(th) / N, -s(th) / N
    f = lambda a: np.ascontiguousarray(a.astype(np.float32))
    return dict(F1c=f(F1c), twRR=f(twRR), twII=f(twII), F2re=f(F2re), F2im=f(F2im), nF2im=f(nF2im), G2c=f(G2c), G2s=f(G2s),
                twcRR=f(twcRR), twcII=f(twcII), G1re=f(G1re), nG1im=f(nG1im))


TAB_ORDER = ["F1c", "twRR", "twII", "F2re", "F2im", "nF2im", "G2c", "G2s", "twcRR", "twcII", "G1re", "nG1im"]


def hyena_consts(n):
    N = 2 * n
    tau = np.arange(N)
    pos = np.where(tau < n, tau, N - tau).astype(np.float32)
    t = (pos / np.float32(n)).astype(np.float32)
    bands = np.arange(1, 17, dtype=np.float32)
    ang = (np.float32(TWO_PI) * t[:, None] * bands).astype(np.float32)
    feats = np.concatenate([t[:, None], np.cos(ang), np.sin(ang)], -1).astype(np.float32)
    lt = abs(np.log(1e-2))
    deltas = np.linspace(lt / 1.5, lt / 0.3, 256, dtype=np.float32)
    win = (np.exp(-t[:, None] * deltas) + np.float32(0.05)).astype(np.float32)
    win[n] = 0.0
    return np.ascontiguousarray(feats.T), np.ascontiguousarray(win.T)


def interleave(gens, width):
    it = iter(gens)
    active = []
    while True:
        while len(active) < width:
            g = next(it, None)
            if g is None:
                break
            active.append(g)
        if not active:
            return
        for g in list(active):
            try:
                next(g)
            except StopIteration:
                active.remove(g)
        yield


def build_k4(sizes):
    nc = bass.Bass("TRN2", target_bir_lowering=False)
    B = 2
    dr = {}
    for si, n in enumerate(sizes):
        N1 = 2 * n // 128
        dr[si] = dict(
            u=nc.dram_tensor(f"u{si}", [3, HCH, B, n + 2], F32, kind="ExternalInput").ap(),
            feats=nc.dram_tensor(f"feats{si}", [33, 2 * n], F32, kind="ExternalInput").ap(),
            win=nc.dram_tensor(f"win{si}", [64, 2 * n], F32, kind="ExternalInput").ap(),
            taps=nc.dram_tensor(f"taps{si}", [64, 2 * n], F32, kind="ExternalOutput").ap(),
            out=nc.dram_tensor(f"o_yh{si}", [HCH, B, n], F32, kind="ExternalOutput").ap(),
            tabs={k: nc.dram_tensor(f"t{si}_{k}", list(v.shape), F32, kind="ExternalInput").ap()
                  for k, v in fft_tables(N1).items()})
    mlpd = nc.dram_tensor("mlp", [64, 64 + 64 + 128 + 4], F32, kind="ExternalInput").ap()
    cwd = nc.dram_tensor("cwv", [3 * HCH * 4], F32, kind="ExternalInput").ap()
    skd = nc.dram_tensor("skv", [2 * HCH], F32, kind="ExternalInput").ap()
    cstd = nc.dram_tensor("cst", [128, 256], F32, kind="ExternalInput").ap()
    with ExitStack() as es:
        P = Prog(nc, es)
        sb, ps, ring = mk_alloc(nc, es)
        banks = [(ps(f"bank{i}", [128, 512], F32), f"bank{i}") for i in range(8)]
        cst = sb("cst_s", [128, 256]); P.dma("sync", cst[:], cstd, writes=["cst"])
        ident, ones = cst[:, 0:128], cst[:, 128:256]
        mlp = sb("mlp_s", [64, 260]); P.dma("sync", mlp[:], mlpd, writes=["mlp"])
        w1, w2 = mlp[0:33, 0:64], mlp[:, 64:128]
        w3 = [mlp[:, 128:192], mlp[:, 192:256]]
        b1, f0, b2, f1 = (mlp[:, 256 + j:257 + j] for j in range(4))
        cwb = sb("cwb", [128, 3 * HCH * 4]); P.dma("sync", cwb[:], bcast_rows(cwd, 128), writes=["cwb"])
        skb = sb("skb", [128, 2 * HCH]); P.dma("sync", skb[:], bcast_rows(skd, 128), writes=["skb"])
        a1r = ring("a1", 4, [64, 512]); rrr = ring("rr", 4, [64, 512]); hhr = ring("hh", 4, [64, 512])
        ftr = ring("ft", 4, [33, 512]); wnr = ring("wn", 4, [64, 512]); tpr = ring("tp", 4, [64, 512])
        As_r = ring("As", 4, [128, 256]); t1r = ring("t1", 4, [128, 256]); t2r = ring("t2", 4, [128, 256])
        Br = ring("Bc", 4, [128, 256]); Yr = ring("Yc", 4, [128, 256]); Dr = ring("Dc", 4, [128, 256])
        pr4 = [ring(f"pp{j}", 4, [128, 128]) for j in range(4)]
        xir = ring("xi", 4, [64, 3, 130]); cvr = ring("cv", 4, [64, 3, 128]); tgr = ring("tg", 4, [64, 128])
        z1r = ring("z1", 4, [64, 128]); z2r = ring("z2", 4, [64, 128]); tlr = ring("tl", 4, [128, 128])
        bkP = Ring(banks[0:8])
        bkA = bkX = bkC = bkY = bkP

        def sin_layer(psrc, pk, n_, bias, freq, dst, dstk):
            a1, a1k = a1r.next(); rr, rrk = rrr.next()
            P.ts("vector", a1[:, 0:n_], psrc, bias, freq, ALU.add, ALU.mult, reads=[pk, "mlp"], writes=[a1k])
            yield
            P.ts("gpsimd", rr[:, 0:n_], a1[:, 0:n_], 1.0 / TWO_PI, RND_MAGIC, ALU.mult, ALU.add, reads=[a1k], writes=[rrk])
            yield
            P.ts("gpsimd", rr[:, 0:n_], rr[:, 0:n_], RND_MAGIC, -TWO_PI, ALU.subtract, ALU.mult, reads=[rrk], writes=[rrk])
            yield
            P.tt("gpsimd", rr[:, 0:n_], rr[:, 0:n_], a1[:, 0:n_], ALU.add, reads=[rrk, a1k], writes=[rrk])
            yield
            P.ts("gpsimd", rr[:, 0:n_], rr[:, 0:n_], np.pi, -np.pi, ALU.min, ALU.max, reads=[rrk], writes=[rrk])
            yield
            P.act(dst, rr[:, 0:n_], AF.Sin, reads=[rrk], writes=[dstk])
            yield

        def cmul(src, srck, n1p, W, tRR, tII, tk, dst, dstk):
            t1, t1k = t1r.next(); t2, t2k = t2r.next()
            P.tt("vector", t1[0:n1p, 0:2 * W], src, tRR, ALU.mult, reads=[srck, tk], writes=[t1k])
            yield
            P.tt("gpsimd", t2[0:n1p, 0:2 * W], src, tII, ALU.mult, reads=[srck, tk], writes=[t2k])
            yield
            P.tt("vector", dst[0:n1p, 0:W], t1[0:n1p, 0:W], t2[0:n1p, W:2 * W], ALU.subtract, reads=[t1k, t2k], writes=[dstk + "r"])
            yield
            P.tt("gpsimd", dst[0:n1p, W:2 * W], t2[0:n1p, 0:W], t1[0:n1p, W:2 * W], ALU.add, reads=[t1k, t2k], writes=[dstk + "i"])
            yield

        def size_body(si, n):
            N = 2 * n
            N1 = N // 128
            Kd = N1 // 2
            d = dr[si]
            T = {}
            for k in TAB_ORDER:
                shp = list(d["tabs"][k].shape)
                T[k] = sb(f"T{si}_{k}", shp)
                P.dma("sync", T[k][:], d["tabs"][k], writes=[f"tab{si}"] if k == TAB_ORDER[-1] else [f"tab{si}_{k}"])
                yield
            tabk = [f"tab{si}"] + [f"tab{si}_{k}" for k in TAB_ORDER[:-1]]
            CH = min(512, n)
            nchunk = N // CH
            l1p = sb(f"l1p{si}", [64, nchunk])
            def mlp_chain(ci):
                c0 = ci * CH
                dirn = 0 if c0 < n else 1
                ft, ftk = ftr.next(); P.dma("sync", ft[:, 0:CH], d["feats"][:, c0:c0 + CH], writes=[ftk])
                wn, wnk = wnr.next(); P.dma("sync", wn[:, 0:CH], d["win"][:, c0:c0 + CH], writes=[wnk])
                bk, bkk = bkA.next()
                P.mm(bk[0:64, 0:CH], w1, ft[:, 0:CH], True, True, reads=[ftk, "mlp"], writes=[bkk])
                yield
                h1, h1k = hhr.next()
                yield from sin_layer(bk[0:64, 0:CH], bkk, CH, b1, f0, h1[:, 0:CH], h1k)
                bk, bkk = bkX.next()
                P.mm(bk[0:64, 0:CH], w2, h1[:, 0:CH], True, True, reads=[h1k, "mlp"], writes=[bkk])
                yield
                h2, h2k = hhr.next()
                yield from sin_layer(bk[0:64, 0:CH], bkk, CH, b2, f1, h2[:, 0:CH], h2k)
                bk, bkk = bkC.next()
                P.mm(bk[0:64, 0:CH], w3[dirn], h2[:, 0:CH], True, True, reads=[h2k, "mlp"], writes=[bkk])
                yield
                tp, tpk = tpr.next()
                P.tt("vector", tp[:, 0:CH], bk[0:64, 0:CH], wn[:, 0:CH], ALU.mult, reads=[bkk, wnk], writes=[tpk])
                yield
                P.gen("vector", lambda e, tp=tp, ci=ci, CH=CH, l1p=l1p: e.reduce_sum(out=l1p[:, ci:ci + 1], in_=tp[:, 0:CH], axis=AX.X,
                                                                         apply_absolute_value=True), reads=[tpk], writes=[f"l1p{si}_{ci}"])
                yield
                P.dma("sync", d["taps"][:, c0:c0 + CH], tp[:, 0:CH], reads=[tpk], writes=[f"taps{si}"], final=True)
                yield
            yield from interleave([mlp_chain(ci) for ci in range(nchunk)], 2)
            l1 = sb(f"l1_{si}", [64, 2])
            P.gen("vector", lambda e, l1=l1, l1p=l1p: e.reduce_sum(out=l1[:, 0:1], in_=l1p[:], axis=AX.X),
                  reads=[f"l1p{si}_{ci}" for ci in range(nchunk)], writes=[f"l1{si}"])
            yield
            P.gen("vector", lambda e, l1=l1: e.reciprocal(out=l1[:, 1:2], in_=l1[:, 0:1]), reads=[f"l1{si}"], writes=[f"l1{si}"])
            yield
            dg = sb(f"dg{si}", [64, 64])
            P.ts("vector", dg[:], ident[0:64, 0:64], l1[:, 1:2], None, ALU.mult, reads=["cst", f"l1{si}"], writes=[f"dg{si}"])
            yield
            bk, bkk = bkY.next()
            P.mm(bk[:, 0:64], ones[0:64, :], dg[:], True, True, reads=["cst", f"dg{si}"], writes=[bkk])
            yield
            rl1b = sb(f"rl1b{si}", [128, 64])
            P.copy("vector", rl1b[:], bk[:, 0:64], reads=[bkk], writes=[f"rl1b{si}"])
            yield

            def fwd_fft(xt, xk, Krows):
                bA, bAk = bkA.next()
                P.mm(bA[:, 0:2 * N1], xt, T["F1c"][0:Krows, :], True, True, reads=[xk] + tabk, writes=[bAk])
                yield
                As, Ask = As_r.next()
                P.copy("scalar", As[:, 0:2 * N1], bA[:, 0:2 * N1], reads=[bAk], writes=[Ask])
                yield
                Bc, Bck = Br.next()
                yield from cmul(As[:, 0:2 * N1], Ask, 128, N1, T["twRR"][:], T["twII"][:], tabk[0], Bc, Bck)
                bX, bXk = bkX.next()
                Bre, Bim = Bc[:, 0:N1], Bc[:, N1:2 * N1]
                P.mm(bX[:, 0:N1], T["F2re"][:], Bre, True, False, reads=[Bck + "r"] + tabk, writes=[bXk])
                P.mm(bX[:, 0:N1], T["nF2im"][:], Bim, False, True, reads=[Bck + "i"] + tabk, writes=[bXk])
                yield
                P.mm(bX[:, N1:2 * N1], T["F2re"][:], Bim, True, False, reads=[Bck + "i"] + tabk, writes=[bXk])
                P.mm(bX[:, N1:2 * N1], T["F2im"][:], Bre, False, True, reads=[Bck + "r"] + tabk, writes=[bXk])
                yield
                return bX, bXk

            H = sb(f"H{si}", [128, 64, 2 * N1])
            def filt_chain(oc):
                tl, tlk = tlr.next()
                P.dma("sync", tl[0:N1, :], d["taps"][oc].rearrange("(a b) -> a b", b=128), reads=[f"taps{si}"], writes=[tlk])
                yield
                bX, bXk = yield from fwd_fft(tl[0:N1, :], tlk, N1)
                P.ts("vector", H[:, oc, :], bX[:, 0:2 * N1], rl1b[:, oc:oc + 1], None, ALU.mult, reads=[bXk, f"rl1b{si}"], writes=[f"H{si}_{oc}"])
                yield

            yield from interleave([filt_chain(oc) for oc in range(64)], 4)
            def long_conv(zt, zk, o, c):
                bX, bXk = yield from fwd_fft(zt, zk, Kd)
                oc = o * HCH + c
                Hre, Him = H[:, oc, 0:N1], H[:, oc, N1:2 * N1]
                hk = f"H{si}_{oc}"
                pp = [r.next() for r in pr4]
                P.tt("vector", pp[0][0][:, 0:N1], bX[:, 0:N1], Hre, ALU.mult, reads=[bXk, hk], writes=[pp[0][1]])
                yield
                P.tt("vector", pp[1][0][:, 0:N1], bX[:, N1:2 * N1], Him, ALU.mult, reads=[bXk, hk], writes=[pp[1][1]])
                yield
                P.tt("vector", pp[2][0][:, 0:N1], bX[:, 0:N1], Him, ALU.mult, reads=[bXk, hk], writes=[pp[2][1]])
                yield
                P.tt("vector", pp[3][0][:, 0:N1], bX[:, N1:2 * N1], Hre, ALU.mult, reads=[bXk, hk], writes=[pp[3][1]])
                yield
                Yc, Yck = Yr.next()
                P.tt("gpsimd", Yc[:, 0:N1], pp[0][0][:, 0:N1], pp[1][0][:, 0:N1], ALU.subtract, reads=[pp[0][1], pp[1][1]], writes=[Yck + "r"])
                yield
                P.tt("gpsimd", Yc[:, N1:2 * N1], pp[2][0][:, 0:N1], pp[3][0][:, 0:N1], ALU.add, reads=[pp[2][1], pp[3][1]], writes=[Yck + "i"])
                yield
                bC, bCk = bkC.next()
                P.mm(bC[0:N1, 0:256], Yc[:, 0:N1], T["G2c"][:], True, False, reads=[Yck + "r"] + tabk, writes=[bCk])
                P.mm(bC[0:N1, 0:256], Yc[:, N1:2 * N1], T["G2s"][:], False, True, reads=[Yck + "i"] + tabk, writes=[bCk])
                yield
                Cs, Csk = As_r.next()
                P.copy("scalar", Cs[0:N1, :], bC[0:N1, 0:256], reads=[bCk], writes=[Csk])
                yield
                Dc, Dck = Dr.next()
                yield from cmul(Cs[0:N1, :], Csk, N1, 128, T["twcRR"][:], T["twcII"][:], tabk[0], Dc, Dck)
                bY, bYk = bkY.next()
                P.mm(bY[0:Kd, 0:128], T["G1re"][:], Dc[0:N1, 0:128], True, False, reads=[Dck + "r"] + tabk, writes=[bYk])
                P.mm(bY[0:Kd, 0:128], T["nG1im"][:], Dc[0:N1, 128:256], False, True, reads=[Dck + "i"] + tabk, writes=[bYk])
                yield
                return bY, bYk

            def data_chain(c, b):
                xi, xik = xir.next()
                src = bass.AP(tensor=d["u"].tensor, offset=d["u"][0, c, b, 0].offset,
                              ap=[[128, Kd], [HCH * B * (n + 2), 3], [1, 130]])
                P.dma("gpsimd", xi[0:Kd], src, writes=[xik])
                yield
                cv, cvk = cvr.next()
                for p in range(3):
                    wo = (p * HCH + c) * 4
                    P.ts("vector", cv[0:Kd, p, :], xi[0:Kd, p, 1:129], cwb[0:Kd, wo + 1:wo + 2], cwb[0:Kd, wo + 3:wo + 4], ALU.mult, ALU.add,
                         reads=[xik, "cwb"], writes=[cvk + str(p)])
                    yield
                    P.stt("vector", cv[0:Kd, p, :], xi[0:Kd, p, 0:128], cwb[0:Kd, wo:wo + 1], cv[0:Kd, p, :], ALU.mult, ALU.add,
                          reads=[xik, "cwb", cvk + str(p)], writes=[cvk + str(p)])
                    yield
                    P.stt("vector", cv[0:Kd, p, :], xi[0:Kd, p, 2:130], cwb[0:Kd, wo + 2:wo + 3], cv[0:Kd, p, :], ALU.mult, ALU.add,
                          reads=[xik, "cwb", cvk + str(p)], writes=[cvk + str(p)])
                    yield
                bY, bYk = yield from long_conv(cv[0:Kd, 0, :], cvk + "0", 0, c)
                tg, tgk = tgr.next()
                P.stt("vector", tg[0:Kd, :], cv[0:Kd, 0, :], skb[0:Kd, c:c + 1], bY[0:Kd, 0:128], ALU.mult, ALU.add,
                      reads=[cvk + "0", "skb", bYk], writes=[tgk])
                yield
                z1, z1k = z1r.next()
                P.tt("gpsimd", z1[0:Kd, :], tg[0:Kd, :], cv[0:Kd, 1, :], ALU.mult, reads=[tgk, cvk + "1"], writes=[z1k])
                yield
                bY, bYk = yield from long_conv(z1[0:Kd, :], z1k, 1, c)
                tg, tgk = tgr.next()
                P.stt("vector", tg[0:Kd, :], z1[0:Kd, :], skb[0:Kd, HCH + c:HCH + c + 1], bY[0:Kd, 0:128], ALU.mult, ALU.add,
                      reads=[z1k, "skb", bYk], writes=[tgk])
                yield
                z2, z2k = z2r.next()
                P.tt("gpsimd", z2[0:Kd, :], tg[0:Kd, :], cv[0:Kd, 2, :], ALU.mult, reads=[tgk, cvk + "2"], writes=[z2k])
                yield
                P.dma("sync", d["out"][c, b].rearrange("(a b) -> a b", b=128), z2[0:Kd, :], reads=[z2k], writes=[z2k + "d"], final=True)
                yield
            yield from interleave([data_chain(c, b) for c in range(HCH) for b in range(B)], 4)

        for si, n in enumerate(sizes):
            for _ in size_body(si, n):
                pass
        P.emit()
    return nc


def run_k4(hy_list, conv_w, conv_b, fparams, skip):
    f_w1, f_b1, f_freq, f_w2, f_b2, f_w3 = fparams
    sizes = [h.shape[1] for h in hy_list]
    nc = build_k4(sizes)
    B = 2
    cst = np.concatenate([np.eye(128, dtype=np.float32), np.ones((128, 128), np.float32)], 1)
    consts = [hyena_consts(n) for n in sizes]
    tabs = [fft_tables(2 * n // 128) for n in sizes]
    w3r = f_w3.reshape(64, 2, 2, 256)
    in_maps = []
    for i in range(NCORES):
        cs = slice(i * HCH, (i + 1) * HCH)
        m = {"cst": cst}
        mlp = np.zeros((64, 260), np.float32)
        mlp[0:33, 0:64] = f_w1
        mlp[:, 64:128] = f_w2
        mlp[:, 128:192] = w3r[:, 0, :, cs].reshape(64, 64)
        mlp[:, 192:256] = w3r[:, 1, :, cs].reshape(64, 64)
        mlp[:, 256] = f_b1; mlp[:, 257] = f_freq[0]; mlp[:, 258] = f_b2; mlp[:, 259] = f_freq[1]
        m["mlp"] = mlp
        cw = np.zeros((3, HCH, 4), np.float32)
        for p in range(3):
            ch = slice(p * 256 + i * HCH, p * 256 + (i + 1) * HCH)
            cw[p, :, 0:3] = conv_w[:, ch].T
            cw[p, :, 3] = conv_b[ch]
        m["cwv"] = cw.reshape(-1)
        m["skv"] = np.ascontiguousarray(skip[:, cs]).reshape(-1)
        for si, (hy, n) in enumerate(zip(hy_list, sizes)):
            u = np.zeros((3, HCH, B, n + 2), np.float32)
            for p in range(3):
                u[p, :, :, 1:n + 1] = hy[:, :, p * 256 + i * HCH:p * 256 + (i + 1) * HCH].transpose(2, 0, 1)
            m[f"u{si}"] = u
            feats, win = consts[si]
            m[f"feats{si}"] = feats
            m[f"win{si}"] = np.ascontiguousarray(np.tile(win[cs], (2, 1)))
            for k, v in tabs[si].items():
                m[f"t{si}_{k}"] = v
        in_maps.append(m)
    res = _run(nc, in_maps)
    outs = []
    for si, n in enumerate(sizes):
        y = np.zeros((B, n, 256), np.float32)
        for i in range(NCORES):
            y[:, :, i * HCH:(i + 1) * HCH] = res.results[i][f"o_yh{si}"].transpose(1, 2, 0)
        outs.append(y)
    return outs, [np.concatenate([res.results[i][f"taps{si}"] for i in range(NCORES)], 0) for si in range(len(sizes))]


def kernel(x, c, ctx, c_ctx, w_mod, b_mod, w_in, mlstm_conv_w, mlstm_conv_b, mlstm_gate_b,
           mlstm_norm_w, attn_q_norm_w, attn_k_norm_w, hyena_conv_w, hyena_conv_b,
           hyena_f_w1, hyena_f_b1, hyena_f_freq, hyena_f_w2, hyena_f_b2, hyena_f_w3,
           hyena_skip, w_out, ln_mix_w, ln_mix_b, router_w, router_b,
           exp_w_gate, exp_w_up, exp_w_down, ln_ffn_w, ln_ffn_b):
    f = lambda a: np.asarray(a, dtype=np.float32)
    x, c, ctx, c_ctx = f(x), f(c), f(ctx), f(c_ctx)
    mod = run_k0(c, c_ctx, f(w_mod), f(b_mod))
    depth = w_in.shape[0]
    for l in range(depth):
        last = l == depth - 1
        P_lat, P_ctx = run_k1(x, ctx, mod[l], f(w_in[l]))
        ym_l, ym_c = run_k2(P_lat, P_ctx, f(mlstm_conv_w[l]), f(mlstm_conv_b[l]), f(mlstm_gate_b[l]), f(mlstm_norm_w[l]))
        ya_l, ya_c = run_k3(P_lat, P_ctx, f(attn_q_norm_w[l]), f(attn_k_norm_w[l]))
        hy = [P_lat[..., 1808:]] + ([] if last else [P_ctx[..., 1808:]])
        fpar = (f(hyena_f_w1[l]), f(hyena_f_b1[l]), f(hyena_f_freq[l]), f(hyena_f_w2[l]), f(hyena_f_b2[l]), f(hyena_f_w3[l]))
        yhs, _ = run_k4(hy, f(hyena_conv_w[l]), f(hyena_conv_b[l]), fpar, f(hyena_skip[l]))
        yh_c = np.zeros_like(ym_c) if last else yhs[1]
        ycat_l = np.concatenate([ym_l, ya_l, yhs[0]], -1)
        ycat_c = np.concatenate([ym_c, ya_c, yh_c], -1)
        x1, c1, h2, h2c, aff, affc = run_k5(ycat_l, ycat_c, x, ctx, mod[l], f(w_out[l]), f(ln_mix_w[l]), f(ln_mix_b[l]),
                                            f(router_w[l]), f(router_b[l]))
        thr, thrc = run_k6(aff, affc)
        x, ctx = run_moe(h2, h2c, aff, affc, thr, thrc, x1, c1, mod[l], f(ln_ffn_w[l]), f(ln_ffn_b[l]),
                         f(exp_w_gate[l]), f(exp_w_up[l]), f(exp_w_down[l]))
    return x.astype(np.float32)


CAP_L, CAP_C = 1024, 32
SLOTS_B = CAP_L + CAP_C
NTOK_ALL = 2 * 8192 + 2 * 256


def build_k7g():
    nc = bass.Bass("TRN2", target_bir_lowering=False)
    T = 18
    TOK = T * 128
    h2d = nc.dram_tensor("h2all", [NTOK_ALL, 1024], BF16, kind="ExternalInput").ap()
    affLd = nc.dram_tensor("affL", [2, 2, 128, 66], F32, kind="ExternalInput").ap()
    thrd = nc.dram_tensor("thr8", [8], F32, kind="ExternalInput").ap()
    tidd = nc.dram_tensor("tid", [2, 128, 66], F32, kind="ExternalInput").ap()
    iotad = nc.dram_tensor("iota", [128, CAP_L + 128], F32, kind="ExternalInput").ap()
    cstd = nc.dram_tensor("cst", [128, 448], F32, kind="ExternalInput").ap()
    wgd = nc.dram_tensor("wg", [2, 128, 8, D_FF], F32, kind="ExternalInput").ap()
    wud = nc.dram_tensor("wu", [2, 128, 8, D_FF], F32, kind="ExternalInput").ap()
    wdd = nc.dram_tensor("wd", [2, 128, NFC, 1024], F32, kind="ExternalInput").ap()
    Yo = nc.dram_tensor("o_Y", [2, TOK, 1024], BF16, kind="ExternalOutput").ap()
    posKo = nc.dram_tensor("o_pos", [2, 2, 128, 66], I32, kind="ExternalOutput").ap()
    with ExitStack() as es:
        P = Prog(nc, es)
        sb, ps, ring = mk_alloc(nc, es)
        banks = [(ps(f"bk{i}", [128, 512], F32), f"bk{i}") for i in range(8)]
        cst = sb("cst_s", [128, 448]); P.dma("sync", cst[:], cstd, writes=["cst"])
        Ust, ones, identf, Ust64 = cst[:, 0:128], cst[:, 128:256], cst[:, 256:384], cst[0:64, 384:448]
        idb = sb("idb", [128, 128], BF16); P.copy("vector", idb[:], identf, reads=["cst"], writes=["idb"])
        thrt = sb("thrt", [128, 8]); P.dma("sync", thrt[:], bcast_rows(thrd, 128), writes=["thrt"])
        iota = sb("iota_s", [128, CAP_L + 128]); P.dma("sync", iota[:], iotad, writes=["iota"])
        h2T = sb("h2T_s", [128, 8, TOK], BF16)
        acc = sb("acc", [128, T, 1024])
        tv = sb("tv", [128, T])
        Ar = ring("A", 2, [128, 66]); Mr = ring("M", 2, [128, 66]); wir = ring("wi", 2, [128, 66]); pfr = ring("pf", 2, [128, 66])
        m2r = ring("m2", 2, [128, 66]); pir = ring("pi", 2, [128, 66], I32)
        tdfr = ring("tdf", 2, [128, 66]); TAr = ring("TA", 2, [128, 66, 2]); selr = ring("sel", 3, [128, 512]); rowt = sb("rowt", [2, SLOTS_B]); lfr = ring("lf", 2, [128, 18])
        tcr = ring("tc", 2, [64, 1]); tbr = ring("tb", 2, [64, 128])
        lsr = ring("ls", 2, [128, 9], I32)
        xsr = ring("xs", 2, [128, 1024], BF16)
        FG = 3
        wgr = ring("wgb", 2, [128, 8, FG * 128], BF16); wur = ring("wub", 2, [128, 8, FG * 128], BF16)
        wdr = ring("wdb", 2, [128, FG, 1024], BF16); stg = ring("stg", 3, [128, 1024])
        actr = ring("actT", 2, [128, FG, 512], BF16); sgr = ring("sg", 2, [128, 512]); yor = ring("yrow", 2, [128, 1024], BF16)
        pgr = Ring(banks[0:2]); pur = Ring(banks[2:4]); pyr = Ring(banks[4:6])
        tgs = [(s, min(512, TOK - s)) for s in range(0, TOK, 512)]
        for e in range(2):
            P.memset("gpsimd", h2T[:], 0.0, writes=["h2T"])
            P.memset("vector", tv[:], 0.0, writes=["tv"])
            for t in range(T):
                P.memset("gpsimd", acc[:, t, :], 0.0, writes=[f"acc{t}a", f"acc{t}b"])
            for b in range(2):
                A, Ak = Ar.next(); P.dma("sync", A[:], affLd[e, b], writes=[Ak])
                M, Mk = Mr.next()
                to = (e * 2 + b) * 2
                P.ts("vector", M[:, 0:64], A[:, 0:64], thrt[:, to:to + 1], None, ALU.is_ge, reads=[Ak, "thrt"], writes=[Mk + "l"])
                P.ts("vector", M[:, 64:66], A[:, 64:66], thrt[:, to + 1:to + 2], None, ALU.is_ge, reads=[Ak, "thrt"], writes=[Mk + "c"])
                wi, wik = wir.next(); pf, pfk = pfr.next(); m2, m2k = m2r.next(); pi, pik = pir.next()
                for (c0, ncol, cap, base, sfx) in ((0, 64, CAP_L, 0, "l"), (64, 2, CAP_C, CAP_L, "c")):
                    bw, bwk = banks[6]
                    P.mm(bw[:, c0:c0 + ncol], Ust, M[:, c0:c0 + ncol], True, True, reads=["cst", Mk + sfx], writes=[bwk + sfx])
                    P.copy("scalar", wi[:, c0:c0 + ncol], bw[:, c0:c0 + ncol], reads=[bwk + sfx], writes=[wik + sfx])
                    bt, btk = banks[7]
                    P.mm(bt[0:ncol, c0:c0 + 1], M[:, c0:c0 + ncol], ones[:, 0:1], True, True, reads=["cst", Mk + sfx], writes=[btk + "t" + sfx])
                    tc, tck = tcr.next()
                    P.copy("vector", tc[0:ncol, :], bt[0:ncol, c0:c0 + 1], reads=[btk + "t" + sfx], writes=[tck])
                    tb, tbk = tbr.next()
                    P.ts("vector", tb[0:ncol, :], ones[0:ncol, :], tc[0:ncol, 0:1], None, ALU.mult, reads=["cst", tck], writes=[tbk])
                    P.mm(bt[:, 128 + c0:128 + c0 + ncol], tb[0:ncol, :], Ust64[0:ncol, 0:ncol], True, True, reads=[tbk, "cst"], writes=[btk + "o" + sfx])
                    sl = slice(c0, c0 + ncol)
                    P.tt("vector", pf[:, sl], bt[:, 128 + c0:128 + c0 + ncol], wi[:, sl], ALU.add, reads=[btk + "o" + sfx, wik + sfx], writes=[pfk + sfx])
                    P.ts("vector", m2[:, sl], pf[:, sl], float(cap) - 0.5, None, ALU.is_lt, reads=[pfk + sfx], writes=[m2k + sfx])
                    P.tt("vector", m2[:, sl], m2[:, sl], M[:, sl], ALU.mult, reads=[m2k + sfx, Mk + sfx], writes=[m2k + sfx])
                    P.ts("vector", pf[:, sl], pf[:, sl], float(base - SLOTS_B), None, ALU.add, reads=[pfk + sfx], writes=[pfk + sfx])
                    P.tt("vector", pf[:, sl], pf[:, sl], m2[:, sl], ALU.mult, reads=[pfk + sfx, m2k + sfx], writes=[pfk + sfx])
                    P.ts("vector", pf[:, sl], pf[:, sl], float(SLOTS_B), None, ALU.add, reads=[pfk + sfx], writes=[pfk + sfx])
                P.copy("vector", pi[:], pf[:], reads=[pfk + "l", pfk + "c"], writes=[pik])
                P.dma("sync", posKo[e, b], pi[:], reads=[pik], writes=[pik + "d"], final=True)
                TA, TAk = TAr.next()
                tdf, tdfk = tdfr.next()
                P.dma("sync", tdf[:], tidd[b], writes=[tdfk])
                P.copy("gpsimd", TA[:, :, 0], tdf[:], reads=[tdfk], writes=[TAk + "t"])
                P.copy("gpsimd", TA[:, :, 1], A[:], reads=[Ak], writes=[TAk + "a"])
                pc, pck = banks[7]
                for piece, (s0_, ns_, js) in enumerate(((0, 512, range(64)), (512, 512, range(64)), (CAP_L, CAP_C, (64, 65)))):
                    sfx = "l" if piece < 2 else "c"
                    for jj, j in enumerate(js):
                        se, sek = selr.next()
                        P.ts("vector", se[:, 0:ns_], iota[:, s0_:s0_ + ns_], pf[:, j:j + 1], None, ALU.is_equal, reads=["iota", pfk + sfx], writes=[sek])
                        P.mm(pc[0:2, 0:ns_], TA[:, j, :], se[:, 0:ns_], jj == 0, jj == len(js) - 1,
                             reads=[sek, TAk + "t", TAk + "a"], writes=[pck + "row"])
                    P.copy("scalar", rowt[0:2, s0_:s0_ + ns_], pc[0:2, 0:ns_], reads=[pck + "row"], writes=[f"rowt{piece}"])
                pq, pqk = banks[6]
                for c in range(9):
                    ns_ = 128 if c < 8 else 32
                    P.tr(pq[0:ns_, 256 + 2 * c:256 + 2 * c + 2], rowt[0:2, c * 128:c * 128 + ns_], identf[0:2, 0:2],
                         reads=[f"rowt{c // 4}", "cst"], writes=[pqk + "q"])
                lf, lfk = lfr.next()
                P.copy("vector", lf[:], pq[:, 256:274], reads=[pqk + "q"], writes=[lfk])
                ls, lsk = lsr.next()
                P.copy("vector", ls[:], lf[:].rearrange("p (c t) -> p c t", t=2)[:, :, 0], reads=[lfk], writes=[lsk])
                for c in range(9):
                    npart = 128 if c < 8 else 32
                    p0 = 0
                    col0 = b * CAP_L + c * 128 if c < 8 else (16 + b) * 128
                    tcol = col0 // 128
                    P.copy("vector", tv[p0:p0 + npart, tcol:tcol + 1], lf[p0:p0 + npart, 2 * c + 1:2 * c + 2], reads=[lfk, "tv"], writes=["tv"])
                    xs, xsk = xsr.next()
                    P.op("gpsimd", lambda en, xs=xs, ls=ls, c=c, npart=npart, p0=p0: en.indirect_dma_start(
                        out=xs[p0:p0 + npart, :], out_offset=None, in_=h2d[:, :],
                        in_offset=bass.IndirectOffsetOnAxis(ap=ls[p0:p0 + npart, c:c + 1], axis=0)), reads=[lsk], writes=[xsk], dma=True)
                    pb, pbk = banks[6]
                    pT = pb[:, :].bitcast(BF16)
                    for kc in range(8):
                        P.tr(pT[:, kc * 128:kc * 128 + npart], xs[p0:p0 + npart, kc * 128:(kc + 1) * 128], idb[p0:p0 + npart, p0:p0 + npart],
                             reads=[xsk, "idb"], writes=[pbk + "l", pbk + "c"])
                    P.copy("scalar", h2T[:, :, col0:col0 + npart], pT[:, 0:1024].rearrange("p (k s) -> p k s", k=8)[:, :, 0:npart],
                           reads=[pbk + "l", pbk + "c"], writes=["h2T"])
            ci = 0
            for f0 in range(0, NFC, FG):
                nf = min(FG, NFC - f0)
                wgb, wgk = wgr.next(); wub, wuk = wur.next(); wdb, wdk = wdr.next()
                for (src, dst, dk) in ((wgd, wgb, wgk), (wud, wub, wuk)):
                    for kp in range(0, 8, 2):
                        st, sk = stg.next()
                        sv = st[:, 0:2 * nf * 128].rearrange("p (a f) -> p a f", a=2)
                        P.dma("sync", sv, src[e, :, kp:kp + 2, f0 * 128:(f0 + nf) * 128], writes=[sk])
                        P.copy("gpsimd" if ci % 2 else "vector", dst[:, kp:kp + 2, 0:nf * 128], sv, reads=[sk], writes=[dk + f"k{kp}"])
                        ci += 1
                for fc in range(nf):
                    st, sk = stg.next()
                    P.dma("sync", st[:], wdd[e, :, f0 + fc, :], writes=[sk])
                    P.copy("gpsimd" if ci % 2 else "vector", wdb[:, fc, :], st[:], reads=[sk], writes=[wdk + f"f{fc}"])
                    ci += 1
                for (s0, ns) in tgs:
                    actT, ak = actr.next()
                    for fc in range(nf):
                        pg, pgk = pgr.next(); pu, puk = pur.next()
                        for kc in range(8):
                            P.mm(pg[:, 0:ns], wgb[:, kc, fc * 128:(fc + 1) * 128], h2T[:, kc, s0:s0 + ns], kc == 0, kc == 7,
                                 reads=["h2T", wgk + f"k{kc - kc % 2}"], writes=[pgk])
                        for kc in range(8):
                            P.mm(pu[:, 0:ns], wub[:, kc, fc * 128:(fc + 1) * 128], h2T[:, kc, s0:s0 + ns], kc == 0, kc == 7,
                                 reads=["h2T", wuk + f"k{kc - kc % 2}"], writes=[puk])
                        sg, sgk = sgr.next()
                        P.act(sg[:, 0:ns], pg[:, 0:ns], AF.Silu, reads=[pgk], writes=[sgk])
                        P.tt("vector", actT[:, fc, 0:ns], pu[:, 0:ns], sg[:, 0:ns], ALU.mult, reads=[puk, sgk], writes=[ak + f"f{fc}"])
                    for tt in range(s0 // 128, (s0 + ns) // 128):
                        for hf in range(2):
                            py, pyk = pyr.next()
                            for fc in range(nf):
                                P.mm(py[:], actT[:, fc, tt * 128 - s0:(tt + 1) * 128 - s0], wdb[:, fc, hf * 512:(hf + 1) * 512],
                                     fc == 0, fc == nf - 1, reads=[ak + f"f{fc}", wdk + f"f{fc}"], writes=[pyk])
                            ah = f"acc{tt}" + "ab"[hf]
                            P.tt("vector", acc[:, tt, hf * 512:(hf + 1) * 512], py[:], acc[:, tt, hf * 512:(hf + 1) * 512], ALU.add,
                                 reads=[pyk, ah], writes=[ah])
            for t in range(T):
                yr_, yrk = yor.next()
                P.act(yr_[:], acc[:, t, :], AF.Copy, reads=[f"acc{t}a", f"acc{t}b", "tv"], writes=[yrk], scale=tv[:, t:t + 1])
                P.dma("sync", Yo[e, t * 128:(t + 1) * 128, :], yr_[:], reads=[yrk], writes=[yrk], final=True)
        P.emit()
    return nc


def build_k8():
    nc = bass.Bass("TRN2", target_bir_lowering=False)
    T = K1_TILES
    Yb = [nc.dram_tensor(f"Yb{e}", [SLOTS_B + 1, 1024], BF16, kind="ExternalInput").ap() for e in range(N_EXP)]
    idxd = nc.dram_tensor("idx", [T, 128, N_EXP], I32, kind="ExternalInput").ap()
    x1d = nc.dram_tensor("i_x1", [T, 128, 1024], F32, kind="ExternalInput").ap()
    rows = nc.dram_tensor("rows", [2, 1024], F32, kind="ExternalInput").ap()
    lnr = nc.dram_tensor("lnr", [2, 1024], F32, kind="ExternalInput").ap()
    x2o = nc.dram_tensor("o_x2", [T, 128, 1024], F32, kind="ExternalOutput").ap()
    with ExitStack() as es:
        P = Prog(nc, es)
        sb, ps, ring = mk_alloc(nc, es)
        rowt = sb("rowt", [128, 2, 1024]); lnt = sb("lnt", [128, 2, 1024])
        for j in range(2):
            P.dma("sync", rowt[:, j, :], bcast_rows(rows[j], 128), writes=[f"row{j}"])
            P.dma("sync", lnt[:, j, :], bcast_rows(lnr[j], 128), writes=[f"ln{j}"])
        idr = ring("idx", 2, [128, N_EXP], I32)
        gr = ring("g", 8, [128, 1024], BF16)
        accr = ring("acc", 2, [128, 1024]); acc2r = ring("accb", 2, [128, 1024])
        xr = ring("x", 2, [128, 1024])
        str_ = ring("st", 2, [128, 2, 6]); mvr = ring("mv", 2, [128, 2]); rsr = ring("rs", 2, [128, 1])
        for t in range(T):
            g_ = 0 if t < 16 else 1
            ix, ixk = idr.next(); P.dma("sync", ix[:], idxd[t], writes=[ixk])
            acc, ack = accr.next(); acc2, ac2k = acc2r.next()
            for e in range(N_EXP):
                gt, gk = gr.next()
                P.op("gpsimd", lambda en, gt=gt, ix=ix, e=e: en.indirect_dma_start(
                    out=gt[:, :], out_offset=None, in_=Yb[e][:, :], in_offset=bass.IndirectOffsetOnAxis(ap=ix[:, e:e + 1], axis=0)),
                    reads=[ixk], writes=[gk], dma=True)
                if e == 0:
                    P.copy("vector", acc[:], gt[:], reads=[gk], writes=[ack])
                else:
                    P.tt("vector", acc[:], acc[:], gt[:], ALU.add, reads=[ack, gk], writes=[ack])
            x, xk = xr.next(); P.dma("sync", x[:], x1d[t], writes=[xk])
            P.tt("gpsimd", acc[:], acc[:], rowt[:, g_, :], ALU.mult, reads=[ack, f"row{g_}"], writes=[ack])
            P.stt("vector", acc[:], x[:], ALPHA, acc[:], ALU.mult, ALU.add, reads=[xk, ack], writes=[ack])
            st, _ = str_.next(); mv, _ = mvr.next(); rs, _ = rsr.next()
            emit_layernorm(P, acc[:], ack, x[:], xk, st, mv, rs, f"lnC{t % 2}")
            P.tt("gpsimd", acc[:], x[:], lnt[:, 0, :], ALU.mult, reads=[xk, "ln0"], writes=[ack])
            P.tt("gpsimd", x[:], acc[:], lnt[:, 1, :], ALU.add, reads=[ack, "ln1"], writes=[xk])
            P.dma("sync", x2o[t], x[:], reads=[xk], writes=[xk], final=True)
        P.emit()
    return nc


def run_moe(h2, h2c, aff, affc, thr, thrc, x1, c1, mod_l, ln_w, ln_b, wg, wu, wd):
    B, n, D = x1.shape
    nctx = c1.shape[1]
    E = aff.shape[-1]
    h2all = np.ascontiguousarray(np.concatenate([h2.reshape(B * n, D), h2c.reshape(B * nctx, D)], 0))
    affall = np.concatenate([aff.reshape(B * n, E), affc.reshape(B * nctx, E)], 0)
    s_idx, j_idx = np.meshgrid(np.arange(128), np.arange(128), indexing="ij")
    cst = np.zeros((128, 448), np.float32)
    cst[:, 0:128] = (s_idx < j_idx); cst[:, 128:256] = 1.0; cst[:, 256:384] = np.eye(128); cst[0:64, 384:448] = (s_idx < j_idx)[:64, :64]
    tid = np.zeros((B, 128, 66), np.float32)
    iota = np.full((128, CAP_L + 128), -1.0, np.float32)
    iota[:, 0:CAP_L] = np.arange(CAP_L)
    iota[:, CAP_L:CAP_L + 32] = CAP_L + np.arange(32)
    for b in range(B):
        tid[b, :, 0:64] = (b * n + np.arange(n)).reshape(64, 128).T
        tid[b, :, 64:66] = (B * n + b * nctx + np.arange(nctx)).reshape(2, 128).T
    nc = build_k7g()
    in_maps = []
    for i in range(NCORES):
        es = [2 * i, 2 * i + 1]
        affL = np.zeros((2, B, 128, 66), np.float32)
        thr8 = np.zeros((2, B, 2), np.float32)
        for el, e in enumerate(es):
            for b in range(B):
                affL[el, b, :, 0:64] = aff[b, :, e].reshape(64, 128).T
                affL[el, b, :, 64:66] = affc[b, :, e].reshape(2, 128).T
                thr8[el, b] = (thr[b, e], thrc[b, e])
        lw = lambda w, kch: np.ascontiguousarray(w.reshape(2, kch, 128, w.shape[-1]).transpose(0, 2, 1, 3))
        in_maps.append({"h2all": h2all, "affL": affL,
                        "thr8": thr8.reshape(-1), "tid": tid, "iota": iota, "cst": cst,
                        "wg": lw(wg[es], 8), "wu": lw(wu[es], 8), "wd": lw(wd[es], NFC)})
    res = _run(nc, in_maps)
    Yb = np.zeros((B, E, SLOTS_B + 1, D), h2all.dtype)
    posL = np.zeros((B, n, E), np.int32); posC = np.zeros((B, nctx, E), np.int32)
    for i in range(NCORES):
        Y = res.results[i]["o_Y"]; pos = res.results[i]["o_pos"]
        for el in range(2):
            e = 2 * i + el
            for b in range(B):
                Yb[b, e, 0:CAP_L] = Y[el, b * CAP_L:(b + 1) * CAP_L]
                Yb[b, e, CAP_L:SLOTS_B] = Y[el, (16 + b) * 128:(16 + b) * 128 + CAP_C]
                posL[b, :, e] = pos[el, b, :, 0:64].T.reshape(n)
                posC[b, :, e] = pos[el, b, :, 64:66].T.reshape(nctx)
    nc8 = build_k8()
    lnr = np.ascontiguousarray(np.stack([ln_w, ln_b]))
    in_maps = []
    for i in range(NCORES):
        b = i // 4
        rows = np.ascontiguousarray(np.stack([mod_l[b, 5120:6144], mod_l[2, 5120:6144]]))
        idx = tok_shard(posL, posC, i)
        idx[-1, 64:, :] = SLOTS_B
        m = {"idx": idx, "i_x1": tok_shard(x1, c1, i), "rows": rows, "lnr": lnr}
        for e in range(E):
            m[f"Yb{e}"] = np.ascontiguousarray(Yb[b, e])
        in_maps.append(m)
    res = _run(nc8, in_maps)
    return tok_unshard([r["o_x2"] for r in res.results], B, n, nctx)
```

```python
import numpy as np
from contextlib import ExitStack
import concourse.bass as bass
import concourse.mybir as mybir
from concourse.bass_utils import run_bass_kernel_spmd

F32 = mybir.dt.float32
BF16 = mybir.dt.bfloat16
I32 = mybir.dt.int32
AF = mybir.ActivationFunctionType
ALU = mybir.AluOpType
AX = mybir.AxisListType

NCORES = 8
STAGE_EXP = False
FUSE_WAIT = True
NO_RAW_SELF = False
SELF_SYNC = True


class Prog:
    ENGS = ("sync", "scalar", "vector", "gpsimd", "tensor")

    def __init__(self, nc, es, n_dma_sems=12):
        self.nc, self.es = nc, es
        self.ops = {e: [] for e in self.ENGS}
        self.esem = {}
        self.ecount = {}
        for e in ("scalar", "vector", "gpsimd", "tensor"):
            self.esem[e] = es.enter_context(nc.semaphore(f"sem_{e}"))
            self.ecount[e] = 0
        self.dpool = {}
        for q in ("sync", "scalar", "gpsimd"):
            self.dpool[q] = dict(
                sems=[es.enter_context(nc.semaphore(f"dsem_{q}_{i}")) for i in range(n_dma_sems)],
                cnt=[0] * n_dma_sems, nxt=0, know=[None] * n_dma_sems)
        self.semobj = {}
        self.lastw = {}
        self.readers = {}
        self.know = {e: {} for e in self.ENGS}
        self.final_tokens = []

    def _need(self, eng, tok, waits):
        sk, v, kn = tok
        if self.know[eng].get(sk, 0) >= v:
            return
        waits.append((sk, v))
        k = self.know[eng]
        for a, b in kn.items():
            if k.get(a, 0) < b:
                k[a] = b
        if k.get(sk, 0) < v:
            k[sk] = v

    def op(self, eng, fn, reads=(), writes=(), dma=False, final=False):
        waits = []
        toks = []
        own = None if dma else ("e", eng)
        for key in reads:
            t = self.lastw.get(key)
            if t is not None:
                toks.append((t, True))
        for key in writes:
            t = self.lastw.get(key)
            if t is not None:
                toks.append((t, False))
            toks.extend((r, False) for r in self.readers.get(key, ()))
        for t, raw in toks:
            if t[0] == own and (eng == "tensor" or not SELF_SYNC or (not raw and eng != "gpsimd") or (NO_RAW_SELF and eng in ("vector", "scalar"))):
                continue
            self._need(eng, t, waits)
        if dma:
            pool = self.dpool[eng]
            j = pool["nxt"]
            pool["nxt"] = (j + 1) % len(pool["sems"])
            if pool["cnt"][j] > 0:
                self._need(eng, (("d", eng, j), pool["cnt"][j], pool["know"][j]), waits)
            pool["cnt"][j] += 16
            sk = ("d", eng, j)
            self.semobj[sk] = pool["sems"][j]
            kn = dict(self.know[eng])
            pool["know"][j] = kn
            tok = (sk, pool["cnt"][j], kn)
            inc = (pool["sems"][j], 16)
        else:
            self.ecount[eng] += 1
            sk = ("e", eng)
            self.semobj[sk] = self.esem[eng]
            tok = (sk, self.ecount[eng], dict(self.know[eng]))
            inc = (self.esem[eng], 1)
        self.ops[eng].append((waits, fn, inc))
        for key in reads:
            self.readers.setdefault(key, []).append(tok)
        for key in writes:
            self.lastw[key] = tok
            self.readers[key] = []
        if final:
            self.final_tokens.append(tok)
        return tok

    def emit(self):
        waits = []
        for t in self.final_tokens:
            self._need("sync", t, waits)
        if waits:
            self.ops["sync"].append((waits, None, None))
        nc = self.nc
        with nc.Block() as block:
            def run(eng_name):
                def body(eng):
                    for waits, fn, inc in self.ops[eng_name]:
                        fused = FUSE_WAIT and fn is not None and len(waits) > 0
                        for sk, v in (waits[:-1] if fused else waits):
                            eng.wait_ge(self.semobj[sk], v)
                        if fn is not None:
                            ins = fn(eng)
                            if fused:
                                ins._wait_ge(self.semobj[waits[-1][0]], waits[-1][1])
                            ins.then_inc(inc[0], inc[1])
                return body
            block.sync(run("sync"))
            block.scalar(run("scalar"))
            block.vector(run("vector"))
            block.gpsimd(run("gpsimd"))
            block.tensor(run("tensor"))

    def dma(self, q, out, in_, reads=(), writes=(), final=False, **kw):
        return self.op(q, lambda e: e.dma_start(out=out, in_=in_, **kw), reads, writes, dma=True, final=final)

    def mm(self, out, lhsT, rhs, start, stop, reads=(), writes=()):
        return self.op("tensor", lambda e: e.matmul(out, lhsT, rhs, start=start, stop=stop), reads, writes)

    def act(self, out, in_, func, reads=(), writes=(), eng="scalar", **kw):
        return self.op(eng, lambda e: e.activation(out=out, in_=in_, func=func, **kw), reads, writes)

    def tt(self, eng, out, in0, in1, op, reads=(), writes=()):
        return self.op(eng, lambda e: e.tensor_tensor(out=out, in0=in0, in1=in1, op=op), reads, writes)

    def ts(self, eng, out, in0, s1, s2, op0, op1=None, reads=(), writes=(), accum_out=None):
        kw = {}
        if op1 is not None:
            kw["op1"] = op1
        if accum_out is not None:
            kw["accum_out"] = accum_out
        return self.op(eng, lambda e: e.tensor_scalar(out=out, in0=in0, scalar1=s1, scalar2=s2, op0=op0, **kw),
                       reads, writes)

    def stt(self, eng, out, in0, scalar, in1, op0, op1, reads=(), writes=()):
        return self.op(eng, lambda e: e.scalar_tensor_tensor(out=out, in0=in0, scalar=scalar, in1=in1,
                                                             op0=op0, op1=op1), reads, writes)

    def copy(self, eng, out, in_, reads=(), writes=()):
        if eng == "scalar":
            return self.op(eng, lambda e: e.activation(out=out, in_=in_, func=AF.Copy), reads, writes)
        return self.op(eng, lambda e: e.tensor_copy(out=out, in_=in_), reads, writes)

    def tr(self, out, in_, ident, reads=(), writes=()):
        return self.op("tensor", lambda e: e.transpose(out, in_, ident), reads, writes)

    def memset(self, eng, ap, val, writes=()):
        return self.op(eng, lambda e: e.memset(ap, val), (), writes)

    def gen(self, eng, f, reads=(), writes=()):
        return self.op(eng, f, reads, writes)


def _run(nc, in_maps):
    return run_bass_kernel_spmd(nc, in_maps, core_ids=list(range(NCORES)))


D_MODEL = 1024
DEPTH = 2
N_MOD = 6
MODC = N_MOD * D_MODEL // NCORES


def build_k0():
    nc = bass.Bass("TRN2", target_bir_lowering=False)
    cvT = nc.dram_tensor("cvT", [128, 8, 3], F32, kind="ExternalInput").ap()
    wm = nc.dram_tensor("wm", [DEPTH, 128, 8, MODC], F32, kind="ExternalInput").ap()
    bm = nc.dram_tensor("bm", [DEPTH, 3, MODC], F32, kind="ExternalInput").ap()
    out = nc.dram_tensor("mod", [DEPTH, 3, MODC], F32, kind="ExternalOutput").ap()
    with ExitStack() as es:
        P = Prog(nc, es)
        sb = lambda name, shape, dt=F32: es.enter_context(nc.sbuf_tensor(name, shape, dt))
        cv = sb("cv", [128, 8, 3])
        cs = sb("cs", [128, 8, 3])
        w = [sb(f"w{l}", [128, 8, MODC]) for l in range(DEPTH)]
        b = sb("b", [3, DEPTH, MODC])
        o = sb("o", [3, DEPTH, MODC])
        ps = [es.enter_context(nc.psum_tensor(f"ps{i}", [128, 512], F32)) for i in range(2)]
        P.dma("sync", cv[:], cvT, writes=["cv"])
        for l in range(DEPTH):
            P.dma("sync" if l == 0 else "gpsimd", w[l][:], wm[l], writes=[f"w{l}"])
            P.dma("sync", b[:, l, :], bm[l], writes=[f"b{l}"])
        P.act(cs[:], cv[:], AF.Silu, reads=["cv"], writes=["cs"])
        H = MODC // 2
        for l in range(DEPTH):
            for h in range(2):
                pt = ps[h]
                for kc in range(8):
                    P.mm(pt[0:3, 0:H], cs[:, kc, :], w[l][:, kc, h * H:(h + 1) * H], kc == 0, kc == 7,
                         reads=["cs", f"w{l}"], writes=[f"ps{h}"])
                P.op("vector", lambda e, l=l, h=h, pt=pt: e.tensor_tensor(
                    out=o[:, l, h * H:(h + 1) * H], in0=pt[0:3, 0:H], in1=b[:, l, h * H:(h + 1) * H], op=ALU.add),
                    reads=[f"ps{h}", f"b{l}"], writes=[f"o{l}{h}"])
            P.dma("sync", out[l], o[:, l, :], reads=[f"o{l}0", f"o{l}1"], final=True)
        P.emit()
    return nc


def run_k0(c, c_ctx, w_mod, b_mod):
    cv = np.concatenate([c, c_ctx[None]], 0)
    cvT = np.ascontiguousarray(cv.T.reshape(8, 128, 3).transpose(1, 0, 2))
    nc = build_k0()
    in_maps = []
    for i in range(NCORES):
        sl = slice(i * MODC, (i + 1) * MODC)
        wm = np.ascontiguousarray(w_mod[:, :, sl].reshape(DEPTH, 8, 128, MODC).transpose(0, 2, 1, 3))
        bm = np.ascontiguousarray(np.broadcast_to(b_mod[:, None, sl], (DEPTH, 3, MODC)))
        in_maps.append({"cvT": cvT, "wm": wm, "bm": bm})
    res = _run(nc, in_maps)
    return np.concatenate([r["mod"] for r in res.results], axis=-1)


class Ring:
    def __init__(self, items):
        self.items, self.i = items, 0

    def next(self):
        it = self.items[self.i % len(self.items)]
        self.i += 1
        return it


def mk_alloc(nc, es):
    def sb(name, shape, dt=F32):
        return es.enter_context(nc.sbuf_tensor(name, shape, dt))

    def ps(name, shape, dt=F32):
        return es.enter_context(nc.psum_tensor(name, shape, dt))

    def ring(name, n, shape, dt=F32, psum=False):
        return Ring([((ps if psum else sb)(f"{name}{i}", shape, dt), f"{name}{i}") for i in range(n)])
    return sb, ps, ring


def bcast_rows(ap1d, nparts):
    return bass.AP(tensor=ap1d.tensor, offset=ap1d.offset, ap=[[0, nparts]] + [list(x) for x in ap1d.ap])


LN_EPS = 1e-5


def emit_layernorm(P, x, xkey, xn, xnkey, st, mv, rs, skey, n=1024):
    for j in range(n // 512):
        P.gen("vector", lambda e, j=j: e.bn_stats(out=st[:, j, :], in_=x[:, j * 512:(j + 1) * 512]),
              reads=[xkey], writes=[skey + f"st{j}"])
    P.gen("vector", lambda e: e.bn_aggr(out=mv[:], in_=st[:]),
          reads=[skey + f"st{j}" for j in range(n // 512)], writes=[skey + "mv"])
    P.ts("vector", rs[:], mv[:, 1:2], LN_EPS, None, ALU.add, reads=[skey + "mv"], writes=[skey + "rs"])
    P.act(rs[:], rs[:], AF.Sqrt, reads=[skey + "rs"], writes=[skey + "rs"])
    P.gen("vector", lambda e: e.reciprocal(out=rs[:], in_=rs[:]), reads=[skey + "rs"], writes=[skey + "rs"])
    P.ts("vector", xn, x, mv[:, 0:1], rs[:, 0:1], ALU.subtract, ALU.mult,
         reads=[xkey, skey + "mv", skey + "rs"], writes=[xnkey])


N_IN = 2576
K1_TILES = 17


def build_k1():
    nc = bass.Bass("TRN2", target_bir_lowering=False)
    xt = nc.dram_tensor("xt", [K1_TILES, 128, 1024], F32, kind="ExternalInput").ap()
    modr = nc.dram_tensor("modr", [2, 2, 1024], F32, kind="ExternalInput").ap()
    win = nc.dram_tensor("win", [128, 8, N_IN], F32, kind="ExternalInput").ap()
    identd = nc.dram_tensor("ident", [128, 128], F32, kind="ExternalInput").ap()
    out = nc.dram_tensor("p", [K1_TILES, 128, N_IN], F32, kind="ExternalOutput").ap()
    with ExitStack() as es:
        P = Prog(nc, es)
        sb, ps, ring = mk_alloc(nc, es)
        idf = sb("idf", [128, 128])
        idb = sb("idb", [128, 128], BF16)
        P.dma("sync", idf[:], identd, writes=["idf"])
        P.copy("vector", idb[:], idf[:], reads=["idf"], writes=["idb"])
        modt = sb("modt", [128, 2, 2, 1024])
        for g in range(2):
            for j in range(2):
                P.dma("sync", modt[:, g, j, :], bcast_rows(modr[g, j], 128), writes=[f"mod{g}{j}"])
            P.ts("vector", modt[:, g, 0, :], modt[:, g, 0, :], 1.0, None, ALU.add,
                 reads=[f"mod{g}0"], writes=[f"mod{g}0"])
        wbf = sb("wbf", [128, 8, N_IN], BF16)
        wst = ring("wst", 2, [128, N_IN])
        for kc in range(8):
            t, k = wst.next()
            P.dma("gpsimd" if kc % 2 else "sync", t[:], win[:, kc, :], writes=[k])
            P.copy("gpsimd" if kc % 2 else "vector", wbf[:, kc, :], t[:], reads=[k], writes=[f"wbf{kc}"])
        wkeys = [f"wbf{kc}" for kc in range(8)]
        xr = ring("x", 2, [128, 1024])
        xnr = ring("xn", 2, [128, 1024])
        h1r = ring("h1", 2, [128, 1024])
        hr = ring("h", 2, [128, 1024], BF16)
        hTr = ring("hT", 2, [128, 1024], BF16)
        orr = ring("o", 2, [128, N_IN])
        str_ = ring("st", 2, [128, 2, 6])
        mvr = ring("mv", 2, [128, 2])
        rsr = ring("rs", 2, [128, 1])
        pTr = ring("pT", 2, [128, 1024], BF16, psum=True)
        pmr = ring("pm", 4, [128, 512], F32, psum=True)
        ev = 0
        for t in range(K1_TILES):
            g = 0 if t < 16 else 1
            x, xk = xr.next()
            P.dma("sync", x[:], xt[t], writes=[xk])
            xn, xnk = xnr.next()
            st, _ = str_.next(); mv, _ = mvr.next(); rs, _ = rsr.next()
            emit_layernorm(P, x[:], xk, xn[:], xnk, st, mv, rs, f"ln{t % 2}")
            h1, h1k = h1r.next()
            P.tt("gpsimd", h1[:], xn[:], modt[:, g, 0, :], ALU.mult, reads=[xnk, f"mod{g}0"], writes=[h1k])
            h, hk = hr.next()
            P.tt("gpsimd", h[:], h1[:], modt[:, g, 1, :], ALU.add, reads=[h1k, f"mod{g}1"], writes=[hk])
            pT, pTk = pTr.next()
            for kc in range(8):
                P.tr(pT[:, kc * 128:(kc + 1) * 128], h[:, kc * 128:(kc + 1) * 128], idb[:],
                     reads=[hk, "idb"], writes=[pTk])
            hT, hTk = hTr.next()
            P.copy("scalar", hT[:], pT[:], reads=[pTk], writes=[hTk])
            o, ok = orr.next()
            for cg in range(6):
                c0 = cg * 512
                n = min(512, N_IN - c0)
                pm, pmk = pmr.next()
                for kc in range(8):
                    P.mm(pm[:, 0:n], hT[:, kc * 128:(kc + 1) * 128], wbf[:, kc, c0:c0 + n], kc == 0, kc == 7,
                         reads=[hTk, wkeys[kc]], writes=[pmk])
                P.copy("scalar" if ev % 2 else "vector", o[:, c0:c0 + n], pm[:, 0:n], reads=[pmk], writes=[ok + f"c{cg}"])
                ev += 1
            P.dma("gpsimd", out[t], o[:], reads=[ok + f"c{cg}" for cg in range(6)], writes=[ok + "dma"], final=True)
        P.emit()
    return nc


def lay_w(w, kchunks):
    return np.ascontiguousarray(w.reshape(kchunks, 128, w.shape[1]).transpose(1, 0, 2))


def run_k1(x, ctx, mod_l, w_in_l):
    nc = build_k1()
    B, n, D = x.shape
    seg = n // 4
    ctxf = ctx.reshape(-1, D)
    ident = np.eye(128, dtype=np.float32)
    win = lay_w(w_in_l, 8)
    in_maps = []
    for i in range(NCORES):
        b, s = i // 4, i % 4
        xt = np.zeros((K1_TILES * 128, D), np.float32)
        xt[:seg] = x[b, s * seg:(s + 1) * seg]
        xt[seg:seg + 64] = ctxf[i * 64:(i + 1) * 64]
        modr = np.stack([np.stack([mod_l[b, 1024:2048], mod_l[b, 0:1024]]),
                         np.stack([mod_l[2, 1024:2048], mod_l[2, 0:1024]])])
        in_maps.append({"xt": xt.reshape(K1_TILES, 128, D), "modr": np.ascontiguousarray(modr), "win": win,
                        "ident": ident})
    res = _run(nc, in_maps)
    P_lat = np.zeros((B, n, N_IN), np.float32)
    P_ctx = np.zeros((B * ctx.shape[1], N_IN), np.float32)
    for i in range(NCORES):
        b, s = i // 4, i % 4
        p = res.results[i]["p"].reshape(K1_TILES * 128, N_IN)
        P_lat[b, s * seg:(s + 1) * seg] = p[:seg]
        P_ctx[i * 64:(i + 1) * 64] = p[seg:seg + 64]
    return P_lat, P_ctx.reshape(B, ctx.shape[1], N_IN)


ALPHA = (2.0 * DEPTH) ** 0.25
N_EXP = 16


def build_k5():
    nc = bass.Bass("TRN2", target_bir_lowering=False)
    T = K1_TILES
    yt = nc.dram_tensor("yt", [T, 128, 1024], F32, kind="ExternalInput").ap()
    xt = nc.dram_tensor("xt", [T, 128, 1024], F32, kind="ExternalInput").ap()
    wout = nc.dram_tensor("wout", [128, 8, 1024], F32, kind="ExternalInput").ap()
    rows = nc.dram_tensor("rows", [2, 3, 1024], F32, kind="ExternalInput").ap()
    lnr = nc.dram_tensor("lnr", [2, 1024], F32, kind="ExternalInput").ap()
    rwd = nc.dram_tensor("rw", [128, 8, N_EXP], F32, kind="ExternalInput").ap()
    rbd = nc.dram_tensor("rb", [N_EXP], F32, kind="ExternalInput").ap()
    identd = nc.dram_tensor("ident", [128, 128], F32, kind="ExternalInput").ap()
    x1o = nc.dram_tensor("o_x1", [T, 128, 1024], F32, kind="ExternalOutput").ap()
    h2o = nc.dram_tensor("o_h2", [T, 128, 1024], BF16, kind="ExternalOutput").ap()
    affo = nc.dram_tensor("o_aff", [T, 128, N_EXP], F32, kind="ExternalOutput").ap()
    with ExitStack() as es:
        P = Prog(nc, es)
        sb, ps, ring = mk_alloc(nc, es)
        idf = sb("idf", [128, 128])
        idb = sb("idb", [128, 128], BF16)
        P.dma("sync", idf[:], identd, writes=["idf"])
        P.copy("vector", idb[:], idf[:], reads=["idf"], writes=["idb"])
        rowt = sb("rowt", [128, 2, 3, 1024])
        for g in range(2):
            for j in range(3):
                P.dma("sync", rowt[:, g, j, :], bcast_rows(rows[g, j], 128), writes=[f"row{g}{j}"])
            P.ts("vector", rowt[:, g, 1, :], rowt[:, g, 1, :], 1.0, None, ALU.add,
                 reads=[f"row{g}1"], writes=[f"row{g}1"])
        lnt = sb("lnt", [128, 2, 1024])
        for j in range(2):
            P.dma("sync", lnt[:, j, :], bcast_rows(lnr[j], 128), writes=[f"ln{j}"])
        rw = sb("rwt", [128, 8, N_EXP])
        P.dma("sync", rw[:], rwd, writes=["rw"])
        rb = sb("rbt", [128, N_EXP])
        P.dma("sync", rb[:], bcast_rows(rbd, 128), writes=["rb"])
        wbf = sb("wbf", [128, 8, 1024], BF16)
        wst = ring("wst", 2, [128, 1024])
        for kc in range(8):
            t, k = wst.next()
            P.dma("gpsimd" if kc % 2 else "sync", t[:], wout[:, kc, :], writes=[k])
            P.copy("gpsimd" if kc % 2 else "vector", wbf[:, kc, :], t[:], reads=[k], writes=[f"wbf{kc}"])
        wkeys = [f"wbf{kc}" for kc in range(8)]
        yr = ring("y", 2, [128, 1024]); ybr = ring("yb", 2, [128, 1024], BF16)
        yTr = ring("yT", 2, [128, 1024], BF16)
        xr = ring("x", 2, [128, 1024]); tmpr = ring("tmp", 2, [128, 1024]); rr = ring("r", 2, [128, 1024])
        xnr = ring("xn", 2, [128, 1024]); x1r = ring("x1_", 2, [128, 1024]); x1ar = ring("x1a", 2, [128, 1024])
        xn2r = ring("xn2", 2, [128, 1024]); h2fr = ring("h2f", 2, [128, 1024]); h2ar = ring("h2a", 2, [128, 1024])
        h2br = ring("h2b", 2, [128, 1024], BF16)
        h2Tr = ring("h2T", 2, [128, 1024])
        str_ = ring("st", 4, [128, 2, 6]); mvr = ring("mv", 4, [128, 2]); rsr = ring("rs", 4, [128, 1])
        lgr = ring("lg", 2, [128, N_EXP]); exr = ring("ex", 2, [128, N_EXP]); afr = ring("af", 2, [128, N_EXP])
        smr = ring("sm", 2, [128, 4])
        pTr = ring("pT", 1, [128, 1024], BF16, psum=True)
        pmr = ring("pm", 2, [128, 512], F32, psum=True)
        pTfr = ring("pTf", 1, [128, 1024], F32, psum=True)
        plr = ring("pl", 1, [128, N_EXP], F32, psum=True)
        lnc = 0
        for t in range(T):
            g = 0 if t < 16 else 1
            y, yk = yr.next(); P.dma("sync", y[:], yt[t], writes=[yk])
            x, xk = xr.next(); P.dma("sync", x[:], xt[t], writes=[xk])
            yb, ybk = ybr.next(); P.copy("gpsimd", yb[:], y[:], reads=[yk], writes=[ybk])
            pT, pTk = pTr.next()
            for kc in range(8):
                P.tr(pT[:, kc * 128:(kc + 1) * 128], yb[:, kc * 128:(kc + 1) * 128], idb[:], reads=[ybk, "idb"], writes=[pTk])
            yT, yTk = yTr.next(); P.copy("scalar", yT[:], pT[:], reads=[pTk], writes=[yTk])
            tmp, tmpk = tmpr.next()
            for hf in range(2):
                pm, pmk = pmr.next()
                for kc in range(8):
                    P.mm(pm[:], yT[:, kc * 128:(kc + 1) * 128], wbf[:, kc, hf * 512:(hf + 1) * 512], kc == 0, kc == 7,
                         reads=[yTk, wkeys[kc]], writes=[pmk])
                P.tt("vector", tmp[:, hf * 512:(hf + 1) * 512], pm[:], rowt[:, g, 0, hf * 512:(hf + 1) * 512], ALU.mult,
                     reads=[pmk, f"row{g}0"], writes=[tmpk + str(hf)])
            r, rk = rr.next()
            P.stt("vector", r[:], x[:], ALPHA, tmp[:], ALU.mult, ALU.add, reads=[xk, tmpk + "0", tmpk + "1"], writes=[rk])
            xn, xnk = xnr.next(); st, _ = str_.next(); mv, _ = mvr.next(); rs, _ = rsr.next()
            emit_layernorm(P, r[:], rk, xn[:], xnk, st, mv, rs, f"lnA{lnc % 4}"); lnc += 1
            x1a, x1ak = x1ar.next(); x1, x1k = x1r.next()
            P.tt("gpsimd", x1a[:], xn[:], lnt[:, 0, :], ALU.mult, reads=[xnk, "ln0"], writes=[x1ak])
            P.tt("gpsimd", x1[:], x1a[:], lnt[:, 1, :], ALU.add, reads=[x1ak, "ln1"], writes=[x1k])
            P.dma("gpsimd", x1o[t], x1[:], reads=[x1k], writes=[x1k + "d"], final=True)
            xn2, xn2k = xn2r.next(); st, _ = str_.next(); mv, _ = mvr.next(); rs, _ = rsr.next()
            emit_layernorm(P, x1[:], x1k, xn2[:], xn2k, st, mv, rs, f"lnA{lnc % 4}"); lnc += 1
            h2a, h2ak = h2ar.next(); h2f, h2fk = h2fr.next(); h2b, h2bk = h2br.next()
            P.tt("gpsimd", h2a[:], xn2[:], rowt[:, g, 1, :], ALU.mult, reads=[xn2k, f"row{g}1"], writes=[h2ak])
            P.tt("vector", h2f[:], h2a[:], rowt[:, g, 2, :], ALU.add, reads=[h2ak, f"row{g}2"], writes=[h2fk])
            P.copy("scalar", h2b[:], h2f[:], reads=[h2fk], writes=[h2bk])
            P.dma("sync", h2o[t], h2b[:], reads=[h2bk], writes=[h2bk + "d"], final=True)
            pTf, pTfk = pTfr.next()
            for kc in range(8):
                P.tr(pTf[:, kc * 128:(kc + 1) * 128], h2f[:, kc * 128:(kc + 1) * 128], idf[:], reads=[h2fk, "idf"], writes=[pTfk])
            h2T, h2Tk = h2Tr.next()
            P.copy("scalar", h2T[:, 0:512], pTf[:, 0:512], reads=[pTfk], writes=[h2Tk + "a"])
            P.copy("vector", h2T[:, 512:1024], pTf[:, 512:1024], reads=[pTfk], writes=[h2Tk + "b"])
            pl, plk = plr.next()
            for kc in range(8):
                P.mm(pl[:], h2T[:, kc * 128:(kc + 1) * 128], rw[:, kc, :], kc == 0, kc == 7,
                     reads=[h2Tk + "a", h2Tk + "b", "rw"], writes=[plk])
            lg, lgk = lgr.next(); ex, exk = exr.next(); af, afk = afr.next(); sm, smk = smr.next()
            P.tt("vector", lg[:], pl[:], rb[:], ALU.add, reads=[plk, "rb"], writes=[lgk])
            P.gen("vector", lambda e, sm=sm, lg=lg: e.reduce_max(out=sm[:, 0:1], in_=lg[:], axis=AX.X), reads=[lgk], writes=[smk + "m"])
            P.ts("vector", sm[:, 1:2], sm[:, 0:1], -1.0, None, ALU.mult, reads=[smk + "m"], writes=[smk + "n"])
            P.act(ex[:], lg[:], AF.Exp, reads=[lgk, smk + "n"], writes=[exk, smk + "s"], bias=sm[:, 1:2], scale=1.0,
                  accum_out=sm[:, 2:3])
            P.gen("vector", lambda e, sm=sm: e.reciprocal(out=sm[:, 3:4], in_=sm[:, 2:3]), reads=[smk + "s"], writes=[smk + "r"])
            P.ts("vector", af[:], ex[:], sm[:, 3:4], None, ALU.mult, reads=[exk, smk + "r"], writes=[afk])
            P.dma("gpsimd", affo[t], af[:], reads=[afk], writes=[afk + "d"], final=True)
        P.emit()
    return nc


def tok_shard(lat, ctx, i):
    B, n, D = lat.shape
    seg = n // 4
    b, s = i // 4, i % 4
    out = np.zeros((K1_TILES * 128, D), lat.dtype)
    out[:seg] = lat[b, s * seg:(s + 1) * seg]
    out[seg:seg + 64] = ctx.reshape(-1, D)[i * 64:(i + 1) * 64]
    return out.reshape(K1_TILES, 128, D)


def tok_unshard(parts, B, n, nctx):
    D = parts[0].shape[-1]
    seg = n // 4
    lat = np.zeros((B, n, D), parts[0].dtype)
    ctx = np.zeros((B * nctx, D), parts[0].dtype)
    for i in range(NCORES):
        b, s = i // 4, i % 4
        p = parts[i].reshape(K1_TILES * 128, D)
        lat[b, s * seg:(s + 1) * seg] = p[:seg]
        ctx[i * 64:(i + 1) * 64] = p[seg:seg + 64]
    return lat, ctx.reshape(B, nctx, D)


def run_k5(ycat_l, ycat_c, x, ctx, mod_l, w_out_l, ln_w, ln_b, router_w_l, router_b_l):
    nc = build_k5()
    B, n, D = x.shape
    ident = np.eye(128, dtype=np.float32)
    wout = lay_w(w_out_l, 8)
    rw = lay_w(router_w_l, 8)
    lnr = np.ascontiguousarray(np.stack([ln_w, ln_b]))
    in_maps = []
    for i in range(NCORES):
        b = i // 4
        rows = np.stack([np.stack([mod_l[m, 2048:3072], mod_l[m, 4096:5120], mod_l[m, 3072:4096]]) for m in (b, 2)])
        in_maps.append({"yt": tok_shard(ycat_l, ycat_c, i), "xt": tok_shard(x, ctx, i), "wout": wout,
                        "rows": np.ascontiguousarray(rows), "lnr": lnr, "rw": rw,
                        "rb": np.ascontiguousarray(router_b_l), "ident": ident})
    res = _run(nc, in_maps)
    nctx = ctx.shape[1]
    x1, c1 = tok_unshard([r["o_x1"] for r in res.results], B, n, nctx)
    h2, h2c = tok_unshard([r["o_h2"] for r in res.results], B, n, nctx)
    aff, affc = tok_unshard([r["o_aff"] for r in res.results], B, n, nctx)
    return x1, c1, h2, h2c, aff, affc


K6_ITERS = 26


def build_k6(F_lat, k_lat, F_ctx, k_ctx):
    nc = bass.Bass("TRN2", target_bir_lowering=False)
    R = 32
    ald = nc.dram_tensor("al", [R, F_lat], F32, kind="ExternalInput").ap()
    acd = nc.dram_tensor("ac", [R, F_ctx], F32, kind="ExternalInput").ap()
    thro = nc.dram_tensor("thr", [R, 2], F32, kind="ExternalOutput").ap()
    with ExitStack() as es:
        P = Prog(nc, es)
        sb, ps, ring = mk_alloc(nc, es)
        res = sb("res", [R, 2])
        for pi, (src, F, k) in enumerate(((ald, F_lat, k_lat), (acd, F_ctx, k_ctx))):
            A = sb(f"A{pi}", [R, F])
            junk = sb(f"junk{pi}", [R, F], BF16)
            sc = sb(f"sc{pi}", [R, 8])
            lo, hi, mid, cnt, cond, t1, t2 = (sc[:, j:j + 1] for j in range(7))
            kk = f"p{pi}"
            P.dma("sync", A[:], src, writes=[kk + "A"])
            P.memset("vector", lo, 0.0, writes=[kk + "lo"])
            P.memset("vector", hi, 1.0, writes=[kk + "hi"])
            for it in range(K6_ITERS):
                P.tt("vector", mid, lo, hi, ALU.add, reads=[kk + "lo", kk + "hi"], writes=[kk + "mid"])
                P.ts("vector", mid, mid, 0.5, None, ALU.mult, reads=[kk + "mid"], writes=[kk + "mid"])
                P.ts("vector", junk[:], A[:], mid, None, ALU.is_ge, ALU.add, reads=[kk + "A", kk + "mid"],
                     writes=[kk + "junk", kk + "cnt"], accum_out=cnt)
                P.ts("vector", cond, cnt, float(k) - 0.5, None, ALU.is_ge, reads=[kk + "cnt"], writes=[kk + "cond"])
                P.tt("vector", t1, cond, mid, ALU.mult, reads=[kk + "cond", kk + "mid"], writes=[kk + "t1"])
                P.stt("vector", t2, cond, 2.0, mid, ALU.mult, ALU.add, reads=[kk + "cond", kk + "mid"], writes=[kk + "t2"])
                P.tt("vector", lo, lo, t1, ALU.max, reads=[kk + "lo", kk + "t1"], writes=[kk + "lo"])
                P.tt("vector", hi, hi, t2, ALU.min, reads=[kk + "hi", kk + "t2"], writes=[kk + "hi"])
            P.copy("vector", res[:, pi:pi + 1], lo, reads=[kk + "lo"], writes=[f"res{pi}"])
        P.dma("sync", thro, res[:], reads=["res0", "res1"], final=True)
        P.emit()
    return nc


def run_k6(aff, affc):
    B, n, E = aff.shape
    ncx = affc.shape[1]
    nc = build_k6(n, 2 * n // E, ncx, 2 * ncx // E)
    al = np.ascontiguousarray(aff.transpose(0, 2, 1).reshape(B * E, n))
    ac = np.ascontiguousarray(affc.transpose(0, 2, 1).reshape(B * E, ncx))
    res = _run(nc, [{"al": al, "ac": ac} for _ in range(NCORES)])
    thr = res.results[0]["thr"]
    return thr[:, 0].reshape(B, E), thr[:, 1].reshape(B, E)


D_FF = 2816
NFC = D_FF // 128
K7_TOK = K1_TILES * 128


def build_k7(n_exp=N_EXP):
    nc = bass.Bass("TRN2", target_bir_lowering=False)
    T = K1_TILES
    h2Td = nc.dram_tensor("h2T", [128, 8, K7_TOK], BF16, kind="ExternalInput").ap()
    affd = nc.dram_tensor("aff", [T, 128, N_EXP], F32, kind="ExternalInput").ap()
    thrd = nc.dram_tensor("thr", [2, N_EXP], F32, kind="ExternalInput").ap()
    x1d = nc.dram_tensor("i_x1", [T, 128, 1024], F32, kind="ExternalInput").ap()
    rows = nc.dram_tensor("rows", [2, 1024], F32, kind="ExternalInput").ap()
    lnr = nc.dram_tensor("lnr", [2, 1024], F32, kind="ExternalInput").ap()
    wgd = nc.dram_tensor("wg", [n_exp, 128, 8, D_FF], F32, kind="ExternalInput").ap()
    wud = nc.dram_tensor("wu", [n_exp, 128, 8, D_FF], F32, kind="ExternalInput").ap()
    wdd = nc.dram_tensor("wd", [n_exp, 128, NFC, 1024], F32, kind="ExternalInput").ap()
    x2o = nc.dram_tensor("o_x2", [T, 128, 1024], F32, kind="ExternalOutput").ap()
    with ExitStack() as es:
        P = Prog(nc, es)
        sb, ps, ring = mk_alloc(nc, es)
        h2T = sb("h2Ts", [128, 8, K7_TOK], BF16)
        for kc in range(8):
            P.dma("sync", h2T[:, kc, :], h2Td[:, kc, :], writes=["h2T"] if kc == 7 else [f"h2T_{kc}"])
        h2keys = ["h2T"] + [f"h2T_{kc}" for kc in range(7)]
        thrb = sb("thrb", [128, 2, N_EXP])
        for g in range(2):
            P.dma("sync", thrb[:, g, :], bcast_rows(thrd[g], 128), writes=[f"thr{g}"])
        wgt = sb("wgt", [128, T, N_EXP])
        msk = sb("msk", [128, T, N_EXP])
        for t in range(T):
            g = 0 if t < 16 else 1
            P.dma("sync", wgt[:, t, :], affd[t], writes=[f"aff{t}"])
            P.tt("vector", msk[:, t, :], wgt[:, t, :], thrb[:, g, :], ALU.is_ge, reads=[f"aff{t}", f"thr{g}"], writes=[f"msk{t}"])
            P.tt("vector", wgt[:, t, :], wgt[:, t, :], msk[:, t, :], ALU.mult, reads=[f"aff{t}", f"msk{t}"], writes=[f"aff{t}"])
        acc = sb("acc", [128, T, 1024])
        for t in range(T):
            P.memset("gpsimd", acc[:, t, :], 0.0, writes=[f"acc{t}a", f"acc{t}b"])
        FG = 4
        wgr = ring("wgb", 2, [128, 8, FG * 128], BF16)
        wur = ring("wub", 2, [128, 8, FG * 128], BF16)
        wdr = ring("wdb", 2, [128, FG, 1024], BF16)
        stg = ring("stg", 4, [128, 1024])
        actr = ring("actT", 2, [128, FG, 512], BF16)
        sgr = ring("sg", 2, [128, 512])
        pgr = ring("pg", 2, [128, 512], F32, psum=True)
        pur = ring("pu", 2, [128, 512], F32, psum=True)
        pyr = ring("py", 2, [128, 512], F32, psum=True)
        tgs = [(s, min(512, K7_TOK - s)) for s in range(0, K7_TOK, 512)]
        ci = 0
        for e in range(n_exp):
            for f0 in range(0, NFC, FG):
                nf = min(FG, NFC - f0)
                wgb, wgk = wgr.next(); wub, wuk = wur.next(); wdb, wdk = wdr.next()
                for (src, dst, dk) in ((wgd, wgb, wgk), (wud, wub, wuk)):
                    for kp in range(0, 8, 2):
                        st, sk = stg.next()
                        q = "sync"
                        sv = st[:, 0:2 * nf * 128].rearrange("p (a f) -> p a f", a=2)
                        P.dma(q, sv, src[e, :, kp:kp + 2, f0 * 128:(f0 + nf) * 128], writes=[sk])
                        P.copy("gpsimd", dst[:, kp:kp + 2, 0:nf * 128], sv, reads=[sk], writes=[dk + f"k{kp}"])
                        ci += 1
                for fc in range(nf):
                    st, sk = stg.next()
                    P.dma("sync", st[:], wdd[e, :, f0 + fc, :], writes=[sk])
                    P.copy("gpsimd", wdb[:, fc, :], st[:], reads=[sk], writes=[wdk + f"f{fc}"])
                    ci += 1
                for (s0, ns) in tgs:
                    actT, ak = actr.next()
                    for fc in range(nf):
                        pg, pgk = pgr.next(); pu, puk = pur.next()
                        for kc in range(8):
                            P.mm(pg[:, 0:ns], wgb[:, kc, fc * 128:(fc + 1) * 128], h2T[:, kc, s0:s0 + ns], kc == 0, kc == 7,
                                 reads=h2keys + [wgk + f"k{kc - kc % 2}"], writes=[pgk])
                        for kc in range(8):
                            P.mm(pu[:, 0:ns], wub[:, kc, fc * 128:(fc + 1) * 128], h2T[:, kc, s0:s0 + ns], kc == 0, kc == 7,
                                 reads=h2keys + [wuk + f"k{kc - kc % 2}"], writes=[puk])
                        sg, sgk = sgr.next()
                        P.act(sg[:, 0:ns], pg[:, 0:ns], AF.Silu, reads=[pgk], writes=[sgk])
                        P.tt("vector", actT[:, fc, 0:ns], pu[:, 0:ns], sg[:, 0:ns], ALU.mult, reads=[puk, sgk], writes=[ak + f"f{fc}"])
                    for tt in range(s0 // 128, (s0 + ns) // 128):
                        for hf in range(2):
                            py, pyk = pyr.next()
                            for fc in range(nf):
                                P.mm(py[:], actT[:, fc, tt * 128 - s0:(tt + 1) * 128 - s0], wdb[:, fc, hf * 512:(hf + 1) * 512],
                                     fc == 0, fc == nf - 1, reads=[ak + f"f{fc}", wdk + f"f{fc}"], writes=[pyk])
                            ah = f"acc{tt}" + "ab"[hf]
                            P.stt("vector", acc[:, tt, hf * 512:(hf + 1) * 512], py[:], wgt[:, tt, e:e + 1],
                                  acc[:, tt, hf * 512:(hf + 1) * 512], ALU.mult, ALU.add, reads=[pyk, f"aff{tt}", ah], writes=[ah])
        rowt = sb("rowt", [128, 2, 1024])
        lnt = sb("lnt", [128, 2, 1024])
        for j in range(2):
            P.dma("sync", rowt[:, j, :], bcast_rows(rows[j], 128), writes=[f"row{j}"])
            P.dma("sync", lnt[:, j, :], bcast_rows(lnr[j], 128), writes=[f"ln{j}"])
        xr = ring("x", 2, [128, 1024])
        str_ = ring("st", 2, [128, 2, 6]); mvr = ring("mv", 2, [128, 2]); rsr = ring("rs", 2, [128, 1])
        for t in range(T):
            g = 0 if t < 16 else 1
            ak2 = [f"acc{t}a", f"acc{t}b"]
            x, xk = xr.next(); P.dma("sync", x[:], x1d[t], writes=[xk])
            P.tt("gpsimd", acc[:, t, :], acc[:, t, :], rowt[:, g, :], ALU.mult, reads=ak2 + [f"row{g}"], writes=ak2)
            P.stt("vector", acc[:, t, :], x[:], ALPHA, acc[:, t, :], ALU.mult, ALU.add, reads=[xk] + ak2, writes=ak2)
            st, _ = str_.next(); mv, _ = mvr.next(); rs, _ = rsr.next()
            emit_layernorm(P, acc[:, t, :], ak2[0], x[:], xk, st, mv, rs, f"lnB{t % 2}")
            P.tt("gpsimd", acc[:, t, :], x[:], lnt[:, 0, :], ALU.mult, reads=[xk, "ln0"], writes=ak2)
            P.tt("gpsimd", x[:], acc[:, t, :], lnt[:, 1, :], ALU.add, reads=ak2 + ["ln1"], writes=[xk])
            P.dma("sync", x2o[t], x[:], reads=[xk], writes=[xk], final=True)
        P.emit()
    return nc


def run_k7(h2, h2c, aff, affc, thr, thrc, x1, c1, mod_l, ln_w, ln_b, wg, wu, wd, n_exp=N_EXP):
    nc = build_k7(n_exp)
    B, n, D = x1.shape
    nctx = c1.shape[1]
    wgl = np.ascontiguousarray(wg[:n_exp].reshape(n_exp, 8, 128, D_FF).transpose(0, 2, 1, 3))
    wul = np.ascontiguousarray(wu[:n_exp].reshape(n_exp, 8, 128, D_FF).transpose(0, 2, 1, 3))
    wdl = np.ascontiguousarray(wd[:n_exp].reshape(n_exp, NFC, 128, D).transpose(0, 2, 1, 3))
    lnr = np.ascontiguousarray(np.stack([ln_w, ln_b]))
    in_maps = []
    for i in range(NCORES):
        b = i // 4
        h2s = tok_shard(h2, h2c, i).reshape(K7_TOK, D)
        h2T = np.ascontiguousarray(h2s.T.reshape(8, 128, K7_TOK).transpose(1, 0, 2))
        rows = np.ascontiguousarray(np.stack([mod_l[b, 5120:6144], mod_l[2, 5120:6144]]))
        in_maps.append({"h2T": h2T, "aff": tok_shard(aff, affc, i), "thr": np.ascontiguousarray(np.stack([thr[b], thrc[b]])),
                        "i_x1": tok_shard(x1, c1, i), "rows": rows, "lnr": lnr, "wg": wgl, "wu": wul, "wd": wdl})
    res = _run(nc, in_maps)
    return tok_unshard([r["o_x2"] for r in res.results], B, n, nctx)


RMS_EPS = 1e-6
NKEY = 8192 + 256
NKT = NKEY // 128
QCOLS = K1_TILES * 512


def build_k3():
    nc = bass.Bass("TRN2", target_bir_lowering=False)
    T = K1_TILES
    qd = nc.dram_tensor("qT", [128, QCOLS], F32, kind="ExternalInput").ap()
    kd = nc.dram_tensor("kT", [128, NKEY], F32, kind="ExternalInput").ap()
    vd = nc.dram_tensor("v", [128, NKT, 2, 64], F32, kind="ExternalInput").ap()
    cqd = nc.dram_tensor("cosq", [128, 16 * 512], F32, kind="ExternalInput").ap()
    sqd = nc.dram_tensor("sinq", [128, 16 * 512], F32, kind="ExternalInput").ap()
    ckd = nc.dram_tensor("cosk", [128, 8192], F32, kind="ExternalInput").ap()
    skd = nc.dram_tensor("sink", [128, 8192], F32, kind="ExternalInput").ap()
    cst = nc.dram_tensor("cst", [128, 258], F32, kind="ExternalInput").ap()
    yo = nc.dram_tensor("o_ya", [T, 128, 512], F32, kind="ExternalOutput").ap()
    with ExitStack() as es:
        P = Prog(nc, es)
        sb, ps, ring = mk_alloc(nc, es)
        cs = sb("cs", [128, 258])
        P.dma("sync", cs[:], cst, writes=["cs"])
        Rm, onesb, qw2, kw2 = cs[:, 0:128], cs[:, 128:256], cs[:, 256:257], cs[:, 257:258]
        bq = sb("bq", [128, 2])
        P.memset("vector", bq[:, 0:1], 64.0 * RMS_EPS, writes=["bq0"])
        P.memset("vector", bq[:, 1:2], RMS_EPS, writes=["bq1"])
        qr = sb("qr", [128, QCOLS], BF16)
        kr = sb("kr", [128, NKEY], BF16)
        vst = ring("vst", 2, [128, 2, 64])
        vaug = sb("vaug", [128, NKT, 2, 65], BF16)
        P.memset("gpsimd", vaug[:], 1.0, writes=["vaug_init"])
        for kt in range(NKT):
            v, vk = vst.next()
            P.dma("sync", v[:], vd[:, kt], writes=[vk])
            P.copy("gpsimd", vaug[:, kt, :, 0:64], v[:], reads=[vk, "vaug_init"], writes=[f"vaug{kt}"])
        xr = ring("px", 2, [128, 512]); sqr = ring("psq", 2, [128, 512]); sdr = ring("psd", 2, [128, 512])
        xnr = ring("pxn", 2, [128, 512]); cr = ring("pc", 2, [128, 512]); sr = ring("psn", 2, [128, 512])
        t1r = ring("pt1", 2, [128, 512]); t2r = ring("pt2", 2, [128, 512])
        bank = [(ps(f"mb{i}", [128, 512], F32), f"mb{i}") for i in range(8)]
        pssr = Ring(bank[0:1])
        prot = Ring(bank[1:2])

        def prep(src, dst, dkey, ncols, nrope, w2, bcol, scale, cosd, sind):
            for c0 in range(0, ncols, 512):
                n = min(512, ncols - c0)
                x, xk = xr.next(); P.dma("sync", x[:, 0:n], src[:, c0:c0 + n], writes=[xk])
                sq, sqk = sqr.next(); P.tt("gpsimd", sq[:, 0:n], x[:, 0:n], x[:, 0:n], ALU.mult, reads=[xk], writes=[sqk])
                pss, pssk = pssr.next()
                P.mm(pss[:, 0:n], onesb, sq[:, 0:n], True, True, reads=[sqk, "cs"], writes=[pssk])
                sd, sdk = sdr.next()
                P.act(sd[:, 0:n], pss[:, 0:n], AF.Sqrt, reads=[pssk, f"bq{bcol}"], writes=[sdk], bias=bq[:, bcol:bcol + 1], scale=scale)
                P.gen("vector", lambda e, sd=sd, n=n: e.reciprocal(out=sd[:, 0:n], in_=sd[:, 0:n]), reads=[sdk], writes=[sdk])
                xn, xnk = xnr.next()
                P.stt("vector", xn[:, 0:n], x[:, 0:n], w2, sd[:, 0:n], ALU.mult, ALU.mult, reads=[xk, sdk, "cs"], writes=[xnk])
                dk = f"{dkey}{c0 // 512}"
                if c0 < nrope:
                    pr, prk = prot.next()
                    P.mm(pr[:, 0:n], Rm, xn[:, 0:n], True, True, reads=[xnk, "cs"], writes=[prk])
                    c, ck = cr.next(); P.dma("sync", c[:, 0:n], cosd[:, c0:c0 + n], writes=[ck])
                    s, sk = sr.next(); P.dma("sync", s[:, 0:n], sind[:, c0:c0 + n], writes=[sk])
                    t1, t1k = t1r.next(); P.tt("gpsimd", t1[:, 0:n], xn[:, 0:n], c[:, 0:n], ALU.mult, reads=[xnk, ck], writes=[t1k])
                    t2, t2k = t2r.next(); P.tt("vector", t2[:, 0:n], pr[:, 0:n], s[:, 0:n], ALU.mult, reads=[prk, sk], writes=[t2k])
                    P.tt("gpsimd", dst[:, c0:c0 + n], t1[:, 0:n], t2[:, 0:n], ALU.add, reads=[t1k, t2k], writes=[dk])
                else:
                    P.copy("gpsimd", dst[:, c0:c0 + n], xn[:, 0:n], reads=[xnk], writes=[dk])

        prep(kd, kr, "kr", NKEY, 8192, kw2, 1, 1.0 / 64.0, ckd, skd)
        prep(qd, qr, "qr", QCOLS, 16 * 512, qw2, 0, 1.0, cqd, sqd)
        LA = 1
        pstR = Ring(bank[0:4])
        accbank = {(par, g): bank[4 + par * 2 + g] for par in range(2) for g in range(2)}
        ptr = ring("PT", 6, [128, 512], BF16)
        yr = ring("yo", 2, [128, 512])
        rcr = ring("rc", 2, [128, 4])
        its = []
        for qt in range(T):
            kts = list(range(NKT)) if qt < 16 else [64, 65]
            for ii, kt in enumerate(kts):
                its.append((qt, ii, kt, len(kts)))
        pend = {}
        ycur = {}
        for idx in range(len(its) + LA):
            if idx < len(its):
                qt, ii, kt, nk = its[idx]
                pair = []
                for g in range(2):
                    pst, pstk = pstR.next()
                    P.mm(pst[:], kr[g * 64:(g + 1) * 64, kt * 128:(kt + 1) * 128], qr[g * 64:(g + 1) * 64, qt * 512:(qt + 1) * 512],
                         True, True, reads=[f"kr{kt // 4}", f"qr{qt}"], writes=[pstk])
                    pair.append((pst, pstk))
                pend[idx] = pair
            j0 = idx - LA
            if j0 < 0:
                continue
            qt, ii, kt, nk = its[j0]
            pair = pend.pop(j0)
            PTs = []
            for g in range(2):
                pst, pstk = pair[g]
                PT, PTk = ptr.next()
                P.act(PT[:], pst[:], AF.Exp, reads=[pstk], writes=[PTk])
                PTs.append((PT, PTk))
            for g in range(2):
                PT, PTk = PTs[g]
                pb, pbk = accbank[(qt % 2, g)]
                for j in range(4):
                    P.op("tensor", lambda e, pb=pb, PT=PT, j=j, kt=kt, g=g, first=(ii == 0 and j == 0), last=(ii == nk - 1): e.matmul(
                        pb[:, j * 128:j * 128 + 65], PT[:, j * 128:(j + 1) * 128], vaug[:, kt, g, :], start=first, stop=last,
                        skip_group_check=True), reads=[PTk, f"vaug{kt}"], writes=[pbk])
            if ii == nk - 1:
                ycur[qt] = yr.next()
                y, yk = ycur[qt]
                for g in range(2):
                    pb, pbk = accbank[(qt % 2, g)]
                    rc, rck = rcr.next()
                    for j in range(4):
                        po = pb[:, j * 128:j * 128 + 65]
                        P.gen("vector", lambda e, rc=rc, po=po, j=j: e.reciprocal(out=rc[:, j:j + 1], in_=po[:, 64:65]), reads=[pbk], writes=[rck + str(j)])
                        c0 = (g * 4 + j) * 64
                        P.ts("vector", y[:, c0:c0 + 64], po[:, 0:64], rc[:, j:j + 1], None, ALU.mult, reads=[pbk, rck + str(j)], writes=[yk + f"{g}{j}"])
                P.dma("sync", yo[qt], y[:], reads=[yk + f"{g_}{j}" for g_ in range(2) for j in range(4)], writes=[yk + "d"], final=True)
        P.emit()
    return nc


def rope_tables(n_lat, grid_w=64, theta=10000.0, hd=64):
    nf = hd // 4
    t = np.arange(n_lat)
    row = (t // grid_w).astype(np.float32)
    col = (t % grid_w).astype(np.float32)
    inv = (theta ** (-np.arange(nf, dtype=np.float32) / nf)).astype(np.float32)
    ar = row[:, None] * inv
    ac = col[:, None] * inv
    ang = np.concatenate([ar, ar, ac, ac], axis=-1)
    return np.cos(ang).astype(np.float32), np.sin(ang).astype(np.float32)


def rope_rot_matrix():
    R = np.zeros((64, 64), np.float32)
    for a in range(2):
        for f in range(16):
            R[a * 32 + 16 + f, a * 32 + f] = -1.0
            R[a * 32 + f, a * 32 + 16 + f] = 1.0
    return R


def run_k3(P_lat, P_ctx, qw, kw):
    nc = build_k3()
    B, n, _ = P_lat.shape
    nctx = P_ctx.shape[1]
    aq_l, ak_l, av_l = P_lat[..., 1040:1552], P_lat[..., 1552:1680], P_lat[..., 1680:1808]
    aq_c, ak_c, av_c = P_ctx[..., 1040:1552], P_ctx[..., 1552:1680], P_ctx[..., 1680:1808]
    cos, sin = rope_tables(n)
    R = rope_rot_matrix()
    Rm = np.zeros((128, 128), np.float32); Rm[:64, :64] = R; Rm[64:, 64:] = R
    ob = np.zeros((128, 128), np.float32); ob[:64, :64] = 1; ob[64:, 64:] = 1
    cst = np.concatenate([Rm, ob, np.tile(qw, 2)[:, None], np.tile(kw, 2)[:, None]], 1).astype(np.float32)
    cosk = np.ascontiguousarray(np.tile(cos.T, (2, 1))); sink = np.ascontiguousarray(np.tile(sin.T, (2, 1)))
    seg = n // 4
    in_maps = []
    for i in range(NCORES):
        b, s = i // 4, i % 4
        q = tok_shard(aq_l, aq_c, i).reshape(K1_TILES, 128, 2, 4, 64)
        qT = np.ascontiguousarray(q.transpose(2, 4, 0, 3, 1)).reshape(128, QCOLS)
        k = np.concatenate([ak_l[b], ak_c[b]], 0).reshape(NKEY, 2, 64)
        kT = np.ascontiguousarray(k.transpose(1, 2, 0)).reshape(128, NKEY)
        v = np.concatenate([av_l[b], av_c[b]], 0).reshape(NKT, 128, 2, 64)
        vv = np.ascontiguousarray(v.transpose(1, 0, 2, 3))
        cq = cos[s * seg:(s + 1) * seg].reshape(16, 128, 64)
        sq = sin[s * seg:(s + 1) * seg].reshape(16, 128, 64)
        cq = np.broadcast_to(cq.transpose(2, 0, 1)[None, :, :, None, :], (2, 64, 16, 4, 128)).reshape(128, 16 * 512)
        sq = np.broadcast_to(sq.transpose(2, 0, 1)[None, :, :, None, :], (2, 64, 16, 4, 128)).reshape(128, 16 * 512)
        in_maps.append({"qT": qT, "kT": kT, "v": vv, "cosq": np.ascontiguousarray(cq), "sinq": np.ascontiguousarray(sq),
                        "cosk": cosk, "sink": sink, "cst": cst})
    res = _run(nc, in_maps)
    return tok_unshard([r["o_ya"] for r in res.results], B, n, nctx)


NCH = NKT
NSEQ = NKEY
MASK_NEG = -30000.0


def build_k2():
    nc = bass.Bass("TRN2", target_bir_lowering=False)
    qpd = nc.dram_tensor("qpT", [64, NSEQ], F32, kind="ExternalInput").ap()
    kpd = nc.dram_tensor("kpT", [64, NSEQ], F32, kind="ExternalInput").ap()
    vd = nc.dram_tensor("v", [128, NCH, 64], F32, kind="ExternalInput").ap()
    od = nc.dram_tensor("og", [128, NCH, 64], F32, kind="ExternalInput").ap()
    gd = nc.dram_tensor("g4", [128, NCH, 4], F32, kind="ExternalInput").ap()
    gbd = nc.dram_tensor("gb", [NCH * 4], F32, kind="ExternalInput").ap()
    cwd = nc.dram_tensor("cw", [64, 8], F32, kind="ExternalInput").ap()
    nwd = nc.dram_tensor("nw", [64], F32, kind="ExternalInput").ap()
    cstd = nc.dram_tensor("cst", [128, 6, 128], F32, kind="ExternalInput").ap()
    yo = nc.dram_tensor("o_ym", [128, NCH, 64], F32, kind="ExternalOutput").ap()
    with ExitStack() as es:
        P = Prog(nc, es)
        sb, ps, ring = mk_alloc(nc, es)
        cst = sb("cst_s", [128, 6, 128])
        P.dma("sync", cst[:], cstd, writes=["cst"])
        Lm = [cst[:, 0, :], cst[:, 1, :]]
        ones, ident = cst[:, 2, :], cst[:, 3, :]
        mneg = [cst[:, 4, :], cst[:, 5, :]]
        idb = sb("idb", [128, 128], BF16)
        P.copy("vector", idb[:], ident, reads=["cst"], writes=["idb"])
        one1 = sb("one1", [128, 1])
        P.memset("vector", one1[:], 1.0, writes=["one1"])
        cw = sb("cw_s", [64, 8])
        P.dma("sync", cw[:], cwd, writes=["cw"])
        banks = [ps(f"bank{i}", [128, 512], F32) for i in range(8)]
        xin = sb("xin", [64, NSEQ]); cacc = sb("cacc", [64, NSEQ])
        qT = sb("qT_s", [64, NSEQ], BF16); kT = sb("kT_s", [64, NSEQ], BF16)
        segs = [(0, 256), (256, NSEQ)]
        for wi, (src, dst, post) in enumerate(((qpd, qT, 0.125), (kpd, kT, 1.0))):
            o = wi * 4
            half = NSEQ // 2
            P.dma("sync", xin[:, 0:half], src[:, 0:half], writes=["xin_a"])
            P.dma("gpsimd", xin[:, half:], src[:, half:], writes=["xin_b"])
            P.ts("vector", cacc[:], xin[:], cw[:, o + 1:o + 2], cw[:, o + 3:o + 4], ALU.mult, ALU.add,
                 reads=["xin_a", "xin_b", "cw"], writes=["cacc"])
            for (a, b) in segs:
                P.stt("vector", cacc[:, a + 1:b], xin[:, a:b - 1], cw[:, o:o + 1], cacc[:, a + 1:b], ALU.mult, ALU.add,
                      reads=["xin_a", "xin_b", "cw", "cacc"], writes=["cacc"])
                P.stt("vector", cacc[:, a:b - 1], xin[:, a + 1:b], cw[:, o + 2:o + 3], cacc[:, a:b - 1], ALU.mult, ALU.add,
                      reads=["xin_a", "xin_b", "cw", "cacc"], writes=["cacc"])
            P.act(cacc[:], cacc[:], AF.Silu, reads=["cacc"], writes=["cacc"])
            P.ts("gpsimd", dst[:], cacc[:], post, None, ALU.mult, reads=["cacc"], writes=[f"T{wi}"])
            P.memset("vector", xin[:, 0:1], 0.0, writes=["xin_a", "xin_b"]) if wi == 0 else None
        ktok = sb("ktok", [128, NCH, 64], BF16)
        for c0 in range(0, NCH, 8):
            n = min(8, NCH - c0)
            pb = banks[7]
            for c in range(c0, c0 + n):
                pt = pb[:, :].bitcast(BF16)[:, (c - c0) * 64:(c - c0 + 1) * 64]
                P.tr(pt, kT[:, c * 128:(c + 1) * 128], idb[0:64, 0:64], reads=["T1", "idb"], writes=["bank7"])
            P.copy("vector", ktok[:, c0:c0 + n, :], pb[:, :].bitcast(BF16)[:, 0:n * 64].rearrange("p (c d) -> p c d", d=64),
                   reads=["bank7"], writes=["ktok"])
        vf = sb("vf", [128, NCH, 65]); vb = sb("vb", [128, NCH, 65], BF16)
        P.memset("gpsimd", vf[:], 1.0, writes=["vf"])
        vtmp = sb("vtmp", [128, NCH, 64])
        P.dma("sync", vtmp[:], vd, writes=["vtmp"])
        P.copy("gpsimd", vf[:, :, 0:64], vtmp[:], reads=["vtmp", "vf"], writes=["vf"])
        P.copy("gpsimd", vb[:], vf[:], reads=["vf"], writes=["vb"])
        G = sb("G", [128, NCH, 4]); GB = sb("GB", [128, NCH, 4])
        P.dma("sync", G[:], gd, writes=["G"])
        P.dma("sync", GB[:].rearrange("p c g -> p (c g)"), bcast_rows(gbd, 128), writes=["GB"])
        P.tt("vector", G[:], G[:], GB[:], ALU.add, reads=["G", "GB"], writes=["G"])
        LF = sb("LF", [128, 2, NCH]); LI = sb("LI", [128, 2, NCH]); TA = sb("TA", [128, 2, NCH]); TB = sb("TB", [128, 2, NCH])
        for dd in range(2):
            P.copy("vector", LI[:, dd, :], G[:, :, 2 * dd], reads=["G"], writes=[f"LI{dd}"])
            P.copy("vector", TA[:, dd, :], G[:, :, 2 * dd + 1], reads=["G"], writes=["TA"])
        P.act(TB[:], TA[:], AF.Abs, reads=["TA"], writes=["TB"])
        P.act(TB[:], TB[:], AF.Exp, reads=["TB"], writes=["TB"], scale=-1.0)
        P.act(TB[:], TB[:], AF.Ln, reads=["TB", "one1"], writes=["TB"], bias=one1[:, 0:1], scale=1.0)
        P.ts("vector", TA[:], TA[:], 0.0, None, ALU.min, reads=["TA"], writes=["TA"])
        P.tt("vector", LF[:], TA[:], TB[:], ALU.subtract, reads=["TA", "TB"], writes=["LF"])
        BC = sb("BC", [128, 2, NCH]); TOT = sb("TOT", [128, 2, NCH]); AA = sb("AA", [128, 2, NCH])
        BD = sb("BD", [128, 2, NCH]); WW = sb("WW", [128, 2, NCH]); DEC = sb("DEC", [128, 2, NCH])
        b6 = banks[6]
        for dd in range(2):
            P.mm(b6[:, dd * NCH:(dd + 1) * NCH], Lm[dd], LF[:, dd, :], True, True, reads=["LF", "cst"], writes=["bank6"])
        P.mm(b6[:, 2 * NCH:4 * NCH], ones, LF[:].rearrange("p a c -> p (a c)"), True, True, reads=["LF", "cst"], writes=["bank6"])
        P.copy("vector", BC[:].rearrange("p a c -> p (a c)"), b6[:, 0:2 * NCH], reads=["bank6"], writes=["BC"])
        P.copy("vector", TOT[:].rearrange("p a c -> p (a c)"), b6[:, 2 * NCH:4 * NCH], reads=["bank6"], writes=["TOT"])
        P.act(AA[:], BC[:], AF.Exp, reads=["BC"], writes=["AA"])
        P.tt("vector", BD[:], LI[:], BC[:], ALU.subtract, reads=["LI0", "LI1", "BC"], writes=["BD"])
        P.tt("vector", WW[:], TOT[:], BD[:], ALU.add, reads=["TOT", "BD"], writes=["WW"])
        P.act(WW[:], WW[:], AF.Exp, reads=["WW"], writes=["WW"])
        P.act(DEC[:], TOT[:], AF.Exp, reads=["TOT"], writes=["DEC"])
        S = [sb(f"S{dd}", [64, 65]) for dd in range(2)]
        Sb = [sb(f"Sb{dd}", [64, 65], BF16) for dd in range(2)]
        for dd in range(2):
            P.memset("vector", S[dd][:], 0.0, writes=[f"S{dd}"])
            P.memset("vector", Sb[dd][:], 0.0, writes=[f"Sb{dd}"])
        hb = [sb(f"hb{dd}", [128, NCH, 64]) for dd in range(2)]
        lfr = ring("lfrep", 2, [128, 128]); dtr = ring("Dt", 2, [128, 128]); ptr = ring("PTm", 2, [128, 128], BF16)
        tmr = ring("tmpi", 2, [128, 65]); ttr = ring("tot", 2, [128, 65]); dnr = ring("den", 2, [128, 4])
        wvr = ring("wv", 2, [128, 65], BF16)
        pD = Ring([(banks[0], "bank0"), (banks[1], "bank1")])
        pST = Ring([(banks[2], "bank2"), (banks[3], "bank3")])
        pOI = Ring([(banks[4], "bank4"), (banks[5], "bank5")])
        order = [list(range(NCH)), [1, 0] + list(range(NCH - 1, 1, -1))]
        for step in range(NCH):
            for dd in range(2):
                c = order[dd][step]
                cs_ = slice(c * 128, (c + 1) * 128)
                lf, lfk = lfr.next()
                P.ts("vector", lf[:], ones, LF[:, dd, c:c + 1], None, ALU.mult, reads=["cst", "LF"], writes=[lfk])
                pd, pdk = pD.next()
                P.mm(pd[:, 0:128], lf[:], Lm[dd], True, False, reads=[lfk, "cst"], writes=[pdk])
                P.mm(pd[:, 0:128], ident, mneg[dd], False, True, reads=["cst"], writes=[pdk])
                dt_, dtk = dtr.next()
                P.act(dt_[:], pd[:, 0:128], AF.Exp, reads=[pdk, "BD"], writes=[dtk], bias=BD[:, dd, c:c + 1], scale=1.0)
                pst, pstk = pST.next()
                P.mm(pst[:, 0:128], kT[:, cs_], qT[:, cs_], True, True, reads=["T0", "T1"], writes=[pstk])
                PT, PTk = ptr.next()
                P.tt("vector", PT[:], pst[:, 0:128], dt_[:], ALU.mult, reads=[pstk, dtk], writes=[PTk])
                poi, poik = pOI.next()
                P.mm(poi[:, 0:65], PT[:], vb[:, c, :], True, True, reads=[PTk, "vb"], writes=[poik + "o"])
                P.mm(poi[:, 128:193], qT[:, cs_], Sb[dd][:], True, True, reads=["T0", f"Sb{dd}"], writes=[poik + "i"])
                tm, tmk = tmr.next()
                P.act(tm[:], poi[:, 128:193], AF.Copy, reads=[poik + "i", "AA"], writes=[tmk], scale=AA[:, dd, c:c + 1])
                tt_, ttk = ttr.next()
                P.tt("vector", tt_[:], poi[:, 0:65], tm[:], ALU.add, reads=[poik + "o", tmk], writes=[ttk])
                dn, dnk = dnr.next()
                P.ts("vector", dn[:, 0:1], tt_[:, 64:65], -1.0, None, ALU.mult, reads=[ttk], writes=[dnk])
                P.stt("vector", dn[:, 1:2], dn[:, 0:1], 1.0, tt_[:, 64:65], ALU.max, ALU.max, reads=[dnk, ttk], writes=[dnk])
                P.gen("vector", lambda e, dn=dn: e.reciprocal(out=dn[:, 2:3], in_=dn[:, 1:2]), reads=[dnk], writes=[dnk])
                P.ts("vector", hb[dd][:, c, :], tt_[:, 0:64], dn[:, 2:3], None, ALU.mult, reads=[ttk, dnk], writes=[f"hb{dd}_{c}"])
                wv, wvk = wvr.next()
                P.ts("vector", wv[:], vf[:, c, :], WW[:, dd, c:c + 1], None, ALU.mult, reads=["vf", "WW"], writes=[wvk])
                p7 = banks[7]
                P.mm(p7[0:64, 256 + dd * 128:256 + dd * 128 + 65], ktok[:, c, :], wv[:], True, True, reads=["ktok", wvk], writes=[f"b7s{dd}"])
                P.stt("vector", S[dd][:], S[dd][:], DEC[0:64, dd, c:c + 1], p7[0:64, 256 + dd * 128:256 + dd * 128 + 65], ALU.mult, ALU.add,
                      reads=[f"S{dd}", "DEC", f"b7s{dd}"], writes=[f"S{dd}"])
                P.copy("gpsimd", Sb[dd][:], S[dd][:], reads=[f"S{dd}"], writes=[f"Sb{dd}"])
        hk = [f"hb{dd}_{c}" for dd in range(2) for c in range(NCH)]
        P.tt("vector", hb[0][:], hb[0][:], hb[1][:], ALU.add, reads=hk, writes=["hsum"])
        sq = hb[1]
        P.tt("gpsimd", sq[:], hb[0][:], hb[0][:], ALU.mult, reads=["hsum"], writes=["hsq"])
        ssum = sb("ssum", [128, NCH])
        P.gen("vector", lambda e: e.reduce_sum(out=ssum[:], in_=sq[:], axis=AX.X), reads=["hsq"], writes=["ssum"])
        P.ts("vector", ssum[:], ssum[:], 1.0 / 64.0, RMS_EPS, ALU.mult, ALU.add, reads=["ssum"], writes=["ssum"])
        P.act(ssum[:], ssum[:], AF.Sqrt, reads=["ssum"], writes=["ssum"])
        P.gen("vector", lambda e: e.reciprocal(out=ssum[:], in_=ssum[:]), reads=["ssum"], writes=["ssum"])
        nw = sb("nw_s", [128, 64])
        P.dma("sync", nw[:], bcast_rows(nwd, 128), writes=["nw"])
        og = vtmp
        P.dma("sync", og[:], od, reads=["vf"], writes=["og"])
        P.act(og[:], og[:], AF.Sigmoid, reads=["og"], writes=["og"])
        for c in range(NCH):
            P.stt("vector", hb[0][:, c, :], hb[0][:, c, :], ssum[:, c:c + 1], nw[:], ALU.mult, ALU.mult,
                  reads=["hsum", "ssum", "nw"], writes=[f"hn{c}"])
        P.tt("gpsimd", hb[0][:], hb[0][:], og[:], ALU.mult, reads=[f"hn{c}" for c in range(NCH)] + ["og"], writes=["ym"])
        P.dma("sync", yo, hb[0][:], reads=["ym"], final=True)
        P.emit()
    return nc


def run_k2(P_lat, P_ctx, conv_w, conv_b, gate_b, norm_w):
    nc = build_k2()
    B, n, _ = P_lat.shape
    nctx = P_ctx.shape[1]
    s_idx, j_idx = np.meshgrid(np.arange(128), np.arange(128), indexing="ij")
    Lf = (s_idx <= j_idx).astype(np.float32); Lb = (s_idx >= j_idx).astype(np.float32)
    cst = np.stack([Lf, Lb, np.ones((128, 128), np.float32), np.eye(128, dtype=np.float32),
                    np.where(s_idx <= j_idx, 0.0, MASK_NEG).astype(np.float32),
                    np.where(s_idx >= j_idx, 0.0, MASK_NEG).astype(np.float32)], 1)
    in_maps = []
    for i in range(NCORES):
        b, h = i // 4, i % 4
        seq = np.concatenate([P_ctx[b], P_lat[b]], 0)
        qs, ks = slice(h * 64, (h + 1) * 64), slice(256 + h * 64, 256 + (h + 1) * 64)
        tm = lambda a: np.ascontiguousarray(a.reshape(NCH, 128, -1).transpose(1, 0, 2))
        gcols = [1024 + 0 * 8 + 0 * 4 + h, 1024 + 0 * 8 + 1 * 4 + h, 1024 + 1 * 8 + 0 * 4 + h, 1024 + 1 * 8 + 1 * 4 + h]
        gb = np.array([gate_b[0, 0, h], gate_b[0, 1, h], gate_b[1, 0, h], gate_b[1, 1, h]], np.float32)
        cw = np.concatenate([conv_w[:, qs].T, conv_b[qs][:, None], conv_w[:, ks].T, conv_b[ks][:, None]], 1).astype(np.float32)
        in_maps.append({"qpT": np.ascontiguousarray(seq[:, qs].T), "kpT": np.ascontiguousarray(seq[:, ks].T),
                        "v": tm(seq[:, 512 + h * 64:512 + (h + 1) * 64]), "og": tm(seq[:, 768 + h * 64:768 + (h + 1) * 64]),
                        "g4": tm(seq[:, gcols]), "gb": np.ascontiguousarray(np.tile(gb, NCH)), "cw": np.ascontiguousarray(cw),
                        "nw": np.ascontiguousarray(norm_w[h * 64:(h + 1) * 64]), "cst": np.ascontiguousarray(cst)})
    res = _run(nc, in_maps)
    ym_l = np.zeros((B, n, 256), np.float32); ym_c = np.zeros((B, nctx, 256), np.float32)
    for i in range(NCORES):
        b, h = i // 4, i % 4
        y = res.results[i]["o_ym"].transpose(1, 0, 2).reshape(NSEQ, 64)
        ym_c[b, :, h * 64:(h + 1) * 64] = y[:nctx]
        ym_l[b, :, h * 64:(h + 1) * 64] = y[nctx:]
    return ym_l, ym_c


HCH = 32
TWO_PI = 2.0 * np.pi
RND_MAGIC = 12582912.0


def fft_tables(N1):
    N2 = 128
    N = N1 * N2
    ar = np.arange
    c, s = np.cos, np.sin
    th = TWO_PI * ar(N1)[:, None] * ar(N1)[None] / N1
    F1c = np.concatenate([c(th), -s(th)], 1)
    th = TWO_PI * ar(N2)[:, None] * ar(N1)[None] / N
    twRR = np.concatenate([c(th), c(th)], 1); twII = np.concatenate([-s(th), -s(th)], 1)
    th = TWO_PI * ar(N2)[:, None] * ar(N2)[None] / N2
    F2re, F2im, nF2im = c(th), -s(th), s(th)
    G2c = np.concatenate([c(th), s(th)], 1); G2s = np.concatenate([-s(th), c(th)], 1)
    th = TWO_PI * ar(N1)[:, None] * ar(N2)[None] / N
    twcRR = np.concatenate([c(th), c(th)], 1); twcII = np.concatenate([s(th), s(th)], 1)
    th = TWO_PI * ar(N1)[:, None] * ar(N1 // 2)[None] / N1
    G1re, nG1im = c(th) / N, -s(th) / N
    f = lambda a: np.ascontiguousarray(a.astype(np.float32))
    return dict(F1c=f(F1c), twRR=f(twRR), twII=f(twII), F2re=f(F2re), F2im=f(F2im), nF2im=f(nF2im), G2c=f(G2c), G2s=f(G2s),
                twcRR=f(twcRR), twcII=f(twcII), G1re=f(G1re), nG1im=f(nG1im))


TAB_ORDER = ["F1c", "twRR", "twII", "F2re", "F2im", "nF2im", "G2c", "G2s", "twcRR", "twcII", "G1re", "nG1im"]


def hyena_consts(n):
    N = 2 * n
    tau = np.arange(N)
    pos = np.where(tau < n, tau, N - tau).astype(np.float32)
    t = (pos / np.float32(n)).astype(np.float32)
    bands = np.arange(1, 17, dtype=np.float32)
    ang = (np.float32(TWO_PI) * t[:, None] * bands).astype(np.float32)
    feats = np.concatenate([t[:, None], np.cos(ang), np.sin(ang)], -1).astype(np.float32)
    lt = abs(np.log(1e-2))
    deltas = np.linspace(lt / 1.5, lt / 0.3, 256, dtype=np.float32)
    win = (np.exp(-t[:, None] * deltas) + np.float32(0.05)).astype(np.float32)
    win[n] = 0.0
    return np.ascontiguousarray(feats.T), np.ascontiguousarray(win.T)


def interleave(gens, width):
    it = iter(gens)
    active = []
    while True:
        while len(active) < width:
            g = next(it, None)
            if g is None:
                break
            active.append(g)
        if not active:
            return
        for g in list(active):
            try:
                next(g)
            except StopIteration:
                active.remove(g)
        yield


def build_k4(sizes):
    nc = bass.Bass("TRN2", target_bir_lowering=False)
    B = 2
    dr = {}
    for si, n in enumerate(sizes):
        N1 = 2 * n // 128
        dr[si] = dict(
            u=nc.dram_tensor(f"u{si}", [3, HCH, B, n + 2], F32, kind="ExternalInput").ap(),
            feats=nc.dram_tensor(f"feats{si}", [33, 2 * n], F32, kind="ExternalInput").ap(),
            win=nc.dram_tensor(f"win{si}", [64, 2 * n], F32, kind="ExternalInput").ap(),
            taps=nc.dram_tensor(f"taps{si}", [64, 2 * n], F32, kind="ExternalOutput").ap(),
            out=nc.dram_tensor(f"o_yh{si}", [HCH, B, n], F32, kind="ExternalOutput").ap(),
            tabs={k: nc.dram_tensor(f"t{si}_{k}", list(v.shape), F32, kind="ExternalInput").ap()
                  for k, v in fft_tables(N1).items()})
    mlpd = nc.dram_tensor("mlp", [64, 64 + 64 + 128 + 4], F32, kind="ExternalInput").ap()
    cwd = nc.dram_tensor("cwv", [3 * HCH * 4], F32, kind="ExternalInput").ap()
    skd = nc.dram_tensor("skv", [2 * HCH], F32, kind="ExternalInput").ap()
    cstd = nc.dram_tensor("cst", [128, 256], F32, kind="ExternalInput").ap()
    with ExitStack() as es:
        P = Prog(nc, es)
        sb, ps, ring = mk_alloc(nc, es)
        banks = [(ps(f"bank{i}", [128, 512], F32), f"bank{i}") for i in range(8)]
        cst = sb("cst_s", [128, 256]); P.dma("sync", cst[:], cstd, writes=["cst"])
        ident, ones = cst[:, 0:128], cst[:, 128:256]
        mlp = sb("mlp_s", [64, 260]); P.dma("sync", mlp[:], mlpd, writes=["mlp"])
        w1, w2 = mlp[0:33, 0:64], mlp[:, 64:128]
        w3 = [mlp[:, 128:192], mlp[:, 192:256]]
        b1, f0, b2, f1 = (mlp[:, 256 + j:257 + j] for j in range(4))
        cwb = sb("cwb", [128, 3 * HCH * 4]); P.dma("sync", cwb[:], bcast_rows(cwd, 128), writes=["cwb"])
        skb = sb("skb", [128, 2 * HCH]); P.dma("sync", skb[:], bcast_rows(skd, 128), writes=["skb"])
        a1r = ring("a1", 4, [64, 512]); rrr = ring("rr", 4, [64, 512]); hhr = ring("hh", 4, [64, 512])
        ftr = ring("ft", 4, [33, 512]); wnr = ring("wn", 4, [64, 512]); tpr = ring("tp", 4, [64, 512])
        As_r = ring("As", 4, [128, 256]); t1r = ring("t1", 4, [128, 256]); t2r = ring("t2", 4, [128, 256])
        Br = ring("Bc", 4, [128, 256]); Yr = ring("Yc", 4, [128, 256]); Dr = ring("Dc", 4, [128, 256])
        pr4 = [ring(f"pp{j}", 4, [128, 128]) for j in range(4)]
        xir = ring("xi", 4, [64, 3, 130]); cvr = ring("cv", 4, [64, 3, 128]); tgr = ring("tg", 4, [64, 128])
        z1r = ring("z1", 4, [64, 128]); z2r = ring("z2", 4, [64, 128]); tlr = ring("tl", 4, [128, 128])
        bkP = Ring(banks[0:8])
        bkA = bkX = bkC = bkY = bkP

        def sin_layer(psrc, pk, n_, bias, freq, dst, dstk):
            a1, a1k = a1r.next(); rr, rrk = rrr.next()
            P.ts("vector", a1[:, 0:n_], psrc, bias, freq, ALU.add, ALU.mult, reads=[pk, "mlp"], writes=[a1k])
            yield
            P.ts("vector", rr[:, 0:n_], a1[:, 0:n_], 1.0 / TWO_PI, RND_MAGIC, ALU.mult, ALU.add, reads=[a1k], writes=[rrk])
            yield
            P.ts("vector", rr[:, 0:n_], rr[:, 0:n_], RND_MAGIC, -TWO_PI, ALU.subtract, ALU.mult, reads=[rrk], writes=[rrk])
            yield
            P.tt("vector", rr[:, 0:n_], rr[:, 0:n_], a1[:, 0:n_], ALU.add, reads=[rrk, a1k], writes=[rrk])
            yield
            P.ts("vector", rr[:, 0:n_], rr[:, 0:n_], np.pi, -np.pi, ALU.min, ALU.max, reads=[rrk], writes=[rrk])
            yield
            P.act(dst, rr[:, 0:n_], AF.Sin, reads=[rrk], writes=[dstk])
            yield

        def cmul(src, srck, n1p, W, tRR, tII, tk, dst, dstk):
            t1, t1k = t1r.next(); t2, t2k = t2r.next()
            P.tt("vector", t1[0:n1p, 0:2 * W], src, tRR, ALU.mult, reads=[srck, tk], writes=[t1k])
            yield
            P.tt("gpsimd", t2[0:n1p, 0:2 * W], src, tII, ALU.mult, reads=[srck, tk], writes=[t2k])
            yield
            P.tt("vector", dst[0:n1p, 0:W], t1[0:n1p, 0:W], t2[0:n1p, W:2 * W], ALU.subtract, reads=[t1k, t2k], writes=[dstk + "r"])
            yield
            P.tt("gpsimd", dst[0:n1p, W:2 * W], t2[0:n1p, 0:W], t1[0:n1p, W:2 * W], ALU.add, reads=[t1k, t2k], writes=[dstk + "i"])
            yield

        def size_body(si, n):
            N = 2 * n
            N1 = N // 128
            Kd = N1 // 2
            d = dr[si]
            T = {}
            for k in TAB_ORDER:
                shp = list(d["tabs"][k].shape)
                T[k] = sb(f"T{si}_{k}", shp)
                P.dma("sync", T[k][:], d["tabs"][k], writes=[f"tab{si}"] if k == TAB_ORDER[-1] else [f"tab{si}_{k}"])
                yield
            tabk = [f"tab{si}"] + [f"tab{si}_{k}" for k in TAB_ORDER[:-1]]
            CH = min(512, n)
            nchunk = N // CH
            l1p = sb(f"l1p{si}", [64, nchunk])
            def mlp_chain(ci):
                c0 = ci * CH
                dirn = 0 if c0 < n else 1
                ft, ftk = ftr.next(); P.dma("sync", ft[:, 0:CH], d["feats"][:, c0:c0 + CH], writes=[ftk])
                wn, wnk = wnr.next(); P.dma("sync", wn[:, 0:CH], d["win"][:, c0:c0 + CH], writes=[wnk])
                bk, bkk = bkA.next()
                P.mm(bk[0:64, 0:CH], w1, ft[:, 0:CH], True, True, reads=[ftk, "mlp"], writes=[bkk])
                yield
                h1, h1k = hhr.next()
                yield from sin_layer(bk[0:64, 0:CH], bkk, CH, b1, f0, h1[:, 0:CH], h1k)
                bk, bkk = bkX.next()
                P.mm(bk[0:64, 0:CH], w2, h1[:, 0:CH], True, True, reads=[h1k, "mlp"], writes=[bkk])
                yield
                h2, h2k = hhr.next()
                yield from sin_layer(bk[0:64, 0:CH], bkk, CH, b2, f1, h2[:, 0:CH], h2k)
                bk, bkk = bkC.next()
                P.mm(bk[0:64, 0:CH], w3[dirn], h2[:, 0:CH], True, True, reads=[h2k, "mlp"], writes=[bkk])
                yield
                tp, tpk = tpr.next()
                P.tt("vector", tp[:, 0:CH], bk[0:64, 0:CH], wn[:, 0:CH], ALU.mult, reads=[bkk, wnk], writes=[tpk])
                yield
                P.gen("vector", lambda e, tp=tp, ci=ci, CH=CH, l1p=l1p: e.reduce_sum(out=l1p[:, ci:ci + 1], in_=tp[:, 0:CH], axis=AX.X,
                                                                         apply_absolute_value=True), reads=[tpk], writes=[f"l1p{si}_{ci}"])
                yield
                P.dma("sync", d["taps"][:, c0:c0 + CH], tp[:, 0:CH], reads=[tpk], writes=[f"taps{si}"], final=True)
                yield
            yield from interleave([mlp_chain(ci) for ci in range(nchunk)], 2)
            l1 = sb(f"l1_{si}", [64, 2])
            P.gen("vector", lambda e, l1=l1, l1p=l1p: e.reduce_sum(out=l1[:, 0:1], in_=l1p[:], axis=AX.X),
                  reads=[f"l1p{si}_{ci}" for ci in range(nchunk)], writes=[f"l1{si}"])
            yield
            P.gen("vector", lambda e, l1=l1: e.reciprocal(out=l1[:, 1:2], in_=l1[:, 0:1]), reads=[f"l1{si}"], writes=[f"l1{si}"])
            yield
            dg = sb(f"dg{si}", [64, 64])
            P.ts("vector", dg[:], ident[0:64, 0:64], l1[:, 1:2], None, ALU.mult, reads=["cst", f"l1{si}"], writes=[f"dg{si}"])
            yield
            bk, bkk = bkY.next()
            P.mm(bk[:, 0:64], ones[0:64, :], dg[:], True, True, reads=["cst", f"dg{si}"], writes=[bkk])
            yield
            rl1b = sb(f"rl1b{si}", [128, 64])
            P.copy("vector", rl1b[:], bk[:, 0:64], reads=[bkk], writes=[f"rl1b{si}"])
            yield

            def fwd_fft(xt, xk, Krows):
                bA, bAk = bkA.next()
                P.mm(bA[:, 0:2 * N1], xt, T["F1c"][0:Krows, :], True, True, reads=[xk] + tabk, writes=[bAk])
                yield
                As, Ask = As_r.next()
                P.copy("scalar", As[:, 0:2 * N1], bA[:, 0:2 * N1], reads=[bAk], writes=[Ask])
                yield
                Bc, Bck = Br.next()
                yield from cmul(As[:, 0:2 * N1], Ask, 128, N1, T["twRR"][:], T["twII"][:], tabk[0], Bc, Bck)
                bX, bXk = bkX.next()
                Bre, Bim = Bc[:, 0:N1], Bc[:, N1:2 * N1]
                P.mm(bX[:, 0:N1], T["F2re"][:], Bre, True, False, reads=[Bck + "r"] + tabk, writes=[bXk])
                P.mm(bX[:, 0:N1], T["nF2im"][:], Bim, False, True, reads=[Bck + "i"] + tabk, writes=[bXk])
                yield
                P.mm(bX[:, N1:2 * N1], T["F2re"][:], Bim, True, False, reads=[Bck + "i"] + tabk, writes=[bXk])
                P.mm(bX[:, N1:2 * N1], T["F2im"][:], Bre, False, True, reads=[Bck + "r"] + tabk, writes=[bXk])
                yield
                return bX, bXk

            H = sb(f"H{si}", [128, 64, 2 * N1])
            def filt_chain(oc):
                tl, tlk = tlr.next()
                P.dma("sync", tl[0:N1, :], d["taps"][oc].rearrange("(a b) -> a b", b=128), reads=[f"taps{si}"], writes=[tlk])
                yield
                bX, bXk = yield from fwd_fft(tl[0:N1, :], tlk, N1)
                P.ts("vector", H[:, oc, :], bX[:, 0:2 * N1], rl1b[:, oc:oc + 1], None, ALU.mult, reads=[bXk, f"rl1b{si}"], writes=[f"H{si}_{oc}"])
                yield

            yield from interleave([filt_chain(oc) for oc in range(64)], 4)
            def long_conv(zt, zk, o, c):
                bX, bXk = yield from fwd_fft(zt, zk, Kd)
                oc = o * HCH + c
                Hre, Him = H[:, oc, 0:N1], H[:, oc, N1:2 * N1]
                hk = f"H{si}_{oc}"
                pp = [r.next() for r in pr4]
                P.tt("vector", pp[0][0][:, 0:N1], bX[:, 0:N1], Hre, ALU.mult, reads=[bXk, hk], writes=[pp[0][1]])
                yield
                P.tt("vector", pp[1][0][:, 0:N1], bX[:, N1:2 * N1], Him, ALU.mult, reads=[bXk, hk], writes=[pp[1][1]])
                yield
                P.tt("vector", pp[2][0][:, 0:N1], bX[:, 0:N1], Him, ALU.mult, reads=[bXk, hk], writes=[pp[2][1]])
                yield
                P.tt("vector", pp[3][0][:, 0:N1], bX[:, N1:2 * N1], Hre, ALU.mult, reads=[bXk, hk], writes=[pp[3][1]])
                yield
                Yc, Yck = Yr.next()
                P.tt("gpsimd", Yc[:, 0:N1], pp[0][0][:, 0:N1], pp[1][0][:, 0:N1], ALU.subtract, reads=[pp[0][1], pp[1][1]], writes=[Yck + "r"])
                yield
                P.tt("gpsimd", Yc[:, N1:2 * N1], pp[2][0][:, 0:N1], pp[3][0][:, 0:N1], ALU.add, reads=[pp[2][1], pp[3][1]], writes=[Yck + "i"])
                yield
                bC, bCk = bkC.next()
                P.mm(bC[0:N1, 0:256], Yc[:, 0:N1], T["G2c"][:], True, False, reads=[Yck + "r"] + tabk, writes=[bCk])
                P.mm(bC[0:N1, 0:256], Yc[:, N1:2 * N1], T["G2s"][:], False, True, reads=[Yck + "i"] + tabk, writes=[bCk])
                yield
                Cs, Csk = As_r.next()
                P.copy("scalar", Cs[0:N1, :], bC[0:N1, 0:256], reads=[bCk], writes=[Csk])
                yield
                Dc, Dck = Dr.next()
                yield from cmul(Cs[0:N1, :], Csk, N1, 128, T["twcRR"][:], T["twcII"][:], tabk[0], Dc, Dck)
                bY, bYk = bkY.next()
                P.mm(bY[0:Kd, 0:128], T["G1re"][:], Dc[0:N1, 0:128], True, False, reads=[Dck + "r"] + tabk, writes=[bYk])
                P.mm(bY[0:Kd, 0:128], T["nG1im"][:], Dc[0:N1, 128:256], False, True, reads=[Dck + "i"] + tabk, writes=[bYk])
                yield
                return bY, bYk

            def data_chain(c, b):
                xi, xik = xir.next()
                src = bass.AP(tensor=d["u"].tensor, offset=d["u"][0, c, b, 0].offset,
                              ap=[[128, Kd], [HCH * B * (n + 2), 3], [1, 130]])
                P.dma("gpsimd", xi[0:Kd], src, writes=[xik])
                yield
                cv, cvk = cvr.next()
                for p in range(3):
                    wo = (p * HCH + c) * 4
                    P.ts("vector", cv[0:Kd, p, :], xi[0:Kd, p, 1:129], cwb[0:Kd, wo + 1:wo + 2], cwb[0:Kd, wo + 3:wo + 4], ALU.mult, ALU.add,
                         reads=[xik, "cwb"], writes=[cvk + str(p)])
                    yield
                    P.stt("vector", cv[0:Kd, p, :], xi[0:Kd, p, 0:128], cwb[0:Kd, wo:wo + 1], cv[0:Kd, p, :], ALU.mult, ALU.add,
                          reads=[xik, "cwb", cvk + str(p)], writes=[cvk + str(p)])
                    yield
                    P.stt("vector", cv[0:Kd, p, :], xi[0:Kd, p, 2:130], cwb[0:Kd, wo + 2:wo + 3], cv[0:Kd, p, :], ALU.mult, ALU.add,
                          reads=[xik, "cwb", cvk + str(p)], writes=[cvk + str(p)])
                    yield
                bY, bYk = yield from long_conv(cv[0:Kd, 0, :], cvk + "0", 0, c)
                tg, tgk = tgr.next()
                P.stt("vector", tg[0:Kd, :], cv[0:Kd, 0, :], skb[0:Kd, c:c + 1], bY[0:Kd, 0:128], ALU.mult, ALU.add,
                      reads=[cvk + "0", "skb", bYk], writes=[tgk])
                yield
                z1, z1k = z1r.next()
                P.tt("gpsimd", z1[0:Kd, :], tg[0:Kd, :], cv[0:Kd, 1, :], ALU.mult, reads=[tgk, cvk + "1"], writes=[z1k])
                yield
                bY, bYk = yield from long_conv(z1[0:Kd, :], z1k, 1, c)
                tg, tgk = tgr.next()
                P.stt("vector", tg[0:Kd, :], z1[0:Kd, :], skb[0:Kd, HCH + c:HCH + c + 1], bY[0:Kd, 0:128], ALU.mult, ALU.add,
                      reads=[z1k, "skb", bYk], writes=[tgk])
                yield
                z2, z2k = z2r.next()
                P.tt("gpsimd", z2[0:Kd, :], tg[0:Kd, :], cv[0:Kd, 2, :], ALU.mult, reads=[tgk, cvk + "2"], writes=[z2k])
                yield
                P.dma("sync", d["out"][c, b].rearrange("(a b) -> a b", b=128), z2[0:Kd, :], reads=[z2k], writes=[z2k + "d"], final=True)
                yield
            yield from interleave([data_chain(c, b) for c in range(HCH) for b in range(B)], 4)

        for si, n in enumerate(sizes):
            for _ in size_body(si, n):
                pass
        P.emit()
    return nc


def run_k4(hy_list, conv_w, conv_b, fparams, skip):
    f_w1, f_b1, f_freq, f_w2, f_b2, f_w3 = fparams
    sizes = [h.shape[1] for h in hy_list]
    nc = build_k4(sizes)
    B = 2
    cst = np.concatenate([np.eye(128, dtype=np.float32), np.ones((128, 128), np.float32)], 1)
    consts = [hyena_consts(n) for n in sizes]
    tabs = [fft_tables(2 * n // 128) for n in sizes]
    w3r = f_w3.reshape(64, 2, 2, 256)
    in_maps = []
    for i in range(NCORES):
        cs = slice(i * HCH, (i + 1) * HCH)
        m = {"cst": cst}
        mlp = np.zeros((64, 260), np.float32)
        mlp[0:33, 0:64] = f_w1
        mlp[:, 64:128] = f_w2
        mlp[:, 128:192] = w3r[:, 0, :, cs].reshape(64, 64)
        mlp[:, 192:256] = w3r[:, 1, :, cs].reshape(64, 64)
        mlp[:, 256] = f_b1; mlp[:, 257] = f_freq[0]; mlp[:, 258] = f_b2; mlp[:, 259] = f_freq[1]
        m["mlp"] = mlp
        cw = np.zeros((3, HCH, 4), np.float32)
        for p in range(3):
            ch = slice(p * 256 + i * HCH, p * 256 + (i + 1) * HCH)
            cw[p, :, 0:3] = conv_w[:, ch].T
            cw[p, :, 3] = conv_b[ch]
        m["cwv"] = cw.reshape(-1)
        m["skv"] = np.ascontiguousarray(skip[:, cs]).reshape(-1)
        for si, (hy, n) in enumerate(zip(hy_list, sizes)):
            u = np.zeros((3, HCH, B, n + 2), np.float32)
            for p in range(3):
                u[p, :, :, 1:n + 1] = hy[:, :, p * 256 + i * HCH:p * 256 + (i + 1) * HCH].transpose(2, 0, 1)
            m[f"u{si}"] = u
            feats, win = consts[si]
            m[f"feats{si}"] = feats
            m[f"win{si}"] = np.ascontiguousarray(np.tile(win[cs], (2, 1)))
            for k, v in tabs[si].items():
                m[f"t{si}_{k}"] = v
        in_maps.append(m)
    res = _run(nc, in_maps)
    outs = []
    for si, n in enumerate(sizes):
        y = np.zeros((B, n, 256), np.float32)
        for i in range(NCORES):
            y[:, :, i * HCH:(i + 1) * HCH] = res.results[i][f"o_yh{si}"].transpose(1, 2, 0)
        outs.append(y)
    return outs, [np.concatenate([res.results[i][f"taps{si}"] for i in range(NCORES)], 0) for si in range(len(sizes))]


def kernel(x, c, ctx, c_ctx, w_mod, b_mod, w_in, mlstm_conv_w, mlstm_conv_b, mlstm_gate_b,
           mlstm_norm_w, attn_q_norm_w, attn_k_norm_w, hyena_conv_w, hyena_conv_b,
           hyena_f_w1, hyena_f_b1, hyena_f_freq, hyena_f_w2, hyena_f_b2, hyena_f_w3,
           hyena_skip, w_out, ln_mix_w, ln_mix_b, router_w, router_b,
           exp_w_gate, exp_w_up, exp_w_down, ln_ffn_w, ln_ffn_b):
    f = lambda a: np.asarray(a, dtype=np.float32)
    x, c, ctx, c_ctx = f(x), f(c), f(ctx), f(c_ctx)
    mod = run_k0(c, c_ctx, f(w_mod), f(b_mod))
    depth = w_in.shape[0]
    for l in range(depth):
        last = l == depth - 1
        P_lat, P_ctx = run_k1(x, ctx, mod[l], f(w_in[l]))
        ym_l, ym_c = run_k2(P_lat, P_ctx, f(mlstm_conv_w[l]), f(mlstm_conv_b[l]), f(mlstm_gate_b[l]), f(mlstm_norm_w[l]))
        ya_l, ya_c = run_k3(P_lat, P_ctx, f(attn_q_norm_w[l]), f(attn_k_norm_w[l]))
        hy = [P_lat[..., 1808:]] + ([] if last else [P_ctx[..., 1808:]])
        fpar = (f(hyena_f_w1[l]), f(hyena_f_b1[l]), f(hyena_f_freq[l]), f(hyena_f_w2[l]), f(hyena_f_b2[l]), f(hyena_f_w3[l]))
        yhs, _ = run_k4(hy, f(hyena_conv_w[l]), f(hyena_conv_b[l]), fpar, f(hyena_skip[l]))
        yh_c = np.zeros_like(ym_c) if last else yhs[1]
        ycat_l = np.concatenate([ym_l, ya_l, yhs[0]], -1)
        ycat_c = np.concatenate([ym_c, ya_c, yh_c], -1)
        x1, c1, h2, h2c, aff, affc = run_k5(ycat_l, ycat_c, x, ctx, mod[l], f(w_out[l]), f(ln_mix_w[l]), f(ln_mix_b[l]),
                                            f(router_w[l]), f(router_b[l]))
        thr, thrc = run_k6(aff, affc)
        x, ctx = run_moe(h2, h2c, aff, affc, thr, thrc, x1, c1, mod[l], f(ln_ffn_w[l]), f(ln_ffn_b[l]),
                         f(exp_w_gate[l]), f(exp_w_up[l]), f(exp_w_down[l]))
    return x.astype(np.float32)


CAP_L, CAP_C = 1024, 32
SLOTS_B = CAP_L + CAP_C
NTOK_ALL = 2 * 8192 + 2 * 256


def build_k7g():
    nc = bass.Bass("TRN2", target_bir_lowering=False)
    T = 18
    TOK = T * 128
    h2d = nc.dram_tensor("h2all", [NTOK_ALL, 1024], BF16, kind="ExternalInput").ap()
    affLd = nc.dram_tensor("affL", [2, 2, 128, 66], F32, kind="ExternalInput").ap()
    thrd = nc.dram_tensor("thr8", [8], F32, kind="ExternalInput").ap()
    tidd = nc.dram_tensor("tid", [2, 128, 66, 2], F32, kind="ExternalInput").ap()
    iotad = nc.dram_tensor("iota", [128, CAP_L + 128], F32, kind="ExternalInput").ap()
    cstd = nc.dram_tensor("cst", [128, 448], F32, kind="ExternalInput").ap()
    wgd = nc.dram_tensor("wg", [2, 128, 8, D_FF], F32, kind="ExternalInput").ap()
    wud = nc.dram_tensor("wu", [2, 128, 8, D_FF], F32, kind="ExternalInput").ap()
    wdd = nc.dram_tensor("wd", [2, 128, NFC, 1024], F32, kind="ExternalInput").ap()
    Yo = nc.dram_tensor("o_Y", [2, TOK, 1024], BF16, kind="ExternalOutput").ap()
    posKo = nc.dram_tensor("o_pos", [2, 2, 128, 66], I32, kind="ExternalOutput").ap()
    with ExitStack() as es:
        P = Prog(nc, es)
        sb, ps, ring = mk_alloc(nc, es)
        banks = [(ps(f"bk{i}", [128, 512], F32), f"bk{i}") for i in range(8)]
        cst = sb("cst_s", [128, 448]); P.dma("sync", cst[:], cstd, writes=["cst"])
        Ust, ones, identf, Ust64 = cst[:, 0:128], cst[:, 128:256], cst[:, 256:384], cst[0:64, 384:448]
        idb = sb("idb", [128, 128], BF16); P.copy("vector", idb[:], identf, reads=["cst"], writes=["idb"])
        thrt = sb("thrt", [128, 8]); P.dma("sync", thrt[:], bcast_rows(thrd, 128), writes=["thrt"])
        iota = sb("iota_s", [128, CAP_L + 128]); P.dma("sync", iota[:], iotad, writes=["iota"])
        h2T = sb("h2T_s", [128, 8, TOK], BF16)
        acc = sb("acc", [128, T, 1024])
        tv = sb("tv", [128, T])
        Ar = ring("A", 2, [128, 66]); Mr = ring("M", 2, [128, 66]); wir = ring("wi", 2, [128, 66]); pfr = ring("pf", 2, [128, 66])
        m2r = ring("m2", 2, [128, 66]); pir = ring("pi", 2, [128, 66], I32)
        tdfr = ring("tdf", 2, [128, 66, 2]); TAr = ring("TA", 2, [128, 66, 5], BF16); spr = ring("sp", 2, [128, 2, 66]); l5r = ring("l5", 2, [128, 9, 5]); selr = ring("sel", 4, [128, 512], BF16); rowt = sb("rowt", [5, SLOTS_B]); lfr = ring("lf", 2, [128, 18])
        tcr = ring("tc", 2, [64, 1]); tbr = ring("tb", 2, [64, 128])
        lsr = ring("ls", 2, [128, 9], I32)
        xsr = ring("xs", 2, [128, 1024], BF16)
        FG = 3
        wgr = ring("wgb", 2, [128, 8, FG * 128], BF16); wur = ring("wub", 2, [128, 8, FG * 128], BF16)
        wdr = ring("wdb", 2, [128, FG, 1024], BF16); stg = ring("stg", 3, [128, 1024])
        actr = ring("actT", 2, [128, FG, 512], BF16); sgr = ring("sg", 2, [128, 512]); yor = ring("yrow", 2, [128, 1024], BF16)
        pgr = Ring(banks[0:2]); pur = Ring(banks[2:4]); pyr = Ring(banks[4:6])
        tgs = [(s, min(512, TOK - s)) for s in range(0, TOK, 512)]
        for e in range(2):
            P.memset("gpsimd", h2T[:], 0.0, writes=["h2T"])
            P.memset("vector", tv[:], 0.0, writes=["tv"])
            for t in range(T):
                P.memset("gpsimd", acc[:, t, :], 0.0, writes=[f"acc{t}a", f"acc{t}b"])
            for b in range(2):
                A, Ak = Ar.next(); P.dma("sync", A[:], affLd[e, b], writes=[Ak])
                M, Mk = Mr.next()
                to = (e * 2 + b) * 2
                P.ts("vector", M[:, 0:64], A[:, 0:64], thrt[:, to:to + 1], None, ALU.is_ge, reads=[Ak, "thrt"], writes=[Mk + "l"])
                P.ts("vector", M[:, 64:66], A[:, 64:66], thrt[:, to + 1:to + 2], None, ALU.is_ge, reads=[Ak, "thrt"], writes=[Mk + "c"])
                wi, wik = wir.next(); pf, pfk = pfr.next(); m2, m2k = m2r.next(); pi, pik = pir.next()
                for (c0, ncol, cap, base, sfx) in ((0, 64, CAP_L, 0, "l"), (64, 2, CAP_C, CAP_L, "c")):
                    bw, bwk = banks[6]
                    P.mm(bw[:, c0:c0 + ncol], Ust, M[:, c0:c0 + ncol], True, True, reads=["cst", Mk + sfx], writes=[bwk + sfx])
                    P.copy("scalar", wi[:, c0:c0 + ncol], bw[:, c0:c0 + ncol], reads=[bwk + sfx], writes=[wik + sfx])
                    bt, btk = banks[7]
                    P.mm(bt[0:ncol, c0:c0 + 1], M[:, c0:c0 + ncol], ones[:, 0:1], True, True, reads=["cst", Mk + sfx], writes=[btk + "t" + sfx])
                    tc, tck = tcr.next()
                    P.copy("vector", tc[0:ncol, :], bt[0:ncol, c0:c0 + 1], reads=[btk + "t" + sfx], writes=[tck])
                    tb, tbk = tbr.next()
                    P.ts("vector", tb[0:ncol, :], ones[0:ncol, :], tc[0:ncol, 0:1], None, ALU.mult, reads=["cst", tck], writes=[tbk])
                    P.mm(bt[:, 128 + c0:128 + c0 + ncol], tb[0:ncol, :], Ust64[0:ncol, 0:ncol], True, True, reads=[tbk, "cst"], writes=[btk + "o" + sfx])
                    sl = slice(c0, c0 + ncol)
                    P.tt("vector", pf[:, sl], bt[:, 128 + c0:128 + c0 + ncol], wi[:, sl], ALU.add, reads=[btk + "o" + sfx, wik + sfx], writes=[pfk + sfx])
                    P.ts("vector", m2[:, sl], pf[:, sl], float(cap) - 0.5, None, ALU.is_lt, reads=[pfk + sfx], writes=[m2k + sfx])
                    P.tt("vector", m2[:, sl], m2[:, sl], M[:, sl], ALU.mult, reads=[m2k + sfx, Mk + sfx], writes=[m2k + sfx])
                    P.ts("vector", pf[:, sl], pf[:, sl], float(base - SLOTS_B), None, ALU.add, reads=[pfk + sfx], writes=[pfk + sfx])
                    P.tt("vector", pf[:, sl], pf[:, sl], m2[:, sl], ALU.mult, reads=[pfk + sfx, m2k + sfx], writes=[pfk + sfx])
                    P.ts("vector", pf[:, sl], pf[:, sl], float(SLOTS_B), None, ALU.add, reads=[pfk + sfx], writes=[pfk + sfx])
                P.copy("vector", pi[:], pf[:], reads=[pfk + "l", pfk + "c"], writes=[pik])
                P.dma("sync", posKo[e, b], pi[:], reads=[pik], writes=[pik + "d"], final=True)
                TA, TAk = TAr.next()
                tdf, tdfk = tdfr.next()
                P.dma("sync", tdf[:], tidd[b], writes=[tdfk])
                P.copy("gpsimd", TA[:, :, 0:2], tdf[:], reads=[tdfk], writes=[TAk + "t"])
                sp, spk = spr.next()
                P.copy("vector", TA[:, :, 2], A[:], reads=[Ak], writes=[TAk + "a0"])
                P.copy("vector", sp[:, 0, :], TA[:, :, 2], reads=[TAk + "a0"], writes=[spk + "0"])
                P.tt("vector", sp[:, 1, :], A[:], sp[:, 0, :], ALU.subtract, reads=[Ak, spk + "0"], writes=[spk + "1"])
                P.copy("vector", TA[:, :, 3], sp[:, 1, :], reads=[spk + "1"], writes=[TAk + "a1"])
                P.copy("vector", sp[:, 0, :], TA[:, :, 3], reads=[TAk + "a1"], writes=[spk + "0"])
                P.tt("vector", sp[:, 1, :], sp[:, 1, :], sp[:, 0, :], ALU.subtract, reads=[spk + "1", spk + "0"], writes=[spk + "1"])
                P.copy("vector", TA[:, :, 4], sp[:, 1, :], reads=[spk + "1"], writes=[TAk + "a2"])
                tak = [TAk + "t", TAk + "a0", TAk + "a1", TAk + "a2"]
                pc, pck = banks[7]
                for piece, (s0_, ns_, js) in enumerate(((0, 512, range(64)), (512, 512, range(64)), (CAP_L, CAP_C, (64, 65)))):
                    sfx = "l" if piece < 2 else "c"
                    for jj, j in enumerate(js):
                        se, sek = selr.next()
                        P.ts("vector", se[:, 0:ns_], iota[:, s0_:s0_ + ns_], pf[:, j:j + 1], None, ALU.is_equal, reads=["iota", pfk + sfx], writes=[sek])
                        P.mm(pc[0:5, 0:ns_], TA[:, j, :], se[:, 0:ns_], jj == 0, jj == len(js) - 1, reads=[sek] + tak, writes=[pck + "row"])
                    P.copy("scalar", rowt[0:5, s0_:s0_ + ns_], pc[0:5, 0:ns_], reads=[pck + "row"], writes=[f"rowt{piece}"])
                pq, pqk = banks[6]
                for c in range(9):
                    ns_ = 128 if c < 8 else 32
                    P.tr(pq[0:ns_, 256 + 5 * c:256 + 5 * c + 5], rowt[0:5, c * 128:c * 128 + ns_], identf[0:5, 0:5],
                         reads=[f"rowt{c // 4}", "cst"], writes=[pqk + "q"])
                l5, l5k = l5r.next()
                P.copy("vector", l5[:].rearrange("p c t -> p (c t)"), pq[:, 256:301], reads=[pqk + "q"], writes=[l5k])
                lf, lfk = lfr.next()
                lf3 = lf[:].rearrange("p (c t) -> p c t", t=2)
                P.stt("vector", lf3[:, :, 0], l5[:, :, 0], 128.0, l5[:, :, 1], ALU.mult, ALU.add, reads=[l5k], writes=[lfk + "i"])
                P.tt("vector", lf3[:, :, 1], l5[:, :, 2], l5[:, :, 3], ALU.add, reads=[l5k], writes=[lfk + "v"])
                P.tt("vector", lf3[:, :, 1], lf3[:, :, 1], l5[:, :, 4], ALU.add, reads=[l5k, lfk + "v"], writes=[lfk + "v"])
                ls, lsk = lsr.next()
                P.copy("vector", ls[:], lf3[:, :, 0], reads=[lfk + "i"], writes=[lsk])
                for c in range(9):
                    npart = 128 if c < 8 else 32
                    p0 = 0
                    col0 = b * CAP_L + c * 128 if c < 8 else (16 + b) * 128
                    tcol = col0 // 128
                    P.copy("vector", tv[p0:p0 + npart, tcol:tcol + 1], lf[p0:p0 + npart, 2 * c + 1:2 * c + 2], reads=[lfk + "v", "tv"], writes=["tv"])
                    xs, xsk = xsr.next()
                    P.op("gpsimd", lambda en, xs=xs, ls=ls, c=c, npart=npart, p0=p0: en.indirect_dma_start(
                        out=xs[p0:p0 + npart, :], out_offset=None, in_=h2d[:, :],
                        in_offset=bass.IndirectOffsetOnAxis(ap=ls[p0:p0 + npart, c:c + 1], axis=0)), reads=[lsk], writes=[xsk], dma=True)
                    pb, pbk = banks[6]
                    pT = pb[:, :].bitcast(BF16)
                    for kc in range(8):
                        P.tr(pT[:, kc * 128:kc * 128 + npart], xs[p0:p0 + npart, kc * 128:(kc + 1) * 128], idb[p0:p0 + npart, p0:p0 + npart],
                             reads=[xsk, "idb"], writes=[pbk + "l", pbk + "c"])
                    P.copy("scalar", h2T[:, :, col0:col0 + npart], pT[:, 0:1024].rearrange("p (k s) -> p k s", k=8)[:, :, 0:npart],
                           reads=[pbk + "l", pbk + "c"], writes=["h2T"])
            ci = 0
            for f0 in range(0, NFC, FG):
                nf = min(FG, NFC - f0)
                wgb, wgk = wgr.next(); wub, wuk = wur.next(); wdb, wdk = wdr.next()
                for (src, dst, dk) in ((wgd, wgb, wgk), (wud, wub, wuk)):
                    for kp in range(0, 8, 2):
                        st, sk = stg.next()
                        sv = st[:, 0:2 * nf * 128].rearrange("p (a f) -> p a f", a=2)
                        P.dma("sync", sv, src[e, :, kp:kp + 2, f0 * 128:(f0 + nf) * 128], writes=[sk])
                        P.copy("gpsimd" if ci % 2 else "vector", dst[:, kp:kp + 2, 0:nf * 128], sv, reads=[sk], writes=[dk + f"k{kp}"])
                        ci += 1
                for fc in range(nf):
                    st, sk = stg.next()
                    P.dma("sync", st[:], wdd[e, :, f0 + fc, :], writes=[sk])
                    P.copy("gpsimd" if ci % 2 else "vector", wdb[:, fc, :], st[:], reads=[sk], writes=[wdk + f"f{fc}"])
                    ci += 1
                for (s0, ns) in tgs:
                    actT, ak = actr.next()
                    for fc in range(nf):
                        pg, pgk = pgr.next(); pu, puk = pur.next()
                        for kc in range(8):
                            P.mm(pg[:, 0:ns], wgb[:, kc, fc * 128:(fc + 1) * 128], h2T[:, kc, s0:s0 + ns], kc == 0, kc == 7,
                                 reads=["h2T", wgk + f"k{kc - kc % 2}"], writes=[pgk])
                        for kc in range(8):
                            P.mm(pu[:, 0:ns], wub[:, kc, fc * 128:(fc + 1) * 128], h2T[:, kc, s0:s0 + ns], kc == 0, kc == 7,
                                 reads=["h2T", wuk + f"k{kc - kc % 2}"], writes=[puk])
                        sg, sgk = sgr.next()
                        P.act(sg[:, 0:ns], pg[:, 0:ns], AF.Silu, reads=[pgk], writes=[sgk])
                        P.tt("vector", actT[:, fc, 0:ns], pu[:, 0:ns], sg[:, 0:ns], ALU.mult, reads=[puk, sgk], writes=[ak + f"f{fc}"])
                    for tt in range(s0 // 128, (s0 + ns) // 128):
                        for hf in range(2):
                            py, pyk = pyr.next()
                            for fc in range(nf):
                                P.mm(py[:], actT[:, fc, tt * 128 - s0:(tt + 1) * 128 - s0], wdb[:, fc, hf * 512:(hf + 1) * 512],
                                     fc == 0, fc == nf - 1, reads=[ak + f"f{fc}", wdk + f"f{fc}"], writes=[pyk])
                            ah = f"acc{tt}" + "ab"[hf]
                            P.tt("vector", acc[:, tt, hf * 512:(hf + 1) * 512], py[:], acc[:, tt, hf * 512:(hf + 1) * 512], ALU.add,
                                 reads=[pyk, ah], writes=[ah])
            for t in range(T):
                yr_, yrk = yor.next()
                P.act(yr_[:], acc[:, t, :], AF.Copy, reads=[f"acc{t}a", f"acc{t}b", "tv"], writes=[yrk], scale=tv[:, t:t + 1])
                P.dma("sync", Yo[e, t * 128:(t + 1) * 128, :], yr_[:], reads=[yrk], writes=[yrk], final=True)
        P.emit()
    return nc


def build_k8():
    nc = bass.Bass("TRN2", target_bir_lowering=False)
    T = K1_TILES
    Yb = [nc.dram_tensor(f"Yb{e}", [SLOTS_B + 1, 1024], BF16, kind="ExternalInput").ap() for e in range(N_EXP)]
    idxd = nc.dram_tensor("idx", [T, 128, N_EXP], I32, kind="ExternalInput").ap()
    x1d = nc.dram_tensor("i_x1", [T, 128, 1024], F32, kind="ExternalInput").ap()
    rows = nc.dram_tensor("rows", [2, 1024], F32, kind="ExternalInput").ap()
    lnr = nc.dram_tensor("lnr", [2, 1024], F32, kind="ExternalInput").ap()
    x2o = nc.dram_tensor("o_x2", [T, 128, 1024], F32, kind="ExternalOutput").ap()
    with ExitStack() as es:
        P = Prog(nc, es)
        sb, ps, ring = mk_alloc(nc, es)
        rowt = sb("rowt", [128, 2, 1024]); lnt = sb("lnt", [128, 2, 1024])
        for j in range(2):
            P.dma("sync", rowt[:, j, :], bcast_rows(rows[j], 128), writes=[f"row{j}"])
            P.dma("sync", lnt[:, j, :], bcast_rows(lnr[j], 128), writes=[f"ln{j}"])
        idr = ring("idx", 2, [128, N_EXP], I32)
        gr = ring("g", 8, [128, 1024], BF16)
        accr = ring("acc", 2, [128, 1024])
        xr = ring("x", 2, [128, 1024])
        str_ = ring("st", 2, [128, 2, 6]); mvr = ring("mv", 2, [128, 2]); rsr = ring("rs", 2, [128, 1])
        for t in range(T):
            g_ = 0 if t < 16 else 1
            ix, ixk = idr.next(); P.dma("sync", ix[:], idxd[t], writes=[ixk])
            acc, ack = accr.next()
            for e in range(N_EXP):
                gt, gk = gr.next()
                P.op("gpsimd", lambda en, gt=gt, ix=ix, e=e: en.indirect_dma_start(
                    out=gt[:, :], out_offset=None, in_=Yb[e][:, :], in_offset=bass.IndirectOffsetOnAxis(ap=ix[:, e:e + 1], axis=0)),
                    reads=[ixk], writes=[gk], dma=True)
                if e == 0:
                    P.copy("vector", acc[:], gt[:], reads=[gk], writes=[ack])
                else:
                    P.tt("vector", acc[:], acc[:], gt[:], ALU.add, reads=[ack, gk], writes=[ack])
            x, xk = xr.next(); P.dma("sync", x[:], x1d[t], writes=[xk])
            P.tt("gpsimd", acc[:], acc[:], rowt[:, g_, :], ALU.mult, reads=[ack, f"row{g_}"], writes=[ack])
            P.stt("vector", acc[:], x[:], ALPHA, acc[:], ALU.mult, ALU.add, reads=[xk, ack], writes=[ack])
            st, _ = str_.next(); mv, _ = mvr.next(); rs, _ = rsr.next()
            emit_layernorm(P, acc[:], ack, x[:], xk, st, mv, rs, f"lnC{t % 2}")
            P.tt("gpsimd", acc[:], x[:], lnt[:, 0, :], ALU.mult, reads=[xk, "ln0"], writes=[ack])
            P.tt("gpsimd", x[:], acc[:], lnt[:, 1, :], ALU.add, reads=[ack, "ln1"], writes=[xk])
            P.dma("sync", x2o[t], x[:], reads=[xk], writes=[xk], final=True)
        P.emit()
    return nc


def run_moe(h2, h2c, aff, affc, thr, thrc, x1, c1, mod_l, ln_w, ln_b, wg, wu, wd):
    B, n, D = x1.shape
    nctx = c1.shape[1]
    E = aff.shape[-1]
    h2all = np.ascontiguousarray(np.concatenate([h2.reshape(B * n, D), h2c.reshape(B * nctx, D)], 0))
    affall = np.concatenate([aff.reshape(B * n, E), affc.reshape(B * nctx, E)], 0)
    s_idx, j_idx = np.meshgrid(np.arange(128), np.arange(128), indexing="ij")
    cst = np.zeros((128, 448), np.float32)
    cst[:, 0:128] = (s_idx < j_idx); cst[:, 128:256] = 1.0; cst[:, 256:384] = np.eye(128); cst[0:64, 384:448] = (s_idx < j_idx)[:64, :64]
    tid = np.zeros((B, 128, 66), np.int64)
    iota = np.full((128, CAP_L + 128), -1.0, np.float32)
    iota[:, 0:CAP_L] = np.arange(CAP_L)
    iota[:, CAP_L:CAP_L + 32] = CAP_L + np.arange(32)
    for b in range(B):
        tid[b, :, 0:64] = (b * n + np.arange(n)).reshape(64, 128).T
        tid[b, :, 64:66] = (B * n + b * nctx + np.arange(nctx)).reshape(2, 128).T
    tid2 = np.ascontiguousarray(np.stack([tid // 128, tid % 128], -1).astype(np.float32))
    nc = build_k7g()
    in_maps = []
    for i in range(NCORES):
        es = [2 * i, 2 * i + 1]
        affL = np.zeros((2, B, 128, 66), np.float32)
        thr8 = np.zeros((2, B, 2), np.float32)
        for el, e in enumerate(es):
            for b in range(B):
                affL[el, b, :, 0:64] = aff[b, :, e].reshape(64, 128).T
                affL[el, b, :, 64:66] = affc[b, :, e].reshape(2, 128).T
                thr8[el, b] = (thr[b, e], thrc[b, e])
        lw = lambda w, kch: np.ascontiguousarray(w.reshape(2, kch, 128, w.shape[-1]).transpose(0, 2, 1, 3))
        in_maps.append({"h2all": h2all, "affL": affL,
                        "thr8": thr8.reshape(-1), "tid": tid2, "iota": iota, "cst": cst,
                        "wg": lw(wg[es], 8), "wu": lw(wu[es], 8), "wd": lw(wd[es], NFC)})
    res = _run(nc, in_maps)
    Yb = np.zeros((B, E, SLOTS_B + 1, D), h2all.dtype)
    posL = np.zeros((B, n, E), np.int32); posC = np.zeros((B, nctx, E), np.int32)
    for i in range(NCORES):
        Y = res.results[i]["o_Y"]; pos = res.results[i]["o_pos"]
        for el in range(2):
            e = 2 * i + el
            for b in range(B):
                Yb[b, e, 0:CAP_L] = Y[el, b * CAP_L:(b + 1) * CAP_L]
                Yb[b, e, CAP_L:SLOTS_B] = Y[el, (16 + b) * 128:(16 + b) * 128 + CAP_C]
                posL[b, :, e] = pos[el, b, :, 0:64].T.reshape(n)
                posC[b, :, e] = pos[el, b, :, 64:66].T.reshape(nctx)
    nc8 = build_k8()
    lnr = np.ascontiguousarray(np.stack([ln_w, ln_b]))
    in_maps = []
    for i in range(NCORES):
        b = i // 4
        rows = np.ascontiguousarray(np.stack([mod_l[b, 5120:6144], mod_l[2, 5120:6144]]))
        idx = tok_shard(posL, posC, i)
        idx[-1, 64:, :] = SLOTS_B
        m = {"idx": idx, "i_x1": tok_shard(x1, c1, i), "rows": rows, "lnr": lnr}
        for e in range(E):
            m[f"Yb{e}"] = np.ascontiguousarray(Yb[b, e])
        in_maps.append(m)
    res = _run(nc8, in_maps)
    return tok_unshard([r["o_x2"] for r in res.results], B, n, nctx)
```

```python
import numpy as np
from contextlib import ExitStack
import concourse.bass as bass
import concourse.mybir as mybir
from concourse.bass_utils import run_bass_kernel_spmd

F32 = mybir.dt.float32
BF16 = mybir.dt.bfloat16
I32 = mybir.dt.int32
AF = mybir.ActivationFunctionType
ALU = mybir.AluOpType
AX = mybir.AxisListType

NCORES = 8
STAGE_EXP = False
FUSE_WAIT = True
NO_RAW_SELF = False
SELF_SYNC = True


class Prog:
    ENGS = ("sync", "scalar", "vector", "gpsimd", "tensor")

    def __init__(self, nc, es, n_dma_sems=12):
        self.nc, self.es = nc, es
        self.ops = {e: [] for e in self.ENGS}
        self.esem = {}
        self.ecount = {}
        for e in ("scalar", "vector", "gpsimd", "tensor"):
            self.esem[e] = es.enter_context(nc.semaphore(f"sem_{e}"))
            self.ecount[e] = 0
        self.dpool = {}
        for q in ("sync", "scalar", "gpsimd"):
            self.dpool[q] = dict(
                sems=[es.enter_context(nc.semaphore(f"dsem_{q}_{i}")) for i in range(n_dma_sems)],
                cnt=[0] * n_dma_sems, nxt=0, know=[None] * n_dma_sems)
        self.semobj = {}
        self.lastw = {}
        self.readers = {}
        self.know = {e: {} for e in self.ENGS}
        self.final_tokens = []

    def _need(self, eng, tok, waits):
        sk, v, kn = tok
        if self.know[eng].get(sk, 0) >= v:
            return
        waits.append((sk, v))
        k = self.know[eng]
        for a, b in kn.items():
            if k.get(a, 0) < b:
                k[a] = b
        if k.get(sk, 0) < v:
            k[sk] = v

    def op(self, eng, fn, reads=(), writes=(), dma=False, final=False):
        waits = []
        toks = []
        own = None if dma else ("e", eng)
        for key in reads:
            t = self.lastw.get(key)
            if t is not None:
                toks.append((t, True))
        for key in writes:
            t = self.lastw.get(key)
            if t is not None:
                toks.append((t, False))
            toks.extend((r, False) for r in self.readers.get(key, ()))
        for t, raw in toks:
            if t[0] == own and (eng == "tensor" or not SELF_SYNC or (not raw and eng != "gpsimd") or (NO_RAW_SELF and eng in ("vector", "scalar"))):
                continue
            self._need(eng, t, waits)
        if dma:
            pool = self.dpool[eng]
            j = pool["nxt"]
            pool["nxt"] = (j + 1) % len(pool["sems"])
            if pool["cnt"][j] > 0:
                self._need(eng, (("d", eng, j), pool["cnt"][j], pool["know"][j]), waits)
            pool["cnt"][j] += 16
            sk = ("d", eng, j)
            self.semobj[sk] = pool["sems"][j]
            kn = dict(self.know[eng])
            pool["know"][j] = kn
            tok = (sk, pool["cnt"][j], kn)
            inc = (pool["sems"][j], 16)
        else:
            self.ecount[eng] += 1
            sk = ("e", eng)
            self.semobj[sk] = self.esem[eng]
            tok = (sk, self.ecount[eng], dict(self.know[eng]))
            inc = (self.esem[eng], 1)
        self.ops[eng].append((waits, fn, inc))
        for key in reads:
            self.readers.setdefault(key, []).append(tok)
        for key in writes:
            self.lastw[key] = tok
            self.readers[key] = []
        if final:
            self.final_tokens.append(tok)
        return tok

    def emit(self):
        waits = []
        for t in self.final_tokens:
            self._need("sync", t, waits)
        if waits:
            self.ops["sync"].append((waits, None, None))
        nc = self.nc
        with nc.Block() as block:
            def run(eng_name):
                def body(eng):
                    for waits, fn, inc in self.ops[eng_name]:
                        fused = FUSE_WAIT and fn is not None and len(waits) > 0
                        for sk, v in (waits[:-1] if fused else waits):
                            eng.wait_ge(self.semobj[sk], v)
                        if fn is not None:
                            ins = fn(eng)
                            if fused:
                                ins._wait_ge(self.semobj[waits[-1][0]], waits[-1][1])
                            ins.then_inc(inc[0], inc[1])
                return body
            block.sync(run("sync"))
            block.scalar(run("scalar"))
            block.vector(run("vector"))
            block.gpsimd(run("gpsimd"))
            block.tensor(run("tensor"))

    def dma(self, q, out, in_, reads=(), writes=(), final=False, **kw):
        return self.op(q, lambda e: e.dma_start(out=out, in_=in_, **kw), reads, writes, dma=True, final=final)

    def mm(self, out, lhsT, rhs, start, stop, reads=(), writes=()):
        return self.op("tensor", lambda e: e.matmul(out, lhsT, rhs, start=start, stop=stop), reads, writes)

    def act(self, out, in_, func, reads=(), writes=(), eng="scalar", **kw):
        return self.op(eng, lambda e: e.activation(out=out, in_=in_, func=func, **kw), reads, writes)

    def tt(self, eng, out, in0, in1, op, reads=(), writes=()):
        return self.op(eng, lambda e: e.tensor_tensor(out=out, in0=in0, in1=in1, op=op), reads, writes)

    def ts(self, eng, out, in0, s1, s2, op0, op1=None, reads=(), writes=(), accum_out=None):
        kw = {}
        if op1 is not None:
            kw["op1"] = op1
        if accum_out is not None:
            kw["accum_out"] = accum_out
        return self.op(eng, lambda e: e.tensor_scalar(out=out, in0=in0, scalar1=s1, scalar2=s2, op0=op0, **kw),
                       reads, writes)

    def stt(self, eng, out, in0, scalar, in1, op0, op1, reads=(), writes=()):
        return self.op(eng, lambda e: e.scalar_tensor_tensor(out=out, in0=in0, scalar=scalar, in1=in1,
                                                             op0=op0, op1=op1), reads, writes)

    def copy(self, eng, out, in_, reads=(), writes=()):
        if eng == "scalar":
            return self.op(eng, lambda e: e.activation(out=out, in_=in_, func=AF.Copy), reads, writes)
        return self.op(eng, lambda e: e.tensor_copy(out=out, in_=in_), reads, writes)

    def tr(self, out, in_, ident, reads=(), writes=()):
        return self.op("tensor", lambda e: e.transpose(out, in_, ident), reads, writes)

    def memset(self, eng, ap, val, writes=()):
        return self.op(eng, lambda e: e.memset(ap, val), (), writes)

    def gen(self, eng, f, reads=(), writes=()):
        return self.op(eng, f, reads, writes)


def _run(nc, in_maps):
    return run_bass_kernel_spmd(nc, in_maps, core_ids=list(range(NCORES)))


D_MODEL = 1024
DEPTH = 2
N_MOD = 6
MODC = N_MOD * D_MODEL // NCORES


def build_k0():
    nc = bass.Bass("TRN2", target_bir_lowering=False)
    cvT = nc.dram_tensor("cvT", [128, 8, 3], F32, kind="ExternalInput").ap()
    wm = nc.dram_tensor("wm", [DEPTH, 128, 8, MODC], F32, kind="ExternalInput").ap()
    bm = nc.dram_tensor("bm", [DEPTH, 3, MODC], F32, kind="ExternalInput").ap()
    out = nc.dram_tensor("mod", [DEPTH, 3, MODC], F32, kind="ExternalOutput").ap()
    with ExitStack() as es:
        P = Prog(nc, es)
        sb = lambda name, shape, dt=F32: es.enter_context(nc.sbuf_tensor(name, shape, dt))
        cv = sb("cv", [128, 8, 3])
        cs = sb("cs", [128, 8, 3])
        w = [sb(f"w{l}", [128, 8, MODC]) for l in range(DEPTH)]
        b = sb("b", [3, DEPTH, MODC])
        o = sb("o", [3, DEPTH, MODC])
        ps = [es.enter_context(nc.psum_tensor(f"ps{i}", [128, 512], F32)) for i in range(2)]
        P.dma("sync", cv[:], cvT, writes=["cv"])
        for l in range(DEPTH):
            P.dma("sync" if l == 0 else "gpsimd", w[l][:], wm[l], writes=[f"w{l}"])
            P.dma("sync", b[:, l, :], bm[l], writes=[f"b{l}"])
        P.act(cs[:], cv[:], AF.Silu, reads=["cv"], writes=["cs"])
        H = MODC // 2
        for l in range(DEPTH):
            for h in range(2):
                pt = ps[h]
                for kc in range(8):
                    P.mm(pt[0:3, 0:H], cs[:, kc, :], w[l][:, kc, h * H:(h + 1) * H], kc == 0, kc == 7,
                         reads=["cs", f"w{l}"], writes=[f"ps{h}"])
                P.op("vector", lambda e, l=l, h=h, pt=pt: e.tensor_tensor(
                    out=o[:, l, h * H:(h + 1) * H], in0=pt[0:3, 0:H], in1=b[:, l, h * H:(h + 1) * H], op=ALU.add),
                    reads=[f"ps{h}", f"b{l}"], writes=[f"o{l}{h}"])
            P.dma("sync", out[l], o[:, l, :], reads=[f"o{l}0", f"o{l}1"], final=True)
        P.emit()
    return nc


def run_k0(c, c_ctx, w_mod, b_mod):
    cv = np.concatenate([c, c_ctx[None]], 0)
    cvT = np.ascontiguousarray(cv.T.reshape(8, 128, 3).transpose(1, 0, 2))
    nc = build_k0()
    in_maps = []
    for i in range(NCORES):
        sl = slice(i * MODC, (i + 1) * MODC)
        wm = np.ascontiguousarray(w_mod[:, :, sl].reshape(DEPTH, 8, 128, MODC).transpose(0, 2, 1, 3))
        bm = np.ascontiguousarray(np.broadcast_to(b_mod[:, None, sl], (DEPTH, 3, MODC)))
        in_maps.append({"cvT": cvT, "wm": wm, "bm": bm})
    res = _run(nc, in_maps)
    return np.concatenate([r["mod"] for r in res.results], axis=-1)


class Ring:
    def __init__(self, items):
        self.items, self.i = items, 0

    def next(self):
        it = self.items[self.i % len(self.items)]
        self.i += 1
        return it


def mk_alloc(nc, es):
    def sb(name, shape, dt=F32):
        return es.enter_context(nc.sbuf_tensor(name, shape, dt))

    def ps(name, shape, dt=F32):
        return es.enter_context(nc.psum_tensor(name, shape, dt))

    def ring(name, n, shape, dt=F32, psum=False):
        return Ring([((ps if psum else sb)(f"{name}{i}", shape, dt), f"{name}{i}") for i in range(n)])
    return sb, ps, ring


def bcast_rows(ap1d, nparts):
    return bass.AP(tensor=ap1d.tensor, offset=ap1d.offset, ap=[[0, nparts]] + [list(x) for x in ap1d.ap])


LN_EPS = 1e-5


def emit_layernorm(P, x, xkey, xn, xnkey, st, mv, rs, skey, n=1024):
    for j in range(n // 512):
        P.gen("vector", lambda e, j=j: e.bn_stats(out=st[:, j, :], in_=x[:, j * 512:(j + 1) * 512]),
              reads=[xkey], writes=[skey + f"st{j}"])
    P.gen("vector", lambda e: e.bn_aggr(out=mv[:], in_=st[:]),
          reads=[skey + f"st{j}" for j in range(n // 512)], writes=[skey + "mv"])
    P.ts("vector", rs[:], mv[:, 1:2], LN_EPS, None, ALU.add, reads=[skey + "mv"], writes=[skey + "rs"])
    P.act(rs[:], rs[:], AF.Sqrt, reads=[skey + "rs"], writes=[skey + "rs"])
    P.gen("vector", lambda e: e.reciprocal(out=rs[:], in_=rs[:]), reads=[skey + "rs"], writes=[skey + "rs"])
    P.ts("vector", xn, x, mv[:, 0:1], rs[:, 0:1], ALU.subtract, ALU.mult,
         reads=[xkey, skey + "mv", skey + "rs"], writes=[xnkey])


N_IN = 2576
K1_TILES = 17


def build_k1():
    nc = bass.Bass("TRN2", target_bir_lowering=False)
    xt = nc.dram_tensor("xt", [K1_TILES, 128, 1024], F32, kind="ExternalInput").ap()
    modr = nc.dram_tensor("modr", [2, 2, 1024], F32, kind="ExternalInput").ap()
    win = nc.dram_tensor("win", [128, 8, N_IN], F32, kind="ExternalInput").ap()
    identd = nc.dram_tensor("ident", [128, 128], F32, kind="ExternalInput").ap()
    out = nc.dram_tensor("p", [K1_TILES, 128, N_IN], F32, kind="ExternalOutput").ap()
    with ExitStack() as es:
        P = Prog(nc, es)
        sb, ps, ring = mk_alloc(nc, es)
        idf = sb("idf", [128, 128])
        idb = sb("idb", [128, 128], BF16)
        P.dma("sync", idf[:], identd, writes=["idf"])
        P.copy("vector", idb[:], idf[:], reads=["idf"], writes=["idb"])
        modt = sb("modt", [128, 2, 2, 1024])
        for g in range(2):
            for j in range(2):
                P.dma("sync", modt[:, g, j, :], bcast_rows(modr[g, j], 128), writes=[f"mod{g}{j}"])
            P.ts("vector", modt[:, g, 0, :], modt[:, g, 0, :], 1.0, None, ALU.add,
                 reads=[f"mod{g}0"], writes=[f"mod{g}0"])
        wbf = sb("wbf", [128, 8, N_IN], BF16)
        wst = ring("wst", 2, [128, N_IN])
        for kc in range(8):
            t, k = wst.next()
            P.dma("gpsimd" if kc % 2 else "sync", t[:], win[:, kc, :], writes=[k])
            P.copy("gpsimd" if kc % 2 else "vector", wbf[:, kc, :], t[:], reads=[k], writes=[f"wbf{kc}"])
        wkeys = [f"wbf{kc}" for kc in range(8)]
        xr = ring("x", 2, [128, 1024])
        xnr = ring("xn", 2, [128, 1024])
        h1r = ring("h1", 2, [128, 1024])
        hr = ring("h", 2, [128, 1024], BF16)
        hTr = ring("hT", 2, [128, 1024], BF16)
        orr = ring("o", 2, [128, N_IN])
        str_ = ring("st", 2, [128, 2, 6])
        mvr = ring("mv", 2, [128, 2])
        rsr = ring("rs", 2, [128, 1])
        pTr = ring("pT", 2, [128, 1024], BF16, psum=True)
        pmr = ring("pm", 4, [128, 512], F32, psum=True)
        ev = 0
        for t in range(K1_TILES):
            g = 0 if t < 16 else 1
            x, xk = xr.next()
            P.dma("sync", x[:], xt[t], writes=[xk])
            xn, xnk = xnr.next()
            st, _ = str_.next(); mv, _ = mvr.next(); rs, _ = rsr.next()
            emit_layernorm(P, x[:], xk, xn[:], xnk, st, mv, rs, f"ln{t % 2}")
            h1, h1k = h1r.next()
            P.tt("gpsimd", h1[:], xn[:], modt[:, g, 0, :], ALU.mult, reads=[xnk, f"mod{g}0"], writes=[h1k])
            h, hk = hr.next()
            P.tt("gpsimd", h[:], h1[:], modt[:, g, 1, :], ALU.add, reads=[h1k, f"mod{g}1"], writes=[hk])
            pT, pTk = pTr.next()
            for kc in range(8):
                P.tr(pT[:, kc * 128:(kc + 1) * 128], h[:, kc * 128:(kc + 1) * 128], idb[:],
                     reads=[hk, "idb"], writes=[pTk])
            hT, hTk = hTr.next()
            P.copy("scalar", hT[:], pT[:], reads=[pTk], writes=[hTk])
            o, ok = orr.next()
            for cg in range(6):
                c0 = cg * 512
                n = min(512, N_IN - c0)
                pm, pmk = pmr.next()
                for kc in range(8):
                    P.mm(pm[:, 0:n], hT[:, kc * 128:(kc + 1) * 128], wbf[:, kc, c0:c0 + n], kc == 0, kc == 7,
                         reads=[hTk, wkeys[kc]], writes=[pmk])
                P.copy("scalar" if ev % 2 else "vector", o[:, c0:c0 + n], pm[:, 0:n], reads=[pmk], writes=[ok + f"c{cg}"])
                ev += 1
            P.dma("gpsimd", out[t], o[:], reads=[ok + f"c{cg}" for cg in range(6)], writes=[ok + "dma"], final=True)
        P.emit()
    return nc


def lay_w(w, kchunks):
    return np.ascontiguousarray(w.reshape(kchunks, 128, w.shape[1]).transpose(1, 0, 2))


def run_k1(x, ctx, mod_l, w_in_l):
    nc = build_k1()
    B, n, D = x.shape
    seg = n // 4
    ctxf = ctx.reshape(-1, D)
    ident = np.eye(128, dtype=np.float32)
    win = lay_w(w_in_l, 8)
    in_maps = []
    for i in range(NCORES):
        b, s = i // 4, i % 4
        xt = np.zeros((K1_TILES * 128, D), np.float32)
        xt[:seg] = x[b, s * seg:(s + 1) * seg]
        xt[seg:seg + 64] = ctxf[i * 64:(i + 1) * 64]
        modr = np.stack([np.stack([mod_l[b, 1024:2048], mod_l[b, 0:1024]]),
                         np.stack([mod_l[2, 1024:2048], mod_l[2, 0:1024]])])
        in_maps.append({"xt": xt.reshape(K1_TILES, 128, D), "modr": np.ascontiguousarray(modr), "win": win,
                        "ident": ident})
    res = _run(nc, in_maps)
    P_lat = np.zeros((B, n, N_IN), np.float32)
    P_ctx = np.zeros((B * ctx.shape[1], N_IN), np.float32)
    for i in range(NCORES):
        b, s = i // 4, i % 4
        p = res.results[i]["p"].reshape(K1_TILES * 128, N_IN)
        P_lat[b, s * seg:(s + 1) * seg] = p[:seg]
        P_ctx[i * 64:(i + 1) * 64] = p[seg:seg + 64]
    return P_lat, P_ctx.reshape(B, ctx.shape[1], N_IN)


ALPHA = (2.0 * DEPTH) ** 0.25
N_EXP = 16


def build_k5():
    nc = bass.Bass("TRN2", target_bir_lowering=False)
    T = K1_TILES
    yt = nc.dram_tensor("yt", [T, 128, 1024], F32, kind="ExternalInput").ap()
    xt = nc.dram_tensor("xt", [T, 128, 1024], F32, kind="ExternalInput").ap()
    wout = nc.dram_tensor("wout", [128, 8, 1024], F32, kind="ExternalInput").ap()
    rows = nc.dram_tensor("rows", [2, 3, 1024], F32, kind="ExternalInput").ap()
    lnr = nc.dram_tensor("lnr", [2, 1024], F32, kind="ExternalInput").ap()
    rwd = nc.dram_tensor("rw", [128, 8, N_EXP], F32, kind="ExternalInput").ap()
    rbd = nc.dram_tensor("rb", [N_EXP], F32, kind="ExternalInput").ap()
    identd = nc.dram_tensor("ident", [128, 128], F32, kind="ExternalInput").ap()
    x1o = nc.dram_tensor("o_x1", [T, 128, 1024], F32, kind="ExternalOutput").ap()
    h2o = nc.dram_tensor("o_h2", [T, 128, 1024], BF16, kind="ExternalOutput").ap()
    affo = nc.dram_tensor("o_aff", [T, 128, N_EXP], F32, kind="ExternalOutput").ap()
    with ExitStack() as es:
        P = Prog(nc, es)
        sb, ps, ring = mk_alloc(nc, es)
        idf = sb("idf", [128, 128])
        idb = sb("idb", [128, 128], BF16)
        P.dma("sync", idf[:], identd, writes=["idf"])
        P.copy("vector", idb[:], idf[:], reads=["idf"], writes=["idb"])
        rowt = sb("rowt", [128, 2, 3, 1024])
        for g in range(2):
            for j in range(3):
                P.dma("sync", rowt[:, g, j, :], bcast_rows(rows[g, j], 128), writes=[f"row{g}{j}"])
            P.ts("vector", rowt[:, g, 1, :], rowt[:, g, 1, :], 1.0, None, ALU.add,
                 reads=[f"row{g}1"], writes=[f"row{g}1"])
        lnt = sb("lnt", [128, 2, 1024])
        for j in range(2):
            P.dma("sync", lnt[:, j, :], bcast_rows(lnr[j], 128), writes=[f"ln{j}"])
        rw = sb("rwt", [128, 8, N_EXP])
        P.dma("sync", rw[:], rwd, writes=["rw"])
        rb = sb("rbt", [128, N_EXP])
        P.dma("sync", rb[:], bcast_rows(rbd, 128), writes=["rb"])
        wbf = sb("wbf", [128, 8, 1024], BF16)
        wst = ring("wst", 2, [128, 1024])
        for kc in range(8):
            t, k = wst.next()
            P.dma("gpsimd" if kc % 2 else "sync", t[:], wout[:, kc, :], writes=[k])
            P.copy("gpsimd" if kc % 2 else "vector", wbf[:, kc, :], t[:], reads=[k], writes=[f"wbf{kc}"])
        wkeys = [f"wbf{kc}" for kc in range(8)]
        yr = ring("y", 2, [128, 1024]); ybr = ring("yb", 2, [128, 1024], BF16)
        yTr = ring("yT", 2, [128, 1024], BF16)
        xr = ring("x", 2, [128, 1024]); tmpr = ring("tmp", 2, [128, 1024]); rr = ring("r", 2, [128, 1024])
        xnr = ring("xn", 2, [128, 1024]); x1r = ring("x1_", 2, [128, 1024]); x1ar = ring("x1a", 2, [128, 1024])
        xn2r = ring("xn2", 2, [128, 1024]); h2fr = ring("h2f", 2, [128, 1024]); h2ar = ring("h2a", 2, [128, 1024])
        h2br = ring("h2b", 2, [128, 1024], BF16)
        h2Tr = ring("h2T", 2, [128, 1024])
        str_ = ring("st", 4, [128, 2, 6]); mvr = ring("mv", 4, [128, 2]); rsr = ring("rs", 4, [128, 1])
        lgr = ring("lg", 2, [128, N_EXP]); exr = ring("ex", 2, [128, N_EXP]); afr = ring("af", 2, [128, N_EXP])
        smr = ring("sm", 2, [128, 4])
        pTr = ring("pT", 1, [128, 1024], BF16, psum=True)
        pmr = ring("pm", 2, [128, 512], F32, psum=True)
        pTfr = ring("pTf", 1, [128, 1024], F32, psum=True)
        plr = ring("pl", 1, [128, N_EXP], F32, psum=True)
        lnc = 0
        for t in range(T):
            g = 0 if t < 16 else 1
            y, yk = yr.next(); P.dma("sync", y[:], yt[t], writes=[yk])
            x, xk = xr.next(); P.dma("sync", x[:], xt[t], writes=[xk])
            yb, ybk = ybr.next(); P.copy("gpsimd", yb[:], y[:], reads=[yk], writes=[ybk])
            pT, pTk = pTr.next()
            for kc in range(8):
                P.tr(pT[:, kc * 128:(kc + 1) * 128], yb[:, kc * 128:(kc + 1) * 128], idb[:], reads=[ybk, "idb"], writes=[pTk])
            yT, yTk = yTr.next(); P.copy("scalar", yT[:], pT[:], reads=[pTk], writes=[yTk])
            tmp, tmpk = tmpr.next()
            for hf in range(2):
                pm, pmk = pmr.next()
                for kc in range(8):
                    P.mm(pm[:], yT[:, kc * 128:(kc + 1) * 128], wbf[:, kc, hf * 512:(hf + 1) * 512], kc == 0, kc == 7,
                         reads=[yTk, wkeys[kc]], writes=[pmk])
                P.tt("vector", tmp[:, hf * 512:(hf + 1) * 512], pm[:], rowt[:, g, 0, hf * 512:(hf + 1) * 512], ALU.mult,
                     reads=[pmk, f"row{g}0"], writes=[tmpk + str(hf)])
            r, rk = rr.next()
            P.stt("vector", r[:], x[:], ALPHA, tmp[:], ALU.mult, ALU.add, reads=[xk, tmpk + "0", tmpk + "1"], writes=[rk])
            xn, xnk = xnr.next(); st, _ = str_.next(); mv, _ = mvr.next(); rs, _ = rsr.next()
            emit_layernorm(P, r[:], rk, xn[:], xnk, st, mv, rs, f"lnA{lnc % 4}"); lnc += 1
            x1a, x1ak = x1ar.next(); x1, x1k = x1r.next()
            P.tt("gpsimd", x1a[:], xn[:], lnt[:, 0, :], ALU.mult, reads=[xnk, "ln0"], writes=[x1ak])
            P.tt("gpsimd", x1[:], x1a[:], lnt[:, 1, :], ALU.add, reads=[x1ak, "ln1"], writes=[x1k])
            P.dma("gpsimd", x1o[t], x1[:], reads=[x1k], writes=[x1k + "d"], final=True)
            xn2, xn2k = xn2r.next(); st, _ = str_.next(); mv, _ = mvr.next(); rs, _ = rsr.next()
            emit_layernorm(P, x1[:], x1k, xn2[:], xn2k, st, mv, rs, f"lnA{lnc % 4}"); lnc += 1
            h2a, h2ak = h2ar.next(); h2f, h2fk = h2fr.next(); h2b, h2bk = h2br.next()
            P.tt("gpsimd", h2a[:], xn2[:], rowt[:, g, 1, :], ALU.mult, reads=[xn2k, f"row{g}1"], writes=[h2ak])
            P.tt("vector", h2f[:], h2a[:], rowt[:, g, 2, :], ALU.add, reads=[h2ak, f"row{g}2"], writes=[h2fk])
            P.copy("scalar", h2b[:], h2f[:], reads=[h2fk], writes=[h2bk])
            P.dma("sync", h2o[t], h2b[:], reads=[h2bk], writes=[h2bk + "d"], final=True)
            pTf, pTfk = pTfr.next()
            for kc in range(8):
                P.tr(pTf[:, kc * 128:(kc + 1) * 128], h2f[:, kc * 128:(kc + 1) * 128], idf[:], reads=[h2fk, "idf"], writes=[pTfk])
            h2T, h2Tk = h2Tr.next()
            P.copy("scalar", h2T[:, 0:512], pTf[:, 0:512], reads=[pTfk], writes=[h2Tk + "a"])
            P.copy("vector", h2T[:, 512:1024], pTf[:, 512:1024], reads=[pTfk], writes=[h2Tk + "b"])
            pl, plk = plr.next()
            for kc in range(8):
                P.mm(pl[:], h2T[:, kc * 128:(kc + 1) * 128], rw[:, kc, :], kc == 0, kc == 7,
                     reads=[h2Tk + "a", h2Tk + "b", "rw"], writes=[plk])
            lg, lgk = lgr.next(); ex, exk = exr.next(); af, afk = afr.next(); sm, smk = smr.next()
            P.tt("vector", lg[:], pl[:], rb[:], ALU.add, reads=[plk, "rb"], writes=[lgk])
            P.gen("vector", lambda e, sm=sm, lg=lg: e.reduce_max(out=sm[:, 0:1], in_=lg[:], axis=AX.X), reads=[lgk], writes=[smk + "m"])
            P.ts("vector", sm[:, 1:2], sm[:, 0:1], -1.0, None, ALU.mult, reads=[smk + "m"], writes=[smk + "n"])
            P.act(ex[:], lg[:], AF.Exp, reads=[lgk, smk + "n"], writes=[exk, smk + "s"], bias=sm[:, 1:2], scale=1.0,
                  accum_out=sm[:, 2:3])
            P.gen("vector", lambda e, sm=sm: e.reciprocal(out=sm[:, 3:4], in_=sm[:, 2:3]), reads=[smk + "s"], writes=[smk + "r"])
            P.ts("vector", af[:], ex[:], sm[:, 3:4], None, ALU.mult, reads=[exk, smk + "r"], writes=[afk])
            P.dma("gpsimd", affo[t], af[:], reads=[afk], writes=[afk + "d"], final=True)
        P.emit()
    return nc


def tok_shard(lat, ctx, i):
    B, n, D = lat.shape
    seg = n // 4
    b, s = i // 4, i % 4
    out = np.zeros((K1_TILES * 128, D), lat.dtype)
    out[:seg] = lat[b, s * seg:(s + 1) * seg]
    out[seg:seg + 64] = ctx.reshape(-1, D)[i * 64:(i + 1) * 64]
    return out.reshape(K1_TILES, 128, D)


def tok_unshard(parts, B, n, nctx):
    D = parts[0].shape[-1]
    seg = n // 4
    lat = np.zeros((B, n, D), parts[0].dtype)
    ctx = np.zeros((B * nctx, D), parts[0].dtype)
    for i in range(NCORES):
        b, s = i // 4, i % 4
        p = parts[i].reshape(K1_TILES * 128, D)
        lat[b, s * seg:(s + 1) * seg] = p[:seg]
        ctx[i * 64:(i + 1) * 64] = p[seg:seg + 64]
    return lat, ctx.reshape(B, nctx, D)


def run_k5(ycat_l, ycat_c, x, ctx, mod_l, w_out_l, ln_w, ln_b, router_w_l, router_b_l):
    nc = build_k5()
    B, n, D = x.shape
    ident = np.eye(128, dtype=np.float32)
    wout = lay_w(w_out_l, 8)
    rw = lay_w(router_w_l, 8)
    lnr = np.ascontiguousarray(np.stack([ln_w, ln_b]))
    in_maps = []
    for i in range(NCORES):
        b = i // 4
        rows = np.stack([np.stack([mod_l[m, 2048:3072], mod_l[m, 4096:5120], mod_l[m, 3072:4096]]) for m in (b, 2)])
        in_maps.append({"yt": tok_shard(ycat_l, ycat_c, i), "xt": tok_shard(x, ctx, i), "wout": wout,
                        "rows": np.ascontiguousarray(rows), "lnr": lnr, "rw": rw,
                        "rb": np.ascontiguousarray(router_b_l), "ident": ident})
    res = _run(nc, in_maps)
    nctx = ctx.shape[1]
    x1, c1 = tok_unshard([r["o_x1"] for r in res.results], B, n, nctx)
    h2, h2c = tok_unshard([r["o_h2"] for r in res.results], B, n, nctx)
    aff, affc = tok_unshard([r["o_aff"] for r in res.results], B, n, nctx)
    return x1, c1, h2, h2c, aff, affc


K6_ITERS = 26


def build_k6(F_lat, k_lat, F_ctx, k_ctx):
    nc = bass.Bass("TRN2", target_bir_lowering=False)
    R = 32
    ald = nc.dram_tensor("al", [R, F_lat], F32, kind="ExternalInput").ap()
    acd = nc.dram_tensor("ac", [R, F_ctx], F32, kind="ExternalInput").ap()
    thro = nc.dram_tensor("thr", [R, 2], F32, kind="ExternalOutput").ap()
    with ExitStack() as es:
        P = Prog(nc, es)
        sb, ps, ring = mk_alloc(nc, es)
        res = sb("res", [R, 2])
        for pi, (src, F, k) in enumerate(((ald, F_lat, k_lat), (acd, F_ctx, k_ctx))):
            A = sb(f"A{pi}", [R, F])
            junk = sb(f"junk{pi}", [R, F], BF16)
            sc = sb(f"sc{pi}", [R, 8])
            lo, hi, mid, cnt, cond, t1, t2 = (sc[:, j:j + 1] for j in range(7))
            kk = f"p{pi}"
            P.dma("sync", A[:], src, writes=[kk + "A"])
            P.memset("vector", lo, 0.0, writes=[kk + "lo"])
            P.memset("vector", hi, 1.0, writes=[kk + "hi"])
            for it in range(K6_ITERS):
                P.tt("vector", mid, lo, hi, ALU.add, reads=[kk + "lo", kk + "hi"], writes=[kk + "mid"])
                P.ts("vector", mid, mid, 0.5, None, ALU.mult, reads=[kk + "mid"], writes=[kk + "mid"])
                P.ts("vector", junk[:], A[:], mid, None, ALU.is_ge, ALU.add, reads=[kk + "A", kk + "mid"],
                     writes=[kk + "junk", kk + "cnt"], accum_out=cnt)
                P.ts("vector", cond, cnt, float(k) - 0.5, None, ALU.is_ge, reads=[kk + "cnt"], writes=[kk + "cond"])
                P.tt("vector", t1, cond, mid, ALU.mult, reads=[kk + "cond", kk + "mid"], writes=[kk + "t1"])
                P.stt("vector", t2, cond, 2.0, mid, ALU.mult, ALU.add, reads=[kk + "cond", kk + "mid"], writes=[kk + "t2"])
                P.tt("vector", lo, lo, t1, ALU.max, reads=[kk + "lo", kk + "t1"], writes=[kk + "lo"])
                P.tt("vector", hi, hi, t2, ALU.min, reads=[kk + "hi", kk + "t2"], writes=[kk + "hi"])
            P.copy("vector", res[:, pi:pi + 1], lo, reads=[kk + "lo"], writes=[f"res{pi}"])
        P.dma("sync", thro, res[:], reads=["res0", "res1"], final=True)
        P.emit()
    return nc


def run_k6(aff, affc):
    B, n, E = aff.shape
    ncx = affc.shape[1]
    nc = build_k6(n, 2 * n // E, ncx, 2 * ncx // E)
    al = np.ascontiguousarray(aff.transpose(0, 2, 1).reshape(B * E, n))
    ac = np.ascontiguousarray(affc.transpose(0, 2, 1).reshape(B * E, ncx))
    res = _run(nc, [{"al": al, "ac": ac} for _ in range(NCORES)])
    thr = res.results[0]["thr"]
    return thr[:, 0].reshape(B, E), thr[:, 1].reshape(B, E)


D_FF = 2816
NFC = D_FF // 128
K7_TOK = K1_TILES * 128


def build_k7(n_exp=N_EXP):
    nc = bass.Bass("TRN2", target_bir_lowering=False)
    T = K1_TILES
    h2Td = nc.dram_tensor("h2T", [128, 8, K7_TOK], BF16, kind="ExternalInput").ap()
    affd = nc.dram_tensor("aff", [T, 128, N_EXP], F32, kind="ExternalInput").ap()
    thrd = nc.dram_tensor("thr", [2, N_EXP], F32, kind="ExternalInput").ap()
    x1d = nc.dram_tensor("i_x1", [T, 128, 1024], F32, kind="ExternalInput").ap()
    rows = nc.dram_tensor("rows", [2, 1024], F32, kind="ExternalInput").ap()
    lnr = nc.dram_tensor("lnr", [2, 1024], F32, kind="ExternalInput").ap()
    wgd = nc.dram_tensor("wg", [n_exp, 128, 8, D_FF], F32, kind="ExternalInput").ap()
    wud = nc.dram_tensor("wu", [n_exp, 128, 8, D_FF], F32, kind="ExternalInput").ap()
    wdd = nc.dram_tensor("wd", [n_exp, 128, NFC, 1024], F32, kind="ExternalInput").ap()
    x2o = nc.dram_tensor("o_x2", [T, 128, 1024], F32, kind="ExternalOutput").ap()
    with ExitStack() as es:
        P = Prog(nc, es)
        sb, ps, ring = mk_alloc(nc, es)
        h2T = sb("h2Ts", [128, 8, K7_TOK], BF16)
        for kc in range(8):
            P.dma("sync", h2T[:, kc, :], h2Td[:, kc, :], writes=["h2T"] if kc == 7 else [f"h2T_{kc}"])
        h2keys = ["h2T"] + [f"h2T_{kc}" for kc in range(7)]
        thrb = sb("thrb", [128, 2, N_EXP])
        for g in range(2):
            P.dma("sync", thrb[:, g, :], bcast_rows(thrd[g], 128), writes=[f"thr{g}"])
        wgt = sb("wgt", [128, T, N_EXP])
        msk = sb("msk", [128, T, N_EXP])
        for t in range(T):
            g = 0 if t < 16 else 1
            P.dma("sync", wgt[:, t, :], affd[t], writes=[f"aff{t}"])
            P.tt("vector", msk[:, t, :], wgt[:, t, :], thrb[:, g, :], ALU.is_ge, reads=[f"aff{t}", f"thr{g}"], writes=[f"msk{t}"])
            P.tt("vector", wgt[:, t, :], wgt[:, t, :], msk[:, t, :], ALU.mult, reads=[f"aff{t}", f"msk{t}"], writes=[f"aff{t}"])
        acc = sb("acc", [128, T, 1024])
        for t in range(T):
            P.memset("gpsimd", acc[:, t, :], 0.0, writes=[f"acc{t}a", f"acc{t}b"])
        FG = 4
        wgr = ring("wgb", 2, [128, 8, FG * 128], BF16)
        wur = ring("wub", 2, [128, 8, FG * 128], BF16)
        wdr = ring("wdb", 2, [128, FG, 1024], BF16)
        stg = ring("stg", 4, [128, 1024])
        actr = ring("actT", 2, [128, FG, 512], BF16)
        sgr = ring("sg", 2, [128, 512])
        pgr = ring("pg", 2, [128, 512], F32, psum=True)
        pur = ring("pu", 2, [128, 512], F32, psum=True)
        pyr = ring("py", 2, [128, 512], F32, psum=True)
        tgs = [(s, min(512, K7_TOK - s)) for s in range(0, K7_TOK, 512)]
        ci = 0
        for e in range(n_exp):
            for f0 in range(0, NFC, FG):
                nf = min(FG, NFC - f0)
                wgb, wgk = wgr.next(); wub, wuk = wur.next(); wdb, wdk = wdr.next()
                for (src, dst, dk) in ((wgd, wgb, wgk), (wud, wub, wuk)):
                    for kp in range(0, 8, 2):
                        st, sk = stg.next()
                        q = "sync"
                        sv = st[:, 0:2 * nf * 128].rearrange("p (a f) -> p a f", a=2)
                        P.dma(q, sv, src[e, :, kp:kp + 2, f0 * 128:(f0 + nf) * 128], writes=[sk])
                        P.copy("gpsimd", dst[:, kp:kp + 2, 0:nf * 128], sv, reads=[sk], writes=[dk + f"k{kp}"])
                        ci += 1
                for fc in range(nf):
                    st, sk = stg.next()
                    P.dma("sync", st[:], wdd[e, :, f0 + fc, :], writes=[sk])
                    P.copy("gpsimd", wdb[:, fc, :], st[:], reads=[sk], writes=[wdk + f"f{fc}"])
                    ci += 1
                for (s0, ns) in tgs:
                    actT, ak = actr.next()
                    for fc in range(nf):
                        pg, pgk = pgr.next(); pu, puk = pur.next()
                        for kc in range(8):
                            P.mm(pg[:, 0:ns], wgb[:, kc, fc * 128:(fc + 1) * 128], h2T[:, kc, s0:s0 + ns], kc == 0, kc == 7,
                                 reads=h2keys + [wgk + f"k{kc - kc % 2}"], writes=[pgk])
                        for kc in range(8):
                            P.mm(pu[:, 0:ns], wub[:, kc, fc * 128:(fc + 1) * 128], h2T[:, kc, s0:s0 + ns], kc == 0, kc == 7,
                                 reads=h2keys + [wuk + f"k{kc - kc % 2}"], writes=[puk])
                        sg, sgk = sgr.next()
                        P.act(sg[:, 0:ns], pg[:, 0:ns], AF.Silu, reads=[pgk], writes=[sgk])
                        P.tt("vector", actT[:, fc, 0:ns], pu[:, 0:ns], sg[:, 0:ns], ALU.mult, reads=[puk, sgk], writes=[ak + f"f{fc}"])
                    for tt in range(s0 // 128, (s0 + ns) // 128):
                        for hf in range(2):
                            py, pyk = pyr.next()
                            for fc in range(nf):
                                P.mm(py[:], actT[:, fc, tt * 128 - s0:(tt + 1) * 128 - s0], wdb[:, fc, hf * 512:(hf + 1) * 512],
                                     fc == 0, fc == nf - 1, reads=[ak + f"f{fc}", wdk + f"f{fc}"], writes=[pyk])
                            ah = f"acc{tt}" + "ab"[hf]
                            P.stt("vector", acc[:, tt, hf * 512:(hf + 1) * 512], py[:], wgt[:, tt, e:e + 1],
                                  acc[:, tt, hf * 512:(hf + 1) * 512], ALU.mult, ALU.add, reads=[pyk, f"aff{tt}", ah], writes=[ah])
        rowt = sb("rowt", [128, 2, 1024])
        lnt = sb("lnt", [128, 2, 1024])
        for j in range(2):
            P.dma("sync", rowt[:, j, :], bcast_rows(rows[j], 128), writes=[f"row{j}"])
            P.dma("sync", lnt[:, j, :], bcast_rows(lnr[j], 128), writes=[f"ln{j}"])
        xr = ring("x", 2, [128, 1024])
        str_ = ring("st", 2, [128, 2, 6]); mvr = ring("mv", 2, [128, 2]); rsr = ring("rs", 2, [128, 1])
        for t in range(T):
            g = 0 if t < 16 else 1
            ak2 = [f"acc{t}a", f"acc{t}b"]
            x, xk = xr.next(); P.dma("sync", x[:], x1d[t], writes=[xk])
            P.tt("gpsimd", acc[:, t, :], acc[:, t, :], rowt[:, g, :], ALU.mult, reads=ak2 + [f"row{g}"], writes=ak2)
            P.stt("vector", acc[:, t, :], x[:], ALPHA, acc[:, t, :], ALU.mult, ALU.add, reads=[xk] + ak2, writes=ak2)
            st, _ = str_.next(); mv, _ = mvr.next(); rs, _ = rsr.next()
            emit_layernorm(P, acc[:, t, :], ak2[0], x[:], xk, st, mv, rs, f"lnB{t % 2}")
            P.tt("gpsimd", acc[:, t, :], x[:], lnt[:, 0, :], ALU.mult, reads=[xk, "ln0"], writes=ak2)
            P.tt("gpsimd", x[:], acc[:, t, :], lnt[:, 1, :], ALU.add, reads=ak2 + ["ln1"], writes=[xk])
            P.dma("sync", x2o[t], x[:], reads=[xk], writes=[xk], final=True)
        P.emit()
    return nc


def run_k7(h2, h2c, aff, affc, thr, thrc, x1, c1, mod_l, ln_w, ln_b, wg, wu, wd, n_exp=N_EXP):
    nc = build_k7(n_exp)
    B, n, D = x1.shape
    nctx = c1.shape[1]
    wgl = np.ascontiguousarray(wg[:n_exp].reshape(n_exp, 8, 128, D_FF).transpose(0, 2, 1, 3))
    wul = np.ascontiguousarray(wu[:n_exp].reshape(n_exp, 8, 128, D_FF).transpose(0, 2, 1, 3))
    wdl = np.ascontiguousarray(wd[:n_exp].reshape(n_exp, NFC, 128, D).transpose(0, 2, 1, 3))
    lnr = np.ascontiguousarray(np.stack([ln_w, ln_b]))
    in_maps = []
    for i in range(NCORES):
        b = i // 4
        h2s = tok_shard(h2, h2c, i).reshape(K7_TOK, D)
        h2T = np.ascontiguousarray(h2s.T.reshape(8, 128, K7_TOK).transpose(1, 0, 2))
        rows = np.ascontiguousarray(np.stack([mod_l[b, 5120:6144], mod_l[2, 5120:6144]]))
        in_maps.append({"h2T": h2T, "aff": tok_shard(aff, affc, i), "thr": np.ascontiguousarray(np.stack([thr[b], thrc[b]])),
                        "i_x1": tok_shard(x1, c1, i), "rows": rows, "lnr": lnr, "wg": wgl, "wu": wul, "wd": wdl})
    res = _run(nc, in_maps)
    return tok_unshard([r["o_x2"] for r in res.results], B, n, nctx)


RMS_EPS = 1e-6
NKEY = 8192 + 256
NKT = NKEY // 128
QCOLS = K1_TILES * 512


def build_k3():
    nc = bass.Bass("TRN2", target_bir_lowering=False)
    T = K1_TILES
    qd = nc.dram_tensor("qT", [128, QCOLS], F32, kind="ExternalInput").ap()
    kd = nc.dram_tensor("kT", [128, NKEY], F32, kind="ExternalInput").ap()
    vd = nc.dram_tensor("v", [128, NKT, 2, 64], F32, kind="ExternalInput").ap()
    cqd = nc.dram_tensor("cosq", [128, 16 * 512], F32, kind="ExternalInput").ap()
    sqd = nc.dram_tensor("sinq", [128, 16 * 512], F32, kind="ExternalInput").ap()
    ckd = nc.dram_tensor("cosk", [128, 8192], F32, kind="ExternalInput").ap()
    skd = nc.dram_tensor("sink", [128, 8192], F32, kind="ExternalInput").ap()
    cst = nc.dram_tensor("cst", [128, 258], F32, kind="ExternalInput").ap()
    yo = nc.dram_tensor("o_ya", [T, 128, 512], F32, kind="ExternalOutput").ap()
    with ExitStack() as es:
        P = Prog(nc, es)
        sb, ps, ring = mk_alloc(nc, es)
        cs = sb("cs", [128, 258])
        P.dma("sync", cs[:], cst, writes=["cs"])
        Rm, onesb, qw2, kw2 = cs[:, 0:128], cs[:, 128:256], cs[:, 256:257], cs[:, 257:258]
        bq = sb("bq", [128, 2])
        P.memset("vector", bq[:, 0:1], 64.0 * RMS_EPS, writes=["bq0"])
        P.memset("vector", bq[:, 1:2], RMS_EPS, writes=["bq1"])
        qr = sb("qr", [128, QCOLS], BF16)
        kr = sb("kr", [128, NKEY], BF16)
        vst = ring("vst", 2, [128, 2, 64])
        vaug = sb("vaug", [128, NKT, 2, 65], BF16)
        P.memset("vector", vaug[:], 1.0, writes=["vaug_init"])
        for kt in range(NKT):
            v, vk = vst.next()
            P.dma("sync", v[:], vd[:, kt], writes=[vk])
            P.copy("vector", vaug[:, kt, :, 0:64], v[:], reads=[vk, "vaug_init"], writes=[f"vaug{kt}"])
        xr = ring("px", 2, [128, 512]); sqr = ring("psq", 2, [128, 512]); sdr = ring("psd", 2, [128, 512])
        xnr = ring("pxn", 2, [128, 512]); cr = ring("pc", 2, [128, 512]); sr = ring("psn", 2, [128, 512])
        t1r = ring("pt1", 2, [128, 512]); t2r = ring("pt2", 2, [128, 512])
        bank = [(ps(f"mb{i}", [128, 512], F32), f"mb{i}") for i in range(8)]
        pssr = Ring(bank[0:1])
        prot = Ring(bank[1:2])

        def prep(src, dst, dkey, ncols, nrope, w2, bcol, scale, cosd, sind):
            for c0 in range(0, ncols, 512):
                n = min(512, ncols - c0)
                x, xk = xr.next(); P.dma("sync", x[:, 0:n], src[:, c0:c0 + n], writes=[xk])
                sq, sqk = sqr.next(); P.tt("vector", sq[:, 0:n], x[:, 0:n], x[:, 0:n], ALU.mult, reads=[xk], writes=[sqk])
                pss, pssk = pssr.next()
                P.mm(pss[:, 0:n], onesb, sq[:, 0:n], True, True, reads=[sqk, "cs"], writes=[pssk])
                sd, sdk = sdr.next()
                P.act(sd[:, 0:n], pss[:, 0:n], AF.Sqrt, reads=[pssk, f"bq{bcol}"], writes=[sdk], bias=bq[:, bcol:bcol + 1], scale=scale)
                P.gen("vector", lambda e, sd=sd, n=n: e.reciprocal(out=sd[:, 0:n], in_=sd[:, 0:n]), reads=[sdk], writes=[sdk])
                xn, xnk = xnr.next()
                P.stt("vector", xn[:, 0:n], x[:, 0:n], w2, sd[:, 0:n], ALU.mult, ALU.mult, reads=[xk, sdk, "cs"], writes=[xnk])
                dk = f"{dkey}{c0 // 512}"
                if c0 < nrope:
                    pr, prk = prot.next()
                    P.mm(pr[:, 0:n], Rm, xn[:, 0:n], True, True, reads=[xnk, "cs"], writes=[prk])
                    c, ck = cr.next(); P.dma("sync", c[:, 0:n], cosd[:, c0:c0 + n], writes=[ck])
                    s, sk = sr.next(); P.dma("sync", s[:, 0:n], sind[:, c0:c0 + n], writes=[sk])
                    t1, t1k = t1r.next(); P.tt("vector", t1[:, 0:n], xn[:, 0:n], c[:, 0:n], ALU.mult, reads=[xnk, ck], writes=[t1k])
                    t2, t2k = t2r.next(); P.tt("vector", t2[:, 0:n], pr[:, 0:n], s[:, 0:n], ALU.mult, reads=[prk, sk], writes=[t2k])
                    P.tt("vector", dst[:, c0:c0 + n], t1[:, 0:n], t2[:, 0:n], ALU.add, reads=[t1k, t2k], writes=[dk])
                else:
                    P.copy("vector", dst[:, c0:c0 + n], xn[:, 0:n], reads=[xnk], writes=[dk])

        prep(kd, kr, "kr", NKEY, 8192, kw2, 1, 1.0 / 64.0, ckd, skd)
        prep(qd, qr, "qr", QCOLS, 16 * 512, qw2, 0, 1.0, cqd, sqd)
        LA = 1
        pstR = Ring(bank[0:4])
        accbank = {(par, g): bank[4 + par * 2 + g] for par in range(2) for g in range(2)}
        ptr = ring("PT", 6, [128, 512], BF16)
        yr = ring("yo", 2, [128, 512])
        rcr = ring("rc", 2, [128, 4])
        its = []
        for qt in range(T):
            kts = list(range(NKT)) if qt < 16 else [64, 65]
            for ii, kt in enumerate(kts):
                its.append((qt, ii, kt, len(kts)))
        pend = {}
        ycur = {}
        for idx in range(len(its) + LA):
            if idx < len(its):
                qt, ii, kt, nk = its[idx]
                pair = []
                for g in range(2):
                    pst, pstk = pstR.next()
                    P.mm(pst[:], kr[g * 64:(g + 1) * 64, kt * 128:(kt + 1) * 128], qr[g * 64:(g + 1) * 64, qt * 512:(qt + 1) * 512],
                         True, True, reads=[f"kr{kt // 4}", f"qr{qt}"], writes=[pstk])
                    pair.append((pst, pstk))
                pend[idx] = pair
            j0 = idx - LA
            if j0 < 0:
                continue
            qt, ii, kt, nk = its[j0]
            pair = pend.pop(j0)
            PTs = []
            for g in range(2):
                pst, pstk = pair[g]
                PT, PTk = ptr.next()
                P.act(PT[:], pst[:], AF.Exp, reads=[pstk], writes=[PTk])
                PTs.append((PT, PTk))
            for g in range(2):
                PT, PTk = PTs[g]
                pb, pbk = accbank[(qt % 2, g)]
                for j in range(4):
                    P.op("tensor", lambda e, pb=pb, PT=PT, j=j, kt=kt, g=g, first=(ii == 0 and j == 0), last=(ii == nk - 1): e.matmul(
                        pb[:, j * 128:j * 128 + 65], PT[:, j * 128:(j + 1) * 128], vaug[:, kt, g, :], start=first, stop=last,
                        skip_group_check=True), reads=[PTk, f"vaug{kt}"], writes=[pbk])
            if ii == nk - 1:
                ycur[qt] = yr.next()
                y, yk = ycur[qt]
                for g in range(2):
                    pb, pbk = accbank[(qt % 2, g)]
                    rc, rck = rcr.next()
                    for j in range(4):
                        po = pb[:, j * 128:j * 128 + 65]
                        P.gen("vector", lambda e, rc=rc, po=po, j=j: e.reciprocal(out=rc[:, j:j + 1], in_=po[:, 64:65]), reads=[pbk], writes=[rck + str(j)])
                        c0 = (g * 4 + j) * 64
                        P.ts("vector", y[:, c0:c0 + 64], po[:, 0:64], rc[:, j:j + 1], None, ALU.mult, reads=[pbk, rck + str(j)], writes=[yk + f"{g}{j}"])
                P.dma("sync", yo[qt], y[:], reads=[yk + f"{g_}{j}" for g_ in range(2) for j in range(4)], writes=[yk + "d"], final=True)
        P.emit()
    return nc


def rope_tables(n_lat, grid_w=64, theta=10000.0, hd=64):
    nf = hd // 4
    t = np.arange(n_lat)
    row = (t // grid_w).astype(np.float32)
    col = (t % grid_w).astype(np.float32)
    inv = (theta ** (-np.arange(nf, dtype=np.float32) / nf)).astype(np.float32)
    ar = row[:, None] * inv
    ac = col[:, None] * inv
    ang = np.concatenate([ar, ar, ac, ac], axis=-1)
    return np.cos(ang).astype(np.float32), np.sin(ang).astype(np.float32)


def rope_rot_matrix():
    R = np.zeros((64, 64), np.float32)
    for a in range(2):
        for f in range(16):
            R[a * 32 + 16 + f, a * 32 + f] = -1.0
            R[a * 32 + f, a * 32 + 16 + f] = 1.0
    return R


def run_k3(P_lat, P_ctx, qw, kw):
    nc = build_k3()
    B, n, _ = P_lat.shape
    nctx = P_ctx.shape[1]
    aq_l, ak_l, av_l = P_lat[..., 1040:1552], P_lat[..., 1552:1680], P_lat[..., 1680:1808]
    aq_c, ak_c, av_c = P_ctx[..., 1040:1552], P_ctx[..., 1552:1680], P_ctx[..., 1680:1808]
    cos, sin = rope_tables(n)
    R = rope_rot_matrix()
    Rm = np.zeros((128, 128), np.float32); Rm[:64, :64] = R; Rm[64:, 64:] = R
    ob = np.zeros((128, 128), np.float32); ob[:64, :64] = 1; ob[64:, 64:] = 1
    cst = np.concatenate([Rm, ob, np.tile(qw, 2)[:, None], np.tile(kw, 2)[:, None]], 1).astype(np.float32)
    cosk = np.ascontiguousarray(np.tile(cos.T, (2, 1))); sink = np.ascontiguousarray(np.tile(sin.T, (2, 1)))
    seg = n // 4
    in_maps = []
    for i in range(NCORES):
        b, s = i // 4, i % 4
        q = tok_shard(aq_l, aq_c, i).reshape(K1_TILES, 128, 2, 4, 64)
        qT = np.ascontiguousarray(q.transpose(2, 4, 0, 3, 1)).reshape(128, QCOLS)
        k = np.concatenate([ak_l[b], ak_c[b]], 0).reshape(NKEY, 2, 64)
        kT = np.ascontiguousarray(k.transpose(1, 2, 0)).reshape(128, NKEY)
        v = np.concatenate([av_l[b], av_c[b]], 0).reshape(NKT, 128, 2, 64)
        vv = np.ascontiguousarray(v.transpose(1, 0, 2, 3))
        cq = cos[s * seg:(s + 1) * seg].reshape(16, 128, 64)
        sq = sin[s * seg:(s + 1) * seg].reshape(16, 128, 64)
        cq = np.broadcast_to(cq.transpose(2, 0, 1)[None, :, :, None, :], (2, 64, 16, 4, 128)).reshape(128, 16 * 512)
        sq = np.broadcast_to(sq.transpose(2, 0, 1)[None, :, :, None, :], (2, 64, 16, 4, 128)).reshape(128, 16 * 512)
        in_maps.append({"qT": qT, "kT": kT, "v": vv, "cosq": np.ascontiguousarray(cq), "sinq": np.ascontiguousarray(sq),
                        "cosk": cosk, "sink": sink, "cst": cst})
    res = _run(nc, in_maps)
    return tok_unshard([r["o_ya"] for r in res.results], B, n, nctx)


NCH = NKT
NSEQ = NKEY
MASK_NEG = -30000.0


def build_k2():
    nc = bass.Bass("TRN2", target_bir_lowering=False)
    qpd = nc.dram_tensor("qpT", [64, NSEQ], F32, kind="ExternalInput").ap()
    kpd = nc.dram_tensor("kpT", [64, NSEQ], F32, kind="ExternalInput").ap()
    vd = nc.dram_tensor("v", [128, NCH, 64], F32, kind="ExternalInput").ap()
    od = nc.dram_tensor("og", [128, NCH, 64], F32, kind="ExternalInput").ap()
    gd = nc.dram_tensor("g4", [128, NCH, 4], F32, kind="ExternalInput").ap()
    gbd = nc.dram_tensor("gb", [NCH * 4], F32, kind="ExternalInput").ap()
    cwd = nc.dram_tensor("cw", [64, 8], F32, kind="ExternalInput").ap()
    nwd = nc.dram_tensor("nw", [64], F32, kind="ExternalInput").ap()
    cstd = nc.dram_tensor("cst", [128, 6, 128], F32, kind="ExternalInput").ap()
    yo = nc.dram_tensor("o_ym", [128, NCH, 64], F32, kind="ExternalOutput").ap()
    with ExitStack() as es:
        P = Prog(nc, es)
        sb, ps, ring = mk_alloc(nc, es)
        cst = sb("cst_s", [128, 6, 128])
        P.dma("sync", cst[:], cstd, writes=["cst"])
        Lm = [cst[:, 0, :], cst[:, 1, :]]
        ones, ident = cst[:, 2, :], cst[:, 3, :]
        mneg = [cst[:, 4, :], cst[:, 5, :]]
        idb = sb("idb", [128, 128], BF16)
        P.copy("vector", idb[:], ident, reads=["cst"], writes=["idb"])
        one1 = sb("one1", [128, 1])
        P.memset("vector", one1[:], 1.0, writes=["one1"])
        cw = sb("cw_s", [64, 8])
        P.dma("sync", cw[:], cwd, writes=["cw"])
        banks = [ps(f"bank{i}", [128, 512], F32) for i in range(8)]
        xin = sb("xin", [64, NSEQ]); cacc = sb("cacc", [64, NSEQ])
        qT = sb("qT_s", [64, NSEQ], BF16); kT = sb("kT_s", [64, NSEQ], BF16)
        segs = [(0, 256), (256, NSEQ)]
        for wi, (src, dst, post) in enumerate(((qpd, qT, 0.125), (kpd, kT, 1.0))):
            o = wi * 4
            half = NSEQ // 2
            P.dma("sync", xin[:, 0:half], src[:, 0:half], writes=["xin_a"])
            P.dma("gpsimd", xin[:, half:], src[:, half:], writes=["xin_b"])
            P.ts("vector", cacc[:], xin[:], cw[:, o + 1:o + 2], cw[:, o + 3:o + 4], ALU.mult, ALU.add,
                 reads=["xin_a", "xin_b", "cw"], writes=["cacc"])
            for (a, b) in segs:
                P.stt("vector", cacc[:, a + 1:b], xin[:, a:b - 1], cw[:, o:o + 1], cacc[:, a + 1:b], ALU.mult, ALU.add,
                      reads=["xin_a", "xin_b", "cw", "cacc"], writes=["cacc"])
                P.stt("vector", cacc[:, a:b - 1], xin[:, a + 1:b], cw[:, o + 2:o + 3], cacc[:, a:b - 1], ALU.mult, ALU.add,
                      reads=["xin_a", "xin_b", "cw", "cacc"], writes=["cacc"])
            P.act(cacc[:], cacc[:], AF.Silu, reads=["cacc"], writes=["cacc"])
            P.act(dst[:], cacc[:], AF.Copy, reads=["cacc"], writes=[f"T{wi}"], scale=post)
            P.memset("vector", xin[:, 0:1], 0.0, writes=["xin_a", "xin_b"]) if wi == 0 else None
        ktok = sb("ktok", [128, NCH, 64], BF16)
        for c0 in range(0, NCH, 8):
            n = min(8, NCH - c0)
            pb = banks[7]
            for c in range(c0, c0 + n):
                pt = pb[:, :].bitcast(BF16)[:, (c - c0) * 64:(c - c0 + 1) * 64]
                P.tr(pt, kT[:, c * 128:(c + 1) * 128], idb[0:64, 0:64], reads=["T1", "idb"], writes=["bank7"])
            P.copy("vector", ktok[:, c0:c0 + n, :], pb[:, :].bitcast(BF16)[:, 0:n * 64].rearrange("p (c d) -> p c d", d=64),
                   reads=["bank7"], writes=["ktok"])
        vf = sb("vf", [128, NCH, 65]); vb = sb("vb", [128, NCH, 65], BF16)
        P.memset("vector", vf[:], 1.0, writes=["vf"])
        vtmp = sb("vtmp", [128, NCH, 64])
        P.dma("sync", vtmp[:], vd, writes=["vtmp"])
        P.copy("vector", vf[:, :, 0:64], vtmp[:], reads=["vtmp", "vf"], writes=["vf"])
        P.copy("scalar", vb[:], vf[:], reads=["vf"], writes=["vb"])
        G = sb("G", [128, NCH, 4]); GB = sb("GB", [128, NCH, 4])
        P.dma("sync", G[:], gd, writes=["G"])
        P.dma("sync", GB[:].rearrange("p c g -> p (c g)"), bcast_rows(gbd, 128), writes=["GB"])
        P.tt("vector", G[:], G[:], GB[:], ALU.add, reads=["G", "GB"], writes=["G"])
        LF = sb("LF", [128, 2, NCH]); LI = sb("LI", [128, 2, NCH]); TA = sb("TA", [128, 2, NCH]); TB = sb("TB", [128, 2, NCH])
        for dd in range(2):
            P.copy("vector", LI[:, dd, :], G[:, :, 2 * dd], reads=["G"], writes=[f"LI{dd}"])
            P.copy("vector", TA[:, dd, :], G[:, :, 2 * dd + 1], reads=["G"], writes=["TA"])
        P.act(TB[:], TA[:], AF.Abs, reads=["TA"], writes=["TB"])
        P.act(TB[:], TB[:], AF.Exp, reads=["TB"], writes=["TB"], scale=-1.0)
        P.act(TB[:], TB[:], AF.Ln, reads=["TB", "one1"], writes=["TB"], bias=one1[:, 0:1], scale=1.0)
        P.ts("vector", TA[:], TA[:], 0.0, None, ALU.min, reads=["TA"], writes=["TA"])
        P.tt("vector", LF[:], TA[:], TB[:], ALU.subtract, reads=["TA", "TB"], writes=["LF"])
        BC = sb("BC", [128, 2, NCH]); TOT = sb("TOT", [128, 2, NCH]); AA = sb("AA", [128, 2, NCH])
        BD = sb("BD", [128, 2, NCH]); WW = sb("WW", [128, 2, NCH]); DEC = sb("DEC", [128, 2, NCH])
        b6 = banks[6]
        for dd in range(2):
            P.mm(b6[:, dd * NCH:(dd + 1) * NCH], Lm[dd], LF[:, dd, :], True, True, reads=["LF", "cst"], writes=["bank6"])
        P.mm(b6[:, 2 * NCH:4 * NCH], ones, LF[:].rearrange("p a c -> p (a c)"), True, True, reads=["LF", "cst"], writes=["bank6"])
        P.copy("vector", BC[:].rearrange("p a c -> p (a c)"), b6[:, 0:2 * NCH], reads=["bank6"], writes=["BC"])
        P.copy("vector", TOT[:].rearrange("p a c -> p (a c)"), b6[:, 2 * NCH:4 * NCH], reads=["bank6"], writes=["TOT"])
        P.act(AA[:], BC[:], AF.Exp, reads=["BC"], writes=["AA"])
        P.tt("vector", BD[:], LI[:], BC[:], ALU.subtract, reads=["LI0", "LI1", "BC"], writes=["BD"])
        P.tt("vector", WW[:], TOT[:], BD[:], ALU.add, reads=["TOT", "BD"], writes=["WW"])
        P.act(WW[:], WW[:], AF.Exp, reads=["WW"], writes=["WW"])
        P.act(DEC[:], TOT[:], AF.Exp, reads=["TOT"], writes=["DEC"])
        S = [sb(f"S{dd}", [64, 65]) for dd in range(2)]
        Sb = [sb(f"Sb{dd}", [64, 65], BF16) for dd in range(2)]
        for dd in range(2):
            P.memset("vector", S[dd][:], 0.0, writes=[f"S{dd}"])
            P.memset("vector", Sb[dd][:], 0.0, writes=[f"Sb{dd}"])
        hb = [sb(f"hb{dd}", [128, NCH, 64]) for dd in range(2)]
        lfr = ring("lfrep", 4, [128, 128]); dtr = ring("Dt", 4, [128, 128]); ptr = ring("PTm", 4, [128, 128], BF16)
        tmr = ring("tmpi", 2, [128, 65]); ttr = ring("tot", 2, [128, 65]); dnr = ring("den", 2, [128, 4])
        wvr = ring("wv", 2, [128, 65], BF16)
        pD = Ring([(banks[0], "bank0"), (banks[1], "bank1")])
        pST = Ring([(banks[2], "bank2"), (banks[3], "bank3")])
        pOI = Ring([(banks[4], "bank4"), (banks[5], "bank5")])
        order = [list(range(NCH)), [1, 0] + list(range(NCH - 1, 1, -1))]
        def stage_a(step, dd):
            c = order[dd][step]
            cs_ = slice(c * 128, (c + 1) * 128)
            lf, lfk = lfr.next()
            P.act(lf[:], ones, AF.Copy, reads=["cst", "LF"], writes=[lfk], scale=LF[:, dd, c:c + 1])
            pd, pdk = pD.next()
            P.mm(pd[:, 0:128], lf[:], Lm[dd], True, False, reads=[lfk, "cst"], writes=[pdk])
            P.mm(pd[:, 0:128], ident, mneg[dd], False, True, reads=["cst"], writes=[pdk])
            dt_, dtk = dtr.next()
            P.act(dt_[:], pd[:, 0:128], AF.Exp, reads=[pdk, "BD"], writes=[dtk], bias=BD[:, dd, c:c + 1], scale=1.0)
            pst, pstk = pST.next()
            P.mm(pst[:, 0:128], kT[:, cs_], qT[:, cs_], True, True, reads=["T0", "T1"], writes=[pstk])
            PT, PTk = ptr.next()
            P.tt("vector", PT[:], pst[:, 0:128], dt_[:], ALU.mult, reads=[pstk, dtk], writes=[PTk])
            return PT, PTk

        def stage_b(step, dd, PT, PTk):
            c = order[dd][step]
            cs_ = slice(c * 128, (c + 1) * 128)
            poi, poik = pOI.next()
            P.mm(poi[:, 0:65], PT[:], vb[:, c, :], True, True, reads=[PTk, "vb"], writes=[poik + "o"])
            P.mm(poi[:, 128:193], qT[:, cs_], Sb[dd][:], True, True, reads=["T0", f"Sb{dd}"], writes=[poik + "i"])
            tm, tmk = tmr.next()
            P.act(tm[:], poi[:, 128:193], AF.Copy, reads=[poik + "i", "AA"], writes=[tmk], scale=AA[:, dd, c:c + 1])
            tt_, ttk = ttr.next()
            P.tt("vector", tt_[:], poi[:, 0:65], tm[:], ALU.add, reads=[poik + "o", tmk], writes=[ttk])
            dn, dnk = dnr.next()
            P.ts("vector", dn[:, 0:1], tt_[:, 64:65], -1.0, None, ALU.mult, reads=[ttk], writes=[dnk])
            P.stt("vector", dn[:, 1:2], dn[:, 0:1], 1.0, tt_[:, 64:65], ALU.max, ALU.max, reads=[dnk, ttk], writes=[dnk])
            P.gen("vector", lambda e, dn=dn: e.reciprocal(out=dn[:, 2:3], in_=dn[:, 1:2]), reads=[dnk], writes=[dnk])
            P.act(hb[dd][:, c, :], tt_[:, 0:64], AF.Copy, reads=[ttk, dnk], writes=[f"hb{dd}_{c}"], scale=dn[:, 2:3])
            wv, wvk = wvr.next()
            P.act(wv[:], vf[:, c, :], AF.Copy, reads=["vf", "WW"], writes=[wvk], scale=WW[:, dd, c:c + 1])
            p7 = banks[7]
            P.mm(p7[0:64, 256 + dd * 128:256 + dd * 128 + 65], ktok[:, c, :], wv[:], True, True, reads=["ktok", wvk], writes=[f"b7s{dd}"])
            P.stt("vector", S[dd][:], S[dd][:], DEC[0:64, dd, c:c + 1], p7[0:64, 256 + dd * 128:256 + dd * 128 + 65], ALU.mult, ALU.add,
                  reads=[f"S{dd}", "DEC", f"b7s{dd}"], writes=[f"S{dd}"])
            P.copy("gpsimd", Sb[dd][:], S[dd][:], reads=[f"S{dd}"], writes=[f"Sb{dd}"])

        pts = {}
        for step in range(NCH + 1):
            if step < NCH:
                for dd in range(2):
                    pts[(step, dd)] = stage_a(step, dd)
            if step >= 1:
                for dd in range(2):
                    stage_b(step - 1, dd, *pts.pop((step - 1, dd)))

        hk = [f"hb{dd}_{c}" for dd in range(2) for c in range(NCH)]
        P.tt("vector", hb[0][:], hb[0][:], hb[1][:], ALU.add, reads=hk, writes=["hsum"])
        sq = hb[1]
        P.tt("vector", sq[:], hb[0][:], hb[0][:], ALU.mult, reads=["hsum"], writes=["hsq"])
        ssum = sb("ssum", [128, NCH])
        P.gen("vector", lambda e: e.reduce_sum(out=ssum[:], in_=sq[:], axis=AX.X), reads=["hsq"], writes=["ssum"])
        P.ts("vector", ssum[:], ssum[:], 1.0 / 64.0, RMS_EPS, ALU.mult, ALU.add, reads=["ssum"], writes=["ssum"])
        P.act(ssum[:], ssum[:], AF.Sqrt, reads=["ssum"], writes=["ssum"])
        P.gen("vector", lambda e: e.reciprocal(out=ssum[:], in_=ssum[:]), reads=["ssum"], writes=["ssum"])
        nw = sb("nw_s", [128, 64])
        P.dma("sync", nw[:], bcast_rows(nwd, 128), writes=["nw"])
        og = vtmp
        P.dma("sync", og[:], od, reads=["vf"], writes=["og"])
        P.act(og[:], og[:], AF.Sigmoid, reads=["og"], writes=["og"])
        for c in range(NCH):
            P.stt("vector", hb[0][:, c, :], hb[0][:, c, :], ssum[:, c:c + 1], nw[:], ALU.mult, ALU.mult,
                  reads=["hsum", "ssum", "nw"], writes=[f"hn{c}"])
        P.tt("vector", hb[0][:], hb[0][:], og[:], ALU.mult, reads=[f"hn{c}" for c in range(NCH)] + ["og"], writes=["ym"])
        P.dma("sync", yo, hb[0][:], reads=["ym"], final=True)
        P.emit()
    return nc


def run_k2(P_lat, P_ctx, conv_w, conv_b, gate_b, norm_w):
    nc = build_k2()
    B, n, _ = P_lat.shape
    nctx = P_ctx.shape[1]
    s_idx, j_idx = np.meshgrid(np.arange(128), np.arange(128), indexing="ij")
    Lf = (s_idx <= j_idx).astype(np.float32); Lb = (s_idx >= j_idx).astype(np.float32)
    cst = np.stack([Lf, Lb, np.ones((128, 128), np.float32), np.eye(128, dtype=np.float32),
                    np.where(s_idx <= j_idx, 0.0, MASK_NEG).astype(np.float32),
                    np.where(s_idx >= j_idx, 0.0, MASK_NEG).astype(np.float32)], 1)
    in_maps = []
    for i in range(NCORES):
        b, h = i // 4, i % 4
        seq = np.concatenate([P_ctx[b], P_lat[b]], 0)
        qs, ks = slice(h * 64, (h + 1) * 64), slice(256 + h * 64, 256 + (h + 1) * 64)
        tm = lambda a: np.ascontiguousarray(a.reshape(NCH, 128, -1).transpose(1, 0, 2))
        gcols = [1024 + 0 * 8 + 0 * 4 + h, 1024 + 0 * 8 + 1 * 4 + h, 1024 + 1 * 8 + 0 * 4 + h, 1024 + 1 * 8 + 1 * 4 + h]
        gb = np.array([gate_b[0, 0, h], gate_b[0, 1, h], gate_b[1, 0, h], gate_b[1, 1, h]], np.float32)
        cw = np.concatenate([conv_w[:, qs].T, conv_b[qs][:, None], conv_w[:, ks].T, conv_b[ks][:, None]], 1).astype(np.float32)
        in_maps.append({"qpT": np.ascontiguousarray(seq[:, qs].T), "kpT": np.ascontiguousarray(seq[:, ks].T),
                        "v": tm(seq[:, 512 + h * 64:512 + (h + 1) * 64]), "og": tm(seq[:, 768 + h * 64:768 + (h + 1) * 64]),
                        "g4": tm(seq[:, gcols]), "gb": np.ascontiguousarray(np.tile(gb, NCH)), "cw": np.ascontiguousarray(cw),
                        "nw": np.ascontiguousarray(norm_w[h * 64:(h + 1) * 64]), "cst": np.ascontiguousarray(cst)})
    res = _run(nc, in_maps)
    ym_l = np.zeros((B, n, 256), np.float32); ym_c = np.zeros((B, nctx, 256), np.float32)
    for i in range(NCORES):
        b, h = i // 4, i % 4
        y = res.results[i]["o_ym"].transpose(1, 0, 2).reshape(NSEQ, 64)
        ym_c[b, :, h * 64:(h + 1) * 64] = y[:nctx]
        ym_l[b, :, h * 64:(h + 1) * 64] = y[nctx:]
    return ym_l, ym_c


HCH = 32
TWO_PI = 2.0 * np.pi
RND_MAGIC = 12582912.0


def fft_tables(N1):
    N2 = 128
    N = N1 * N2
    ar = np.arange
    c, s = np.cos, np.sin
    th = TWO_PI * ar(N1)[:, None] * ar(N1)[None] / N1
    F1c = np.concatenate([c(th), -s(th)], 1)
    th = TWO_PI * ar(N2)[:, None] * ar(N1)[None] / N
    twRR = np.concatenate([c(th), c(th)], 1); twII = np.concatenate([-s(th), -s(th)], 1)
    th = TWO_PI * ar(N2)[:, None] * ar(N2)[None] / N2
    F2re, F2im, nF2im = c(th), -s(th), s(th)
    G2c = np.concatenate([c(th), s(th)], 1); G2s = np.concatenate([-s(th), c(th)], 1)
    th = TWO_PI * ar(N1)[:, None] * ar(N2)[None] / N
    twcRR = np.concatenate([c(th), c(th)], 1); twcII = np.concatenate([s(th), s(th)], 1)
    th = TWO_PI * ar(N1)[:, None] * ar(N1 // 2)[None] / N1
    G1re, nG1im = c(th) / N, -s(th) / N
    f = lambda a: np.ascontiguousarray(a.astype(np.float32))
    return dict(F1c=f(F1c), twRR=f(twRR), twII=f(twII), F2re=f(F2re), F2im=f(F2im), nF2im=f(nF2im), G2c=f(G2c), G2s=f(G2s),
                twcRR=f(twcRR), twcII=f(twcII), G1re=f(G1re), nG1im=f(nG1im))


TAB_ORDER = ["F1c", "twRR", "twII", "F2re", "F2im", "nF2im", "G2c", "G2s", "twcRR", "twcII", "G1re", "nG1im"]


def hyena_consts(n):
    N = 2 * n
    tau = np.arange(N)
    pos = np.where(tau < n, tau, N - tau).astype(np.float32)
    t = (pos / np.float32(n)).astype(np.float32)
    bands = np.arange(1, 17, dtype=np.float32)
    ang = (np.float32(TWO_PI) * t[:, None] * bands).astype(np.float32)
    feats = np.concatenate([t[:, None], np.cos(ang), np.sin(ang)], -1).astype(np.float32)
    lt = abs(np.log(1e-2))
    deltas = np.linspace(lt / 1.5, lt / 0.3, 256, dtype=np.float32)
    win = (np.exp(-t[:, None] * deltas) + np.float32(0.05)).astype(np.float32)
    win[n] = 0.0
    return np.ascontiguousarray(feats.T), np.ascontiguousarray(win.T)


def interleave(gens, width):
    it = iter(gens)
    active = []
    while True:
        while len(active) < width:
            g = next(it, None)
            if g is None:
                break
            active.append(g)
        if not active:
            return
        for g in list(active):
            try:
                next(g)
            except StopIteration:
                active.remove(g)
        yield


def build_k4(sizes):
    nc = bass.Bass("TRN2", target_bir_lowering=False)
    B = 2
    dr = {}
    for si, n in enumerate(sizes):
        N1 = 2 * n // 128
        dr[si] = dict(
            u=nc.dram_tensor(f"u{si}", [3, HCH, B, n + 2], F32, kind="ExternalInput").ap(),
            feats=nc.dram_tensor(f"feats{si}", [33, 2 * n], F32, kind="ExternalInput").ap(),
            win=nc.dram_tensor(f"win{si}", [64, 2 * n], F32, kind="ExternalInput").ap(),
            taps=nc.dram_tensor(f"taps{si}", [64, 2 * n], F32, kind="ExternalOutput").ap(),
            out=nc.dram_tensor(f"o_yh{si}", [HCH, B, n], F32, kind="ExternalOutput").ap(),
            tabs={k: nc.dram_tensor(f"t{si}_{k}", list(v.shape), F32, kind="ExternalInput").ap()
                  for k, v in fft_tables(N1).items()})
    mlpd = nc.dram_tensor("mlp", [64, 64 + 64 + 128 + 4], F32, kind="ExternalInput").ap()
    cwd = nc.dram_tensor("cwv", [3 * HCH * 4], F32, kind="ExternalInput").ap()
    skd = nc.dram_tensor("skv", [2 * HCH], F32, kind="ExternalInput").ap()
    cstd = nc.dram_tensor("cst", [128, 256], F32, kind="ExternalInput").ap()
    with ExitStack() as es:
        P = Prog(nc, es)
        sb, ps, ring = mk_alloc(nc, es)
        banks = [(ps(f"bank{i}", [128, 512], F32), f"bank{i}") for i in range(8)]
        cst = sb("cst_s", [128, 256]); P.dma("sync", cst[:], cstd, writes=["cst"])
        ident, ones = cst[:, 0:128], cst[:, 128:256]
        mlp = sb("mlp_s", [64, 260]); P.dma("sync", mlp[:], mlpd, writes=["mlp"])
        w1, w2 = mlp[0:33, 0:64], mlp[:, 64:128]
        w3 = [mlp[:, 128:192], mlp[:, 192:256]]
        b1, f0, b2, f1 = (mlp[:, 256 + j:257 + j] for j in range(4))
        cwb = sb("cwb", [128, 3 * HCH * 4]); P.dma("sync", cwb[:], bcast_rows(cwd, 128), writes=["cwb"])
        skb = sb("skb", [128, 2 * HCH]); P.dma("sync", skb[:], bcast_rows(skd, 128), writes=["skb"])
        a1r = ring("a1", 4, [64, 512]); rrr = ring("rr", 4, [64, 512]); hhr = ring("hh", 4, [64, 512])
        ftr = ring("ft", 4, [33, 512]); wnr = ring("wn", 4, [64, 512]); tpr = ring("tp", 4, [64, 512])
        As_r = ring("As", 4, [128, 256]); t1r = ring("t1", 4, [128, 256]); t2r = ring("t2", 4, [128, 256])
        Br = ring("Bc", 4, [128, 256]); Yr = ring("Yc", 4, [128, 256]); Dr = ring("Dc", 4, [128, 256])
        pr4 = [ring(f"pp{j}", 4, [128, 128]) for j in range(4)]
        xir = ring("xi", 4, [64, 3, 130]); cvr = ring("cv", 4, [64, 3, 128]); tgr = ring("tg", 4, [64, 128])
        z1r = ring("z1", 4, [64, 128]); z2r = ring("z2", 4, [64, 128]); tlr = ring("tl", 4, [128, 128])
        bkP = Ring(banks[0:8])
        bkA = bkX = bkC = bkY = bkP

        def sin_layer(psrc, pk, n_, bias, freq, dst, dstk):
            a1, a1k = a1r.next(); rr, rrk = rrr.next()
            P.ts("vector", a1[:, 0:n_], psrc, bias, freq, ALU.add, ALU.mult, reads=[pk, "mlp"], writes=[a1k])
            yield
            P.ts("vector", rr[:, 0:n_], a1[:, 0:n_], 1.0 / TWO_PI, RND_MAGIC, ALU.mult, ALU.add, reads=[a1k], writes=[rrk])
            yield
            P.ts("vector", rr[:, 0:n_], rr[:, 0:n_], RND_MAGIC, -TWO_PI, ALU.subtract, ALU.mult, reads=[rrk], writes=[rrk])
            yield
            P.tt("vector", rr[:, 0:n_], rr[:, 0:n_], a1[:, 0:n_], ALU.add, reads=[rrk, a1k], writes=[rrk])
            yield
            P.ts("vector", rr[:, 0:n_], rr[:, 0:n_], np.pi, -np.pi, ALU.min, ALU.max, reads=[rrk], writes=[rrk])
            yield
            P.act(dst, rr[:, 0:n_], AF.Sin, reads=[rrk], writes=[dstk])
            yield

        def cmul(src, srck, n1p, W, tRR, tII, tk, dst, dstk):
            t1, t1k = t1r.next(); t2, t2k = t2r.next()
            P.tt("vector", t1[0:n1p, 0:2 * W], src, tRR, ALU.mult, reads=[srck, tk], writes=[t1k])
            yield
            P.tt("gpsimd", t2[0:n1p, 0:2 * W], src, tII, ALU.mult, reads=[srck, tk], writes=[t2k])
            yield
            P.tt("vector", dst[0:n1p, 0:W], t1[0:n1p, 0:W], t2[0:n1p, W:2 * W], ALU.subtract, reads=[t1k, t2k], writes=[dstk + "r"])
            yield
            P.tt("gpsimd", dst[0:n1p, W:2 * W], t2[0:n1p, 0:W], t1[0:n1p, W:2 * W], ALU.add, reads=[t1k, t2k], writes=[dstk + "i"])
            yield

        def size_body(si, n):
            N = 2 * n
            N1 = N // 128
            Kd = N1 // 2
            d = dr[si]
            T = {}
            for k in TAB_ORDER:
                shp = list(d["tabs"][k].shape)
                T[k] = sb(f"T{si}_{k}", shp)
                P.dma("sync", T[k][:], d["tabs"][k], writes=[f"tab{si}"] if k == TAB_ORDER[-1] else [f"tab{si}_{k}"])
                yield
            tabk = [f"tab{si}"] + [f"tab{si}_{k}" for k in TAB_ORDER[:-1]]
            CH = min(512, n)
            nchunk = N // CH
            l1p = sb(f"l1p{si}", [64, nchunk])
            def mlp_chain(ci):
                c0 = ci * CH
                dirn = 0 if c0 < n else 1
                ft, ftk = ftr.next(); P.dma("sync", ft[:, 0:CH], d["feats"][:, c0:c0 + CH], writes=[ftk])
                wn, wnk = wnr.next(); P.dma("sync", wn[:, 0:CH], d["win"][:, c0:c0 + CH], writes=[wnk])
                bk, bkk = bkA.next()
                P.mm(bk[0:64, 0:CH], w1, ft[:, 0:CH], True, True, reads=[ftk, "mlp"], writes=[bkk])
                yield
                h1, h1k = hhr.next()
                yield from sin_layer(bk[0:64, 0:CH], bkk, CH, b1, f0, h1[:, 0:CH], h1k)
                bk, bkk = bkX.next()
                P.mm(bk[0:64, 0:CH], w2, h1[:, 0:CH], True, True, reads=[h1k, "mlp"], writes=[bkk])
                yield
                h2, h2k = hhr.next()
                yield from sin_layer(bk[0:64, 0:CH], bkk, CH, b2, f1, h2[:, 0:CH], h2k)
                bk, bkk = bkC.next()
                P.mm(bk[0:64, 0:CH], w3[dirn], h2[:, 0:CH], True, True, reads=[h2k, "mlp"], writes=[bkk])
                yield
                tp, tpk = tpr.next()
                P.tt("vector", tp[:, 0:CH], bk[0:64, 0:CH], wn[:, 0:CH], ALU.mult, reads=[bkk, wnk], writes=[tpk])
                yield
                P.gen("vector", lambda e, tp=tp, ci=ci, CH=CH, l1p=l1p: e.reduce_sum(out=l1p[:, ci:ci + 1], in_=tp[:, 0:CH], axis=AX.X,
                                                                         apply_absolute_value=True), reads=[tpk], writes=[f"l1p{si}_{ci}"])
                yield
                P.dma("sync", d["taps"][:, c0:c0 + CH], tp[:, 0:CH], reads=[tpk], writes=[f"taps{si}"], final=True)
                yield
            yield from interleave([mlp_chain(ci) for ci in range(nchunk)], 2)
            l1 = sb(f"l1_{si}", [64, 2])
            P.gen("vector", lambda e, l1=l1, l1p=l1p: e.reduce_sum(out=l1[:, 0:1], in_=l1p[:], axis=AX.X),
                  reads=[f"l1p{si}_{ci}" for ci in range(nchunk)], writes=[f"l1{si}"])
            yield
            P.gen("vector", lambda e, l1=l1: e.reciprocal(out=l1[:, 1:2], in_=l1[:, 0:1]), reads=[f"l1{si}"], writes=[f"l1{si}"])
            yield
            dg = sb(f"dg{si}", [64, 64])
            P.ts("vector", dg[:], ident[0:64, 0:64], l1[:, 1:2], None, ALU.mult, reads=["cst", f"l1{si}"], writes=[f"dg{si}"])
            yield
            bk, bkk = bkY.next()
            P.mm(bk[:, 0:64], ones[0:64, :], dg[:], True, True, reads=["cst", f"dg{si}"], writes=[bkk])
            yield
            rl1b = sb(f"rl1b{si}", [128, 64])
            P.copy("vector", rl1b[:], bk[:, 0:64], reads=[bkk], writes=[f"rl1b{si}"])
            yield

            def fwd_fft(xt, xk, Krows):
                bA, bAk = bkA.next()
                P.mm(bA[:, 0:2 * N1], xt, T["F1c"][0:Krows, :], True, True, reads=[xk] + tabk, writes=[bAk])
                yield
                As, Ask = As_r.next()
                P.copy("scalar", As[:, 0:2 * N1], bA[:, 0:2 * N1], reads=[bAk], writes=[Ask])
                yield
                Bc, Bck = Br.next()
                yield from cmul(As[:, 0:2 * N1], Ask, 128, N1, T["twRR"][:], T["twII"][:], tabk[0], Bc, Bck)
                bX, bXk = bkX.next()
                Bre, Bim = Bc[:, 0:N1], Bc[:, N1:2 * N1]
                P.mm(bX[:, 0:N1], T["F2re"][:], Bre, True, False, reads=[Bck + "r"] + tabk, writes=[bXk])
                P.mm(bX[:, 0:N1], T["nF2im"][:], Bim, False, True, reads=[Bck + "i"] + tabk, writes=[bXk])
                yield
                P.mm(bX[:, N1:2 * N1], T["F2re"][:], Bim, True, False, reads=[Bck + "i"] + tabk, writes=[bXk])
                P.mm(bX[:, N1:2 * N1], T["F2im"][:], Bre, False, True, reads=[Bck + "r"] + tabk, writes=[bXk])
                yield
                return bX, bXk

            H = sb(f"H{si}", [128, 64, 2 * N1])
            def filt_chain(oc):
                tl, tlk = tlr.next()
                P.dma("sync", tl[0:N1, :], d["taps"][oc].rearrange("(a b) -> a b", b=128), reads=[f"taps{si}"], writes=[tlk])
                yield
                bX, bXk = yield from fwd_fft(tl[0:N1, :], tlk, N1)
                P.ts("vector", H[:, oc, :], bX[:, 0:2 * N1], rl1b[:, oc:oc + 1], None, ALU.mult, reads=[bXk, f"rl1b{si}"], writes=[f"H{si}_{oc}"])
                yield

            yield from interleave([filt_chain(oc) for oc in range(64)], 4)
            def long_conv(zt, zk, o, c):
                bX, bXk = yield from fwd_fft(zt, zk, Kd)
                oc = o * HCH + c
                Hre, Him = H[:, oc, 0:N1], H[:, oc, N1:2 * N1]
                hk = f"H{si}_{oc}"
                pp = [r.next() for r in pr4]
                P.tt("vector", pp[0][0][:, 0:N1], bX[:, 0:N1], Hre, ALU.mult, reads=[bXk, hk], writes=[pp[0][1]])
                yield
                P.tt("vector", pp[1][0][:, 0:N1], bX[:, N1:2 * N1], Him, ALU.mult, reads=[bXk, hk], writes=[pp[1][1]])
                yield
                P.tt("vector", pp[2][0][:, 0:N1], bX[:, 0:N1], Him, ALU.mult, reads=[bXk, hk], writes=[pp[2][1]])
                yield
                P.tt("vector", pp[3][0][:, 0:N1], bX[:, N1:2 * N1], Hre, ALU.mult, reads=[bXk, hk], writes=[pp[3][1]])
                yield
                Yc, Yck = Yr.next()
                P.tt("gpsimd", Yc[:, 0:N1], pp[0][0][:, 0:N1], pp[1][0][:, 0:N1], ALU.subtract, reads=[pp[0][1], pp[1][1]], writes=[Yck + "r"])
                yield
                P.tt("gpsimd", Yc[:, N1:2 * N1], pp[2][0][:, 0:N1], pp[3][0][:, 0:N1], ALU.add, reads=[pp[2][1], pp[3][1]], writes=[Yck + "i"])
                yield
                bC, bCk = bkC.next()
                P.mm(bC[0:N1, 0:256], Yc[:, 0:N1], T["G2c"][:], True, False, reads=[Yck + "r"] + tabk, writes=[bCk])
                P.mm(bC[0:N1, 0:256], Yc[:, N1:2 * N1], T["G2s"][:], False, True, reads=[Yck + "i"] + tabk, writes=[bCk])
                yield
                Cs, Csk = As_r.next()
                P.copy("scalar", Cs[0:N1, :], bC[0:N1, 0:256], reads=[bCk], writes=[Csk])
                yield
                Dc, Dck = Dr.next()
                yield from cmul(Cs[0:N1, :], Csk, N1, 128, T["twcRR"][:], T["twcII"][:], tabk[0], Dc, Dck)
                bY, bYk = bkY.next()
                P.mm(bY[0:Kd, 0:128], T["G1re"][:], Dc[0:N1, 0:128], True, False, reads=[Dck + "r"] + tabk, writes=[bYk])
                P.mm(bY[0:Kd, 0:128], T["nG1im"][:], Dc[0:N1, 128:256], False, True, reads=[Dck + "i"] + tabk, writes=[bYk])
                yield
                return bY, bYk

            def data_chain(c, b):
                xi, xik = xir.next()
                src = bass.AP(tensor=d["u"].tensor, offset=d["u"][0, c, b, 0].offset,
                              ap=[[128, Kd], [HCH * B * (n + 2), 3], [1, 130]])
                P.dma("gpsimd", xi[0:Kd], src, writes=[xik])
                yield
                cv, cvk = cvr.next()
                for p in range(3):
                    wo = (p * HCH + c) * 4
                    P.ts("vector", cv[0:Kd, p, :], xi[0:Kd, p, 1:129], cwb[0:Kd, wo + 1:wo + 2], cwb[0:Kd, wo + 3:wo + 4], ALU.mult, ALU.add,
                         reads=[xik, "cwb"], writes=[cvk + str(p)])
                    yield
                    P.stt("vector", cv[0:Kd, p, :], xi[0:Kd, p, 0:128], cwb[0:Kd, wo:wo + 1], cv[0:Kd, p, :], ALU.mult, ALU.add,
                          reads=[xik, "cwb", cvk + str(p)], writes=[cvk + str(p)])
                    yield
                    P.stt("vector", cv[0:Kd, p, :], xi[0:Kd, p, 2:130], cwb[0:Kd, wo + 2:wo + 3], cv[0:Kd, p, :], ALU.mult, ALU.add,
                          reads=[xik, "cwb", cvk + str(p)], writes=[cvk + str(p)])
                    yield
                bY, bYk = yield from long_conv(cv[0:Kd, 0, :], cvk + "0", 0, c)
                tg, tgk = tgr.next()
                P.stt("vector", tg[0:Kd, :], cv[0:Kd, 0, :], skb[0:Kd, c:c + 1], bY[0:Kd, 0:128], ALU.mult, ALU.add,
                      reads=[cvk + "0", "skb", bYk], writes=[tgk])
                yield
                z1, z1k = z1r.next()
                P.tt("gpsimd", z1[0:Kd, :], tg[0:Kd, :], cv[0:Kd, 1, :], ALU.mult, reads=[tgk, cvk + "1"], writes=[z1k])
                yield
                bY, bYk = yield from long_conv(z1[0:Kd, :], z1k, 1, c)
                tg, tgk = tgr.next()
                P.stt("vector", tg[0:Kd, :], z1[0:Kd, :], skb[0:Kd, HCH + c:HCH + c + 1], bY[0:Kd, 0:128], ALU.mult, ALU.add,
                      reads=[z1k, "skb", bYk], writes=[tgk])
                yield
                z2, z2k = z2r.next()
                P.tt("gpsimd", z2[0:Kd, :], tg[0:Kd, :], cv[0:Kd, 2, :], ALU.mult, reads=[tgk, cvk + "2"], writes=[z2k])
                yield
                P.dma("sync", d["out"][c, b].rearrange("(a b) -> a b", b=128), z2[0:Kd, :], reads=[z2k], writes=[z2k + "d"], final=True)
                yield
            yield from interleave([data_chain(c, b) for c in range(HCH) for b in range(B)], 4)

        for si, n in enumerate(sizes):
            for _ in size_body(si, n):
                pass
        P.emit()
    return nc


def run_k4(hy_list, conv_w, conv_b, fparams, skip):
    f_w1, f_b1, f_freq, f_w2, f_b2, f_w3 = fparams
    sizes = [h.shape[1] for h in hy_list]
    nc = build_k4(sizes)
    B = 2
    cst = np.concatenate([np.eye(128, dtype=np.float32), np.ones((128, 128), np.float32)], 1)
    consts = [hyena_consts(n) for n in sizes]
    tabs = [fft_tables(2 * n // 128) for n in sizes]
    w3r = f_w3.reshape(64, 2, 2, 256)
    in_maps = []
    for i in range(NCORES):
        cs = slice(i * HCH, (i + 1) * HCH)
        m = {"cst": cst}
        mlp = np.zeros((64, 260), np.float32)
        mlp[0:33, 0:64] = f_w1
        mlp[:, 64:128] = f_w2
        mlp[:, 128:192] = w3r[:, 0, :, cs].reshape(64, 64)
        mlp[:, 192:256] = w3r[:, 1, :, cs].reshape(64, 64)
        mlp[:, 256] = f_b1; mlp[:, 257] = f_freq[0]; mlp[:, 258] = f_b2; mlp[:, 259] = f_freq[1]
        m["mlp"] = mlp
        cw = np.zeros((3, HCH, 4), np.float32)
        for p in range(3):
            ch = slice(p * 256 + i * HCH, p * 256 + (i + 1) * HCH)
            cw[p, :, 0:3] = conv_w[:, ch].T
            cw[p, :, 3] = conv_b[ch]
        m["cwv"] = cw.reshape(-1)
        m["skv"] = np.ascontiguousarray(skip[:, cs]).reshape(-1)
        for si, (hy, n) in enumerate(zip(hy_list, sizes)):
            u = np.zeros((3, HCH, B, n + 2), np.float32)
            for p in range(3):
                u[p, :, :, 1:n + 1] = hy[:, :, p * 256 + i * HCH:p * 256 + (i + 1) * HCH].transpose(2, 0, 1)
            m[f"u{si}"] = u
            feats, win = consts[si]
            m[f"feats{si}"] = feats
            m[f"win{si}"] = np.ascontiguousarray(np.tile(win[cs], (2, 1)))
            for k, v in tabs[si].items():
                m[f"t{si}_{k}"] = v
        in_maps.append(m)
    res = _run(nc, in_maps)
    outs = []
    for si, n in enumerate(sizes):
        y = np.zeros((B, n, 256), np.float32)
        for i in range(NCORES):
            y[:, :, i * HCH:(i + 1) * HCH] = res.results[i][f"o_yh{si}"].transpose(1, 2, 0)
        outs.append(y)
    return outs, [np.concatenate([res.results[i][f"taps{si}"] for i in range(NCORES)], 0) for si in range(len(sizes))]


def kernel(x, c, ctx, c_ctx, w_mod, b_mod, w_in, mlstm_conv_w, mlstm_conv_b, mlstm_gate_b,
           mlstm_norm_w, attn_q_norm_w, attn_k_norm_w, hyena_conv_w, hyena_conv_b,
           hyena_f_w1, hyena_f_b1, hyena_f_freq, hyena_f_w2, hyena_f_b2, hyena_f_w3,
           hyena_skip, w_out, ln_mix_w, ln_mix_b, router_w, router_b,
           exp_w_gate, exp_w_up, exp_w_down, ln_ffn_w, ln_ffn_b):
    f = lambda a: np.asarray(a, dtype=np.float32)
    x, c, ctx, c_ctx = f(x), f(c), f(ctx), f(c_ctx)
    mod = run_k0(c, c_ctx, f(w_mod), f(b_mod))
    depth = w_in.shape[0]
    for l in range(depth):
        last = l == depth - 1
        P_lat, P_ctx = run_k1(x, ctx, mod[l], f(w_in[l]))
        ym_l, ym_c = run_k2(P_lat, P_ctx, f(mlstm_conv_w[l]), f(mlstm_conv_b[l]), f(mlstm_gate_b[l]), f(mlstm_norm_w[l]))
        ya_l, ya_c = run_k3(P_lat, P_ctx, f(attn_q_norm_w[l]), f(attn_k_norm_w[l]))
        hy = [P_lat[..., 1808:]] + ([] if last else [P_ctx[..., 1808:]])
        fpar = (f(hyena_f_w1[l]), f(hyena_f_b1[l]), f(hyena_f_freq[l]), f(hyena_f_w2[l]), f(hyena_f_b2[l]), f(hyena_f_w3[l]))
        yhs, _ = run_k4(hy, f(hyena_conv_w[l]), f(hyena_conv_b[l]), fpar, f(hyena_skip[l]))
        yh_c = np.zeros_like(ym_c) if last else yhs[1]
        ycat_l = np.concatenate([ym_l, ya_l, yhs[0]], -1)
        ycat_c = np.concatenate([ym_c, ya_c, yh_c], -1)
        x1, c1, h2, h2c, aff, affc = run_k5(ycat_l, ycat_c, x, ctx, mod[l], f(w_out[l]), f(ln_mix_w[l]), f(ln_mix_b[l]),
                                            f(router_w[l]), f(router_b[l]))
        thr, thrc = run_k6(aff, affc)
        x, ctx = run_moe(h2, h2c, aff, affc, thr, thrc, x1, c1, mod[l], f(ln_ffn_w[l]), f(ln_ffn_b[l]),
                         f(exp_w_gate[l]), f(exp_w_up[l]), f(exp_w_down[l]))
    return x.astype(np.float32)


CAP_L, CAP_C = 1024, 32
SLOTS_B = CAP_L + CAP_C
NTOK_ALL = 2 * 8192 + 2 * 256


def build_k7g():
    nc = bass.Bass("TRN2", target_bir_lowering=False)
    T = 18
    TOK = T * 128
    h2d = nc.dram_tensor("h2all", [NTOK_ALL, 1024], BF16, kind="ExternalInput").ap()
    affLd = nc.dram_tensor("affL", [2, 2, 128, 66], F32, kind="ExternalInput").ap()
    thrd = nc.dram_tensor("thr8", [8], F32, kind="ExternalInput").ap()
    tidd = nc.dram_tensor("tid", [2, 128, 66, 2], F32, kind="ExternalInput").ap()
    iotad = nc.dram_tensor("iota", [128, CAP_L + 128], F32, kind="ExternalInput").ap()
    cstd = nc.dram_tensor("cst", [128, 448], F32, kind="ExternalInput").ap()
    wgd = nc.dram_tensor("wg", [2, 128, 8, D_FF], F32, kind="ExternalInput").ap()
    wud = nc.dram_tensor("wu", [2, 128, 8, D_FF], F32, kind="ExternalInput").ap()
    wdd = nc.dram_tensor("wd", [2, 128, NFC, 1024], F32, kind="ExternalInput").ap()
    Yo = nc.dram_tensor("o_Y", [2, TOK, 1024], BF16, kind="ExternalOutput").ap()
    posKo = nc.dram_tensor("o_pos", [2, 2, 128, 66], I32, kind="ExternalOutput").ap()
    with ExitStack() as es:
        P = Prog(nc, es)
        sb, ps, ring = mk_alloc(nc, es)
        banks = [(ps(f"bk{i}", [128, 512], F32), f"bk{i}") for i in range(8)]
        cst = sb("cst_s", [128, 448]); P.dma("sync", cst[:], cstd, writes=["cst"])
        Ust, ones, identf, Ust64 = cst[:, 0:128], cst[:, 128:256], cst[:, 256:384], cst[0:64, 384:448]
        idb = sb("idb", [128, 128], BF16); P.copy("vector", idb[:], identf, reads=["cst"], writes=["idb"])
        thrt = sb("thrt", [128, 8]); P.dma("sync", thrt[:], bcast_rows(thrd, 128), writes=["thrt"])
        iota = sb("iota_s", [128, CAP_L + 128]); P.dma("sync", iota[:], iotad, writes=["iota"])
        h2T = sb("h2T_s", [128, 8, TOK], BF16)
        acc = sb("acc", [128, T, 1024])
        tv = sb("tv", [128, T])
        Ar = ring("A", 2, [128, 66]); Mr = ring("M", 2, [128, 66]); wir = ring("wi", 2, [128, 66]); pfr = ring("pf", 2, [128, 66])
        m2r = ring("m2", 2, [128, 66]); pir = ring("pi", 2, [128, 66], I32)
        tdfr = ring("tdf", 2, [128, 66, 2]); TAr = ring("TA", 2, [128, 66, 5], BF16); spr = ring("sp", 2, [128, 2, 66]); l5r = ring("l5", 2, [128, 9, 5]); selr = ring("sel", 4, [128, 512], BF16); rowt = sb("rowt", [5, SLOTS_B]); lfr = ring("lf", 2, [128, 18])
        tcr = ring("tc", 2, [64, 1]); tbr = ring("tb", 2, [64, 128])
        lsr = ring("ls", 2, [128, 9], I32)
        xsr = ring("xs", 2, [128, 1024], BF16)
        FG = 3
        wgr = ring("wgb", 2, [128, 8, FG * 128], BF16); wur = ring("wub", 2, [128, 8, FG * 128], BF16)
        wdr = ring("wdb", 2, [128, FG, 1024], BF16); stg = ring("stg", 3, [128, 1024])
        actr = ring("actT", 2, [128, FG, 512], BF16); sgr = ring("sg", 2, [128, 512]); yor = ring("yrow", 2, [128, 1024], BF16)
        pgr = Ring(banks[0:2]); pur = Ring(banks[2:4]); pyr = Ring(banks[4:6])
        tgs = [(s, min(512, TOK - s)) for s in range(0, TOK, 512)]
        for e in range(2):
            P.memset("gpsimd", h2T[:], 0.0, writes=["h2T"])
            P.memset("vector", tv[:], 0.0, writes=["tv"])
            for t in range(T):
                P.memset("gpsimd", acc[:, t, :], 0.0, writes=[f"acc{t}a", f"acc{t}b"])
            for b in range(2):
                A, Ak = Ar.next(); P.dma("sync", A[:], affLd[e, b], writes=[Ak])
                M, Mk = Mr.next()
                to = (e * 2 + b) * 2
                P.ts("vector", M[:, 0:64], A[:, 0:64], thrt[:, to:to + 1], None, ALU.is_ge, reads=[Ak, "thrt"], writes=[Mk + "l"])
                P.ts("vector", M[:, 64:66], A[:, 64:66], thrt[:, to + 1:to + 2], None, ALU.is_ge, reads=[Ak, "thrt"], writes=[Mk + "c"])
                wi, wik = wir.next(); pf, pfk = pfr.next(); m2, m2k = m2r.next(); pi, pik = pir.next()
                for (c0, ncol, cap, base, sfx) in ((0, 64, CAP_L, 0, "l"), (64, 2, CAP_C, CAP_L, "c")):
                    bw, bwk = banks[6]
                    P.mm(bw[:, c0:c0 + ncol], Ust, M[:, c0:c0 + ncol], True, True, reads=["cst", Mk + sfx], writes=[bwk + sfx])
                    P.copy("scalar", wi[:, c0:c0 + ncol], bw[:, c0:c0 + ncol], reads=[bwk + sfx], writes=[wik + sfx])
                    bt, btk = banks[7]
                    P.mm(bt[0:ncol, c0:c0 + 1], M[:, c0:c0 + ncol], ones[:, 0:1], True, True, reads=["cst", Mk + sfx], writes=[btk + "t" + sfx])
                    tc, tck = tcr.next()
                    P.copy("vector", tc[0:ncol, :], bt[0:ncol, c0:c0 + 1], reads=[btk + "t" + sfx], writes=[tck])
                    tb, tbk = tbr.next()
                    P.ts("vector", tb[0:ncol, :], ones[0:ncol, :], tc[0:ncol, 0:1], None, ALU.mult, reads=["cst", tck], writes=[tbk])
                    P.mm(bt[:, 128 + c0:128 + c0 + ncol], tb[0:ncol, :], Ust64[0:ncol, 0:ncol], True, True, reads=[tbk, "cst"], writes=[btk + "o" + sfx])
                    sl = slice(c0, c0 + ncol)
                    P.tt("vector", pf[:, sl], bt[:, 128 + c0:128 + c0 + ncol], wi[:, sl], ALU.add, reads=[btk + "o" + sfx, wik + sfx], writes=[pfk + sfx])
                    P.ts("vector", m2[:, sl], pf[:, sl], float(cap) - 0.5, None, ALU.is_lt, reads=[pfk + sfx], writes=[m2k + sfx])
                    P.tt("vector", m2[:, sl], m2[:, sl], M[:, sl], ALU.mult, reads=[m2k + sfx, Mk + sfx], writes=[m2k + sfx])
                    P.ts("vector", pf[:, sl], pf[:, sl], float(base - SLOTS_B), None, ALU.add, reads=[pfk + sfx], writes=[pfk + sfx])
                    P.tt("vector", pf[:, sl], pf[:, sl], m2[:, sl], ALU.mult, reads=[pfk + sfx, m2k + sfx], writes=[pfk + sfx])
                    P.ts("vector", pf[:, sl], pf[:, sl], float(SLOTS_B), None, ALU.add, reads=[pfk + sfx], writes=[pfk + sfx])
                P.copy("vector", pi[:], pf[:], reads=[pfk + "l", pfk + "c"], writes=[pik])
                P.dma("sync", posKo[e, b], pi[:], reads=[pik], writes=[pik + "d"], final=True)
                TA, TAk = TAr.next()
                tdf, tdfk = tdfr.next()
                P.dma("sync", tdf[:], tidd[b], writes=[tdfk])
                P.copy("gpsimd", TA[:, :, 0:2], tdf[:], reads=[tdfk], writes=[TAk + "t"])
                sp, spk = spr.next()
                P.copy("vector", TA[:, :, 2], A[:], reads=[Ak], writes=[TAk + "a0"])
                P.copy("vector", sp[:, 0, :], TA[:, :, 2], reads=[TAk + "a0"], writes=[spk + "0"])
                P.tt("vector", sp[:, 1, :], A[:], sp[:, 0, :], ALU.subtract, reads=[Ak, spk + "0"], writes=[spk + "1"])
                P.copy("vector", TA[:, :, 3], sp[:, 1, :], reads=[spk + "1"], writes=[TAk + "a1"])
                P.copy("vector", sp[:, 0, :], TA[:, :, 3], reads=[TAk + "a1"], writes=[spk + "0"])
                P.tt("vector", sp[:, 1, :], sp[:, 1, :], sp[:, 0, :], ALU.subtract, reads=[spk + "1", spk + "0"], writes=[spk + "1"])
                P.copy("vector", TA[:, :, 4], sp[:, 1, :], reads=[spk + "1"], writes=[TAk + "a2"])
                tak = [TAk + "t", TAk + "a0", TAk + "a1", TAk + "a2"]
                pc, pck = banks[7]
                for piece, (s0_, ns_, js) in enumerate(((0, 512, range(64)), (512, 512, range(64)), (CAP_L, CAP_C, (64, 65)))):
                    sfx = "l" if piece < 2 else "c"
                    for jj, j in enumerate(js):
                        se, sek = selr.next()
                        P.ts("vector", se[:, 0:ns_], iota[:, s0_:s0_ + ns_], pf[:, j:j + 1], None, ALU.is_equal, reads=["iota", pfk + sfx], writes=[sek])
                        P.mm(pc[0:5, 0:ns_], TA[:, j, :], se[:, 0:ns_], jj == 0, jj == len(js) - 1, reads=[sek] + tak, writes=[pck + "row"])
                    P.copy("scalar", rowt[0:5, s0_:s0_ + ns_], pc[0:5, 0:ns_], reads=[pck + "row"], writes=[f"rowt{piece}"])
                pq, pqk = banks[6]
                for c in range(9):
                    ns_ = 128 if c < 8 else 32
                    P.tr(pq[0:ns_, 256 + 5 * c:256 + 5 * c + 5], rowt[0:5, c * 128:c * 128 + ns_], identf[0:5, 0:5],
                         reads=[f"rowt{c // 4}", "cst"], writes=[pqk + "q"])
                l5, l5k = l5r.next()
                P.copy("vector", l5[:].rearrange("p c t -> p (c t)"), pq[:, 256:301], reads=[pqk + "q"], writes=[l5k])
                lf, lfk = lfr.next()
                lf3 = lf[:].rearrange("p (c t) -> p c t", t=2)
                P.stt("vector", lf3[:, :, 0], l5[:, :, 0], 128.0, l5[:, :, 1], ALU.mult, ALU.add, reads=[l5k], writes=[lfk + "i"])
                P.tt("vector", lf3[:, :, 1], l5[:, :, 2], l5[:, :, 3], ALU.add, reads=[l5k], writes=[lfk + "v"])
                P.tt("vector", lf3[:, :, 1], lf3[:, :, 1], l5[:, :, 4], ALU.add, reads=[l5k, lfk + "v"], writes=[lfk + "v"])
                ls, lsk = lsr.next()
                P.copy("vector", ls[:], lf3[:, :, 0], reads=[lfk + "i"], writes=[lsk])
                for c in range(9):
                    npart = 128 if c < 8 else 32
                    p0 = 0
                    col0 = b * CAP_L + c * 128 if c < 8 else (16 + b) * 128
                    tcol = col0 // 128
                    P.copy("vector", tv[p0:p0 + npart, tcol:tcol + 1], lf[p0:p0 + npart, 2 * c + 1:2 * c + 2], reads=[lfk + "v", "tv"], writes=["tv"])
                    xs, xsk = xsr.next()
                    P.op("gpsimd", lambda en, xs=xs, ls=ls, c=c, npart=npart, p0=p0: en.indirect_dma_start(
                        out=xs[p0:p0 + npart, :], out_offset=None, in_=h2d[:, :],
                        in_offset=bass.IndirectOffsetOnAxis(ap=ls[p0:p0 + npart, c:c + 1], axis=0)), reads=[lsk], writes=[xsk], dma=True)
                    pb, pbk = banks[6]
                    pT = pb[:, :].bitcast(BF16)
                    for kc in range(8):
                        P.tr(pT[:, kc * 128:kc * 128 + npart], xs[p0:p0 + npart, kc * 128:(kc + 1) * 128], idb[p0:p0 + npart, p0:p0 + npart],
                             reads=[xsk, "idb"], writes=[pbk + "l", pbk + "c"])
                    P.copy("scalar", h2T[:, :, col0:col0 + npart], pT[:, 0:1024].rearrange("p (k s) -> p k s", k=8)[:, :, 0:npart],
                           reads=[pbk + "l", pbk + "c"], writes=["h2T"])
            ci = 0
            for f0 in range(0, NFC, FG):
                nf = min(FG, NFC - f0)
                wgb, wgk = wgr.next(); wub, wuk = wur.next(); wdb, wdk = wdr.next()
                for (src, dst, dk) in ((wgd, wgb, wgk), (wud, wub, wuk)):
                    for kp in range(0, 8, 2):
                        st, sk = stg.next()
                        sv = st[:, 0:2 * nf * 128].rearrange("p (a f) -> p a f", a=2)
                        P.dma("sync", sv, src[e, :, kp:kp + 2, f0 * 128:(f0 + nf) * 128], writes=[sk])
                        P.copy("gpsimd" if ci % 2 else "vector", dst[:, kp:kp + 2, 0:nf * 128], sv, reads=[sk], writes=[dk + f"k{kp}"])
                        ci += 1
                for fc in range(nf):
                    st, sk = stg.next()
                    P.dma("sync", st[:], wdd[e, :, f0 + fc, :], writes=[sk])
                    P.copy("gpsimd" if ci % 2 else "vector", wdb[:, fc, :], st[:], reads=[sk], writes=[wdk + f"f{fc}"])
                    ci += 1
                for (s0, ns) in tgs:
                    actT, ak = actr.next()
                    for fc in range(nf):
                        pg, pgk = pgr.next(); pu, puk = pur.next()
                        for kc in range(8):
                            P.mm(pg[:, 0:ns], wgb[:, kc, fc * 128:(fc + 1) * 128], h2T[:, kc, s0:s0 + ns], kc == 0, kc == 7,
                                 reads=["h2T", wgk + f"k{kc - kc % 2}"], writes=[pgk])
                        for kc in range(8):
                            P.mm(pu[:, 0:ns], wub[:, kc, fc * 128:(fc + 1) * 128], h2T[:, kc, s0:s0 + ns], kc == 0, kc == 7,
                                 reads=["h2T", wuk + f"k{kc - kc % 2}"], writes=[puk])
                        sg, sgk = sgr.next()
                        P.act(sg[:, 0:ns], pg[:, 0:ns], AF.Silu, reads=[pgk], writes=[sgk])
                        P.tt("vector", actT[:, fc, 0:ns], pu[:, 0:ns], sg[:, 0:ns], ALU.mult, reads=[puk, sgk], writes=[ak + f"f{fc}"])
                    for tt in range(s0 // 128, (s0 + ns) // 128):
                        for hf in range(2):
                            py, pyk = pyr.next()
                            for fc in range(nf):
                                P.mm(py[:], actT[:, fc, tt * 128 - s0:(tt + 1) * 128 - s0], wdb[:, fc, hf * 512:(hf + 1) * 512],
                                     fc == 0, fc == nf - 1, reads=[ak + f"f{fc}", wdk + f"f{fc}"], writes=[pyk])
                            ah = f"acc{tt}" + "ab"[hf]
                            P.tt("vector", acc[:, tt, hf * 512:(hf + 1) * 512], py[:], acc[:, tt, hf * 512:(hf + 1) * 512], ALU.add,
                                 reads=[pyk, ah], writes=[ah])
            for t in range(T):
                yr_, yrk = yor.next()
                P.act(yr_[:], acc[:, t, :], AF.Copy, reads=[f"acc{t}a", f"acc{t}b", "tv"], writes=[yrk], scale=tv[:, t:t + 1])
                P.dma("sync", Yo[e, t * 128:(t + 1) * 128, :], yr_[:], reads=[yrk], writes=[yrk], final=True)
        P.emit()
    return nc


def build_k8():
    nc = bass.Bass("TRN2", target_bir_lowering=False)
    T = K1_TILES
    Yb = [nc.dram_tensor(f"Yb{e}", [SLOTS_B + 1, 1024], BF16, kind="ExternalInput").ap() for e in range(N_EXP)]
    idxd = nc.dram_tensor("idx", [T, 128, N_EXP], I32, kind="ExternalInput").ap()
    x1d = nc.dram_tensor("i_x1", [T, 128, 1024], F32, kind="ExternalInput").ap()
    rows = nc.dram_tensor("rows", [2, 1024], F32, kind="ExternalInput").ap()
    lnr = nc.dram_tensor("lnr", [2, 1024], F32, kind="ExternalInput").ap()
    x2o = nc.dram_tensor("o_x2", [T, 128, 1024], F32, kind="ExternalOutput").ap()
    with ExitStack() as es:
        P = Prog(nc, es)
        sb, ps, ring = mk_alloc(nc, es)
        rowt = sb("rowt", [128, 2, 1024]); lnt = sb("lnt", [128, 2, 1024])
        for j in range(2):
            P.dma("sync", rowt[:, j, :], bcast_rows(rows[j], 128), writes=[f"row{j}"])
            P.dma("sync", lnt[:, j, :], bcast_rows(lnr[j], 128), writes=[f"ln{j}"])
        idr = ring("idx", 2, [128, N_EXP], I32)
        gr = ring("g", 8, [128, 1024], BF16)
        accr = ring("acc", 2, [128, 1024])
        xr = ring("x", 2, [128, 1024])
        str_ = ring("st", 2, [128, 2, 6]); mvr = ring("mv", 2, [128, 2]); rsr = ring("rs", 2, [128, 1])
        for t in range(T):
            g_ = 0 if t < 16 else 1
            ix, ixk = idr.next(); P.dma("sync", ix[:], idxd[t], writes=[ixk])
            acc, ack = accr.next()
            for e in range(N_EXP):
                gt, gk = gr.next()
                P.op("gpsimd", lambda en, gt=gt, ix=ix, e=e: en.indirect_dma_start(
                    out=gt[:, :], out_offset=None, in_=Yb[e][:, :], in_offset=bass.IndirectOffsetOnAxis(ap=ix[:, e:e + 1], axis=0)),
                    reads=[ixk], writes=[gk], dma=True)
                if e == 0:
                    P.copy("vector", acc[:], gt[:], reads=[gk], writes=[ack])
                else:
                    P.tt("vector", acc[:], acc[:], gt[:], ALU.add, reads=[ack, gk], writes=[ack])
            x, xk = xr.next(); P.dma("sync", x[:], x1d[t], writes=[xk])
            P.tt("gpsimd", acc[:], acc[:], rowt[:, g_, :], ALU.mult, reads=[ack, f"row{g_}"], writes=[ack])
            P.stt("vector", acc[:], x[:], ALPHA, acc[:], ALU.mult, ALU.add, reads=[xk, ack], writes=[ack])
            st, _ = str_.next(); mv, _ = mvr.next(); rs, _ = rsr.next()
            emit_layernorm(P, acc[:], ack, x[:], xk, st, mv, rs, f"lnC{t % 2}")
            P.tt("gpsimd", acc[:], x[:], lnt[:, 0, :], ALU.mult, reads=[xk, "ln0"], writes=[ack])
            P.tt("gpsimd", x[:], acc[:], lnt[:, 1, :], ALU.add, reads=[ack, "ln1"], writes=[xk])
            P.dma("sync", x2o[t], x[:], reads=[xk], writes=[xk], final=True)
        P.emit()
    return nc


def run_moe(h2, h2c, aff, affc, thr, thrc, x1, c1, mod_l, ln_w, ln_b, wg, wu, wd):
    B, n, D = x1.shape
    nctx = c1.shape[1]
    E = aff.shape[-1]
    h2all = np.ascontiguousarray(np.concatenate([h2.reshape(B * n, D), h2c.reshape(B * nctx, D)], 0))
    affall = np.concatenate([aff.reshape(B * n, E), affc.reshape(B * nctx, E)], 0)
    s_idx, j_idx = np.meshgrid(np.arange(128), np.arange(128), indexing="ij")
    cst = np.zeros((128, 448), np.float32)
    cst[:, 0:128] = (s_idx < j_idx); cst[:, 128:256] = 1.0; cst[:, 256:384] = np.eye(128); cst[0:64, 384:448] = (s_idx < j_idx)[:64, :64]
    tid = np.zeros((B, 128, 66), np.int64)
    iota = np.full((128, CAP_L + 128), -1.0, np.float32)
    iota[:, 0:CAP_L] = np.arange(CAP_L)
    iota[:, CAP_L:CAP_L + 32] = CAP_L + np.arange(32)
    for b in range(B):
        tid[b, :, 0:64] = (b * n + np.arange(n)).reshape(64, 128).T
        tid[b, :, 64:66] = (B * n + b * nctx + np.arange(nctx)).reshape(2, 128).T
    tid2 = np.ascontiguousarray(np.stack([tid // 128, tid % 128], -1).astype(np.float32))
    nc = build_k7g()
    in_maps = []
    for i in range(NCORES):
        es = [2 * i, 2 * i + 1]
        affL = np.zeros((2, B, 128, 66), np.float32)
        thr8 = np.zeros((2, B, 2), np.float32)
        for el, e in enumerate(es):
            for b in range(B):
                affL[el, b, :, 0:64] = aff[b, :, e].reshape(64, 128).T
                affL[el, b, :, 64:66] = affc[b, :, e].reshape(2, 128).T
                thr8[el, b] = (thr[b, e], thrc[b, e])
        lw = lambda w, kch: np.ascontiguousarray(w.reshape(2, kch, 128, w.shape[-1]).transpose(0, 2, 1, 3))
        in_maps.append({"h2all": h2all, "affL": affL,
                        "thr8": thr8.reshape(-1), "tid": tid2, "iota": iota, "cst": cst,
                        "wg": lw(wg[es], 8), "wu": lw(wu[es], 8), "wd": lw(wd[es], NFC)})
    res = _run(nc, in_maps)
    Yb = np.zeros((B, E, SLOTS_B + 1, D), h2all.dtype)
    posL = np.zeros((B, n, E), np.int32); posC = np.zeros((B, nctx, E), np.int32)
    for i in range(NCORES):
        Y = res.results[i]["o_Y"]; pos = res.results[i]["o_pos"]
        for el in range(2):
            e = 2 * i + el
            for b in range(B):
                Yb[b, e, 0:CAP_L] = Y[el, b * CAP_L:(b + 1) * CAP_L]
                Yb[b, e, CAP_L:SLOTS_B] = Y[el, (16 + b) * 128:(16 + b) * 128 + CAP_C]
                posL[b, :, e] = pos[el, b, :, 0:64].T.reshape(n)
                posC[b, :, e] = pos[el, b, :, 64:66].T.reshape(nctx)
    nc8 = build_k8()
    lnr = np.ascontiguousarray(np.stack([ln_w, ln_b]))
    in_maps = []
    for i in range(NCORES):
        b = i // 4
        rows = np.ascontiguousarray(np.stack([mod_l[b, 5120:6144], mod_l[2, 5120:6144]]))
        idx = tok_shard(posL, posC, i)
        idx[-1, 64:, :] = SLOTS_B
        m = {"idx": idx, "i_x1": tok_shard(x1, c1, i), "rows": rows, "lnr": lnr}
        for e in range(E):
            m[f"Yb{e}"] = np.ascontiguousarray(Yb[b, e])
        in_maps.append(m)
    res = _run(nc8, in_maps)
    return tok_unshard([r["o_x2"] for r in res.results], B, n, nctx)
```

```python
import numpy as np
from contextlib import ExitStack
import concourse.bass as bass
import concourse.mybir as mybir
from concourse.bass_utils import run_bass_kernel_spmd

F32 = mybir.dt.float32
BF16 = mybir.dt.bfloat16
I32 = mybir.dt.int32
AF = mybir.ActivationFunctionType
ALU = mybir.AluOpType
AX = mybir.AxisListType

NCORES = 8
STAGE_EXP = False
FUSE_WAIT = True
NO_RAW_SELF = False
SELF_SYNC = True


class Prog:
    ENGS = ("sync", "scalar", "vector", "gpsimd", "tensor")

    def __init__(self, nc, es, n_dma_sems=12):
        self.nc, self.es = nc, es
        self.ops = {e: [] for e in self.ENGS}
        self.esem = {}
        self.ecount = {}
        for e in ("scalar", "vector", "gpsimd", "tensor"):
            self.esem[e] = es.enter_context(nc.semaphore(f"sem_{e}"))
            self.ecount[e] = 0
        self.dpool = {}
        for q in ("sync", "scalar", "gpsimd"):
            self.dpool[q] = dict(
                sems=[es.enter_context(nc.semaphore(f"dsem_{q}_{i}")) for i in range(n_dma_sems)],
                cnt=[0] * n_dma_sems, nxt=0, know=[None] * n_dma_sems)
        self.semobj = {}
        self.lastw = {}
        self.readers = {}
        self.know = {e: {} for e in self.ENGS}
        self.final_tokens = []

    def _need(self, eng, tok, waits):
        sk, v, kn = tok
        if self.know[eng].get(sk, 0) >= v:
            return
        waits.append((sk, v))
        k = self.know[eng]
        for a, b in kn.items():
            if k.get(a, 0) < b:
                k[a] = b
        if k.get(sk, 0) < v:
            k[sk] = v

    def op(self, eng, fn, reads=(), writes=(), dma=False, final=False):
        waits = []
        toks = []
        own = None if dma else ("e", eng)
        for key in reads:
            t = self.lastw.get(key)
            if t is not None:
                toks.append((t, True))
        for key in writes:
            t = self.lastw.get(key)
            if t is not None:
                toks.append((t, False))
            toks.extend((r, False) for r in self.readers.get(key, ()))
        for t, raw in toks:
            if t[0] == own and (eng == "tensor" or not SELF_SYNC or (not raw and eng != "gpsimd") or (NO_RAW_SELF and eng in ("vector", "scalar"))):
                continue
            self._need(eng, t, waits)
        if dma:
            pool = self.dpool[eng]
            j = pool["nxt"]
            pool["nxt"] = (j + 1) % len(pool["sems"])
            if pool["cnt"][j] > 0:
                self._need(eng, (("d", eng, j), pool["cnt"][j], pool["know"][j]), waits)
            pool["cnt"][j] += 16
            sk = ("d", eng, j)
            self.semobj[sk] = pool["sems"][j]
            kn = dict(self.know[eng])
            pool["know"][j] = kn
            tok = (sk, pool["cnt"][j], kn)
            inc = (pool["sems"][j], 16)
        else:
            self.ecount[eng] += 1
            sk = ("e", eng)
            self.semobj[sk] = self.esem[eng]
            tok = (sk, self.ecount[eng], dict(self.know[eng]))
            inc = (self.esem[eng], 1)
        self.ops[eng].append((waits, fn, inc))
        for key in reads:
            self.readers.setdefault(key, []).append(tok)
        for key in writes:
            self.lastw[key] = tok
            self.readers[key] = []
        if final:
            self.final_tokens.append(tok)
        return tok

    def emit(self):
        waits = []
        for t in self.final_tokens:
            self._need("sync", t, waits)
        if waits:
            self.ops["sync"].append((waits, None, None))
        nc = self.nc
        with nc.Block() as block:
            def run(eng_name):
                def body(eng):
                    for waits, fn, inc in self.ops[eng_name]:
                        fused = FUSE_WAIT and fn is not None and len(waits) > 0
                        for sk, v in (waits[:-1] if fused else waits):
                            eng.wait_ge(self.semobj[sk], v)
                        if fn is not None:
                            ins = fn(eng)
                            if fused:
                                ins._wait_ge(self.semobj[waits[-1][0]], waits[-1][1])
                            ins.then_inc(inc[0], inc[1])
                return body
            block.sync(run("sync"))
            block.scalar(run("scalar"))
            block.vector(run("vector"))
            block.gpsimd(run("gpsimd"))
            block.tensor(run("tensor"))

    def dma(self, q, out, in_, reads=(), writes=(), final=False, **kw):
        return self.op(q, lambda e: e.dma_start(out=out, in_=in_, **kw), reads, writes, dma=True, final=final)

    def mm(self, out, lhsT, rhs, start, stop, reads=(), writes=()):
        return self.op("tensor", lambda e: e.matmul(out, lhsT, rhs, start=start, stop=stop), reads, writes)

    def act(self, out, in_, func, reads=(), writes=(), eng="scalar", **kw):
        return self.op(eng, lambda e: e.activation(out=out, in_=in_, func=func, **kw), reads, writes)

    def tt(self, eng, out, in0, in1, op, reads=(), writes=()):
        return self.op(eng, lambda e: e.tensor_tensor(out=out, in0=in0, in1=in1, op=op), reads, writes)

    def ts(self, eng, out, in0, s1, s2, op0, op1=None, reads=(), writes=(), accum_out=None):
        kw = {}
        if op1 is not None:
            kw["op1"] = op1
        if accum_out is not None:
            kw["accum_out"] = accum_out
        return self.op(eng, lambda e: e.tensor_scalar(out=out, in0=in0, scalar1=s1, scalar2=s2, op0=op0, **kw),
                       reads, writes)

    def stt(self, eng, out, in0, scalar, in1, op0, op1, reads=(), writes=()):
        return self.op(eng, lambda e: e.scalar_tensor_tensor(out=out, in0=in0, scalar=scalar, in1=in1,
                                                             op0=op0, op1=op1), reads, writes)

    def copy(self, eng, out, in_, reads=(), writes=()):
        if eng == "scalar":
            return self.op(eng, lambda e: e.activation(out=out, in_=in_, func=AF.Copy), reads, writes)
        return self.op(eng, lambda e: e.tensor_copy(out=out, in_=in_), reads, writes)

    def tr(self, out, in_, ident, reads=(), writes=()):
        return self.op("tensor", lambda e: e.transpose(out, in_, ident), reads, writes)

    def memset(self, eng, ap, val, writes=()):
        return self.op(eng, lambda e: e.memset(ap, val), (), writes)

    def gen(self, eng, f, reads=(), writes=()):
        return self.op(eng, f, reads, writes)


def _run(nc, in_maps):
    return run_bass_kernel_spmd(nc, in_maps, core_ids=list(range(NCORES)))


D_MODEL = 1024
DEPTH = 2
N_MOD = 6
MODC = N_MOD * D_MODEL // NCORES


def build_k0():
    nc = bass.Bass("TRN2", target_bir_lowering=False)
    cvT = nc.dram_tensor("cvT", [128, 8, 3], F32, kind="ExternalInput").ap()
    wm = nc.dram_tensor("wm", [DEPTH, 128, 8, MODC], F32, kind="ExternalInput").ap()
    bm = nc.dram_tensor("bm", [DEPTH, 3, MODC], F32, kind="ExternalInput").ap()
    out = nc.dram_tensor("mod", [DEPTH, 3, MODC], F32, kind="ExternalOutput").ap()
    with ExitStack() as es:
        P = Prog(nc, es)
        sb = lambda name, shape, dt=F32: es.enter_context(nc.sbuf_tensor(name, shape, dt))
        cv = sb("cv", [128, 8, 3])
        cs = sb("cs", [128, 8, 3])
        w = [sb(f"w{l}", [128, 8, MODC]) for l in range(DEPTH)]
        b = sb("b", [3, DEPTH, MODC])
        o = sb("o", [3, DEPTH, MODC])
        ps = [es.enter_context(nc.psum_tensor(f"ps{i}", [128, 512], F32)) for i in range(2)]
        P.dma("sync", cv[:], cvT, writes=["cv"])
        for l in range(DEPTH):
            P.dma("sync" if l == 0 else "gpsimd", w[l][:], wm[l], writes=[f"w{l}"])
            P.dma("sync", b[:, l, :], bm[l], writes=[f"b{l}"])
        P.act(cs[:], cv[:], AF.Silu, reads=["cv"], writes=["cs"])
        H = MODC // 2
        for l in range(DEPTH):
            for h in range(2):
                pt = ps[h]
                for kc in range(8):
                    P.mm(pt[0:3, 0:H], cs[:, kc, :], w[l][:, kc, h * H:(h + 1) * H], kc == 0, kc == 7,
                         reads=["cs", f"w{l}"], writes=[f"ps{h}"])
                P.op("vector", lambda e, l=l, h=h, pt=pt: e.tensor_tensor(
                    out=o[:, l, h * H:(h + 1) * H], in0=pt[0:3, 0:H], in1=b[:, l, h * H:(h + 1) * H], op=ALU.add),
                    reads=[f"ps{h}", f"b{l}"], writes=[f"o{l}{h}"])
            P.dma("sync", out[l], o[:, l, :], reads=[f"o{l}0", f"o{l}1"], final=True)
        P.emit()
    return nc


def run_k0(c, c_ctx, w_mod, b_mod):
    cv = np.concatenate([c, c_ctx[None]], 0)
    cvT = np.ascontiguousarray(cv.T.reshape(8, 128, 3).transpose(1, 0, 2))
    nc = build_k0()
    in_maps = []
    for i in range(NCORES):
        sl = slice(i * MODC, (i + 1) * MODC)
        wm = np.ascontiguousarray(w_mod[:, :, sl].reshape(DEPTH, 8, 128, MODC).transpose(0, 2, 1, 3))
        bm = np.ascontiguousarray(np.broadcast_to(b_mod[:, None, sl], (DEPTH, 3, MODC)))
        in_maps.append({"cvT": cvT, "wm": wm, "bm": bm})
    res = _run(nc, in_maps)
    return np.concatenate([r["mod"] for r in res.results], axis=-1)


class Ring:
    def __init__(self, items):
        self.items, self.i = items, 0

    def next(self):
        it = self.items[self.i % len(self.items)]
        self.i += 1
        return it


def mk_alloc(nc, es):
    def sb(name, shape, dt=F32):
        return es.enter_context(nc.sbuf_tensor(name, shape, dt))

    def ps(name, shape, dt=F32):
        return es.enter_context(nc.psum_tensor(name, shape, dt))

    def ring(name, n, shape, dt=F32, psum=False):
        return Ring([((ps if psum else sb)(f"{name}{i}", shape, dt), f"{name}{i}") for i in range(n)])
    return sb, ps, ring


def bcast_rows(ap1d, nparts):
    return bass.AP(tensor=ap1d.tensor, offset=ap1d.offset, ap=[[0, nparts]] + [list(x) for x in ap1d.ap])


LN_EPS = 1e-5


def emit_layernorm(P, x, xkey, xn, xnkey, st, mv, rs, skey, n=1024):
    for j in range(n // 512):
        P.gen("vector", lambda e, j=j: e.bn_stats(out=st[:, j, :], in_=x[:, j * 512:(j + 1) * 512]),
              reads=[xkey], writes=[skey + f"st{j}"])
    P.gen("vector", lambda e: e.bn_aggr(out=mv[:], in_=st[:]),
          reads=[skey + f"st{j}" for j in range(n // 512)], writes=[skey + "mv"])
    P.ts("vector", rs[:], mv[:, 1:2], LN_EPS, None, ALU.add, reads=[skey + "mv"], writes=[skey + "rs"])
    P.act(rs[:], rs[:], AF.Sqrt, reads=[skey + "rs"], writes=[skey + "rs"])
    P.gen("vector", lambda e: e.reciprocal(out=rs[:], in_=rs[:]), reads=[skey + "rs"], writes=[skey + "rs"])
    P.ts("vector", xn, x, mv[:, 0:1], rs[:, 0:1], ALU.subtract, ALU.mult,
         reads=[xkey, skey + "mv", skey + "rs"], writes=[xnkey])


N_IN = 2576
K1_TILES = 17


def build_k1():
    nc = bass.Bass("TRN2", target_bir_lowering=False)
    xt = nc.dram_tensor("xt", [K1_TILES, 128, 1024], F32, kind="ExternalInput").ap()
    modr = nc.dram_tensor("modr", [2, 2, 1024], F32, kind="ExternalInput").ap()
    win = nc.dram_tensor("win", [128, 8, N_IN], F32, kind="ExternalInput").ap()
    identd = nc.dram_tensor("ident", [128, 128], F32, kind="ExternalInput").ap()
    out = nc.dram_tensor("p", [K1_TILES, 128, N_IN], F32, kind="ExternalOutput").ap()
    with ExitStack() as es:
        P = Prog(nc, es)
        sb, ps, ring = mk_alloc(nc, es)
        idf = sb("idf", [128, 128])
        idb = sb("idb", [128, 128], BF16)
        P.dma("sync", idf[:], identd, writes=["idf"])
        P.copy("vector", idb[:], idf[:], reads=["idf"], writes=["idb"])
        modt = sb("modt", [128, 2, 2, 1024])
        for g in range(2):
            for j in range(2):
                P.dma("sync", modt[:, g, j, :], bcast_rows(modr[g, j], 128), writes=[f"mod{g}{j}"])
            P.ts("vector", modt[:, g, 0, :], modt[:, g, 0, :], 1.0, None, ALU.add,
                 reads=[f"mod{g}0"], writes=[f"mod{g}0"])
        wbf = sb("wbf", [128, 8, N_IN], BF16)
        wst = ring("wst", 2, [128, N_IN])
        for kc in range(8):
            t, k = wst.next()
            P.dma("gpsimd" if kc % 2 else "sync", t[:], win[:, kc, :], writes=[k])
            P.copy("gpsimd" if kc % 2 else "vector", wbf[:, kc, :], t[:], reads=[k], writes=[f"wbf{kc}"])
        wkeys = [f"wbf{kc}" for kc in range(8)]
        xr = ring("x", 2, [128, 1024])
        xnr = ring("xn", 2, [128, 1024])
        h1r = ring("h1", 2, [128, 1024])
        hr = ring("h", 2, [128, 1024], BF16)
        hTr = ring("hT", 2, [128, 1024], BF16)
        orr = ring("o", 2, [128, N_IN])
        str_ = ring("st", 2, [128, 2, 6])
        mvr = ring("mv", 2, [128, 2])
        rsr = ring("rs", 2, [128, 1])
        pTr = ring("pT", 2, [128, 1024], BF16, psum=True)
        pmr = ring("pm", 4, [128, 512], F32, psum=True)
        ev = 0
        for t in range(K1_TILES):
            g = 0 if t < 16 else 1
            x, xk = xr.next()
            P.dma("sync", x[:], xt[t], writes=[xk])
            xn, xnk = xnr.next()
            st, _ = str_.next(); mv, _ = mvr.next(); rs, _ = rsr.next()
            emit_layernorm(P, x[:], xk, xn[:], xnk, st, mv, rs, f"ln{t % 2}")
            h1, h1k = h1r.next()
            P.tt("gpsimd", h1[:], xn[:], modt[:, g, 0, :], ALU.mult, reads=[xnk, f"mod{g}0"], writes=[h1k])
            h, hk = hr.next()
            P.tt("gpsimd", h[:], h1[:], modt[:, g, 1, :], ALU.add, reads=[h1k, f"mod{g}1"], writes=[hk])
            pT, pTk = pTr.next()
            for kc in range(8):
                P.tr(pT[:, kc * 128:(kc + 1) * 128], h[:, kc * 128:(kc + 1) * 128], idb[:],
                     reads=[hk, "idb"], writes=[pTk])
            hT, hTk = hTr.next()
            P.copy("scalar", hT[:], pT[:], reads=[pTk], writes=[hTk])
            o, ok = orr.next()
            for cg in range(6):
                c0 = cg * 512
                n = min(512, N_IN - c0)
                pm, pmk = pmr.next()
                for kc in range(8):
                    P.mm(pm[:, 0:n], hT[:, kc * 128:(kc + 1) * 128], wbf[:, kc, c0:c0 + n], kc == 0, kc == 7,
                         reads=[hTk, wkeys[kc]], writes=[pmk])
                P.copy("scalar" if ev % 2 else "vector", o[:, c0:c0 + n], pm[:, 0:n], reads=[pmk], writes=[ok + f"c{cg}"])
                ev += 1
            P.dma("gpsimd", out[t], o[:], reads=[ok + f"c{cg}" for cg in range(6)], writes=[ok + "dma"], final=True)
        P.emit()
    return nc


def lay_w(w, kchunks):
    return np.ascontiguousarray(w.reshape(kchunks, 128, w.shape[1]).transpose(1, 0, 2))


def run_k1(x, ctx, mod_l, w_in_l):
    nc = build_k1()
    B, n, D = x.shape
    seg = n // 4
    ctxf = ctx.reshape(-1, D)
    ident = np.eye(128, dtype=np.float32)
    win = lay_w(w_in_l, 8)
    in_maps = []
    for i in range(NCORES):
        b, s = i // 4, i % 4
        xt = np.zeros((K1_TILES * 128, D), np.float32)
        xt[:seg] = x[b, s * seg:(s + 1) * seg]
        xt[seg:seg + 64] = ctxf[i * 64:(i + 1) * 64]
        modr = np.stack([np.stack([mod_l[b, 1024:2048], mod_l[b, 0:1024]]),
                         np.stack([mod_l[2, 1024:2048], mod_l[2, 0:1024]])])
        in_maps.append({"xt": xt.reshape(K1_TILES, 128, D), "modr": np.ascontiguousarray(modr), "win": win,
                        "ident": ident})
    res = _run(nc, in_maps)
    P_lat = np.zeros((B, n, N_IN), np.float32)
    P_ctx = np.zeros((B * ctx.shape[1], N_IN), np.float32)
    for i in range(NCORES):
        b, s = i // 4, i % 4
        p = res.results[i]["p"].reshape(K1_TILES * 128, N_IN)
        P_lat[b, s * seg:(s + 1) * seg] = p[:seg]
        P_ctx[i * 64:(i + 1) * 64] = p[seg:seg + 64]
    return P_lat, P_ctx.reshape(B, ctx.shape[1], N_IN)


ALPHA = (2.0 * DEPTH) ** 0.25
N_EXP = 16


def build_k5():
    nc = bass.Bass("TRN2", target_bir_lowering=False)
    T = K1_TILES
    yt = nc.dram_tensor("yt", [T, 128, 1024], F32, kind="ExternalInput").ap()
    xt = nc.dram_tensor("xt", [T, 128, 1024], F32, kind="ExternalInput").ap()
    wout = nc.dram_tensor("wout", [128, 8, 1024], F32, kind="ExternalInput").ap()
    rows = nc.dram_tensor("rows", [2, 3, 1024], F32, kind="ExternalInput").ap()
    lnr = nc.dram_tensor("lnr", [2, 1024], F32, kind="ExternalInput").ap()
    rwd = nc.dram_tensor("rw", [128, 8, N_EXP], F32, kind="ExternalInput").ap()
    rbd = nc.dram_tensor("rb", [N_EXP], F32, kind="ExternalInput").ap()
    identd = nc.dram_tensor("ident", [128, 128], F32, kind="ExternalInput").ap()
    x1o = nc.dram_tensor("o_x1", [T, 128, 1024], F32, kind="ExternalOutput").ap()
    h2o = nc.dram_tensor("o_h2", [T, 128, 1024], BF16, kind="ExternalOutput").ap()
    affo = nc.dram_tensor("o_aff", [T, 128, N_EXP], F32, kind="ExternalOutput").ap()
    with ExitStack() as es:
        P = Prog(nc, es)
        sb, ps, ring = mk_alloc(nc, es)
        idf = sb("idf", [128, 128])
        idb = sb("idb", [128, 128], BF16)
        P.dma("sync", idf[:], identd, writes=["idf"])
        P.copy("vector", idb[:], idf[:], reads=["idf"], writes=["idb"])
        rowt = sb("rowt", [128, 2, 3, 1024])
        for g in range(2):
            for j in range(3):
                P.dma("sync", rowt[:, g, j, :], bcast_rows(rows[g, j], 128), writes=[f"row{g}{j}"])
            P.ts("vector", rowt[:, g, 1, :], rowt[:, g, 1, :], 1.0, None, ALU.add,
                 reads=[f"row{g}1"], writes=[f"row{g}1"])
        lnt = sb("lnt", [128, 2, 1024])
        for j in range(2):
            P.dma("sync", lnt[:, j, :], bcast_rows(lnr[j], 128), writes=[f"ln{j}"])
        rw = sb("rwt", [128, 8, N_EXP])
        P.dma("sync", rw[:], rwd, writes=["rw"])
        rb = sb("rbt", [128, N_EXP])
        P.dma("sync", rb[:], bcast_rows(rbd, 128), writes=["rb"])
        wbf = sb("wbf", [128, 8, 1024], BF16)
        wst = ring("wst", 2, [128, 1024])
        for kc in range(8):
            t, k = wst.next()
            P.dma("gpsimd" if kc % 2 else "sync", t[:], wout[:, kc, :], writes=[k])
            P.copy("gpsimd" if kc % 2 else "vector", wbf[:, kc, :], t[:], reads=[k], writes=[f"wbf{kc}"])
        wkeys = [f"wbf{kc}" for kc in range(8)]
        yr = ring("y", 2, [128, 1024]); ybr = ring("yb", 2, [128, 1024], BF16)
        yTr = ring("yT", 2, [128, 1024], BF16)
        xr = ring("x", 2, [128, 1024]); tmpr = ring("tmp", 2, [128, 1024]); rr = ring("r", 3, [128, 1024])
        xnr = ring("xn", 2, [128, 1024]); x1r = ring("x1_", 2, [128, 1024]); x1ar = ring("x1a", 2, [128, 1024])
        xn2r = ring("xn2", 2, [128, 1024]); h2fr = ring("h2f", 2, [128, 1024]); h2ar = ring("h2a", 2, [128, 1024])
        h2br = ring("h2b", 2, [128, 1024], BF16)
        h2Tr = ring("h2T", 2, [128, 1024])
        str_ = ring("st", 4, [128, 2, 6]); mvr = ring("mv", 4, [128, 2]); rsr = ring("rs", 4, [128, 1])
        lgr = ring("lg", 2, [128, N_EXP]); exr = ring("ex", 2, [128, N_EXP]); afr = ring("af", 2, [128, N_EXP])
        smr = ring("sm", 2, [128, 4])
        pTr = ring("pT", 1, [128, 1024], BF16, psum=True)
        pmr = ring("pm", 2, [128, 512], F32, psum=True)
        pTfr = ring("pTf", 1, [128, 1024], F32, psum=True)
        plr = ring("pl", 1, [128, N_EXP], F32, psum=True)
        lnc = 0
        lnc_box = [0]

        def stage_a(t):
            g = 0 if t < 16 else 1
            y, yk = yr.next(); P.dma("sync", y[:], yt[t], writes=[yk])
            x, xk = xr.next(); P.dma("sync", x[:], xt[t], writes=[xk])
            yb, ybk = ybr.next(); P.copy("gpsimd", yb[:], y[:], reads=[yk], writes=[ybk])
            pT, pTk = pTr.next()
            for kc in range(8):
                P.tr(pT[:, kc * 128:(kc + 1) * 128], yb[:, kc * 128:(kc + 1) * 128], idb[:], reads=[ybk, "idb"], writes=[pTk])
            yT, yTk = yTr.next(); P.copy("scalar", yT[:], pT[:], reads=[pTk], writes=[yTk])
            tmp, tmpk = tmpr.next()
            for hf in range(2):
                pm, pmk = pmr.next()
                for kc in range(8):
                    P.mm(pm[:], yT[:, kc * 128:(kc + 1) * 128], wbf[:, kc, hf * 512:(hf + 1) * 512], kc == 0, kc == 7,
                         reads=[yTk, wkeys[kc]], writes=[pmk])
                P.tt("vector", tmp[:, hf * 512:(hf + 1) * 512], pm[:], rowt[:, g, 0, hf * 512:(hf + 1) * 512], ALU.mult,
                     reads=[pmk, f"row{g}0"], writes=[tmpk + str(hf)])
            r, rk = rr.next()
            P.stt("vector", r[:], x[:], ALPHA, tmp[:], ALU.mult, ALU.add, reads=[xk, tmpk + "0", tmpk + "1"], writes=[rk])
            return r, rk

        def stage_b(t, r, rk):
            g = 0 if t < 16 else 1
            lnc = lnc_box[0]
            xn, xnk = xnr.next(); st, _ = str_.next(); mv, _ = mvr.next(); rs, _ = rsr.next()
            emit_layernorm(P, r[:], rk, xn[:], xnk, st, mv, rs, f"lnA{lnc % 4}"); lnc += 1
            x1a, x1ak = x1ar.next(); x1, x1k = x1r.next()
            P.tt("gpsimd", x1a[:], xn[:], lnt[:, 0, :], ALU.mult, reads=[xnk, "ln0"], writes=[x1ak])
            P.tt("gpsimd", x1[:], x1a[:], lnt[:, 1, :], ALU.add, reads=[x1ak, "ln1"], writes=[x1k])
            P.dma("gpsimd", x1o[t], x1[:], reads=[x1k], writes=[x1k + "d"], final=True)
            xn2, xn2k = xn2r.next(); st, _ = str_.next(); mv, _ = mvr.next(); rs, _ = rsr.next()
            emit_layernorm(P, x1[:], x1k, xn2[:], xn2k, st, mv, rs, f"lnA{lnc % 4}"); lnc += 1
            h2a, h2ak = h2ar.next(); h2f, h2fk = h2fr.next(); h2b, h2bk = h2br.next()
            P.tt("gpsimd", h2a[:], xn2[:], rowt[:, g, 1, :], ALU.mult, reads=[xn2k, f"row{g}1"], writes=[h2ak])
            P.tt("vector", h2f[:], h2a[:], rowt[:, g, 2, :], ALU.add, reads=[h2ak, f"row{g}2"], writes=[h2fk])
            P.copy("scalar", h2b[:], h2f[:], reads=[h2fk], writes=[h2bk])
            P.dma("sync", h2o[t], h2b[:], reads=[h2bk], writes=[h2bk + "d"], final=True)
            pTf, pTfk = pTfr.next()
            for kc in range(8):
                P.tr(pTf[:, kc * 128:(kc + 1) * 128], h2f[:, kc * 128:(kc + 1) * 128], idf[:], reads=[h2fk, "idf"], writes=[pTfk])
            h2T, h2Tk = h2Tr.next()
            P.copy("scalar", h2T[:, 0:512], pTf[:, 0:512], reads=[pTfk], writes=[h2Tk + "a"])
            P.copy("vector", h2T[:, 512:1024], pTf[:, 512:1024], reads=[pTfk], writes=[h2Tk + "b"])
            pl, plk = plr.next()
            for kc in range(8):
                P.mm(pl[:], h2T[:, kc * 128:(kc + 1) * 128], rw[:, kc, :], kc == 0, kc == 7,
                     reads=[h2Tk + "a", h2Tk + "b", "rw"], writes=[plk])
            lg, lgk = lgr.next(); ex, exk = exr.next(); af, afk = afr.next(); sm, smk = smr.next()
            P.tt("vector", lg[:], pl[:], rb[:], ALU.add, reads=[plk, "rb"], writes=[lgk])
            P.gen("vector", lambda e, sm=sm, lg=lg: e.reduce_max(out=sm[:, 0:1], in_=lg[:], axis=AX.X), reads=[lgk], writes=[smk + "m"])
            P.ts("vector", sm[:, 1:2], sm[:, 0:1], -1.0, None, ALU.mult, reads=[smk + "m"], writes=[smk + "n"])
            P.act(ex[:], lg[:], AF.Exp, reads=[lgk, smk + "n"], writes=[exk, smk + "s"], bias=sm[:, 1:2], scale=1.0,
                  accum_out=sm[:, 2:3])
            P.gen("vector", lambda e, sm=sm: e.reciprocal(out=sm[:, 3:4], in_=sm[:, 2:3]), reads=[smk + "s"], writes=[smk + "r"])
            P.ts("vector", af[:], ex[:], sm[:, 3:4], None, ALU.mult, reads=[exk, smk + "r"], writes=[afk])
            P.dma("gpsimd", affo[t], af[:], reads=[afk], writes=[afk + "d"], final=True)
            lnc_box[0] = lnc

        held = {}
        for t in range(T + 1):
            if t < T:
                held[t] = stage_a(t)
            if t >= 1:
                stage_b(t - 1, *held.pop(t - 1))

        P.emit()
    return nc


def tok_shard(lat, ctx, i):
    B, n, D = lat.shape
    seg = n // 4
    b, s = i // 4, i % 4
    out = np.zeros((K1_TILES * 128, D), lat.dtype)
    out[:seg] = lat[b, s * seg:(s + 1) * seg]
    out[seg:seg + 64] = ctx.reshape(-1, D)[i * 64:(i + 1) * 64]
    return out.reshape(K1_TILES, 128, D)


def tok_unshard(parts, B, n, nctx):
    D = parts[0].shape[-1]
    seg = n // 4
    lat = np.zeros((B, n, D), parts[0].dtype)
    ctx = np.zeros((B * nctx, D), parts[0].dtype)
    for i in range(NCORES):
        b, s = i // 4, i % 4
        p = parts[i].reshape(K1_TILES * 128, D)
        lat[b, s * seg:(s + 1) * seg] = p[:seg]
        ctx[i * 64:(i + 1) * 64] = p[seg:seg + 64]
    return lat, ctx.reshape(B, nctx, D)


def run_k5(ycat_l, ycat_c, x, ctx, mod_l, w_out_l, ln_w, ln_b, router_w_l, router_b_l):
    nc = build_k5()
    B, n, D = x.shape
    ident = np.eye(128, dtype=np.float32)
    wout = lay_w(w_out_l, 8)
    rw = lay_w(router_w_l, 8)
    lnr = np.ascontiguousarray(np.stack([ln_w, ln_b]))
    in_maps = []
    for i in range(NCORES):
        b = i // 4
        rows = np.stack([np.stack([mod_l[m, 2048:3072], mod_l[m, 4096:5120], mod_l[m, 3072:4096]]) for m in (b, 2)])
        in_maps.append({"yt": tok_shard(ycat_l, ycat_c, i), "xt": tok_shard(x, ctx, i), "wout": wout,
                        "rows": np.ascontiguousarray(rows), "lnr": lnr, "rw": rw,
                        "rb": np.ascontiguousarray(router_b_l), "ident": ident})
    res = _run(nc, in_maps)
    nctx = ctx.shape[1]
    x1, c1 = tok_unshard([r["o_x1"] for r in res.results], B, n, nctx)
    h2, h2c = tok_unshard([r["o_h2"] for r in res.results], B, n, nctx)
    aff, affc = tok_unshard([r["o_aff"] for r in res.results], B, n, nctx)
    return x1, c1, h2, h2c, aff, affc


K6_ITERS = 26


def build_k6(F_lat, k_lat, F_ctx, k_ctx):
    nc = bass.Bass("TRN2", target_bir_lowering=False)
    R = 32
    ald = nc.dram_tensor("al", [R, F_lat], F32, kind="ExternalInput").ap()
    acd = nc.dram_tensor("ac", [R, F_ctx], F32, kind="ExternalInput").ap()
    thro = nc.dram_tensor("thr", [R, 2], F32, kind="ExternalOutput").ap()
    with ExitStack() as es:
        P = Prog(nc, es)
        sb, ps, ring = mk_alloc(nc, es)
        res = sb("res", [R, 2])
        for pi, (src, F, k) in enumerate(((ald, F_lat, k_lat), (acd, F_ctx, k_ctx))):
            A = sb(f"A{pi}", [R, F])
            junk = sb(f"junk{pi}", [R, F], BF16)
            sc = sb(f"sc{pi}", [R, 8])
            lo, hi, mid, cnt, cond, t1, t2 = (sc[:, j:j + 1] for j in range(7))
            kk = f"p{pi}"
            P.dma("sync", A[:], src, writes=[kk + "A"])
            P.memset("vector", lo, 0.0, writes=[kk + "lo"])
            P.memset("vector", hi, 1.0, writes=[kk + "hi"])
            for it in range(K6_ITERS):
                P.tt("vector", mid, lo, hi, ALU.add, reads=[kk + "lo", kk + "hi"], writes=[kk + "mid"])
                P.ts("vector", mid, mid, 0.5, None, ALU.mult, reads=[kk + "mid"], writes=[kk + "mid"])
                P.ts("vector", junk[:], A[:], mid, None, ALU.is_ge, ALU.add, reads=[kk + "A", kk + "mid"],
                     writes=[kk + "junk", kk + "cnt"], accum_out=cnt)
                P.ts("vector", cond, cnt, float(k) - 0.5, None, ALU.is_ge, reads=[kk + "cnt"], writes=[kk + "cond"])
                P.tt("vector", t1, cond, mid, ALU.mult, reads=[kk + "cond", kk + "mid"], writes=[kk + "t1"])
                P.stt("vector", t2, cond, 2.0, mid, ALU.mult, ALU.add, reads=[kk + "cond", kk + "mid"], writes=[kk + "t2"])
                P.tt("vector", lo, lo, t1, ALU.max, reads=[kk + "lo", kk + "t1"], writes=[kk + "lo"])
                P.tt("vector", hi, hi, t2, ALU.min, reads=[kk + "hi", kk + "t2"], writes=[kk + "hi"])
            P.copy("vector", res[:, pi:pi + 1], lo, reads=[kk + "lo"], writes=[f"res{pi}"])
        P.dma("sync", thro, res[:], reads=["res0", "res1"], final=True)
        P.emit()
    return nc


def run_k6(aff, affc):
    B, n, E = aff.shape
    ncx = affc.shape[1]
    nc = build_k6(n, 2 * n // E, ncx, 2 * ncx // E)
    al = np.ascontiguousarray(aff.transpose(0, 2, 1).reshape(B * E, n))
    ac = np.ascontiguousarray(affc.transpose(0, 2, 1).reshape(B * E, ncx))
    res = _run(nc, [{"al": al, "ac": ac} for _ in range(NCORES)])
    thr = res.results[0]["thr"]
    return thr[:, 0].reshape(B, E), thr[:, 1].reshape(B, E)


D_FF = 2816
NFC = D_FF // 128
K7_TOK = K1_TILES * 128


def build_k7(n_exp=N_EXP):
    nc = bass.Bass("TRN2", target_bir_lowering=False)
    T = K1_TILES
    h2Td = nc.dram_tensor("h2T", [128, 8, K7_TOK], BF16, kind="ExternalInput").ap()
    affd = nc.dram_tensor("aff", [T, 128, N_EXP], F32, kind="ExternalInput").ap()
    thrd = nc.dram_tensor("thr", [2, N_EXP], F32, kind="ExternalInput").ap()
    x1d = nc.dram_tensor("i_x1", [T, 128, 1024], F32, kind="ExternalInput").ap()
    rows = nc.dram_tensor("rows", [2, 1024], F32, kind="ExternalInput").ap()
    lnr = nc.dram_tensor("lnr", [2, 1024], F32, kind="ExternalInput").ap()
    wgd = nc.dram_tensor("wg", [n_exp, 128, 8, D_FF], F32, kind="ExternalInput").ap()
    wud = nc.dram_tensor("wu", [n_exp, 128, 8, D_FF], F32, kind="ExternalInput").ap()
    wdd = nc.dram_tensor("wd", [n_exp, 128, NFC, 1024], F32, kind="ExternalInput").ap()
    x2o = nc.dram_tensor("o_x2", [T, 128, 1024], F32, kind="ExternalOutput").ap()
    with ExitStack() as es:
        P = Prog(nc, es)
        sb, ps, ring = mk_alloc(nc, es)
        h2T = sb("h2Ts", [128, 8, K7_TOK], BF16)
        for kc in range(8):
            P.dma("sync", h2T[:, kc, :], h2Td[:, kc, :], writes=["h2T"] if kc == 7 else [f"h2T_{kc}"])
        h2keys = ["h2T"] + [f"h2T_{kc}" for kc in range(7)]
        thrb = sb("thrb", [128, 2, N_EXP])
        for g in range(2):
            P.dma("sync", thrb[:, g, :], bcast_rows(thrd[g], 128), writes=[f"thr{g}"])
        wgt = sb("wgt", [128, T, N_EXP])
        msk = sb("msk", [128, T, N_EXP])
        for t in range(T):
            g = 0 if t < 16 else 1
            P.dma("sync", wgt[:, t, :], affd[t], writes=[f"aff{t}"])
            P.tt("vector", msk[:, t, :], wgt[:, t, :], thrb[:, g, :], ALU.is_ge, reads=[f"aff{t}", f"thr{g}"], writes=[f"msk{t}"])
            P.tt("vector", wgt[:, t, :], wgt[:, t, :], msk[:, t, :], ALU.mult, reads=[f"aff{t}", f"msk{t}"], writes=[f"aff{t}"])
        acc = sb("acc", [128, T, 1024])
        for t in range(T):
            P.memset("gpsimd", acc[:, t, :], 0.0, writes=[f"acc{t}a", f"acc{t}b"])
        FG = 4
        wgr = ring("wgb", 2, [128, 8, FG * 128], BF16)
        wur = ring("wub", 2, [128, 8, FG * 128], BF16)
        wdr = ring("wdb", 2, [128, FG, 1024], BF16)
        stg = ring("stg", 4, [128, 1024])
        actr = ring("actT", 2, [128, FG, 512], BF16)
        sgr = ring("sg", 2, [128, 512])
        pgr = ring("pg", 2, [128, 512], F32, psum=True)
        pur = ring("pu", 2, [128, 512], F32, psum=True)
        pyr = ring("py", 2, [128, 512], F32, psum=True)
        tgs = [(s, min(512, K7_TOK - s)) for s in range(0, K7_TOK, 512)]
        ci = 0
        for e in range(n_exp):
            for f0 in range(0, NFC, FG):
                nf = min(FG, NFC - f0)
                wgb, wgk = wgr.next(); wub, wuk = wur.next(); wdb, wdk = wdr.next()
                for (src, dst, dk) in ((wgd, wgb, wgk), (wud, wub, wuk)):
                    for kp in range(0, 8, 2):
                        st, sk = stg.next()
                        q = "sync"
                        sv = st[:, 0:2 * nf * 128].rearrange("p (a f) -> p a f", a=2)
                        P.dma(q, sv, src[e, :, kp:kp + 2, f0 * 128:(f0 + nf) * 128], writes=[sk])
                        P.copy("gpsimd", dst[:, kp:kp + 2, 0:nf * 128], sv, reads=[sk], writes=[dk + f"k{kp}"])
                        ci += 1
                for fc in range(nf):
                    st, sk = stg.next()
                    P.dma("sync", st[:], wdd[e, :, f0 + fc, :], writes=[sk])
                    P.copy("gpsimd", wdb[:, fc, :], st[:], reads=[sk], writes=[wdk + f"f{fc}"])
                    ci += 1
                for (s0, ns) in tgs:
                    actT, ak = actr.next()
                    for fc in range(nf):
                        pg, pgk = pgr.next(); pu, puk = pur.next()
                        for kc in range(8):
                            P.mm(pg[:, 0:ns], wgb[:, kc, fc * 128:(fc + 1) * 128], h2T[:, kc, s0:s0 + ns], kc == 0, kc == 7,
                                 reads=h2keys + [wgk + f"k{kc - kc % 2}"], writes=[pgk])
                        for kc in range(8):
                            P.mm(pu[:, 0:ns], wub[:, kc, fc * 128:(fc + 1) * 128], h2T[:, kc, s0:s0 + ns], kc == 0, kc == 7,
                                 reads=h2keys + [wuk + f"k{kc - kc % 2}"], writes=[puk])
                        sg, sgk = sgr.next()
                        P.act(sg[:, 0:ns], pg[:, 0:ns], AF.Silu, reads=[pgk], writes=[sgk])
                        P.tt("vector", actT[:, fc, 0:ns], pu[:, 0:ns], sg[:, 0:ns], ALU.mult, reads=[puk, sgk], writes=[ak + f"f{fc}"])
                    for tt in range(s0 // 128, (s0 + ns) // 128):
                        for hf in range(2):
                            py, pyk = pyr.next()
                            for fc in range(nf):
                                P.mm(py[:], actT[:, fc, tt * 128 - s0:(tt + 1) * 128 - s0], wdb[:, fc, hf * 512:(hf + 1) * 512],
                                     fc == 0, fc == nf - 1, reads=[ak + f"f{fc}", wdk + f"f{fc}"], writes=[pyk])
                            ah = f"acc{tt}" + "ab"[hf]
                            P.stt("vector", acc[:, tt, hf * 512:(hf + 1) * 512], py[:], wgt[:, tt, e:e + 1],
                                  acc[:, tt, hf * 512:(hf + 1) * 512], ALU.mult, ALU.add, reads=[pyk, f"aff{tt}", ah], writes=[ah])
        rowt = sb("rowt", [128, 2, 1024])
        lnt = sb("lnt", [128, 2, 1024])
        for j in range(2):
            P.dma("sync", rowt[:, j, :], bcast_rows(rows[j], 128), writes=[f"row{j}"])
            P.dma("sync", lnt[:, j, :], bcast_rows(lnr[j], 128), writes=[f"ln{j}"])
        xr = ring("x", 2, [128, 1024])
        str_ = ring("st", 2, [128, 2, 6]); mvr = ring("mv", 2, [128, 2]); rsr = ring("rs", 2, [128, 1])
        for t in range(T):
            g = 0 if t < 16 else 1
            ak2 = [f"acc{t}a", f"acc{t}b"]
            x, xk = xr.next(); P.dma("sync", x[:], x1d[t], writes=[xk])
            P.tt("gpsimd", acc[:, t, :], acc[:, t, :], rowt[:, g, :], ALU.mult, reads=ak2 + [f"row{g}"], writes=ak2)
            P.stt("vector", acc[:, t, :], x[:], ALPHA, acc[:, t, :], ALU.mult, ALU.add, reads=[xk] + ak2, writes=ak2)
            st, _ = str_.next(); mv, _ = mvr.next(); rs, _ = rsr.next()
            emit_layernorm(P, acc[:, t, :], ak2[0], x[:], xk, st, mv, rs, f"lnB{t % 2}")
            P.tt("gpsimd", acc[:, t, :], x[:], lnt[:, 0, :], ALU.mult, reads=[xk, "ln0"], writes=ak2)
            P.tt("gpsimd", x[:], acc[:, t, :], lnt[:, 1, :], ALU.add, reads=ak2 + ["ln1"], writes=[xk])
            P.dma("sync", x2o[t], x[:], reads=[xk], writes=[xk], final=True)
        P.emit()
    return nc


def run_k7(h2, h2c, aff, affc, thr, thrc, x1, c1, mod_l, ln_w, ln_b, wg, wu, wd, n_exp=N_EXP):
    nc = build_k7(n_exp)
    B, n, D = x1.shape
    nctx = c1.shape[1]
    wgl = np.ascontiguousarray(wg[:n_exp].reshape(n_exp, 8, 128, D_FF).transpose(0, 2, 1, 3))
    wul = np.ascontiguousarray(wu[:n_exp].reshape(n_exp, 8, 128, D_FF).transpose(0, 2, 1, 3))
    wdl = np.ascontiguousarray(wd[:n_exp].reshape(n_exp, NFC, 128, D).transpose(0, 2, 1, 3))
    lnr = np.ascontiguousarray(np.stack([ln_w, ln_b]))
    in_maps = []
    for i in range(NCORES):
        b = i // 4
        h2s = tok_shard(h2, h2c, i).reshape(K7_TOK, D)
        h2T = np.ascontiguousarray(h2s.T.reshape(8, 128, K7_TOK).transpose(1, 0, 2))
        rows = np.ascontiguousarray(np.stack([mod_l[b, 5120:6144], mod_l[2, 5120:6144]]))
        in_maps.append({"h2T": h2T, "aff": tok_shard(aff, affc, i), "thr": np.ascontiguousarray(np.stack([thr[b], thrc[b]])),
                        "i_x1": tok_shard(x1, c1, i), "rows": rows, "lnr": lnr, "wg": wgl, "wu": wul, "wd": wdl})
    res = _run(nc, in_maps)
    return tok_unshard([r["o_x2"] for r in res.results], B, n, nctx)


RMS_EPS = 1e-6
NKEY = 8192 + 256
NKT = NKEY // 128
QCOLS = K1_TILES * 512


def build_k3():
    nc = bass.Bass("TRN2", target_bir_lowering=False)
    T = K1_TILES
    qd = nc.dram_tensor("qT", [128, QCOLS], F32, kind="ExternalInput").ap()
    kd = nc.dram_tensor("kT", [128, NKEY], F32, kind="ExternalInput").ap()
    vd = nc.dram_tensor("v", [128, NKT, 2, 64], F32, kind="ExternalInput").ap()
    cqd = nc.dram_tensor("cosq", [128, 16 * 512], F32, kind="ExternalInput").ap()
    sqd = nc.dram_tensor("sinq", [128, 16 * 512], F32, kind="ExternalInput").ap()
    ckd = nc.dram_tensor("cosk", [128, 8192], F32, kind="ExternalInput").ap()
    skd = nc.dram_tensor("sink", [128, 8192], F32, kind="ExternalInput").ap()
    cst = nc.dram_tensor("cst", [128, 258], F32, kind="ExternalInput").ap()
    yo = nc.dram_tensor("o_ya", [T, 128, 512], F32, kind="ExternalOutput").ap()
    with ExitStack() as es:
        P = Prog(nc, es)
        sb, ps, ring = mk_alloc(nc, es)
        cs = sb("cs", [128, 258])
        P.dma("sync", cs[:], cst, writes=["cs"])
        Rm, onesb, qw2, kw2 = cs[:, 0:128], cs[:, 128:256], cs[:, 256:257], cs[:, 257:258]
        bq = sb("bq", [128, 2])
        P.memset("vector", bq[:, 0:1], 64.0 * RMS_EPS, writes=["bq0"])
        P.memset("vector", bq[:, 1:2], RMS_EPS, writes=["bq1"])
        qr = sb("qr", [128, QCOLS], BF16)
        kr = sb("kr", [128, NKEY], BF16)
        vst = ring("vst", 2, [128, 2, 64])
        vaug = sb("vaug", [128, NKT, 2, 65], BF16)
        P.memset("vector", vaug[:], 1.0, writes=["vaug_init"])
        for kt in range(NKT):
            v, vk = vst.next()
            P.dma("sync", v[:], vd[:, kt], writes=[vk])
            P.copy("vector", vaug[:, kt, :, 0:64], v[:], reads=[vk, "vaug_init"], writes=[f"vaug{kt}"])
        xr = ring("px", 2, [128, 512]); sqr = ring("psq", 2, [128, 512]); sdr = ring("psd", 2, [128, 512])
        xnr = ring("pxn", 2, [128, 512]); cr = ring("pc", 2, [128, 512]); sr = ring("psn", 2, [128, 512])
        t1r = ring("pt1", 2, [128, 512]); t2r = ring("pt2", 2, [128, 512])
        bank = [(ps(f"mb{i}", [128, 512], F32), f"mb{i}") for i in range(8)]
        pssr = Ring(bank[0:1])
        prot = Ring(bank[1:2])

        def prep(src, dst, dkey, ncols, nrope, w2, bcol, scale, cosd, sind):
            for c0 in range(0, ncols, 512):
                n = min(512, ncols - c0)
                x, xk = xr.next(); P.dma("sync", x[:, 0:n], src[:, c0:c0 + n], writes=[xk])
                sq, sqk = sqr.next(); P.tt("vector", sq[:, 0:n], x[:, 0:n], x[:, 0:n], ALU.mult, reads=[xk], writes=[sqk])
                pss, pssk = pssr.next()
                P.mm(pss[:, 0:n], onesb, sq[:, 0:n], True, True, reads=[sqk, "cs"], writes=[pssk])
                sd, sdk = sdr.next()
                P.act(sd[:, 0:n], pss[:, 0:n], AF.Sqrt, reads=[pssk, f"bq{bcol}"], writes=[sdk], bias=bq[:, bcol:bcol + 1], scale=scale)
                P.gen("vector", lambda e, sd=sd, n=n: e.reciprocal(out=sd[:, 0:n], in_=sd[:, 0:n]), reads=[sdk], writes=[sdk])
                xn, xnk = xnr.next()
                P.stt("vector", xn[:, 0:n], x[:, 0:n], w2, sd[:, 0:n], ALU.mult, ALU.mult, reads=[xk, sdk, "cs"], writes=[xnk])
                dk = f"{dkey}{c0 // 512}"
                if c0 < nrope:
                    pr, prk = prot.next()
                    P.mm(pr[:, 0:n], Rm, xn[:, 0:n], True, True, reads=[xnk, "cs"], writes=[prk])
                    c, ck = cr.next(); P.dma("sync", c[:, 0:n], cosd[:, c0:c0 + n], writes=[ck])
                    s, sk = sr.next(); P.dma("sync", s[:, 0:n], sind[:, c0:c0 + n], writes=[sk])
                    t1, t1k = t1r.next(); P.tt("vector", t1[:, 0:n], xn[:, 0:n], c[:, 0:n], ALU.mult, reads=[xnk, ck], writes=[t1k])
                    t2, t2k = t2r.next(); P.tt("vector", t2[:, 0:n], pr[:, 0:n], s[:, 0:n], ALU.mult, reads=[prk, sk], writes=[t2k])
                    P.tt("vector", dst[:, c0:c0 + n], t1[:, 0:n], t2[:, 0:n], ALU.add, reads=[t1k, t2k], writes=[dk])
                else:
                    P.copy("vector", dst[:, c0:c0 + n], xn[:, 0:n], reads=[xnk], writes=[dk])

        prep(kd, kr, "kr", NKEY, 8192, kw2, 1, 1.0 / 64.0, ckd, skd)
        prep(qd, qr, "qr", QCOLS, 16 * 512, qw2, 0, 1.0, cqd, sqd)
        LA = 1
        pstR = Ring(bank[0:4])
        accbank = {(par, g): bank[4 + par * 2 + g] for par in range(2) for g in range(2)}
        ptr = ring("PT", 6, [128, 512], BF16)
        yr = ring("yo", 2, [128, 512])
        rcr = ring("rc", 2, [128, 4])
        its = []
        for qt in range(T):
            kts = list(range(NKT)) if qt < 16 else [64, 65]
            for ii, kt in enumerate(kts):
                its.append((qt, ii, kt, len(kts)))
        pend = {}
        ycur = {}
        for idx in range(len(its) + LA):
            if idx < len(its):
                qt, ii, kt, nk = its[idx]
                pair = []
                for g in range(2):
                    pst, pstk = pstR.next()
                    P.mm(pst[:], kr[g * 64:(g + 1) * 64, kt * 128:(kt + 1) * 128], qr[g * 64:(g + 1) * 64, qt * 512:(qt + 1) * 512],
                         True, True, reads=[f"kr{kt // 4}", f"qr{qt}"], writes=[pstk])
                    pair.append((pst, pstk))
                pend[idx] = pair
            j0 = idx - LA
            if j0 < 0:
                continue
            qt, ii, kt, nk = its[j0]
            pair = pend.pop(j0)
            PTs = []
            for g in range(2):
                pst, pstk = pair[g]
                PT, PTk = ptr.next()
                P.act(PT[:], pst[:], AF.Exp, reads=[pstk], writes=[PTk])
                PTs.append((PT, PTk))
            for g in range(2):
                PT, PTk = PTs[g]
                pb, pbk = accbank[(qt % 2, g)]
                for j in range(4):
                    P.op("tensor", lambda e, pb=pb, PT=PT, j=j, kt=kt, g=g, first=(ii == 0 and j == 0), last=(ii == nk - 1): e.matmul(
                        pb[:, j * 128:j * 128 + 65], PT[:, j * 128:(j + 1) * 128], vaug[:, kt, g, :], start=first, stop=last,
                        skip_group_check=True), reads=[PTk, f"vaug{kt}"], writes=[pbk])
            if ii == nk - 1:
                ycur[qt] = yr.next()
                y, yk = ycur[qt]
                for g in range(2):
                    pb, pbk = accbank[(qt % 2, g)]
                    rc, rck = rcr.next()
                    for j in range(4):
                        po = pb[:, j * 128:j * 128 + 65]
                        P.gen("vector", lambda e, rc=rc, po=po, j=j: e.reciprocal(out=rc[:, j:j + 1], in_=po[:, 64:65]), reads=[pbk], writes=[rck + str(j)])
                        c0 = (g * 4 + j) * 64
                        P.ts("vector", y[:, c0:c0 + 64], po[:, 0:64], rc[:, j:j + 1], None, ALU.mult, reads=[pbk, rck + str(j)], writes=[yk + f"{g}{j}"])
                P.dma("sync", yo[qt], y[:], reads=[yk + f"{g_}{j}" for g_ in range(2) for j in range(4)], writes=[yk + "d"], final=True)
        P.emit()
    return nc


def rope_tables(n_lat, grid_w=64, theta=10000.0, hd=64):
    nf = hd // 4
    t = np.arange(n_lat)
    row = (t // grid_w).astype(np.float32)
    col = (t % grid_w).astype(np.float32)
    inv = (theta ** (-np.arange(nf, dtype=np.float32) / nf)).astype(np.float32)
    ar = row[:, None] * inv
    ac = col[:, None] * inv
    ang = np.concatenate([ar, ar, ac, ac], axis=-1)
    return np.cos(ang).astype(np.float32), np.sin(ang).astype(np.float32)


def rope_rot_matrix():
    R = np.zeros((64, 64), np.float32)
    for a in range(2):
        for f in range(16):
            R[a * 32 + 16 + f, a * 32 + f] = -1.0
            R[a * 32 + f, a * 32 + 16 + f] = 1.0
    return R


def run_k3(P_lat, P_ctx, qw, kw):
    nc = build_k3()
    B, n, _ = P_lat.shape
    nctx = P_ctx.shape[1]
    aq_l, ak_l, av_l = P_lat[..., 1040:1552], P_lat[..., 1552:1680], P_lat[..., 1680:1808]
    aq_c, ak_c, av_c = P_ctx[..., 1040:1552], P_ctx[..., 1552:1680], P_ctx[..., 1680:1808]
    cos, sin = rope_tables(n)
    R = rope_rot_matrix()
    Rm = np.zeros((128, 128), np.float32); Rm[:64, :64] = R; Rm[64:, 64:] = R
    ob = np.zeros((128, 128), np.float32); ob[:64, :64] = 1; ob[64:, 64:] = 1
    cst = np.concatenate([Rm, ob, np.tile(qw, 2)[:, None], np.tile(kw, 2)[:, None]], 1).astype(np.float32)
    cosk = np.ascontiguousarray(np.tile(cos.T, (2, 1))); sink = np.ascontiguousarray(np.tile(sin.T, (2, 1)))
    seg = n // 4
    in_maps = []
    for i in range(NCORES):
        b, s = i // 4, i % 4
        q = tok_shard(aq_l, aq_c, i).reshape(K1_TILES, 128, 2, 4, 64)
        qT = np.ascontiguousarray(q.transpose(2, 4, 0, 3, 1)).reshape(128, QCOLS)
        k = np.concatenate([ak_l[b], ak_c[b]], 0).reshape(NKEY, 2, 64)
        kT = np.ascontiguousarray(k.transpose(1, 2, 0)).reshape(128, NKEY)
        v = np.concatenate([av_l[b], av_c[b]], 0).reshape(NKT, 128, 2, 64)
        vv = np.ascontiguousarray(v.transpose(1, 0, 2, 3))
        cq = cos[s * seg:(s + 1) * seg].reshape(16, 128, 64)
        sq = sin[s * seg:(s + 1) * seg].reshape(16, 128, 64)
        cq = np.broadcast_to(cq.transpose(2, 0, 1)[None, :, :, None, :], (2, 64, 16, 4, 128)).reshape(128, 16 * 512)
        sq = np.broadcast_to(sq.transpose(2, 0, 1)[None, :, :, None, :], (2, 64, 16, 4, 128)).reshape(128, 16 * 512)
        in_maps.append({"qT": qT, "kT": kT, "v": vv, "cosq": np.ascontiguousarray(cq), "sinq": np.ascontiguousarray(sq),
                        "cosk": cosk, "sink": sink, "cst": cst})
    res = _run(nc, in_maps)
    return tok_unshard([r["o_ya"] for r in res.results], B, n, nctx)


NCH = NKT
NSEQ = NKEY
MASK_NEG = -30000.0


def build_k2():
    nc = bass.Bass("TRN2", target_bir_lowering=False)
    qpd = nc.dram_tensor("qpT", [64, NSEQ], F32, kind="ExternalInput").ap()
    kpd = nc.dram_tensor("kpT", [64, NSEQ], F32, kind="ExternalInput").ap()
    vd = nc.dram_tensor("v", [128, NCH, 64], F32, kind="ExternalInput").ap()
    od = nc.dram_tensor("og", [128, NCH, 64], F32, kind="ExternalInput").ap()
    gd = nc.dram_tensor("g4", [128, NCH, 4], F32, kind="ExternalInput").ap()
    gbd = nc.dram_tensor("gb", [NCH * 4], F32, kind="ExternalInput").ap()
    cwd = nc.dram_tensor("cw", [64, 8], F32, kind="ExternalInput").ap()
    nwd = nc.dram_tensor("nw", [64], F32, kind="ExternalInput").ap()
    cstd = nc.dram_tensor("cst", [128, 6, 128], F32, kind="ExternalInput").ap()
    yo = nc.dram_tensor("o_ym", [128, NCH, 64], F32, kind="ExternalOutput").ap()
    with ExitStack() as es:
        P = Prog(nc, es)
        sb, ps, ring = mk_alloc(nc, es)
        cst = sb("cst_s", [128, 6, 128])
        P.dma("sync", cst[:], cstd, writes=["cst"])
        Lm = [cst[:, 0, :], cst[:, 1, :]]
        ones, ident = cst[:, 2, :], cst[:, 3, :]
        mneg = [cst[:, 4, :], cst[:, 5, :]]
        idb = sb("idb", [128, 128], BF16)
        P.copy("vector", idb[:], ident, reads=["cst"], writes=["idb"])
        one1 = sb("one1", [128, 1])
        P.memset("vector", one1[:], 1.0, writes=["one1"])
        cw = sb("cw_s", [64, 8])
        P.dma("sync", cw[:], cwd, writes=["cw"])
        banks = [ps(f"bank{i}", [128, 512], F32) for i in range(8)]
        xin = sb("xin", [64, NSEQ]); cacc = sb("cacc", [64, NSEQ])
        qT = sb("qT_s", [64, NSEQ], BF16); kT = sb("kT_s", [64, NSEQ], BF16)
        segs = [(0, 256), (256, NSEQ)]
        for wi, (src, dst, post) in enumerate(((qpd, qT, 0.125), (kpd, kT, 1.0))):
            o = wi * 4
            half = NSEQ // 2
            P.dma("sync", xin[:, 0:half], src[:, 0:half], writes=["xin_a"])
            P.dma("gpsimd", xin[:, half:], src[:, half:], writes=["xin_b"])
            P.ts("vector", cacc[:], xin[:], cw[:, o + 1:o + 2], cw[:, o + 3:o + 4], ALU.mult, ALU.add,
                 reads=["xin_a", "xin_b", "cw"], writes=["cacc"])
            for (a, b) in segs:
                P.stt("vector", cacc[:, a + 1:b], xin[:, a:b - 1], cw[:, o:o + 1], cacc[:, a + 1:b], ALU.mult, ALU.add,
                      reads=["xin_a", "xin_b", "cw", "cacc"], writes=["cacc"])
                P.stt("vector", cacc[:, a:b - 1], xin[:, a + 1:b], cw[:, o + 2:o + 3], cacc[:, a:b - 1], ALU.mult, ALU.add,
                      reads=["xin_a", "xin_b", "cw", "cacc"], writes=["cacc"])
            P.act(cacc[:], cacc[:], AF.Silu, reads=["cacc"], writes=["cacc"])
            P.act(dst[:], cacc[:], AF.Copy, reads=["cacc"], writes=[f"T{wi}"], scale=post)
            P.memset("vector", xin[:, 0:1], 0.0, writes=["xin_a", "xin_b"]) if wi == 0 else None
        ktok = sb("ktok", [128, NCH, 64], BF16)
        for c0 in range(0, NCH, 8):
            n = min(8, NCH - c0)
            pb = banks[7]
            for c in range(c0, c0 + n):
                pt = pb[:, :].bitcast(BF16)[:, (c - c0) * 64:(c - c0 + 1) * 64]
                P.tr(pt, kT[:, c * 128:(c + 1) * 128], idb[0:64, 0:64], reads=["T1", "idb"], writes=["bank7"])
            P.copy("vector", ktok[:, c0:c0 + n, :], pb[:, :].bitcast(BF16)[:, 0:n * 64].rearrange("p (c d) -> p c d", d=64),
                   reads=["bank7"], writes=["ktok"])
        vf = sb("vf", [128, NCH, 65]); vb = sb("vb", [128, NCH, 65], BF16)
        P.memset("vector", vf[:], 1.0, writes=["vf"])
        vtmp = sb("vtmp", [128, NCH, 64])
        P.dma("sync", vtmp[:], vd, writes=["vtmp"])
        P.copy("vector", vf[:, :, 0:64], vtmp[:], reads=["vtmp", "vf"], writes=["vf"])
        P.copy("scalar", vb[:], vf[:], reads=["vf"], writes=["vb"])
        G = sb("G", [128, NCH, 4]); GB = sb("GB", [128, NCH, 4])
        P.dma("sync", G[:], gd, writes=["G"])
        P.dma("sync", GB[:].rearrange("p c g -> p (c g)"), bcast_rows(gbd, 128), writes=["GB"])
        P.tt("vector", G[:], G[:], GB[:], ALU.add, reads=["G", "GB"], writes=["G"])
        LF = sb("LF", [128, 2, NCH]); LI = sb("LI", [128, 2, NCH]); TA = sb("TA", [128, 2, NCH]); TB = sb("TB", [128, 2, NCH])
        for dd in range(2):
            P.copy("vector", LI[:, dd, :], G[:, :, 2 * dd], reads=["G"], writes=[f"LI{dd}"])
            P.copy("vector", TA[:, dd, :], G[:, :, 2 * dd + 1], reads=["G"], writes=["TA"])
        P.act(TB[:], TA[:], AF.Abs, reads=["TA"], writes=["TB"])
        P.act(TB[:], TB[:], AF.Exp, reads=["TB"], writes=["TB"], scale=-1.0)
        P.act(TB[:], TB[:], AF.Ln, reads=["TB", "one1"], writes=["TB"], bias=one1[:, 0:1], scale=1.0)
        P.ts("vector", TA[:], TA[:], 0.0, None, ALU.min, reads=["TA"], writes=["TA"])
        P.tt("vector", LF[:], TA[:], TB[:], ALU.subtract, reads=["TA", "TB"], writes=["LF"])
        BC = sb("BC", [128, 2, NCH]); TOT = sb("TOT", [128, 2, NCH]); AA = sb("AA", [128, 2, NCH])
        BD = sb("BD", [128, 2, NCH]); WW = sb("WW", [128, 2, NCH]); DEC = sb("DEC", [128, 2, NCH])
        b6 = banks[6]
        for dd in range(2):
            P.mm(b6[:, dd * NCH:(dd + 1) * NCH], Lm[dd], LF[:, dd, :], True, True, reads=["LF", "cst"], writes=["bank6"])
        P.mm(b6[:, 2 * NCH:4 * NCH], ones, LF[:].rearrange("p a c -> p (a c)"), True, True, reads=["LF", "cst"], writes=["bank6"])
        P.copy("vector", BC[:].rearrange("p a c -> p (a c)"), b6[:, 0:2 * NCH], reads=["bank6"], writes=["BC"])
        P.copy("vector", TOT[:].rearrange("p a c -> p (a c)"), b6[:, 2 * NCH:4 * NCH], reads=["bank6"], writes=["TOT"])
        P.act(AA[:], BC[:], AF.Exp, reads=["BC"], writes=["AA"])
        P.tt("vector", BD[:], LI[:], BC[:], ALU.subtract, reads=["LI0", "LI1", "BC"], writes=["BD"])
        P.tt("vector", WW[:], TOT[:], BD[:], ALU.add, reads=["TOT", "BD"], writes=["WW"])
        P.act(WW[:], WW[:], AF.Exp, reads=["WW"], writes=["WW"])
        P.act(DEC[:], TOT[:], AF.Exp, reads=["TOT"], writes=["DEC"])
        S = [sb(f"S{dd}", [64, 65]) for dd in range(2)]
        Sb = [sb(f"Sb{dd}", [64, 65], BF16) for dd in range(2)]
        for dd in range(2):
            P.memset("vector", S[dd][:], 0.0, writes=[f"S{dd}"])
            P.memset("vector", Sb[dd][:], 0.0, writes=[f"Sb{dd}"])
        hb = [sb(f"hb{dd}", [128, NCH, 64]) for dd in range(2)]
        lfr = ring("lfrep", 4, [128, 128]); dtr = ring("Dt", 4, [128, 128]); ptr = ring("PTm", 4, [128, 128], BF16)
        tmr = ring("tmpi", 2, [128, 65]); ttr = ring("tot", 2, [128, 65]); dnr = ring("den", 2, [128, 4])
        wvr = ring("wv", 2, [128, 65], BF16)
        pD = Ring([(banks[0], "bank0"), (banks[1], "bank1")])
        pST = Ring([(banks[2], "bank2"), (banks[3], "bank3")])
        pOI = Ring([(banks[4], "bank4"), (banks[5], "bank5")])
        order = [list(range(NCH)), [1, 0] + list(range(NCH - 1, 1, -1))]
        def stage_a(step, dd):
            c = order[dd][step]
            cs_ = slice(c * 128, (c + 1) * 128)
            lf, lfk = lfr.next()
            P.act(lf[:], ones, AF.Copy, reads=["cst", "LF"], writes=[lfk], scale=LF[:, dd, c:c + 1])
            pd, pdk = pD.next()
            P.mm(pd[:, 0:128], lf[:], Lm[dd], True, False, reads=[lfk, "cst"], writes=[pdk])
            P.mm(pd[:, 0:128], ident, mneg[dd], False, True, reads=["cst"], writes=[pdk])
            dt_, dtk = dtr.next()
            P.act(dt_[:], pd[:, 0:128], AF.Exp, reads=[pdk, "BD"], writes=[dtk], bias=BD[:, dd, c:c + 1], scale=1.0)
            pst, pstk = pST.next()
            P.mm(pst[:, 0:128], kT[:, cs_], qT[:, cs_], True, True, reads=["T0", "T1"], writes=[pstk])
            PT, PTk = ptr.next()
            P.tt("vector", PT[:], pst[:, 0:128], dt_[:], ALU.mult, reads=[pstk, dtk], writes=[PTk])
            return PT, PTk

        def stage_b(step, dd, PT, PTk):
            c = order[dd][step]
            cs_ = slice(c * 128, (c + 1) * 128)
            poi, poik = pOI.next()
            P.mm(poi[:, 0:65], PT[:], vb[:, c, :], True, True, reads=[PTk, "vb"], writes=[poik + "o"])
            P.mm(poi[:, 128:193], qT[:, cs_], Sb[dd][:], True, True, reads=["T0", f"Sb{dd}"], writes=[poik + "i"])
            tm, tmk = tmr.next()
            P.act(tm[:], poi[:, 128:193], AF.Copy, reads=[poik + "i", "AA"], writes=[tmk], scale=AA[:, dd, c:c + 1])
            tt_, ttk = ttr.next()
            P.tt("vector", tt_[:], poi[:, 0:65], tm[:], ALU.add, reads=[poik + "o", tmk], writes=[ttk])
            dn, dnk = dnr.next()
            P.ts("vector", dn[:, 0:1], tt_[:, 64:65], -1.0, None, ALU.mult, reads=[ttk], writes=[dnk])
            P.stt("vector", dn[:, 1:2], dn[:, 0:1], 1.0, tt_[:, 64:65], ALU.max, ALU.max, reads=[dnk, ttk], writes=[dnk])
            P.gen("vector", lambda e, dn=dn: e.reciprocal(out=dn[:, 2:3], in_=dn[:, 1:2]), reads=[dnk], writes=[dnk])
            P.act(hb[dd][:, c, :], tt_[:, 0:64], AF.Copy, reads=[ttk, dnk], writes=[f"hb{dd}_{c}"], scale=dn[:, 2:3])
            wv, wvk = wvr.next()
            P.act(wv[:], vf[:, c, :], AF.Copy, reads=["vf", "WW"], writes=[wvk], scale=WW[:, dd, c:c + 1])
            p7 = banks[7]
            P.mm(p7[0:64, 256 + dd * 128:256 + dd * 128 + 65], ktok[:, c, :], wv[:], True, True, reads=["ktok", wvk], writes=[f"b7s{dd}"])
            P.stt("vector", S[dd][:], S[dd][:], DEC[0:64, dd, c:c + 1], p7[0:64, 256 + dd * 128:256 + dd * 128 + 65], ALU.mult, ALU.add,
                  reads=[f"S{dd}", "DEC", f"b7s{dd}"], writes=[f"S{dd}"])
            P.copy("gpsimd", Sb[dd][:], S[dd][:], reads=[f"S{dd}"], writes=[f"Sb{dd}"])

        pts = {}
        for step in range(NCH + 1):
            if step < NCH:
                for dd in range(2):
                    pts[(step, dd)] = stage_a(step, dd)
            if step >= 1:
                for dd in range(2):
                    stage_b(step - 1, dd, *pts.pop((step - 1, dd)))

        hk = [f"hb{dd}_{c}" for dd in range(2) for c in range(NCH)]
        P.tt("vector", hb[0][:], hb[0][:], hb[1][:], ALU.add, reads=hk, writes=["hsum"])
        sq = hb[1]
        P.tt("vector", sq[:], hb[0][:], hb[0][:], ALU.mult, reads=["hsum"], writes=["hsq"])
        ssum = sb("ssum", [128, NCH])
        P.gen("vector", lambda e: e.reduce_sum(out=ssum[:], in_=sq[:], axis=AX.X), reads=["hsq"], writes=["ssum"])
        P.ts("vector", ssum[:], ssum[:], 1.0 / 64.0, RMS_EPS, ALU.mult, ALU.add, reads=["ssum"], writes=["ssum"])
        P.act(ssum[:], ssum[:], AF.Sqrt, reads=["ssum"], writes=["ssum"])
        P.gen("vector", lambda e: e.reciprocal(out=ssum[:], in_=ssum[:]), reads=["ssum"], writes=["ssum"])
        nw = sb("nw_s", [128, 64])
        P.dma("sync", nw[:], bcast_rows(nwd, 128), writes=["nw"])
        og = vtmp
        P.dma("sync", og[:], od, reads=["vf"], writes=["og"])
        P.act(og[:], og[:], AF.Sigmoid, reads=["og"], writes=["og"])
        for c in range(NCH):
            P.stt("vector", hb[0][:, c, :], hb[0][:, c, :], ssum[:, c:c + 1], nw[:], ALU.mult, ALU.mult,
                  reads=["hsum", "ssum", "nw"], writes=[f"hn{c}"])
        P.tt("vector", hb[0][:], hb[0][:], og[:], ALU.mult, reads=[f"hn{c}" for c in range(NCH)] + ["og"], writes=["ym"])
        P.dma("sync", yo, hb[0][:], reads=["ym"], final=True)
        P.emit()
    return nc


def run_k2(P_lat, P_ctx, conv_w, conv_b, gate_b, norm_w):
    nc = build_k2()
    B, n, _ = P_lat.shape
    nctx = P_ctx.shape[1]
    s_idx, j_idx = np.meshgrid(np.arange(128), np.arange(128), indexing="ij")
    Lf = (s_idx <= j_idx).astype(np.float32); Lb = (s_idx >= j_idx).astype(np.float32)
    cst = np.stack([Lf, Lb, np.ones((128, 128), np.float32), np.eye(128, dtype=np.float32),
                    np.where(s_idx <= j_idx, 0.0, MASK_NEG).astype(np.float32),
                    np.where(s_idx >= j_idx, 0.0, MASK_NEG).astype(np.float32)], 1)
    in_maps = []
    for i in range(NCORES):
        b, h = i // 4, i % 4
        seq = np.concatenate([P_ctx[b], P_lat[b]], 0)
        qs, ks = slice(h * 64, (h + 1) * 64), slice(256 + h * 64, 256 + (h + 1) * 64)
        tm = lambda a: np.ascontiguousarray(a.reshape(NCH, 128, -1).transpose(1, 0, 2))
        gcols = [1024 + 0 * 8 + 0 * 4 + h, 1024 + 0 * 8 + 1 * 4 + h, 1024 + 1 * 8 + 0 * 4 + h, 1024 + 1 * 8 + 1 * 4 + h]
        gb = np.array([gate_b[0, 0, h], gate_b[0, 1, h], gate_b[1, 0, h], gate_b[1, 1, h]], np.float32)
        cw = np.concatenate([conv_w[:, qs].T, conv_b[qs][:, None], conv_w[:, ks].T, conv_b[ks][:, None]], 1).astype(np.float32)
        in_maps.append({"qpT": np.ascontiguousarray(seq[:, qs].T), "kpT": np.ascontiguousarray(seq[:, ks].T),
                        "v": tm(seq[:, 512 + h * 64:512 + (h + 1) * 64]), "og": tm(seq[:, 768 + h * 64:768 + (h + 1) * 64]),
                        "g4": tm(seq[:, gcols]), "gb": np.ascontiguousarray(np.tile(gb, NCH)), "cw": np.ascontiguousarray(cw),
                        "nw": np.ascontiguousarray(norm_w[h * 64:(h + 1) * 64]), "cst": np.ascontiguousarray(cst)})
    res = _run(nc, in_maps)
    ym_l = np.zeros((B, n, 256), np.float32); ym_c = np.zeros((B, nctx, 256), np.float32)
    for i in range(NCORES):
        b, h = i // 4, i % 4
        y = res.results[i]["o_ym"].transpose(1, 0, 2).reshape(NSEQ, 64)
        ym_c[b, :, h * 64:(h + 1) * 64] = y[:nctx]
        ym_l[b, :, h * 64:(h + 1) * 64] = y[nctx:]
    return ym_l, ym_c


HCH = 32
TWO_PI = 2.0 * np.pi
RND_MAGIC = 12582912.0


def fft_tables(N1):
    N2 = 128
    N = N1 * N2
    ar = np.arange
    c, s = np.cos, np.sin
    th = TWO_PI * ar(N1)[:, None] * ar(N1)[None] / N1
    F1c = np.concatenate([c(th), -s(th)], 1)
    th = TWO_PI * ar(N2)[:, None] * ar(N1)[None] / N
    twRR = np.concatenate([c(th), c(th)], 1); twII = np.concatenate([-s(th), -s(th)], 1)
    th = TWO_PI * ar(N2)[:, None] * ar(N2)[None] / N2
    F2re, F2im, nF2im = c(th), -s(th), s(th)
    G2c = np.concatenate([c(th), s(th)], 1); G2s = np.concatenate([-s(th), c(th)], 1)
    th = TWO_PI * ar(N1)[:, None] * ar(N2)[None] / N
    twcRR = np.concatenate([c(th), c(th)], 1); twcII = np.concatenate([s(th), s(th)], 1)
    th = TWO_PI * ar(N1)[:, None] * ar(N1 // 2)[None] / N1
    G1re, nG1im = c(th) / N, -s(th) / N
    f = lambda a: np.ascontiguousarray(a.astype(np.float32))
    return dict(F1c=f(F1c), twRR=f(twRR), twII=f(twII), F2re=f(F2re), F2im=f(F2im), nF2im=f(nF2im), G2c=f(G2c), G2s=f(G2s),
                twcRR=f(twcRR), twcII=f(twcII), G1re=f(G1re), nG1im=f(nG1im))


TAB_ORDER = ["F1c", "twRR", "twII", "F2re", "F2im", "nF2im", "G2c", "G2s", "twcRR", "twcII", "G1re", "nG1im"]


def hyena_consts(n):
    N = 2 * n
    tau = np.arange(N)
    pos = np.where(tau < n, tau, N - tau).astype(np.float32)
    t = (pos / np.float32(n)).astype(np.float32)
    bands = np.arange(1, 17, dtype=np.float32)
    ang = (np.float32(TWO_PI) * t[:, None] * bands).astype(np.float32)
    feats = np.concatenate([t[:, None], np.cos(ang), np.sin(ang)], -1).astype(np.float32)
    lt = abs(np.log(1e-2))
    deltas = np.linspace(lt / 1.5, lt / 0.3, 256, dtype=np.float32)
    win = (np.exp(-t[:, None] * deltas) + np.float32(0.05)).astype(np.float32)
    win[n] = 0.0
    return np.ascontiguousarray(feats.T), np.ascontiguousarray(win.T)


def interleave(gens, width):
    it = iter(gens)
    active = []
    while True:
        while len(active) < width:
            g = next(it, None)
            if g is None:
                break
            active.append(g)
        if not active:
            return
        for g in list(active):
            try:
                next(g)
            except StopIteration:
                active.remove(g)
        yield


def build_k4(sizes):
    nc = bass.Bass("TRN2", target_bir_lowering=False)
    B = 2
    dr = {}
    for si, n in enumerate(sizes):
        N1 = 2 * n // 128
        dr[si] = dict(
            u=nc.dram_tensor(f"u{si}", [3, HCH, B, n + 2], F32, kind="ExternalInput").ap(),
            feats=nc.dram_tensor(f"feats{si}", [33, 2 * n], F32, kind="ExternalInput").ap(),
            win=nc.dram_tensor(f"win{si}", [64, 2 * n], F32, kind="ExternalInput").ap(),
            taps=nc.dram_tensor(f"taps{si}", [64, 2 * n], F32, kind="ExternalOutput").ap(),
            out=nc.dram_tensor(f"o_yh{si}", [HCH, B, n], F32, kind="ExternalOutput").ap(),
            tabs={k: nc.dram_tensor(f"t{si}_{k}", list(v.shape), F32, kind="ExternalInput").ap()
                  for k, v in fft_tables(N1).items()})
    mlpd = nc.dram_tensor("mlp", [64, 64 + 64 + 128 + 4], F32, kind="ExternalInput").ap()
    cwd = nc.dram_tensor("cwv", [3 * HCH * 4], F32, kind="ExternalInput").ap()
    skd = nc.dram_tensor("skv", [2 * HCH], F32, kind="ExternalInput").ap()
    cstd = nc.dram_tensor("cst", [128, 256], F32, kind="ExternalInput").ap()
    with ExitStack() as es:
        P = Prog(nc, es)
        sb, ps, ring = mk_alloc(nc, es)
        banks = [(ps(f"bank{i}", [128, 512], F32), f"bank{i}") for i in range(8)]
        cst = sb("cst_s", [128, 256]); P.dma("sync", cst[:], cstd, writes=["cst"])
        ident, ones = cst[:, 0:128], cst[:, 128:256]
        mlp = sb("mlp_s", [64, 260]); P.dma("sync", mlp[:], mlpd, writes=["mlp"])
        w1, w2 = mlp[0:33, 0:64], mlp[:, 64:128]
        w3 = [mlp[:, 128:192], mlp[:, 192:256]]
        b1, f0, b2, f1 = (mlp[:, 256 + j:257 + j] for j in range(4))
        cwb = sb("cwb", [128, 3 * HCH * 4]); P.dma("sync", cwb[:], bcast_rows(cwd, 128), writes=["cwb"])
        skb = sb("skb", [128, 2 * HCH]); P.dma("sync", skb[:], bcast_rows(skd, 128), writes=["skb"])
        a1r = ring("a1", 4, [64, 512]); rrr = ring("rr", 4, [64, 512]); hhr = ring("hh", 4, [64, 512])
        ftr = ring("ft", 4, [33, 512]); wnr = ring("wn", 4, [64, 512]); tpr = ring("tp", 4, [64, 512])
        As_r = ring("As", 4, [128, 256]); t1r = ring("t1", 4, [128, 256]); t2r = ring("t2", 4, [128, 256])
        Br = ring("Bc", 4, [128, 256]); Yr = ring("Yc", 4, [128, 256]); Dr = ring("Dc", 4, [128, 256])
        pr4 = [ring(f"pp{j}", 4, [128, 128]) for j in range(4)]
        xir = ring("xi", 4, [64, 3, 130]); cvr = ring("cv", 4, [64, 3, 128]); tgr = ring("tg", 4, [64, 128])
        z1r = ring("z1", 4, [64, 128]); z2r = ring("z2", 4, [64, 128]); tlr = ring("tl", 4, [128, 128])
        bkP = Ring(banks[0:8])
        bkA = bkX = bkC = bkY = bkP

        def sin_layer(psrc, pk, n_, bias, freq, dst, dstk):
            a1, a1k = a1r.next(); rr, rrk = rrr.next()
            P.ts("vector", a1[:, 0:n_], psrc, bias, freq, ALU.add, ALU.mult, reads=[pk, "mlp"], writes=[a1k])
            yield
            P.ts("vector", rr[:, 0:n_], a1[:, 0:n_], 1.0 / TWO_PI, RND_MAGIC, ALU.mult, ALU.add, reads=[a1k], writes=[rrk])
            yield
            P.ts("vector", rr[:, 0:n_], rr[:, 0:n_], RND_MAGIC, -TWO_PI, ALU.subtract, ALU.mult, reads=[rrk], writes=[rrk])
            yield
            P.tt("vector", rr[:, 0:n_], rr[:, 0:n_], a1[:, 0:n_], ALU.add, reads=[rrk, a1k], writes=[rrk])
            yield
            P.ts("vector", rr[:, 0:n_], rr[:, 0:n_], np.pi, -np.pi, ALU.min, ALU.max, reads=[rrk], writes=[rrk])
            yield
            P.act(dst, rr[:, 0:n_], AF.Sin, reads=[rrk], writes=[dstk])
            yield

        def cmul(src, srck, n1p, W, tRR, tII, tk, dst, dstk):
            t1, t1k = t1r.next(); t2, t2k = t2r.next()
            P.tt("vector", t1[0:n1p, 0:2 * W], src, tRR, ALU.mult, reads=[srck, tk], writes=[t1k])
            yield
            P.tt("gpsimd", t2[0:n1p, 0:2 * W], src, tII, ALU.mult, reads=[srck, tk], writes=[t2k])
            yield
            P.tt("vector", dst[0:n1p, 0:W], t1[0:n1p, 0:W], t2[0:n1p, W:2 * W], ALU.subtract, reads=[t1k, t2k], writes=[dstk + "r"])
            yield
            P.tt("gpsimd", dst[0:n1p, W:2 * W], t2[0:n1p, 0:W], t1[0:n1p, W:2 * W], ALU.add, reads=[t1k, t2k], writes=[dstk + "i"])
            yield

        def size_body(si, n):
            N = 2 * n
            N1 = N // 128
            Kd = N1 // 2
            d = dr[si]
            T = {}
            for k in TAB_ORDER:
                shp = list(d["tabs"][k].shape)
                T[k] = sb(f"T{si}_{k}", shp)
                P.dma("sync", T[k][:], d["tabs"][k], writes=[f"tab{si}"] if k == TAB_ORDER[-1] else [f"tab{si}_{k}"])
                yield
            tabk = [f"tab{si}"] + [f"tab{si}_{k}" for k in TAB_ORDER[:-1]]
            CH = min(512, n)
            nchunk = N // CH
            l1p = sb(f"l1p{si}", [64, nchunk])
            def mlp_chain(ci):
                c0 = ci * CH
                dirn = 0 if c0 < n else 1
                ft, ftk = ftr.next(); P.dma("sync", ft[:, 0:CH], d["feats"][:, c0:c0 + CH], writes=[ftk])
                wn, wnk = wnr.next(); P.dma("sync", wn[:, 0:CH], d["win"][:, c0:c0 + CH], writes=[wnk])
                bk, bkk = bkA.next()
                P.mm(bk[0:64, 0:CH], w1, ft[:, 0:CH], True, True, reads=[ftk, "mlp"], writes=[bkk])
                yield
                h1, h1k = hhr.next()
                yield from sin_layer(bk[0:64, 0:CH], bkk, CH, b1, f0, h1[:, 0:CH], h1k)
                bk, bkk = bkX.next()
                P.mm(bk[0:64, 0:CH], w2, h1[:, 0:CH], True, True, reads=[h1k, "mlp"], writes=[bkk])
                yield
                h2, h2k = hhr.next()
                yield from sin_layer(bk[0:64, 0:CH], bkk, CH, b2, f1, h2[:, 0:CH], h2k)
                bk, bkk = bkC.next()
                P.mm(bk[0:64, 0:CH], w3[dirn], h2[:, 0:CH], True, True, reads=[h2k, "mlp"], writes=[bkk])
                yield
                tp, tpk = tpr.next()
                P.tt("vector", tp[:, 0:CH], bk[0:64, 0:CH], wn[:, 0:CH], ALU.mult, reads=[bkk, wnk], writes=[tpk])
                yield
                P.gen("vector", lambda e, tp=tp, ci=ci, CH=CH, l1p=l1p: e.reduce_sum(out=l1p[:, ci:ci + 1], in_=tp[:, 0:CH], axis=AX.X,
                                                                         apply_absolute_value=True), reads=[tpk], writes=[f"l1p{si}_{ci}"])
                yield
                P.dma("sync", d["taps"][:, c0:c0 + CH], tp[:, 0:CH], reads=[tpk], writes=[f"taps{si}"], final=True)
                yield
            yield from interleave([mlp_chain(ci) for ci in range(nchunk)], 2)
            l1 = sb(f"l1_{si}", [64, 2])
            P.gen("vector", lambda e, l1=l1, l1p=l1p: e.reduce_sum(out=l1[:, 0:1], in_=l1p[:], axis=AX.X),
                  reads=[f"l1p{si}_{ci}" for ci in range(nchunk)], writes=[f"l1{si}"])
            yield
            P.gen("vector", lambda e, l1=l1: e.reciprocal(out=l1[:, 1:2], in_=l1[:, 0:1]), reads=[f"l1{si}"], writes=[f"l1{si}"])
            yield
            dg = sb(f"dg{si}", [64, 64])
            P.ts("vector", dg[:], ident[0:64, 0:64], l1[:, 1:2], None, ALU.mult, reads=["cst", f"l1{si}"], writes=[f"dg{si}"])
            yield
            bk, bkk = bkY.next()
            P.mm(bk[:, 0:64], ones[0:64, :], dg[:], True, True, reads=["cst", f"dg{si}"], writes=[bkk])
            yield
            rl1b = sb(f"rl1b{si}", [128, 64])
            P.copy("vector", rl1b[:], bk[:, 0:64], reads=[bkk], writes=[f"rl1b{si}"])
            yield

            def fwd_fft(xt, xk, Krows):
                bA, bAk = bkA.next()
                P.mm(bA[:, 0:2 * N1], xt, T["F1c"][0:Krows, :], True, True, reads=[xk] + tabk, writes=[bAk])
                yield
                As, Ask = As_r.next()
                P.copy("scalar", As[:, 0:2 * N1], bA[:, 0:2 * N1], reads=[bAk], writes=[Ask])
                yield
                Bc, Bck = Br.next()
                yield from cmul(As[:, 0:2 * N1], Ask, 128, N1, T["twRR"][:], T["twII"][:], tabk[0], Bc, Bck)
                bX, bXk = bkX.next()
                Bre, Bim = Bc[:, 0:N1], Bc[:, N1:2 * N1]
                P.mm(bX[:, 0:N1], T["F2re"][:], Bre, True, False, reads=[Bck + "r"] + tabk, writes=[bXk])
                P.mm(bX[:, 0:N1], T["nF2im"][:], Bim, False, True, reads=[Bck + "i"] + tabk, writes=[bXk])
                yield
                P.mm(bX[:, N1:2 * N1], T["F2re"][:], Bim, True, False, reads=[Bck + "i"] + tabk, writes=[bXk])
                P.mm(bX[:, N1:2 * N1], T["F2im"][:], Bre, False, True, reads=[Bck + "r"] + tabk, writes=[bXk])
                yield
                return bX, bXk

            H = sb(f"H{si}", [128, 64, 2 * N1])
            def filt_chain(oc):
                tl, tlk = tlr.next()
                P.dma("sync", tl[0:N1, :], d["taps"][oc].rearrange("(a b) -> a b", b=128), reads=[f"taps{si}"], writes=[tlk])
                yield
                bX, bXk = yield from fwd_fft(tl[0:N1, :], tlk, N1)
                P.ts("vector", H[:, oc, :], bX[:, 0:2 * N1], rl1b[:, oc:oc + 1], None, ALU.mult, reads=[bXk, f"rl1b{si}"], writes=[f"H{si}_{oc}"])
                yield

            yield from interleave([filt_chain(oc) for oc in range(64)], 4)
            def long_conv(zt, zk, o, c):
                bX, bXk = yield from fwd_fft(zt, zk, Kd)
                oc = o * HCH + c
                Hre, Him = H[:, oc, 0:N1], H[:, oc, N1:2 * N1]
                hk = f"H{si}_{oc}"
                pp = [r.next() for r in pr4]
                P.tt("vector", pp[0][0][:, 0:N1], bX[:, 0:N1], Hre, ALU.mult, reads=[bXk, hk], writes=[pp[0][1]])
                yield
                P.tt("vector", pp[1][0][:, 0:N1], bX[:, N1:2 * N1], Him, ALU.mult, reads=[bXk, hk], writes=[pp[1][1]])
                yield
                P.tt("vector", pp[2][0][:, 0:N1], bX[:, 0:N1], Him, ALU.mult, reads=[bXk, hk], writes=[pp[2][1]])
                yield
                P.tt("vector", pp[3][0][:, 0:N1], bX[:, N1:2 * N1], Hre, ALU.mult, reads=[bXk, hk], writes=[pp[3][1]])
                yield
                Yc, Yck = Yr.next()
                P.tt("gpsimd", Yc[:, 0:N1], pp[0][0][:, 0:N1], pp[1][0][:, 0:N1], ALU.subtract, reads=[pp[0][1], pp[1][1]], writes=[Yck + "r"])
                yield
                P.tt("gpsimd", Yc[:, N1:2 * N1], pp[2][0][:, 0:N1], pp[3][0][:, 0:N1], ALU.add, reads=[pp[2][1], pp[3][1]], writes=[Yck + "i"])
                yield
                bC, bCk = bkC.next()
                P.mm(bC[0:N1, 0:256], Yc[:, 0:N1], T["G2c"][:], True, False, reads=[Yck + "r"] + tabk, writes=[bCk])
                P.mm(bC[0:N1, 0:256], Yc[:, N1:2 * N1], T["G2s"][:], False, True, reads=[Yck + "i"] + tabk, writes=[bCk])
                yield
                Cs, Csk = As_r.next()
                P.copy("scalar", Cs[0:N1, :], bC[0:N1, 0:256], reads=[bCk], writes=[Csk])
                yield
                Dc, Dck = Dr.next()
                yield from cmul(Cs[0:N1, :], Csk, N1, 128, T["twcRR"][:], T["twcII"][:], tabk[0], Dc, Dck)
                bY, bYk = bkY.next()
                P.mm(bY[0:Kd, 0:128], T["G1re"][:], Dc[0:N1, 0:128], True, False, reads=[Dck + "r"] + tabk, writes=[bYk])
                P.mm(bY[0:Kd, 0:128], T["nG1im"][:], Dc[0:N1, 128:256], False, True, reads=[Dck + "i"] + tabk, writes=[bYk])
                yield
                return bY, bYk

            def data_chain(c, b):
                xi, xik = xir.next()
                src = bass.AP(tensor=d["u"].tensor, offset=d["u"][0, c, b, 0].offset,
                              ap=[[128, Kd], [HCH * B * (n + 2), 3], [1, 130]])
                P.dma("gpsimd", xi[0:Kd], src, writes=[xik])
                yield
                cv, cvk = cvr.next()
                for p in range(3):
                    wo = (p * HCH + c) * 4
                    P.ts("vector", cv[0:Kd, p, :], xi[0:Kd, p, 1:129], cwb[0:Kd, wo + 1:wo + 2], cwb[0:Kd, wo + 3:wo + 4], ALU.mult, ALU.add,
                         reads=[xik, "cwb"], writes=[cvk + str(p)])
                    yield
                    P.stt("vector", cv[0:Kd, p, :], xi[0:Kd, p, 0:128], cwb[0:Kd, wo:wo + 1], cv[0:Kd, p, :], ALU.mult, ALU.add,
                          reads=[xik, "cwb", cvk + str(p)], writes=[cvk + str(p)])
                    yield
                    P.stt("vector", cv[0:Kd, p, :], xi[0:Kd, p, 2:130], cwb[0:Kd, wo + 2:wo + 3], cv[0:Kd, p, :], ALU.mult, ALU.add,
                          reads=[xik, "cwb", cvk + str(p)], writes=[cvk + str(p)])
                    yield
                bY, bYk = yield from long_conv(cv[0:Kd, 0, :], cvk + "0", 0, c)
                tg, tgk = tgr.next()
                P.stt("vector", tg[0:Kd, :], cv[0:Kd, 0, :], skb[0:Kd, c:c + 1], bY[0:Kd, 0:128], ALU.mult, ALU.add,
                      reads=[cvk + "0", "skb", bYk], writes=[tgk])
                yield
                z1, z1k = z1r.next()
                P.tt("gpsimd", z1[0:Kd, :], tg[0:Kd, :], cv[0:Kd, 1, :], ALU.mult, reads=[tgk, cvk + "1"], writes=[z1k])
                yield
                bY, bYk = yield from long_conv(z1[0:Kd, :], z1k, 1, c)
                tg, tgk = tgr.next()
                P.stt("vector", tg[0:Kd, :], z1[0:Kd, :], skb[0:Kd, HCH + c:HCH + c + 1], bY[0:Kd, 0:128], ALU.mult, ALU.add,
                      reads=[z1k, "skb", bYk], writes=[tgk])
                yield
                z2, z2k = z2r.next()
                P.tt("gpsimd", z2[0:Kd, :], tg[0:Kd, :], cv[0:Kd, 2, :], ALU.mult, reads=[tgk, cvk + "2"], writes=[z2k])
                yield
                P.dma("sync", d["out"][c, b].rearrange("(a b) -> a b", b=128), z2[0:Kd, :], reads=[z2k], writes=[z2k + "d"], final=True)
                yield
            yield from interleave([data_chain(c, b) for c in range(HCH) for b in range(B)], 4)

        for si, n in enumerate(sizes):
            for _ in size_body(si, n):
                pass
        P.emit()
    return nc


def run_k4(hy_list, conv_w, conv_b, fparams, skip):
    f_w1, f_b1, f_freq, f_w2, f_b2, f_w3 = fparams
    sizes = [h.shape[1] for h in hy_list]
    nc = build_k4(sizes)
    B = 2
    cst = np.concatenate([np.eye(128, dtype=np.float32), np.ones((128, 128), np.float32)], 1)
    consts = [hyena_consts(n) for n in sizes]
    tabs = [fft_tables(2 * n // 128) for n in sizes]
    w3r = f_w3.reshape(64, 2, 2, 256)
    in_maps = []
    for i in range(NCORES):
        cs = slice(i * HCH, (i + 1) * HCH)
        m = {"cst": cst}
        mlp = np.zeros((64, 260), np.float32)
        mlp[0:33, 0:64] = f_w1
        mlp[:, 64:128] = f_w2
        mlp[:, 128:192] = w3r[:, 0, :, cs].reshape(64, 64)
        mlp[:, 192:256] = w3r[:, 1, :, cs].reshape(64, 64)
        mlp[:, 256] = f_b1; mlp[:, 257] = f_freq[0]; mlp[:, 258] = f_b2; mlp[:, 259] = f_freq[1]
        m["mlp"] = mlp
        cw = np.zeros((3, HCH, 4), np.float32)
        for p in range(3):
            ch = slice(p * 256 + i * HCH, p * 256 + (i + 1) * HCH)
            cw[p, :, 0:3] = conv_w[:, ch].T
            cw[p, :, 3] = conv_b[ch]
        m["cwv"] = cw.reshape(-1)
        m["skv"] = np.ascontiguousarray(skip[:, cs]).reshape(-1)
        for si, (hy, n) in enumerate(zip(hy_list, sizes)):
            u = np.zeros((3, HCH, B, n + 2), np.float32)
            for p in range(3):
                u[p, :, :, 1:n + 1] = hy[:, :, p * 256 + i * HCH:p * 256 + (i + 1) * HCH].transpose(2, 0, 1)
            m[f"u{si}"] = u
            feats, win = consts[si]
            m[f"feats{si}"] = feats
            m[f"win{si}"] = np.ascontiguousarray(np.tile(win[cs], (2, 1)))
            for k, v in tabs[si].items():
                m[f"t{si}_{k}"] = v
        in_maps.append(m)
    res = _run(nc, in_maps)
    outs = []
    for si, n in enumerate(sizes):
        y = np.zeros((B, n, 256), np.float32)
        for i in range(NCORES):
            y[:, :, i * HCH:(i + 1) * HCH] = res.results[i][f"o_yh{si}"].transpose(1, 2, 0)
        outs.append(y)
    return outs, [np.concatenate([res.results[i][f"taps{si}"] for i in range(NCORES)], 0) for si in range(len(sizes))]


def kernel(x, c, ctx, c_ctx, w_mod, b_mod, w_in, mlstm_conv_w, mlstm_conv_b, mlstm_gate_b,
           mlstm_norm_w, attn_q_norm_w, attn_k_norm_w, hyena_conv_w, hyena_conv_b,
           hyena_f_w1, hyena_f_b1, hyena_f_freq, hyena_f_w2, hyena_f_b2, hyena_f_w3,
           hyena_skip, w_out, ln_mix_w, ln_mix_b, router_w, router_b,
           exp_w_gate, exp_w_up, exp_w_down, ln_ffn_w, ln_ffn_b):
    f = lambda a: np.asarray(a, dtype=np.float32)
    x, c, ctx, c_ctx = f(x), f(c), f(ctx), f(c_ctx)
    mod = run_k0(c, c_ctx, f(w_mod), f(b_mod))
    depth = w_in.shape[0]
    for l in range(depth):
        last = l == depth - 1
        P_lat, P_ctx = run_k1(x, ctx, mod[l], f(w_in[l]))
        ym_l, ym_c = run_k2(P_lat, P_ctx, f(mlstm_conv_w[l]), f(mlstm_conv_b[l]), f(mlstm_gate_b[l]), f(mlstm_norm_w[l]))
        ya_l, ya_c = run_k3(P_lat, P_ctx, f(attn_q_norm_w[l]), f(attn_k_norm_w[l]))
        hy = [P_lat[..., 1808:]] + ([] if last else [P_ctx[..., 1808:]])
        fpar = (f(hyena_f_w1[l]), f(hyena_f_b1[l]), f(hyena_f_freq[l]), f(hyena_f_w2[l]), f(hyena_f_b2[l]), f(hyena_f_w3[l]))
        yhs, _ = run_k4(hy, f(hyena_conv_w[l]), f(hyena_conv_b[l]), fpar, f(hyena_skip[l]))
        yh_c = np.zeros_like(ym_c) if last else yhs[1]
        ycat_l = np.concatenate([ym_l, ya_l, yhs[0]], -1)
        ycat_c = np.concatenate([ym_c, ya_c, yh_c], -1)
        x1, c1, h2, h2c, aff, affc = run_k5(ycat_l, ycat_c, x, ctx, mod[l], f(w_out[l]), f(ln_mix_w[l]), f(ln_mix_b[l]),
                                            f(router_w[l]), f(router_b[l]))
        thr, thrc = run_k6(aff, affc)
        x, ctx = run_moe(h2, h2c, aff, affc, thr, thrc, x1, c1, mod[l], f(ln_ffn_w[l]), f(ln_ffn_b[l]),
                         f(exp_w_gate[l]), f(exp_w_up[l]), f(exp_w_down[l]))
    return x.astype(np.float32)


CAP_L, CAP_C = 1024, 32
SLOTS_B = CAP_L + CAP_C
NTOK_ALL = 2 * 8192 + 2 * 256


def build_k7g():
    nc = bass.Bass("TRN2", target_bir_lowering=False)
    T = 18
    TOK = T * 128
    h2d = nc.dram_tensor("h2all", [NTOK_ALL, 1024], BF16, kind="ExternalInput").ap()
    affLd = nc.dram_tensor("affL", [2, 2, 128, 66], F32, kind="ExternalInput").ap()
    thrd = nc.dram_tensor("thr8", [8], F32, kind="ExternalInput").ap()
    tidd = nc.dram_tensor("tid", [2, 128, 66, 2], F32, kind="ExternalInput").ap()
    iotad = nc.dram_tensor("iota", [128, CAP_L + 128], F32, kind="ExternalInput").ap()
    cstd = nc.dram_tensor("cst", [128, 448], F32, kind="ExternalInput").ap()
    wgd = nc.dram_tensor("wg", [2, 128, 8, D_FF], F32, kind="ExternalInput").ap()
    wud = nc.dram_tensor("wu", [2, 128, 8, D_FF], F32, kind="ExternalInput").ap()
    wdd = nc.dram_tensor("wd", [2, 128, NFC, 1024], F32, kind="ExternalInput").ap()
    Yo = nc.dram_tensor("o_Y", [2, TOK, 1024], BF16, kind="ExternalOutput").ap()
    posKo = nc.dram_tensor("o_pos", [2, 2, 128, 66], I32, kind="ExternalOutput").ap()
    with ExitStack() as es:
        P = Prog(nc, es)
        sb, ps, ring = mk_alloc(nc, es)
        banks = [(ps(f"bk{i}", [128, 512], F32), f"bk{i}") for i in range(8)]
        cst = sb("cst_s", [128, 448]); P.dma("sync", cst[:], cstd, writes=["cst"])
        Ust, ones, identf, Ust64 = cst[:, 0:128], cst[:, 128:256], cst[:, 256:384], cst[0:64, 384:448]
        idb = sb("idb", [128, 128], BF16); P.copy("vector", idb[:], identf, reads=["cst"], writes=["idb"])
        thrt = sb("thrt", [128, 8]); P.dma("sync", thrt[:], bcast_rows(thrd, 128), writes=["thrt"])
        iota = sb("iota_s", [128, CAP_L + 128]); P.dma("sync", iota[:], iotad, writes=["iota"])
        h2T = sb("h2T_s", [128, 8, TOK], BF16)
        acc = sb("acc", [128, T, 1024])
        tv = sb("tv", [128, T])
        Ar = ring("A", 2, [128, 66]); Mr = ring("M", 2, [128, 66]); wir = ring("wi", 2, [128, 66]); pfr = ring("pf", 2, [128, 66])
        m2r = ring("m2", 2, [128, 66]); pir = ring("pi", 2, [128, 66], I32)
        tdfr = ring("tdf", 2, [128, 66, 2]); TAr = ring("TA", 2, [128, 66, 5], BF16); spr = ring("sp", 2, [128, 2, 66]); l5r = ring("l5", 2, [128, 9, 5]); selr = ring("sel", 4, [128, 512], BF16); rowt = sb("rowt", [5, SLOTS_B]); lfr = ring("lf", 2, [128, 18])
        tcr = ring("tc", 2, [64, 1]); tbr = ring("tb", 2, [64, 128])
        lsr = ring("ls", 2, [128, 9], I32)
        xsr = ring("xs", 2, [128, 1024], BF16)
        FG = 3
        wgr = ring("wgb", 2, [128, 8, FG * 128], BF16); wur = ring("wub", 2, [128, 8, FG * 128], BF16)
        wdr = ring("wdb", 2, [128, FG, 1024], BF16); stg = ring("stg", 3, [128, 1024])
        actr = ring("actT", 2, [128, FG, 512], BF16); sgr = ring("sg", 2, [128, 512]); yor = ring("yrow", 2, [128, 1024], BF16)
        pgr = Ring(banks[0:2]); pur = Ring(banks[2:4]); pyr = Ring(banks[4:6])
        tgs = [(s, min(512, TOK - s)) for s in range(0, TOK, 512)]
        for e in range(2):
            P.memset("gpsimd", h2T[:], 0.0, writes=["h2T"])
            P.memset("vector", tv[:], 0.0, writes=["tv"])
            for t in range(T):
                P.memset("gpsimd", acc[:, t, :], 0.0, writes=[f"acc{t}a", f"acc{t}b"])
            for b in range(2):
                A, Ak = Ar.next(); P.dma("sync", A[:], affLd[e, b], writes=[Ak])
                M, Mk = Mr.next()
                to = (e * 2 + b) * 2
                P.ts("vector", M[:, 0:64], A[:, 0:64], thrt[:, to:to + 1], None, ALU.is_ge, reads=[Ak, "thrt"], writes=[Mk + "l"])
                P.ts("vector", M[:, 64:66], A[:, 64:66], thrt[:, to + 1:to + 2], None, ALU.is_ge, reads=[Ak, "thrt"], writes=[Mk + "c"])
                wi, wik = wir.next(); pf, pfk = pfr.next(); m2, m2k = m2r.next(); pi, pik = pir.next()
                for (c0, ncol, cap, base, sfx) in ((0, 64, CAP_L, 0, "l"), (64, 2, CAP_C, CAP_L, "c")):
                    bw, bwk = banks[6]
                    P.mm(bw[:, c0:c0 + ncol], Ust, M[:, c0:c0 + ncol], True, True, reads=["cst", Mk + sfx], writes=[bwk + sfx])
                    P.copy("scalar", wi[:, c0:c0 + ncol], bw[:, c0:c0 + ncol], reads=[bwk + sfx], writes=[wik + sfx])
                    bt, btk = banks[7]
                    P.mm(bt[0:ncol, c0:c0 + 1], M[:, c0:c0 + ncol], ones[:, 0:1], True, True, reads=["cst", Mk + sfx], writes=[btk + "t" + sfx])
                    tc, tck = tcr.next()
                    P.copy("vector", tc[0:ncol, :], bt[0:ncol, c0:c0 + 1], reads=[btk + "t" + sfx], writes=[tck])
                    tb, tbk = tbr.next()
                    P.ts("vector", tb[0:ncol, :], ones[0:ncol, :], tc[0:ncol, 0:1], None, ALU.mult, reads=["cst", tck], writes=[tbk])
                    P.mm(bt[:, 128 + c0:128 + c0 + ncol], tb[0:ncol, :], Ust64[0:ncol, 0:ncol], True, True, reads=[tbk, "cst"], writes=[btk + "o" + sfx])
                    sl = slice(c0, c0 + ncol)
                    P.tt("vector", pf[:, sl], bt[:, 128 + c0:128 + c0 + ncol], wi[:, sl], ALU.add, reads=[btk + "o" + sfx, wik + sfx], writes=[pfk + sfx])
                    P.ts("vector", m2[:, sl], pf[:, sl], float(cap) - 0.5, None, ALU.is_lt, reads=[pfk + sfx], writes=[m2k + sfx])
                    P.tt("vector", m2[:, sl], m2[:, sl], M[:, sl], ALU.mult, reads=[m2k + sfx, Mk + sfx], writes=[m2k + sfx])
                    P.ts("vector", pf[:, sl], pf[:, sl], float(base - SLOTS_B), None, ALU.add, reads=[pfk + sfx], writes=[pfk + sfx])
                    P.tt("vector", pf[:, sl], pf[:, sl], m2[:, sl], ALU.mult, reads=[pfk + sfx, m2k + sfx], writes=[pfk + sfx])
                    P.ts("vector", pf[:, sl], pf[:, sl], float(SLOTS_B), None, ALU.add, reads=[pfk + sfx], writes=[pfk + sfx])
                P.copy("vector", pi[:], pf[:], reads=[pfk + "l", pfk + "c"], writes=[pik])
                P.dma("sync", posKo[e, b], pi[:], reads=[pik], writes=[pik + "d"], final=True)
                TA, TAk = TAr.next()
                tdf, tdfk = tdfr.next()
                P.dma("sync", tdf[:], tidd[b], writes=[tdfk])
                P.copy("gpsimd", TA[:, :, 0:2], tdf[:], reads=[tdfk], writes=[TAk + "t"])
                sp, spk = spr.next()
                P.copy("vector", TA[:, :, 2], A[:], reads=[Ak], writes=[TAk + "a0"])
                P.copy("vector", sp[:, 0, :], TA[:, :, 2], reads=[TAk + "a0"], writes=[spk + "0"])
                P.tt("vector", sp[:, 1, :], A[:], sp[:, 0, :], ALU.subtract, reads=[Ak, spk + "0"], writes=[spk + "1"])
                P.copy("vector", TA[:, :, 3], sp[:, 1, :], reads=[spk + "1"], writes=[TAk + "a1"])
                P.copy("vector", sp[:, 0, :], TA[:, :, 3], reads=[TAk + "a1"], writes=[spk + "0"])
                P.tt("vector", sp[:, 1, :], sp[:, 1, :], sp[:, 0, :], ALU.subtract, reads=[spk + "1", spk + "0"], writes=[spk + "1"])
                P.copy("vector", TA[:, :, 4], sp[:, 1, :], reads=[spk + "1"], writes=[TAk + "a2"])
                tak = [TAk + "t", TAk + "a0", TAk + "a1", TAk + "a2"]
                pc, pck = banks[7]
                for piece, (s0_, ns_, js) in enumerate(((0, 512, range(64)), (512, 512, range(64)), (CAP_L, CAP_C, (64, 65)))):
                    sfx = "l" if piece < 2 else "c"
                    for jj, j in enumerate(js):
                        se, sek = selr.next()
                        P.ts("vector", se[:, 0:ns_], iota[:, s0_:s0_ + ns_], pf[:, j:j + 1], None, ALU.is_equal, reads=["iota", pfk + sfx], writes=[sek])
                        P.mm(pc[0:5, 0:ns_], TA[:, j, :], se[:, 0:ns_], jj == 0, jj == len(js) - 1, reads=[sek] + tak, writes=[pck + "row"])
                    P.copy("scalar", rowt[0:5, s0_:s0_ + ns_], pc[0:5, 0:ns_], reads=[pck + "row"], writes=[f"rowt{piece}"])
                pq, pqk = banks[6]
                for c in range(9):
                    ns_ = 128 if c < 8 else 32
                    P.tr(pq[0:ns_, 256 + 5 * c:256 + 5 * c + 5], rowt[0:5, c * 128:c * 128 + ns_], identf[0:5, 0:5],
                         reads=[f"rowt{c // 4}", "cst"], writes=[pqk + "q"])
                l5, l5k = l5r.next()
                P.copy("vector", l5[:].rearrange("p c t -> p (c t)"), pq[:, 256:301], reads=[pqk + "q"], writes=[l5k])
                lf, lfk = lfr.next()
                lf3 = lf[:].rearrange("p (c t) -> p c t", t=2)
                P.stt("vector", lf3[:, :, 0], l5[:, :, 0], 128.0, l5[:, :, 1], ALU.mult, ALU.add, reads=[l5k], writes=[lfk + "i"])
                P.tt("vector", lf3[:, :, 1], l5[:, :, 2], l5[:, :, 3], ALU.add, reads=[l5k], writes=[lfk + "v"])
                P.tt("vector", lf3[:, :, 1], lf3[:, :, 1], l5[:, :, 4], ALU.add, reads=[l5k, lfk + "v"], writes=[lfk + "v"])
                ls, lsk = lsr.next()
                P.copy("vector", ls[:], lf3[:, :, 0], reads=[lfk + "i"], writes=[lsk])
                for c in range(9):
                    npart = 128 if c < 8 else 32
                    p0 = 0
                    col0 = b * CAP_L + c * 128 if c < 8 else (16 + b) * 128
                    tcol = col0 // 128
                    P.copy("vector", tv[p0:p0 + npart, tcol:tcol + 1], lf[p0:p0 + npart, 2 * c + 1:2 * c + 2], reads=[lfk + "v", "tv"], writes=["tv"])
                    xs, xsk = xsr.next()
                    P.op("gpsimd", lambda en, xs=xs, ls=ls, c=c, npart=npart, p0=p0: en.indirect_dma_start(
                        out=xs[p0:p0 + npart, :], out_offset=None, in_=h2d[:, :],
                        in_offset=bass.IndirectOffsetOnAxis(ap=ls[p0:p0 + npart, c:c + 1], axis=0)), reads=[lsk], writes=[xsk], dma=True)
                    pb, pbk = banks[6]
                    pT = pb[:, :].bitcast(BF16)
                    for kc in range(8):
                        P.tr(pT[:, kc * 128:kc * 128 + npart], xs[p0:p0 + npart, kc * 128:(kc + 1) * 128], idb[p0:p0 + npart, p0:p0 + npart],
                             reads=[xsk, "idb"], writes=[pbk + "l", pbk + "c"])
                    P.copy("scalar", h2T[:, :, col0:col0 + npart], pT[:, 0:1024].rearrange("p (k s) -> p k s", k=8)[:, :, 0:npart],
                           reads=[pbk + "l", pbk + "c"], writes=["h2T"])
            ci = 0
            for f0 in range(0, NFC, FG):
                nf = min(FG, NFC - f0)
                wgb, wgk = wgr.next(); wub, wuk = wur.next(); wdb, wdk = wdr.next()
                for (src, dst, dk) in ((wgd, wgb, wgk), (wud, wub, wuk)):
                    for kp in range(0, 8, 2):
                        st, sk = stg.next()
                        sv = st[:, 0:2 * nf * 128].rearrange("p (a f) -> p a f", a=2)
                        P.dma("sync", sv, src[e, :, kp:kp + 2, f0 * 128:(f0 + nf) * 128], writes=[sk])
                        P.copy("gpsimd" if ci % 2 else "vector", dst[:, kp:kp + 2, 0:nf * 128], sv, reads=[sk], writes=[dk + f"k{kp}"])
                        ci += 1
                for fc in range(nf):
                    st, sk = stg.next()
                    P.dma("sync", st[:], wdd[e, :, f0 + fc, :], writes=[sk])
                    P.copy("gpsimd" if ci % 2 else "vector", wdb[:, fc, :], st[:], reads=[sk], writes=[wdk + f"f{fc}"])
                    ci += 1
                for (s0, ns) in tgs:
                    actT, ak = actr.next()
                    for fc in range(nf):
                        pg, pgk = pgr.next(); pu, puk = pur.next()
                        for kc in range(8):
                            P.mm(pg[:, 0:ns], wgb[:, kc, fc * 128:(fc + 1) * 128], h2T[:, kc, s0:s0 + ns], kc == 0, kc == 7,
                                 reads=["h2T", wgk + f"k{kc - kc % 2}"], writes=[pgk])
                        for kc in range(8):
                            P.mm(pu[:, 0:ns], wub[:, kc, fc * 128:(fc + 1) * 128], h2T[:, kc, s0:s0 + ns], kc == 0, kc == 7,
                                 reads=["h2T", wuk + f"k{kc - kc % 2}"], writes=[puk])
                        sg, sgk = sgr.next()
                        P.act(sg[:, 0:ns], pg[:, 0:ns], AF.Silu, reads=[pgk], writes=[sgk])
                        P.tt("vector", actT[:, fc, 0:ns], pu[:, 0:ns], sg[:, 0:ns], ALU.mult, reads=[puk, sgk], writes=[ak + f"f{fc}"])
                    for tt in range(s0 // 128, (s0 + ns) // 128):
                        for hf in range(2):
                            py, pyk = pyr.next()
                            for fc in range(nf):
                                P.mm(py[:], actT[:, fc, tt * 128 - s0:(tt + 1) * 128 - s0], wdb[:, fc, hf * 512:(hf + 1) * 512],
                                     fc == 0, fc == nf - 1, reads=[ak + f"f{fc}", wdk + f"f{fc}"], writes=[pyk])
                            ah = f"acc{tt}" + "ab"[hf]
                            P.tt("vector", acc[:, tt, hf * 512:(hf + 1) * 512], py[:], acc[:, tt, hf * 512:(hf + 1) * 512], ALU.add,
                                 reads=[pyk, ah], writes=[ah])
            for t in range(T):
                yr_, yrk = yor.next()
                P.act(yr_[:], acc[:, t, :], AF.Copy, reads=[f"acc{t}a", f"acc{t}b", "tv"], writes=[yrk], scale=tv[:, t:t + 1])
                P.dma("sync", Yo[e, t * 128:(t + 1) * 128, :], yr_[:], reads=[yrk], writes=[yrk], final=True)
        P.emit()
    return nc


def build_k8():
    nc = bass.Bass("TRN2", target_bir_lowering=False)
    T = K1_TILES
    Yb = [nc.dram_tensor(f"Yb{e}", [SLOTS_B + 1, 1024], BF16, kind="ExternalInput").ap() for e in range(N_EXP)]
    idxd = nc.dram_tensor("idx", [T, 128, N_EXP], I32, kind="ExternalInput").ap()
    x1d = nc.dram_tensor("i_x1", [T, 128, 1024], F32, kind="ExternalInput").ap()
    rows = nc.dram_tensor("rows", [2, 1024], F32, kind="ExternalInput").ap()
    lnr = nc.dram_tensor("lnr", [2, 1024], F32, kind="ExternalInput").ap()
    x2o = nc.dram_tensor("o_x2", [T, 128, 1024], F32, kind="ExternalOutput").ap()
    with ExitStack() as es:
        P = Prog(nc, es)
        sb, ps, ring = mk_alloc(nc, es)
        rowt = sb("rowt", [128, 2, 1024]); lnt = sb("lnt", [128, 2, 1024])
        for j in range(2):
            P.dma("sync", rowt[:, j, :], bcast_rows(rows[j], 128), writes=[f"row{j}"])
            P.dma("sync", lnt[:, j, :], bcast_rows(lnr[j], 128), writes=[f"ln{j}"])
        idr = ring("idx", 2, [128, N_EXP], I32)
        gr = ring("g", 8, [128, 1024], BF16)
        accr = ring("acc", 2, [128, 1024])
        xr = ring("x", 2, [128, 1024])
        str_ = ring("st", 2, [128, 2, 6]); mvr = ring("mv", 2, [128, 2]); rsr = ring("rs", 2, [128, 1])
        for t in range(T):
            g_ = 0 if t < 16 else 1
            ix, ixk = idr.next(); P.dma("sync", ix[:], idxd[t], writes=[ixk])
            acc, ack = accr.next()
            for e in range(N_EXP):
                gt, gk = gr.next()
                P.op("gpsimd", lambda en, gt=gt, ix=ix, e=e: en.indirect_dma_start(
                    out=gt[:, :], out_offset=None, in_=Yb[e][:, :], in_offset=bass.IndirectOffsetOnAxis(ap=ix[:, e:e + 1], axis=0)),
                    reads=[ixk], writes=[gk], dma=True)
                if e == 0:
                    P.copy("vector", acc[:], gt[:], reads=[gk], writes=[ack])
                else:
                    P.tt("vector", acc[:], acc[:], gt[:], ALU.add, reads=[ack, gk], writes=[ack])
            x, xk = xr.next(); P.dma("sync", x[:], x1d[t], writes=[xk])
            P.tt("gpsimd", acc[:], acc[:], rowt[:, g_, :], ALU.mult, reads=[ack, f"row{g_}"], writes=[ack])
            P.stt("vector", acc[:], x[:], ALPHA, acc[:], ALU.mult, ALU.add, reads=[xk, ack], writes=[ack])
            st, _ = str_.next(); mv, _ = mvr.next(); rs, _ = rsr.next()
            emit_layernorm(P, acc[:], ack, x[:], xk, st, mv, rs, f"lnC{t % 2}")
            P.tt("gpsimd", acc[:], x[:], lnt[:, 0, :], ALU.mult, reads=[xk, "ln0"], writes=[ack])
            P.tt("gpsimd", x[:], acc[:], lnt[:, 1, :], ALU.add, reads=[ack, "ln1"], writes=[xk])
            P.dma("sync", x2o[t], x[:], reads=[xk], writes=[xk], final=True)
        P.emit()
    return nc


def run_moe(h2, h2c, aff, affc, thr, thrc, x1, c1, mod_l, ln_w, ln_b, wg, wu, wd):
    B, n, D = x1.shape
    nctx = c1.shape[1]
    E = aff.shape[-1]
    h2all = np.ascontiguousarray(np.concatenate([h2.reshape(B * n, D), h2c.reshape(B * nctx, D)], 0))
    affall = np.concatenate([aff.reshape(B * n, E), affc.reshape(B * nctx, E)], 0)
    s_idx, j_idx = np.meshgrid(np.arange(128), np.arange(128), indexing="ij")
    cst = np.zeros((128, 448), np.float32)
    cst[:, 0:128] = (s_idx < j_idx); cst[:, 128:256] = 1.0; cst[:, 256:384] = np.eye(128); cst[0:64, 384:448] = (s_idx < j_idx)[:64, :64]
    tid = np.zeros((B, 128, 66), np.int64)
    iota = np.full((128, CAP_L + 128), -1.0, np.float32)
    iota[:, 0:CAP_L] = np.arange(CAP_L)
    iota[:, CAP_L:CAP_L + 32] = CAP_L + np.arange(32)
    for b in range(B):
        tid[b, :, 0:64] = (b * n + np.arange(n)).reshape(64, 128).T
        tid[b, :, 64:66] = (B * n + b * nctx + np.arange(nctx)).reshape(2, 128).T
    tid2 = np.ascontiguousarray(np.stack([tid // 128, tid % 128], -1).astype(np.float32))
    nc = build_k7g()
    in_maps = []
    for i in range(NCORES):
        es = [2 * i, 2 * i + 1]
        affL = np.zeros((2, B, 128, 66), np.float32)
        thr8 = np.zeros((2, B, 2), np.float32)
        for el, e in enumerate(es):
            for b in range(B):
                affL[el, b, :, 0:64] = aff[b, :, e].reshape(64, 128).T
                affL[el, b, :, 64:66] = affc[b, :, e].reshape(2, 128).T
                thr8[el, b] = (thr[b, e], thrc[b, e])
        lw = lambda w, kch: np.ascontiguousarray(w.reshape(2, kch, 128, w.shape[-1]).transpose(0, 2, 1, 3))
        in_maps.append({"h2all": h2all, "affL": affL,
                        "thr8": thr8.reshape(-1), "tid": tid2, "iota": iota, "cst": cst,
                        "wg": lw(wg[es], 8), "wu": lw(wu[es], 8), "wd": lw(wd[es], NFC)})
    res = _run(nc, in_maps)
    Yb = np.zeros((B, E, SLOTS_B + 1, D), h2all.dtype)
    posL = np.zeros((B, n, E), np.int32); posC = np.zeros((B, nctx, E), np.int32)
    for i in range(NCORES):
        Y = res.results[i]["o_Y"]; pos = res.results[i]["o_pos"]
        for el in range(2):
            e = 2 * i + el
            for b in range(B):
                Yb[b, e, 0:CAP_L] = Y[el, b * CAP_L:(b + 1) * CAP_L]
                Yb[b, e, CAP_L:SLOTS_B] = Y[el, (16 + b) * 128:(16 + b) * 128 + CAP_C]
                posL[b, :, e] = pos[el, b, :, 0:64].T.reshape(n)
                posC[b, :, e] = pos[el, b, :, 64:66].T.reshape(nctx)
    nc8 = build_k8()
    lnr = np.ascontiguousarray(np.stack([ln_w, ln_b]))
    in_maps = []
    for i in range(NCORES):
        b = i // 4
        rows = np.ascontiguousarray(np.stack([mod_l[b, 5120:6144], mod_l[2, 5120:6144]]))
        idx = tok_shard(posL, posC, i)
        idx[-1, 64:, :] = SLOTS_B
        m = {"idx": idx, "i_x1": tok_shard(x1, c1, i), "rows": rows, "lnr": lnr}
        for e in range(E):
            m[f"Yb{e}"] = np.ascontiguousarray(Yb[b, e])
        in_maps.append(m)
    res = _run(nc8, in_maps)
    return tok_unshard([r["o_x2"] for r in res.results], B, n, nctx)
```

```python
import numpy as np
from contextlib import ExitStack
import concourse.bass as bass
import concourse.mybir as mybir
from concourse.bass_utils import run_bass_kernel_spmd

F32 = mybir.dt.float32
BF16 = mybir.dt.bfloat16
I32 = mybir.dt.int32
AF = mybir.ActivationFunctionType
ALU = mybir.AluOpType
AX = mybir.AxisListType

NCORES = 8
STAGE_EXP = False
FUSE_WAIT = True
NO_RAW_SELF = False
SELF_SYNC = True


class Prog:
    ENGS = ("sync", "scalar", "vector", "gpsimd", "tensor")

    def __init__(self, nc, es, n_dma_sems=12):
        self.nc, self.es = nc, es
        self.ops = {e: [] for e in self.ENGS}
        self.esem = {}
        self.ecount = {}
        for e in ("scalar", "vector", "gpsimd", "tensor"):
            self.esem[e] = es.enter_context(nc.semaphore(f"sem_{e}"))
            self.ecount[e] = 0
        self.dpool = {}
        for q in ("sync", "scalar", "gpsimd"):
            self.dpool[q] = dict(
                sems=[es.enter_context(nc.semaphore(f"dsem_{q}_{i}")) for i in range(n_dma_sems)],
                cnt=[0] * n_dma_sems, nxt=0, know=[None] * n_dma_sems)
        self.semobj = {}
        self.lastw = {}
        self.readers = {}
        self.know = {e: {} for e in self.ENGS}
        self.final_tokens = []

    def _need(self, eng, tok, waits):
        sk, v, kn = tok
        if self.know[eng].get(sk, 0) >= v:
            return
        waits.append((sk, v))
        k = self.know[eng]
        for a, b in kn.items():
            if k.get(a, 0) < b:
                k[a] = b
        if k.get(sk, 0) < v:
            k[sk] = v

    def op(self, eng, fn, reads=(), writes=(), dma=False, final=False):
        waits = []
        toks = []
        own = None if dma else ("e", eng)
        for key in reads:
            t = self.lastw.get(key)
            if t is not None:
                toks.append((t, True))
        for key in writes:
            t = self.lastw.get(key)
            if t is not None:
                toks.append((t, False))
            toks.extend((r, False) for r in self.readers.get(key, ()))
        for t, raw in toks:
            if t[0] == own and (eng == "tensor" or not SELF_SYNC or (not raw and eng != "gpsimd") or (NO_RAW_SELF and eng in ("vector", "scalar"))):
                continue
            self._need(eng, t, waits)
        if dma:
            pool = self.dpool[eng]
            j = pool["nxt"]
            pool["nxt"] = (j + 1) % len(pool["sems"])
            if pool["cnt"][j] > 0:
                self._need(eng, (("d", eng, j), pool["cnt"][j], pool["know"][j]), waits)
            pool["cnt"][j] += 16
            sk = ("d", eng, j)
            self.semobj[sk] = pool["sems"][j]
            kn = dict(self.know[eng])
            pool["know"][j] = kn
            tok = (sk, pool["cnt"][j], kn)
            inc = (pool["sems"][j], 16)
        else:
            self.ecount[eng] += 1
            sk = ("e", eng)
            self.semobj[sk] = self.esem[eng]
            tok = (sk, self.ecount[eng], dict(self.know[eng]))
            inc = (self.esem[eng], 1)
        self.ops[eng].append((waits, fn, inc))
        for key in reads:
            self.readers.setdefault(key, []).append(tok)
        for key in writes:
            self.lastw[key] = tok
            self.readers[key] = []
        if final:
            self.final_tokens.append(tok)
        return tok

    def emit(self):
        waits = []
        for t in self.final_tokens:
            self._need("sync", t, waits)
        if waits:
            self.ops["sync"].append((waits, None, None))
        nc = self.nc
        with nc.Block() as block:
            def run(eng_name):
                def body(eng):
                    for waits, fn, inc in self.ops[eng_name]:
                        fused = FUSE_WAIT and fn is not None and len(waits) > 0
                        for sk, v in (waits[:-1] if fused else waits):
                            eng.wait_ge(self.semobj[sk], v)
                        if fn is not None:
                            ins = fn(eng)
                            if fused:
                                ins._wait_ge(self.semobj[waits[-1][0]], waits[-1][1])
                            ins.then_inc(inc[0], inc[1])
                return body
            block.sync(run("sync"))
            block.scalar(run("scalar"))
            block.vector(run("vector"))
            block.gpsimd(run("gpsimd"))
            block.tensor(run("tensor"))

    def dma(self, q, out, in_, reads=(), writes=(), final=False, **kw):
        return self.op(q, lambda e: e.dma_start(out=out, in_=in_, **kw), reads, writes, dma=True, final=final)

    def mm(self, out, lhsT, rhs, start, stop, reads=(), writes=()):
        return self.op("tensor", lambda e: e.matmul(out, lhsT, rhs, start=start, stop=stop), reads, writes)

    def act(self, out, in_, func, reads=(), writes=(), eng="scalar", **kw):
        return self.op(eng, lambda e: e.activation(out=out, in_=in_, func=func, **kw), reads, writes)

    def tt(self, eng, out, in0, in1, op, reads=(), writes=()):
        return self.op(eng, lambda e: e.tensor_tensor(out=out, in0=in0, in1=in1, op=op), reads, writes)

    def ts(self, eng, out, in0, s1, s2, op0, op1=None, reads=(), writes=(), accum_out=None):
        kw = {}
        if op1 is not None:
            kw["op1"] = op1
        if accum_out is not None:
            kw["accum_out"] = accum_out
        return self.op(eng, lambda e: e.tensor_scalar(out=out, in0=in0, scalar1=s1, scalar2=s2, op0=op0, **kw),
                       reads, writes)

    def stt(self, eng, out, in0, scalar, in1, op0, op1, reads=(), writes=()):
        return self.op(eng, lambda e: e.scalar_tensor_tensor(out=out, in0=in0, scalar=scalar, in1=in1,
                                                             op0=op0, op1=op1), reads, writes)

    def copy(self, eng, out, in_, reads=(), writes=()):
        if eng == "scalar":
            return self.op(eng, lambda e: e.activation(out=out, in_=in_, func=AF.Copy), reads, writes)
        return self.op(eng, lambda e: e.tensor_copy(out=out, in_=in_), reads, writes)

    def tr(self, out, in_, ident, reads=(), writes=()):
        return self.op("tensor", lambda e: e.transpose(out, in_, ident), reads, writes)

    def memset(self, eng, ap, val, writes=()):
        return self.op(eng, lambda e: e.memset(ap, val), (), writes)

    def gen(self, eng, f, reads=(), writes=()):
        return self.op(eng, f, reads, writes)


def _run(nc, in_maps):
    return run_bass_kernel_spmd(nc, in_maps, core_ids=list(range(NCORES)))


D_MODEL = 1024
DEPTH = 2
N_MOD = 6
MODC = N_MOD * D_MODEL // NCORES


def build_k0():
    nc = bass.Bass("TRN2", target_bir_lowering=False)
    cvT = nc.dram_tensor("cvT", [128, 8, 3], F32, kind="ExternalInput").ap()
    wm = nc.dram_tensor("wm", [DEPTH, 128, 8, MODC], F32, kind="ExternalInput").ap()
    bm = nc.dram_tensor("bm", [DEPTH, 3, MODC], F32, kind="ExternalInput").ap()
    out = nc.dram_tensor("mod", [DEPTH, 3, MODC], F32, kind="ExternalOutput").ap()
    with ExitStack() as es:
        P = Prog(nc, es)
        sb = lambda name, shape, dt=F32: es.enter_context(nc.sbuf_tensor(name, shape, dt))
        cv = sb("cv", [128, 8, 3])
        cs = sb("cs", [128, 8, 3])
        w = [sb(f"w{l}", [128, 8, MODC]) for l in range(DEPTH)]
        b = sb("b", [3, DEPTH, MODC])
        o = sb("o", [3, DEPTH, MODC])
        ps = [es.enter_context(nc.psum_tensor(f"ps{i}", [128, 512], F32)) for i in range(2)]
        P.dma("sync", cv[:], cvT, writes=["cv"])
        for l in range(DEPTH):
            P.dma("sync" if l == 0 else "gpsimd", w[l][:], wm[l], writes=[f"w{l}"])
            P.dma("sync", b[:, l, :], bm[l], writes=[f"b{l}"])
        P.act(cs[:], cv[:], AF.Silu, reads=["cv"], writes=["cs"])
        H = MODC // 2
        for l in range(DEPTH):
            for h in range(2):
                pt = ps[h]
                for kc in range(8):
                    P.mm(pt[0:3, 0:H], cs[:, kc, :], w[l][:, kc, h * H:(h + 1) * H], kc == 0, kc == 7,
                         reads=["cs", f"w{l}"], writes=[f"ps{h}"])
                P.op("vector", lambda e, l=l, h=h, pt=pt: e.tensor_tensor(
                    out=o[:, l, h * H:(h + 1) * H], in0=pt[0:3, 0:H], in1=b[:, l, h * H:(h + 1) * H], op=ALU.add),
                    reads=[f"ps{h}", f"b{l}"], writes=[f"o{l}{h}"])
            P.dma("sync", out[l], o[:, l, :], reads=[f"o{l}0", f"o{l}1"], final=True)
        P.emit()
    return nc


def run_k0(c, c_ctx, w_mod, b_mod):
    cv = np.concatenate([c, c_ctx[None]], 0)
    cvT = np.ascontiguousarray(cv.T.reshape(8, 128, 3).transpose(1, 0, 2))
    nc = build_k0()
    in_maps = []
    for i in range(NCORES):
        sl = slice(i * MODC, (i + 1) * MODC)
        wm = np.ascontiguousarray(w_mod[:, :, sl].reshape(DEPTH, 8, 128, MODC).transpose(0, 2, 1, 3))
        bm = np.ascontiguousarray(np.broadcast_to(b_mod[:, None, sl], (DEPTH, 3, MODC)))
        in_maps.append({"cvT": cvT, "wm": wm, "bm": bm})
    res = _run(nc, in_maps)
    return np.concatenate([r["mod"] for r in res.results], axis=-1)


class Ring:
    def __init__(self, items):
        self.items, self.i = items, 0

    def next(self):
        it = self.items[self.i % len(self.items)]
        self.i += 1
        return it


def mk_alloc(nc, es):
    def sb(name, shape, dt=F32):
        return es.enter_context(nc.sbuf_tensor(name, shape, dt))

    def ps(name, shape, dt=F32):
        return es.enter_context(nc.psum_tensor(name, shape, dt))

    def ring(name, n, shape, dt=F32, psum=False):
        return Ring([((ps if psum else sb)(f"{name}{i}", shape, dt), f"{name}{i}") for i in range(n)])
    return sb, ps, ring


def bcast_rows(ap1d, nparts):
    return bass.AP(tensor=ap1d.tensor, offset=ap1d.offset, ap=[[0, nparts]] + [list(x) for x in ap1d.ap])


LN_EPS = 1e-5


def emit_layernorm(P, x, xkey, xn, xnkey, st, mv, rs, skey, n=1024):
    for j in range(n // 512):
        P.gen("vector", lambda e, j=j: e.bn_stats(out=st[:, j, :], in_=x[:, j * 512:(j + 1) * 512]),
              reads=[xkey], writes=[skey + f"st{j}"])
    P.gen("vector", lambda e: e.bn_aggr(out=mv[:], in_=st[:]),
          reads=[skey + f"st{j}" for j in range(n // 512)], writes=[skey + "mv"])
    P.ts("vector", rs[:], mv[:, 1:2], LN_EPS, None, ALU.add, reads=[skey + "mv"], writes=[skey + "rs"])
    P.act(rs[:], rs[:], AF.Sqrt, reads=[skey + "rs"], writes=[skey + "rs"])
    P.gen("vector", lambda e: e.reciprocal(out=rs[:], in_=rs[:]), reads=[skey + "rs"], writes=[skey + "rs"])
    P.ts("vector", xn, x, mv[:, 0:1], rs[:, 0:1], ALU.subtract, ALU.mult,
         reads=[xkey, skey + "mv", skey + "rs"], writes=[xnkey])


N_IN = 2576
K1_TILES = 17


def build_k1():
    nc = bass.Bass("TRN2", target_bir_lowering=False)
    xt = nc.dram_tensor("xt", [K1_TILES, 128, 1024], F32, kind="ExternalInput").ap()
    modr = nc.dram_tensor("modr", [2, 2, 1024], F32, kind="ExternalInput").ap()
    win = nc.dram_tensor("win", [128, 8, N_IN], F32, kind="ExternalInput").ap()
    identd = nc.dram_tensor("ident", [128, 128], F32, kind="ExternalInput").ap()
    out = nc.dram_tensor("p", [K1_TILES, 128, N_IN], F32, kind="ExternalOutput").ap()
    with ExitStack() as es:
        P = Prog(nc, es)
        sb, ps, ring = mk_alloc(nc, es)
        idf = sb("idf", [128, 128])
        idb = sb("idb", [128, 128], BF16)
        P.dma("sync", idf[:], identd, writes=["idf"])
        P.copy("vector", idb[:], idf[:], reads=["idf"], writes=["idb"])
        modt = sb("modt", [128, 2, 2, 1024])
        for g in range(2):
            for j in range(2):
                P.dma("sync", modt[:, g, j, :], bcast_rows(modr[g, j], 128), writes=[f"mod{g}{j}"])
            P.ts("vector", modt[:, g, 0, :], modt[:, g, 0, :], 1.0, None, ALU.add,
                 reads=[f"mod{g}0"], writes=[f"mod{g}0"])
        wbf = sb("wbf", [128, 8, N_IN], BF16)
        wst = ring("wst", 2, [128, N_IN])
        for kc in range(8):
            t, k = wst.next()
            P.dma("gpsimd" if kc % 2 else "sync", t[:], win[:, kc, :], writes=[k])
            P.copy("gpsimd" if kc % 2 else "vector", wbf[:, kc, :], t[:], reads=[k], writes=[f"wbf{kc}"])
        wkeys = [f"wbf{kc}" for kc in range(8)]
        xr = ring("x", 2, [128, 1024])
        xnr = ring("xn", 2, [128, 1024])
        h1r = ring("h1", 2, [128, 1024])
        hr = ring("h", 2, [128, 1024], BF16)
        hTr = ring("hT", 2, [128, 1024], BF16)
        orr = ring("o", 2, [128, N_IN])
        str_ = ring("st", 2, [128, 2, 6])
        mvr = ring("mv", 2, [128, 2])
        rsr = ring("rs", 2, [128, 1])
        pTr = ring("pT", 2, [128, 1024], BF16, psum=True)
        pmr = ring("pm", 4, [128, 512], F32, psum=True)
        ev = 0
        for t in range(K1_TILES):
            g = 0 if t < 16 else 1
            x, xk = xr.next()
            P.dma("sync", x[:], xt[t], writes=[xk])
            xn, xnk = xnr.next()
            st, _ = str_.next(); mv, _ = mvr.next(); rs, _ = rsr.next()
            emit_layernorm(P, x[:], xk, xn[:], xnk, st, mv, rs, f"ln{t % 2}")
            h1, h1k = h1r.next()
            P.tt("gpsimd", h1[:], xn[:], modt[:, g, 0, :], ALU.mult, reads=[xnk, f"mod{g}0"], writes=[h1k])
            h, hk = hr.next()
            P.tt("gpsimd", h[:], h1[:], modt[:, g, 1, :], ALU.add, reads=[h1k, f"mod{g}1"], writes=[hk])
            pT, pTk = pTr.next()
            for kc in range(8):
                P.tr(pT[:, kc * 128:(kc + 1) * 128], h[:, kc * 128:(kc + 1) * 128], idb[:],
                     reads=[hk, "idb"], writes=[pTk])
            hT, hTk = hTr.next()
            P.copy("scalar", hT[:], pT[:], reads=[pTk], writes=[hTk])
            o, ok = orr.next()
            for cg in range(6):
                c0 = cg * 512
                n = min(512, N_IN - c0)
                pm, pmk = pmr.next()
                for kc in range(8):
                    P.mm(pm[:, 0:n], hT[:, kc * 128:(kc + 1) * 128], wbf[:, kc, c0:c0 + n], kc == 0, kc == 7,
                         reads=[hTk, wkeys[kc]], writes=[pmk])
                P.copy("scalar" if ev % 2 else "vector", o[:, c0:c0 + n], pm[:, 0:n], reads=[pmk], writes=[ok + f"c{cg}"])
                ev += 1
            P.dma("gpsimd", out[t], o[:], reads=[ok + f"c{cg}" for cg in range(6)], writes=[ok + "dma"], final=True)
        P.emit()
    return nc


def lay_w(w, kchunks):
    return np.ascontiguousarray(w.reshape(kchunks, 128, w.shape[1]).transpose(1, 0, 2))


def run_k1(x, ctx, mod_l, w_in_l):
    nc = build_k1()
    B, n, D = x.shape
    seg = n // 4
    ctxf = ctx.reshape(-1, D)
    ident = np.eye(128, dtype=np.float32)
    win = lay_w(w_in_l, 8)
    in_maps = []
    for i in range(NCORES):
        b, s = i // 4, i % 4
        xt = np.zeros((K1_TILES * 128, D), np.float32)
        xt[:seg] = x[b, s * seg:(s + 1) * seg]
        xt[seg:seg + 64] = ctxf[i * 64:(i + 1) * 64]
        modr = np.stack([np.stack([mod_l[b, 1024:2048], mod_l[b, 0:1024]]),
                         np.stack([mod_l[2, 1024:2048], mod_l[2, 0:1024]])])
        in_maps.append({"xt": xt.reshape(K1_TILES, 128, D), "modr": np.ascontiguousarray(modr), "win": win,
                        "ident": ident})
    res = _run(nc, in_maps)
    P_lat = np.zeros((B, n, N_IN), np.float32)
    P_ctx = np.zeros((B * ctx.shape[1], N_IN), np.float32)
    for i in range(NCORES):
        b, s = i // 4, i % 4
        p = res.results[i]["p"].reshape(K1_TILES * 128, N_IN)
        P_lat[b, s * seg:(s + 1) * seg] = p[:seg]
        P_ctx[i * 64:(i + 1) * 64] = p[seg:seg + 64]
    return P_lat, P_ctx.reshape(B, ctx.shape[1], N_IN)


ALPHA = (2.0 * DEPTH) ** 0.25
N_EXP = 16


def build_k5():
    nc = bass.Bass("TRN2", target_bir_lowering=False)
    T = K1_TILES
    yt = nc.dram_tensor("yt", [T, 128, 1024], F32, kind="ExternalInput").ap()
    xt = nc.dram_tensor("xt", [T, 128, 1024], F32, kind="ExternalInput").ap()
    wout = nc.dram_tensor("wout", [128, 8, 1024], F32, kind="ExternalInput").ap()
    rows = nc.dram_tensor("rows", [2, 3, 1024], F32, kind="ExternalInput").ap()
    lnr = nc.dram_tensor("lnr", [2, 1024], F32, kind="ExternalInput").ap()
    rwd = nc.dram_tensor("rw", [128, 8, N_EXP], F32, kind="ExternalInput").ap()
    rbd = nc.dram_tensor("rb", [N_EXP], F32, kind="ExternalInput").ap()
    identd = nc.dram_tensor("ident", [128, 128], F32, kind="ExternalInput").ap()
    x1o = nc.dram_tensor("o_x1", [T, 128, 1024], F32, kind="ExternalOutput").ap()
    h2o = nc.dram_tensor("o_h2", [T, 128, 1024], BF16, kind="ExternalOutput").ap()
    affo = nc.dram_tensor("o_aff", [T, 128, N_EXP], F32, kind="ExternalOutput").ap()
    with ExitStack() as es:
        P = Prog(nc, es)
        sb, ps, ring = mk_alloc(nc, es)
        idf = sb("idf", [128, 128])
        idb = sb("idb", [128, 128], BF16)
        P.dma("sync", idf[:], identd, writes=["idf"])
        P.copy("vector", idb[:], idf[:], reads=["idf"], writes=["idb"])
        rowt = sb("rowt", [128, 2, 3, 1024])
        for g in range(2):
            for j in range(3):
                P.dma("sync", rowt[:, g, j, :], bcast_rows(rows[g, j], 128), writes=[f"row{g}{j}"])
            P.ts("vector", rowt[:, g, 1, :], rowt[:, g, 1, :], 1.0, None, ALU.add,
                 reads=[f"row{g}1"], writes=[f"row{g}1"])
        lnt = sb("lnt", [128, 2, 1024])
        for j in range(2):
            P.dma("sync", lnt[:, j, :], bcast_rows(lnr[j], 128), writes=[f"ln{j}"])
        rw = sb("rwt", [128, 8, N_EXP])
        P.dma("sync", rw[:], rwd, writes=["rw"])
        rb = sb("rbt", [128, N_EXP])
        P.dma("sync", rb[:], bcast_rows(rbd, 128), writes=["rb"])
        wbf = sb("wbf", [128, 8, 1024], BF16)
        wst = ring("wst", 2, [128, 1024])
        for kc in range(8):
            t, k = wst.next()
            P.dma("gpsimd" if kc % 2 else "sync", t[:], wout[:, kc, :], writes=[k])
            P.copy("gpsimd" if kc % 2 else "vector", wbf[:, kc, :], t[:], reads=[k], writes=[f"wbf{kc}"])
        wkeys = [f"wbf{kc}" for kc in range(8)]
        yr = ring("y", 2, [128, 1024]); ybr = ring("yb", 2, [128, 1024], BF16)
        yTr = ring("yT", 2, [128, 1024], BF16)
        xr = ring("x", 2, [128, 1024]); tmpr = ring("tmp", 2, [128, 1024]); rr = ring("r", 3, [128, 1024])
        xnr = ring("xn", 2, [128, 1024]); x1r = ring("x1_", 2, [128, 1024]); x1ar = ring("x1a", 2, [128, 1024])
        xn2r = ring("xn2", 2, [128, 1024]); h2fr = ring("h2f", 2, [128, 1024]); h2ar = ring("h2a", 2, [128, 1024])
        h2br = ring("h2b", 2, [128, 1024], BF16)
        h2Tr = ring("h2T", 2, [128, 1024])
        str_ = ring("st", 4, [128, 2, 6]); mvr = ring("mv", 4, [128, 2]); rsr = ring("rs", 4, [128, 1])
        lgr = ring("lg", 2, [128, N_EXP]); exr = ring("ex", 2, [128, N_EXP]); afr = ring("af", 2, [128, N_EXP])
        smr = ring("sm", 2, [128, 4])
        pTr = ring("pT", 1, [128, 1024], BF16, psum=True)
        pmr = ring("pm", 2, [128, 512], F32, psum=True)
        pTfr = ring("pTf", 1, [128, 1024], F32, psum=True)
        plr = ring("pl", 1, [128, N_EXP], F32, psum=True)
        lnc = 0
        lnc_box = [0]

        def stage_a(t):
            g = 0 if t < 16 else 1
            y, yk = yr.next(); P.dma("sync", y[:], yt[t], writes=[yk])
            x, xk = xr.next(); P.dma("sync", x[:], xt[t], writes=[xk])
            yb, ybk = ybr.next(); P.copy("gpsimd", yb[:], y[:], reads=[yk], writes=[ybk])
            pT, pTk = pTr.next()
            for kc in range(8):
                P.tr(pT[:, kc * 128:(kc + 1) * 128], yb[:, kc * 128:(kc + 1) * 128], idb[:], reads=[ybk, "idb"], writes=[pTk])
            yT, yTk = yTr.next(); P.copy("scalar", yT[:], pT[:], reads=[pTk], writes=[yTk])
            tmp, tmpk = tmpr.next()
            for hf in range(2):
                pm, pmk = pmr.next()
                for kc in range(8):
                    P.mm(pm[:], yT[:, kc * 128:(kc + 1) * 128], wbf[:, kc, hf * 512:(hf + 1) * 512], kc == 0, kc == 7,
                         reads=[yTk, wkeys[kc]], writes=[pmk])
                P.tt("vector", tmp[:, hf * 512:(hf + 1) * 512], pm[:], rowt[:, g, 0, hf * 512:(hf + 1) * 512], ALU.mult,
                     reads=[pmk, f"row{g}0"], writes=[tmpk + str(hf)])
            r, rk = rr.next()
            P.stt("vector", r[:], x[:], ALPHA, tmp[:], ALU.mult, ALU.add, reads=[xk, tmpk + "0", tmpk + "1"], writes=[rk])
            return r, rk

        def stage_b(t, r, rk):
            g = 0 if t < 16 else 1
            lnc = lnc_box[0]
            xn, xnk = xnr.next(); st, _ = str_.next(); mv, _ = mvr.next(); rs, _ = rsr.next()
            emit_layernorm(P, r[:], rk, xn[:], xnk, st, mv, rs, f"lnA{lnc % 4}"); lnc += 1
            x1a, x1ak = x1ar.next(); x1, x1k = x1r.next()
            P.tt("gpsimd", x1a[:], xn[:], lnt[:, 0, :], ALU.mult, reads=[xnk, "ln0"], writes=[x1ak])
            P.tt("gpsimd", x1[:], x1a[:], lnt[:, 1, :], ALU.add, reads=[x1ak, "ln1"], writes=[x1k])
            P.dma("gpsimd", x1o[t], x1[:], reads=[x1k], writes=[x1k + "d"], final=True)
            xn2, xn2k = xn2r.next(); st, _ = str_.next(); mv, _ = mvr.next(); rs, _ = rsr.next()
            emit_layernorm(P, x1[:], x1k, xn2[:], xn2k, st, mv, rs, f"lnA{lnc % 4}"); lnc += 1
            h2a, h2ak = h2ar.next(); h2f, h2fk = h2fr.next(); h2b, h2bk = h2br.next()
            P.tt("gpsimd", h2a[:], xn2[:], rowt[:, g, 1, :], ALU.mult, reads=[xn2k, f"row{g}1"], writes=[h2ak])
            P.tt("vector", h2f[:], h2a[:], rowt[:, g, 2, :], ALU.add, reads=[h2ak, f"row{g}2"], writes=[h2fk])
            P.copy("scalar", h2b[:], h2f[:], reads=[h2fk], writes=[h2bk])
            P.dma("sync", h2o[t], h2b[:], reads=[h2bk], writes=[h2bk + "d"], final=True)
            pTf, pTfk = pTfr.next()
            for kc in range(8):
                P.tr(pTf[:, kc * 128:(kc + 1) * 128], h2f[:, kc * 128:(kc + 1) * 128], idf[:], reads=[h2fk, "idf"], writes=[pTfk])
            h2T, h2Tk = h2Tr.next()
            P.copy("scalar", h2T[:, 0:512], pTf[:, 0:512], reads=[pTfk], writes=[h2Tk + "a"])
            P.copy("vector", h2T[:, 512:1024], pTf[:, 512:1024], reads=[pTfk], writes=[h2Tk + "b"])
            pl, plk = plr.next()
            for kc in range(8):
                P.mm(pl[:], h2T[:, kc * 128:(kc + 1) * 128], rw[:, kc, :], kc == 0, kc == 7,
                     reads=[h2Tk + "a", h2Tk + "b", "rw"], writes=[plk])
            lg, lgk = lgr.next(); ex, exk = exr.next(); af, afk = afr.next(); sm, smk = smr.next()
            P.tt("vector", lg[:], pl[:], rb[:], ALU.add, reads=[plk, "rb"], writes=[lgk])
            P.gen("vector", lambda e, sm=sm, lg=lg: e.reduce_max(out=sm[:, 0:1], in_=lg[:], axis=AX.X), reads=[lgk], writes=[smk + "m"])
            P.ts("vector", sm[:, 1:2], sm[:, 0:1], -1.0, None, ALU.mult, reads=[smk + "m"], writes=[smk + "n"])
            P.act(ex[:], lg[:], AF.Exp, reads=[lgk, smk + "n"], writes=[exk, smk + "s"], bias=sm[:, 1:2], scale=1.0,
                  accum_out=sm[:, 2:3])
            P.gen("vector", lambda e, sm=sm: e.reciprocal(out=sm[:, 3:4], in_=sm[:, 2:3]), reads=[smk + "s"], writes=[smk + "r"])
            P.ts("vector", af[:], ex[:], sm[:, 3:4], None, ALU.mult, reads=[exk, smk + "r"], writes=[afk])
            P.dma("gpsimd", affo[t], af[:], reads=[afk], writes=[afk + "d"], final=True)
            lnc_box[0] = lnc

        held = {}
        for t in range(T + 1):
            if t < T:
                held[t] = stage_a(t)
            if t >= 1:
                stage_b(t - 1, *held.pop(t - 1))

        P.emit()
    return nc


def tok_shard(lat, ctx, i):
    B, n, D = lat.shape
    seg = n // 4
    b, s = i // 4, i % 4
    out = np.zeros((K1_TILES * 128, D), lat.dtype)
    out[:seg] = lat[b, s * seg:(s + 1) * seg]
    out[seg:seg + 64] = ctx.reshape(-1, D)[i * 64:(i + 1) * 64]
    return out.reshape(K1_TILES, 128, D)


def tok_unshard(parts, B, n, nctx):
    D = parts[0].shape[-1]
    seg = n // 4
    lat = np.zeros((B, n, D), parts[0].dtype)
    ctx = np.zeros((B * nctx, D), parts[0].dtype)
    for i in range(NCORES):
        b, s = i // 4, i % 4
        p = parts[i].reshape(K1_TILES * 128, D)
        lat[b, s * seg:(s + 1) * seg] = p[:seg]
        ctx[i * 64:(i + 1) * 64] = p[seg:seg + 64]
    return lat, ctx.reshape(B, nctx, D)


def run_k5(ycat_l, ycat_c, x, ctx, mod_l, w_out_l, ln_w, ln_b, router_w_l, router_b_l):
    nc = build_k5()
    B, n, D = x.shape
    ident = np.eye(128, dtype=np.float32)
    wout = lay_w(w_out_l, 8)
    rw = lay_w(router_w_l, 8)
    lnr = np.ascontiguousarray(np.stack([ln_w, ln_b]))
    in_maps = []
    for i in range(NCORES):
        b = i // 4
        rows = np.stack([np.stack([mod_l[m, 2048:3072], mod_l[m, 4096:5120], mod_l[m, 3072:4096]]) for m in (b, 2)])
        in_maps.append({"yt": tok_shard(ycat_l, ycat_c, i), "xt": tok_shard(x, ctx, i), "wout": wout,
                        "rows": np.ascontiguousarray(rows), "lnr": lnr, "rw": rw,
                        "rb": np.ascontiguousarray(router_b_l), "ident": ident})
    res = _run(nc, in_maps)
    nctx = ctx.shape[1]
    x1, c1 = tok_unshard([r["o_x1"] for r in res.results], B, n, nctx)
    h2, h2c = tok_unshard([r["o_h2"] for r in res.results], B, n, nctx)
    aff, affc = tok_unshard([r["o_aff"] for r in res.results], B, n, nctx)
    return x1, c1, h2, h2c, aff, affc


K6_ITERS = 26


def build_k6(F_lat, k_lat, F_ctx, k_ctx):
    nc = bass.Bass("TRN2", target_bir_lowering=False)
    R = 32
    ald = nc.dram_tensor("al", [R, F_lat], F32, kind="ExternalInput").ap()
    acd = nc.dram_tensor("ac", [R, F_ctx], F32, kind="ExternalInput").ap()
    thro = nc.dram_tensor("thr", [R, 2], F32, kind="ExternalOutput").ap()
    with ExitStack() as es:
        P = Prog(nc, es)
        sb, ps, ring = mk_alloc(nc, es)
        res = sb("res", [R, 2])
        for pi, (src, F, k) in enumerate(((ald, F_lat, k_lat), (acd, F_ctx, k_ctx))):
            A = sb(f"A{pi}", [R, F])
            junk = sb(f"junk{pi}", [R, F], BF16)
            sc = sb(f"sc{pi}", [R, 8])
            lo, hi, mid, cnt, cond, t1, t2 = (sc[:, j:j + 1] for j in range(7))
            kk = f"p{pi}"
            P.dma("sync", A[:], src, writes=[kk + "A"])
            P.memset("vector", lo, 0.0, writes=[kk + "lo"])
            P.memset("vector", hi, 1.0, writes=[kk + "hi"])
            for it in range(K6_ITERS):
                P.tt("vector", mid, lo, hi, ALU.add, reads=[kk + "lo", kk + "hi"], writes=[kk + "mid"])
                P.ts("vector", mid, mid, 0.5, None, ALU.mult, reads=[kk + "mid"], writes=[kk + "mid"])
                P.ts("vector", junk[:], A[:], mid, None, ALU.is_ge, ALU.add, reads=[kk + "A", kk + "mid"],
                     writes=[kk + "junk", kk + "cnt"], accum_out=cnt)
                P.ts("vector", cond, cnt, float(k) - 0.5, None, ALU.is_ge, reads=[kk + "cnt"], writes=[kk + "cond"])
                P.tt("vector", t1, cond, mid, ALU.mult, reads=[kk + "cond", kk + "mid"], writes=[kk + "t1"])
                P.stt("vector", t2, cond, 2.0, mid, ALU.mult, ALU.add, reads=[kk + "cond", kk + "mid"], writes=[kk + "t2"])
                P.tt("vector", lo, lo, t1, ALU.max, reads=[kk + "lo", kk + "t1"], writes=[kk + "lo"])
                P.tt("vector", hi, hi, t2, ALU.min, reads=[kk + "hi", kk + "t2"], writes=[kk + "hi"])
            P.copy("vector", res[:, pi:pi + 1], lo, reads=[kk + "lo"], writes=[f"res{pi}"])
        P.dma("sync", thro, res[:], reads=["res0", "res1"], final=True)
        P.emit()
    return nc


def run_k6(aff, affc):
    B, n, E = aff.shape
    ncx = affc.shape[1]
    nc = build_k6(n, 2 * n // E, ncx, 2 * ncx // E)
    al = np.ascontiguousarray(aff.transpose(0, 2, 1).reshape(B * E, n))
    ac = np.ascontiguousarray(affc.transpose(0, 2, 1).reshape(B * E, ncx))
    res = _run(nc, [{"al": al, "ac": ac} for _ in range(NCORES)])
    thr = res.results[0]["thr"]
    return thr[:, 0].reshape(B, E), thr[:, 1].reshape(B, E)


D_FF = 2816
NFC = D_FF // 128
K7_TOK = K1_TILES * 128


def build_k7(n_exp=N_EXP):
    nc = bass.Bass("TRN2", target_bir_lowering=False)
    T = K1_TILES
    h2Td = nc.dram_tensor("h2T", [128, 8, K7_TOK], BF16, kind="ExternalInput").ap()
    affd = nc.dram_tensor("aff", [T, 128, N_EXP], F32, kind="ExternalInput").ap()
    thrd = nc.dram_tensor("thr", [2, N_EXP], F32, kind="ExternalInput").ap()
    x1d = nc.dram_tensor("i_x1", [T, 128, 1024], F32, kind="ExternalInput").ap()
    rows = nc.dram_tensor("rows", [2, 1024], F32, kind="ExternalInput").ap()
    lnr = nc.dram_tensor("lnr", [2, 1024], F32, kind="ExternalInput").ap()
    wgd = nc.dram_tensor("wg", [n_exp, 128, 8, D_FF], F32, kind="ExternalInput").ap()
    wud = nc.dram_tensor("wu", [n_exp, 128, 8, D_FF], F32, kind="ExternalInput").ap()
    wdd = nc.dram_tensor("wd", [n_exp, 128, NFC, 1024], F32, kind="ExternalInput").ap()
    x2o = nc.dram_tensor("o_x2", [T, 128, 1024], F32, kind="ExternalOutput").ap()
    with ExitStack() as es:
        P = Prog(nc, es)
        sb, ps, ring = mk_alloc(nc, es)
        h2T = sb("h2Ts", [128, 8, K7_TOK], BF16)
        for kc in range(8):
            P.dma("sync", h2T[:, kc, :], h2Td[:, kc, :], writes=["h2T"] if kc == 7 else [f"h2T_{kc}"])
        h2keys = ["h2T"] + [f"h2T_{kc}" for kc in range(7)]
        thrb = sb("thrb", [128, 2, N_EXP])
        for g in range(2):
            P.dma("sync", thrb[:, g, :], bcast_rows(thrd[g], 128), writes=[f"thr{g}"])
        wgt = sb("wgt", [128, T, N_EXP])
        msk = sb("msk", [128, T, N_EXP])
        for t in range(T):
            g = 0 if t < 16 else 1
            P.dma("sync", wgt[:, t, :], affd[t], writes=[f"aff{t}"])
            P.tt("vector", msk[:, t, :], wgt[:, t, :], thrb[:, g, :], ALU.is_ge, reads=[f"aff{t}", f"thr{g}"], writes=[f"msk{t}"])
            P.tt("vector", wgt[:, t, :], wgt[:, t, :], msk[:, t, :], ALU.mult, reads=[f"aff{t}", f"msk{t}"], writes=[f"aff{t}"])
        acc = sb("acc", [128, T, 1024])
        for t in range(T):
            P.memset("gpsimd", acc[:, t, :], 0.0, writes=[f"acc{t}a", f"acc{t}b"])
        FG = 4
        wgr = ring("wgb", 2, [128, 8, FG * 128], BF16)
        wur = ring("wub", 2, [128, 8, FG * 128], BF16)
        wdr = ring("wdb", 2, [128, FG, 1024], BF16)
        stg = ring("stg", 4, [128, 1024])
        actr = ring("actT", 2, [128, FG, 512], BF16)
        sgr = ring("sg", 2, [128, 512])
        pgr = ring("pg", 2, [128, 512], F32, psum=True)
        pur = ring("pu", 2, [128, 512], F32, psum=True)
        pyr = ring("py", 2, [128, 512], F32, psum=True)
        tgs = [(s, min(512, K7_TOK - s)) for s in range(0, K7_TOK, 512)]
        ci = 0
        for e in range(n_exp):
            for f0 in range(0, NFC, FG):
                nf = min(FG, NFC - f0)
                wgb, wgk = wgr.next(); wub, wuk = wur.next(); wdb, wdk = wdr.next()
                for (src, dst, dk) in ((wgd, wgb, wgk), (wud, wub, wuk)):
                    for kp in range(0, 8, 2):
                        st, sk = stg.next()
                        q = "sync"
                        sv = st[:, 0:2 * nf * 128].rearrange("p (a f) -> p a f", a=2)
                        P.dma(q, sv, src[e, :, kp:kp + 2, f0 * 128:(f0 + nf) * 128], writes=[sk])
                        P.copy("gpsimd", dst[:, kp:kp + 2, 0:nf * 128], sv, reads=[sk], writes=[dk + f"k{kp}"])
                        ci += 1
                for fc in range(nf):
                    st, sk = stg.next()
                    P.dma("sync", st[:], wdd[e, :, f0 + fc, :], writes=[sk])
                    P.copy("gpsimd", wdb[:, fc, :], st[:], reads=[sk], writes=[wdk + f"f{fc}"])
                    ci += 1
                for (s0, ns) in tgs:
                    actT, ak = actr.next()
                    for fc in range(nf):
                        pg, pgk = pgr.next(); pu, puk = pur.next()
                        for kc in range(8):
                            P.mm(pg[:, 0:ns], wgb[:, kc, fc * 128:(fc + 1) * 128], h2T[:, kc, s0:s0 + ns], kc == 0, kc == 7,
                                 reads=h2keys + [wgk + f"k{kc - kc % 2}"], writes=[pgk])
                        for kc in range(8):
                            P.mm(pu[:, 0:ns], wub[:, kc, fc * 128:(fc + 1) * 128], h2T[:, kc, s0:s0 + ns], kc == 0, kc == 7,
                                 reads=h2keys + [wuk + f"k{kc - kc % 2}"], writes=[puk])
                        sg, sgk = sgr.next()
                        P.act(sg[:, 0:ns], pg[:, 0:ns], AF.Silu, reads=[pgk], writes=[sgk])
                        P.tt("vector", actT[:, fc, 0:ns], pu[:, 0:ns], sg[:, 0:ns], ALU.mult, reads=[puk, sgk], writes=[ak + f"f{fc}"])
                    for tt in range(s0 // 128, (s0 + ns) // 128):
                        for hf in range(2):
                            py, pyk = pyr.next()
                            for fc in range(nf):
                                P.mm(py[:], actT[:, fc, tt * 128 - s0:(tt + 1) * 128 - s0], wdb[:, fc, hf * 512:(hf + 1) * 512],
                                     fc == 0, fc == nf - 1, reads=[ak + f"f{fc}", wdk + f"f{fc}"], writes=[pyk])
                            ah = f"acc{tt}" + "ab"[hf]
                            P.stt("vector", acc[:, tt, hf * 512:(hf + 1) * 512], py[:], wgt[:, tt, e:e + 1],
                                  acc[:, tt, hf * 512:(hf + 1) * 512], ALU.mult, ALU.add, reads=[pyk, f"aff{tt}", ah], writes=[ah])
        rowt = sb("rowt", [128, 2, 1024])
        lnt = sb("lnt", [128, 2, 1024])
        for j in range(2):
            P.dma("sync", rowt[:, j, :], bcast_rows(rows[j], 128), writes=[f"row{j}"])
            P.dma("sync", lnt[:, j, :], bcast_rows(lnr[j], 128), writes=[f"ln{j}"])
        xr = ring("x", 2, [128, 1024])
        str_ = ring("st", 2, [128, 2, 6]); mvr = ring("mv", 2, [128, 2]); rsr = ring("rs", 2, [128, 1])
        for t in range(T):
            g = 0 if t < 16 else 1
            ak2 = [f"acc{t}a", f"acc{t}b"]
            x, xk = xr.next(); P.dma("sync", x[:], x1d[t], writes=[xk])
            P.tt("gpsimd", acc[:, t, :], acc[:, t, :], rowt[:, g, :], ALU.mult, reads=ak2 + [f"row{g}"], writes=ak2)
            P.stt("vector", acc[:, t, :], x[:], ALPHA, acc[:, t, :], ALU.mult, ALU.add, reads=[xk] + ak2, writes=ak2)
            st, _ = str_.next(); mv, _ = mvr.next(); rs, _ = rsr.next()
            emit_layernorm(P, acc[:, t, :], ak2[0], x[:], xk, st, mv, rs, f"lnB{t % 2}")
            P.tt("gpsimd", acc[:, t, :], x[:], lnt[:, 0, :], ALU.mult, reads=[xk, "ln0"], writes=ak2)
            P.tt("gpsimd", x[:], acc[:, t, :], lnt[:, 1, :], ALU.add, reads=ak2 + ["ln1"], writes=[xk])
            P.dma("sync", x2o[t], x[:], reads=[xk], writes=[xk], final=True)
        P.emit()
    return nc


def run_k7(h2, h2c, aff, affc, thr, thrc, x1, c1, mod_l, ln_w, ln_b, wg, wu, wd, n_exp=N_EXP):
    nc = build_k7(n_exp)
    B, n, D = x1.shape
    nctx = c1.shape[1]
    wgl = np.ascontiguousarray(wg[:n_exp].reshape(n_exp, 8, 128, D_FF).transpose(0, 2, 1, 3))
    wul = np.ascontiguousarray(wu[:n_exp].reshape(n_exp, 8, 128, D_FF).transpose(0, 2, 1, 3))
    wdl = np.ascontiguousarray(wd[:n_exp].reshape(n_exp, NFC, 128, D).transpose(0, 2, 1, 3))
    lnr = np.ascontiguousarray(np.stack([ln_w, ln_b]))
    in_maps = []
    for i in range(NCORES):
        b = i // 4
        h2s = tok_shard(h2, h2c, i).reshape(K7_TOK, D)
        h2T = np.ascontiguousarray(h2s.T.reshape(8, 128, K7_TOK).transpose(1, 0, 2))
        rows = np.ascontiguousarray(np.stack([mod_l[b, 5120:6144], mod_l[2, 5120:6144]]))
        in_maps.append({"h2T": h2T, "aff": tok_shard(aff, affc, i), "thr": np.ascontiguousarray(np.stack([thr[b], thrc[b]])),
                        "i_x1": tok_shard(x1, c1, i), "rows": rows, "lnr": lnr, "wg": wgl, "wu": wul, "wd": wdl})
    res = _run(nc, in_maps)
    return tok_unshard([r["o_x2"] for r in res.results], B, n, nctx)


RMS_EPS = 1e-6
NKEY = 8192 + 256
NKT = NKEY // 128
QCOLS = K1_TILES * 512


def build_k3():
    nc = bass.Bass("TRN2", target_bir_lowering=False)
    T = K1_TILES
    qd = nc.dram_tensor("qT", [128, QCOLS], F32, kind="ExternalInput").ap()
    kd = nc.dram_tensor("kT", [128, NKEY], F32, kind="ExternalInput").ap()
    vd = nc.dram_tensor("v", [128, NKT, 2, 64], F32, kind="ExternalInput").ap()
    cqd = nc.dram_tensor("cosq", [128, 16 * 512], F32, kind="ExternalInput").ap()
    sqd = nc.dram_tensor("sinq", [128, 16 * 512], F32, kind="ExternalInput").ap()
    ckd = nc.dram_tensor("cosk", [128, 8192], F32, kind="ExternalInput").ap()
    skd = nc.dram_tensor("sink", [128, 8192], F32, kind="ExternalInput").ap()
    cst = nc.dram_tensor("cst", [128, 258], F32, kind="ExternalInput").ap()
    yo = nc.dram_tensor("o_ya", [T, 128, 512], F32, kind="ExternalOutput").ap()
    with ExitStack() as es:
        P = Prog(nc, es)
        sb, ps, ring = mk_alloc(nc, es)
        cs = sb("cs", [128, 258])
        P.dma("sync", cs[:], cst, writes=["cs"])
        Rm, onesb, qw2, kw2 = cs[:, 0:128], cs[:, 128:256], cs[:, 256:257], cs[:, 257:258]
        bq = sb("bq", [128, 2])
        P.memset("vector", bq[:, 0:1], 64.0 * RMS_EPS, writes=["bq0"])
        P.memset("vector", bq[:, 1:2], RMS_EPS, writes=["bq1"])
        qr = sb("qr", [128, QCOLS], BF16)
        kr = sb("kr", [128, NKEY], BF16)
        vst = ring("vst", 2, [128, 2, 64])
        vaug = sb("vaug", [128, NKT, 2, 65], BF16)
        P.memset("vector", vaug[:], 1.0, writes=["vaug_init"])
        for kt in range(NKT):
            v, vk = vst.next()
            P.dma("sync", v[:], vd[:, kt], writes=[vk])
            P.copy("vector", vaug[:, kt, :, 0:64], v[:], reads=[vk, "vaug_init"], writes=[f"vaug{kt}"])
        xr = ring("px", 2, [128, 512]); sqr = ring("psq", 2, [128, 512]); sdr = ring("psd", 2, [128, 512])
        xnr = ring("pxn", 2, [128, 512]); cr = ring("pc", 2, [128, 512]); sr = ring("psn", 2, [128, 512])
        t1r = ring("pt1", 2, [128, 512]); t2r = ring("pt2", 2, [128, 512])
        bank = [(ps(f"mb{i}", [128, 512], F32), f"mb{i}") for i in range(8)]
        pssr = Ring(bank[0:1])
        prot = Ring(bank[1:2])

        def prep(src, dst, dkey, ncols, nrope, w2, bcol, scale, cosd, sind):
            for c0 in range(0, ncols, 512):
                n = min(512, ncols - c0)
                x, xk = xr.next(); P.dma("sync", x[:, 0:n], src[:, c0:c0 + n], writes=[xk])
                sq, sqk = sqr.next(); P.tt("vector", sq[:, 0:n], x[:, 0:n], x[:, 0:n], ALU.mult, reads=[xk], writes=[sqk])
                pss, pssk = pssr.next()
                P.mm(pss[:, 0:n], onesb, sq[:, 0:n], True, True, reads=[sqk, "cs"], writes=[pssk])
                sd, sdk = sdr.next()
                P.act(sd[:, 0:n], pss[:, 0:n], AF.Sqrt, reads=[pssk, f"bq{bcol}"], writes=[sdk], bias=bq[:, bcol:bcol + 1], scale=scale)
                P.gen("vector", lambda e, sd=sd, n=n: e.reciprocal(out=sd[:, 0:n], in_=sd[:, 0:n]), reads=[sdk], writes=[sdk])
                xn, xnk = xnr.next()
                P.stt("vector", xn[:, 0:n], x[:, 0:n], w2, sd[:, 0:n], ALU.mult, ALU.mult, reads=[xk, sdk, "cs"], writes=[xnk])
                dk = f"{dkey}{c0 // 512}"
                if c0 < nrope:
                    pr, prk = prot.next()
                    P.mm(pr[:, 0:n], Rm, xn[:, 0:n], True, True, reads=[xnk, "cs"], writes=[prk])
                    c, ck = cr.next(); P.dma("sync", c[:, 0:n], cosd[:, c0:c0 + n], writes=[ck])
                    s, sk = sr.next(); P.dma("sync", s[:, 0:n], sind[:, c0:c0 + n], writes=[sk])
                    t1, t1k = t1r.next(); P.tt("vector", t1[:, 0:n], xn[:, 0:n], c[:, 0:n], ALU.mult, reads=[xnk, ck], writes=[t1k])
                    t2, t2k = t2r.next(); P.tt("vector", t2[:, 0:n], pr[:, 0:n], s[:, 0:n], ALU.mult, reads=[prk, sk], writes=[t2k])
                    P.tt("vector", dst[:, c0:c0 + n], t1[:, 0:n], t2[:, 0:n], ALU.add, reads=[t1k, t2k], writes=[dk])
                else:
                    P.copy("vector", dst[:, c0:c0 + n], xn[:, 0:n], reads=[xnk], writes=[dk])

        prep(kd, kr, "kr", NKEY, 8192, kw2, 1, 1.0 / 64.0, ckd, skd)
        prep(qd, qr, "qr", QCOLS, 16 * 512, qw2, 0, 1.0, cqd, sqd)
        LA = 1
        pstR = Ring(bank[0:4])
        accbank = {(par, g): bank[4 + par * 2 + g] for par in range(2) for g in range(2)}
        ptr = ring("PT", 6, [128, 512], BF16)
        yr = ring("yo", 2, [128, 512])
        rcr = ring("rc", 2, [128, 4])
        its = []
        for qt in range(T):
            kts = list(range(NKT)) if qt < 16 else [64, 65]
            for ii, kt in enumerate(kts):
                its.append((qt, ii, kt, len(kts)))
        pend = {}
        ycur = {}
        for idx in range(len(its) + LA):
            if idx < len(its):
                qt, ii, kt, nk = its[idx]
                pair = []
                for g in range(2):
                    pst, pstk = pstR.next()
                    P.mm(pst[:], kr[g * 64:(g + 1) * 64, kt * 128:(kt + 1) * 128], qr[g * 64:(g + 1) * 64, qt * 512:(qt + 1) * 512],
                         True, True, reads=[f"kr{kt // 4}", f"qr{qt}"], writes=[pstk])
                    pair.append((pst, pstk))
                pend[idx] = pair
            j0 = idx - LA
            if j0 < 0:
                continue
            qt, ii, kt, nk = its[j0]
            pair = pend.pop(j0)
            PTs = []
            for g in range(2):
                pst, pstk = pair[g]
                PT, PTk = ptr.next()
                P.act(PT[:], pst[:], AF.Exp, reads=[pstk], writes=[PTk])
                PTs.append((PT, PTk))
            for g in range(2):
                PT, PTk = PTs[g]
                pb, pbk = accbank[(qt % 2, g)]
                for j in range(4):
                    P.op("tensor", lambda e, pb=pb, PT=PT, j=j, kt=kt, g=g, first=(ii == 0 and j == 0), last=(ii == nk - 1): e.matmul(
                        pb[:, j * 128:j * 128 + 65], PT[:, j * 128:(j + 1) * 128], vaug[:, kt, g, :], start=first, stop=last,
                        skip_group_check=True), reads=[PTk, f"vaug{kt}"], writes=[pbk])
            if ii == nk - 1:
                ycur[qt] = yr.next()
                y, yk = ycur[qt]
                for g in range(2):
                    pb, pbk = accbank[(qt % 2, g)]
                    rc, rck = rcr.next()
                    for j in range(4):
                        po = pb[:, j * 128:j * 128 + 65]
                        P.gen("vector", lambda e, rc=rc, po=po, j=j: e.reciprocal(out=rc[:, j:j + 1], in_=po[:, 64:65]), reads=[pbk], writes=[rck + str(j)])
                        c0 = (g * 4 + j) * 64
                        P.ts("vector", y[:, c0:c0 + 64], po[:, 0:64], rc[:, j:j + 1], None, ALU.mult, reads=[pbk, rck + str(j)], writes=[yk + f"{g}{j}"])
                P.dma("sync", yo[qt], y[:], reads=[yk + f"{g_}{j}" for g_ in range(2) for j in range(4)], writes=[yk + "d"], final=True)
        P.emit()
    return nc


def rope_tables(n_lat, grid_w=64, theta=10000.0, hd=64):
    nf = hd // 4
    t = np.arange(n_lat)
    row = (t // grid_w).astype(np.float32)
    col = (t % grid_w).astype(np.float32)
    inv = (theta ** (-np.arange(nf, dtype=np.float32) / nf)).astype(np.float32)
    ar = row[:, None] * inv
    ac = col[:, None] * inv
    ang = np.concatenate([ar, ar, ac, ac], axis=-1)
    return np.cos(ang).astype(np.float32), np.sin(ang).astype(np.float32)


def rope_rot_matrix():
    R = np.zeros((64, 64), np.float32)
    for a in range(2):
        for f in range(16):
            R[a * 32 + 16 + f, a * 32 + f] = -1.0
            R[a * 32 + f, a * 32 + 16 + f] = 1.0
    return R


def run_k3(P_lat, P_ctx, qw, kw):
    nc = build_k3()
    B, n, _ = P_lat.shape
    nctx = P_ctx.shape[1]
    aq_l, ak_l, av_l = P_lat[..., 1040:1552], P_lat[..., 1552:1680], P_lat[..., 1680:1808]
    aq_c, ak_c, av_c = P_ctx[..., 1040:1552], P_ctx[..., 1552:1680], P_ctx[..., 1680:1808]
    cos, sin = rope_tables(n)
    R = rope_rot_matrix()
    Rm = np.zeros((128, 128), np.float32); Rm[:64, :64] = R; Rm[64:, 64:] = R
    ob = np.zeros((128, 128), np.float32); ob[:64, :64] = 1; ob[64:, 64:] = 1
    cst = np.concatenate([Rm, ob, np.tile(qw, 2)[:, None], np.tile(kw, 2)[:, None]], 1).astype(np.float32)
    cosk = np.ascontiguousarray(np.tile(cos.T, (2, 1))); sink = np.ascontiguousarray(np.tile(sin.T, (2, 1)))
    seg = n // 4
    in_maps = []
    for i in range(NCORES):
        b, s = i // 4, i % 4
        q = tok_shard(aq_l, aq_c, i).reshape(K1_TILES, 128, 2, 4, 64)
        qT = np.ascontiguousarray(q.transpose(2, 4, 0, 3, 1)).reshape(128, QCOLS)
        k = np.concatenate([ak_l[b], ak_c[b]], 0).reshape(NKEY, 2, 64)
        kT = np.ascontiguousarray(k.transpose(1, 2, 0)).reshape(128, NKEY)
        v = np.concatenate([av_l[b], av_c[b]], 0).reshape(NKT, 128, 2, 64)
        vv = np.ascontiguousarray(v.transpose(1, 0, 2, 3))
        cq = cos[s * seg:(s + 1) * seg].reshape(16, 128, 64)
        sq = sin[s * seg:(s + 1) * seg].reshape(16, 128, 64)
        cq = np.broadcast_to(cq.transpose(2, 0, 1)[None, :, :, None, :], (2, 64, 16, 4, 128)).reshape(128, 16 * 512)
        sq = np.broadcast_to(sq.transpose(2, 0, 1)[None, :, :, None, :], (2, 64, 16, 4, 128)).reshape(128, 16 * 512)
        in_maps.append({"qT": qT, "kT": kT, "v": vv, "cosq": np.ascontiguousarray(cq), "sinq": np.ascontiguousarray(sq),
                        "cosk": cosk, "sink": sink, "cst": cst})
    res = _run(nc, in_maps)
    return tok_unshard([r["o_ya"] for r in res.results], B, n, nctx)


NCH = NKT
NSEQ = NKEY
MASK_NEG = -30000.0


def build_k2():
    nc = bass.Bass("TRN2", target_bir_lowering=False)
    qpd = nc.dram_tensor("qpT", [64, NSEQ], F32, kind="ExternalInput").ap()
    kpd = nc.dram_tensor("kpT", [64, NSEQ], F32, kind="ExternalInput").ap()
    vd = nc.dram_tensor("v", [128, NCH, 64], F32, kind="ExternalInput").ap()
    od = nc.dram_tensor("og", [128, NCH, 64], F32, kind="ExternalInput").ap()
    gd = nc.dram_tensor("g4", [128, NCH, 4], F32, kind="ExternalInput").ap()
    gbd = nc.dram_tensor("gb", [NCH * 4], F32, kind="ExternalInput").ap()
    cwd = nc.dram_tensor("cw", [64, 8], F32, kind="ExternalInput").ap()
    nwd = nc.dram_tensor("nw", [64], F32, kind="ExternalInput").ap()
    cstd = nc.dram_tensor("cst", [128, 6, 128], F32, kind="ExternalInput").ap()
    yo = nc.dram_tensor("o_ym", [128, NCH, 64], F32, kind="ExternalOutput").ap()
    with ExitStack() as es:
        P = Prog(nc, es)
        sb, ps, ring = mk_alloc(nc, es)
        cst = sb("cst_s", [128, 6, 128])
        P.dma("sync", cst[:], cstd, writes=["cst"])
        Lm = [cst[:, 0, :], cst[:, 1, :]]
        ones, ident = cst[:, 2, :], cst[:, 3, :]
        mneg = [cst[:, 4, :], cst[:, 5, :]]
        idb = sb("idb", [128, 128], BF16)
        P.copy("vector", idb[:], ident, reads=["cst"], writes=["idb"])
        one1 = sb("one1", [128, 1])
        P.memset("vector", one1[:], 1.0, writes=["one1"])
        cw = sb("cw_s", [64, 8])
        P.dma("sync", cw[:], cwd, writes=["cw"])
        banks = [ps(f"bank{i}", [128, 512], F32) for i in range(8)]
        xin = sb("xin", [64, NSEQ]); cacc = sb("cacc", [64, NSEQ])
        qT = sb("qT_s", [64, NSEQ], BF16); kT = sb("kT_s", [64, NSEQ], BF16)
        segs = [(0, 256), (256, NSEQ)]
        for wi, (src, dst, post) in enumerate(((qpd, qT, 0.125), (kpd, kT, 1.0))):
            o = wi * 4
            half = NSEQ // 2
            P.dma("sync", xin[:, 0:half], src[:, 0:half], writes=["xin_a"])
            P.dma("gpsimd", xin[:, half:], src[:, half:], writes=["xin_b"])
            P.ts("vector", cacc[:], xin[:], cw[:, o + 1:o + 2], cw[:, o + 3:o + 4], ALU.mult, ALU.add,
                 reads=["xin_a", "xin_b", "cw"], writes=["cacc"])
            for (a, b) in segs:
                P.stt("vector", cacc[:, a + 1:b], xin[:, a:b - 1], cw[:, o:o + 1], cacc[:, a + 1:b], ALU.mult, ALU.add,
                      reads=["xin_a", "xin_b", "cw", "cacc"], writes=["cacc"])
                P.stt("vector", cacc[:, a:b - 1], xin[:, a + 1:b], cw[:, o + 2:o + 3], cacc[:, a:b - 1], ALU.mult, ALU.add,
                      reads=["xin_a", "xin_b", "cw", "cacc"], writes=["cacc"])
            P.act(cacc[:], cacc[:], AF.Silu, reads=["cacc"], writes=["cacc"])
            P.act(dst[:], cacc[:], AF.Copy, reads=["cacc"], writes=[f"T{wi}"], scale=post)
            P.memset("vector", xin[:, 0:1], 0.0, writes=["xin_a", "xin_b"]) if wi == 0 else None
        ktok = sb("ktok", [128, NCH, 64], BF16)
        for c0 in range(0, NCH, 8):
            n = min(8, NCH - c0)
            pb = banks[7]
            for c in range(c0, c0 + n):
                pt = pb[:, :].bitcast(BF16)[:, (c - c0) * 64:(c - c0 + 1) * 64]
                P.tr(pt, kT[:, c * 128:(c + 1) * 128], idb[0:64, 0:64], reads=["T1", "idb"], writes=["bank7"])
            P.copy("vector", ktok[:, c0:c0 + n, :], pb[:, :].bitcast(BF16)[:, 0:n * 64].rearrange("p (c d) -> p c d", d=64),
                   reads=["bank7"], writes=["ktok"])
        vf = sb("vf", [128, NCH, 65]); vb = sb("vb", [128, NCH, 65], BF16)
        P.memset("vector", vf[:], 1.0, writes=["vf"])
        vtmp = sb("vtmp", [128, NCH, 64])
        P.dma("sync", vtmp[:], vd, writes=["vtmp"])
        P.copy("vector", vf[:, :, 0:64], vtmp[:], reads=["vtmp", "vf"], writes=["vf"])
        P.copy("scalar", vb[:], vf[:], reads=["vf"], writes=["vb"])
        G = sb("G", [128, NCH, 4]); GB = sb("GB", [128, NCH, 4])
        P.dma("sync", G[:], gd, writes=["G"])
        P.dma("sync", GB[:].rearrange("p c g -> p (c g)"), bcast_rows(gbd, 128), writes=["GB"])
        P.tt("vector", G[:], G[:], GB[:], ALU.add, reads=["G", "GB"], writes=["G"])
        LF = sb("LF", [128, 2, NCH]); LI = sb("LI", [128, 2, NCH]); TA = sb("TA", [128, 2, NCH]); TB = sb("TB", [128, 2, NCH])
        for dd in range(2):
            P.copy("vector", LI[:, dd, :], G[:, :, 2 * dd], reads=["G"], writes=[f"LI{dd}"])
            P.copy("vector", TA[:, dd, :], G[:, :, 2 * dd + 1], reads=["G"], writes=["TA"])
        P.act(TB[:], TA[:], AF.Abs, reads=["TA"], writes=["TB"])
        P.act(TB[:], TB[:], AF.Exp, reads=["TB"], writes=["TB"], scale=-1.0)
        P.act(TB[:], TB[:], AF.Ln, reads=["TB", "one1"], writes=["TB"], bias=one1[:, 0:1], scale=1.0)
        P.ts("vector", TA[:], TA[:], 0.0, None, ALU.min, reads=["TA"], writes=["TA"])
        P.tt("vector", LF[:], TA[:], TB[:], ALU.subtract, reads=["TA", "TB"], writes=["LF"])
        BC = sb("BC", [128, 2, NCH]); TOT = sb("TOT", [128, 2, NCH]); AA = sb("AA", [128, 2, NCH])
        BD = sb("BD", [128, 2, NCH]); WW = sb("WW", [128, 2, NCH]); DEC = sb("DEC", [128, 2, NCH])
        b6 = banks[6]
        for dd in range(2):
            P.mm(b6[:, dd * NCH:(dd + 1) * NCH], Lm[dd], LF[:, dd, :], True, True, reads=["LF", "cst"], writes=["bank6"])
        P.mm(b6[:, 2 * NCH:4 * NCH], ones, LF[:].rearrange("p a c -> p (a c)"), True, True, reads=["LF", "cst"], writes=["bank6"])
        P.copy("vector", BC[:].rearrange("p a c -> p (a c)"), b6[:, 0:2 * NCH], reads=["bank6"], writes=["BC"])
        P.copy("vector", TOT[:].rearrange("p a c -> p (a c)"), b6[:, 2 * NCH:4 * NCH], reads=["bank6"], writes=["TOT"])
        P.act(AA[:], BC[:], AF.Exp, reads=["BC"], writes=["AA"])
        P.tt("vector", BD[:], LI[:], BC[:], ALU.subtract, reads=["LI0", "LI1", "BC"], writes=["BD"])
        P.tt("vector", WW[:], TOT[:], BD[:], ALU.add, reads=["TOT", "BD"], writes=["WW"])
        P.act(WW[:], WW[:], AF.Exp, reads=["WW"], writes=["WW"])
        P.act(DEC[:], TOT[:], AF.Exp, reads=["TOT"], writes=["DEC"])
        S = [sb(f"S{dd}", [64, 65]) for dd in range(2)]
        Sb = [sb(f"Sb{dd}", [64, 65], BF16) for dd in range(2)]
        for dd in range(2):
            P.memset("vector", S[dd][:], 0.0, writes=[f"S{dd}"])
            P.memset("vector", Sb[dd][:], 0.0, writes=[f"Sb{dd}"])
        hb = [sb(f"hb{dd}", [128, NCH, 64]) for dd in range(2)]
        lfr = ring("lfrep", 4, [128, 128]); dtr = ring("Dt", 4, [128, 128]); ptr = ring("PTm", 4, [128, 128], BF16)
        tmr = ring("tmpi", 2, [128, 65]); ttr = ring("tot", 2, [128, 65]); dnr = ring("den", 2, [128, 4])
        wvr = ring("wv", 2, [128, 65], BF16)
        pD = Ring([(banks[0], "bank0"), (banks[1], "bank1")])
        pST = Ring([(banks[2], "bank2"), (banks[3], "bank3")])
        pOI = Ring([(banks[4], "bank4"), (banks[5], "bank5")])
        order = [list(range(NCH)), [1, 0] + list(range(NCH - 1, 1, -1))]
        def stage_a(step, dd):
            c = order[dd][step]
            cs_ = slice(c * 128, (c + 1) * 128)
            lf, lfk = lfr.next()
            P.act(lf[:], ones, AF.Copy, reads=["cst", "LF"], writes=[lfk], scale=LF[:, dd, c:c + 1])
            pd, pdk = pD.next()
            P.mm(pd[:, 0:128], lf[:], Lm[dd], True, False, reads=[lfk, "cst"], writes=[pdk])
            P.mm(pd[:, 0:128], ident, mneg[dd], False, True, reads=["cst"], writes=[pdk])
            dt_, dtk = dtr.next()
            P.act(dt_[:], pd[:, 0:128], AF.Exp, reads=[pdk, "BD"], writes=[dtk], bias=BD[:, dd, c:c + 1], scale=1.0)
            pst, pstk = pST.next()
            P.mm(pst[:, 0:128], kT[:, cs_], qT[:, cs_], True, True, reads=["T0", "T1"], writes=[pstk])
            PT, PTk = ptr.next()
            P.tt("vector", PT[:], pst[:, 0:128], dt_[:], ALU.mult, reads=[pstk, dtk], writes=[PTk])
            return PT, PTk

        def stage_b(step, dd, PT, PTk):
            c = order[dd][step]
            cs_ = slice(c * 128, (c + 1) * 128)
            poi, poik = pOI.next()
            P.mm(poi[:, 0:65], PT[:], vb[:, c, :], True, True, reads=[PTk, "vb"], writes=[poik + "o"])
            P.mm(poi[:, 128:193], qT[:, cs_], Sb[dd][:], True, True, reads=["T0", f"Sb{dd}"], writes=[poik + "i"])
            tm, tmk = tmr.next()
            P.act(tm[:], poi[:, 128:193], AF.Copy, reads=[poik + "i", "AA"], writes=[tmk], scale=AA[:, dd, c:c + 1])
            tt_, ttk = ttr.next()
            P.tt("vector", tt_[:], poi[:, 0:65], tm[:], ALU.add, reads=[poik + "o", tmk], writes=[ttk])
            dn, dnk = dnr.next()
            P.ts("vector", dn[:, 0:1], tt_[:, 64:65], -1.0, None, ALU.mult, reads=[ttk], writes=[dnk])
            P.stt("vector", dn[:, 1:2], dn[:, 0:1], 1.0, tt_[:, 64:65], ALU.max, ALU.max, reads=[dnk, ttk], writes=[dnk])
            P.gen("vector", lambda e, dn=dn: e.reciprocal(out=dn[:, 2:3], in_=dn[:, 1:2]), reads=[dnk], writes=[dnk])
            P.act(hb[dd][:, c, :], tt_[:, 0:64], AF.Copy, reads=[ttk, dnk], writes=[f"hb{dd}_{c}"], scale=dn[:, 2:3])
            wv, wvk = wvr.next()
            P.act(wv[:], vf[:, c, :], AF.Copy, reads=["vf", "WW"], writes=[wvk], scale=WW[:, dd, c:c + 1])
            p7 = banks[7]
            P.mm(p7[0:64, 256 + dd * 128:256 + dd * 128 + 65], ktok[:, c, :], wv[:], True, True, reads=["ktok", wvk], writes=[f"b7s{dd}"])
            P.stt("vector", S[dd][:], S[dd][:], DEC[0:64, dd, c:c + 1], p7[0:64, 256 + dd * 128:256 + dd * 128 + 65], ALU.mult, ALU.add,
                  reads=[f"S{dd}", "DEC", f"b7s{dd}"], writes=[f"S{dd}"])
            P.copy("gpsimd", Sb[dd][:], S[dd][:], reads=[f"S{dd}"], writes=[f"Sb{dd}"])

        pts = {}
        for step in range(NCH + 1):
            if step < NCH:
                for dd in range(2):
                    pts[(step, dd)] = stage_a(step, dd)
            if step >= 1:
                for dd in range(2):
                    stage_b(step - 1, dd, *pts.pop((step - 1, dd)))

        hk = [f"hb{dd}_{c}" for dd in range(2) for c in range(NCH)]
        P.tt("vector", hb[0][:], hb[0][:], hb[1][:], ALU.add, reads=hk, writes=["hsum"])
        sq = hb[1]
        P.tt("vector", sq[:], hb[0][:], hb[0][:], ALU.mult, reads=["hsum"], writes=["hsq"])
        ssum = sb("ssum", [128, NCH])
        P.gen("vector", lambda e: e.reduce_sum(out=ssum[:], in_=sq[:], axis=AX.X), reads=["hsq"], writes=["ssum"])
        P.ts("vector", ssum[:], ssum[:], 1.0 / 64.0, RMS_EPS, ALU.mult, ALU.add, reads=["ssum"], writes=["ssum"])
        P.act(ssum[:], ssum[:], AF.Sqrt, reads=["ssum"], writes=["ssum"])
        P.gen("vector", lambda e: e.reciprocal(out=ssum[:], in_=ssum[:]), reads=["ssum"], writes=["ssum"])
        nw = sb("nw_s", [128, 64])
        P.dma("sync", nw[:], bcast_rows(nwd, 128), writes=["nw"])
        og = vtmp
        P.dma("sync", og[:], od, reads=["vf"], writes=["og"])
        P.act(og[:], og[:], AF.Sigmoid, reads=["og"], writes=["og"])
        for c in range(NCH):
            P.stt("vector", hb[0][:, c, :], hb[0][:, c, :], ssum[:, c:c + 1], nw[:], ALU.mult, ALU.mult,
                  reads=["hsum", "ssum", "nw"], writes=[f"hn{c}"])
        P.tt("vector", hb[0][:], hb[0][:], og[:], ALU.mult, reads=[f"hn{c}" for c in range(NCH)] + ["og"], writes=["ym"])
        P.dma("sync", yo, hb[0][:], reads=["ym"], final=True)
        P.emit()
    return nc


def run_k2(P_lat, P_ctx, conv_w, conv_b, gate_b, norm_w):
    nc = build_k2()
    B, n, _ = P_lat.shape
    nctx = P_ctx.shape[1]
    s_idx, j_idx = np.meshgrid(np.arange(128), np.arange(128), indexing="ij")
    Lf = (s_idx <= j_idx).astype(np.float32); Lb = (s_idx >= j_idx).astype(np.float32)
    cst = np.stack([Lf, Lb, np.ones((128, 128), np.float32), np.eye(128, dtype=np.float32),
                    np.where(s_idx <= j_idx, 0.0, MASK_NEG).astype(np.float32),
                    np.where(s_idx >= j_idx, 0.0, MASK_NEG).astype(np.float32)], 1)
    in_maps = []
    for i in range(NCORES):
        b, h = i // 4, i % 4
        seq = np.concatenate([P_ctx[b], P_lat[b]], 0)
        qs, ks = slice(h * 64, (h + 1) * 64), slice(256 + h * 64, 256 + (h + 1) * 64)
        tm = lambda a: np.ascontiguousarray(a.reshape(NCH, 128, -1).transpose(1, 0, 2))
        gcols = [1024 + 0 * 8 + 0 * 4 + h, 1024 + 0 * 8 + 1 * 4 + h, 1024 + 1 * 8 + 0 * 4 + h, 1024 + 1 * 8 + 1 * 4 + h]
        gb = np.array([gate_b[0, 0, h], gate_b[0, 1, h], gate_b[1, 0, h], gate_b[1, 1, h]], np.float32)
        cw = np.concatenate([conv_w[:, qs].T, conv_b[qs][:, None], conv_w[:, ks].T, conv_b[ks][:, None]], 1).astype(np.float32)
        in_maps.append({"qpT": np.ascontiguousarray(seq[:, qs].T), "kpT": np.ascontiguousarray(seq[:, ks].T),
                        "v": tm(seq[:, 512 + h * 64:512 + (h + 1) * 64]), "og": tm(seq[:, 768 + h * 64:768 + (h + 1) * 64]),
                        "g4": tm(seq[:, gcols]), "gb": np.ascontiguousarray(np.tile(gb, NCH)), "cw": np.ascontiguousarray(cw),
                        "nw": np.ascontiguousarray(norm_w[h * 64:(h + 1) * 64]), "cst": np.ascontiguousarray(cst)})
    res = _run(nc, in_maps)
    ym_l = np.zeros((B, n, 256), np.float32); ym_c = np.zeros((B, nctx, 256), np.float32)
    for i in range(NCORES):
        b, h = i // 4, i % 4
        y = res.results[i]["o_ym"].transpose(1, 0, 2).reshape(NSEQ, 64)
        ym_c[b, :, h * 64:(h + 1) * 64] = y[:nctx]
        ym_l[b, :, h * 64:(h + 1) * 64] = y[nctx:]
    return ym_l, ym_c


HCH = 32
TWO_PI = 2.0 * np.pi
RND_MAGIC = 12582912.0


def fft_tables(N1):
    N2 = 128
    N = N1 * N2
    ar = np.arange
    c, s = np.cos, np.sin
    th = TWO_PI * ar(N1)[:, None] * ar(N1)[None] / N1
    F1c = np.concatenate([c(th), -s(th)], 1)
    th = TWO_PI * ar(N2)[:, None] * ar(N1)[None] / N
    twRR = np.concatenate([c(th), c(th)], 1); twII = np.concatenate([-s(th), -s(th)], 1)
    th = TWO_PI * ar(N2)[:, None] * ar(N2)[None] / N2
    F2re, F2im, nF2im = c(th), -s(th), s(th)
    G2c = np.concatenate([c(th), s(th)], 1); G2s = np.concatenate([-s(th), c(th)], 1)
    th = TWO_PI * ar(N1)[:, None] * ar(N2)[None] / N
    twcRR = np.concatenate([c(th), c(th)], 1); twcII = np.concatenate([s(th), s(th)], 1)
    th = TWO_PI * ar(N1)[:, None] * ar(N1 // 2)[None] / N1
    G1re, nG1im = c(th) / N, -s(th) / N
    f = lambda a: np.ascontiguousarray(a.astype(np.float32))
    return dict(F1c=f(F1c), twRR=f(twRR), twII=f(twII), F2re=f(F2re), F2im=f(F2im), nF2im=f(nF2im), G2c=f(G2c), G2s=f(G2s),
                twcRR=f(twcRR), twcII=f(twcII), G1re=f(G1re), nG1im=f(nG1im))


TAB_ORDER = ["F1c", "twRR", "twII", "F2re", "F2im", "nF2im", "G2c", "G2s", "twcRR", "twcII", "G1re", "nG1im"]


def hyena_consts(n):
    N = 2 * n
    tau = np.arange(N)
    pos = np.where(tau < n, tau, N - tau).astype(np.float32)
    t = (pos / np.float32(n)).astype(np.float32)
    bands = np.arange(1, 17, dtype=np.float32)
    ang = (np.float32(TWO_PI) * t[:, None] * bands).astype(np.float32)
    feats = np.concatenate([t[:, None], np.cos(ang), np.sin(ang)], -1).astype(np.float32)
    lt = abs(np.log(1e-2))
    deltas = np.linspace(lt / 1.5, lt / 0.3, 256, dtype=np.float32)
    win = (np.exp(-t[:, None] * deltas) + np.float32(0.05)).astype(np.float32)
    win[n] = 0.0
    return np.ascontiguousarray(feats.T), np.ascontiguousarray(win.T)


def interleave(gens, width):
    it = iter(gens)
    active = []
    while True:
        while len(active) < width:
            g = next(it, None)
            if g is None:
                break
            active.append(g)
        if not active:
            return
        for g in list(active):
            try:
                next(g)
            except StopIteration:
                active.remove(g)
        yield


def build_k4(sizes):
    nc = bass.Bass("TRN2", target_bir_lowering=False)
    B = 2
    dr = {}
    for si, n in enumerate(sizes):
        N1 = 2 * n // 128
        dr[si] = dict(
            u=nc.dram_tensor(f"u{si}", [3, HCH, B, n + 2], F32, kind="ExternalInput").ap(),
            feats=nc.dram_tensor(f"feats{si}", [33, 2 * n], F32, kind="ExternalInput").ap(),
            win=nc.dram_tensor(f"win{si}", [64, 2 * n], F32, kind="ExternalInput").ap(),
            taps=nc.dram_tensor(f"taps{si}", [64, 2 * n], F32, kind="ExternalOutput").ap(),
            out=nc.dram_tensor(f"o_yh{si}", [HCH, B, n], F32, kind="ExternalOutput").ap(),
            tabs={k: nc.dram_tensor(f"t{si}_{k}", list(v.shape), F32, kind="ExternalInput").ap()
                  for k, v in fft_tables(N1).items()})
    mlpd = nc.dram_tensor("mlp", [64, 64 + 64 + 128 + 4], F32, kind="ExternalInput").ap()
    cwd = nc.dram_tensor("cwv", [3 * HCH * 4], F32, kind="ExternalInput").ap()
    skd = nc.dram_tensor("skv", [2 * HCH], F32, kind="ExternalInput").ap()
    cstd = nc.dram_tensor("cst", [128, 256], F32, kind="ExternalInput").ap()
    with ExitStack() as es:
        P = Prog(nc, es)
        sb, ps, ring = mk_alloc(nc, es)
        banks = [(ps(f"bank{i}", [128, 512], F32), f"bank{i}") for i in range(8)]
        cst = sb("cst_s", [128, 256]); P.dma("sync", cst[:], cstd, writes=["cst"])
        ident, ones = cst[:, 0:128], cst[:, 128:256]
        mlp = sb("mlp_s", [64, 260]); P.dma("sync", mlp[:], mlpd, writes=["mlp"])
        w1, w2 = mlp[0:33, 0:64], mlp[:, 64:128]
        w3 = [mlp[:, 128:192], mlp[:, 192:256]]
        b1, f0, b2, f1 = (mlp[:, 256 + j:257 + j] for j in range(4))
        cwb = sb("cwb", [128, 3 * HCH * 4]); P.dma("sync", cwb[:], bcast_rows(cwd, 128), writes=["cwb"])
        skb = sb("skb", [128, 2 * HCH]); P.dma("sync", skb[:], bcast_rows(skd, 128), writes=["skb"])
        a1r = ring("a1", 4, [64, 512]); rrr = ring("rr", 4, [64, 512]); hhr = ring("hh", 4, [64, 512])
        ftr = ring("ft", 4, [33, 512]); wnr = ring("wn", 4, [64, 512]); tpr = ring("tp", 4, [64, 512])
        As_r = ring("As", 4, [128, 256]); t1r = ring("t1", 4, [128, 256]); t2r = ring("t2", 4, [128, 256])
        Br = ring("Bc", 4, [128, 256]); Yr = ring("Yc", 4, [128, 256], BF16); Dr = ring("Dc", 4, [128, 256], BF16)
        Brb = ring("Bcb", 4, [128, 256], BF16); zbr = ring("zb", 4, [64, 128], BF16)
        pr4 = [ring(f"pp{j}", 4, [128, 128]) for j in range(4)]
        xir = ring("xi", 4, [64, 3, 130]); cvr = ring("cv", 4, [64, 3, 128]); tgr = ring("tg", 4, [64, 128])
        z1r = ring("z1", 4, [64, 128]); z2r = ring("z2", 4, [64, 128]); tlr = ring("tl", 4, [128, 128])
        bkP = Ring(banks[0:8])
        bkA = bkX = bkC = bkY = bkP

        def sin_layer(psrc, pk, n_, bias, freq, dst, dstk):
            a1, a1k = a1r.next(); rr, rrk = rrr.next()
            P.ts("vector", a1[:, 0:n_], psrc, bias, freq, ALU.add, ALU.mult, reads=[pk, "mlp"], writes=[a1k])
            yield
            P.ts("vector", rr[:, 0:n_], a1[:, 0:n_], 1.0 / TWO_PI, RND_MAGIC, ALU.mult, ALU.add, reads=[a1k], writes=[rrk])
            yield
            P.ts("vector", rr[:, 0:n_], rr[:, 0:n_], RND_MAGIC, -TWO_PI, ALU.subtract, ALU.mult, reads=[rrk], writes=[rrk])
            yield
            P.tt("vector", rr[:, 0:n_], rr[:, 0:n_], a1[:, 0:n_], ALU.add, reads=[rrk, a1k], writes=[rrk])
            yield
            P.ts("vector", rr[:, 0:n_], rr[:, 0:n_], np.pi, -np.pi, ALU.min, ALU.max, reads=[rrk], writes=[rrk])
            yield
            P.act(dst, rr[:, 0:n_], AF.Sin, reads=[rrk], writes=[dstk])
            yield

        def cmul(src, srck, n1p, W, tRR, tII, tk, dst, dstk):
            t1, t1k = t1r.next(); t2, t2k = t2r.next()
            P.tt("vector", t1[0:n1p, 0:2 * W], src, tRR, ALU.mult, reads=[srck, tk], writes=[t1k])
            yield
            P.tt("gpsimd", t2[0:n1p, 0:2 * W], src, tII, ALU.mult, reads=[srck, tk], writes=[t2k])
            yield
            P.tt("vector", dst[0:n1p, 0:W], t1[0:n1p, 0:W], t2[0:n1p, W:2 * W], ALU.subtract, reads=[t1k, t2k], writes=[dstk + "r"])
            yield
            P.tt("gpsimd", dst[0:n1p, W:2 * W], t2[0:n1p, 0:W], t1[0:n1p, W:2 * W], ALU.add, reads=[t1k, t2k], writes=[dstk + "i"])
            yield

        def size_body(si, n):
            N = 2 * n
            N1 = N // 128
            Kd = N1 // 2
            d = dr[si]
            T = {}
            for k in TAB_ORDER:
                shp = list(d["tabs"][k].shape)
                T[k] = sb(f"T{si}_{k}", shp)
                P.dma("sync", T[k][:], d["tabs"][k], writes=[f"tab{si}"] if k == TAB_ORDER[-1] else [f"tab{si}_{k}"])
                yield
            tabk = [f"tab{si}"] + [f"tab{si}_{k}" for k in TAB_ORDER[:-1]]
            Tb = {}
            for k_ in ("F1c", "F2re", "F2im", "nF2im", "G2c", "G2s", "G1re", "nG1im"):
                Tb[k_] = sb(f"Tb{si}_{k_}", list(d["tabs"][k_].shape), BF16)
                P.copy("vector", Tb[k_][:], T[k_][:], reads=tabk, writes=[f"tabb{si}"])
            tabk = tabk + [f"tabb{si}"]
            CH = min(512, n)
            nchunk = N // CH
            l1p = sb(f"l1p{si}", [64, nchunk])
            def mlp_chain(ci):
                c0 = ci * CH
                dirn = 0 if c0 < n else 1
                ft, ftk = ftr.next(); P.dma("sync", ft[:, 0:CH], d["feats"][:, c0:c0 + CH], writes=[ftk])
                wn, wnk = wnr.next(); P.dma("sync", wn[:, 0:CH], d["win"][:, c0:c0 + CH], writes=[wnk])
                bk, bkk = bkA.next()
                P.mm(bk[0:64, 0:CH], w1, ft[:, 0:CH], True, True, reads=[ftk, "mlp"], writes=[bkk])
                yield
                h1, h1k = hhr.next()
                yield from sin_layer(bk[0:64, 0:CH], bkk, CH, b1, f0, h1[:, 0:CH], h1k)
                bk, bkk = bkX.next()
                P.mm(bk[0:64, 0:CH], w2, h1[:, 0:CH], True, True, reads=[h1k, "mlp"], writes=[bkk])
                yield
                h2, h2k = hhr.next()
                yield from sin_layer(bk[0:64, 0:CH], bkk, CH, b2, f1, h2[:, 0:CH], h2k)
                bk, bkk = bkC.next()
                P.mm(bk[0:64, 0:CH], w3[dirn], h2[:, 0:CH], True, True, reads=[h2k, "mlp"], writes=[bkk])
                yield
                tp, tpk = tpr.next()
                P.tt("vector", tp[:, 0:CH], bk[0:64, 0:CH], wn[:, 0:CH], ALU.mult, reads=[bkk, wnk], writes=[tpk])
                yield
                P.gen("vector", lambda e, tp=tp, ci=ci, CH=CH, l1p=l1p: e.reduce_sum(out=l1p[:, ci:ci + 1], in_=tp[:, 0:CH], axis=AX.X,
                                                                         apply_absolute_value=True), reads=[tpk], writes=[f"l1p{si}_{ci}"])
                yield
                P.dma("sync", d["taps"][:, c0:c0 + CH], tp[:, 0:CH], reads=[tpk], writes=[f"taps{si}"], final=True)
                yield
            yield from interleave([mlp_chain(ci) for ci in range(nchunk)], 2)
            l1 = sb(f"l1_{si}", [64, 2])
            P.gen("vector", lambda e, l1=l1, l1p=l1p: e.reduce_sum(out=l1[:, 0:1], in_=l1p[:], axis=AX.X),
                  reads=[f"l1p{si}_{ci}" for ci in range(nchunk)], writes=[f"l1{si}"])
            yield
            P.gen("vector", lambda e, l1=l1: e.reciprocal(out=l1[:, 1:2], in_=l1[:, 0:1]), reads=[f"l1{si}"], writes=[f"l1{si}"])
            yield
            dg = sb(f"dg{si}", [64, 64])
            P.ts("vector", dg[:], ident[0:64, 0:64], l1[:, 1:2], None, ALU.mult, reads=["cst", f"l1{si}"], writes=[f"dg{si}"])
            yield
            bk, bkk = bkY.next()
            P.mm(bk[:, 0:64], ones[0:64, :], dg[:], True, True, reads=["cst", f"dg{si}"], writes=[bkk])
            yield
            rl1b = sb(f"rl1b{si}", [128, 64])
            P.copy("vector", rl1b[:], bk[:, 0:64], reads=[bkk], writes=[f"rl1b{si}"])
            yield

            def fwd_fft(xt, xk, Krows, lowp=False):
                TT = Tb if lowp else T
                bA, bAk = bkA.next()
                P.mm(bA[:, 0:2 * N1], xt, TT["F1c"][0:Krows, :], True, True, reads=[xk] + tabk, writes=[bAk])
                yield
                As, Ask = As_r.next()
                P.copy("scalar", As[:, 0:2 * N1], bA[:, 0:2 * N1], reads=[bAk], writes=[Ask])
                yield
                Bc, Bck = (Brb if lowp else Br).next()
                yield from cmul(As[:, 0:2 * N1], Ask, 128, N1, T["twRR"][:], T["twII"][:], tabk[0], Bc, Bck)
                bX, bXk = bkX.next()
                Bre, Bim = Bc[:, 0:N1], Bc[:, N1:2 * N1]
                P.mm(bX[:, 0:N1], TT["F2re"][:], Bre, True, False, reads=[Bck + "r"] + tabk, writes=[bXk])
                P.mm(bX[:, 0:N1], TT["nF2im"][:], Bim, False, True, reads=[Bck + "i"] + tabk, writes=[bXk])
                yield
                P.mm(bX[:, N1:2 * N1], TT["F2re"][:], Bim, True, False, reads=[Bck + "i"] + tabk, writes=[bXk])
                P.mm(bX[:, N1:2 * N1], TT["F2im"][:], Bre, False, True, reads=[Bck + "r"] + tabk, writes=[bXk])
                yield
                return bX, bXk

            H = sb(f"H{si}", [128, 64, 2 * N1])
            def filt_chain(oc):
                tl, tlk = tlr.next()
                P.dma("sync", tl[0:N1, :], d["taps"][oc].rearrange("(a b) -> a b", b=128), reads=[f"taps{si}"], writes=[tlk])
                yield
                bX, bXk = yield from fwd_fft(tl[0:N1, :], tlk, N1)
                P.ts("vector", H[:, oc, :], bX[:, 0:2 * N1], rl1b[:, oc:oc + 1], None, ALU.mult, reads=[bXk, f"rl1b{si}"], writes=[f"H{si}_{oc}"])
                yield

            yield from interleave([filt_chain(oc) for oc in range(64)], 4)
            def long_conv(zt, zk, o, c):
                zb, zbk = zbr.next()
                P.copy("scalar", zb[0:Kd, :], zt, reads=[zk], writes=[zbk])
                yield
                bX, bXk = yield from fwd_fft(zb[0:Kd, :], zbk, Kd, lowp=True)
                oc = o * HCH + c
                Hre, Him = H[:, oc, 0:N1], H[:, oc, N1:2 * N1]
                hk = f"H{si}_{oc}"
                pp = [r.next() for r in pr4]
                P.tt("vector", pp[0][0][:, 0:N1], bX[:, 0:N1], Hre, ALU.mult, reads=[bXk, hk], writes=[pp[0][1]])
                yield
                P.tt("vector", pp[1][0][:, 0:N1], bX[:, N1:2 * N1], Him, ALU.mult, reads=[bXk, hk], writes=[pp[1][1]])
                yield
                P.tt("vector", pp[2][0][:, 0:N1], bX[:, 0:N1], Him, ALU.mult, reads=[bXk, hk], writes=[pp[2][1]])
                yield
                P.tt("vector", pp[3][0][:, 0:N1], bX[:, N1:2 * N1], Hre, ALU.mult, reads=[bXk, hk], writes=[pp[3][1]])
                yield
                Yc, Yck = Yr.next()
                P.tt("gpsimd", Yc[:, 0:N1], pp[0][0][:, 0:N1], pp[1][0][:, 0:N1], ALU.subtract, reads=[pp[0][1], pp[1][1]], writes=[Yck + "r"])
                yield
                P.tt("gpsimd", Yc[:, N1:2 * N1], pp[2][0][:, 0:N1], pp[3][0][:, 0:N1], ALU.add, reads=[pp[2][1], pp[3][1]], writes=[Yck + "i"])
                yield
                bC, bCk = bkC.next()
                P.mm(bC[0:N1, 0:256], Yc[:, 0:N1], Tb["G2c"][:], True, False, reads=[Yck + "r"] + tabk, writes=[bCk])
                P.mm(bC[0:N1, 0:256], Yc[:, N1:2 * N1], Tb["G2s"][:], False, True, reads=[Yck + "i"] + tabk, writes=[bCk])
                yield
                Cs, Csk = As_r.next()
                P.copy("scalar", Cs[0:N1, :], bC[0:N1, 0:256], reads=[bCk], writes=[Csk])
                yield
                Dc, Dck = Dr.next()
                yield from cmul(Cs[0:N1, :], Csk, N1, 128, T["twcRR"][:], T["twcII"][:], tabk[0], Dc, Dck)
                bY, bYk = bkY.next()
                P.mm(bY[0:Kd, 0:128], Tb["G1re"][:], Dc[0:N1, 0:128], True, False, reads=[Dck + "r"] + tabk, writes=[bYk])
                P.mm(bY[0:Kd, 0:128], Tb["nG1im"][:], Dc[0:N1, 128:256], False, True, reads=[Dck + "i"] + tabk, writes=[bYk])
                yield
                return bY, bYk

            def data_chain(c, b):
                xi, xik = xir.next()
                src = bass.AP(tensor=d["u"].tensor, offset=d["u"][0, c, b, 0].offset,
                              ap=[[128, Kd], [HCH * B * (n + 2), 3], [1, 130]])
                P.dma("gpsimd", xi[0:Kd], src, writes=[xik])
                yield
                cv, cvk = cvr.next()
                for p in range(3):
                    wo = (p * HCH + c) * 4
                    P.ts("vector", cv[0:Kd, p, :], xi[0:Kd, p, 1:129], cwb[0:Kd, wo + 1:wo + 2], cwb[0:Kd, wo + 3:wo + 4], ALU.mult, ALU.add,
                         reads=[xik, "cwb"], writes=[cvk + str(p)])
                    yield
                    P.stt("vector", cv[0:Kd, p, :], xi[0:Kd, p, 0:128], cwb[0:Kd, wo:wo + 1], cv[0:Kd, p, :], ALU.mult, ALU.add,
                          reads=[xik, "cwb", cvk + str(p)], writes=[cvk + str(p)])
                    yield
                    P.stt("vector", cv[0:Kd, p, :], xi[0:Kd, p, 2:130], cwb[0:Kd, wo + 2:wo + 3], cv[0:Kd, p, :], ALU.mult, ALU.add,
                          reads=[xik, "cwb", cvk + str(p)], writes=[cvk + str(p)])
                    yield
                bY, bYk = yield from long_conv(cv[0:Kd, 0, :], cvk + "0", 0, c)
                tg, tgk = tgr.next()
                P.stt("vector", tg[0:Kd, :], cv[0:Kd, 0, :], skb[0:Kd, c:c + 1], bY[0:Kd, 0:128], ALU.mult, ALU.add,
                      reads=[cvk + "0", "skb", bYk], writes=[tgk])
                yield
                z1, z1k = z1r.next()
                P.tt("gpsimd", z1[0:Kd, :], tg[0:Kd, :], cv[0:Kd, 1, :], ALU.mult, reads=[tgk, cvk + "1"], writes=[z1k])
                yield
                bY, bYk = yield from long_conv(z1[0:Kd, :], z1k, 1, c)
                tg, tgk = tgr.next()
                P.stt("vector", tg[0:Kd, :], z1[0:Kd, :], skb[0:Kd, HCH + c:HCH + c + 1], bY[0:Kd, 0:128], ALU.mult, ALU.add,
                      reads=[z1k, "skb", bYk], writes=[tgk])
                yield
                z2, z2k = z2r.next()
                P.tt("gpsimd", z2[0:Kd, :], tg[0:Kd, :], cv[0:Kd, 2, :], ALU.mult, reads=[tgk, cvk + "2"], writes=[z2k])
                yield
                P.dma("sync", d["out"][c, b].rearrange("(a b) -> a b", b=128), z2[0:Kd, :], reads=[z2k], writes=[z2k + "d"], final=True)
                yield
            yield from interleave([data_chain(c, b) for c in range(HCH) for b in range(B)], 4)

        for si, n in enumerate(sizes):
            for _ in size_body(si, n):
                pass
        P.emit()
    return nc


def run_k4(hy_list, conv_w, conv_b, fparams, skip):
    f_w1, f_b1, f_freq, f_w2, f_b2, f_w3 = fparams
    sizes = [h.shape[1] for h in hy_list]
    nc = build_k4(sizes)
    B = 2
    cst = np.concatenate([np.eye(128, dtype=np.float32), np.ones((128, 128), np.float32)], 1)
    consts = [hyena_consts(n) for n in sizes]
    tabs = [fft_tables(2 * n // 128) for n in sizes]
    w3r = f_w3.reshape(64, 2, 2, 256)
    in_maps = []
    for i in range(NCORES):
        cs = slice(i * HCH, (i + 1) * HCH)
        m = {"cst": cst}
        mlp = np.zeros((64, 260), np.float32)
        mlp[0:33, 0:64] = f_w1
        mlp[:, 64:128] = f_w2
        mlp[:, 128:192] = w3r[:, 0, :, cs].reshape(64, 64)
        mlp[:, 192:256] = w3r[:, 1, :, cs].reshape(64, 64)
        mlp[:, 256] = f_b1; mlp[:, 257] = f_freq[0]; mlp[:, 258] = f_b2; mlp[:, 259] = f_freq[1]
        m["mlp"] = mlp
        cw = np.zeros((3, HCH, 4), np.float32)
        for p in range(3):
            ch = slice(p * 256 + i * HCH, p * 256 + (i + 1) * HCH)
            cw[p, :, 0:3] = conv_w[:, ch].T
            cw[p, :, 3] = conv_b[ch]
        m["cwv"] = cw.reshape(-1)
        m["skv"] = np.ascontiguousarray(skip[:, cs]).reshape(-1)
        for si, (hy, n) in enumerate(zip(hy_list, sizes)):
            u = np.zeros((3, HCH, B, n + 2), np.float32)
            for p in range(3):
                u[p, :, :, 1:n + 1] = hy[:, :, p * 256 + i * HCH:p * 256 + (i + 1) * HCH].transpose(2, 0, 1)
            m[f"u{si}"] = u
            feats, win = consts[si]
            m[f"feats{si}"] = feats
            m[f"win{si}"] = np.ascontiguousarray(np.tile(win[cs], (2, 1)))
            for k, v in tabs[si].items():
                m[f"t{si}_{k}"] = v
        in_maps.append(m)
    res = _run(nc, in_maps)
    outs = []
    for si, n in enumerate(sizes):
        y = np.zeros((B, n, 256), np.float32)
        for i in range(NCORES):
            y[:, :, i * HCH:(i + 1) * HCH] = res.results[i][f"o_yh{si}"].transpose(1, 2, 0)
        outs.append(y)
    return outs, [np.concatenate([res.results[i][f"taps{si}"] for i in range(NCORES)], 0) for si in range(len(sizes))]


def kernel(x, c, ctx, c_ctx, w_mod, b_mod, w_in, mlstm_conv_w, mlstm_conv_b, mlstm_gate_b,
           mlstm_norm_w, attn_q_norm_w, attn_k_norm_w, hyena_conv_w, hyena_conv_b,
           hyena_f_w1, hyena_f_b1, hyena_f_freq, hyena_f_w2, hyena_f_b2, hyena_f_w3,
           hyena_skip, w_out, ln_mix_w, ln_mix_b, router_w, router_b,
           exp_w_gate, exp_w_up, exp_w_down, ln_ffn_w, ln_ffn_b):
    f = lambda a: np.asarray(a, dtype=np.float32)
    x, c, ctx, c_ctx = f(x), f(c), f(ctx), f(c_ctx)
    mod = run_k0(c, c_ctx, f(w_mod), f(b_mod))
    depth = w_in.shape[0]
    for l in range(depth):
        last = l == depth - 1
        P_lat, P_ctx = run_k1(x, ctx, mod[l], f(w_in[l]))
        ym_l, ym_c = run_k2(P_lat, P_ctx, f(mlstm_conv_w[l]), f(mlstm_conv_b[l]), f(mlstm_gate_b[l]), f(mlstm_norm_w[l]))
        ya_l, ya_c = run_k3(P_lat, P_ctx, f(attn_q_norm_w[l]), f(attn_k_norm_w[l]))
        hy = [P_lat[..., 1808:]] + ([] if last else [P_ctx[..., 1808:]])
        fpar = (f(hyena_f_w1[l]), f(hyena_f_b1[l]), f(hyena_f_freq[l]), f(hyena_f_w2[l]), f(hyena_f_b2[l]), f(hyena_f_w3[l]))
        yhs, _ = run_k4(hy, f(hyena_conv_w[l]), f(hyena_conv_b[l]), fpar, f(hyena_skip[l]))
        yh_c = np.zeros_like(ym_c) if last else yhs[1]
        ycat_l = np.concatenate([ym_l, ya_l, yhs[0]], -1)
        ycat_c = np.concatenate([ym_c, ya_c, yh_c], -1)
        x1, c1, h2, h2c, aff, affc = run_k5(ycat_l, ycat_c, x, ctx, mod[l], f(w_out[l]), f(ln_mix_w[l]), f(ln_mix_b[l]),
                                            f(router_w[l]), f(router_b[l]))
        thr, thrc = run_k6(aff, affc)
        x, ctx = run_moe(h2, h2c, aff, affc, thr, thrc, x1, c1, mod[l], f(ln_ffn_w[l]), f(ln_ffn_b[l]),
                         f(exp_w_gate[l]), f(exp_w_up[l]), f(exp_w_down[l]))
    return x.astype(np.float32)


CAP_L, CAP_C = 1024, 32
SLOTS_B = CAP_L + CAP_C
NTOK_ALL = 2 * 8192 + 2 * 256


def build_k7g():
    nc = bass.Bass("TRN2", target_bir_lowering=False)
    T = 18
    TOK = T * 128
    h2d = nc.dram_tensor("h2all", [NTOK_ALL, 1024], BF16, kind="ExternalInput").ap()
    affLd = nc.dram_tensor("affL", [2, 2, 128, 66], F32, kind="ExternalInput").ap()
    thrd = nc.dram_tensor("thr8", [8], F32, kind="ExternalInput").ap()
    tidd = nc.dram_tensor("tid", [2, 128, 66, 2], F32, kind="ExternalInput").ap()
    iotad = nc.dram_tensor("iota", [128, CAP_L + 128], F32, kind="ExternalInput").ap()
    cstd = nc.dram_tensor("cst", [128, 448], F32, kind="ExternalInput").ap()
    wgd = nc.dram_tensor("wg", [2, 128, 8, D_FF], F32, kind="ExternalInput").ap()
    wud = nc.dram_tensor("wu", [2, 128, 8, D_FF], F32, kind="ExternalInput").ap()
    wdd = nc.dram_tensor("wd", [2, 128, NFC, 1024], F32, kind="ExternalInput").ap()
    Yo = nc.dram_tensor("o_Y", [2, TOK, 1024], BF16, kind="ExternalOutput").ap()
    posKo = nc.dram_tensor("o_pos", [2, 2, 128, 66], I32, kind="ExternalOutput").ap()
    with ExitStack() as es:
        P = Prog(nc, es)
        sb, ps, ring = mk_alloc(nc, es)
        banks = [(ps(f"bk{i}", [128, 512], F32), f"bk{i}") for i in range(8)]
        cst = sb("cst_s", [128, 448]); P.dma("sync", cst[:], cstd, writes=["cst"])
        Ust, ones, identf, Ust64 = cst[:, 0:128], cst[:, 128:256], cst[:, 256:384], cst[0:64, 384:448]
        idb = sb("idb", [128, 128], BF16); P.copy("vector", idb[:], identf, reads=["cst"], writes=["idb"])
        thrt = sb("thrt", [128, 8]); P.dma("sync", thrt[:], bcast_rows(thrd, 128), writes=["thrt"])
        iota = sb("iota_s", [128, CAP_L + 128]); P.dma("sync", iota[:], iotad, writes=["iota"])
        h2T = sb("h2T_s", [128, 8, TOK], BF16)
        acc = sb("acc", [128, T, 1024])
        tv = sb("tv", [128, T])
        Ar = ring("A", 2, [128, 66]); Mr = ring("M", 2, [128, 66]); wir = ring("wi", 2, [128, 66]); pfr = ring("pf", 2, [128, 66])
        m2r = ring("m2", 2, [128, 66]); pir = ring("pi", 2, [128, 66], I32)
        tdfr = ring("tdf", 2, [128, 66, 2]); TAr = ring("TA", 2, [128, 66, 5], BF16); spr = ring("sp", 2, [128, 2, 66]); l5r = ring("l5", 2, [128, 9, 5]); selr = ring("sel", 4, [128, 512], BF16); rowt = sb("rowt", [5, SLOTS_B]); lfr = ring("lf", 2, [128, 18])
        tcr = ring("tc", 2, [64, 1]); tbr = ring("tb", 2, [64, 128])
        lsr = ring("ls", 2, [128, 9], I32)
        xsr = ring("xs", 2, [128, 1024], BF16)
        FG = 3
        wgr = ring("wgb", 2, [128, 8, FG * 128], BF16); wur = ring("wub", 2, [128, 8, FG * 128], BF16)
        wdr = ring("wdb", 2, [128, FG, 1024], BF16); stg = ring("stg", 3, [128, 1024])
        actr = ring("actT", 2, [128, FG, 512], BF16); sgr = ring("sg", 2, [128, 512]); yor = ring("yrow", 2, [128, 1024], BF16)
        pgr = Ring(banks[0:2]); pur = Ring(banks[2:4]); pyr = Ring(banks[4:6])
        tgs = [(s, min(512, TOK - s)) for s in range(0, TOK, 512)]
        for e in range(2):
            P.memset("gpsimd", h2T[:], 0.0, writes=["h2T"])
            P.memset("vector", tv[:], 0.0, writes=["tv"])
            for t in range(T):
                P.memset("gpsimd", acc[:, t, :], 0.0, writes=[f"acc{t}a", f"acc{t}b"])
            for b in range(2):
                A, Ak = Ar.next(); P.dma("sync", A[:], affLd[e, b], writes=[Ak])
                M, Mk = Mr.next()
                to = (e * 2 + b) * 2
                P.ts("vector", M[:, 0:64], A[:, 0:64], thrt[:, to:to + 1], None, ALU.is_ge, reads=[Ak, "thrt"], writes=[Mk + "l"])
                P.ts("vector", M[:, 64:66], A[:, 64:66], thrt[:, to + 1:to + 2], None, ALU.is_ge, reads=[Ak, "thrt"], writes=[Mk + "c"])
                wi, wik = wir.next(); pf, pfk = pfr.next(); m2, m2k = m2r.next(); pi, pik = pir.next()
                for (c0, ncol, cap, base, sfx) in ((0, 64, CAP_L, 0, "l"), (64, 2, CAP_C, CAP_L, "c")):
                    bw, bwk = banks[6]
                    P.mm(bw[:, c0:c0 + ncol], Ust, M[:, c0:c0 + ncol], True, True, reads=["cst", Mk + sfx], writes=[bwk + sfx])
                    P.copy("scalar", wi[:, c0:c0 + ncol], bw[:, c0:c0 + ncol], reads=[bwk + sfx], writes=[wik + sfx])
                    bt, btk = banks[7]
                    P.mm(bt[0:ncol, c0:c0 + 1], M[:, c0:c0 + ncol], ones[:, 0:1], True, True, reads=["cst", Mk + sfx], writes=[btk + "t" + sfx])
                    tc, tck = tcr.next()
                    P.copy("vector", tc[0:ncol, :], bt[0:ncol, c0:c0 + 1], reads=[btk + "t" + sfx], writes=[tck])
                    tb, tbk = tbr.next()
                    P.ts("vector", tb[0:ncol, :], ones[0:ncol, :], tc[0:ncol, 0:1], None, ALU.mult, reads=["cst", tck], writes=[tbk])
                    P.mm(bt[:, 128 + c0:128 + c0 + ncol], tb[0:ncol, :], Ust64[0:ncol, 0:ncol], True, True, reads=[tbk, "cst"], writes=[btk + "o" + sfx])
                    sl = slice(c0, c0 + ncol)
                    P.tt("vector", pf[:, sl], bt[:, 128 + c0:128 + c0 + ncol], wi[:, sl], ALU.add, reads=[btk + "o" + sfx, wik + sfx], writes=[pfk + sfx])
                    P.ts("vector", m2[:, sl], pf[:, sl], float(cap) - 0.5, None, ALU.is_lt, reads=[pfk + sfx], writes=[m2k + sfx])
                    P.tt("vector", m2[:, sl], m2[:, sl], M[:, sl], ALU.mult, reads=[m2k + sfx, Mk + sfx], writes=[m2k + sfx])
                    P.ts("vector", pf[:, sl], pf[:, sl], float(base - SLOTS_B), None, ALU.add, reads=[pfk + sfx], writes=[pfk + sfx])
                    P.tt("vector", pf[:, sl], pf[:, sl], m2[:, sl], ALU.mult, reads=[pfk + sfx, m2k + sfx], writes=[pfk + sfx])
                    P.ts("vector", pf[:, sl], pf[:, sl], float(SLOTS_B), None, ALU.add, reads=[pfk + sfx], writes=[pfk + sfx])
                P.copy("vector", pi[:], pf[:], reads=[pfk + "l", pfk + "c"], writes=[pik])
                P.dma("sync", posKo[e, b], pi[:], reads=[pik], writes=[pik + "d"], final=True)
                TA, TAk = TAr.next()
                tdf, tdfk = tdfr.next()
                P.dma("sync", tdf[:], tidd[b], writes=[tdfk])
                P.copy("gpsimd", TA[:, :, 0:2], tdf[:], reads=[tdfk], writes=[TAk + "t"])
                sp, spk = spr.next()
                P.copy("vector", TA[:, :, 2], A[:], reads=[Ak], writes=[TAk + "a0"])
                P.copy("vector", sp[:, 0, :], TA[:, :, 2], reads=[TAk + "a0"], writes=[spk + "0"])
                P.tt("vector", sp[:, 1, :], A[:], sp[:, 0, :], ALU.subtract, reads=[Ak, spk + "0"], writes=[spk + "1"])
                P.copy("vector", TA[:, :, 3], sp[:, 1, :], reads=[spk + "1"], writes=[TAk + "a1"])
                P.copy("vector", sp[:, 0, :], TA[:, :, 3], reads=[TAk + "a1"], writes=[spk + "0"])
                P.tt("vector", sp[:, 1, :], sp[:, 1, :], sp[:, 0, :], ALU.subtract, reads=[spk + "1", spk + "0"], writes=[spk + "1"])
                P.copy("vector", TA[:, :, 4], sp[:, 1, :], reads=[spk + "1"], writes=[TAk + "a2"])
                tak = [TAk + "t", TAk + "a0", TAk + "a1", TAk + "a2"]
                pc, pck = banks[7]
                for piece, (s0_, ns_, js) in enumerate(((0, 512, range(64)), (512, 512, range(64)), (CAP_L, CAP_C, (64, 65)))):
                    sfx = "l" if piece < 2 else "c"
                    for jj, j in enumerate(js):
                        se, sek = selr.next()
                        P.ts("vector", se[:, 0:ns_], iota[:, s0_:s0_ + ns_], pf[:, j:j + 1], None, ALU.is_equal, reads=["iota", pfk + sfx], writes=[sek])
                        P.mm(pc[0:5, 0:ns_], TA[:, j, :], se[:, 0:ns_], jj == 0, jj == len(js) - 1, reads=[sek] + tak, writes=[pck + "row"])
                    P.copy("scalar", rowt[0:5, s0_:s0_ + ns_], pc[0:5, 0:ns_], reads=[pck + "row"], writes=[f"rowt{piece}"])
                pq, pqk = banks[6]
                for c in range(9):
                    ns_ = 128 if c < 8 else 32
                    P.tr(pq[0:ns_, 256 + 5 * c:256 + 5 * c + 5], rowt[0:5, c * 128:c * 128 + ns_], identf[0:5, 0:5],
                         reads=[f"rowt{c // 4}", "cst"], writes=[pqk + "q"])
                l5, l5k = l5r.next()
                P.copy("vector", l5[:].rearrange("p c t -> p (c t)"), pq[:, 256:301], reads=[pqk + "q"], writes=[l5k])
                lf, lfk = lfr.next()
                lf3 = lf[:].rearrange("p (c t) -> p c t", t=2)
                P.stt("vector", lf3[:, :, 0], l5[:, :, 0], 128.0, l5[:, :, 1], ALU.mult, ALU.add, reads=[l5k], writes=[lfk + "i"])
                P.tt("vector", lf3[:, :, 1], l5[:, :, 2], l5[:, :, 3], ALU.add, reads=[l5k], writes=[lfk + "v"])
                P.tt("vector", lf3[:, :, 1], lf3[:, :, 1], l5[:, :, 4], ALU.add, reads=[l5k, lfk + "v"], writes=[lfk + "v"])
                ls, lsk = lsr.next()
                P.copy("vector", ls[:], lf3[:, :, 0], reads=[lfk + "i"], writes=[lsk])
                for c in range(9):
                    npart = 128 if c < 8 else 32
                    p0 = 0
                    col0 = b * CAP_L + c * 128 if c < 8 else (16 + b) * 128
                    tcol = col0 // 128
                    P.copy("vector", tv[p0:p0 + npart, tcol:tcol + 1], lf[p0:p0 + npart, 2 * c + 1:2 * c + 2], reads=[lfk + "v", "tv"], writes=["tv"])
                    xs, xsk = xsr.next()
                    P.op("gpsimd", lambda en, xs=xs, ls=ls, c=c, npart=npart, p0=p0: en.indirect_dma_start(
                        out=xs[p0:p0 + npart, :], out_offset=None, in_=h2d[:, :],
                        in_offset=bass.IndirectOffsetOnAxis(ap=ls[p0:p0 + npart, c:c + 1], axis=0)), reads=[lsk], writes=[xsk], dma=True)
                    pb, pbk = banks[6]
                    pT = pb[:, :].bitcast(BF16)
                    for kc in range(8):
                        P.tr(pT[:, kc * 128:kc * 128 + npart], xs[p0:p0 + npart, kc * 128:(kc + 1) * 128], idb[p0:p0 + npart, p0:p0 + npart],
                             reads=[xsk, "idb"], writes=[pbk + "l", pbk + "c"])
                    P.copy("scalar", h2T[:, :, col0:col0 + npart], pT[:, 0:1024].rearrange("p (k s) -> p k s", k=8)[:, :, 0:npart],
                           reads=[pbk + "l", pbk + "c"], writes=["h2T"])
            ci = 0
            for f0 in range(0, NFC, FG):
                nf = min(FG, NFC - f0)
                wgb, wgk = wgr.next(); wub, wuk = wur.next(); wdb, wdk = wdr.next()
                for (src, dst, dk) in ((wgd, wgb, wgk), (wud, wub, wuk)):
                    for kp in range(0, 8, 2):
                        st, sk = stg.next()
                        sv = st[:, 0:2 * nf * 128].rearrange("p (a f) -> p a f", a=2)
                        P.dma("sync", sv, src[e, :, kp:kp + 2, f0 * 128:(f0 + nf) * 128], writes=[sk])
                        P.copy("gpsimd" if ci % 2 else "vector", dst[:, kp:kp + 2, 0:nf * 128], sv, reads=[sk], writes=[dk + f"k{kp}"])
                        ci += 1
                for fc in range(nf):
                    st, sk = stg.next()
                    P.dma("sync", st[:], wdd[e, :, f0 + fc, :], writes=[sk])
                    P.copy("gpsimd" if ci % 2 else "vector", wdb[:, fc, :], st[:], reads=[sk], writes=[wdk + f"f{fc}"])
                    ci += 1
                for (s0, ns) in tgs:
                    actT, ak = actr.next()
                    for fc in range(nf):
                        pg, pgk = pgr.next(); pu, puk = pur.next()
                        for kc in range(8):
                            P.mm(pg[:, 0:ns], wgb[:, kc, fc * 128:(fc + 1) * 128], h2T[:, kc, s0:s0 + ns], kc == 0, kc == 7,
                                 reads=["h2T", wgk + f"k{kc - kc % 2}"], writes=[pgk])
                        for kc in range(8):
                            P.mm(pu[:, 0:ns], wub[:, kc, fc * 128:(fc + 1) * 128], h2T[:, kc, s0:s0 + ns], kc == 0, kc == 7,
                                 reads=["h2T", wuk + f"k{kc - kc % 2}"], writes=[puk])
                        sg, sgk = sgr.next()
                        P.act(sg[:, 0:ns], pg[:, 0:ns], AF.Silu, reads=[pgk], writes=[sgk])
                        P.tt("vector", actT[:, fc, 0:ns], pu[:, 0:ns], sg[:, 0:ns], ALU.mult, reads=[puk, sgk], writes=[ak + f"f{fc}"])
                    for tt in range(s0 // 128, (s0 + ns) // 128):
                        for hf in range(2):
                            py, pyk = pyr.next()
                            for fc in range(nf):
                                P.mm(py[:], actT[:, fc, tt * 128 - s0:(tt + 1) * 128 - s0], wdb[:, fc, hf * 512:(hf + 1) * 512],
                                     fc == 0, fc == nf - 1, reads=[ak + f"f{fc}", wdk + f"f{fc}"], writes=[pyk])
                            ah = f"acc{tt}" + "ab"[hf]
                            P.tt("vector", acc[:, tt, hf * 512:(hf + 1) * 512], py[:], acc[:, tt, hf * 512:(hf + 1) * 512], ALU.add,
                                 reads=[pyk, ah], writes=[ah])
            for t in range(T):
                yr_, yrk = yor.next()
                P.act(yr_[:], acc[:, t, :], AF.Copy, reads=[f"acc{t}a", f"acc{t}b", "tv"], writes=[yrk], scale=tv[:, t:t + 1])
                P.dma("sync", Yo[e, t * 128:(t + 1) * 128, :], yr_[:], reads=[yrk], writes=[yrk], final=True)
        P.emit()
    return nc


def build_k8():
    nc = bass.Bass("TRN2", target_bir_lowering=False)
    T = K1_TILES
    Yb = [nc.dram_tensor(f"Yb{e}", [SLOTS_B + 1, 1024], BF16, kind="ExternalInput").ap() for e in range(N_EXP)]
    idxd = nc.dram_tensor("idx", [T, 128, N_EXP], I32, kind="ExternalInput").ap()
    x1d = nc.dram_tensor("i_x1", [T, 128, 1024], F32, kind="ExternalInput").ap()
    rows = nc.dram_tensor("rows", [2, 1024], F32, kind="ExternalInput").ap()
    lnr = nc.dram_tensor("lnr", [2, 1024], F32, kind="ExternalInput").ap()
    x2o = nc.dram_tensor("o_x2", [T, 128, 1024], F32, kind="ExternalOutput").ap()
    with ExitStack() as es:
        P = Prog(nc, es)
        sb, ps, ring = mk_alloc(nc, es)
        rowt = sb("rowt", [128, 2, 1024]); lnt = sb("lnt", [128, 2, 1024])
        for j in range(2):
            P.dma("sync", rowt[:, j, :], bcast_rows(rows[j], 128), writes=[f"row{j}"])
            P.dma("sync", lnt[:, j, :], bcast_rows(lnr[j], 128), writes=[f"ln{j}"])
        idr = ring("idx", 2, [128, N_EXP], I32)
        gr = ring("g", 8, [128, 1024], BF16)
        accr = ring("acc", 2, [128, 1024])
        xr = ring("x", 2, [128, 1024])
        str_ = ring("st", 2, [128, 2, 6]); mvr = ring("mv", 2, [128, 2]); rsr = ring("rs", 2, [128, 1])
        for t in range(T):
            g_ = 0 if t < 16 else 1
            ix, ixk = idr.next(); P.dma("sync", ix[:], idxd[t], writes=[ixk])
            acc, ack = accr.next()
            for e in range(N_EXP):
                gt, gk = gr.next()
                P.op("gpsimd", lambda en, gt=gt, ix=ix, e=e: en.indirect_dma_start(
                    out=gt[:, :], out_offset=None, in_=Yb[e][:, :], in_offset=bass.IndirectOffsetOnAxis(ap=ix[:, e:e + 1], axis=0)),
                    reads=[ixk], writes=[gk], dma=True)
                if e == 0:
                    P.copy("vector", acc[:], gt[:], reads=[gk], writes=[ack])
                else:
                    P.tt("vector", acc[:], acc[:], gt[:], ALU.add, reads=[ack, gk], writes=[ack])
            x, xk = xr.next(); P.dma("sync", x[:], x1d[t], writes=[xk])
            P.tt("gpsimd", acc[:], acc[:], rowt[:, g_, :], ALU.mult, reads=[ack, f"row{g_}"], writes=[ack])
            P.stt("vector", acc[:], x[:], ALPHA, acc[:], ALU.mult, ALU.add, reads=[xk, ack], writes=[ack])
            st, _ = str_.next(); mv, _ = mvr.next(); rs, _ = rsr.next()
            emit_layernorm(P, acc[:], ack, x[:], xk, st, mv, rs, f"lnC{t % 2}")
            P.tt("gpsimd", acc[:], x[:], lnt[:, 0, :], ALU.mult, reads=[xk, "ln0"], writes=[ack])
            P.tt("gpsimd", x[:], acc[:], lnt[:, 1, :], ALU.add, reads=[ack, "ln1"], writes=[xk])
            P.dma("sync", x2o[t], x[:], reads=[xk], writes=[xk], final=True)
        P.emit()
    return nc


def run_moe(h2, h2c, aff, affc, thr, thrc, x1, c1, mod_l, ln_w, ln_b, wg, wu, wd):
    B, n, D = x1.shape
    nctx = c1.shape[1]
    E = aff.shape[-1]
    h2all = np.ascontiguousarray(np.concatenate([h2.reshape(B * n, D), h2c.reshape(B * nctx, D)], 0))
    affall = np.concatenate([aff.reshape(B * n, E), affc.reshape(B * nctx, E)], 0)
    s_idx, j_idx = np.meshgrid(np.arange(128), np.arange(128), indexing="ij")
    cst = np.zeros((128, 448), np.float32)
    cst[:, 0:128] = (s_idx < j_idx); cst[:, 128:256] = 1.0; cst[:, 256:384] = np.eye(128); cst[0:64, 384:448] = (s_idx < j_idx)[:64, :64]
    tid = np.zeros((B, 128, 66), np.int64)
    iota = np.full((128, CAP_L + 128), -1.0, np.float32)
    iota[:, 0:CAP_L] = np.arange(CAP_L)
    iota[:, CAP_L:CAP_L + 32] = CAP_L + np.arange(32)
    for b in range(B):
        tid[b, :, 0:64] = (b * n + np.arange(n)).reshape(64, 128).T
        tid[b, :, 64:66] = (B * n + b * nctx + np.arange(nctx)).reshape(2, 128).T
    tid2 = np.ascontiguousarray(np.stack([tid // 128, tid % 128], -1).astype(np.float32))
    nc = build_k7g()
    in_maps = []
    for i in range(NCORES):
        es = [2 * i, 2 * i + 1]
        affL = np.zeros((2, B, 128, 66), np.float32)
        thr8 = np.zeros((2, B, 2), np.float32)
        for el, e in enumerate(es):
            for b in range(B):
                affL[el, b, :, 0:64] = aff[b, :, e].reshape(64, 128).T
                affL[el, b, :, 64:66] = affc[b, :, e].reshape(2, 128).T
                thr8[el, b] = (thr[b, e], thrc[b, e])
        lw = lambda w, kch: np.ascontiguousarray(w.reshape(2, kch, 128, w.shape[-1]).transpose(0, 2, 1, 3))
        in_maps.append({"h2all": h2all, "affL": affL,
                        "thr8": thr8.reshape(-1), "tid": tid2, "iota": iota, "cst": cst,
                        "wg": lw(wg[es], 8), "wu": lw(wu[es], 8), "wd": lw(wd[es], NFC)})
    res = _run(nc, in_maps)
    Yb = np.zeros((B, E, SLOTS_B + 1, D), h2all.dtype)
    posL = np.zeros((B, n, E), np.int32); posC = np.zeros((B, nctx, E), np.int32)
    for i in range(NCORES):
        Y = res.results[i]["o_Y"]; pos = res.results[i]["o_pos"]
        for el in range(2):
            e = 2 * i + el
            for b in range(B):
                Yb[b, e, 0:CAP_L] = Y[el, b * CAP_L:(b + 1) * CAP_L]
                Yb[b, e, CAP_L:SLOTS_B] = Y[el, (16 + b) * 128:(16 + b) * 128 + CAP_C]
                posL[b, :, e] = pos[el, b, :, 0:64].T.reshape(n)
                posC[b, :, e] = pos[el, b, :, 64:66].T.reshape(nctx)
    nc8 = build_k8()
    lnr = np.ascontiguousarray(np.stack([ln_w, ln_b]))
    in_maps = []
    for i in range(NCORES):
        b = i // 4
        rows = np.ascontiguousarray(np.stack([mod_l[b, 5120:6144], mod_l[2, 5120:6144]]))
        idx = tok_shard(posL, posC, i)
        idx[-1, 64:, :] = SLOTS_B
        m = {"idx": idx, "i_x1": tok_shard(x1, c1, i), "rows": rows, "lnr": lnr}
        for e in range(E):
            m[f"Yb{e}"] = np.ascontiguousarray(Yb[b, e])
        in_maps.append(m)
    res = _run(nc8, in_maps)
    return tok_unshard([r["o_x2"] for r in res.results], B, n, nctx)
```

```python
import numpy as np
from contextlib import ExitStack
import concourse.bass as bass
import concourse.mybir as mybir
from concourse.bass_utils import run_bass_kernel_spmd

F32 = mybir.dt.float32
BF16 = mybir.dt.bfloat16
I32 = mybir.dt.int32
AF = mybir.ActivationFunctionType
ALU = mybir.AluOpType
AX = mybir.AxisListType

NCORES = 8
STAGE_EXP = False
FUSE_WAIT = True
NO_RAW_SELF = False
SELF_SYNC = True


class Prog:
    ENGS = ("sync", "scalar", "vector", "gpsimd", "tensor")

    def __init__(self, nc, es, n_dma_sems=12):
        self.nc, self.es = nc, es
        self.ops = {e: [] for e in self.ENGS}
        self.esem = {}
        self.ecount = {}
        for e in ("scalar", "vector", "gpsimd", "tensor"):
            self.esem[e] = es.enter_context(nc.semaphore(f"sem_{e}"))
            self.ecount[e] = 0
        self.dpool = {}
        for q in ("sync", "scalar", "gpsimd"):
            self.dpool[q] = dict(
                sems=[es.enter_context(nc.semaphore(f"dsem_{q}_{i}")) for i in range(n_dma_sems)],
                cnt=[0] * n_dma_sems, nxt=0, know=[None] * n_dma_sems)
        self.semobj = {}
        self.lastw = {}
        self.readers = {}
        self.know = {e: {} for e in self.ENGS}
        self.final_tokens = []

    def _need(self, eng, tok, waits):
        sk, v, kn = tok
        if self.know[eng].get(sk, 0) >= v:
            return
        waits.append((sk, v))
        k = self.know[eng]
        for a, b in kn.items():
            if k.get(a, 0) < b:
                k[a] = b
        if k.get(sk, 0) < v:
            k[sk] = v

    def op(self, eng, fn, reads=(), writes=(), dma=False, final=False):
        waits = []
        toks = []
        own = None if dma else ("e", eng)
        for key in reads:
            t = self.lastw.get(key)
            if t is not None:
                toks.append((t, True))
        for key in writes:
            t = self.lastw.get(key)
            if t is not None:
                toks.append((t, False))
            toks.extend((r, False) for r in self.readers.get(key, ()))
        for t, raw in toks:
            if t[0] == own and (eng == "tensor" or not SELF_SYNC or (not raw and eng != "gpsimd") or (NO_RAW_SELF and eng in ("vector", "scalar"))):
                continue
            self._need(eng, t, waits)
        if dma:
            pool = self.dpool[eng]
            j = pool["nxt"]
            pool["nxt"] = (j + 1) % len(pool["sems"])
            if pool["cnt"][j] > 0:
                self._need(eng, (("d", eng, j), pool["cnt"][j], pool["know"][j]), waits)
            pool["cnt"][j] += 16
            sk = ("d", eng, j)
            self.semobj[sk] = pool["sems"][j]
            kn = dict(self.know[eng])
            pool["know"][j] = kn
            tok = (sk, pool["cnt"][j], kn)
            inc = (pool["sems"][j], 16)
        else:
            self.ecount[eng] += 1
            sk = ("e", eng)
            self.semobj[sk] = self.esem[eng]
            tok = (sk, self.ecount[eng], dict(self.know[eng]))
            inc = (self.esem[eng], 1)
        self.ops[eng].append((waits, fn, inc))
        for key in reads:
            self.readers.setdefault(key, []).append(tok)
        for key in writes:
            self.lastw[key] = tok
            self.readers[key] = []
        if final:
            self.final_tokens.append(tok)
        return tok

    def emit(self):
        waits = []
        for t in self.final_tokens:
            self._need("sync", t, waits)
        if waits:
            self.ops["sync"].append((waits, None, None))
        nc = self.nc
        with nc.Block() as block:
            def run(eng_name):
                def body(eng):
                    for waits, fn, inc in self.ops[eng_name]:
                        fused = FUSE_WAIT and fn is not None and len(waits) > 0
                        for sk, v in (waits[:-1] if fused else waits):
                            eng.wait_ge(self.semobj[sk], v)
                        if fn is not None:
                            ins = fn(eng)
                            if fused:
                                ins._wait_ge(self.semobj[waits[-1][0]], waits[-1][1])
                            ins.then_inc(inc[0], inc[1])
                return body
            block.sync(run("sync"))
            block.scalar(run("scalar"))
            block.vector(run("vector"))
            block.gpsimd(run("gpsimd"))
            block.tensor(run("tensor"))

    def dma(self, q, out, in_, reads=(), writes=(), final=False, **kw):
        return self.op(q, lambda e: e.dma_start(out=out, in_=in_, **kw), reads, writes, dma=True, final=final)

    def mm(self, out, lhsT, rhs, start, stop, reads=(), writes=()):
        return self.op("tensor", lambda e: e.matmul(out, lhsT, rhs, start=start, stop=stop), reads, writes)

    def act(self, out, in_, func, reads=(), writes=(), eng="scalar", **kw):
        return self.op(eng, lambda e: e.activation(out=out, in_=in_, func=func, **kw), reads, writes)

    def tt(self, eng, out, in0, in1, op, reads=(), writes=()):
        return self.op(eng, lambda e: e.tensor_tensor(out=out, in0=in0, in1=in1, op=op), reads, writes)

    def ts(self, eng, out, in0, s1, s2, op0, op1=None, reads=(), writes=(), accum_out=None):
        kw = {}
        if op1 is not None:
            kw["op1"] = op1
        if accum_out is not None:
            kw["accum_out"] = accum_out
        return self.op(eng, lambda e: e.tensor_scalar(out=out, in0=in0, scalar1=s1, scalar2=s2, op0=op0, **kw),
                       reads, writes)

    def stt(self, eng, out, in0, scalar, in1, op0, op1, reads=(), writes=()):
        return self.op(eng, lambda e: e.scalar_tensor_tensor(out=out, in0=in0, scalar=scalar, in1=in1,
                                                             op0=op0, op1=op1), reads, writes)

    def copy(self, eng, out, in_, reads=(), writes=()):
        if eng == "scalar":
            return self.op(eng, lambda e: e.activation(out=out, in_=in_, func=AF.Copy), reads, writes)
        return self.op(eng, lambda e: e.tensor_copy(out=out, in_=in_), reads, writes)

    def tr(self, out, in_, ident, reads=(), writes=()):
        return self.op("tensor", lambda e: e.transpose(out, in_, ident), reads, writes)

    def memset(self, eng, ap, val, writes=()):
        return self.op(eng, lambda e: e.memset(ap, val), (), writes)

    def gen(self, eng, f, reads=(), writes=()):
        return self.op(eng, f, reads, writes)


def _run(nc, in_maps):
    return run_bass_kernel_spmd(nc, in_maps, core_ids=list(range(NCORES)))


D_MODEL = 1024
DEPTH = 2
N_MOD = 6
MODC = N_MOD * D_MODEL // NCORES


def build_k0():
    nc = bass.Bass("TRN2", target_bir_lowering=False)
    cvT = nc.dram_tensor("cvT", [128, 8, 3], F32, kind="ExternalInput").ap()
    wm = nc.dram_tensor("wm", [DEPTH, 128, 8, MODC], F32, kind="ExternalInput").ap()
    bm = nc.dram_tensor("bm", [DEPTH, 3, MODC], F32, kind="ExternalInput").ap()
    out = nc.dram_tensor("mod", [DEPTH, 3, MODC], F32, kind="ExternalOutput").ap()
    with ExitStack() as es:
        P = Prog(nc, es)
        sb = lambda name, shape, dt=F32: es.enter_context(nc.sbuf_tensor(name, shape, dt))
        cv = sb("cv", [128, 8, 3])
        cs = sb("cs", [128, 8, 3])
        w = [sb(f"w{l}", [128, 8, MODC]) for l in range(DEPTH)]
        b = sb("b", [3, DEPTH, MODC])
        o = sb("o", [3, DEPTH, MODC])
        ps = [es.enter_context(nc.psum_tensor(f"ps{i}", [128, 512], F32)) for i in range(2)]
        P.dma("sync", cv[:], cvT, writes=["cv"])
        for l in range(DEPTH):
            P.dma("sync" if l == 0 else "gpsimd", w[l][:], wm[l], writes=[f"w{l}"])
            P.dma("sync", b[:, l, :], bm[l], writes=[f"b{l}"])
        P.act(cs[:], cv[:], AF.Silu, reads=["cv"], writes=["cs"])
        H = MODC // 2
        for l in range(DEPTH):
            for h in range(2):
                pt = ps[h]
                for kc in range(8):
                    P.mm(pt[0:3, 0:H], cs[:, kc, :], w[l][:, kc, h * H:(h + 1) * H], kc == 0, kc == 7,
                         reads=["cs", f"w{l}"], writes=[f"ps{h}"])
                P.op("vector", lambda e, l=l, h=h, pt=pt: e.tensor_tensor(
                    out=o[:, l, h * H:(h + 1) * H], in0=pt[0:3, 0:H], in1=b[:, l, h * H:(h + 1) * H], op=ALU.add),
                    reads=[f"ps{h}", f"b{l}"], writes=[f"o{l}{h}"])
            P.dma("sync", out[l], o[:, l, :], reads=[f"o{l}0", f"o{l}1"], final=True)
        P.emit()
    return nc


def run_k0(c, c_ctx, w_mod, b_mod):
    cv = np.concatenate([c, c_ctx[None]], 0)
    cvT = np.ascontiguousarray(cv.T.reshape(8, 128, 3).transpose(1, 0, 2))
    nc = build_k0()
    in_maps = []
    for i in range(NCORES):
        sl = slice(i * MODC, (i + 1) * MODC)
        wm = np.ascontiguousarray(w_mod[:, :, sl].reshape(DEPTH, 8, 128, MODC).transpose(0, 2, 1, 3))
        bm = np.ascontiguousarray(np.broadcast_to(b_mod[:, None, sl], (DEPTH, 3, MODC)))
        in_maps.append({"cvT": cvT, "wm": wm, "bm": bm})
    res = _run(nc, in_maps)
    return np.concatenate([r["mod"] for r in res.results], axis=-1)


class Ring:
    def __init__(self, items):
        self.items, self.i = items, 0

    def next(self):
        it = self.items[self.i % len(self.items)]
        self.i += 1
        return it


def mk_alloc(nc, es):
    def sb(name, shape, dt=F32):
        return es.enter_context(nc.sbuf_tensor(name, shape, dt))

    def ps(name, shape, dt=F32):
        return es.enter_context(nc.psum_tensor(name, shape, dt))

    def ring(name, n, shape, dt=F32, psum=False):
        return Ring([((ps if psum else sb)(f"{name}{i}", shape, dt), f"{name}{i}") for i in range(n)])
    return sb, ps, ring


def bcast_rows(ap1d, nparts):
    return bass.AP(tensor=ap1d.tensor, offset=ap1d.offset, ap=[[0, nparts]] + [list(x) for x in ap1d.ap])


LN_EPS = 1e-5


def emit_layernorm(P, x, xkey, xn, xnkey, st, mv, rs, skey, n=1024):
    for j in range(n // 512):
        P.gen("vector", lambda e, j=j: e.bn_stats(out=st[:, j, :], in_=x[:, j * 512:(j + 1) * 512]),
              reads=[xkey], writes=[skey + f"st{j}"])
    P.gen("vector", lambda e: e.bn_aggr(out=mv[:], in_=st[:]),
          reads=[skey + f"st{j}" for j in range(n // 512)], writes=[skey + "mv"])
    P.ts("vector", rs[:], mv[:, 1:2], LN_EPS, None, ALU.add, reads=[skey + "mv"], writes=[skey + "rs"])
    P.act(rs[:], rs[:], AF.Sqrt, reads=[skey + "rs"], writes=[skey + "rs"])
    P.gen("vector", lambda e: e.reciprocal(out=rs[:], in_=rs[:]), reads=[skey + "rs"], writes=[skey + "rs"])
    P.ts("vector", xn, x, mv[:, 0:1], rs[:, 0:1], ALU.subtract, ALU.mult,
         reads=[xkey, skey + "mv", skey + "rs"], writes=[xnkey])


N_IN = 2576
K1_TILES = 17


def build_k1():
    nc = bass.Bass("TRN2", target_bir_lowering=False)
    xt = nc.dram_tensor("xt", [K1_TILES, 128, 1024], F32, kind="ExternalInput").ap()
    modr = nc.dram_tensor("modr", [2, 2, 1024], F32, kind="ExternalInput").ap()
    win = nc.dram_tensor("win", [128, 8, N_IN], F32, kind="ExternalInput").ap()
    identd = nc.dram_tensor("ident", [128, 128], F32, kind="ExternalInput").ap()
    out = nc.dram_tensor("p", [K1_TILES, 128, N_IN], F32, kind="ExternalOutput").ap()
    with ExitStack() as es:
        P = Prog(nc, es)
        sb, ps, ring = mk_alloc(nc, es)
        idf = sb("idf", [128, 128])
        idb = sb("idb", [128, 128], BF16)
        P.dma("sync", idf[:], identd, writes=["idf"])
        P.copy("vector", idb[:], idf[:], reads=["idf"], writes=["idb"])
        modt = sb("modt", [128, 2, 2, 1024])
        for g in range(2):
            for j in range(2):
                P.dma("sync", modt[:, g, j, :], bcast_rows(modr[g, j], 128), writes=[f"mod{g}{j}"])
            P.ts("vector", modt[:, g, 0, :], modt[:, g, 0, :], 1.0, None, ALU.add,
                 reads=[f"mod{g}0"], writes=[f"mod{g}0"])
        wbf = sb("wbf", [128, 8, N_IN], BF16)
        wst = ring("wst", 2, [128, N_IN])
        for kc in range(8):
            t, k = wst.next()
            P.dma("gpsimd" if kc % 2 else "sync", t[:], win[:, kc, :], writes=[k])
            P.copy("gpsimd" if kc % 2 else "vector", wbf[:, kc, :], t[:], reads=[k], writes=[f"wbf{kc}"])
        wkeys = [f"wbf{kc}" for kc in range(8)]
        xr = ring("x", 2, [128, 1024])
        xnr = ring("xn", 2, [128, 1024])
        h1r = ring("h1", 2, [128, 1024])
        hr = ring("h", 2, [128, 1024], BF16)
        hTr = ring("hT", 2, [128, 1024], BF16)
        orr = ring("o", 2, [128, N_IN])
        str_ = ring("st", 2, [128, 2, 6])
        mvr = ring("mv", 2, [128, 2])
        rsr = ring("rs", 2, [128, 1])
        pTr = ring("pT", 2, [128, 1024], BF16, psum=True)
        pmr = ring("pm", 4, [128, 512], F32, psum=True)
        ev = 0
        for t in range(K1_TILES):
            g = 0 if t < 16 else 1
            x, xk = xr.next()
            P.dma("sync", x[:], xt[t], writes=[xk])
            xn, xnk = xnr.next()
            st, _ = str_.next(); mv, _ = mvr.next(); rs, _ = rsr.next()
            emit_layernorm(P, x[:], xk, xn[:], xnk, st, mv, rs, f"ln{t % 2}")
            h1, h1k = h1r.next()
            P.tt("gpsimd", h1[:], xn[:], modt[:, g, 0, :], ALU.mult, reads=[xnk, f"mod{g}0"], writes=[h1k])
            h, hk = hr.next()
            P.tt("gpsimd", h[:], h1[:], modt[:, g, 1, :], ALU.add, reads=[h1k, f"mod{g}1"], writes=[hk])
            pT, pTk = pTr.next()
            for kc in range(8):
                P.tr(pT[:, kc * 128:(kc + 1) * 128], h[:, kc * 128:(kc + 1) * 128], idb[:],
                     reads=[hk, "idb"], writes=[pTk])
            hT, hTk = hTr.next()
            P.copy("scalar", hT[:], pT[:], reads=[pTk], writes=[hTk])
            o, ok = orr.next()
            for cg in range(6):
                c0 = cg * 512
                n = min(512, N_IN - c0)
                pm, pmk = pmr.next()
                for kc in range(8):
                    P.mm(pm[:, 0:n], hT[:, kc * 128:(kc + 1) * 128], wbf[:, kc, c0:c0 + n], kc == 0, kc == 7,
                         reads=[hTk, wkeys[kc]], writes=[pmk])
                P.copy("scalar" if ev % 2 else "vector", o[:, c0:c0 + n], pm[:, 0:n], reads=[pmk], writes=[ok + f"c{cg}"])
                ev += 1
            P.dma("gpsimd", out[t], o[:], reads=[ok + f"c{cg}" for cg in range(6)], writes=[ok + "dma"], final=True)
        P.emit()
    return nc


def lay_w(w, kchunks):
    return np.ascontiguousarray(w.reshape(kchunks, 128, w.shape[1]).transpose(1, 0, 2))


def run_k1(x, ctx, mod_l, w_in_l):
    nc = build_k1()
    B, n, D = x.shape
    seg = n // 4
    ctxf = ctx.reshape(-1, D)
    ident = np.eye(128, dtype=np.float32)
    win = lay_w(w_in_l, 8)
    in_maps = []
    for i in range(NCORES):
        b, s = i // 4, i % 4
        xt = np.zeros((K1_TILES * 128, D), np.float32)
        xt[:seg] = x[b, s * seg:(s + 1) * seg]
        xt[seg:seg + 64] = ctxf[i * 64:(i + 1) * 64]
        modr = np.stack([np.stack([mod_l[b, 1024:2048], mod_l[b, 0:1024]]),
                         np.stack([mod_l[2, 1024:2048], mod_l[2, 0:1024]])])
        in_maps.append({"xt": xt.reshape(K1_TILES, 128, D), "modr": np.ascontiguousarray(modr), "win": win,
                        "ident": ident})
    res = _run(nc, in_maps)
    P_lat = np.zeros((B, n, N_IN), np.float32)
    P_ctx = np.zeros((B * ctx.shape[1], N_IN), np.float32)
    for i in range(NCORES):
        b, s = i // 4, i % 4
        p = res.results[i]["p"].reshape(K1_TILES * 128, N_IN)
        P_lat[b, s * seg:(s + 1) * seg] = p[:seg]
        P_ctx[i * 64:(i + 1) * 64] = p[seg:seg + 64]
    return P_lat, P_ctx.reshape(B, ctx.shape[1], N_IN)


ALPHA = (2.0 * DEPTH) ** 0.25
N_EXP = 16


def build_k5():
    nc = bass.Bass("TRN2", target_bir_lowering=False)
    T = K1_TILES
    yt = nc.dram_tensor("yt", [T, 128, 1024], F32, kind="ExternalInput").ap()
    xt = nc.dram_tensor("xt", [T, 128, 1024], F32, kind="ExternalInput").ap()
    wout = nc.dram_tensor("wout", [128, 8, 1024], F32, kind="ExternalInput").ap()
    rows = nc.dram_tensor("rows", [2, 3, 1024], F32, kind="ExternalInput").ap()
    lnr = nc.dram_tensor("lnr", [2, 1024], F32, kind="ExternalInput").ap()
    rwd = nc.dram_tensor("rw", [128, 8, N_EXP], F32, kind="ExternalInput").ap()
    rbd = nc.dram_tensor("rb", [N_EXP], F32, kind="ExternalInput").ap()
    identd = nc.dram_tensor("ident", [128, 128], F32, kind="ExternalInput").ap()
    x1o = nc.dram_tensor("o_x1", [T, 128, 1024], F32, kind="ExternalOutput").ap()
    h2o = nc.dram_tensor("o_h2", [T, 128, 1024], BF16, kind="ExternalOutput").ap()
    affo = nc.dram_tensor("o_aff", [T, 128, N_EXP], F32, kind="ExternalOutput").ap()
    with ExitStack() as es:
        P = Prog(nc, es)
        sb, ps, ring = mk_alloc(nc, es)
        idf = sb("idf", [128, 128])
        idb = sb("idb", [128, 128], BF16)
        P.dma("sync", idf[:], identd, writes=["idf"])
        P.copy("vector", idb[:], idf[:], reads=["idf"], writes=["idb"])
        rowt = sb("rowt", [128, 2, 3, 1024])
        for g in range(2):
            for j in range(3):
                P.dma("sync", rowt[:, g, j, :], bcast_rows(rows[g, j], 128), writes=[f"row{g}{j}"])
            P.ts("vector", rowt[:, g, 1, :], rowt[:, g, 1, :], 1.0, None, ALU.add,
                 reads=[f"row{g}1"], writes=[f"row{g}1"])
        lnt = sb("lnt", [128, 2, 1024])
        for j in range(2):
            P.dma("sync", lnt[:, j, :], bcast_rows(lnr[j], 128), writes=[f"ln{j}"])
        rw = sb("rwt", [128, 8, N_EXP])
        P.dma("sync", rw[:], rwd, writes=["rw"])
        rb = sb("rbt", [128, N_EXP])
        P.dma("sync", rb[:], bcast_rows(rbd, 128), writes=["rb"])
        wbf = sb("wbf", [128, 8, 1024], BF16)
        wst = ring("wst", 2, [128, 1024])
        for kc in range(8):
            t, k = wst.next()
            P.dma("gpsimd" if kc % 2 else "sync", t[:], wout[:, kc, :], writes=[k])
            P.copy("gpsimd" if kc % 2 else "vector", wbf[:, kc, :], t[:], reads=[k], writes=[f"wbf{kc}"])
        wkeys = [f"wbf{kc}" for kc in range(8)]
        yr = ring("y", 2, [128, 1024]); ybr = ring("yb", 2, [128, 1024], BF16)
        yTr = ring("yT", 2, [128, 1024], BF16)
        xr = ring("x", 2, [128, 1024]); tmpr = ring("tmp", 2, [128, 1024]); rr = ring("r", 3, [128, 1024])
        xnr = ring("xn", 2, [128, 1024]); x1r = ring("x1_", 2, [128, 1024]); x1ar = ring("x1a", 2, [128, 1024])
        xn2r = ring("xn2", 2, [128, 1024]); h2fr = ring("h2f", 2, [128, 1024]); h2ar = ring("h2a", 2, [128, 1024])
        h2br = ring("h2b", 2, [128, 1024], BF16)
        h2Tr = ring("h2T", 2, [128, 1024])
        str_ = ring("st", 4, [128, 2, 6]); mvr = ring("mv", 4, [128, 2]); rsr = ring("rs", 4, [128, 1])
        lgr = ring("lg", 2, [128, N_EXP]); exr = ring("ex", 2, [128, N_EXP]); afr = ring("af", 2, [128, N_EXP])
        smr = ring("sm", 2, [128, 4])
        pTr = ring("pT", 1, [128, 1024], BF16, psum=True)
        pmr = ring("pm", 2, [128, 512], F32, psum=True)
        pTfr = ring("pTf", 1, [128, 1024], F32, psum=True)
        plr = ring("pl", 1, [128, N_EXP], F32, psum=True)
        lnc = 0
        lnc_box = [0]

        def stage_a(t):
            g = 0 if t < 16 else 1
            y, yk = yr.next(); P.dma("sync", y[:], yt[t], writes=[yk])
            x, xk = xr.next(); P.dma("sync", x[:], xt[t], writes=[xk])
            yb, ybk = ybr.next(); P.copy("gpsimd", yb[:], y[:], reads=[yk], writes=[ybk])
            pT, pTk = pTr.next()
            for kc in range(8):
                P.tr(pT[:, kc * 128:(kc + 1) * 128], yb[:, kc * 128:(kc + 1) * 128], idb[:], reads=[ybk, "idb"], writes=[pTk])
            yT, yTk = yTr.next(); P.copy("scalar", yT[:], pT[:], reads=[pTk], writes=[yTk])
            tmp, tmpk = tmpr.next()
            for hf in range(2):
                pm, pmk = pmr.next()
                for kc in range(8):
                    P.mm(pm[:], yT[:, kc * 128:(kc + 1) * 128], wbf[:, kc, hf * 512:(hf + 1) * 512], kc == 0, kc == 7,
                         reads=[yTk, wkeys[kc]], writes=[pmk])
                P.tt("vector", tmp[:, hf * 512:(hf + 1) * 512], pm[:], rowt[:, g, 0, hf * 512:(hf + 1) * 512], ALU.mult,
                     reads=[pmk, f"row{g}0"], writes=[tmpk + str(hf)])
            r, rk = rr.next()
            P.stt("vector", r[:], x[:], ALPHA, tmp[:], ALU.mult, ALU.add, reads=[xk, tmpk + "0", tmpk + "1"], writes=[rk])
            return r, rk

        def stage_b(t, r, rk):
            g = 0 if t < 16 else 1
            lnc = lnc_box[0]
            xn, xnk = xnr.next(); st, _ = str_.next(); mv, _ = mvr.next(); rs, _ = rsr.next()
            emit_layernorm(P, r[:], rk, xn[:], xnk, st, mv, rs, f"lnA{lnc % 4}"); lnc += 1
            x1a, x1ak = x1ar.next(); x1, x1k = x1r.next()
            P.tt("gpsimd", x1a[:], xn[:], lnt[:, 0, :], ALU.mult, reads=[xnk, "ln0"], writes=[x1ak])
            P.tt("gpsimd", x1[:], x1a[:], lnt[:, 1, :], ALU.add, reads=[x1ak, "ln1"], writes=[x1k])
            P.dma("gpsimd", x1o[t], x1[:], reads=[x1k], writes=[x1k + "d"], final=True)
            xn2, xn2k = xn2r.next(); st, _ = str_.next(); mv, _ = mvr.next(); rs, _ = rsr.next()
            emit_layernorm(P, x1[:], x1k, xn2[:], xn2k, st, mv, rs, f"lnA{lnc % 4}"); lnc += 1
            h2a, h2ak = h2ar.next(); h2f, h2fk = h2fr.next(); h2b, h2bk = h2br.next()
            P.tt("gpsimd", h2a[:], xn2[:], rowt[:, g, 1, :], ALU.mult, reads=[xn2k, f"row{g}1"], writes=[h2ak])
            P.tt("vector", h2f[:], h2a[:], rowt[:, g, 2, :], ALU.add, reads=[h2ak, f"row{g}2"], writes=[h2fk])
            P.copy("scalar", h2b[:], h2f[:], reads=[h2fk], writes=[h2bk])
            P.dma("sync", h2o[t], h2b[:], reads=[h2bk], writes=[h2bk + "d"], final=True)
            pTf, pTfk = pTfr.next()
            for kc in range(8):
                P.tr(pTf[:, kc * 128:(kc + 1) * 128], h2f[:, kc * 128:(kc + 1) * 128], idf[:], reads=[h2fk, "idf"], writes=[pTfk])
            h2T, h2Tk = h2Tr.next()
            P.copy("scalar", h2T[:, 0:512], pTf[:, 0:512], reads=[pTfk], writes=[h2Tk + "a"])
            P.copy("vector", h2T[:, 512:1024], pTf[:, 512:1024], reads=[pTfk], writes=[h2Tk + "b"])
            pl, plk = plr.next()
            for kc in range(8):
                P.mm(pl[:], h2T[:, kc * 128:(kc + 1) * 128], rw[:, kc, :], kc == 0, kc == 7,
                     reads=[h2Tk + "a", h2Tk + "b", "rw"], writes=[plk])
            lg, lgk = lgr.next(); ex, exk = exr.next(); af, afk = afr.next(); sm, smk = smr.next()
            P.tt("vector", lg[:], pl[:], rb[:], ALU.add, reads=[plk, "rb"], writes=[lgk])
            P.gen("vector", lambda e, sm=sm, lg=lg: e.reduce_max(out=sm[:, 0:1], in_=lg[:], axis=AX.X), reads=[lgk], writes=[smk + "m"])
            P.ts("vector", sm[:, 1:2], sm[:, 0:1], -1.0, None, ALU.mult, reads=[smk + "m"], writes=[smk + "n"])
            P.act(ex[:], lg[:], AF.Exp, reads=[lgk, smk + "n"], writes=[exk, smk + "s"], bias=sm[:, 1:2], scale=1.0,
                  accum_out=sm[:, 2:3])
            P.gen("vector", lambda e, sm=sm: e.reciprocal(out=sm[:, 3:4], in_=sm[:, 2:3]), reads=[smk + "s"], writes=[smk + "r"])
            P.ts("vector", af[:], ex[:], sm[:, 3:4], None, ALU.mult, reads=[exk, smk + "r"], writes=[afk])
            P.dma("gpsimd", affo[t], af[:], reads=[afk], writes=[afk + "d"], final=True)
            lnc_box[0] = lnc

        held = {}
        for t in range(T + 1):
            if t < T:
                held[t] = stage_a(t)
            if t >= 1:
                stage_b(t - 1, *held.pop(t - 1))

        P.emit()
    return nc


def tok_shard(lat, ctx, i):
    B, n, D = lat.shape
    seg = n // 4
    b, s = i // 4, i % 4
    out = np.zeros((K1_TILES * 128, D), lat.dtype)
    out[:seg] = lat[b, s * seg:(s + 1) * seg]
    out[seg:seg + 64] = ctx.reshape(-1, D)[i * 64:(i + 1) * 64]
    return out.reshape(K1_TILES, 128, D)


def tok_unshard(parts, B, n, nctx):
    D = parts[0].shape[-1]
    seg = n // 4
    lat = np.zeros((B, n, D), parts[0].dtype)
    ctx = np.zeros((B * nctx, D), parts[0].dtype)
    for i in range(NCORES):
        b, s = i // 4, i % 4
        p = parts[i].reshape(K1_TILES * 128, D)
        lat[b, s * seg:(s + 1) * seg] = p[:seg]
        ctx[i * 64:(i + 1) * 64] = p[seg:seg + 64]
    return lat, ctx.reshape(B, nctx, D)


def run_k5(ycat_l, ycat_c, x, ctx, mod_l, w_out_l, ln_w, ln_b, router_w_l, router_b_l):
    nc = build_k5()
    B, n, D = x.shape
    ident = np.eye(128, dtype=np.float32)
    wout = lay_w(w_out_l, 8)
    rw = lay_w(router_w_l, 8)
    lnr = np.ascontiguousarray(np.stack([ln_w, ln_b]))
    in_maps = []
    for i in range(NCORES):
        b = i // 4
        rows = np.stack([np.stack([mod_l[m, 2048:3072], mod_l[m, 4096:5120], mod_l[m, 3072:4096]]) for m in (b, 2)])
        in_maps.append({"yt": tok_shard(ycat_l, ycat_c, i), "xt": tok_shard(x, ctx, i), "wout": wout,
                        "rows": np.ascontiguousarray(rows), "lnr": lnr, "rw": rw,
                        "rb": np.ascontiguousarray(router_b_l), "ident": ident})
    res = _run(nc, in_maps)
    nctx = ctx.shape[1]
    x1, c1 = tok_unshard([r["o_x1"] for r in res.results], B, n, nctx)
    h2, h2c = tok_unshard([r["o_h2"] for r in res.results], B, n, nctx)
    aff, affc = tok_unshard([r["o_aff"] for r in res.results], B, n, nctx)
    return x1, c1, h2, h2c, aff, affc


K6_ITERS = 26


def build_k6(F_lat, k_lat, F_ctx, k_ctx):
    nc = bass.Bass("TRN2", target_bir_lowering=False)
    R = 32
    ald = nc.dram_tensor("al", [R, F_lat], F32, kind="ExternalInput").ap()
    acd = nc.dram_tensor("ac", [R, F_ctx], F32, kind="ExternalInput").ap()
    thro = nc.dram_tensor("thr", [R, 2], F32, kind="ExternalOutput").ap()
    with ExitStack() as es:
        P = Prog(nc, es)
        sb, ps, ring = mk_alloc(nc, es)
        res = sb("res", [R, 2])
        for pi, (src, F, k) in enumerate(((ald, F_lat, k_lat), (acd, F_ctx, k_ctx))):
            A = sb(f"A{pi}", [R, F])
            junk = sb(f"junk{pi}", [R, F], BF16)
            sc = sb(f"sc{pi}", [R, 8])
            lo, hi, mid, cnt, cond, t1, t2 = (sc[:, j:j + 1] for j in range(7))
            kk = f"p{pi}"
            P.dma("sync", A[:], src, writes=[kk + "A"])
            P.memset("vector", lo, 0.0, writes=[kk + "lo"])
            P.memset("vector", hi, 1.0, writes=[kk + "hi"])
            for it in range(K6_ITERS):
                P.tt("vector", mid, lo, hi, ALU.add, reads=[kk + "lo", kk + "hi"], writes=[kk + "mid"])
                P.ts("vector", mid, mid, 0.5, None, ALU.mult, reads=[kk + "mid"], writes=[kk + "mid"])
                P.ts("vector", junk[:], A[:], mid, None, ALU.is_ge, ALU.add, reads=[kk + "A", kk + "mid"],
                     writes=[kk + "junk", kk + "cnt"], accum_out=cnt)
                P.ts("vector", cond, cnt, float(k) - 0.5, None, ALU.is_ge, reads=[kk + "cnt"], writes=[kk + "cond"])
                P.tt("vector", t1, cond, mid, ALU.mult, reads=[kk + "cond", kk + "mid"], writes=[kk + "t1"])
                P.stt("vector", t2, cond, 2.0, mid, ALU.mult, ALU.add, reads=[kk + "cond", kk + "mid"], writes=[kk + "t2"])
                P.tt("vector", lo, lo, t1, ALU.max, reads=[kk + "lo", kk + "t1"], writes=[kk + "lo"])
                P.tt("vector", hi, hi, t2, ALU.min, reads=[kk + "hi", kk + "t2"], writes=[kk + "hi"])
            P.copy("vector", res[:, pi:pi + 1], lo, reads=[kk + "lo"], writes=[f"res{pi}"])
        P.dma("sync", thro, res[:], reads=["res0", "res1"], final=True)
        P.emit()
    return nc


def run_k6(aff, affc):
    B, n, E = aff.shape
    ncx = affc.shape[1]
    nc = build_k6(n, 2 * n // E, ncx, 2 * ncx // E)
    al = np.ascontiguousarray(aff.transpose(0, 2, 1).reshape(B * E, n))
    ac = np.ascontiguousarray(affc.transpose(0, 2, 1).reshape(B * E, ncx))
    res = _run(nc, [{"al": al, "ac": ac} for _ in range(NCORES)])
    thr = res.results[0]["thr"]
    return thr[:, 0].reshape(B, E), thr[:, 1].reshape(B, E)


D_FF = 2816
NFC = D_FF // 128
K7_TOK = K1_TILES * 128


def build_k7(n_exp=N_EXP):
    nc = bass.Bass("TRN2", target_bir_lowering=False)
    T = K1_TILES
    h2Td = nc.dram_tensor("h2T", [128, 8, K7_TOK], BF16, kind="ExternalInput").ap()
    affd = nc.dram_tensor("aff", [T, 128, N_EXP], F32, kind="ExternalInput").ap()
    thrd = nc.dram_tensor("thr", [2, N_EXP], F32, kind="ExternalInput").ap()
    x1d = nc.dram_tensor("i_x1", [T, 128, 1024], F32, kind="ExternalInput").ap()
    rows = nc.dram_tensor("rows", [2, 1024], F32, kind="ExternalInput").ap()
    lnr = nc.dram_tensor("lnr", [2, 1024], F32, kind="ExternalInput").ap()
    wgd = nc.dram_tensor("wg", [n_exp, 128, 8, D_FF], F32, kind="ExternalInput").ap()
    wud = nc.dram_tensor("wu", [n_exp, 128, 8, D_FF], F32, kind="ExternalInput").ap()
    wdd = nc.dram_tensor("wd", [n_exp, 128, NFC, 1024], F32, kind="ExternalInput").ap()
    x2o = nc.dram_tensor("o_x2", [T, 128, 1024], F32, kind="ExternalOutput").ap()
    with ExitStack() as es:
        P = Prog(nc, es)
        sb, ps, ring = mk_alloc(nc, es)
        h2T = sb("h2Ts", [128, 8, K7_TOK], BF16)
        for kc in range(8):
            P.dma("sync", h2T[:, kc, :], h2Td[:, kc, :], writes=["h2T"] if kc == 7 else [f"h2T_{kc}"])
        h2keys = ["h2T"] + [f"h2T_{kc}" for kc in range(7)]
        thrb = sb("thrb", [128, 2, N_EXP])
        for g in range(2):
            P.dma("sync", thrb[:, g, :], bcast_rows(thrd[g], 128), writes=[f"thr{g}"])
        wgt = sb("wgt", [128, T, N_EXP])
        msk = sb("msk", [128, T, N_EXP])
        for t in range(T):
            g = 0 if t < 16 else 1
            P.dma("sync", wgt[:, t, :], affd[t], writes=[f"aff{t}"])
            P.tt("vector", msk[:, t, :], wgt[:, t, :], thrb[:, g, :], ALU.is_ge, reads=[f"aff{t}", f"thr{g}"], writes=[f"msk{t}"])
            P.tt("vector", wgt[:, t, :], wgt[:, t, :], msk[:, t, :], ALU.mult, reads=[f"aff{t}", f"msk{t}"], writes=[f"aff{t}"])
        acc = sb("acc", [128, T, 1024])
        for t in range(T):
            P.memset("gpsimd", acc[:, t, :], 0.0, writes=[f"acc{t}a", f"acc{t}b"])
        FG = 4
        wgr = ring("wgb", 2, [128, 8, FG * 128], BF16)
        wur = ring("wub", 2, [128, 8, FG * 128], BF16)
        wdr = ring("wdb", 2, [128, FG, 1024], BF16)
        stg = ring("stg", 4, [128, 1024])
        actr = ring("actT", 2, [128, FG, 512], BF16)
        sgr = ring("sg", 2, [128, 512])
        pgr = ring("pg", 2, [128, 512], F32, psum=True)
        pur = ring("pu", 2, [128, 512], F32, psum=True)
        pyr = ring("py", 2, [128, 512], F32, psum=True)
        tgs = [(s, min(512, K7_TOK - s)) for s in range(0, K7_TOK, 512)]
        ci = 0
        for e in range(n_exp):
            for f0 in range(0, NFC, FG):
                nf = min(FG, NFC - f0)
                wgb, wgk = wgr.next(); wub, wuk = wur.next(); wdb, wdk = wdr.next()
                for (src, dst, dk) in ((wgd, wgb, wgk), (wud, wub, wuk)):
                    for kp in range(0, 8, 2):
                        st, sk = stg.next()
                        q = "sync"
                        sv = st[:, 0:2 * nf * 128].rearrange("p (a f) -> p a f", a=2)
                        P.dma(q, sv, src[e, :, kp:kp + 2, f0 * 128:(f0 + nf) * 128], writes=[sk])
                        P.copy("gpsimd", dst[:, kp:kp + 2, 0:nf * 128], sv, reads=[sk], writes=[dk + f"k{kp}"])
                        ci += 1
                for fc in range(nf):
                    st, sk = stg.next()
                    P.dma("sync", st[:], wdd[e, :, f0 + fc, :], writes=[sk])
                    P.copy("gpsimd", wdb[:, fc, :], st[:], reads=[sk], writes=[wdk + f"f{fc}"])
                    ci += 1
                for (s0, ns) in tgs:
                    actT, ak = actr.next()
                    for fc in range(nf):
                        pg, pgk = pgr.next(); pu, puk = pur.next()
                        for kc in range(8):
                            P.mm(pg[:, 0:ns], wgb[:, kc, fc * 128:(fc + 1) * 128], h2T[:, kc, s0:s0 + ns], kc == 0, kc == 7,
                                 reads=h2keys + [wgk + f"k{kc - kc % 2}"], writes=[pgk])
                        for kc in range(8):
                            P.mm(pu[:, 0:ns], wub[:, kc, fc * 128:(fc + 1) * 128], h2T[:, kc, s0:s0 + ns], kc == 0, kc == 7,
                                 reads=h2keys + [wuk + f"k{kc - kc % 2}"], writes=[puk])
                        sg, sgk = sgr.next()
                        P.act(sg[:, 0:ns], pg[:, 0:ns], AF.Silu, reads=[pgk], writes=[sgk])
                        P.tt("vector", actT[:, fc, 0:ns], pu[:, 0:ns], sg[:, 0:ns], ALU.mult, reads=[puk, sgk], writes=[ak + f"f{fc}"])
                    for tt in range(s0 // 128, (s0 + ns) // 128):
                        for hf in range(2):
                            py, pyk = pyr.next()
                            for fc in range(nf):
                                P.mm(py[:], actT[:, fc, tt * 128 - s0:(tt + 1) * 128 - s0], wdb[:, fc, hf * 512:(hf + 1) * 512],
                                     fc == 0, fc == nf - 1, reads=[ak + f"f{fc}", wdk + f"f{fc}"], writes=[pyk])
                            ah = f"acc{tt}" + "ab"[hf]
                            P.stt("vector", acc[:, tt, hf * 512:(hf + 1) * 512], py[:], wgt[:, tt, e:e + 1],
                                  acc[:, tt, hf * 512:(hf + 1) * 512], ALU.mult, ALU.add, reads=[pyk, f"aff{tt}", ah], writes=[ah])
        rowt = sb("rowt", [128, 2, 1024])
        lnt = sb("lnt", [128, 2, 1024])
        for j in range(2):
            P.dma("sync", rowt[:, j, :], bcast_rows(rows[j], 128), writes=[f"row{j}"])
            P.dma("sync", lnt[:, j, :], bcast_rows(lnr[j], 128), writes=[f"ln{j}"])
        xr = ring("x", 2, [128, 1024])
        str_ = ring("st", 2, [128, 2, 6]); mvr = ring("mv", 2, [128, 2]); rsr = ring("rs", 2, [128, 1])
        for t in range(T):
            g = 0 if t < 16 else 1
            ak2 = [f"acc{t}a", f"acc{t}b"]
            x, xk = xr.next(); P.dma("sync", x[:], x1d[t], writes=[xk])
            P.tt("gpsimd", acc[:, t, :], acc[:, t, :], rowt[:, g, :], ALU.mult, reads=ak2 + [f"row{g}"], writes=ak2)
            P.stt("vector", acc[:, t, :], x[:], ALPHA, acc[:, t, :], ALU.mult, ALU.add, reads=[xk] + ak2, writes=ak2)
            st, _ = str_.next(); mv, _ = mvr.next(); rs, _ = rsr.next()
            emit_layernorm(P, acc[:, t, :], ak2[0], x[:], xk, st, mv, rs, f"lnB{t % 2}")
            P.tt("gpsimd", acc[:, t, :], x[:], lnt[:, 0, :], ALU.mult, reads=[xk, "ln0"], writes=ak2)
            P.tt("gpsimd", x[:], acc[:, t, :], lnt[:, 1, :], ALU.add, reads=ak2 + ["ln1"], writes=[xk])
            P.dma("sync", x2o[t], x[:], reads=[xk], writes=[xk], final=True)
        P.emit()
    return nc


def run_k7(h2, h2c, aff, affc, thr, thrc, x1, c1, mod_l, ln_w, ln_b, wg, wu, wd, n_exp=N_EXP):
    nc = build_k7(n_exp)
    B, n, D = x1.shape
    nctx = c1.shape[1]
    wgl = np.ascontiguousarray(wg[:n_exp].reshape(n_exp, 8, 128, D_FF).transpose(0, 2, 1, 3))
    wul = np.ascontiguousarray(wu[:n_exp].reshape(n_exp, 8, 128, D_FF).transpose(0, 2, 1, 3))
    wdl = np.ascontiguousarray(wd[:n_exp].reshape(n_exp, NFC, 128, D).transpose(0, 2, 1, 3))
    lnr = np.ascontiguousarray(np.stack([ln_w, ln_b]))
    in_maps = []
    for i in range(NCORES):
        b = i // 4
        h2s = tok_shard(h2, h2c, i).reshape(K7_TOK, D)
        h2T = np.ascontiguousarray(h2s.T.reshape(8, 128, K7_TOK).transpose(1, 0, 2))
        rows = np.ascontiguousarray(np.stack([mod_l[b, 5120:6144], mod_l[2, 5120:6144]]))
        in_maps.append({"h2T": h2T, "aff": tok_shard(aff, affc, i), "thr": np.ascontiguousarray(np.stack([thr[b], thrc[b]])),
                        "i_x1": tok_shard(x1, c1, i), "rows": rows, "lnr": lnr, "wg": wgl, "wu": wul, "wd": wdl})
    res = _run(nc, in_maps)
    return tok_unshard([r["o_x2"] for r in res.results], B, n, nctx)


RMS_EPS = 1e-6
NKEY = 8192 + 256
NKT = NKEY // 128
QCOLS = K1_TILES * 512


def build_k3():
    nc = bass.Bass("TRN2", target_bir_lowering=False)
    T = K1_TILES
    qd = nc.dram_tensor("qT", [128, QCOLS], F32, kind="ExternalInput").ap()
    kd = nc.dram_tensor("kT", [128, NKEY], F32, kind="ExternalInput").ap()
    vd = nc.dram_tensor("v", [128, NKT, 2, 64], F32, kind="ExternalInput").ap()
    cqd = nc.dram_tensor("cosq", [128, 16 * 512], F32, kind="ExternalInput").ap()
    sqd = nc.dram_tensor("sinq", [128, 16 * 512], F32, kind="ExternalInput").ap()
    ckd = nc.dram_tensor("cosk", [128, 8192], F32, kind="ExternalInput").ap()
    skd = nc.dram_tensor("sink", [128, 8192], F32, kind="ExternalInput").ap()
    cst = nc.dram_tensor("cst", [128, 258], F32, kind="ExternalInput").ap()
    yo = nc.dram_tensor("o_ya", [T, 128, 512], F32, kind="ExternalOutput").ap()
    with ExitStack() as es:
        P = Prog(nc, es)
        sb, ps, ring = mk_alloc(nc, es)
        cs = sb("cs", [128, 258])
        P.dma("sync", cs[:], cst, writes=["cs"])
        Rm, onesb, qw2, kw2 = cs[:, 0:128], cs[:, 128:256], cs[:, 256:257], cs[:, 257:258]
        bq = sb("bq", [128, 2])
        P.memset("vector", bq[:, 0:1], 64.0 * RMS_EPS, writes=["bq0"])
        P.memset("vector", bq[:, 1:2], RMS_EPS, writes=["bq1"])
        qr = sb("qr", [128, QCOLS], BF16)
        kr = sb("kr", [128, NKEY], BF16)
        vst = ring("vst", 2, [128, 2, 64])
        vaug = sb("vaug", [128, NKT, 2, 65], BF16)
        P.memset("vector", vaug[:], 1.0, writes=["vaug_init"])
        for kt in range(NKT):
            v, vk = vst.next()
            P.dma("sync", v[:], vd[:, kt], writes=[vk])
            P.copy("vector", vaug[:, kt, :, 0:64], v[:], reads=[vk, "vaug_init"], writes=[f"vaug{kt}"])
        xr = ring("px", 2, [128, 512]); sqr = ring("psq", 2, [128, 512]); sdr = ring("psd", 2, [128, 512])
        xnr = ring("pxn", 2, [128, 512]); cr = ring("pc", 2, [128, 512]); sr = ring("psn", 2, [128, 512])
        t1r = ring("pt1", 2, [128, 512]); t2r = ring("pt2", 2, [128, 512])
        bank = [(ps(f"mb{i}", [128, 512], F32), f"mb{i}") for i in range(8)]
        pssr = Ring(bank[0:1])
        prot = Ring(bank[1:2])

        def prep(src, dst, dkey, ncols, nrope, w2, bcol, scale, cosd, sind):
            for c0 in range(0, ncols, 512):
                n = min(512, ncols - c0)
                x, xk = xr.next(); P.dma("sync", x[:, 0:n], src[:, c0:c0 + n], writes=[xk])
                sq, sqk = sqr.next(); P.tt("vector", sq[:, 0:n], x[:, 0:n], x[:, 0:n], ALU.mult, reads=[xk], writes=[sqk])
                pss, pssk = pssr.next()
                P.mm(pss[:, 0:n], onesb, sq[:, 0:n], True, True, reads=[sqk, "cs"], writes=[pssk])
                sd, sdk = sdr.next()
                P.act(sd[:, 0:n], pss[:, 0:n], AF.Sqrt, reads=[pssk, f"bq{bcol}"], writes=[sdk], bias=bq[:, bcol:bcol + 1], scale=scale)
                P.gen("vector", lambda e, sd=sd, n=n: e.reciprocal(out=sd[:, 0:n], in_=sd[:, 0:n]), reads=[sdk], writes=[sdk])
                xn, xnk = xnr.next()
                P.stt("vector", xn[:, 0:n], x[:, 0:n], w2, sd[:, 0:n], ALU.mult, ALU.mult, reads=[xk, sdk, "cs"], writes=[xnk])
                dk = f"{dkey}{c0 // 512}"
                if c0 < nrope:
                    pr, prk = prot.next()
                    P.mm(pr[:, 0:n], Rm, xn[:, 0:n], True, True, reads=[xnk, "cs"], writes=[prk])
                    c, ck = cr.next(); P.dma("sync", c[:, 0:n], cosd[:, c0:c0 + n], writes=[ck])
                    s, sk = sr.next(); P.dma("sync", s[:, 0:n], sind[:, c0:c0 + n], writes=[sk])
                    t1, t1k = t1r.next(); P.tt("vector", t1[:, 0:n], xn[:, 0:n], c[:, 0:n], ALU.mult, reads=[xnk, ck], writes=[t1k])
                    t2, t2k = t2r.next(); P.tt("vector", t2[:, 0:n], pr[:, 0:n], s[:, 0:n], ALU.mult, reads=[prk, sk], writes=[t2k])
                    P.tt("vector", dst[:, c0:c0 + n], t1[:, 0:n], t2[:, 0:n], ALU.add, reads=[t1k, t2k], writes=[dk])
                else:
                    P.copy("vector", dst[:, c0:c0 + n], xn[:, 0:n], reads=[xnk], writes=[dk])

        prep(kd, kr, "kr", NKEY, 8192, kw2, 1, 1.0 / 64.0, ckd, skd)
        prep(qd, qr, "qr", QCOLS, 16 * 512, qw2, 0, 1.0, cqd, sqd)
        LA = 1
        pstR = Ring(bank[0:4])
        accbank = {(par, g): bank[4 + par * 2 + g] for par in range(2) for g in range(2)}
        ptr = ring("PT", 6, [128, 512], BF16)
        yr = ring("yo", 2, [128, 512])
        rcr = ring("rc", 2, [128, 4])
        its = []
        for qt in range(T):
            kts = list(range(NKT)) if qt < 16 else [64, 65]
            for ii, kt in enumerate(kts):
                its.append((qt, ii, kt, len(kts)))
        pend = {}
        ycur = {}
        for idx in range(len(its) + LA):
            if idx < len(its):
                qt, ii, kt, nk = its[idx]
                pair = []
                for g in range(2):
                    pst, pstk = pstR.next()
                    P.mm(pst[:], kr[g * 64:(g + 1) * 64, kt * 128:(kt + 1) * 128], qr[g * 64:(g + 1) * 64, qt * 512:(qt + 1) * 512],
                         True, True, reads=[f"kr{kt // 4}", f"qr{qt}"], writes=[pstk])
                    pair.append((pst, pstk))
                pend[idx] = pair
            j0 = idx - LA
            if j0 < 0:
                continue
            qt, ii, kt, nk = its[j0]
            pair = pend.pop(j0)
            PTs = []
            for g in range(2):
                pst, pstk = pair[g]
                PT, PTk = ptr.next()
                P.act(PT[:], pst[:], AF.Exp, reads=[pstk], writes=[PTk])
                PTs.append((PT, PTk))
            for g in range(2):
                PT, PTk = PTs[g]
                pb, pbk = accbank[(qt % 2, g)]
                for j in range(4):
                    P.op("tensor", lambda e, pb=pb, PT=PT, j=j, kt=kt, g=g, first=(ii == 0 and j == 0), last=(ii == nk - 1): e.matmul(
                        pb[:, j * 128:j * 128 + 65], PT[:, j * 128:(j + 1) * 128], vaug[:, kt, g, :], start=first, stop=last,
                        skip_group_check=True), reads=[PTk, f"vaug{kt}"], writes=[pbk])
            if ii == nk - 1:
                ycur[qt] = yr.next()
                y, yk = ycur[qt]
                for g in range(2):
                    pb, pbk = accbank[(qt % 2, g)]
                    rc, rck = rcr.next()
                    for j in range(4):
                        po = pb[:, j * 128:j * 128 + 65]
                        P.gen("vector", lambda e, rc=rc, po=po, j=j: e.reciprocal(out=rc[:, j:j + 1], in_=po[:, 64:65]), reads=[pbk], writes=[rck + str(j)])
                        c0 = (g * 4 + j) * 64
                        P.ts("vector", y[:, c0:c0 + 64], po[:, 0:64], rc[:, j:j + 1], None, ALU.mult, reads=[pbk, rck + str(j)], writes=[yk + f"{g}{j}"])
                P.dma("sync", yo[qt], y[:], reads=[yk + f"{g_}{j}" for g_ in range(2) for j in range(4)], writes=[yk + "d"], final=True)
        P.emit()
    return nc


def rope_tables(n_lat, grid_w=64, theta=10000.0, hd=64):
    nf = hd // 4
    t = np.arange(n_lat)
    row = (t // grid_w).astype(np.float32)
    col = (t % grid_w).astype(np.float32)
    inv = (theta ** (-np.arange(nf, dtype=np.float32) / nf)).astype(np.float32)
    ar = row[:, None] * inv
    ac = col[:, None] * inv
    ang = np.concatenate([ar, ar, ac, ac], axis=-1)
    return np.cos(ang).astype(np.float32), np.sin(ang).astype(np.float32)


def rope_rot_matrix():
    R = np.zeros((64, 64), np.float32)
    for a in range(2):
        for f in range(16):
            R[a * 32 + 16 + f, a * 32 + f] = -1.0
            R[a * 32 + f, a * 32 + 16 + f] = 1.0
    return R


def run_k3(P_lat, P_ctx, qw, kw):
    nc = build_k3()
    B, n, _ = P_lat.shape
    nctx = P_ctx.shape[1]
    aq_l, ak_l, av_l = P_lat[..., 1040:1552], P_lat[..., 1552:1680], P_lat[..., 1680:1808]
    aq_c, ak_c, av_c = P_ctx[..., 1040:1552], P_ctx[..., 1552:1680], P_ctx[..., 1680:1808]
    cos, sin = rope_tables(n)
    R = rope_rot_matrix()
    Rm = np.zeros((128, 128), np.float32); Rm[:64, :64] = R; Rm[64:, 64:] = R
    ob = np.zeros((128, 128), np.float32); ob[:64, :64] = 1; ob[64:, 64:] = 1
    cst = np.concatenate([Rm, ob, np.tile(qw, 2)[:, None], np.tile(kw, 2)[:, None]], 1).astype(np.float32)
    cosk = np.ascontiguousarray(np.tile(cos.T, (2, 1))); sink = np.ascontiguousarray(np.tile(sin.T, (2, 1)))
    seg = n // 4
    in_maps = []
    for i in range(NCORES):
        b, s = i // 4, i % 4
        q = tok_shard(aq_l, aq_c, i).reshape(K1_TILES, 128, 2, 4, 64)
        qT = np.ascontiguousarray(q.transpose(2, 4, 0, 3, 1)).reshape(128, QCOLS)
        k = np.concatenate([ak_l[b], ak_c[b]], 0).reshape(NKEY, 2, 64)
        kT = np.ascontiguousarray(k.transpose(1, 2, 0)).reshape(128, NKEY)
        v = np.concatenate([av_l[b], av_c[b]], 0).reshape(NKT, 128, 2, 64)
        vv = np.ascontiguousarray(v.transpose(1, 0, 2, 3))
        cq = cos[s * seg:(s + 1) * seg].reshape(16, 128, 64)
        sq = sin[s * seg:(s + 1) * seg].reshape(16, 128, 64)
        cq = np.broadcast_to(cq.transpose(2, 0, 1)[None, :, :, None, :], (2, 64, 16, 4, 128)).reshape(128, 16 * 512)
        sq = np.broadcast_to(sq.transpose(2, 0, 1)[None, :, :, None, :], (2, 64, 16, 4, 128)).reshape(128, 16 * 512)
        in_maps.append({"qT": qT, "kT": kT, "v": vv, "cosq": np.ascontiguousarray(cq), "sinq": np.ascontiguousarray(sq),
                        "cosk": cosk, "sink": sink, "cst": cst})
    res = _run(nc, in_maps)
    return tok_unshard([r["o_ya"] for r in res.results], B, n, nctx)


NCH = NKT
NSEQ = NKEY
MASK_NEG = -30000.0


def build_k2():
    nc = bass.Bass("TRN2", target_bir_lowering=False)
    qpd = nc.dram_tensor("qpT", [64, NSEQ], F32, kind="ExternalInput").ap()
    kpd = nc.dram_tensor("kpT", [64, NSEQ], F32, kind="ExternalInput").ap()
    vd = nc.dram_tensor("v", [128, NCH, 64], F32, kind="ExternalInput").ap()
    od = nc.dram_tensor("og", [128, NCH, 64], F32, kind="ExternalInput").ap()
    gd = nc.dram_tensor("g4", [128, NCH, 4], F32, kind="ExternalInput").ap()
    gbd = nc.dram_tensor("gb", [NCH * 4], F32, kind="ExternalInput").ap()
    cwd = nc.dram_tensor("cw", [64, 8], F32, kind="ExternalInput").ap()
    nwd = nc.dram_tensor("nw", [64], F32, kind="ExternalInput").ap()
    cstd = nc.dram_tensor("cst", [128, 6, 128], F32, kind="ExternalInput").ap()
    yo = nc.dram_tensor("o_ym", [128, NCH, 64], F32, kind="ExternalOutput").ap()
    with ExitStack() as es:
        P = Prog(nc, es)
        sb, ps, ring = mk_alloc(nc, es)
        cst = sb("cst_s", [128, 6, 128])
        P.dma("sync", cst[:], cstd, writes=["cst"])
        Lm = [cst[:, 0, :], cst[:, 1, :]]
        ones, ident = cst[:, 2, :], cst[:, 3, :]
        mneg = [cst[:, 4, :], cst[:, 5, :]]
        idb = sb("idb", [128, 128], BF16)
        P.copy("vector", idb[:], ident, reads=["cst"], writes=["idb"])
        one1 = sb("one1", [128, 1])
        P.memset("vector", one1[:], 1.0, writes=["one1"])
        cw = sb("cw_s", [64, 8])
        P.dma("sync", cw[:], cwd, writes=["cw"])
        banks = [ps(f"bank{i}", [128, 512], F32) for i in range(8)]
        xin = sb("xin", [64, NSEQ]); cacc = sb("cacc", [64, NSEQ])
        qT = sb("qT_s", [64, NSEQ], BF16); kT = sb("kT_s", [64, NSEQ], BF16)
        segs = [(0, 256), (256, NSEQ)]
        for wi, (src, dst, post) in enumerate(((qpd, qT, 0.125), (kpd, kT, 1.0))):
            o = wi * 4
            half = NSEQ // 2
            P.dma("sync", xin[:, 0:half], src[:, 0:half], writes=["xin_a"])
            P.dma("gpsimd", xin[:, half:], src[:, half:], writes=["xin_b"])
            P.ts("vector", cacc[:], xin[:], cw[:, o + 1:o + 2], cw[:, o + 3:o + 4], ALU.mult, ALU.add,
                 reads=["xin_a", "xin_b", "cw"], writes=["cacc"])
            for (a, b) in segs:
                P.stt("vector", cacc[:, a + 1:b], xin[:, a:b - 1], cw[:, o:o + 1], cacc[:, a + 1:b], ALU.mult, ALU.add,
                      reads=["xin_a", "xin_b", "cw", "cacc"], writes=["cacc"])
                P.stt("vector", cacc[:, a:b - 1], xin[:, a + 1:b], cw[:, o + 2:o + 3], cacc[:, a:b - 1], ALU.mult, ALU.add,
                      reads=["xin_a", "xin_b", "cw", "cacc"], writes=["cacc"])
            P.act(cacc[:], cacc[:], AF.Silu, reads=["cacc"], writes=["cacc"])
            P.act(dst[:], cacc[:], AF.Copy, reads=["cacc"], writes=[f"T{wi}"], scale=post)
            P.memset("vector", xin[:, 0:1], 0.0, writes=["xin_a", "xin_b"]) if wi == 0 else None
        ktok = sb("ktok", [128, NCH, 64], BF16)
        for c0 in range(0, NCH, 8):
            n = min(8, NCH - c0)
            pb = banks[7]
            for c in range(c0, c0 + n):
                pt = pb[:, :].bitcast(BF16)[:, (c - c0) * 64:(c - c0 + 1) * 64]
                P.tr(pt, kT[:, c * 128:(c + 1) * 128], idb[0:64, 0:64], reads=["T1", "idb"], writes=["bank7"])
            P.copy("vector", ktok[:, c0:c0 + n, :], pb[:, :].bitcast(BF16)[:, 0:n * 64].rearrange("p (c d) -> p c d", d=64),
                   reads=["bank7"], writes=["ktok"])
        vf = sb("vf", [128, NCH, 65]); vb = sb("vb", [128, NCH, 65], BF16)
        P.memset("vector", vf[:], 1.0, writes=["vf"])
        vtmp = sb("vtmp", [128, NCH, 64])
        P.dma("sync", vtmp[:], vd, writes=["vtmp"])
        P.copy("vector", vf[:, :, 0:64], vtmp[:], reads=["vtmp", "vf"], writes=["vf"])
        P.copy("scalar", vb[:], vf[:], reads=["vf"], writes=["vb"])
        G = sb("G", [128, NCH, 4]); GB = sb("GB", [128, NCH, 4])
        P.dma("sync", G[:], gd, writes=["G"])
        P.dma("sync", GB[:].rearrange("p c g -> p (c g)"), bcast_rows(gbd, 128), writes=["GB"])
        P.tt("vector", G[:], G[:], GB[:], ALU.add, reads=["G", "GB"], writes=["G"])
        LF = sb("LF", [128, 2, NCH]); LI = sb("LI", [128, 2, NCH]); TA = sb("TA", [128, 2, NCH]); TB = sb("TB", [128, 2, NCH])
        for dd in range(2):
            P.copy("vector", LI[:, dd, :], G[:, :, 2 * dd], reads=["G"], writes=[f"LI{dd}"])
            P.copy("vector", TA[:, dd, :], G[:, :, 2 * dd + 1], reads=["G"], writes=["TA"])
        P.act(TB[:], TA[:], AF.Abs, reads=["TA"], writes=["TB"])
        P.act(TB[:], TB[:], AF.Exp, reads=["TB"], writes=["TB"], scale=-1.0)
        P.act(TB[:], TB[:], AF.Ln, reads=["TB", "one1"], writes=["TB"], bias=one1[:, 0:1], scale=1.0)
        P.ts("vector", TA[:], TA[:], 0.0, None, ALU.min, reads=["TA"], writes=["TA"])
        P.tt("vector", LF[:], TA[:], TB[:], ALU.subtract, reads=["TA", "TB"], writes=["LF"])
        BC = sb("BC", [128, 2, NCH]); TOT = sb("TOT", [128, 2, NCH]); AA = sb("AA", [128, 2, NCH])
        BD = sb("BD", [128, 2, NCH]); WW = sb("WW", [128, 2, NCH]); DEC = sb("DEC", [128, 2, NCH])
        b6 = banks[6]
        for dd in range(2):
            P.mm(b6[:, dd * NCH:(dd + 1) * NCH], Lm[dd], LF[:, dd, :], True, True, reads=["LF", "cst"], writes=["bank6"])
        P.mm(b6[:, 2 * NCH:4 * NCH], ones, LF[:].rearrange("p a c -> p (a c)"), True, True, reads=["LF", "cst"], writes=["bank6"])
        P.copy("vector", BC[:].rearrange("p a c -> p (a c)"), b6[:, 0:2 * NCH], reads=["bank6"], writes=["BC"])
        P.copy("vector", TOT[:].rearrange("p a c -> p (a c)"), b6[:, 2 * NCH:4 * NCH], reads=["bank6"], writes=["TOT"])
        P.act(AA[:], BC[:], AF.Exp, reads=["BC"], writes=["AA"])
        P.tt("vector", BD[:], LI[:], BC[:], ALU.subtract, reads=["LI0", "LI1", "BC"], writes=["BD"])
        P.tt("vector", WW[:], TOT[:], BD[:], ALU.add, reads=["TOT", "BD"], writes=["WW"])
        P.act(WW[:], WW[:], AF.Exp, reads=["WW"], writes=["WW"])
        P.act(DEC[:], TOT[:], AF.Exp, reads=["TOT"], writes=["DEC"])
        S = [sb(f"S{dd}", [64, 65]) for dd in range(2)]
        Sb = [sb(f"Sb{dd}", [64, 65], BF16) for dd in range(2)]
        for dd in range(2):
            P.memset("vector", S[dd][:], 0.0, writes=[f"S{dd}"])
            P.memset("vector", Sb[dd][:], 0.0, writes=[f"Sb{dd}"])
        hb = [sb(f"hb{dd}", [128, NCH, 64]) for dd in range(2)]
        lfr = ring("lfrep", 4, [128, 128]); dtr = ring("Dt", 4, [128, 128]); ptr = ring("PTm", 4, [128, 128], BF16)
        tmr = ring("tmpi", 2, [128, 65]); ttr = ring("tot", 2, [128, 65]); dnr = ring("den", 2, [128, 4])
        wvr = ring("wv", 2, [128, 65], BF16)
        pD = Ring([(banks[0], "bank0"), (banks[1], "bank1")])
        pST = Ring([(banks[2], "bank2"), (banks[3], "bank3")])
        pOI = Ring([(banks[4], "bank4"), (banks[5], "bank5")])
        order = [list(range(NCH)), [1, 0] + list(range(NCH - 1, 1, -1))]
        def stage_a(step, dd):
            c = order[dd][step]
            cs_ = slice(c * 128, (c + 1) * 128)
            lf, lfk = lfr.next()
            P.act(lf[:], ones, AF.Copy, reads=["cst", "LF"], writes=[lfk], scale=LF[:, dd, c:c + 1])
            pd, pdk = pD.next()
            P.mm(pd[:, 0:128], lf[:], Lm[dd], True, False, reads=[lfk, "cst"], writes=[pdk])
            P.mm(pd[:, 0:128], ident, mneg[dd], False, True, reads=["cst"], writes=[pdk])
            dt_, dtk = dtr.next()
            P.act(dt_[:], pd[:, 0:128], AF.Exp, reads=[pdk, "BD"], writes=[dtk], bias=BD[:, dd, c:c + 1], scale=1.0)
            pst, pstk = pST.next()
            P.mm(pst[:, 0:128], kT[:, cs_], qT[:, cs_], True, True, reads=["T0", "T1"], writes=[pstk])
            PT, PTk = ptr.next()
            P.tt("vector", PT[:], pst[:, 0:128], dt_[:], ALU.mult, reads=[pstk, dtk], writes=[PTk])
            return PT, PTk

        def stage_b(step, dd, PT, PTk):
            c = order[dd][step]
            cs_ = slice(c * 128, (c + 1) * 128)
            poi, poik = pOI.next()
            P.mm(poi[:, 0:65], PT[:], vb[:, c, :], True, True, reads=[PTk, "vb"], writes=[poik + "o"])
            P.mm(poi[:, 128:193], qT[:, cs_], Sb[dd][:], True, True, reads=["T0", f"Sb{dd}"], writes=[poik + "i"])
            tm, tmk = tmr.next()
            P.act(tm[:], poi[:, 128:193], AF.Copy, reads=[poik + "i", "AA"], writes=[tmk], scale=AA[:, dd, c:c + 1])
            tt_, ttk = ttr.next()
            P.tt("vector", tt_[:], poi[:, 0:65], tm[:], ALU.add, reads=[poik + "o", tmk], writes=[ttk])
            dn, dnk = dnr.next()
            P.ts("vector", dn[:, 0:1], tt_[:, 64:65], -1.0, None, ALU.mult, reads=[ttk], writes=[dnk])
            P.stt("vector", dn[:, 1:2], dn[:, 0:1], 1.0, tt_[:, 64:65], ALU.max, ALU.max, reads=[dnk, ttk], writes=[dnk])
            P.gen("vector", lambda e, dn=dn: e.reciprocal(out=dn[:, 2:3], in_=dn[:, 1:2]), reads=[dnk], writes=[dnk])
            P.act(hb[dd][:, c, :], tt_[:, 0:64], AF.Copy, reads=[ttk, dnk], writes=[f"hb{dd}_{c}"], scale=dn[:, 2:3])
            wv, wvk = wvr.next()
            P.act(wv[:], vf[:, c, :], AF.Copy, reads=["vf", "WW"], writes=[wvk], scale=WW[:, dd, c:c + 1])
            p7 = banks[7]
            P.mm(p7[0:64, 256 + dd * 128:256 + dd * 128 + 65], ktok[:, c, :], wv[:], True, True, reads=["ktok", wvk], writes=[f"b7s{dd}"])
            P.stt("vector", S[dd][:], S[dd][:], DEC[0:64, dd, c:c + 1], p7[0:64, 256 + dd * 128:256 + dd * 128 + 65], ALU.mult, ALU.add,
                  reads=[f"S{dd}", "DEC", f"b7s{dd}"], writes=[f"S{dd}"])
            P.copy("gpsimd", Sb[dd][:], S[dd][:], reads=[f"S{dd}"], writes=[f"Sb{dd}"])

        pts = {}
        for step in range(NCH + 1):
            if step < NCH:
                for dd in range(2):
                    pts[(step, dd)] = stage_a(step, dd)
            if step >= 1:
                for dd in range(2):
                    stage_b(step - 1, dd, *pts.pop((step - 1, dd)))

        hk = [f"hb{dd}_{c}" for dd in range(2) for c in range(NCH)]
        P.tt("vector", hb[0][:], hb[0][:], hb[1][:], ALU.add, reads=hk, writes=["hsum"])
        sq = hb[1]
        P.tt("vector", sq[:], hb[0][:], hb[0][:], ALU.mult, reads=["hsum"], writes=["hsq"])
        ssum = sb("ssum", [128, NCH])
        P.gen("vector", lambda e: e.reduce_sum(out=ssum[:], in_=sq[:], axis=AX.X), reads=["hsq"], writes=["ssum"])
        P.ts("vector", ssum[:], ssum[:], 1.0 / 64.0, RMS_EPS, ALU.mult, ALU.add, reads=["ssum"], writes=["ssum"])
        P.act(ssum[:], ssum[:], AF.Sqrt, reads=["ssum"], writes=["ssum"])
        P.gen("vector", lambda e: e.reciprocal(out=ssum[:], in_=ssum[:]), reads=["ssum"], writes=["ssum"])
        nw = sb("nw_s", [128, 64])
        P.dma("sync", nw[:], bcast_rows(nwd, 128), writes=["nw"])
        og = vtmp
        P.dma("sync", og[:], od, reads=["vf"], writes=["og"])
        P.act(og[:], og[:], AF.Sigmoid, reads=["og"], writes=["og"])
        for c in range(NCH):
            P.stt("vector", hb[0][:, c, :], hb[0][:, c, :], ssum[:, c:c + 1], nw[:], ALU.mult, ALU.mult,
                  reads=["hsum", "ssum", "nw"], writes=[f"hn{c}"])
        P.tt("vector", hb[0][:], hb[0][:], og[:], ALU.mult, reads=[f"hn{c}" for c in range(NCH)] + ["og"], writes=["ym"])
        P.dma("sync", yo, hb[0][:], reads=["ym"], final=True)
        P.emit()
    return nc


def run_k2(P_lat, P_ctx, conv_w, conv_b, gate_b, norm_w):
    nc = build_k2()
    B, n, _ = P_lat.shape
    nctx = P_ctx.shape[1]
    s_idx, j_idx = np.meshgrid(np.arange(128), np.arange(128), indexing="ij")
    Lf = (s_idx <= j_idx).astype(np.float32); Lb = (s_idx >= j_idx).astype(np.float32)
    cst = np.stack([Lf, Lb, np.ones((128, 128), np.float32), np.eye(128, dtype=np.float32),
                    np.where(s_idx <= j_idx, 0.0, MASK_NEG).astype(np.float32),
                    np.where(s_idx >= j_idx, 0.0, MASK_NEG).astype(np.float32)], 1)
    in_maps = []
    for i in range(NCORES):
        b, h = i // 4, i % 4
        seq = np.concatenate([P_ctx[b], P_lat[b]], 0)
        qs, ks = slice(h * 64, (h + 1) * 64), slice(256 + h * 64, 256 + (h + 1) * 64)
        tm = lambda a: np.ascontiguousarray(a.reshape(NCH, 128, -1).transpose(1, 0, 2))
        gcols = [1024 + 0 * 8 + 0 * 4 + h, 1024 + 0 * 8 + 1 * 4 + h, 1024 + 1 * 8 + 0 * 4 + h, 1024 + 1 * 8 + 1 * 4 + h]
        gb = np.array([gate_b[0, 0, h], gate_b[0, 1, h], gate_b[1, 0, h], gate_b[1, 1, h]], np.float32)
        cw = np.concatenate([conv_w[:, qs].T, conv_b[qs][:, None], conv_w[:, ks].T, conv_b[ks][:, None]], 1).astype(np.float32)
        in_maps.append({"qpT": np.ascontiguousarray(seq[:, qs].T), "kpT": np.ascontiguousarray(seq[:, ks].T),
                        "v": tm(seq[:, 512 + h * 64:512 + (h + 1) * 64]), "og": tm(seq[:, 768 + h * 64:768 + (h + 1) * 64]),
                        "g4": tm(seq[:, gcols]), "gb": np.ascontiguousarray(np.tile(gb, NCH)), "cw": np.ascontiguousarray(cw),
                        "nw": np.ascontiguousarray(norm_w[h * 64:(h + 1) * 64]), "cst": np.ascontiguousarray(cst)})
    res = _run(nc, in_maps)
    ym_l = np.zeros((B, n, 256), np.float32); ym_c = np.zeros((B, nctx, 256), np.float32)
    for i in range(NCORES):
        b, h = i // 4, i % 4
        y = res.results[i]["o_ym"].transpose(1, 0, 2).reshape(NSEQ, 64)
        ym_c[b, :, h * 64:(h + 1) * 64] = y[:nctx]
        ym_l[b, :, h * 64:(h + 1) * 64] = y[nctx:]
    return ym_l, ym_c


HCH = 32
TWO_PI = 2.0 * np.pi
RND_MAGIC = 12582912.0


def fft_tables(N1):
    N2 = 128
    N = N1 * N2
    ar = np.arange
    c, s = np.cos, np.sin
    th = TWO_PI * ar(N1)[:, None] * ar(N1)[None] / N1
    F1c = np.concatenate([c(th), -s(th)], 1)
    th = TWO_PI * ar(N2)[:, None] * ar(N1)[None] / N
    twRR = np.concatenate([c(th), c(th)], 1); twII = np.concatenate([-s(th), -s(th)], 1)
    th = TWO_PI * ar(N2)[:, None] * ar(N2)[None] / N2
    F2re, F2im, nF2im = c(th), -s(th), s(th)
    G2c = np.concatenate([c(th), s(th)], 1); G2s = np.concatenate([-s(th), c(th)], 1)
    th = TWO_PI * ar(N1)[:, None] * ar(N2)[None] / N
    twcRR = np.concatenate([c(th), c(th)], 1); twcII = np.concatenate([s(th), s(th)], 1)
    th = TWO_PI * ar(N1)[:, None] * ar(N1 // 2)[None] / N1
    G1re, nG1im = c(th) / N, -s(th) / N
    f = lambda a: np.ascontiguousarray(a.astype(np.float32))
    return dict(F1c=f(F1c), twRR=f(twRR), twII=f(twII), F2re=f(F2re), F2im=f(F2im), nF2im=f(nF2im), G2c=f(G2c), G2s=f(G2s),
                twcRR=f(twcRR), twcII=f(twcII), G1re=f(G1re), nG1im=f(nG1im))


TAB_ORDER = ["F1c", "twRR", "twII", "F2re", "F2im", "nF2im", "G2c", "G2s", "twcRR", "twcII", "G1re", "nG1im"]


def hyena_consts(n):
    N = 2 * n
    tau = np.arange(N)
    pos = np.where(tau < n, tau, N - tau).astype(np.float32)
    t = (pos / np.float32(n)).astype(np.float32)
    bands = np.arange(1, 17, dtype=np.float32)
    ang = (np.float32(TWO_PI) * t[:, None] * bands).astype(np.float32)
    feats = np.concatenate([t[:, None], np.cos(ang), np.sin(ang)], -1).astype(np.float32)
    lt = abs(np.log(1e-2))
    deltas = np.linspace(lt / 1.5, lt / 0.3, 256, dtype=np.float32)
    win = (np.exp(-t[:, None] * deltas) + np.float32(0.05)).astype(np.float32)
    win[n] = 0.0
    return np.ascontiguousarray(feats.T), np.ascontiguousarray(win.T)


def interleave(gens, width):
    it = iter(gens)
    active = []
    while True:
        while len(active) < width:
            g = next(it, None)
            if g is None:
                break
            active.append(g)
        if not active:
            return
        for g in list(active):
            try:
                next(g)
            except StopIteration:
                active.remove(g)
        yield


def build_k4(sizes):
    nc = bass.Bass("TRN2", target_bir_lowering=False)
    B = 2
    dr = {}
    for si, n in enumerate(sizes):
        N1 = 2 * n // 128
        dr[si] = dict(
            u=nc.dram_tensor(f"u{si}", [3, HCH, B, n + 2], F32, kind="ExternalInput").ap(),
            feats=nc.dram_tensor(f"feats{si}", [33, 2 * n], F32, kind="ExternalInput").ap(),
            win=nc.dram_tensor(f"win{si}", [64, 2 * n], F32, kind="ExternalInput").ap(),
            taps=nc.dram_tensor(f"taps{si}", [64, 2 * n], F32, kind="ExternalOutput").ap(),
            out=nc.dram_tensor(f"o_yh{si}", [HCH, B, n], F32, kind="ExternalOutput").ap(),
            tabs={k: nc.dram_tensor(f"t{si}_{k}", list(v.shape), F32, kind="ExternalInput").ap()
                  for k, v in fft_tables(N1).items()})
    mlpd = nc.dram_tensor("mlp", [64, 64 + 64 + 128 + 4], F32, kind="ExternalInput").ap()
    cwd = nc.dram_tensor("cwv", [3 * HCH * 4], F32, kind="ExternalInput").ap()
    skd = nc.dram_tensor("skv", [2 * HCH], F32, kind="ExternalInput").ap()
    cstd = nc.dram_tensor("cst", [128, 256], F32, kind="ExternalInput").ap()
    with ExitStack() as es:
        P = Prog(nc, es)
        sb, ps, ring = mk_alloc(nc, es)
        banks = [(ps(f"bank{i}", [128, 512], F32), f"bank{i}") for i in range(8)]
        cst = sb("cst_s", [128, 256]); P.dma("sync", cst[:], cstd, writes=["cst"])
        ident, ones = cst[:, 0:128], cst[:, 128:256]
        mlp = sb("mlp_s", [64, 260]); P.dma("sync", mlp[:], mlpd, writes=["mlp"])
        w1, w2 = mlp[0:33, 0:64], mlp[:, 64:128]
        w3 = [mlp[:, 128:192], mlp[:, 192:256]]
        b1, f0, b2, f1 = (mlp[:, 256 + j:257 + j] for j in range(4))
        cwb = sb("cwb", [128, 3 * HCH * 4]); P.dma("sync", cwb[:], bcast_rows(cwd, 128), writes=["cwb"])
        skb = sb("skb", [128, 2 * HCH]); P.dma("sync", skb[:], bcast_rows(skd, 128), writes=["skb"])
        a1r = ring("a1", 4, [64, 512]); rrr = ring("rr", 4, [64, 512]); hhr = ring("hh", 4, [64, 512])
        ftr = ring("ft", 4, [33, 512]); wnr = ring("wn", 4, [64, 512]); tpr = ring("tp", 4, [64, 512])
        As_r = ring("As", 4, [128, 256]); t1r = ring("t1", 4, [128, 256]); t2r = ring("t2", 4, [128, 256])
        Br = ring("Bc", 4, [128, 256]); Yr = ring("Yc", 4, [128, 256], BF16); Dr = ring("Dc", 4, [128, 256], BF16)
        Brb = ring("Bcb", 4, [128, 256], BF16); zbr = ring("zb", 4, [64, 128], BF16); tlbr = ring("tlb", 4, [128, 128], BF16)
        pr4 = [ring(f"pp{j}", 4, [128, 128]) for j in range(4)]
        xir = ring("xi", 4, [64, 3, 130]); cvr = ring("cv", 4, [64, 3, 128]); tgr = ring("tg", 4, [64, 128])
        z1r = ring("z1", 4, [64, 128]); z2r = ring("z2", 4, [64, 128]); tlr = ring("tl", 4, [128, 128])
        bkP = Ring(banks[0:8])
        bkA = bkX = bkC = bkY = bkP

        def sin_layer(psrc, pk, n_, bias, freq, dst, dstk):
            a1, a1k = a1r.next(); rr, rrk = rrr.next()
            P.ts("vector", a1[:, 0:n_], psrc, bias, freq, ALU.add, ALU.mult, reads=[pk, "mlp"], writes=[a1k])
            yield
            P.ts("vector", rr[:, 0:n_], a1[:, 0:n_], 1.0 / TWO_PI, RND_MAGIC, ALU.mult, ALU.add, reads=[a1k], writes=[rrk])
            yield
            P.ts("vector", rr[:, 0:n_], rr[:, 0:n_], RND_MAGIC, -TWO_PI, ALU.subtract, ALU.mult, reads=[rrk], writes=[rrk])
            yield
            P.tt("vector", rr[:, 0:n_], rr[:, 0:n_], a1[:, 0:n_], ALU.add, reads=[rrk, a1k], writes=[rrk])
            yield
            P.ts("vector", rr[:, 0:n_], rr[:, 0:n_], np.pi, -np.pi, ALU.min, ALU.max, reads=[rrk], writes=[rrk])
            yield
            P.act(dst, rr[:, 0:n_], AF.Sin, reads=[rrk], writes=[dstk])
            yield

        def cmul(src, srck, n1p, W, tRR, tII, tk, dst, dstk):
            t1, t1k = t1r.next(); t2, t2k = t2r.next()
            P.tt("vector", t1[0:n1p, 0:2 * W], src, tRR, ALU.mult, reads=[srck, tk], writes=[t1k])
            yield
            P.tt("gpsimd", t2[0:n1p, 0:2 * W], src, tII, ALU.mult, reads=[srck, tk], writes=[t2k])
            yield
            P.tt("vector", dst[0:n1p, 0:W], t1[0:n1p, 0:W], t2[0:n1p, W:2 * W], ALU.subtract, reads=[t1k, t2k], writes=[dstk + "r"])
            yield
            P.tt("gpsimd", dst[0:n1p, W:2 * W], t2[0:n1p, 0:W], t1[0:n1p, W:2 * W], ALU.add, reads=[t1k, t2k], writes=[dstk + "i"])
            yield

        def size_body(si, n):
            N = 2 * n
            N1 = N // 128
            Kd = N1 // 2
            d = dr[si]
            T = {}
            for k in TAB_ORDER:
                shp = list(d["tabs"][k].shape)
                T[k] = sb(f"T{si}_{k}", shp)
                P.dma("sync", T[k][:], d["tabs"][k], writes=[f"tab{si}"] if k == TAB_ORDER[-1] else [f"tab{si}_{k}"])
                yield
            tabk = [f"tab{si}"] + [f"tab{si}_{k}" for k in TAB_ORDER[:-1]]
            Tb = {}
            for k_ in ("F1c", "F2re", "F2im", "nF2im", "G2c", "G2s", "G1re", "nG1im"):
                Tb[k_] = sb(f"Tb{si}_{k_}", list(d["tabs"][k_].shape), BF16)
                P.copy("vector", Tb[k_][:], T[k_][:], reads=tabk, writes=[f"tabb{si}"])
            tabk = tabk + [f"tabb{si}"]
            CH = min(512, n)
            nchunk = N // CH
            l1p = sb(f"l1p{si}", [64, nchunk])
            def mlp_chain(ci):
                c0 = ci * CH
                dirn = 0 if c0 < n else 1
                ft, ftk = ftr.next(); P.dma("sync", ft[:, 0:CH], d["feats"][:, c0:c0 + CH], writes=[ftk])
                wn, wnk = wnr.next(); P.dma("sync", wn[:, 0:CH], d["win"][:, c0:c0 + CH], writes=[wnk])
                bk, bkk = bkA.next()
                P.mm(bk[0:64, 0:CH], w1, ft[:, 0:CH], True, True, reads=[ftk, "mlp"], writes=[bkk])
                yield
                h1, h1k = hhr.next()
                yield from sin_layer(bk[0:64, 0:CH], bkk, CH, b1, f0, h1[:, 0:CH], h1k)
                bk, bkk = bkX.next()
                P.mm(bk[0:64, 0:CH], w2, h1[:, 0:CH], True, True, reads=[h1k, "mlp"], writes=[bkk])
                yield
                h2, h2k = hhr.next()
                yield from sin_layer(bk[0:64, 0:CH], bkk, CH, b2, f1, h2[:, 0:CH], h2k)
                bk, bkk = bkC.next()
                P.mm(bk[0:64, 0:CH], w3[dirn], h2[:, 0:CH], True, True, reads=[h2k, "mlp"], writes=[bkk])
                yield
                tp, tpk = tpr.next()
                P.tt("vector", tp[:, 0:CH], bk[0:64, 0:CH], wn[:, 0:CH], ALU.mult, reads=[bkk, wnk], writes=[tpk])
                yield
                P.gen("vector", lambda e, tp=tp, ci=ci, CH=CH, l1p=l1p: e.reduce_sum(out=l1p[:, ci:ci + 1], in_=tp[:, 0:CH], axis=AX.X,
                                                                         apply_absolute_value=True), reads=[tpk], writes=[f"l1p{si}_{ci}"])
                yield
                P.dma("sync", d["taps"][:, c0:c0 + CH], tp[:, 0:CH], reads=[tpk], writes=[f"taps{si}"], final=True)
                yield
            yield from interleave([mlp_chain(ci) for ci in range(nchunk)], 2)
            l1 = sb(f"l1_{si}", [64, 2])
            P.gen("vector", lambda e, l1=l1, l1p=l1p: e.reduce_sum(out=l1[:, 0:1], in_=l1p[:], axis=AX.X),
                  reads=[f"l1p{si}_{ci}" for ci in range(nchunk)], writes=[f"l1{si}"])
            yield
            P.gen("vector", lambda e, l1=l1: e.reciprocal(out=l1[:, 1:2], in_=l1[:, 0:1]), reads=[f"l1{si}"], writes=[f"l1{si}"])
            yield
            dg = sb(f"dg{si}", [64, 64])
            P.ts("vector", dg[:], ident[0:64, 0:64], l1[:, 1:2], None, ALU.mult, reads=["cst", f"l1{si}"], writes=[f"dg{si}"])
            yield
            bk, bkk = bkY.next()
            P.mm(bk[:, 0:64], ones[0:64, :], dg[:], True, True, reads=["cst", f"dg{si}"], writes=[bkk])
            yield
            rl1b = sb(f"rl1b{si}", [128, 64])
            P.copy("vector", rl1b[:], bk[:, 0:64], reads=[bkk], writes=[f"rl1b{si}"])
            yield

            def fwd_fft(xt, xk, Krows, lowp=False):
                TT = Tb if lowp else T
                bA, bAk = bkA.next()
                P.mm(bA[:, 0:2 * N1], xt, TT["F1c"][0:Krows, :], True, True, reads=[xk] + tabk, writes=[bAk])
                yield
                As, Ask = As_r.next()
                P.copy("scalar", As[:, 0:2 * N1], bA[:, 0:2 * N1], reads=[bAk], writes=[Ask])
                yield
                Bc, Bck = (Brb if lowp else Br).next()
                yield from cmul(As[:, 0:2 * N1], Ask, 128, N1, T["twRR"][:], T["twII"][:], tabk[0], Bc, Bck)
                bX, bXk = bkX.next()
                Bre, Bim = Bc[:, 0:N1], Bc[:, N1:2 * N1]
                P.mm(bX[:, 0:N1], TT["F2re"][:], Bre, True, False, reads=[Bck + "r"] + tabk, writes=[bXk])
                P.mm(bX[:, 0:N1], TT["nF2im"][:], Bim, False, True, reads=[Bck + "i"] + tabk, writes=[bXk])
                yield
                P.mm(bX[:, N1:2 * N1], TT["F2re"][:], Bim, True, False, reads=[Bck + "i"] + tabk, writes=[bXk])
                P.mm(bX[:, N1:2 * N1], TT["F2im"][:], Bre, False, True, reads=[Bck + "r"] + tabk, writes=[bXk])
                yield
                return bX, bXk

            H = sb(f"H{si}", [128, 64, 2 * N1])
            def filt_chain(oc):
                tl, tlk = tlr.next()
                P.dma("sync", tl[0:N1, :], d["taps"][oc].rearrange("(a b) -> a b", b=128), reads=[f"taps{si}"], writes=[tlk])
                yield
                tlb, tlbk = tlbr.next()
                P.copy("scalar", tlb[0:N1, :], tl[0:N1, :], reads=[tlk], writes=[tlbk])
                yield
                bX, bXk = yield from fwd_fft(tlb[0:N1, :], tlbk, N1, lowp=True)
                P.ts("vector", H[:, oc, :], bX[:, 0:2 * N1], rl1b[:, oc:oc + 1], None, ALU.mult, reads=[bXk, f"rl1b{si}"], writes=[f"H{si}_{oc}"])
                yield

            yield from interleave([filt_chain(oc) for oc in range(64)], 4)
            def long_conv(zt, zk, o, c):
                zb, zbk = zbr.next()
                P.copy("scalar", zb[0:Kd, :], zt, reads=[zk], writes=[zbk])
                yield
                bX, bXk = yield from fwd_fft(zb[0:Kd, :], zbk, Kd, lowp=True)
                oc = o * HCH + c
                Hre, Him = H[:, oc, 0:N1], H[:, oc, N1:2 * N1]
                hk = f"H{si}_{oc}"
                pp = [r.next() for r in pr4]
                P.tt("vector", pp[0][0][:, 0:N1], bX[:, 0:N1], Hre, ALU.mult, reads=[bXk, hk], writes=[pp[0][1]])
                yield
                P.tt("vector", pp[1][0][:, 0:N1], bX[:, N1:2 * N1], Him, ALU.mult, reads=[bXk, hk], writes=[pp[1][1]])
                yield
                P.tt("vector", pp[2][0][:, 0:N1], bX[:, 0:N1], Him, ALU.mult, reads=[bXk, hk], writes=[pp[2][1]])
                yield
                P.tt("vector", pp[3][0][:, 0:N1], bX[:, N1:2 * N1], Hre, ALU.mult, reads=[bXk, hk], writes=[pp[3][1]])
                yield
                Yc, Yck = Yr.next()
                P.tt("gpsimd", Yc[:, 0:N1], pp[0][0][:, 0:N1], pp[1][0][:, 0:N1], ALU.subtract, reads=[pp[0][1], pp[1][1]], writes=[Yck + "r"])
                yield
                P.tt("gpsimd", Yc[:, N1:2 * N1], pp[2][0][:, 0:N1], pp[3][0][:, 0:N1], ALU.add, reads=[pp[2][1], pp[3][1]], writes=[Yck + "i"])
                yield
                bC, bCk = bkC.next()
                P.mm(bC[0:N1, 0:256], Yc[:, 0:N1], Tb["G2c"][:], True, False, reads=[Yck + "r"] + tabk, writes=[bCk])
                P.mm(bC[0:N1, 0:256], Yc[:, N1:2 * N1], Tb["G2s"][:], False, True, reads=[Yck + "i"] + tabk, writes=[bCk])
                yield
                Cs, Csk = As_r.next()
                P.copy("scalar", Cs[0:N1, :], bC[0:N1, 0:256], reads=[bCk], writes=[Csk])
                yield
                Dc, Dck = Dr.next()
                yield from cmul(Cs[0:N1, :], Csk, N1, 128, T["twcRR"][:], T["twcII"][:], tabk[0], Dc, Dck)
                bY, bYk = bkY.next()
                P.mm(bY[0:Kd, 0:128], Tb["G1re"][:], Dc[0:N1, 0:128], True, False, reads=[Dck + "r"] + tabk, writes=[bYk])
                P.mm(bY[0:Kd, 0:128], Tb["nG1im"][:], Dc[0:N1, 128:256], False, True, reads=[Dck + "i"] + tabk, writes=[bYk])
                yield
                return bY, bYk

            def data_chain(c, b):
                xi, xik = xir.next()
                src = bass.AP(tensor=d["u"].tensor, offset=d["u"][0, c, b, 0].offset,
                              ap=[[128, Kd], [HCH * B * (n + 2), 3], [1, 130]])
                P.dma("gpsimd", xi[0:Kd], src, writes=[xik])
                yield
                cv, cvk = cvr.next()
                for p in range(3):
                    wo = (p * HCH + c) * 4
                    P.ts("vector", cv[0:Kd, p, :], xi[0:Kd, p, 1:129], cwb[0:Kd, wo + 1:wo + 2], cwb[0:Kd, wo + 3:wo + 4], ALU.mult, ALU.add,
                         reads=[xik, "cwb"], writes=[cvk + str(p)])
                    yield
                    P.stt("vector", cv[0:Kd, p, :], xi[0:Kd, p, 0:128], cwb[0:Kd, wo:wo + 1], cv[0:Kd, p, :], ALU.mult, ALU.add,
                          reads=[xik, "cwb", cvk + str(p)], writes=[cvk + str(p)])
                    yield
                    P.stt("vector", cv[0:Kd, p, :], xi[0:Kd, p, 2:130], cwb[0:Kd, wo + 2:wo + 3], cv[0:Kd, p, :], ALU.mult, ALU.add,
                          reads=[xik, "cwb", cvk + str(p)], writes=[cvk + str(p)])
                    yield
                bY, bYk = yield from long_conv(cv[0:Kd, 0, :], cvk + "0", 0, c)
                tg, tgk = tgr.next()
                P.stt("vector", tg[0:Kd, :], cv[0:Kd, 0, :], skb[0:Kd, c:c + 1], bY[0:Kd, 0:128], ALU.mult, ALU.add,
                      reads=[cvk + "0", "skb", bYk], writes=[tgk])
                yield
                z1, z1k = z1r.next()
                P.tt("gpsimd", z1[0:Kd, :], tg[0:Kd, :], cv[0:Kd, 1, :], ALU.mult, reads=[tgk, cvk + "1"], writes=[z1k])
                yield
                bY, bYk = yield from long_conv(z1[0:Kd, :], z1k, 1, c)
                tg, tgk = tgr.next()
                P.stt("vector", tg[0:Kd, :], z1[0:Kd, :], skb[0:Kd, HCH + c:HCH + c + 1], bY[0:Kd, 0:128], ALU.mult, ALU.add,
                      reads=[z1k, "skb", bYk], writes=[tgk])
                yield
                z2, z2k = z2r.next()
                P.tt("gpsimd", z2[0:Kd, :], tg[0:Kd, :], cv[0:Kd, 2, :], ALU.mult, reads=[tgk, cvk + "2"], writes=[z2k])
                yield
                P.dma("sync", d["out"][c, b].rearrange("(a b) -> a b", b=128), z2[0:Kd, :], reads=[z2k], writes=[z2k + "d"], final=True)
                yield
            yield from interleave([data_chain(c, b) for c in range(HCH) for b in range(B)], 4)

        for si, n in enumerate(sizes):
            for _ in size_body(si, n):
                pass
        P.emit()
    return nc


def run_k4(hy_list, conv_w, conv_b, fparams, skip):
    f_w1, f_b1, f_freq, f_w2, f_b2, f_w3 = fparams
    sizes = [h.shape[1] for h in hy_list]
    nc = build_k4(sizes)
    B = 2
    cst = np.concatenate([np.eye(128, dtype=np.float32), np.ones((128, 128), np.float32)], 1)
    consts = [hyena_consts(n) for n in sizes]
    tabs = [fft_tables(2 * n // 128) for n in sizes]
    w3r = f_w3.reshape(64, 2, 2, 256)
    in_maps = []
    for i in range(NCORES):
        cs = slice(i * HCH, (i + 1) * HCH)
        m = {"cst": cst}
        mlp = np.zeros((64, 260), np.float32)
        mlp[0:33, 0:64] = f_w1
        mlp[:, 64:128] = f_w2
        mlp[:, 128:192] = w3r[:, 0, :, cs].reshape(64, 64)
        mlp[:, 192:256] = w3r[:, 1, :, cs].reshape(64, 64)
        mlp[:, 256] = f_b1; mlp[:, 257] = f_freq[0]; mlp[:, 258] = f_b2; mlp[:, 259] = f_freq[1]
        m["mlp"] = mlp
        cw = np.zeros((3, HCH, 4), np.float32)
        for p in range(3):
            ch = slice(p * 256 + i * HCH, p * 256 + (i + 1) * HCH)
            cw[p, :, 0:3] = conv_w[:, ch].T
            cw[p, :, 3] = conv_b[ch]
        m["cwv"] = cw.reshape(-1)
        m["skv"] = np.ascontiguousarray(skip[:, cs]).reshape(-1)
        for si, (hy, n) in enumerate(zip(hy_list, sizes)):
            u = np.zeros((3, HCH, B, n + 2), np.float32)
            for p in range(3):
                u[p, :, :, 1:n + 1] = hy[:, :, p * 256 + i * HCH:p * 256 + (i + 1) * HCH].transpose(2, 0, 1)
            m[f"u{si}"] = u
            feats, win = consts[si]
            m[f"feats{si}"] = feats
            m[f"win{si}"] = np.ascontiguousarray(np.tile(win[cs], (2, 1)))
            for k, v in tabs[si].items():
                m[f"t{si}_{k}"] = v
        in_maps.append(m)
    res = _run(nc, in_maps)
    outs = []
    for si, n in enumerate(sizes):
        y = np.zeros((B, n, 256), np.float32)
        for i in range(NCORES):
            y[:, :, i * HCH:(i + 1) * HCH] = res.results[i][f"o_yh{si}"].transpose(1, 2, 0)
        outs.append(y)
    return outs, [np.concatenate([res.results[i][f"taps{si}"] for i in range(NCORES)], 0) for si in range(len(sizes))]


def kernel(x, c, ctx, c_ctx, w_mod, b_mod, w_in, mlstm_conv_w, mlstm_conv_b, mlstm_gate_b,
           mlstm_norm_w, attn_q_norm_w, attn_k_norm_w, hyena_conv_w, hyena_conv_b,
           hyena_f_w1, hyena_f_b1, hyena_f_freq, hyena_f_w2, hyena_f_b2, hyena_f_w3,
           hyena_skip, w_out, ln_mix_w, ln_mix_b, router_w, router_b,
           exp_w_gate, exp_w_up, exp_w_down, ln_ffn_w, ln_ffn_b):
    f = lambda a: np.asarray(a, dtype=np.float32)
    x, c, ctx, c_ctx = f(x), f(c), f(ctx), f(c_ctx)
    mod = run_k0(c, c_ctx, f(w_mod), f(b_mod))
    depth = w_in.shape[0]
    for l in range(depth):
        last = l == depth - 1
        P_lat, P_ctx = run_k1(x, ctx, mod[l], f(w_in[l]))
        ym_l, ym_c = run_k2(P_lat, P_ctx, f(mlstm_conv_w[l]), f(mlstm_conv_b[l]), f(mlstm_gate_b[l]), f(mlstm_norm_w[l]))
        ya_l, ya_c = run_k3(P_lat, P_ctx, f(attn_q_norm_w[l]), f(attn_k_norm_w[l]))
        hy = [P_lat[..., 1808:]] + ([] if last else [P_ctx[..., 1808:]])
        fpar = (f(hyena_f_w1[l]), f(hyena_f_b1[l]), f(hyena_f_freq[l]), f(hyena_f_w2[l]), f(hyena_f_b2[l]), f(hyena_f_w3[l]))
        yhs, _ = run_k4(hy, f(hyena_conv_w[l]), f(hyena_conv_b[l]), fpar, f(hyena_skip[l]))
        yh_c = np.zeros_like(ym_c) if last else yhs[1]
        ycat_l = np.concatenate([ym_l, ya_l, yhs[0]], -1)
        ycat_c = np.concatenate([ym_c, ya_c, yh_c], -1)
        x1, c1, h2, h2c, aff, affc = run_k5(ycat_l, ycat_c, x, ctx, mod[l], f(w_out[l]), f(ln_mix_w[l]), f(ln_mix_b[l]),
                                            f(router_w[l]), f(router_b[l]))
        thr, thrc = run_k6(aff, affc)
        x, ctx = run_moe(h2, h2c, aff, affc, thr, thrc, x1, c1, mod[l], f(ln_ffn_w[l]), f(ln_ffn_b[l]),
                         f(exp_w_gate[l]), f(exp_w_up[l]), f(exp_w_down[l]))
    return x.astype(np.float32)


CAP_L, CAP_C = 1024, 32
SLOTS_B = CAP_L + CAP_C
NTOK_ALL = 2 * 8192 + 2 * 256


def build_k7g():
    nc = bass.Bass("TRN2", target_bir_lowering=False)
    T = 18
    TOK = T * 128
    h2d = nc.dram_tensor("h2all", [NTOK_ALL, 1024], BF16, kind="ExternalInput").ap()
    affLd = nc.dram_tensor("affL", [2, 2, 128, 66], F32, kind="ExternalInput").ap()
    thrd = nc.dram_tensor("thr8", [8], F32, kind="ExternalInput").ap()
    tidd = nc.dram_tensor("tid", [2, 128, 66, 2], F32, kind="ExternalInput").ap()
    iotad = nc.dram_tensor("iota", [128, CAP_L + 128], F32, kind="ExternalInput").ap()
    cstd = nc.dram_tensor("cst", [128, 448], F32, kind="ExternalInput").ap()
    wgd = nc.dram_tensor("wg", [2, 128, 8, D_FF], F32, kind="ExternalInput").ap()
    wud = nc.dram_tensor("wu", [2, 128, 8, D_FF], F32, kind="ExternalInput").ap()
    wdd = nc.dram_tensor("wd", [2, 128, NFC, 1024], F32, kind="ExternalInput").ap()
    Yo = nc.dram_tensor("o_Y", [2, TOK, 1024], BF16, kind="ExternalOutput").ap()
    posKo = nc.dram_tensor("o_pos", [2, 2, 128, 66], I32, kind="ExternalOutput").ap()
    with ExitStack() as es:
        P = Prog(nc, es)
        sb, ps, ring = mk_alloc(nc, es)
        banks = [(ps(f"bk{i}", [128, 512], F32), f"bk{i}") for i in range(8)]
        cst = sb("cst_s", [128, 448]); P.dma("sync", cst[:], cstd, writes=["cst"])
        Ust, ones, identf, Ust64 = cst[:, 0:128], cst[:, 128:256], cst[:, 256:384], cst[0:64, 384:448]
        idb = sb("idb", [128, 128], BF16); P.copy("vector", idb[:], identf, reads=["cst"], writes=["idb"])
        thrt = sb("thrt", [128, 8]); P.dma("sync", thrt[:], bcast_rows(thrd, 128), writes=["thrt"])
        iota = sb("iota_s", [128, CAP_L + 128]); P.dma("sync", iota[:], iotad, writes=["iota"])
        h2T = sb("h2T_s", [128, 8, TOK], BF16)
        acc = sb("acc", [128, T, 1024])
        tv = sb("tv", [128, T])
        Ar = ring("A", 2, [128, 66]); Mr = ring("M", 2, [128, 66]); wir = ring("wi", 2, [128, 66]); pfr = ring("pf", 2, [128, 66])
        m2r = ring("m2", 2, [128, 66]); pir = ring("pi", 2, [128, 66], I32)
        tdfr = ring("tdf", 2, [128, 66, 2]); TAr = ring("TA", 2, [128, 66, 5], BF16); spr = ring("sp", 2, [128, 2, 66]); l5r = ring("l5", 2, [128, 9, 5]); selr = ring("sel", 4, [128, 512], BF16); rowt = sb("rowt", [5, SLOTS_B]); lfr = ring("lf", 2, [128, 18])
        tcr = ring("tc", 2, [64, 1]); tbr = ring("tb", 2, [64, 128])
        lsr = ring("ls", 2, [128, 9], I32)
        xsr = ring("xs", 2, [128, 1024], BF16)
        FG = 3
        wgr = ring("wgb", 2, [128, 8, FG * 128], BF16); wur = ring("wub", 2, [128, 8, FG * 128], BF16)
        wdr = ring("wdb", 2, [128, FG, 1024], BF16); stg = ring("stg", 3, [128, 1024])
        actr = ring("actT", 2, [128, FG, 512], BF16); sgr = ring("sg", 2, [128, 512]); yor = ring("yrow", 2, [128, 1024], BF16)
        pgr = Ring(banks[0:2]); pur = Ring(banks[2:4]); pyr = Ring(banks[4:6])
        tgs = [(s, min(512, TOK - s)) for s in range(0, TOK, 512)]
        for e in range(2):
            P.memset("gpsimd", h2T[:], 0.0, writes=["h2T"])
            P.memset("vector", tv[:], 0.0, writes=["tv"])
            for t in range(T):
                P.memset("gpsimd", acc[:, t, :], 0.0, writes=[f"acc{t}a", f"acc{t}b"])
            for b in range(2):
                A, Ak = Ar.next(); P.dma("sync", A[:], affLd[e, b], writes=[Ak])
                M, Mk = Mr.next()
                to = (e * 2 + b) * 2
                P.ts("vector", M[:, 0:64], A[:, 0:64], thrt[:, to:to + 1], None, ALU.is_ge, reads=[Ak, "thrt"], writes=[Mk + "l"])
                P.ts("vector", M[:, 64:66], A[:, 64:66], thrt[:, to + 1:to + 2], None, ALU.is_ge, reads=[Ak, "thrt"], writes=[Mk + "c"])
                wi, wik = wir.next(); pf, pfk = pfr.next(); m2, m2k = m2r.next(); pi, pik = pir.next()
                for (c0, ncol, cap, base, sfx) in ((0, 64, CAP_L, 0, "l"), (64, 2, CAP_C, CAP_L, "c")):
                    bw, bwk = banks[6]
                    P.mm(bw[:, c0:c0 + ncol], Ust, M[:, c0:c0 + ncol], True, True, reads=["cst", Mk + sfx], writes=[bwk + sfx])
                    P.copy("scalar", wi[:, c0:c0 + ncol], bw[:, c0:c0 + ncol], reads=[bwk + sfx], writes=[wik + sfx])
                    bt, btk = banks[7]
                    P.mm(bt[0:ncol, c0:c0 + 1], M[:, c0:c0 + ncol], ones[:, 0:1], True, True, reads=["cst", Mk + sfx], writes=[btk + "t" + sfx])
                    tc, tck = tcr.next()
                    P.copy("vector", tc[0:ncol, :], bt[0:ncol, c0:c0 + 1], reads=[btk + "t" + sfx], writes=[tck])
                    tb, tbk = tbr.next()
                    P.ts("vector", tb[0:ncol, :], ones[0:ncol, :], tc[0:ncol, 0:1], None, ALU.mult, reads=["cst", tck], writes=[tbk])
                    P.mm(bt[:, 128 + c0:128 + c0 + ncol], tb[0:ncol, :], Ust64[0:ncol, 0:ncol], True, True, reads=[tbk, "cst"], writes=[btk + "o" + sfx])
                    sl = slice(c0, c0 + ncol)
                    P.tt("vector", pf[:, sl], bt[:, 128 + c0:128 + c0 + ncol], wi[:, sl], ALU.add, reads=[btk + "o" + sfx, wik + sfx], writes=[pfk + sfx])
                    P.ts("vector", m2[:, sl], pf[:, sl], float(cap) - 0.5, None, ALU.is_lt, reads=[pfk + sfx], writes=[m2k + sfx])
                    P.tt("vector", m2[:, sl], m2[:, sl], M[:, sl], ALU.mult, reads=[m2k + sfx, Mk + sfx], writes=[m2k + sfx])
                    P.ts("vector", pf[:, sl], pf[:, sl], float(base - SLOTS_B), None, ALU.add, reads=[pfk + sfx], writes=[pfk + sfx])
                    P.tt("vector", pf[:, sl], pf[:, sl], m2[:, sl], ALU.mult, reads=[pfk + sfx, m2k + sfx], writes=[pfk + sfx])
                    P.ts("vector", pf[:, sl], pf[:, sl], float(SLOTS_B), None, ALU.add, reads=[pfk + sfx], writes=[pfk + sfx])
                P.copy("vector", pi[:], pf[:], reads=[pfk + "l", pfk + "c"], writes=[pik])
                P.dma("sync", posKo[e, b], pi[:], reads=[pik], writes=[pik + "d"], final=True)
                TA, TAk = TAr.next()
                tdf, tdfk = tdfr.next()
                P.dma("sync", tdf[:], tidd[b], writes=[tdfk])
                P.copy("gpsimd", TA[:, :, 0:2], tdf[:], reads=[tdfk], writes=[TAk + "t"])
                sp, spk = spr.next()
                P.copy("vector", TA[:, :, 2], A[:], reads=[Ak], writes=[TAk + "a0"])
                P.copy("vector", sp[:, 0, :], TA[:, :, 2], reads=[TAk + "a0"], writes=[spk + "0"])
                P.tt("vector", sp[:, 1, :], A[:], sp[:, 0, :], ALU.subtract, reads=[Ak, spk + "0"], writes=[spk + "1"])
                P.copy("vector", TA[:, :, 3], sp[:, 1, :], reads=[spk + "1"], writes=[TAk + "a1"])
                P.copy("vector", sp[:, 0, :], TA[:, :, 3], reads=[TAk + "a1"], writes=[spk + "0"])
                P.tt("vector", sp[:, 1, :], sp[:, 1, :], sp[:, 0, :], ALU.subtract, reads=[spk + "1", spk + "0"], writes=[spk + "1"])
                P.copy("vector", TA[:, :, 4], sp[:, 1, :], reads=[spk + "1"], writes=[TAk + "a2"])
                tak = [TAk + "t", TAk + "a0", TAk + "a1", TAk + "a2"]
                pc, pck = banks[7]
                for piece, (s0_, ns_, js) in enumerate(((0, 512, range(64)), (512, 512, range(64)), (CAP_L, CAP_C, (64, 65)))):
                    sfx = "l" if piece < 2 else "c"
                    for jj, j in enumerate(js):
                        se, sek = selr.next()
                        P.ts("vector", se[:, 0:ns_], iota[:, s0_:s0_ + ns_], pf[:, j:j + 1], None, ALU.is_equal, reads=["iota", pfk + sfx], writes=[sek])
                        P.mm(pc[0:5, 0:ns_], TA[:, j, :], se[:, 0:ns_], jj == 0, jj == len(js) - 1, reads=[sek] + tak, writes=[pck + "row"])
                    P.copy("scalar", rowt[0:5, s0_:s0_ + ns_], pc[0:5, 0:ns_], reads=[pck + "row"], writes=[f"rowt{piece}"])
                pq, pqk = banks[6]
                for c in range(9):
                    ns_ = 128 if c < 8 else 32
                    P.tr(pq[0:ns_, 256 + 5 * c:256 + 5 * c + 5], rowt[0:5, c * 128:c * 128 + ns_], identf[0:5, 0:5],
                         reads=[f"rowt{c // 4}", "cst"], writes=[pqk + "q"])
                l5, l5k = l5r.next()
                P.copy("vector", l5[:].rearrange("p c t -> p (c t)"), pq[:, 256:301], reads=[pqk + "q"], writes=[l5k])
                lf, lfk = lfr.next()
                lf3 = lf[:].rearrange("p (c t) -> p c t", t=2)
                P.stt("vector", lf3[:, :, 0], l5[:, :, 0], 128.0, l5[:, :, 1], ALU.mult, ALU.add, reads=[l5k], writes=[lfk + "i"])
                P.tt("vector", lf3[:, :, 1], l5[:, :, 2], l5[:, :, 3], ALU.add, reads=[l5k], writes=[lfk + "v"])
                P.tt("vector", lf3[:, :, 1], lf3[:, :, 1], l5[:, :, 4], ALU.add, reads=[l5k, lfk + "v"], writes=[lfk + "v"])
                ls, lsk = lsr.next()
                P.copy("vector", ls[:], lf3[:, :, 0], reads=[lfk + "i"], writes=[lsk])
                for c in range(9):
                    npart = 128 if c < 8 else 32
                    p0 = 0
                    col0 = b * CAP_L + c * 128 if c < 8 else (16 + b) * 128
                    tcol = col0 // 128
                    P.copy("vector", tv[p0:p0 + npart, tcol:tcol + 1], lf[p0:p0 + npart, 2 * c + 1:2 * c + 2], reads=[lfk + "v", "tv"], writes=["tv"])
                    xs, xsk = xsr.next()
                    P.op("gpsimd", lambda en, xs=xs, ls=ls, c=c, npart=npart, p0=p0: en.indirect_dma_start(
                        out=xs[p0:p0 + npart, :], out_offset=None, in_=h2d[:, :],
                        in_offset=bass.IndirectOffsetOnAxis(ap=ls[p0:p0 + npart, c:c + 1], axis=0)), reads=[lsk], writes=[xsk], dma=True)
                    pb, pbk = banks[6]
                    pT = pb[:, :].bitcast(BF16)
                    for kc in range(8):
                        P.tr(pT[:, kc * 128:kc * 128 + npart], xs[p0:p0 + npart, kc * 128:(kc + 1) * 128], idb[p0:p0 + npart, p0:p0 + npart],
                             reads=[xsk, "idb"], writes=[pbk + "l", pbk + "c"])
                    P.copy("scalar", h2T[:, :, col0:col0 + npart], pT[:, 0:1024].rearrange("p (k s) -> p k s", k=8)[:, :, 0:npart],
                           reads=[pbk + "l", pbk + "c"], writes=["h2T"])
            ci = 0
            for f0 in range(0, NFC, FG):
                nf = min(FG, NFC - f0)
                wgb, wgk = wgr.next(); wub, wuk = wur.next(); wdb, wdk = wdr.next()
                for (src, dst, dk) in ((wgd, wgb, wgk), (wud, wub, wuk)):
                    for kp in range(0, 8, 2):
                        st, sk = stg.next()
                        sv = st[:, 0:2 * nf * 128].rearrange("p (a f) -> p a f", a=2)
                        P.dma("sync", sv, src[e, :, kp:kp + 2, f0 * 128:(f0 + nf) * 128], writes=[sk])
                        P.copy("gpsimd" if ci % 2 else "vector", dst[:, kp:kp + 2, 0:nf * 128], sv, reads=[sk], writes=[dk + f"k{kp}"])
                        ci += 1
                for fc in range(nf):
                    st, sk = stg.next()
                    P.dma("sync", st[:], wdd[e, :, f0 + fc, :], writes=[sk])
                    P.copy("gpsimd" if ci % 2 else "vector", wdb[:, fc, :], st[:], reads=[sk], writes=[wdk + f"f{fc}"])
                    ci += 1
                for (s0, ns) in tgs:
                    actT, ak = actr.next()
                    for fc in range(nf):
                        pg, pgk = pgr.next(); pu, puk = pur.next()
                        for kc in range(8):
                            P.mm(pg[:, 0:ns], wgb[:, kc, fc * 128:(fc + 1) * 128], h2T[:, kc, s0:s0 + ns], kc == 0, kc == 7,
                                 reads=["h2T", wgk + f"k{kc - kc % 2}"], writes=[pgk])
                        for kc in range(8):
                            P.mm(pu[:, 0:ns], wub[:, kc, fc * 128:(fc + 1) * 128], h2T[:, kc, s0:s0 + ns], kc == 0, kc == 7,
                                 reads=["h2T", wuk + f"k{kc - kc % 2}"], writes=[puk])
                        sg, sgk = sgr.next()
                        P.act(sg[:, 0:ns], pg[:, 0:ns], AF.Silu, reads=[pgk], writes=[sgk])
                        P.tt("vector", actT[:, fc, 0:ns], pu[:, 0:ns], sg[:, 0:ns], ALU.mult, reads=[puk, sgk], writes=[ak + f"f{fc}"])
                    for tt in range(s0 // 128, (s0 + ns) // 128):
                        for hf in range(2):
                            py, pyk = pyr.next()
                            for fc in range(nf):
                                P.mm(py[:], actT[:, fc, tt * 128 - s0:(tt + 1) * 128 - s0], wdb[:, fc, hf * 512:(hf + 1) * 512],
                                     fc == 0, fc == nf - 1, reads=[ak + f"f{fc}", wdk + f"f{fc}"], writes=[pyk])
                            ah = f"acc{tt}" + "ab"[hf]
                            P.tt("vector", acc[:, tt, hf * 512:(hf + 1) * 512], py[:], acc[:, tt, hf * 512:(hf + 1) * 512], ALU.add,
                                 reads=[pyk, ah], writes=[ah])
            for t in range(T):
                yr_, yrk = yor.next()
                P.act(yr_[:], acc[:, t, :], AF.Copy, reads=[f"acc{t}a", f"acc{t}b", "tv"], writes=[yrk], scale=tv[:, t:t + 1])
                P.dma("sync", Yo[e, t * 128:(t + 1) * 128, :], yr_[:], reads=[yrk], writes=[yrk], final=True)
        P.emit()
    return nc


def build_k8():
    nc = bass.Bass("TRN2", target_bir_lowering=False)
    T = K1_TILES
    Yb = [nc.dram_tensor(f"Yb{e}", [SLOTS_B + 1, 1024], BF16, kind="ExternalInput").ap() for e in range(N_EXP)]
    idxd = nc.dram_tensor("idx", [T, 128, N_EXP], I32, kind="ExternalInput").ap()
    x1d = nc.dram_tensor("i_x1", [T, 128, 1024], F32, kind="ExternalInput").ap()
    rows = nc.dram_tensor("rows", [2, 1024], F32, kind="ExternalInput").ap()
    lnr = nc.dram_tensor("lnr", [2, 1024], F32, kind="ExternalInput").ap()
    x2o = nc.dram_tensor("o_x2", [T, 128, 1024], F32, kind="ExternalOutput").ap()
    with ExitStack() as es:
        P = Prog(nc, es)
        sb, ps, ring = mk_alloc(nc, es)
        rowt = sb("rowt", [128, 2, 1024]); lnt = sb("lnt", [128, 2, 1024])
        for j in range(2):
            P.dma("sync", rowt[:, j, :], bcast_rows(rows[j], 128), writes=[f"row{j}"])
            P.dma("sync", lnt[:, j, :], bcast_rows(lnr[j], 128), writes=[f"ln{j}"])
        idr = ring("idx", 2, [128, N_EXP], I32)
        gr = ring("g", 8, [128, 1024], BF16)
        accr = ring("acc", 2, [128, 1024])
        xr = ring("x", 2, [128, 1024])
        str_ = ring("st", 2, [128, 2, 6]); mvr = ring("mv", 2, [128, 2]); rsr = ring("rs", 2, [128, 1])
        for t in range(T):
            g_ = 0 if t < 16 else 1
            ix, ixk = idr.next(); P.dma("sync", ix[:], idxd[t], writes=[ixk])
            acc, ack = accr.next()
            for e in range(N_EXP):
                gt, gk = gr.next()
                P.op("gpsimd", lambda en, gt=gt, ix=ix, e=e: en.indirect_dma_start(
                    out=gt[:, :], out_offset=None, in_=Yb[e][:, :], in_offset=bass.IndirectOffsetOnAxis(ap=ix[:, e:e + 1], axis=0)),
                    reads=[ixk], writes=[gk], dma=True)
                if e == 0:
                    P.copy("vector", acc[:], gt[:], reads=[gk], writes=[ack])
                else:
                    P.tt("vector", acc[:], acc[:], gt[:], ALU.add, reads=[ack, gk], writes=[ack])
            x, xk = xr.next(); P.dma("sync", x[:], x1d[t], writes=[xk])
            P.tt("gpsimd", acc[:], acc[:], rowt[:, g_, :], ALU.mult, reads=[ack, f"row{g_}"], writes=[ack])
            P.stt("vector", acc[:], x[:], ALPHA, acc[:], ALU.mult, ALU.add, reads=[xk, ack], writes=[ack])
            st, _ = str_.next(); mv, _ = mvr.next(); rs, _ = rsr.next()
            emit_layernorm(P, acc[:], ack, x[:], xk, st, mv, rs, f"lnC{t % 2}")
            P.tt("gpsimd", acc[:], x[:], lnt[:, 0, :], ALU.mult, reads=[xk, "ln0"], writes=[ack])
            P.tt("gpsimd", x[:], acc[:], lnt[:, 1, :], ALU.add, reads=[ack, "ln1"], writes=[xk])
            P.dma("sync", x2o[t], x[:], reads=[xk], writes=[xk], final=True)
        P.emit()
    return nc


def run_moe(h2, h2c, aff, affc, thr, thrc, x1, c1, mod_l, ln_w, ln_b, wg, wu, wd):
    B, n, D = x1.shape
    nctx = c1.shape[1]
    E = aff.shape[-1]
    h2all = np.ascontiguousarray(np.concatenate([h2.reshape(B * n, D), h2c.reshape(B * nctx, D)], 0))
    affall = np.concatenate([aff.reshape(B * n, E), affc.reshape(B * nctx, E)], 0)
    s_idx, j_idx = np.meshgrid(np.arange(128), np.arange(128), indexing="ij")
    cst = np.zeros((128, 448), np.float32)
    cst[:, 0:128] = (s_idx < j_idx); cst[:, 128:256] = 1.0; cst[:, 256:384] = np.eye(128); cst[0:64, 384:448] = (s_idx < j_idx)[:64, :64]
    tid = np.zeros((B, 128, 66), np.int64)
    iota = np.full((128, CAP_L + 128), -1.0, np.float32)
    iota[:, 0:CAP_L] = np.arange(CAP_L)
    iota[:, CAP_L:CAP_L + 32] = CAP_L + np.arange(32)
    for b in range(B):
        tid[b, :, 0:64] = (b * n + np.arange(n)).reshape(64, 128).T
        tid[b, :, 64:66] = (B * n + b * nctx + np.arange(nctx)).reshape(2, 128).T
    tid2 = np.ascontiguousarray(np.stack([tid // 128, tid % 128], -1).astype(np.float32))
    nc = build_k7g()
    in_maps = []
    for i in range(NCORES):
        es = [2 * i, 2 * i + 1]
        affL = np.zeros((2, B, 128, 66), np.float32)
        thr8 = np.zeros((2, B, 2), np.float32)
        for el, e in enumerate(es):
            for b in range(B):
                affL[el, b, :, 0:64] = aff[b, :, e].reshape(64, 128).T
                affL[el, b, :, 64:66] = affc[b, :, e].reshape(2, 128).T
                thr8[el, b] = (thr[b, e], thrc[b, e])
        lw = lambda w, kch: np.ascontiguousarray(w.reshape(2, kch, 128, w.shape[-1]).transpose(0, 2, 1, 3))
        in_maps.append({"h2all": h2all, "affL": affL,
                        "thr8": thr8.reshape(-1), "tid": tid2, "iota": iota, "cst": cst,
                        "wg": lw(wg[es], 8), "wu": lw(wu[es], 8), "wd": lw(wd[es], NFC)})
    res = _run(nc, in_maps)
    Yb = np.zeros((B, E, SLOTS_B + 1, D), h2all.dtype)
    posL = np.zeros((B, n, E), np.int32); posC = np.zeros((B, nctx, E), np.int32)
    for i in range(NCORES):
        Y = res.results[i]["o_Y"]; pos = res.results[i]["o_pos"]
        for el in range(2):
            e = 2 * i + el
            for b in range(B):
                Yb[b, e, 0:CAP_L] = Y[el, b * CAP_L:(b + 1) * CAP_L]
                Yb[b, e, CAP_L:SLOTS_B] = Y[el, (16 + b) * 128:(16 + b) * 128 + CAP_C]
                posL[b, :, e] = pos[el, b, :, 0:64].T.reshape(n)
                posC[b, :, e] = pos[el, b, :, 64:66].T.reshape(nctx)
    nc8 = build_k8()
    lnr = np.ascontiguousarray(np.stack([ln_w, ln_b]))
    in_maps = []
    for i in range(NCORES):
        b = i // 4
        rows = np.ascontiguousarray(np.stack([mod_l[b, 5120:6144], mod_l[2, 5120:6144]]))
        idx = tok_shard(posL, posC, i)
        idx[-1, 64:, :] = SLOTS_B
        m = {"idx": idx, "i_x1": tok_shard(x1, c1, i), "rows": rows, "lnr": lnr}
        for e in range(E):
            m[f"Yb{e}"] = np.ascontiguousarray(Yb[b, e])
        in_maps.append(m)
    res = _run(nc8, in_maps)
    return tok_unshard([r["o_x2"] for r in res.results], B, n, nctx)
```
